# Optimizing a Trainium2 kernel written in Bass

```python
import math
import jax, jax.numpy as jnp
from jax import lax
import numpy as np

D_MODEL = 1024
BATCH = 16
SEQ = 256
DEPTH = 2
DEC_BATCH = 4
DEC_SEQ = 4096
PAST_LEN = 256

GRID_W = 64
ROPE_BASE = 10000.0
EPS = 1e-6
Q_BLOCK = 128
H_A = 4
DK_A = 64
DV_A = 128
GATE_RANK = 16
GATE_TAU = 16.0
CHUNK = 64
W_A = H_A * DV_A
H_B = 8
Q_RANK = 384
KV_RANK = 256
NOPE_B = 64
ROPE_B = 32
V_B = 64
QK_B = NOPE_B + ROPE_B
W_B = H_B * V_B
H_C = 4
D_C = 64
W_C = H_C * 2 * D_C
D_FF = 4 * D_MODEL
N_BRANCH = 3
IN_WIDTHS = (H_A * DK_A, H_A * DK_A, W_A, W_A, 2 * GATE_RANK, Q_RANK, KV_RANK, ROPE_B, H_C * 2 * D_C, H_C * 2 * D_C, W_C, N_BRANCH * D_MODEL)
IN_COLS = sum(IN_WIDTHS)

kernel_name = 'hybrid_gla_mla_diff_prefix_dit_step'


def rms_norm(x, g):
    xf = x.astype(jnp.float32)
    y = xf * lax.rsqrt(jnp.mean(xf * xf, axis=-1, keepdims=True) + EPS)
    return y.astype(x.dtype) * g


def grid_positions(n):
    rows = n // GRID_W
    t = jnp.arange(rows * GRID_W)
    return t // GRID_W, t % GRID_W


def _rotate(xh, pos):
    nf = xh.shape[-1] // 2
    inv = ROPE_BASE ** (-jnp.arange(nf, dtype=jnp.float32) / nf)
    ang = pos.astype(jnp.float32)[:, None] * inv[None, :]
    ang = ang.reshape((1, ang.shape[0]) + (1,) * (xh.ndim - 3) + (nf,))
    cos = jnp.cos(ang).astype(xh.dtype)
    sin = jnp.sin(ang).astype(xh.dtype)
    x1, x2 = xh[..., :nf], xh[..., nf:]
    return jnp.concatenate([x1 * cos - x2 * sin, x1 * sin + x2 * cos], axis=-1)


def axial_rope(x, row, col):
    half = x.shape[-1] // 2
    return jnp.concatenate([_rotate(x[..., :half], row), _rotate(x[..., half:], col)], axis=-1)


def rope_tail(x, row, col):
    return jnp.concatenate([x[..., :NOPE_B], axial_rope(x[..., NOPE_B:], row, col)], axis=-1)


def gla_chunk_scan(q, k, v, g, s0):
    B, S, H, _ = q.shape
    dv = v.shape[-1]
    n = S // CHUNK

    def chunks(t):
        return t.reshape(B, n, CHUNK, H, t.shape[-1]).transpose(1, 0, 3, 2, 4)

    causal = jnp.tril(jnp.ones((CHUNK, CHUNK), dtype=bool))[:, :, None]

    def step(state, inp):
        qc, kc, vc, gc = inp
        b = jnp.cumsum(gc, axis=2)
        inter = jnp.einsum('bhtd,bhdv->bhtv', qc * jnp.exp(b), state)
        rel = jnp.where(causal, b[:, :, :, None, :] - b[:, :, None, :, :], -jnp.inf)
        att = jnp.einsum('bhtd,bhsd,bhtsd->bhts', qc, kc, jnp.exp(rel))
        intra = jnp.einsum('bhts,bhsv->bhtv', att, vc)
        b_end = b[:, :, -1:, :]
        new_state = jnp.exp(b_end[:, :, 0, :, None]) * state + jnp.einsum('bhsd,bhsv->bhdv', kc * jnp.exp(b_end - b), vc)
        return new_state, inter + intra

    final, out = lax.scan(step, s0.astype(jnp.float32), (chunks(q), chunks(k), chunks(v), chunks(g)))
    out = out.transpose(1, 0, 3, 2, 4).reshape(B, S, H, dv)
    return out, final


def gla_branch(q, k, v, r, a_lr, w_a2, b_a, g_out, s0):
    B, S, _ = q.shape
    f32 = jnp.float32
    q = q.reshape(B, S, H_A, DK_A).astype(f32) * (DK_A ** -0.5)
    k = k.reshape(B, S, H_A, DK_A).astype(f32)
    v = v.reshape(B, S, H_A, DV_A).astype(f32)
    a = a_lr.reshape(B, S, 2, GATE_RANK).astype(f32)
    logits = jnp.einsum('bsjr,jrk->bsjk', a, w_a2.astype(f32)) + b_a.astype(f32)
    g = (jax.nn.log_sigmoid(logits) / GATE_TAU).reshape(B, S, 2, H_A, DK_A)
    o_f, s_f = gla_chunk_scan(q, k, v, g[:, :, 0], s0[:, 0])
    flip = lambda t: jnp.flip(t, axis=1)
    o_b, s_b = gla_chunk_scan(flip(q), flip(k), flip(v), flip(g[:, :, 1]), s0[:, 1])
    o = rms_norm(o_f + flip(o_b), g_out.astype(f32))
    y = o.reshape(B, S, W_A).astype(r.dtype) * jax.nn.silu(r)
    return y, jnp.stack([s_f, s_b], axis=1)


def mla_queries(qd, g_qa, w_uq, g_q):
    B, S, _ = qd.shape
    q = (rms_norm(qd, g_qa) @ w_uq).reshape(B, S, H_B, QK_B)
    return rms_norm(q, g_q)


def mla_keys_values(ckv, krope, w_uk, w_uv, g_k):
    B, T, _ = ckv.shape
    k_nope = (ckv @ w_uk).reshape(B, T, H_B, NOPE_B)
    k_rope = jnp.broadcast_to(krope[:, :, None, :], (B, T, H_B, ROPE_B))
    k = rms_norm(jnp.concatenate([k_nope, k_rope], axis=-1), g_k)
    v = (ckv @ w_uv).reshape(B, T, H_B, V_B)
    return k, v


def _blocks(q):
    B, S = q.shape[:2]
    return q.reshape((B, S // Q_BLOCK, Q_BLOCK) + q.shape[2:]).swapaxes(0, 1)


def _unblocks(o):
    n, B, qb = o.shape[:3]
    return o.swapaxes(0, 1).reshape((B, n * qb) + o.shape[3:])


def softmax_attend(q, k, v):
    scale = q.shape[-1] ** -0.5

    def block(qb):
        s = jnp.einsum('bqhd,bkhd->bhqk', qb, k).astype(jnp.float32) * scale
        p = jax.nn.softmax(s, axis=-1).astype(v.dtype)
        return jnp.einsum('bhqk,bkhe->bqhe', p, v)

    return _unblocks(lax.map(block, _blocks(q)))


def diff_attend(q, k, v, lam):
    scale = q.shape[-1] ** -0.5

    def block(qb):
        s = jnp.einsum('bqhcd,bkhcd->bchqk', qb, k).astype(jnp.float32) * scale
        p = jax.nn.softmax(s, axis=-1)
        a = (p[:, 0] - lam * p[:, 1]).astype(v.dtype)
        return jnp.einsum('bhqk,bkhe->bqhe', a, v)

    return _unblocks(lax.map(block, _blocks(q)))


def layer(x, cond, lp, lam_init, cached):
    B, S, _ = x.shape
    mod = jax.nn.silu(cond) @ lp['w_mod'] + lp['b_mod']
    sh1, sc1, gt1, sh2, sc2, gt2 = jnp.split(mod, 6, axis=-1)
    h = rms_norm(x, lp['g_norm1']) * (1 + sc1) + sh1
    offsets = [int(o) for o in np.cumsum(IN_WIDTHS)[:-1]]
    (aq, ak, av, ar, aa, qd, kvd, kr, dq, dk, dv, gates) = jnp.split(h @ lp['w_in'], offsets, axis=-1)
    is_latent = cached is not None

    s0 = cached[0] if is_latent else jnp.zeros((B, 2, H_A, DK_A, DV_A), jnp.float32)
    y_a, gla_state = gla_branch(aq, ak, av, ar, aa, lp['w_gla_a2'], lp['b_gla_a'], lp['g_gla_out'], s0)

    q_b = mla_queries(qd, lp['g_mla_qa'], lp['w_mla_uq'], lp['g_mla_q'])
    ckv = rms_norm(kvd, lp['g_mla_kva'])
    k_b, v_b = mla_keys_values(ckv, kr, lp['w_mla_uk'], lp['w_mla_uv'], lp['g_mla_k'])

    q_c = rms_norm(dq.reshape(B, S, H_C, 2, D_C), lp['g_diff_q'])
    k_c = rms_norm(dk.reshape(B, S, H_C, 2, D_C), lp['g_diff_k'])
    v_c = dv.reshape(B, S, H_C, 2 * D_C)

    if is_latent:
        row, col = grid_positions(S)
        q_b = rope_tail(q_b, row, col)
        k_b = rope_tail(k_b, row, col)
        k_bc, v_bc = mla_keys_values(cached[1], cached[2], lp['w_mla_uk'], lp['w_mla_uv'], lp['g_mla_k'])
        k_b = jnp.concatenate([k_bc, k_b], axis=1)
        v_b = jnp.concatenate([v_bc, v_b], axis=1)
        q_c = axial_rope(q_c, row, col)
        k_c = jnp.concatenate([cached[3], axial_rope(k_c, row, col)], axis=1)
        v_c = jnp.concatenate([cached[4], v_c], axis=1)
        new_state = None
    else:
        new_state = (gla_state.astype(x.dtype), ckv, kr, k_c, v_c)

    y_b = softmax_attend(q_b, k_b, v_b).reshape(B, S, W_B)
    lam_p = lp['lam_qk'].astype(jnp.float32)
    lam = jnp.exp(jnp.sum(lam_p[0] * lam_p[1])) - jnp.exp(jnp.sum(lam_p[2] * lam_p[3])) + lam_init
    y_c = rms_norm(diff_attend(q_c, k_c, v_c, lam), lp['g_diff_sub']) * (1.0 - lam_init)
    y_c = y_c.reshape(B, S, W_C)

    g_a, g_b, g_c = jnp.split(jax.nn.sigmoid(gates), N_BRANCH, axis=-1)
    merged = g_a * (y_a @ lp['w_o_gla']) + g_b * (y_b @ lp['w_o_mla']) + g_c * (y_c @ lp['w_o_diff'])
    x = x + gt1 * (merged @ lp['w_out'])

    h2 = rms_norm(x, lp['g_norm2']) * (1 + sc2) + sh2
    x = x + gt2 * (jnp.square(jax.nn.relu(h2 @ lp['w_mlp1'])) @ lp['w_mlp2'])
    return x, new_state


def setup_inputs(seed: int = 0) -> dict:
    key = jax.random.key(seed)
    ks = iter(jax.random.split(key, 48))
    f32 = jnp.float32

    def nrm(shape, scale=1.0):
        return jax.random.normal(next(ks), shape, f32) * scale

    def gain(shape):
        return 1.0 + nrm(shape, 0.02)

    D = D_MODEL
    return {
        'x_prompt': nrm((BATCH, SEQ, D)),
        'x_sample': nrm((DEC_BATCH, DEC_SEQ, D)),
        'state_gla': nrm((DEC_BATCH, DEPTH, 2, H_A, DK_A, DV_A), 0.5),
        'cache_mla_ckv': nrm((DEC_BATCH, DEPTH, PAST_LEN, KV_RANK)),
        'cache_mla_krope': nrm((DEC_BATCH, DEPTH, PAST_LEN, ROPE_B)),
        'cache_diff_k': nrm((DEC_BATCH, DEPTH, PAST_LEN, H_C, 2, D_C)),
        'cache_diff_v': nrm((DEC_BATCH, DEPTH, PAST_LEN, H_C, 2 * D_C)),
        'c': nrm((DEC_BATCH, D)),
        'c_ctx': nrm((D,)),
        'w_mod': nrm((DEPTH, D, 6 * D), 0.5 * D ** -0.5),
        'b_mod': nrm((DEPTH, 6 * D), 0.02),
        'g_norm1': gain((DEPTH, D)),
        'g_norm2': gain((DEPTH, D)),
        'w_in': nrm((DEPTH, D, IN_COLS), D ** -0.5),
        'w_gla_a2': nrm((DEPTH, 2, GATE_RANK, H_A * DK_A), GATE_RANK ** -0.5),
        'b_gla_a': nrm((DEPTH, 2, H_A * DK_A), 0.1),
        'g_gla_out': gain((DEPTH, DV_A)),
        'g_mla_qa': gain((DEPTH, Q_RANK)),
        'g_mla_kva': gain((DEPTH, KV_RANK)),
        'w_mla_uq': nrm((DEPTH, Q_RANK, H_B * QK_B), Q_RANK ** -0.5),
        'w_mla_uk': nrm((DEPTH, KV_RANK, H_B * NOPE_B), KV_RANK ** -0.5),
        'w_mla_uv': nrm((DEPTH, KV_RANK, H_B * V_B), KV_RANK ** -0.5),
        'g_mla_q': gain((DEPTH, QK_B)),
        'g_mla_k': gain((DEPTH, QK_B)),
        'g_diff_q': gain((DEPTH, D_C)),
        'g_diff_k': gain((DEPTH, D_C)),
        'lam_qk': nrm((DEPTH, 4, D_C), 0.1),
        'g_diff_sub': gain((DEPTH, 2 * D_C)),
        'w_o_gla': nrm((DEPTH, W_A, D), W_A ** -0.5),
        'w_o_mla': nrm((DEPTH, W_B, D), W_B ** -0.5),
        'w_o_diff': nrm((DEPTH, W_C, D), W_C ** -0.5),
        'w_out': nrm((DEPTH, D, D), D ** -0.5),
        'w_mlp1': nrm((DEPTH, D, D_FF), D ** -0.5),
        'w_mlp2': nrm((DEPTH, D_FF, D), D_FF ** -0.5),
    }


def reference(x_prompt, x_sample, state_gla, cache_mla_ckv, cache_mla_krope, cache_diff_k, cache_diff_v, c, c_ctx, w_mod, b_mod, g_norm1, g_norm2, w_in, w_gla_a2, b_gla_a, g_gla_out, g_mla_qa, g_mla_kva, w_mla_uq, w_mla_uk, w_mla_uv, g_mla_q, g_mla_k, g_diff_q, g_diff_k, lam_qk, g_diff_sub, w_o_gla, w_o_mla, w_o_diff, w_out, w_mlp1, w_mlp2):
    y_prompt = x_prompt
    y_sample = x_sample
    ctx_cond = c_ctx[None, None, :]
    lat_cond = c[:, None, :]
    st_gla, st_ckv, st_krope, st_dk, st_dv = [], [], [], [], []
    for l in range(DEPTH):
        lp = dict(w_mod=w_mod[l], b_mod=b_mod[l], g_norm1=g_norm1[l], g_norm2=g_norm2[l], w_in=w_in[l], w_gla_a2=w_gla_a2[l], b_gla_a=b_gla_a[l], g_gla_out=g_gla_out[l], g_mla_qa=g_mla_qa[l], g_mla_kva=g_mla_kva[l], w_mla_uq=w_mla_uq[l], w_mla_uk=w_mla_uk[l], w_mla_uv=w_mla_uv[l], g_mla_q=g_mla_q[l], g_mla_k=g_mla_k[l], g_diff_q=g_diff_q[l], g_diff_k=g_diff_k[l], lam_qk=lam_qk[l], g_diff_sub=g_diff_sub[l], w_o_gla=w_o_gla[l], w_o_mla=w_o_mla[l], w_o_diff=w_o_diff[l], w_out=w_out[l], w_mlp1=w_mlp1[l], w_mlp2=w_mlp2[l])
        lam_init = 0.8 - 0.6 * math.exp(-0.3 * l)
        y_prompt, st = layer(y_prompt, ctx_cond, lp, lam_init, None)
        st_gla.append(st[0])
        st_ckv.append(st[1])
        st_krope.append(st[2])
        st_dk.append(st[3])
        st_dv.append(st[4])
        cached = (state_gla[:, l], cache_mla_ckv[:, l], cache_mla_krope[:, l], cache_diff_k[:, l], cache_diff_v[:, l])
        y_sample, _ = layer(y_sample, lat_cond, lp, lam_init, cached)
    new_state_gla = jnp.stack(st_gla, axis=1)
    new_mla_ckv = jnp.stack(st_ckv, axis=1)
    new_mla_krope = jnp.stack(st_krope, axis=1)
    new_diff_k = jnp.stack(st_dk, axis=1)
    new_diff_v = jnp.stack(st_dv, axis=1)
    return (y_prompt, y_sample, new_state_gla, new_mla_ckv, new_mla_krope, new_diff_k, new_diff_v)
```

```python
import math
import numpy as np
import ml_dtypes
from contextlib import ExitStack
import concourse.bass as bass
import concourse.mybir as mybir
from concourse.bass_utils import run_bass_kernel_spmd

F32 = mybir.dt.float32
BF16 = mybir.dt.bfloat16
AF = mybir.ActivationFunctionType
ALU = mybir.AluOpType

D = 1024
L = 2
NP_TOK = 512
NS_TOK = 4096
T = NP_TOK + NS_TOK
TT = 512
NT = T // TT
PAST = 256
TKS = PAST + NS_TOK
EPS = 1e-6
O_AQ, O_AK, O_AV, O_AR, O_AA, O_QD, O_KVD, O_KR, O_DQ, O_DK, O_DV, O_G = 0, 256, 512, 1024, 1536, 1568, 1952, 2208, 2240, 2752, 3264, 3776
IN_COLS = 6848


class Buf:
    __slots__ = ("lw", "rd", "psum")

    def __init__(self, psum=False):
        self.lw = None
        self.rd = []
        self.psum = psum


class _Rec:
    def __init__(self):
        self.call = None

    def __getattr__(self, name):
        def f(*a, **k):
            self.call = (name, a, k)
            return self
        return f


class Eng:
    def __init__(self, fw, name, handle):
        self.name = name
        self.h = handle
        self.sem = fw.new_sem("e_" + name)
        self.count = 0
        self.waited = {}
        self.thunks = []


class FW:
    def __init__(self, nc, ctx, n_dma_sems=48):
        self.nc = nc
        self.ctx = ctx
        self.pe = Eng(self, "pe", nc.tensor)
        self.dve = Eng(self, "dve", nc.vector)
        self.act = Eng(self, "act", nc.scalar)
        self.pool = Eng(self, "pool", nc.gpsimd)
        self.sp = Eng(self, "sp", nc.sync)
        self.dma_sems = [self.new_sem(f"d{i}") for i in range(n_dma_sems)]
        self.dma_cnt = [0] * n_dma_sems
        self.dma_rr = 0
        self.ew_rr = 0

    def new_sem(self, name):
        return self.ctx.enter_context(self.nc.semaphore(name))

    def _wait(self, E, ev):
        if ev is None:
            return
        sem, val = ev
        if sem is E.sem and E.name == "pe":
            return
        key = id(sem)
        if E.waited.get(key, 0) >= val:
            return
        E.waited[key] = val
        E.thunks.append(lambda h=E.h, s=sem, v=val: h.wait_ge(s, v))

    def _deps(self, E, reads, writes):
        for b in reads:
            self._wait(E, b.lw)
        for b in writes:
            self._wait(E, b.lw)
            for ev in b.rd:
                self._wait(E, ev)

    def _post(self, ev, reads, writes):
        for b in reads:
            b.rd.append(ev)
            if len(b.rd) > 64:
                b.rd = b.rd[-48:]
        for b in writes:
            b.lw = ev
            b.rd = []

    def op(self, E, fn, reads=(), writes=(), inc=True):
        if any(b.psum for b in reads):
            writes = list(writes) + [b for b in reads if b.psum]
            reads = [b for b in reads if not b.psum]
        self._deps(E, reads, writes)
        rec = _Rec()
        fn(rec)
        mname, margs, mkw = rec.call
        if inc:
            E.count += 1
            ev = (E.sem, E.count)
            E.thunks.append(lambda h=E.h, n=mname, a=margs, k=mkw, s=E.sem: getattr(h, n)(*a, **k).then_inc(s, 1))
        else:
            ev = (E.sem, E.count + 1)
            E.thunks.append(lambda h=E.h, n=mname, a=margs, k=mkw: getattr(h, n)(*a, **k))
        self._post(ev, reads, writes)
        return ev

    def dma(self, Q, out_ap, in_ap, reads=(), writes=(), **kw):
        self._deps(Q, reads, writes)
        i = self.dma_rr
        self.dma_rr = (self.dma_rr + 1) % len(self.dma_sems)
        self.dma_cnt[i] += 16
        sem = self.dma_sems[i]
        ev = (sem, self.dma_cnt[i])
        Q.thunks.append(lambda h=Q.h, o=out_ap, a=in_ap, s=sem, k=kw: h.dma_start(out=o, in_=a, **k).then_inc(s, 16))
        self._post(ev, reads, writes)
        return ev

    def barrier(self):
        engs = [self.pe, self.dve, self.act, self.pool, self.sp]
        for E in engs:
            for E2 in engs:
                if E2 is not E and E2.count > 0:
                    self._wait(E, (E2.sem, E2.count))
            for i, sem in enumerate(self.dma_sems):
                if self.dma_cnt[i] > 0:
                    self._wait(E, (sem, self.dma_cnt[i]))

    def finish(self, final_bufs):
        for b in final_bufs:
            self._wait(self.sp, b.lw)
        nc = self.nc
        engs = self
        with nc.Block() as block:
            @block.tensor
            def _(e):
                for t in engs.pe.thunks:
                    t()

            @block.vector
            def _(e):
                for t in engs.dve.thunks:
                    t()

            @block.scalar
            def _(e):
                for t in engs.act.thunks:
                    t()

            @block.gpsimd
            def _(e):
                for t in engs.pool.thunks:
                    t()

            @block.sync
            def _(e):
                for t in engs.sp.thunks:
                    t()


class Arena:
    def __init__(self, tensor, nbytes):
        self.t = tensor
        self.nbytes = nbytes
        self.off = 0
        self.peak = 0

    def reset(self):
        self.off = 0

    def alloc(self, name, shape, dt):
        esz = 2 if dt == BF16 else 4
        n = 1
        for d in shape[1:]:
            n *= d
        nb = (n * esz + 63) // 64 * 64
        assert self.off + nb <= self.nbytes, (name, self.off, nb, self.nbytes)
        ap = self.t[0:shape[0], self.off // 4:(self.off + nb) // 4]
        self.off += nb
        self.peak = max(self.peak, self.off)
        if dt == BF16:
            ap = ap.bitcast(BF16)
        ap = ap[:, 0:n]
        if len(shape) == 3:
            ap = ap.rearrange("p (a b) -> p a b", b=shape[2])
        elif len(shape) == 4:
            ap = ap.rearrange("p (a b c) -> p a b c", b=shape[2], c=shape[3])
        return ap


class Ring:
    def __init__(self, alloc, name, shape, dt, n, psum=False):
        self.t = [alloc(f"{name}{i}", shape, dt) for i in range(n)]
        self.b = [Buf(psum) for _ in range(n)]
        self.i = 0

    def get(self):
        i = self.i
        self.i = (i + 1) % len(self.t)
        return self.t[i], self.b[i]


def build_program(debug=False):
    nc = bass.Bass("TRN2", target_bir_lowering=False)
    dram_in = lambda name, shape, dt=F32: nc.dram_tensor(name, list(shape), dt, kind="ExternalInput").ap()
    dram_out = lambda name, shape, dt=F32: nc.dram_tensor(name, list(shape), dt, kind="ExternalOutput").ap()
    dram_tmp = lambda name, shape, dt=F32: nc.dram_tensor(name, list(shape), dt).ap()

    xin = dram_in("xin", [T, D])
    condT = dram_in("condT", [128, 8, 2])
    st_gla = dram_in("st_gla", [L, 2, 64, 4, 128])
    c_ckv = dram_in("c_ckv", [L, PAST, 256])
    c_kr = dram_in("c_kr", [L, PAST, 32])
    c_dk = dram_in("c_dk", [L, PAST, 512])
    c_dv = dram_in("c_dv", [L, PAST, 512])
    w_mod = dram_in("w_mod", [L, D, 6 * D])
    b_modT = dram_in("b_modT", [L, 128, 48])
    gnT = dram_in("gnT", [L, 128, 16])
    w_in = dram_in("w_in", [L, D, IN_COLS])
    w_a2bd = dram_in("w_a2bd", [L, 32, 512])
    b_a = dram_in("b_a", [L, 1, 512])
    gcol = dram_in("gcol", [L, 128, 12])
    w_uq = dram_in("w_uq", [L, 384, 8, 96])
    w_ukp = dram_in("w_ukp", [L, 256, 8, 96])
    w_uv = dram_in("w_uv", [L, 256, 512])
    lam_qk = dram_in("lam_qk", [L, 1, 256])
    w_o3 = dram_in("w_o3", [L, 3, 512, D])
    w_out = dram_in("w_out", [L, D, D])
    w_m1 = dram_in("w_m1", [L, D, 4 * D])
    w_m2 = dram_in("w_m2", [L, 4 * D, D])
    ident_d = dram_in("ident", [128, 128])
    cmat_d = dram_in("cmat", [128, 8, 128], BF16)
    pmat_d = dram_in("pmat", [128, 3, 128], BF16)
    tri_d = dram_in("tri", [64, 4, 64])
    ropeM = dram_in("ropeM", [2, 96, NS_TOK])
    ropeD = dram_in("ropeD", [2, 128, NS_TOK])

    y_out = dram_out("y_out", [T, D])
    o_gla = dram_out("o_gla", [L, 2, 2, 64, 4, 128])
    o_ckv = dram_out("o_ckv", [L, NP_TOK, 256])
    o_kr = dram_out("o_kr", [L, NP_TOK, 32])
    o_dk = dram_out("o_dk", [L, NP_TOK, 512])
    o_dv = dram_out("o_dv", [L, NP_TOK, 512])

    mk = dram_out if debug else dram_tmp
    xT_s = dram_tmp("xT_s", [8, 128, T])
    gq_s = dram_tmp("gq_s", [64, 4, T], BF16)
    gk_s = dram_tmp("gk_s", [64, 4, T], BF16)
    gkt_s = dram_tmp("gkt_s", [T, 256], BF16)
    gvt_s = dram_tmp("gvt_s", [T, 512], BF16)
    gG_s = dram_tmp("gG_s", [T, 512])
    go_s = dram_tmp("go_s", [128, 4, T])
    q_s = dram_tmp("q_s", [8, 96, T], BF16)
    k_s = dram_tmp("k_s", [8, 96, NP_TOK + TKS], BF16)
    v_s = dram_tmp("v_s", [8, 128, (NP_TOK + TKS) // 128, 128], BF16)
    dq_s = dram_tmp("dq_s", [4, 128, T], BF16)
    dk_s = dram_tmp("dk_s", [4, 128, NP_TOK + TKS], BF16)
    dv_s = dram_tmp("dv_s", [4, 128, (NP_TOK + TKS) // 128, 128], BF16)
    ya_s = mk("ya_s", [4, 128, T], BF16)
    yb_s = mk("yb_s", [4, 128, T], BF16)
    yc_s = mk("yc_s", [4, 128, T], BF16)

    with ExitStack() as ctx:
        fw = FW(nc, ctx)
        PE, DVE, ACT, POOL, SP = fw.pe, fw.dve, fw.act, fw.pool, fw.sp
        sb = lambda name, shape, dt=F32: ctx.enter_context(nc.sbuf_tensor("s_" + name, list(shape), dt))
        psb = lambda name, shape, dt=F32: ctx.enter_context(nc.psum_tensor("p_" + name, list(shape), dt))

        psA = Ring(psb, "psA", [128, 512], F32, 4, psum=True)
        psB = Ring(psb, "psB", [128, 512], F32, 4, psum=True)

        OVB = 72 * 1024
        arena = Arena(sb("arena", [128, OVB // 4], F32), OVB)
        ov = arena.alloc

        def ew():
            fw.ew_rr ^= 1
            return DVE if fw.ew_rr else POOL

        ident = sb("ident", [128, 128]); b_ident = Buf()
        cmat = sb("cmat", [128, 8, 128], BF16); b_cmat = Buf()
        pmat = sb("pmat", [128, 3, 128], BF16); b_pmat = Buf()
        tri = sb("tri", [64, 4, 64]); b_tri = Buf()
        ones_row = sb("ones_row", [1, 128], BF16); b_onesrow = Buf()
        ones_rowf = sb("ones_rowf", [1, 128]); b_onesrowf = Buf()
        ones_col = sb("ones_col", [64, 1]); b_onescol = Buf()
        fw.dma(SP, ident[:], ident_d, writes=[b_ident])
        fw.dma(SP, cmat[:], cmat_d, writes=[b_cmat])
        fw.dma(SP, pmat[:], pmat_d, writes=[b_pmat])
        fw.dma(SP, tri[:], tri_d, writes=[b_tri])
        fw.op(DVE, lambda h: h.memset(ones_row[:], 1.0), writes=[b_onesrow])
        fw.op(DVE, lambda h: h.memset(ones_rowf[:], 1.0), writes=[b_onesrowf])
        fw.op(DVE, lambda h: h.memset(ones_col[:], 1.0), writes=[b_onescol])
        C_1024, C_384, C_256, C_96, C_BLK64, C_128, C_ONE, C_SEL32 = range(8)
        P_96, P_128, _ = range(3)

        cT = sb("cT", [128, 8, 2]); b_cT = Buf()
        scT = sb("scT", [128, 8, 2]); b_scT = Buf()
        fw.dma(SP, cT[:], condT, writes=[b_cT])
        fw.op(ACT, lambda h: h.activation(out=scT[:], in_=cT[:], func=AF.Silu), reads=[b_cT], writes=[b_scT])

        modT = sb("modT", [128, 48, 2]); b_modT_b = Buf()
        bmod = sb("bmod", [128, 48]); b_bmod = Buf()
        gn = sb("gn", [128, 16]); b_gn = Buf()
        A1 = sb("A1", [128, 8, 2]); A2 = sb("A2", [128, 8, 2]); b_A = Buf()
        gc = sb("gc", [128, 12]); b_gc = Buf()
        wa2 = sb("wa2", [32, 512], BF16); b_wa2 = Buf()
        ba = sb("ba", [1, 512], BF16); b_ba = Buf()
        lamt = sb("lamt", [1, 256]); b_lamt = Buf()
        lam1 = sb("lam1", [1, 8]); b_lam1 = Buf()
        lamc = sb("lamc", [128, 2]); b_lamc = Buf()
        arena.reset()
        wmod_r = Ring(ov, "wmod", [128, 8, 256], F32, 2)

        w8 = Ring(sb, "w8", [128, 8, 512], BF16, 3)
        w4 = Ring(sb, "w4", [128, 4, 1024], BF16, 2)

        def load_w8(src_ap_rows_cols, ncols, nk=8):
            t, b = w8.get()
            fw.dma(POOL, t[:, 0:nk, 0:ncols], src_ap_rows_cols.rearrange("(k p) c -> p k c", p=128), writes=[b])
            return t, b

        xt_r = Ring(sb, "xt", [128, 8, TT], F32, 1)
        sq_r = Ring(sb, "sq", [128, 8, TT], BF16, 1)
        hT_r = Ring(sb, "hT", [128, 8, TT], BF16, 2)
        rs_r = Ring(sb, "rs", [128, TT], F32, 3)
        f32_r = Ring(sb, "f32", [128, TT], F32, 6)
        bf_r = Ring(sb, "bf", [128, TT], BF16, 8)

        def evac(ps_ap, out_ap, b_ps, b_out, eng=None, func=None, scale=1.0):
            if func is not None or eng is ACT:
                f = func if func is not None else AF.Copy
                return fw.op(ACT, lambda h: h.activation(out=out_ap, in_=ps_ap, func=f, scale=scale), reads=[b_ps], writes=[b_out])
            return fw.op(DVE, lambda h: h.tensor_copy(out=out_ap, in_=ps_ap), reads=[b_ps], writes=[b_out])

        b_xT = [Buf() for _ in range(NT)]
        arena.reset()
        xtok_r = Ring(ov, "xtok", [128, 4, D], F32, 2)

        def phase0(ti):
            xk, bxk = xtok_r.get()
            fw.dma(SP, xk[:], xin[ti * TT:(ti + 1) * TT, :].rearrange("(g p) f -> p g f", p=128), writes=[bxk])
            xt, bxt = xt_r.get()
            for k in range(8):
                ps, bps = psA.get()
                for g in range(4):
                    fw.op(PE, lambda h, ps=ps, g=g, k=k: h.transpose(ps[:, g * 128:(g + 1) * 128], xk[:, g, k * 128:(k + 1) * 128], ident[:]),
                          reads=[bxk, b_ident], writes=[bps], inc=(g == 3))
                evac(ps[:], xt[:, k, :], bps, bxt, eng=(ACT if k % 2 else DVE))
            fw.dma(SP, xT_s[:, :, ti * TT:(ti + 1) * TT].rearrange("k p t -> p k t"), xt[:], reads=[bxt], writes=[b_xT[ti]])

        import os
        for ti in range(int(os.environ.get("KPH0", str(NT)))):
            phase0(ti)
        fw.barrier()

        def rsqrt_eps(src_ap, dst_ap, bsrc, bdst):
            fw.op(ACT, lambda h: h.activation(out=dst_ap, in_=src_ap, func=AF.Sqrt, bias=EPS), reads=[bsrc], writes=[bdst])
            fw.op(DVE, lambda h: h.reciprocal(out=dst_ap, in_=dst_ap), reads=[bdst], writes=[bdst])

        def rms_rstd(sq_aps, ones_ap, nparts, eps=EPS):
            ps, bps = psA.get()
            n = len(sq_aps)
            for i, (a, b) in enumerate(sq_aps):
                fw.op(PE, lambda h, a=a, i=i: h.matmul(ps[0:nparts, :], lhsT=ones_ap, rhs=a, start=(i == 0), stop=(i == n - 1)),
                      reads=[b, b_cmat], writes=[bps], inc=(i == n - 1))
            rs, brs = rs_r.get()
            rsqrt_eps(ps[0:nparts, :], rs[0:nparts, :], bps, brs)
            return rs, brs

        def load_xT(ti):
            xt, bxt = xt_r.get()
            fw.dma(SP, xt[:], xT_s[:, :, ti * TT:(ti + 1) * TT].rearrange("k p t -> p k t"), reads=[b_xT[ti]], writes=[bxt])
            return xt, bxt

        def norm_mod(xt, bxt, Amod, shift_lo, cond):
            sq, bsq = sq_r.get()
            fw.op(ACT, lambda h: h.activation(out=sq[:], in_=xt[:], func=AF.Square), reads=[bxt], writes=[bsq])
            rs, brs = rms_rstd([(sq[:, k, :], bsq) for k in range(8)], cmat[:, C_1024, :], 128)
            hT, bh = hT_r.get()
            for k in range(8):
                tmp, btmp = f32_r.get()
                e = DVE if k % 2 else POOL
                fw.op(e, lambda h, k=k, tmp=tmp: h.tensor_tensor(out=tmp[:], in0=xt[:, k, :], in1=rs[:], op=ALU.mult), reads=[bxt, brs], writes=[btmp])
                fw.op(ACT, lambda h, k=k, tmp=tmp: h.activation(out=hT[:, k, :], in_=tmp[:], func=AF.Identity,
                                                               scale=Amod[:, k, cond:cond + 1], bias=modT[:, shift_lo + k, cond:cond + 1]),
                      reads=[btmp, b_A, b_modT_b], writes=[bh])
            return hT, bh

        def proj_fm(wt, bw, col0, m, rhs_fn, rhs_bufs, nk, out_parts=None):
            ps, bps = psA.get()
            for k in range(nk):
                fw.op(PE, lambda h, k=k: h.matmul(ps[0:m, :], lhsT=wt[:, k, col0:col0 + m], rhs=rhs_fn(k), start=(k == 0), stop=(k == nk - 1)),
                      reads=[bw] + rhs_bufs, writes=[bps], inc=(k == nk - 1))
            return ps, bps

        def norm_rope_store(ps, bps, npart, ones_ap, gcolumn, rope, pm_idx, ropetab, dst_ap, dst_buf, keep_f32=None):
            xf, bxf = f32_r.get()
            evac(ps[0:npart, :], xf[0:npart, :], bps, bxf, eng=DVE)
            sq, bsq = bf_r.get()
            fw.op(ACT, lambda h: h.activation(out=sq[0:npart, :], in_=xf[0:npart, :], func=AF.Square), reads=[bxf], writes=[bsq])
            rs, brs = rms_rstd([(sq[0:npart, :], bsq)], ones_ap, npart)
            xn, bxn = f32_r.get()
            fw.op(DVE, lambda h: h.scalar_tensor_tensor(out=xn[0:npart, :], in0=xf[0:npart, :], scalar=gc[0:npart, gcolumn:gcolumn + 1], in1=rs[0:npart, :],
                                                        op0=ALU.mult, op1=ALU.mult), reads=[bxf, brs, b_gc], writes=[bxn])
            if keep_f32 is not None:
                keep_f32(xn, bxn)
            ob, bob = bf_r.get()
            if not rope:
                fw.op(POOL, lambda h: h.tensor_copy(out=ob[0:npart, :], in_=xn[0:npart, :]), reads=[bxn], writes=[bob])
            else:
                ctab, stab, btab = ropetab
                xb, bxb = bf_r.get()
                fw.op(POOL, lambda h: h.tensor_copy(out=xb[0:npart, :], in_=xn[0:npart, :]), reads=[bxn], writes=[bxb])
                ps2, bps2 = psA.get()
                fw.op(PE, lambda h: h.matmul(ps2[0:npart, :], lhsT=pmat[0:npart, pm_idx, 0:npart], rhs=xb[0:npart, :], start=True, stop=True),
                      reads=[bxb, b_pmat], writes=[bps2])
                t1, bt1 = f32_r.get()
                fw.op(POOL, lambda h: h.tensor_tensor(out=t1[0:npart, :], in0=xn[0:npart, :], in1=ctab, op=ALU.mult), reads=[bxn, btab], writes=[bt1])
                t2, bt2 = f32_r.get()
                fw.op(DVE, lambda h: h.tensor_tensor(out=t2[0:npart, :], in0=ps2[0:npart, :], in1=stab, op=ALU.mult), reads=[bps2, btab], writes=[bt2])
                fw.op(POOL, lambda h: h.tensor_tensor(out=ob[0:npart, :], in0=t1[0:npart, :], in1=t2[0:npart, :], op=ALU.add), reads=[bt1, bt2], writes=[bob])
            fw.dma(SP, dst_ap, ob[0:npart, :], reads=[bob], writes=[dst_buf])

        def transpose_out(src_fn, src_bufs, nfeat_parts, ncols_total, dst_rows_fn, colslices):
            for g in range(4):
                ps, bps = psA.get()
                n = len(colslices)
                for i, (j, pj, c0) in enumerate(colslices):
                    fw.op(PE, lambda h, j=j, pj=pj, c0=c0, g=g: h.transpose(ps[:, c0:c0 + pj], src_fn(j)[:, g * 128:(g + 1) * 128], ident[0:pj, 0:pj]),
                          reads=src_bufs + [b_ident], writes=[bps], inc=(i == n - 1))
                o, bo = f32_r.get()
                evac(ps[:, 0:ncols_total], o[:, 0:ncols_total], bps, bo, eng=DVE)
                fw.dma(SP, dst_rows_fn(g), o[:, 0:ncols_total], reads=[bo], writes=[Buf()])

        b_gq = Buf(); b_gk = Buf(); b_gkt = Buf(); b_gvt = Buf(); b_gG = Buf(); b_go = Buf()
        b_q = Buf(); b_k = Buf(); b_v = Buf(); b_dq = Buf(); b_dk = Buf(); b_dv = Buf()
        b_ya = Buf(); b_yb = Buf(); b_yc = Buf()
        out_bufs = []

        arena.reset()
        ropM_r = Ring(ov, "ropM", [96, 2, TT], F32, 1)
        ropD_r = Ring(ov, "ropD", [128, 2, TT], F32, 1)
        vaug_r = Ring(ov, "vaug", [128, 8, 4, 128], BF16, 1)
        tok_r = Ring(ov, "tok", [128, 4, 512], BF16, 2)
        tokf_r = Ring(ov, "tokf", [128, 4, 512], F32, 1)
        keep_r = Ring(ov, "keep", [128, 4, TT], F32, 1)
        qdn_r = Ring(ov, "qdn", [128, 3, TT], BF16, 1)
        ckv_r = Ring(ov, "ckv", [128, 2, TT], BF16, 1)
        ckvf_r = Ring(ov, "ckvf", [128, 2, TT], F32, 1)
        krb_r = Ring(ov, "krb", [32, TT], BF16, 1)
        krf_r = Ring(ov, "krf", [32, TT], F32, 1)
        aab_r = Ring(ov, "aab", [32, TT], BF16, 1)
        g4_r = Ring(ov, "g4", [64, 4, TT], BF16, 1)
        r4_r = Ring(ov, "r4", [128, 4, TT], BF16, 1)
        ctok_r = Ring(ov, "ctok", [128, 2, 512], F32, 2)

        def layer_setup(l):
            fw.dma(SP, bmod[:], b_modT[l], writes=[b_bmod])
            fw.dma(SP, gn[:], gnT[l], writes=[b_gn])
            fw.dma(SP, gc[:], gcol[l], writes=[b_gc])
            fw.dma(POOL, wa2[:], w_a2bd[l], writes=[b_wa2])
            fw.dma(POOL, ba[:], b_a[l], writes=[b_ba])
            fw.dma(SP, lamt[:], lam_qk[l], writes=[b_lamt])
            for cb in range(24):
                wm, bwm = wmod_r.get()
                fw.dma(SP, wm[:], w_mod[l, :, cb * 256:(cb + 1) * 256].rearrange("(k p) c -> p k c", p=128), writes=[bwm])
                ps, bps = psA.get()
                for j in range(2):
                    for k in range(8):
                        fw.op(PE, lambda h, j=j, k=k, wm=wm, ps=ps: h.matmul(ps[:, j * 2:j * 2 + 2], lhsT=wm[:, k, j * 128:(j + 1) * 128], rhs=scT[:, k, :],
                                                                          start=(k == 0), stop=(k == 7)),
                              reads=[bwm, b_scT], writes=[bps], inc=(j == 1 and k == 7))
                fw.op(DVE, lambda h, cb=cb, ps=ps: h.tensor_tensor(out=modT[:, cb * 2:(cb + 1) * 2, :], in0=ps[:, 0:4].rearrange("p (j c) -> p j c", c=2),
                                                                 in1=bmod[:, cb * 2:(cb + 1) * 2].unsqueeze(2).broadcast_to([128, 2, 2]), op=ALU.add),
                      reads=[bps, b_bmod], writes=[b_modT_b])
            fw.op(DVE, lambda h: h.scalar_tensor_tensor(out=A1[:], in0=modT[:, 8:16, :], scalar=1.0, in1=gn[:, 0:8].unsqueeze(2).broadcast_to([128, 8, 2]),
                                                        op0=ALU.add, op1=ALU.mult), reads=[b_modT_b, b_gn], writes=[b_A])
            fw.op(DVE, lambda h: h.scalar_tensor_tensor(out=A2[:], in0=modT[:, 32:40, :], scalar=1.0, in1=gn[:, 8:16].unsqueeze(2).broadcast_to([128, 8, 2]),
                                                        op0=ALU.add, op1=ALU.mult), reads=[b_modT_b, b_gn], writes=[b_A])
            lam_init = 0.8 - 0.6 * math.exp(-0.3 * l)
            fw.op(DVE, lambda h: h.tensor_tensor(out=lamt[:, 0:64], in0=lamt[:, 0:64], in1=lamt[:, 64:128], op=ALU.mult), reads=[b_lamt], writes=[b_lamt])
            fw.op(DVE, lambda h: h.tensor_tensor(out=lamt[:, 128:192], in0=lamt[:, 128:192], in1=lamt[:, 192:256], op=ALU.mult), reads=[b_lamt], writes=[b_lamt])
            fw.op(DVE, lambda h: h.reduce_sum(out=lam1[:, 0:1], in_=lamt[:, 0:64], axis=mybir.AxisListType.X), reads=[b_lamt], writes=[b_lam1])
            fw.op(DVE, lambda h: h.reduce_sum(out=lam1[:, 1:2], in_=lamt[:, 128:192], axis=mybir.AxisListType.X), reads=[b_lamt], writes=[b_lam1])
            fw.op(ACT, lambda h: h.activation(out=lam1[:, 2:4], in_=lam1[:, 0:2], func=AF.Exp), reads=[b_lam1], writes=[b_lam1])
            fw.op(DVE, lambda h: h.scalar_tensor_tensor(out=lam1[:, 4:5], in0=lam1[:, 3:4], scalar=-lam_init, in1=lam1[:, 2:3], op0=ALU.add, op1=ALU.subtract),
                  reads=[b_lam1], writes=[b_lam1])
            ps, bps = psA.get()
            fw.op(PE, lambda h: h.matmul(ps[:, 0:1], lhsT=ones_rowf[:, :], rhs=lam1[:, 4:5], start=True, stop=True), reads=[b_onesrowf, b_lam1], writes=[bps])
            fw.op(DVE, lambda h: h.tensor_copy(out=lamc[:, 0:1], in_=ps[:, 0:1]), reads=[bps], writes=[b_lamc])
            return lam_init

        def key_base(ti):
            return 0 if ti == 0 else NP_TOK + PAST + (ti - 1) * TT

        import os as _os
        KP1S = int(_os.environ.get("KP1S", "99"))
        KP1 = int(_os.environ.get("KP1", str(NT)))

        def phase1(l, ti):
            if ti >= KP1:
                return
            lat = ti > 0
            cond = 1 if lat else 0
            t0 = ti * TT
            kb = key_base(ti)
            xt, bxt = load_xT(ti)
            hT, bh = norm_mod(xt, bxt, A1, 0, cond)
            rhs_h = lambda k: hT[:, k, :]
            if lat:
                rM, brM = ropM_r.get()
                fw.dma(SP, rM[:], ropeM[:, :, (ti - 1) * TT:ti * TT].rearrange("c p t -> p c t"), writes=[brM])
                rD, brD = ropD_r.get()
                fw.dma(SP, rD[:], ropeD[:, :, (ti - 1) * TT:ti * TT].rearrange("c p t -> p c t"), writes=[brD])
                tabM = (rM[:, 0, :], rM[:, 1, :], brM)
                tabD = (rD[:, 0, :], rD[:, 1, :], brD)
            else:
                tabM = tabD = None

            if KP1S < 1:
                return
            wt, bw = load_w8(w_in[l, :, O_AQ:O_AQ + 512], 512)
            for which, dst, bdst in ((0, gq_s, b_gq), (1, gk_s, b_gk)):
                g4, bg4 = g4_r.get()
                for hh in range(4):
                    ps, bps = proj_fm(wt, bw, which * 256 + hh * 64, 64, rhs_h, [bh], 8)
                    evac(ps[0:64, :], g4[:, hh, :], bps, bg4, eng=(ACT if hh % 2 else DVE))
                fw.dma(SP, dst[:, :, t0:t0 + TT], g4[:], reads=[bg4], writes=[bdst])
            if KP1S < 2:
                return
            tk, btk = tok_r.get()
            for g in range(4):
                ps, bps = psA.get()
                for k in range(8):
                    fw.op(PE, lambda h, k=k, g=g, ps=ps: h.matmul(ps[:, 0:256], lhsT=hT[:, k, g * 128:(g + 1) * 128], rhs=wt[:, k, 256:512], start=(k == 0), stop=(k == 7)),
                          reads=[bw, bh], writes=[bps], inc=(k == 7))
                evac(ps[:, 0:256], tk[:, g, 0:256], bps, btk, eng=(ACT if g % 2 else DVE))
            fw.dma(SP, gkt_s[t0:t0 + TT, :].rearrange("(g p) c -> p g c", p=128), tk[:, :, 0:256], reads=[btk], writes=[b_gkt])
            wt, bw = load_w8(w_in[l, :, O_AV:O_AV + 512], 512)
            tk, btk = tok_r.get()
            for g in range(4):
                ps, bps = psA.get()
                for k in range(8):
                    fw.op(PE, lambda h, k=k, g=g, ps=ps, wt=wt: h.matmul(ps[:, :], lhsT=hT[:, k, g * 128:(g + 1) * 128], rhs=wt[:, k, 0:512], start=(k == 0), stop=(k == 7)),
                          reads=[bw, bh], writes=[bps], inc=(k == 7))
                evac(ps[:, :], tk[:, g, :], bps, btk, eng=(ACT if g % 2 else DVE))
            fw.dma(SP, gvt_s[t0:t0 + TT, :].rearrange("(g p) c -> p g c", p=128), tk[:], reads=[btk], writes=[b_gvt])
            if KP1S < 3:
                return
            wt, bw = load_w8(w_in[l, :, O_AA:O_AA + 416], 416)
            ps, bps = proj_fm(wt, bw, 0, 32, rhs_h, [bh], 8)
            aab, baab = aab_r.get()
            evac(ps[0:32, :], aab[:, :], bps, baab, eng=DVE)
            gt, bgt = tokf_r.get()
            for g in range(4):
                ps, bps = psA.get()
                fw.op(PE, lambda h, g=g, ps=ps: h.matmul(ps[:, :], lhsT=aab[:, g * 128:(g + 1) * 128], rhs=wa2[:, :], start=True, stop=False),
                      reads=[baab, b_wa2], writes=[bps], inc=False)
                fw.op(PE, lambda h, ps=ps: h.matmul(ps[:, :], lhsT=ones_row[:, :], rhs=ba[:, :], start=False, stop=True), reads=[b_onesrow, b_ba], writes=[bps])
                e1, be1 = f32_r.get()
                fw.op(ACT, lambda h, ps=ps, e1=e1: h.activation(out=e1[:], in_=ps[:], func=AF.Exp, scale=-1.0), reads=[bps], writes=[be1])
                fw.op(ACT, lambda h, g=g, e1=e1: h.activation(out=gt[:, g, :], in_=e1[:], func=AF.Ln, bias=1.0), reads=[be1], writes=[bgt])
            fw.dma(SP, gG_s[t0:t0 + TT, :].rearrange("(g p) c -> p g c", p=128), gt[:], reads=[bgt], writes=[b_gG])

            if KP1S < 4:
                return
            wt_r, bw_r = load_w8(w_in[l, :, O_AR:O_AR + 512], 512)
            rr, brr = r4_r.get()
            for hh in range(4):
                ps_r, bps_r = proj_fm(wt_r, bw_r, hh * 128, 128, rhs_h, [bh], 8)
                fw.op(ACT, lambda h, hh=hh, ps_r=ps_r: h.activation(out=rr[:, hh, :], in_=ps_r[:], func=AF.Silu), reads=[bps_r], writes=[brr])
            fw.dma(SP, rs_s[:, :, t0:t0 + TT], rr[:], reads=[brr], writes=[b_rs])

            if KP1S < 5:
                return
            qdn, bqdn = qdn_r.get()
            qf = []
            for j in range(3):
                ps, bps = proj_fm(wt, bw, 32 + j * 128, 128, rhs_h, [bh], 8)
                xf, bxf = f32_r.get()
                evac(ps[:], xf[:], bps, bxf, eng=DVE)
                sq, bsq = bf_r.get()
                fw.op(ACT, lambda h, sq=sq, xf=xf: h.activation(out=sq[:], in_=xf[:], func=AF.Square), reads=[bxf], writes=[bsq])
                qf.append((xf, bxf, sq, bsq))
            rs, brs = rms_rstd([(q[2][:], q[3]) for q in qf], cmat[:, C_384, :], 128)
            for j in range(3):
                xf, bxf = qf[j][0], qf[j][1]
                fw.op(DVE, lambda h, j=j, xf=xf: h.scalar_tensor_tensor(out=qdn[:, j, :], in0=xf[:], scalar=gc[:, j:j + 1], in1=rs[:], op0=ALU.mult, op1=ALU.mult),
                      reads=[bxf, brs, b_gc], writes=[bqdn])
            for half in range(2):
                wq, bwq = w8.get()
                fw.dma(POOL, wq[:, 0:3, 0:384], w_uq[l, :, half * 4:(half + 1) * 4, :].rearrange("(k p) h c -> p k (h c)", p=128), writes=[bwq])
                for hh in range(4):
                    ps, bps = proj_fm(wq, bwq, hh * 96, 96, lambda k: qdn[:, k, :], [bqdn], 3)
                    head = half * 4 + hh
                    norm_rope_store(ps, bps, 96, cmat[0:96, C_96, 0:96], 5, lat, P_96, tabM, q_s[head, :, t0:t0 + TT], b_q)

            if KP1S < 6:
                return
            wt, bw = load_w8(w_in[l, :, O_KVD:O_KVD + 288], 288)
            ckv, bckv = ckv_r.get()
            ckvf, bckvf = ckvf_r.get()
            kf = []
            for j in range(2):
                ps, bps = proj_fm(wt, bw, j * 128, 128, rhs_h, [bh], 8)
                xf, bxf = f32_r.get()
                evac(ps[:], xf[:], bps, bxf, eng=DVE)
                sq, bsq = bf_r.get()
                fw.op(ACT, lambda h, sq=sq, xf=xf: h.activation(out=sq[:], in_=xf[:], func=AF.Square), reads=[bxf], writes=[bsq])
                kf.append((xf, bxf, sq, bsq))
            rs, brs = rms_rstd([(q[2][:], q[3]) for q in kf], cmat[:, C_256, :], 128)
            for j in range(2):
                xf, bxf = kf[j][0], kf[j][1]
                fw.op(DVE, lambda h, j=j, xf=xf: h.scalar_tensor_tensor(out=ckvf[:, j, :], in0=xf[:], scalar=gc[:, 3 + j:4 + j], in1=rs[:], op0=ALU.mult, op1=ALU.mult),
                      reads=[bxf, brs, b_gc], writes=[bckvf])
            fw.op(POOL, lambda h: h.tensor_copy(out=ckv[:], in_=ckvf[:]), reads=[bckvf], writes=[bckv])
            ps, bps = proj_fm(wt, bw, 256, 32, rhs_h, [bh], 8)
            krf, bkrf = krf_r.get()
            krb, bkrb = krb_r.get()
            evac(ps[0:32, :], krf[:, :], bps, bkrf, eng=DVE)
            fw.op(POOL, lambda h: h.tensor_copy(out=krb[:], in_=krf[:]), reads=[bkrf], writes=[bkrb])
            KSUB = _os.environ.get("KSUB", "abcd")
            if not lat:
                if "a" in KSUB:
                    transpose_out(lambda j: ckvf[:, j, :], [bckvf], 128, 256, lambda g: o_ckv[l, g * 128:(g + 1) * 128, :], [(0, 128, 0), (1, 128, 128)])
                if "b" in KSUB:
                    transpose_out(lambda j: krf[:, :], [bkrf], 32, 32, lambda g: o_kr[l, g * 128:(g + 1) * 128, :], [(0, 32, 0)])
            if "c" in KSUB:
                mla_keys(l, lambda k: ckv[:, k, :], [bckv], krb, bkrb, lat, tabM, kb)
            if "d" in KSUB:
                mla_values(l, lambda k, g: ckv[:, k, g * 128:(g + 1) * 128], [bckv], kb)

            if KP1S < 7:
                return
            for which, off, dst, bdst, gcolumn in ((0, O_DQ, dq_s, b_dq, 7), (1, O_DK, dk_s, b_dk, 8)):
                wt, bw = load_w8(w_in[l, :, off:off + 512], 512)
                keep = None
                if which == 1 and not lat:
                    kp, bkp = keep_r.get()
                for hh in range(4):
                    ps, bps = proj_fm(wt, bw, hh * 128, 128, rhs_h, [bh], 8)
                    kf32 = None
                    if which == 1 and not lat:
                        kf32 = lambda xn, bxn, hh=hh: fw.op(POOL, lambda h: h.tensor_copy(out=kp[:, hh, :], in_=xn[:]), reads=[bxn], writes=[bkp])
                    col0 = t0 if which == 0 else kb
                    norm_rope_store(ps, bps, 128, cmat[:, C_BLK64, :], gcolumn, lat, P_128, tabD, dst[hh, :, col0:col0 + TT], bdst, keep_f32=kf32)
                if which == 1 and not lat:
                    transpose_out(lambda j: kp[:, j, :], [bkp], 128, 512, lambda g: o_dk[l, g * 128:(g + 1) * 128, :], [(j, 128, j * 128) for j in range(4)])
            wt, bw = load_w8(w_in[l, :, O_DV:O_DV + 512], 512)
            tk, btk = tok_r.get()
            if not lat:
                tf, btf = tokf_r.get()
            for g in range(4):
                ps, bps = psA.get()
                for k in range(8):
                    fw.op(PE, lambda h, k=k, g=g, ps=ps, wt=wt: h.matmul(ps[:, :], lhsT=hT[:, k, g * 128:(g + 1) * 128], rhs=wt[:, k, 0:512], start=(k == 0), stop=(k == 7)),
                          reads=[bw, bh], writes=[bps], inc=(k == 7))
                evac(ps[:, :], tk[:, g, :], bps, btk, eng=ACT)
                if not lat:
                    evac(ps[:, :], tf[:, g, :], bps, btf, eng=DVE)
            for hh in range(4):
                fw.dma(SP, dv_s[hh, :, kb // 128:kb // 128 + 4, :], tk[:, :, hh * 128:(hh + 1) * 128], reads=[btk], writes=[b_dv])
            if not lat:
                ob = Buf(); out_bufs.append(ob)
                fw.dma(SP, o_dv[l, :, :].rearrange("(g p) c -> p g c", p=128), tf[:], reads=[btf], writes=[ob])

        def mla_keys(l, ckv_fn, ckv_bufs, krb, bkrb, rope, tabM, kb, ntok=TT):
            for half in range(2):
                wk, bwk = w8.get()
                fw.dma(POOL, wk[:, 0:2, 0:384], w_ukp[l, :, half * 4:(half + 1) * 4, :].rearrange("(k p) h c -> p k (h c)", p=128), writes=[bwk])
                for hh in range(4):
                    ps, bps = psA.get()
                    for k in range(2):
                        fw.op(PE, lambda h, k=k, hh=hh, ps=ps, wk=wk: h.matmul(ps[0:96, 0:ntok], lhsT=wk[:, k, hh * 96:(hh + 1) * 96], rhs=ckv_fn(k), start=(k == 0), stop=False),
                              reads=[bwk] + ckv_bufs, writes=[bps], inc=False)
                    fw.op(PE, lambda h, ps=ps: h.matmul(ps[0:96, 0:ntok], lhsT=cmat[0:32, C_SEL32, 0:96], rhs=krb[:, 0:ntok], start=False, stop=True),
                          reads=[b_cmat, bkrb], writes=[bps])
                    head = half * 4 + hh
                    if ntok == TT:
                        norm_rope_store(ps, bps, 96, cmat[0:96, C_96, 0:96], 6, rope, P_96, tabM, k_s[head, :, kb:kb + TT], b_k)
                    else:
                        norm_store_small(ps, bps, ntok, head, kb)

        def norm_store_small(ps, bps, ntok, head, kb):
            xf, bxf = f32_r.get()
            evac(ps[0:96, 0:ntok], xf[0:96, 0:ntok], bps, bxf, eng=DVE)
            sq, bsq = bf_r.get()
            fw.op(ACT, lambda h: h.activation(out=sq[0:96, 0:ntok], in_=xf[0:96, 0:ntok], func=AF.Square), reads=[bxf], writes=[bsq])
            ps2, bps2 = psA.get()
            fw.op(PE, lambda h: h.matmul(ps2[0:96, 0:ntok], lhsT=cmat[0:96, C_96, 0:96], rhs=sq[0:96, 0:ntok], start=True, stop=True), reads=[bsq, b_cmat], writes=[bps2])
            rs, brs = rs_r.get()
            rsqrt_eps(ps2[0:96, 0:ntok], rs[0:96, 0:ntok], bps2, brs)
            ob, bob = bf_r.get()
            fw.op(DVE, lambda h: h.scalar_tensor_tensor(out=ob[0:96, 0:ntok], in0=xf[0:96, 0:ntok], scalar=gc[0:96, 6:7], in1=rs[0:96, 0:ntok], op0=ALU.mult, op1=ALU.mult),
                  reads=[bxf, brs, b_gc], writes=[bob])
            fw.dma(SP, k_s[head, :, kb:kb + ntok], ob[0:96, 0:ntok], reads=[bob], writes=[b_k])

        vaug_init = [False, False]

        def mla_values(l, ckvT_fn, ckv_bufs, kb, ngrp=4):
            wv, bwv = w8.get()
            fw.dma(POOL, wv[:, 0:2, 0:512], w_uv[l].rearrange("(k p) c -> p k c", p=128), writes=[bwv])
            idx = vaug_r.i
            va, bva = vaug_r.get()
            if not vaug_init[idx]:
                vaug_init[idx] = True
                fw.op(POOL, lambda h: h.memset(va[:], 1.0), writes=[bva])
            for g in range(ngrp):
                ps, bps = psA.get()
                for k in range(2):
                    fw.op(PE, lambda h, k=k, g=g, ps=ps: h.matmul(ps[:, :], lhsT=ckvT_fn(k, g), rhs=wv[:, k, 0:512], start=(k == 0), stop=(k == 1)),
                          reads=[bwv] + ckv_bufs, writes=[bps], inc=(k == 1))
                fw.op(ACT if g % 2 else DVE, (lambda h, g=g, ps=ps: h.activation(out=va[:, :, g, 0:64], in_=ps[:, :].rearrange("p (h c) -> p h c", c=64), func=AF.Copy)) if g % 2
                      else (lambda h, g=g, ps=ps: h.tensor_copy(out=va[:, :, g, 0:64], in_=ps[:, :].rearrange("p (h c) -> p h c", c=64))), reads=[bps], writes=[bva])
            c0 = kb // 128
            fw.dma(SP, v_s[:, :, c0:c0 + ngrp, :].rearrange("h p g e -> p h (g e)"), va[:, :, 0:ngrp, :].rearrange("p h g e -> p h (g e)"), reads=[bva], writes=[b_v])


        def prep_cache(l):
            kb = NP_TOK
            ct, bct = ctok_r.get()
            fw.dma(SP, ct[:, :, 0:256], c_ckv[l].rearrange("(g p) c -> p g c", p=128), writes=[bct])
            ckv, bckv = ckv_r.get()
            for j in range(2):
                ps, bps = psA.get()
                for g in range(2):
                    fw.op(PE, lambda h, j=j, g=g, ps=ps: h.transpose(ps[:, g * 128:(g + 1) * 128], ct[:, g, j * 128:(j + 1) * 128], ident[:]),
                          reads=[bct, b_ident], writes=[bps], inc=(g == 1))
                evac(ps[:, 0:256], ckv[:, j, 0:256], bps, bckv, eng=DVE)
            ct2, bct2 = ctok_r.get()
            fw.dma(SP, ct2[:, :, 0:32], c_kr[l].rearrange("(g p) c -> p g c", p=128), writes=[bct2])
            krb, bkrb = krb_r.get()
            ps, bps = psA.get()
            for g in range(2):
                fw.op(PE, lambda h, g=g, ps=ps: h.transpose(ps[0:32, g * 128:(g + 1) * 128], ct2[:, g, 0:32], ident[:]), reads=[bct2, b_ident], writes=[bps], inc=(g == 1))
            evac(ps[0:32, 0:256], krb[:, 0:256], bps, bkrb, eng=DVE)
            mla_keys(l, lambda k: ckv[:, k, 0:256], [bckv], krb, bkrb, False, None, kb, ntok=256)
            mla_values(l, lambda k, g: ckv[:, k, g * 128:(g + 1) * 128], [bckv], kb, ngrp=2)
            ct, bct = ctok_r.get()
            fw.dma(SP, ct[:], c_dk[l].rearrange("(g p) c -> p g c", p=128), writes=[bct])
            for hh in range(4):
                ps, bps = psA.get()
                for g in range(2):
                    fw.op(PE, lambda h, hh=hh, g=g, ps=ps: h.transpose(ps[:, g * 128:(g + 1) * 128], ct[:, g, hh * 128:(hh + 1) * 128], ident[:]),
                          reads=[bct, b_ident], writes=[bps], inc=(g == 1))
                ob, bob = bf_r.get()
                evac(ps[:, 0:256], ob[:, 0:256], bps, bob, eng=DVE)
                fw.dma(SP, dk_s[hh, :, kb:kb + 256], ob[:, 0:256], reads=[bob], writes=[b_dk])
            ct3, bct3 = ctok_r.get()
            fw.dma(SP, ct3[:], c_dv[l].rearrange("(g p) c -> p g c", p=128), writes=[bct3])
            tkc, btkc = tok_r.get()
            fw.op(DVE, lambda h: h.tensor_copy(out=tkc[:, 0:2, :], in_=ct3[:]), reads=[bct3], writes=[btkc])
            for hh in range(4):
                fw.dma(SP, dv_s[hh, :, kb // 128:kb // 128 + 2, :], tkc[:, 0:2, hh * 128:(hh + 1) * 128], reads=[btkc], writes=[b_dv])

        arena.reset()
        gl_q = Ring(ov, "glq", [64, 4, TT], BF16, 1)
        gl_k = Ring(ov, "glk", [64, 4, TT], BF16, 1)
        gl_kt = Ring(ov, "glkt", [64, 8, 256], BF16, 1)
        gl_vt = Ring(ov, "glvt", [64, 8, 512], BF16, 1)
        gl_G = Ring(ov, "glG", [64, 8, 256], F32, 1)
        gl_o = Ring(ov, "glo", [128, 4, TT], F32, 1)
        gl_of = Ring(ov, "glof", [128, 4, TT], F32, 1)
        gl_r = Ring(ov, "glr", [128, 4, TT], BF16, 2)
        S_f = ov("S_f", [64, 4, 128], F32); S_b = ov("S_b", [64, 4, 128], BF16); b_S = Buf(); b_Sb = Buf()
        sm_r = Ring(ov, "sm", [64, 256], F32, 6)
        smb_r = Ring(ov, "smb", [64, 256], BF16, 8)
        dd_r = Ring(ov, "dd", [64, 4], F32, 3)

        def gla_dir(l, d, seqs):
            incl = tri[:, d, :]
            strict = tri[:, 2 + d, :]
            for (tok0, ntok, seq_idx, lat) in seqs:
                if lat:
                    fw.dma(SP, S_f[:], st_gla[l, d], reads=[b_Sb], writes=[b_S])
                else:
                    fw.op(DVE, lambda h: h.memset(S_f[:], 0.0), reads=[b_Sb], writes=[b_S])
                fw.op(POOL, lambda h: h.tensor_copy(out=S_b[:], in_=S_f[:]), reads=[b_S], writes=[b_Sb])
                tiles = list(range(tok0, tok0 + ntok, TT)) if ntok >= TT else [tok0]
                if d == 1:
                    tiles = tiles[::-1]
                for tt0 in tiles:
                    base_tile = (tt0 // TT) * TT
                    cq, bcq = gl_q.get(); ck, bck = gl_k.get(); ckt, bckt = gl_kt.get(); cvt, bcvt = gl_vt.get(); cG, bcG = gl_G.get()
                    fw.dma(SP, cq[:], gq_s[:, :, base_tile:base_tile + TT], reads=[b_gq], writes=[bcq])
                    fw.dma(SP, ck[:], gk_s[:, :, base_tile:base_tile + TT], reads=[b_gk], writes=[bck])
                    fw.dma(SP, ckt[:], gkt_s[base_tile:base_tile + TT, :].rearrange("(c p) f -> p c f", p=64), reads=[b_gkt], writes=[bckt])
                    fw.dma(SP, cvt[:], gvt_s[base_tile:base_tile + TT, :].rearrange("(c p) f -> p c f", p=64), reads=[b_gvt], writes=[bcvt])
                    fw.dma(SP, cG[:], gG_s[base_tile:base_tile + TT, d * 256:(d + 1) * 256].rearrange("(c p) f -> p c f", p=64), reads=[b_gG], writes=[bcG])
                    ot, bot = gl_o.get()
                    c_lo = (tt0 - base_tile) // 64
                    nch = min(ntok, TT) // 64
                    chunks = list(range(c_lo, c_lo + nch))
                    if d == 1:
                        chunks = chunks[::-1]
                    for c in chunks:
                        Gc = cG[:, c, :]
                        psb_, bpsb = psA.get()
                        for hh in range(4):
                            fw.op(PE, lambda h, hh=hh, Gc=Gc, p=psb_: h.matmul(p[0:64, hh * 64:(hh + 1) * 64], lhsT=Gc[:, hh * 64:(hh + 1) * 64], rhs=incl, start=True, stop=True),
                                  reads=[bcG, b_tri], writes=[bpsb], inc=(hh == 3))
                        ep, bep = sm_r.get(); en, ben = sm_r.get()
                        fw.op(ACT, lambda h, p=psb_, ep=ep: h.activation(out=ep[:, :], in_=p[0:64, 0:256], func=AF.Exp, scale=-1.0 / 16), reads=[bpsb], writes=[bep])
                        fw.op(ACT, lambda h, p=psb_, en=en: h.activation(out=en[:, :], in_=p[0:64, 0:256], func=AF.Exp, scale=1.0 / 16), reads=[bpsb], writes=[ben])
                        qt, bqt = smb_r.get(); kt_, bkt = smb_r.get()
                        fw.op(DVE, lambda h, c=c, qt=qt, ep=ep: h.scalar_tensor_tensor(out=qt[:, :].rearrange("p (h t) -> p h t", t=64), in0=cq[:, :, c * 64:(c + 1) * 64], scalar=0.125,
                                                                                      in1=ep[:, :].rearrange("p (h t) -> p h t", t=64), op0=ALU.mult, op1=ALU.mult),
                              reads=[bcq, bep], writes=[bqt])
                        fw.op(POOL, lambda h, c=c, kt_=kt_, en=en: h.tensor_tensor(out=kt_[:, :].rearrange("p (h t) -> p h t", t=64), in0=ck[:, :, c * 64:(c + 1) * 64],
                                                                                  in1=en[:, :].rearrange("p (h t) -> p h t", t=64), op=ALU.mult),
                              reads=[bck, ben], writes=[bkt])
                        ps2, bps2 = psA.get()
                        fw.op(PE, lambda h, Gc=Gc, p=ps2: h.matmul(p[0:64, 0:256], lhsT=strict, rhs=Gc, start=True, stop=True), reads=[bcG, b_tri], writes=[bps2])
                        e2, be2 = sm_r.get()
                        fw.op(ACT, lambda h, p=ps2, e2=e2: h.activation(out=e2[:, :], in_=p[0:64, 0:256], func=AF.Exp, scale=-1.0 / 16), reads=[bps2], writes=[be2])
                        kh, bkh = smb_r.get()
                        fw.op(POOL, lambda h, c=c, kh=kh, e2=e2: h.tensor_tensor(out=kh[:, :], in0=ckt[:, c, :], in1=e2[:, :], op=ALU.mult), reads=[bckt, be2], writes=[bkh])
                        ps3, bps3 = psA.get()
                        for hh in range(4):
                            fw.op(PE, lambda h, hh=hh, Gc=Gc, p=ps3: h.matmul(p[0:64, hh:hh + 1], lhsT=Gc[:, hh * 64:(hh + 1) * 64], rhs=ones_col[:, :], start=True, stop=True),
                                  reads=[bcG, b_onescol], writes=[bps3], inc=(hh == 3))
                        dd, bdd = dd_r.get()
                        fw.op(ACT, lambda h, p=ps3, dd=dd: h.activation(out=dd[:, :], in_=p[0:64, 0:4], func=AF.Exp, scale=-1.0 / 16), reads=[bps3], writes=[bdd])
                        ps4, bps4 = psA.get()
                        for hh in range(4):
                            fw.op(PE, lambda h, hh=hh, p=ps4, kt_=kt_, qt=qt: h.matmul(p[0:64, hh * 64:(hh + 1) * 64], lhsT=kt_[:, hh * 64:(hh + 1) * 64], rhs=qt[:, hh * 64:(hh + 1) * 64],
                                                                                     start=True, stop=True), reads=[bkt, bqt], writes=[bps4], inc=(hh == 3))
                        am, bam = smb_r.get()
                        fw.op(DVE, lambda h, p=ps4, am=am: h.tensor_tensor(out=am[:, :].rearrange("p (h t) -> p h t", t=64), in0=p[0:64, 0:256].rearrange("p (h t) -> p h t", t=64),
                                                                         in1=incl.unsqueeze(1).broadcast_to([64, 4, 64]), op=ALU.mult), reads=[bps4, b_tri], writes=[bam])
                        ps5, bps5 = psA.get()
                        for hh in range(4):
                            fw.op(PE, lambda h, hh=hh, c=c, p=ps5, am=am: h.matmul(p[:, hh * 64:(hh + 1) * 64], lhsT=cvt[:, c, hh * 128:(hh + 1) * 128], rhs=am[:, hh * 64:(hh + 1) * 64],
                                                                                 start=True, stop=False), reads=[bcvt, bam], writes=[bps5], inc=False)
                            fw.op(PE, lambda h, hh=hh, p=ps5, qt=qt: h.matmul(p[:, hh * 64:(hh + 1) * 64], lhsT=S_b[:, hh, :], rhs=qt[:, hh * 64:(hh + 1) * 64], start=False, stop=True),
                                  reads=[b_Sb, bqt], writes=[bps5], inc=(hh == 3))
                        fw.op(ACT, lambda h, c=c, p=ps5, ot=ot: h.activation(out=ot[:, :, c * 64:(c + 1) * 64], in_=p[:, 0:256].rearrange("p (h t) -> p h t", t=64), func=AF.Copy),
                              reads=[bps5], writes=[bot])
                        ps6, bps6 = psB.get()
                        for hh in range(4):
                            fw.op(PE, lambda h, hh=hh, c=c, p=ps6, kh=kh: h.matmul(p[0:64, hh * 128:(hh + 1) * 128], lhsT=kh[:, hh * 64:(hh + 1) * 64], rhs=cvt[:, c, hh * 128:(hh + 1) * 128],
                                                                                 start=True, stop=True), reads=[bkh, bcvt], writes=[bps6], inc=(hh == 3))
                        fw.op(DVE, lambda h, dd=dd: h.tensor_tensor(out=S_f[:], in0=S_f[:], in1=dd[:, :].unsqueeze(2).broadcast_to([64, 4, 128]), op=ALU.mult), reads=[bdd], writes=[b_S])
                        fw.op(DVE, lambda h, p=ps6: h.tensor_tensor(out=S_f[:], in0=S_f[:], in1=p[0:64, :].rearrange("p (h v) -> p h v", v=128), op=ALU.add), reads=[bps6], writes=[b_S])
                        fw.op(POOL, lambda h: h.tensor_copy(out=S_b[:], in_=S_f[:]), reads=[b_S], writes=[b_Sb])
                    cols = slice(tt0 - base_tile, tt0 - base_tile + min(ntok, TT))
                    ncol = min(ntok, TT)
                    if d == 0:
                        fw.dma(SP, go_s[:, :, tt0:tt0 + ncol], ot[:, :, cols], reads=[bot], writes=[b_go])
                    else:
                        of, bof = gl_of.get()
                        fw.dma(SP, of[:, :, 0:ncol], go_s[:, :, tt0:tt0 + ncol], reads=[b_go], writes=[bof])
                        rr, brr = gl_r.get()
                        fw.dma(SP, rr[:, :, 0:ncol], rs_s[:, :, tt0:tt0 + ncol], reads=[b_rs], writes=[brr])
                        fw.op(DVE, lambda h, ot=ot, of=of: h.tensor_tensor(out=of[:, :, 0:ncol], in0=of[:, :, 0:ncol], in1=ot[:, :, cols], op=ALU.add), reads=[bot], writes=[bof])
                        sq, bsq = sq_r.get()
                        fw.op(ACT, lambda h, sq=sq, of=of: h.activation(out=sq[:, 0:4, 0:ncol], in_=of[:, :, 0:ncol], func=AF.Square), reads=[bof], writes=[bsq])
                        ya, bya = gl_r.get()
                        for hh in range(4):
                            ps, bps = psA.get()
                            fw.op(PE, lambda h, hh=hh, ps=ps, sq=sq: h.matmul(ps[:, 0:ncol], lhsT=cmat[:, C_128, :], rhs=sq[:, hh, 0:ncol], start=True, stop=True), reads=[bsq, b_cmat], writes=[bps])
                            rs, brs = rs_r.get()
                            rsqrt_eps(ps[:, 0:ncol], rs[:, 0:ncol], bps, brs)
                            tmp, btmp = f32_r.get()
                            fw.op(DVE, lambda h, hh=hh, tmp=tmp, of=of, rs=rs: h.scalar_tensor_tensor(out=tmp[:, 0:ncol], in0=of[:, hh, 0:ncol], scalar=gc[:, 9:10], in1=rs[:, 0:ncol],
                                                                                                     op0=ALU.mult, op1=ALU.mult), reads=[bof, brs, b_gc], writes=[btmp])
                            fw.op(POOL, lambda h, hh=hh, tmp=tmp, ya=ya, rr=rr: h.tensor_tensor(out=ya[:, hh, 0:ncol], in0=tmp[:, 0:ncol], in1=rr[:, hh, 0:ncol], op=ALU.mult),
                                  reads=[btmp, brr], writes=[bya])
                        fw.dma(SP, ya_s[:, :, tt0:tt0 + ncol].rearrange("h p t -> p h t"), ya[:, :, 0:ncol], reads=[bya], writes=[b_ya])
                if seq_idx is not None:
                    ob = Buf(); out_bufs.append(ob)
                    fw.dma(SP, o_gla[l, seq_idx, d], S_f[:], reads=[b_S], writes=[ob])

        rs_s = dram_tmp("rs_s", [128, 4, T], BF16)
        b_rs = Buf()

        def phase1b(l, ti):
            cond = 1 if ti > 0 else 0
            xt, bxt = load_xT(ti)
            hT, bh = norm_mod(xt, bxt, A1, 0, cond)
            wt, bw = load_w8(w_in[l, :, O_AR:O_AR + 512], 512)
            rr, brr = gl_r.get()
            for hh in range(4):
                ps, bps = proj_fm(wt, bw, hh * 128, 128, lambda k: hT[:, k, :], [bh], 8)
                fw.op(ACT, lambda h, hh=hh, ps=ps: h.activation(out=rr[:, hh, :], in_=ps[:], func=AF.Silu), reads=[bps], writes=[brr])
            fw.dma(SP, rs_s[:, :, ti * TT:(ti + 1) * TT], rr[:], reads=[brr], writes=[b_rs])

        arena.reset()
        at_k = Ring(ov, "atk", [128, TKS], BF16, 2)
        at_v = Ring(ov, "atv", [128, TKS // 128, 128], BF16, 2)
        at_q = Ring(ov, "atq", [128, TT], BF16, 2)
        at_p = Ring(ov, "atp", [128, TT], BF16, 4)
        at_o = Ring(ov, "ato", [128, TT], BF16, 2)

        def mla_attn(l, groups):
            sc = 96 ** -0.5
            for (q0, nq, k0, nk) in groups:
                nkc = nk // 128
                for head in range(8):
                    kt, bkt = at_k.get(); vt, bvt = at_v.get()
                    fw.dma(SP, kt[0:96, 0:nk], k_s[head, :, k0:k0 + nk], reads=[b_k], writes=[bkt])
                    fw.dma(SP, vt[:, 0:nkc, :], v_s[head, :, k0 // 128:k0 // 128 + nkc, :], reads=[b_v], writes=[bvt])
                    for qq in range(q0, q0 + nq, TT):
                        nqt = min(TT, nq)
                        qt, bqt = at_q.get()
                        fw.dma(SP, qt[0:96, 0:nqt], q_s[head, :, qq:qq + nqt], reads=[b_q], writes=[bqt])
                        po, bpo = psB.get()
                        for c in range(nkc):
                            ps, bps = psA.get()
                            fw.op(PE, lambda h, c=c, ps=ps, kt=kt, qt=qt: h.matmul(ps[:, 0:nqt], lhsT=kt[0:96, c * 128:(c + 1) * 128], rhs=qt[0:96, 0:nqt], start=True, stop=True),
                                  reads=[bkt, bqt], writes=[bps])
                            pt, bpt = at_p.get()
                            fw.op(ACT, lambda h, ps=ps, pt=pt: h.activation(out=pt[:, 0:nqt], in_=ps[:, 0:nqt], func=AF.Exp, scale=sc), reads=[bps], writes=[bpt])
                            fw.op(PE, lambda h, c=c, po=po, vt=vt, pt=pt: h.matmul(po[:, 0:nqt], lhsT=vt[:, c, :], rhs=pt[:, 0:nqt], start=(c == 0), stop=(c == nkc - 1)),
                                  reads=[bvt, bpt], writes=[bpo], inc=(c == nkc - 1))
                        rc, brc = f32_r.get()
                        fw.op(DVE, lambda h, po=po, rc=rc: h.reciprocal(out=rc[64:128, 0:nqt], in_=po[64:128, 0:nqt]), reads=[bpo], writes=[brc])
                        r2, br2 = f32_r.get()
                        fw.op(DVE, lambda h, rc=rc, r2=r2: h.tensor_copy(out=r2[0:64, 0:nqt], in_=rc[64:128, 0:nqt]), reads=[brc], writes=[br2])
                        ob, bob = at_o.get()
                        fw.op(DVE, lambda h, po=po, r2=r2, ob=ob: h.tensor_tensor(out=ob[0:64, 0:nqt], in0=po[0:64, 0:nqt], in1=r2[0:64, 0:nqt], op=ALU.mult), reads=[bpo, br2], writes=[bob])
                        fw.dma(SP, yb_s[head // 2, (head % 2) * 64:(head % 2) * 64 + 64, qq:qq + nqt], ob[0:64, 0:nqt], reads=[bob], writes=[b_yb])

        def diff_attn(l, groups, lam_init):
            sc = 64 ** -0.5
            for (q0, nq, k0, nk) in groups:
                nkc = nk // 128
                for head in range(4):
                    kt, bkt = at_k.get(); vt, bvt = at_v.get()
                    fw.dma(SP, kt[:, 0:nk], dk_s[head, :, k0:k0 + nk], reads=[b_dk], writes=[bkt])
                    fw.dma(SP, vt[:, 0:nkc, :], dv_s[head, :, k0 // 128:k0 // 128 + nkc, :], reads=[b_dv], writes=[bvt])
                    for qq in range(q0, q0 + nq, TT):
                        nqt = min(TT, nq)
                        qt, bqt = at_q.get()
                        fw.dma(SP, qt[:, 0:nqt], dq_s[head, :, qq:qq + nqt], reads=[b_dq], writes=[bqt])
                        acc = [psB.get() for _ in range(4)]
                        for c in range(nkc):
                            for comp in range(2):
                                ps, bps = psA.get()
                                lo = comp * 64
                                fw.op(PE, lambda h, c=c, ps=ps, kt=kt, qt=qt, lo=lo: h.matmul(ps[:, 0:nqt], lhsT=kt[lo:lo + 64, c * 128:(c + 1) * 128], rhs=qt[lo:lo + 64, 0:nqt],
                                                                                            start=True, stop=True), reads=[bkt, bqt], writes=[bps])
                                pt, bpt = at_p.get()
                                fw.op(ACT, lambda h, ps=ps, pt=pt: h.activation(out=pt[:, 0:nqt], in_=ps[:, 0:nqt], func=AF.Exp, scale=sc), reads=[bps], writes=[bpt])
                                po, bpo = acc[comp]
                                pl, bpl = acc[2 + comp]
                                fw.op(PE, lambda h, c=c, po=po, vt=vt, pt=pt: h.matmul(po[:, 0:nqt], lhsT=vt[:, c, :], rhs=pt[:, 0:nqt], start=(c == 0), stop=(c == nkc - 1)),
                                      reads=[bvt, bpt], writes=[bpo], inc=(c == nkc - 1))
                                fw.op(PE, lambda h, c=c, pl=pl, pt=pt: h.matmul(pl[:, 0:nqt], lhsT=cmat[:, C_ONE, :], rhs=pt[:, 0:nqt], start=(c == 0), stop=(c == nkc - 1)),
                                      reads=[b_cmat, bpt], writes=[bpl], inc=(c == nkc - 1))
                        r0, br0 = f32_r.get(); r1, br1 = f32_r.get()
                        fw.op(DVE, lambda h, r0=r0, p=acc[2][0]: h.reciprocal(out=r0[:, 0:nqt], in_=p[:, 0:nqt]), reads=[acc[2][1]], writes=[br0])
                        fw.op(DVE, lambda h, r1=r1, p=acc[3][0]: h.reciprocal(out=r1[:, 0:nqt], in_=p[:, 0:nqt]), reads=[acc[3][1]], writes=[br1])
                        o0, bo0 = f32_r.get(); o1, bo1 = f32_r.get()
                        fw.op(DVE, lambda h, o0=o0, r0=r0, p=acc[0][0]: h.tensor_tensor(out=o0[:, 0:nqt], in0=p[:, 0:nqt], in1=r0[:, 0:nqt], op=ALU.mult), reads=[acc[0][1], br0], writes=[bo0])
                        fw.op(DVE, lambda h, o1=o1, r1=r1, p=acc[1][0]: h.tensor_tensor(out=o1[:, 0:nqt], in0=p[:, 0:nqt], in1=r1[:, 0:nqt], op=ALU.mult), reads=[acc[1][1], br1], writes=[bo1])
                        fw.op(DVE, lambda h, o0=o0, o1=o1: h.scalar_tensor_tensor(out=o0[:, 0:nqt], in0=o1[:, 0:nqt], scalar=lamc[:, 0:1], in1=o0[:, 0:nqt], op0=ALU.mult, op1=ALU.add),
                              reads=[bo1, b_lamc], writes=[bo0])
                        sq, bsq = bf_r.get()
                        fw.op(ACT, lambda h, sq=sq, o0=o0: h.activation(out=sq[:, 0:nqt], in_=o0[:, 0:nqt], func=AF.Square), reads=[bo0], writes=[bsq])
                        ps, bps = psA.get()
                        fw.op(PE, lambda h, ps=ps, sq=sq: h.matmul(ps[:, 0:nqt], lhsT=cmat[:, C_128, :], rhs=sq[:, 0:nqt], start=True, stop=True), reads=[bsq, b_cmat], writes=[bps])
                        rs, brs = rs_r.get()
                        rsqrt_eps(ps[:, 0:nqt], rs[:, 0:nqt], bps, brs)
                        t1, bt1 = f32_r.get()
                        fw.op(DVE, lambda h, t1=t1, o0=o0, rs=rs: h.scalar_tensor_tensor(out=t1[:, 0:nqt], in0=o0[:, 0:nqt], scalar=gc[:, 10:11], in1=rs[:, 0:nqt], op0=ALU.mult, op1=ALU.mult),
                              reads=[bo0, brs, b_gc], writes=[bt1])
                        ob, bob = at_o.get()
                        fw.op(ACT, lambda h, t1=t1, ob=ob: h.activation(out=ob[:, 0:nqt], in_=t1[:, 0:nqt], func=AF.Copy, scale=(1.0 - lam_init)), reads=[bt1], writes=[bob])
                        fw.dma(SP, yc_s[head, :, qq:qq + nqt], ob[:, 0:nqt], reads=[bob], writes=[b_yc])

        arena.reset()
        y3_r = Ring(ov, "y3", [128, 12, TT], BF16, 1)
        mg_r = Ring(ov, "mg", [128, 8, TT], BF16, 1)
        a_r = Ring(ov, "aT", [128, 32, TT], BF16, 1)
        xo_r = Ring(ov, "xo", [128, 4, D], F32, 1)

        def phase3(l, ti, last):
            cond = 1 if ti > 0 else 0
            t0 = ti * TT
            xt, bxt = load_xT(ti)
            hT, bh = norm_mod(xt, bxt, A1, 0, cond)
            y3, by3 = y3_r.get()
            for bi, (src, bsrc) in enumerate(((ya_s, b_ya), (yb_s, b_yb), (yc_s, b_yc))):
                fw.dma(SP, y3[:, bi * 4:(bi + 1) * 4, :], src[:, :, t0:t0 + TT].rearrange("h p t -> p h t"), reads=[bsrc], writes=[by3])
            mg, bmg = mg_r.get()
            for oc in range(8):
                wg, bwg = w8.get()
                for bi in range(3):
                    c0 = O_G + bi * 1024 + oc * 128
                    fw.dma(POOL, wg[:, :, bi * 128:(bi + 1) * 128], w_in[l, :, c0:c0 + 128].rearrange("(k p) c -> p k c", p=128), writes=[bwg])
                wo, bwo = w4.get()
                for bi in range(3):
                    fw.dma(POOL, wo[:, :, bi * 128:(bi + 1) * 128], w_o3[l, bi, :, oc * 128:(oc + 1) * 128].rearrange("(k p) c -> p k c", p=128), writes=[bwo])
                acc_t, bacc = f32_r.get()
                for bi in range(3):
                    psg, bpsg = proj_fm(wg, bwg, bi * 128, 128, lambda k: hT[:, k, :], [bh], 8)
                    sg, bsg = f32_r.get()
                    fw.op(ACT, lambda h, psg=psg, sg=sg: h.activation(out=sg[:], in_=psg[:], func=AF.Sigmoid), reads=[bpsg], writes=[bsg])
                    pso, bpso = proj_fm(wo, bwo, bi * 128, 128, lambda k, bi=bi: y3[:, bi * 4 + k, :], [by3], 4)
                    if bi == 0:
                        fw.op(DVE, lambda h, sg=sg, pso=pso, a=acc_t: h.tensor_tensor(out=a[:], in0=sg[:], in1=pso[:], op=ALU.mult), reads=[bsg, bpso], writes=[bacc])
                    else:
                        tmp, btmp = f32_r.get()
                        fw.op(DVE, lambda h, sg=sg, pso=pso, tmp=tmp: h.tensor_tensor(out=tmp[:], in0=sg[:], in1=pso[:], op=ALU.mult), reads=[bsg, bpso], writes=[btmp])
                        if bi == 1:
                            fw.op(POOL, lambda h, tmp=tmp, a=acc_t: h.tensor_tensor(out=a[:], in0=a[:], in1=tmp[:], op=ALU.add), reads=[btmp], writes=[bacc])
                        else:
                            fw.op(POOL, lambda h, tmp=tmp, a=acc_t, oc=oc: h.tensor_tensor(out=mg[:, oc, :], in0=a[:], in1=tmp[:], op=ALU.add), reads=[btmp, bacc], writes=[bmg])
            for half in range(2):
                wt, bw = load_w8(w_out[l, :, half * 512:(half + 1) * 512], 512)
                for j in range(4):
                    oc = half * 4 + j
                    ps, bps = proj_fm(wt, bw, j * 128, 128, lambda k: mg[:, k, :], [bmg], 8)
                    fw.op(DVE, lambda h, oc=oc, ps=ps: h.scalar_tensor_tensor(out=xt[:, oc, :], in0=ps[:], scalar=modT[:, 16 + oc, cond:cond + 1], in1=xt[:, oc, :], op0=ALU.mult, op1=ALU.add),
                          reads=[bps, b_modT_b], writes=[bxt])
            h2, bh2 = norm_mod(xt, bxt, A2, 24, cond)
            aT, baT = a_r.get()
            for fb in range(8):
                wt, bw = load_w8(w_m1[l, :, fb * 512:(fb + 1) * 512], 512)
                for j in range(4):
                    ps, bps = proj_fm(wt, bw, j * 128, 128, lambda k: h2[:, k, :], [bh2], 8)
                    rl, brl = f32_r.get()
                    fw.op(ACT, lambda h, ps=ps, rl=rl: h.activation(out=rl[:], in_=ps[:], func=AF.Relu), reads=[bps], writes=[brl])
                    fw.op(DVE if j % 2 else POOL, lambda h, rl=rl, fb=fb, j=j: h.tensor_tensor(out=aT[:, fb * 4 + j, :], in0=rl[:], in1=rl[:], op=ALU.mult), reads=[brl], writes=[baT])
            for oc in range(8):
                ps, bps = psB.get()
                for kb4 in range(8):
                    wt, bw = w4.get()
                    fw.dma(POOL, wt[:, :, 0:128], w_m2[l, kb4 * 512:(kb4 + 1) * 512, oc * 128:(oc + 1) * 128].rearrange("(k p) c -> p k c", p=128), writes=[bw])
                    for k in range(4):
                        kk = kb4 * 4 + k
                        fw.op(PE, lambda h, k=k, kk=kk, ps=ps, wt=wt: h.matmul(ps[:], lhsT=wt[:, k, 0:128], rhs=aT[:, kk, :], start=(kk == 0), stop=(kk == 31)),
                              reads=[bw, baT], writes=[bps], inc=(k == 3))
                fw.op(DVE, lambda h, oc=oc, ps=ps: h.scalar_tensor_tensor(out=xt[:, oc, :], in0=ps[:], scalar=modT[:, 40 + oc, cond:cond + 1], in1=xt[:, oc, :], op0=ALU.mult, op1=ALU.add),
                      reads=[bps, b_modT_b], writes=[bxt])
            if not last:
                fw.dma(SP, xT_s[:, :, t0:t0 + TT].rearrange("k p t -> p k t"), xt[:], reads=[bxt], writes=[b_xT[ti]])
            else:
                xo, bxo = xo_r.get()
                for g in range(4):
                    for half in range(2):
                        ps, bps = psA.get()
                        for k in range(4):
                            kk = half * 4 + k
                            fw.op(PE, lambda h, g=g, k=k, kk=kk, ps=ps: h.transpose(ps[:, k * 128:(k + 1) * 128], xt[:, kk, g * 128:(g + 1) * 128], ident[:]),
                                  reads=[bxt, b_ident], writes=[bps], inc=(k == 3))
                        evac(ps[:], xo[:, g, half * 512:(half + 1) * 512], bps, bxo, eng=(ACT if half else DVE))
                ob = Buf(); out_bufs.append(ob)
                fw.dma(SP, y_out[t0:t0 + TT, :].rearrange("(g p) f -> p g f", p=128), xo[:], reads=[bxo], writes=[ob])

        import os
        KSTOP = int(os.environ.get("KSTOP", "99"))
        step = [0]

        def reached():
            step[0] += 1
            return step[0] > KSTOP

        for l in range(L):
            if reached():
                break
            lam_init = layer_setup(l)
            fw.barrier()
            vaug_init[0] = vaug_init[1] = False
            if reached():
                break
            prep_cache(l)
            if reached():
                break
            for ti in range(NT):
                phase1(l, ti)
            fw.barrier()
            if reached():
                break
            seqs = [(0, 256, 0, False), (256, 256, 1, False), (NP_TOK, NS_TOK, None, True)]
            gla_dir(l, 0, seqs)
            fw.barrier()
            gla_dir(l, 1, seqs)
            fw.barrier()
            if reached():
                break
            groups = [(0, 256, 0, 256), (256, 256, 256, 256), (NP_TOK, NS_TOK, NP_TOK, TKS)]
            mla_attn(l, groups)
            if reached():
                break
            diff_attn(l, groups, lam_init)
            if reached():
                break
            fw.barrier()
            for ti in range(NT):
                phase3(l, ti, l == L - 1)
            fw.barrier()

        fw.finish(out_bufs)
    return nc


def _rope_tables():
    t = np.arange(NS_TOK)
    row, col = t // 64, t % 64

    def cs(nf, pos):
        inv = (10000.0 ** (-np.arange(nf, dtype=np.float32) / nf)).astype(np.float32)
        ang = pos.astype(np.float32)[None, :] * inv[:, None]
        c, s = np.cos(ang).astype(np.float32), np.sin(ang).astype(np.float32)
        return np.concatenate([c, c], 0), np.concatenate([s, s], 0)

    cr, sr = cs(8, row); cc, sc_ = cs(8, col)
    cM = np.concatenate([cr, cc, np.ones((64, NS_TOK), np.float32)], 0)
    sM = np.concatenate([sr, sc_, np.zeros((64, NS_TOK), np.float32)], 0)
    cr, sr = cs(16, row); cc, sc_ = cs(16, col)
    c64 = np.concatenate([cr, cc], 0); s64 = np.concatenate([sr, sc_], 0)
    cD = np.concatenate([c64, c64], 0); sD = np.concatenate([s64, s64], 0)
    return np.stack([cM, sM]).astype(np.float32), np.stack([cD, sD]).astype(np.float32)


def _rot_matrix(nf):
    R = np.zeros((2 * nf, 2 * nf), np.float32)
    for i in range(nf):
        R[i, nf + i] = -1.0
        R[nf + i, i] = 1.0
    return R


def _consts():
    bf = ml_dtypes.bfloat16
    cmat = np.zeros((128, 8, 128), np.float32)
    cmat[:, 0, :] = 1.0 / 1024
    cmat[:, 1, :] = 1.0 / 384
    cmat[:, 2, :] = 1.0 / 256
    cmat[:96, 3, :96] = 1.0 / 96
    cmat[:64, 4, :64] = 1.0 / 64
    cmat[64:, 4, 64:] = 1.0 / 64
    cmat[:, 5, :] = 1.0 / 128
    cmat[:, 6, :] = 1.0
    cmat[:32, 7, :32] = np.eye(32)
    pm = np.zeros((128, 3, 128), np.float32)
    R96 = np.zeros((96, 96), np.float32)
    R96[0:16, 0:16] = _rot_matrix(8); R96[16:32, 16:32] = _rot_matrix(8)
    pm[:96, 0, :96] = R96.T
    R64 = np.zeros((64, 64), np.float32)
    R64[0:32, 0:32] = _rot_matrix(16); R64[32:64, 32:64] = _rot_matrix(16)
    R128 = np.zeros((128, 128), np.float32)
    R128[:64, :64] = R64; R128[64:, 64:] = R64
    pm[:, 1, :] = R128.T
    i = np.arange(64)
    tri = np.zeros((64, 4, 64), np.float32)
    tri[:, 0, :] = (i[:, None] <= i[None, :])
    tri[:, 1, :] = (i[:, None] >= i[None, :])
    tri[:, 2, :] = (i[:, None] > i[None, :])
    tri[:, 3, :] = (i[:, None] < i[None, :])
    return cmat.astype(bf), pm.astype(bf), tri


_PROG = {}


def _get_prog():
    if "nc" not in _PROG:
        _PROG["nc"] = build_program()
    return _PROG["nc"]


def make_in_maps(x_prompt, x_sample, state_gla, cache_mla_ckv, cache_mla_krope, cache_diff_k, cache_diff_v, c, c_ctx, w_mod, b_mod, g_norm1, g_norm2, w_in,
                 w_gla_a2, b_gla_a, g_gla_out, g_mla_qa, g_mla_kva, w_mla_uq, w_mla_uk, w_mla_uv, g_mla_q, g_mla_k, g_diff_q, g_diff_k, lam_qk, g_diff_sub,
                 w_o_gla, w_o_mla, w_o_diff, w_out, w_mlp1, w_mlp2):
    f = lambda a: np.ascontiguousarray(np.asarray(a, dtype=np.float32))
    cmat, pm, tri = _consts()
    ropeM, ropeD = _rope_tables()
    perm = np.concatenate([np.arange(64, 96), np.arange(0, 64)])
    w_uq_p = f(w_mla_uq).reshape(L, 384, 8, 96)[:, :, :, perm]
    w_ukp = np.zeros((L, 256, 8, 96), np.float32)
    w_ukp[:, :, :, 32:] = f(w_mla_uk).reshape(L, 256, 8, 64)
    gcol = np.zeros((L, 128, 12), np.float32)
    gcol[:, :, 0:3] = f(g_mla_qa).reshape(L, 3, 128).transpose(0, 2, 1)
    gcol[:, :, 3:5] = f(g_mla_kva).reshape(L, 2, 128).transpose(0, 2, 1)
    gcol[:, :96, 5] = f(g_mla_q)[:, perm]
    gcol[:, :96, 6] = f(g_mla_k)[:, perm]
    gcol[:, :, 7] = np.tile(f(g_diff_q), (1, 2))
    gcol[:, :, 8] = np.tile(f(g_diff_k), (1, 2))
    gcol[:, :, 9] = f(g_gla_out)
    gcol[:, :, 10] = f(g_diff_sub)
    w_a2bd = np.zeros((L, 32, 512), np.float32)
    w_a2bd[:, 0:16, 0:256] = f(w_gla_a2)[:, 0]
    w_a2bd[:, 16:32, 256:512] = f(w_gla_a2)[:, 1]
    b_a = f(b_gla_a).reshape(L, 1, 512)
    gnT = np.concatenate([f(g_norm1).reshape(L, 8, 128).transpose(0, 2, 1), f(g_norm2).reshape(L, 8, 128).transpose(0, 2, 1)], axis=2)
    b_modT = f(b_mod).reshape(L, 48, 128).transpose(0, 2, 1)
    w_o3 = np.stack([f(w_o_gla), f(w_o_mla), f(w_o_diff)], axis=1)
    shared = dict(w_mod=f(w_mod), b_modT=f(b_modT), gnT=f(gnT), w_in=f(w_in), w_a2bd=w_a2bd, b_a=b_a, gcol=gcol, w_uq=f(w_uq_p), w_ukp=w_ukp,
                  w_uv=f(w_mla_uv), lam_qk=f(lam_qk).reshape(L, 1, 256), w_o3=f(w_o3), w_out=f(w_out), w_m1=f(w_mlp1), w_m2=f(w_mlp2),
                  ident=np.eye(128, dtype=np.float32), cmat=cmat, pmat=pm, tri=tri, ropeM=ropeM, ropeD=ropeD)
    xp, xs = f(x_prompt), f(x_sample)
    in_maps = []
    for core in range(8):
        b = core % 4
        m = dict(shared)
        m["xin"] = np.concatenate([xp[2 * core], xp[2 * core + 1], xs[b]], axis=0)
        cond = np.stack([f(c_ctx), f(c)[b]], axis=0)
        m["condT"] = f(cond.reshape(2, 8, 128).transpose(2, 1, 0))
        m["st_gla"] = f(f(state_gla)[b].transpose(0, 1, 3, 2, 4))
        m["c_ckv"] = f(cache_mla_ckv)[b]
        m["c_kr"] = f(cache_mla_krope)[b]
        m["c_dk"] = f(cache_diff_k)[b].reshape(L, PAST, 512)
        m["c_dv"] = f(cache_diff_v)[b].reshape(L, PAST, 512)
        in_maps.append(m)
    return in_maps


def kernel(**inputs):
    nc = _get_prog()
    in_maps = make_in_maps(**inputs)
    res = run_bass_kernel_spmd(nc, in_maps, core_ids=list(range(8)))
    R = res.results
    y_prompt = np.zeros((16, 256, D), np.float32)
    y_sample = np.zeros((4, NS_TOK, D), np.float32)
    n_gla = np.zeros((16, L, 2, 4, 64, 128), np.float32)
    n_ckv = np.zeros((16, L, 256, 256), np.float32)
    n_kr = np.zeros((16, L, 256, 32), np.float32)
    n_dk = np.zeros((16, L, 256, 4, 2, 64), np.float32)
    n_dv = np.zeros((16, L, 256, 4, 128), np.float32)
    for core in range(8):
        r = R[core]
        for s in range(2):
            bi = 2 * core + s
            y_prompt[bi] = r["y_out"][s * 256:(s + 1) * 256]
            n_gla[bi] = r["o_gla"][:, s].transpose(0, 1, 3, 2, 4)
            n_ckv[bi] = r["o_ckv"][:, s * 256:(s + 1) * 256]
            n_kr[bi] = r["o_kr"][:, s * 256:(s + 1) * 256]
            n_dk[bi] = r["o_dk"][:, s * 256:(s + 1) * 256].reshape(L, 256, 4, 2, 64)
            n_dv[bi] = r["o_dv"][:, s * 256:(s + 1) * 256].reshape(L, 256, 4, 128)
        if core < 4:
            y_sample[core] = r["y_out"][NP_TOK:]
    return (y_prompt, y_sample, n_gla, n_ckv, n_kr, n_dk, n_dv)
```

```python
import math
import numpy as np
import ml_dtypes
from contextlib import ExitStack
import concourse.bass as bass
import concourse.mybir as mybir
from concourse.bass_utils import run_bass_kernel_spmd

F32 = mybir.dt.float32
BF16 = mybir.dt.bfloat16
AF = mybir.ActivationFunctionType
ALU = mybir.AluOpType

D = 1024
L = 2
NP_TOK = 512
NS_TOK = 4096
T = NP_TOK + NS_TOK
TT = 512
NT = T // TT
PAST = 256
TKS = PAST + NS_TOK
EPS = 1e-6
O_AQ, O_AK, O_AV, O_AR, O_AA, O_QD, O_KVD, O_KR, O_DQ, O_DK, O_DV, O_G = 0, 256, 512, 1024, 1536, 1568, 1952, 2208, 2240, 2752, 3264, 3776
IN_COLS = 6848


class Buf:
    __slots__ = ("lw", "rd", "psum")

    def __init__(self, psum=False):
        self.lw = None
        self.rd = []
        self.psum = psum


class _Rec:
    def __init__(self):
        self.call = None

    def __getattr__(self, name):
        def f(*a, **k):
            self.call = (name, a, k)
            return self
        return f


class Eng:
    def __init__(self, fw, name, handle):
        self.name = name
        self.h = handle
        self.sem = fw.new_sem("e_" + name)
        self.count = 0
        self.waited = {}
        self.thunks = []


class FW:
    def __init__(self, nc, ctx, n_dma_sems=48):
        self.nc = nc
        self.ctx = ctx
        self.pe = Eng(self, "pe", nc.tensor)
        self.dve = Eng(self, "dve", nc.vector)
        self.act = Eng(self, "act", nc.scalar)
        self.pool = Eng(self, "pool", nc.gpsimd)
        self.sp = Eng(self, "sp", nc.sync)
        self.dma_sems = [self.new_sem(f"d{i}") for i in range(n_dma_sems)]
        self.dma_cnt = [0] * n_dma_sems
        self.dma_rr = 0
        self.ew_rr = 0

    def new_sem(self, name):
        return self.ctx.enter_context(self.nc.semaphore(name))

    def _wait(self, E, ev):
        if ev is None:
            return
        sem, val = ev
        if sem is E.sem and E.name == "pe":
            return
        key = id(sem)
        if E.waited.get(key, 0) >= val:
            return
        E.waited[key] = val
        E.thunks.append(lambda h=E.h, s=sem, v=val: h.wait_ge(s, v))

    def _deps(self, E, reads, writes):
        for b in reads:
            self._wait(E, b.lw)
        for b in writes:
            self._wait(E, b.lw)
            for ev in b.rd:
                self._wait(E, ev)

    def _post(self, ev, reads, writes):
        for b in reads:
            b.rd.append(ev)
            if len(b.rd) > 64:
                b.rd = b.rd[-48:]
        for b in writes:
            b.lw = ev
            b.rd = []

    def op(self, E, fn, reads=(), writes=(), inc=True):
        if any(b.psum for b in reads):
            writes = list(writes) + [b for b in reads if b.psum]
            reads = [b for b in reads if not b.psum]
        self._deps(E, reads, writes)
        rec = _Rec()
        fn(rec)
        mname, margs, mkw = rec.call
        if inc:
            E.count += 1
            ev = (E.sem, E.count)
            E.thunks.append(lambda h=E.h, n=mname, a=margs, k=mkw, s=E.sem: getattr(h, n)(*a, **k).then_inc(s, 1))
        else:
            ev = (E.sem, E.count + 1)
            E.thunks.append(lambda h=E.h, n=mname, a=margs, k=mkw: getattr(h, n)(*a, **k))
        self._post(ev, reads, writes)
        return ev

    def dma(self, Q, out_ap, in_ap, reads=(), writes=(), **kw):
        self._deps(Q, reads, writes)
        i = self.dma_rr
        self.dma_rr = (self.dma_rr + 1) % len(self.dma_sems)
        self.dma_cnt[i] += 16
        sem = self.dma_sems[i]
        ev = (sem, self.dma_cnt[i])
        Q.thunks.append(lambda h=Q.h, o=out_ap, a=in_ap, s=sem, k=kw: h.dma_start(out=o, in_=a, **k).then_inc(s, 16))
        self._post(ev, reads, writes)
        return ev

    def barrier(self):
        engs = [self.pe, self.dve, self.act, self.pool, self.sp]
        for E in engs:
            for E2 in engs:
                if E2 is not E and E2.count > 0:
                    self._wait(E, (E2.sem, E2.count))
            for i, sem in enumerate(self.dma_sems):
                if self.dma_cnt[i] > 0:
                    self._wait(E, (sem, self.dma_cnt[i]))

    def finish(self, final_bufs):
        for b in final_bufs:
            self._wait(self.sp, b.lw)
        nc = self.nc
        engs = self
        with nc.Block() as block:
            @block.tensor
            def _(e):
                for t in engs.pe.thunks:
                    t()

            @block.vector
            def _(e):
                for t in engs.dve.thunks:
                    t()

            @block.scalar
            def _(e):
                for t in engs.act.thunks:
                    t()

            @block.gpsimd
            def _(e):
                for t in engs.pool.thunks:
                    t()

            @block.sync
            def _(e):
                for t in engs.sp.thunks:
                    t()


class Arena:
    def __init__(self, tensor, nbytes):
        self.t = tensor
        self.nbytes = nbytes
        self.off = 0
        self.peak = 0

    def reset(self):
        self.off = 0

    def alloc(self, name, shape, dt):
        esz = 2 if dt == BF16 else 4
        n = 1
        for d in shape[1:]:
            n *= d
        nb = (n * esz + 63) // 64 * 64
        assert self.off + nb <= self.nbytes, (name, self.off, nb, self.nbytes)
        ap = self.t[0:shape[0], self.off // 4:(self.off + nb) // 4]
        self.off += nb
        self.peak = max(self.peak, self.off)
        if dt == BF16:
            ap = ap.bitcast(BF16)
        ap = ap[:, 0:n]
        if len(shape) == 3:
            ap = ap.rearrange("p (a b) -> p a b", b=shape[2])
        elif len(shape) == 4:
            ap = ap.rearrange("p (a b c) -> p a b c", b=shape[2], c=shape[3])
        return ap


class Ring:
    def __init__(self, alloc, name, shape, dt, n, psum=False):
        self.t = [alloc(f"{name}{i}", shape, dt) for i in range(n)]
        self.b = [Buf(psum) for _ in range(n)]
        self.i = 0

    def get(self):
        i = self.i
        self.i = (i + 1) % len(self.t)
        return self.t[i], self.b[i]


def build_program(debug=False):
    nc = bass.Bass("TRN2", target_bir_lowering=False)
    dram_in = lambda name, shape, dt=F32: nc.dram_tensor(name, list(shape), dt, kind="ExternalInput").ap()
    dram_out = lambda name, shape, dt=F32: nc.dram_tensor(name, list(shape), dt, kind="ExternalOutput").ap()
    dram_tmp = lambda name, shape, dt=F32: nc.dram_tensor(name, list(shape), dt).ap()

    xin = dram_in("xin", [T, D])
    condT = dram_in("condT", [128, 8, 2])
    st_gla = dram_in("st_gla", [L, 2, 64, 4, 128])
    c_ckv = dram_in("c_ckv", [L, PAST, 256])
    c_kr = dram_in("c_kr", [L, PAST, 32])
    c_dk = dram_in("c_dk", [L, PAST, 512])
    c_dv = dram_in("c_dv", [L, PAST, 512])
    w_mod = dram_in("w_mod", [L, D, 6 * D])
    b_modT = dram_in("b_modT", [L, 128, 48])
    gnT = dram_in("gnT", [L, 128, 16])
    w_in = dram_in("w_in", [L, D, IN_COLS])
    w_a2bd = dram_in("w_a2bd", [L, 32, 512])
    b_a = dram_in("b_a", [L, 1, 512])
    gcol = dram_in("gcol", [L, 128, 12])
    w_uq = dram_in("w_uq", [L, 384, 8, 96])
    w_ukp = dram_in("w_ukp", [L, 256, 8, 96])
    w_uv = dram_in("w_uv", [L, 256, 512])
    lam_qk = dram_in("lam_qk", [L, 1, 256])
    w_o3 = dram_in("w_o3", [L, 3, 512, D])
    w_out = dram_in("w_out", [L, D, D])
    w_m1 = dram_in("w_m1", [L, D, 4 * D])
    w_m2 = dram_in("w_m2", [L, 4 * D, D])
    ident_d = dram_in("ident", [128, 128])
    cmat_d = dram_in("cmat", [128, 8, 128], BF16)
    pmat_d = dram_in("pmat", [128, 3, 128], BF16)
    tri_d = dram_in("tri", [64, 4, 64])
    ropeM = dram_in("ropeM", [2, 96, NS_TOK])
    ropeD = dram_in("ropeD", [2, 128, NS_TOK])

    y_out = dram_out("y_out", [T, D])
    o_gla = dram_out("o_gla", [L, 2, 2, 64, 4, 128])
    o_ckv = dram_out("o_ckv", [L, NP_TOK, 256])
    o_kr = dram_out("o_kr", [L, NP_TOK, 32])
    o_dk = dram_out("o_dk", [L, NP_TOK, 512])
    o_dv = dram_out("o_dv", [L, NP_TOK, 512])

    mk = dram_out if debug else dram_tmp
    xT_s = dram_tmp("xT_s", [8, 128, T])
    gq_s = dram_tmp("gq_s", [64, 4, T], BF16)
    gk_s = dram_tmp("gk_s", [64, 4, T], BF16)
    gkt_s = dram_tmp("gkt_s", [T, 256], BF16)
    gvt_s = dram_tmp("gvt_s", [T, 512], BF16)
    gG_s = dram_tmp("gG_s", [T, 512])
    go_s = dram_tmp("go_s", [128, 4, T])
    q_s = dram_tmp("q_s", [8, 96, T], BF16)
    k_s = dram_tmp("k_s", [8, 96, NP_TOK + TKS], BF16)
    v_s = dram_tmp("v_s", [8, 128, (NP_TOK + TKS) // 128, 128], BF16)
    dq_s = dram_tmp("dq_s", [4, 128, T], BF16)
    dk_s = dram_tmp("dk_s", [4, 128, NP_TOK + TKS], BF16)
    dv_s = dram_tmp("dv_s", [4, 128, (NP_TOK + TKS) // 128, 128], BF16)
    ya_s = mk("ya_s", [4, 128, T], BF16)
    yb_s = mk("yb_s", [4, 128, T], BF16)
    yc_s = mk("yc_s", [4, 128, T], BF16)

    with ExitStack() as ctx:
        fw = FW(nc, ctx)
        PE, DVE, ACT, POOL, SP = fw.pe, fw.dve, fw.act, fw.pool, fw.sp
        sb = lambda name, shape, dt=F32: ctx.enter_context(nc.sbuf_tensor("s_" + name, list(shape), dt))
        psb = lambda name, shape, dt=F32: ctx.enter_context(nc.psum_tensor("p_" + name, list(shape), dt))

        psA = Ring(psb, "psA", [128, 512], F32, 4, psum=True)
        psB = Ring(psb, "psB", [128, 512], F32, 4, psum=True)

        OVB = 72 * 1024
        arena = Arena(sb("arena", [128, OVB // 4], F32), OVB)
        ov = arena.alloc

        def ew():
            fw.ew_rr ^= 1
            return DVE if fw.ew_rr else POOL

        ident = sb("ident", [128, 128]); b_ident = Buf()
        cmat = sb("cmat", [128, 8, 128], BF16); b_cmat = Buf()
        pmat = sb("pmat", [128, 3, 128], BF16); b_pmat = Buf()
        tri = sb("tri", [64, 4, 64]); b_tri = Buf()
        ones_row = sb("ones_row", [1, 128], BF16); b_onesrow = Buf()
        ones_rowf = sb("ones_rowf", [1, 128]); b_onesrowf = Buf()
        ones_col = sb("ones_col", [64, 1]); b_onescol = Buf()
        fw.dma(SP, ident[:], ident_d, writes=[b_ident])
        fw.dma(SP, cmat[:], cmat_d, writes=[b_cmat])
        fw.dma(SP, pmat[:], pmat_d, writes=[b_pmat])
        fw.dma(SP, tri[:], tri_d, writes=[b_tri])
        fw.op(DVE, lambda h: h.memset(ones_row[:], 1.0), writes=[b_onesrow])
        fw.op(DVE, lambda h: h.memset(ones_rowf[:], 1.0), writes=[b_onesrowf])
        fw.op(DVE, lambda h: h.memset(ones_col[:], 1.0), writes=[b_onescol])
        C_1024, C_384, C_256, C_96, C_BLK64, C_128, C_ONE, C_SEL32 = range(8)
        P_96, P_128, _ = range(3)

        cT = sb("cT", [128, 8, 2]); b_cT = Buf()
        scT = sb("scT", [128, 8, 2]); b_scT = Buf()
        fw.dma(SP, cT[:], condT, writes=[b_cT])
        fw.op(ACT, lambda h: h.activation(out=scT[:], in_=cT[:], func=AF.Silu), reads=[b_cT], writes=[b_scT])

        modT = sb("modT", [128, 48, 2]); b_modT_b = Buf()
        bmod = sb("bmod", [128, 48]); b_bmod = Buf()
        gn = sb("gn", [128, 16]); b_gn = Buf()
        A1 = sb("A1", [128, 8, 2]); A2 = sb("A2", [128, 8, 2]); b_A = Buf()
        gc = sb("gc", [128, 12]); b_gc = Buf()
        wa2 = sb("wa2", [32, 512], BF16); b_wa2 = Buf()
        ba = sb("ba", [1, 512], BF16); b_ba = Buf()
        lamt = sb("lamt", [1, 256]); b_lamt = Buf()
        lam1 = sb("lam1", [1, 8]); b_lam1 = Buf()
        lamc = sb("lamc", [128, 2]); b_lamc = Buf()
        arena.reset()
        wmod_r = Ring(ov, "wmod", [128, 8, 256], F32, 2)

        w8 = Ring(sb, "w8", [128, 8, 512], BF16, 3)
        w4 = Ring(sb, "w4", [128, 4, 1024], BF16, 2)

        def load_w8(src_ap_rows_cols, ncols, nk=8):
            t, b = w8.get()
            fw.dma(POOL, t[:, 0:nk, 0:ncols], src_ap_rows_cols.rearrange("(k p) c -> p k c", p=128), writes=[b])
            return t, b

        xt_r = Ring(sb, "xt", [128, 8, TT], F32, 1)
        sq_r = Ring(sb, "sq", [128, 8, TT], BF16, 1)
        hT_r = Ring(sb, "hT", [128, 8, TT], BF16, 2)
        rs_r = Ring(sb, "rs", [128, TT], F32, 3)
        f32_r = Ring(sb, "f32", [128, TT], F32, 6)
        bf_r = Ring(sb, "bf", [128, TT], BF16, 8)

        def evac(ps_ap, out_ap, b_ps, b_out, eng=None, func=None, scale=1.0):
            if func is not None or eng is ACT:
                f = func if func is not None else AF.Copy
                return fw.op(ACT, lambda h: h.activation(out=out_ap, in_=ps_ap, func=f, scale=scale), reads=[b_ps], writes=[b_out])
            return fw.op(DVE, lambda h: h.tensor_copy(out=out_ap, in_=ps_ap), reads=[b_ps], writes=[b_out])

        b_xT = [Buf() for _ in range(NT)]
        arena.reset()
        xtok_r = Ring(ov, "xtok", [128, 4, D], F32, 2)

        def phase0(ti):
            xk, bxk = xtok_r.get()
            fw.dma(SP, xk[:], xin[ti * TT:(ti + 1) * TT, :].rearrange("(g p) f -> p g f", p=128), writes=[bxk])
            xt, bxt = xt_r.get()
            for k in range(8):
                ps, bps = psA.get()
                for g in range(4):
                    fw.op(PE, lambda h, ps=ps, g=g, k=k: h.transpose(ps[:, g * 128:(g + 1) * 128], xk[:, g, k * 128:(k + 1) * 128], ident[:]),
                          reads=[bxk, b_ident], writes=[bps], inc=(g == 3))
                evac(ps[:], xt[:, k, :], bps, bxt, eng=(ACT if k % 2 else DVE))
            fw.dma(SP, xT_s[:, :, ti * TT:(ti + 1) * TT].rearrange("k p t -> p k t"), xt[:], reads=[bxt], writes=[b_xT[ti]])

        import os
        for ti in range(int(os.environ.get("KPH0", str(NT)))):
            phase0(ti)
        fw.barrier()

        def rsqrt_eps(src_ap, dst_ap, bsrc, bdst):
            fw.op(ACT, lambda h: h.activation(out=dst_ap, in_=src_ap, func=AF.Sqrt, bias=EPS), reads=[bsrc], writes=[bdst])
            fw.op(DVE, lambda h: h.reciprocal(out=dst_ap, in_=dst_ap), reads=[bdst], writes=[bdst])

        def rms_rstd(sq_aps, ones_ap, nparts, eps=EPS):
            ps, bps = psA.get()
            n = len(sq_aps)
            for i, (a, b) in enumerate(sq_aps):
                fw.op(PE, lambda h, a=a, i=i: h.matmul(ps[0:nparts, :], lhsT=ones_ap, rhs=a, start=(i == 0), stop=(i == n - 1)),
                      reads=[b, b_cmat], writes=[bps], inc=(i == n - 1))
            rs, brs = rs_r.get()
            rsqrt_eps(ps[0:nparts, :], rs[0:nparts, :], bps, brs)
            return rs, brs

        def load_xT(ti):
            xt, bxt = xt_r.get()
            fw.dma(SP, xt[:], xT_s[:, :, ti * TT:(ti + 1) * TT].rearrange("k p t -> p k t"), reads=[b_xT[ti]], writes=[bxt])
            return xt, bxt

        def norm_mod(xt, bxt, Amod, shift_lo, cond):
            sq, bsq = sq_r.get()
            fw.op(ACT, lambda h: h.activation(out=sq[:], in_=xt[:], func=AF.Square), reads=[bxt], writes=[bsq])
            rs, brs = rms_rstd([(sq[:, k, :], bsq) for k in range(8)], cmat[:, C_1024, :], 128)
            hT, bh = hT_r.get()
            for k in range(8):
                tmp, btmp = f32_r.get()
                e = DVE
                fw.op(e, lambda h, k=k, tmp=tmp: h.tensor_tensor(out=tmp[:], in0=xt[:, k, :], in1=rs[:], op=ALU.mult), reads=[bxt, brs], writes=[btmp])
                fw.op(ACT, lambda h, k=k, tmp=tmp: h.activation(out=hT[:, k, :], in_=tmp[:], func=AF.Identity,
                                                               scale=Amod[:, k, cond:cond + 1], bias=modT[:, shift_lo + k, cond:cond + 1]),
                      reads=[btmp, b_A, b_modT_b], writes=[bh])
            return hT, bh

        def proj_fm(wt, bw, col0, m, rhs_fn, rhs_bufs, nk, out_parts=None):
            ps, bps = psA.get()
            for k in range(nk):
                fw.op(PE, lambda h, k=k: h.matmul(ps[0:m, :], lhsT=wt[:, k, col0:col0 + m], rhs=rhs_fn(k), start=(k == 0), stop=(k == nk - 1)),
                      reads=[bw] + rhs_bufs, writes=[bps], inc=(k == nk - 1))
            return ps, bps

        def norm_rope_store(ps, bps, npart, ones_ap, gcolumn, rope, pm_idx, ropetab, dst_ap, dst_buf, keep_f32=None):
            xf, bxf = f32_r.get()
            evac(ps[0:npart, :], xf[0:npart, :], bps, bxf, eng=DVE)
            sq, bsq = bf_r.get()
            fw.op(ACT, lambda h: h.activation(out=sq[0:npart, :], in_=xf[0:npart, :], func=AF.Square), reads=[bxf], writes=[bsq])
            rs, brs = rms_rstd([(sq[0:npart, :], bsq)], ones_ap, npart)
            xn, bxn = f32_r.get()
            fw.op(DVE, lambda h: h.scalar_tensor_tensor(out=xn[0:npart, :], in0=xf[0:npart, :], scalar=gc[0:npart, gcolumn:gcolumn + 1], in1=rs[0:npart, :],
                                                        op0=ALU.mult, op1=ALU.mult), reads=[bxf, brs, b_gc], writes=[bxn])
            if keep_f32 is not None:
                keep_f32(xn, bxn)
            ob, bob = bf_r.get()
            if not rope:
                fw.op(ACT, lambda h: h.activation(out=ob[0:npart, :], in_=xn[0:npart, :], func=AF.Copy), reads=[bxn], writes=[bob])
            else:
                ctab, stab, btab = ropetab
                xb, bxb = bf_r.get()
                fw.op(ACT, lambda h: h.activation(out=xb[0:npart, :], in_=xn[0:npart, :], func=AF.Copy), reads=[bxn], writes=[bxb])
                ps2, bps2 = psA.get()
                fw.op(PE, lambda h: h.matmul(ps2[0:npart, :], lhsT=pmat[0:npart, pm_idx, 0:npart], rhs=xb[0:npart, :], start=True, stop=True),
                      reads=[bxb, b_pmat], writes=[bps2])
                t1, bt1 = f32_r.get()
                fw.op(DVE, lambda h: h.tensor_tensor(out=t1[0:npart, :], in0=xn[0:npart, :], in1=ctab, op=ALU.mult), reads=[bxn, btab], writes=[bt1])
                t2, bt2 = f32_r.get()
                fw.op(DVE, lambda h: h.tensor_tensor(out=t2[0:npart, :], in0=ps2[0:npart, :], in1=stab, op=ALU.mult), reads=[bps2, btab], writes=[bt2])
                fw.op(DVE, lambda h: h.tensor_tensor(out=ob[0:npart, :], in0=t1[0:npart, :], in1=t2[0:npart, :], op=ALU.add), reads=[bt1, bt2], writes=[bob])
            fw.dma(SP, dst_ap, ob[0:npart, :], reads=[bob], writes=[dst_buf])

        def transpose_out(src_fn, src_bufs, nfeat_parts, ncols_total, dst_rows_fn, colslices):
            for g in range(4):
                ps, bps = psA.get()
                n = len(colslices)
                for i, (j, pj, c0) in enumerate(colslices):
                    fw.op(PE, lambda h, j=j, pj=pj, c0=c0, g=g: h.transpose(ps[:, c0:c0 + pj], src_fn(j)[:, g * 128:(g + 1) * 128], ident[0:pj, 0:pj]),
                          reads=src_bufs + [b_ident], writes=[bps], inc=(i == n - 1))
                o, bo = f32_r.get()
                evac(ps[:, 0:ncols_total], o[:, 0:ncols_total], bps, bo, eng=DVE)
                fw.dma(SP, dst_rows_fn(g), o[:, 0:ncols_total], reads=[bo], writes=[Buf()])

        b_gq = Buf(); b_gk = Buf(); b_gkt = Buf(); b_gvt = Buf(); b_gG = Buf(); b_go = Buf()
        b_q = Buf(); b_k = Buf(); b_v = Buf(); b_dq = Buf(); b_dk = Buf(); b_dv = Buf()
        b_ya = Buf(); b_yb = Buf(); b_yc = Buf()
        out_bufs = []

        arena.reset()
        ropM_r = Ring(ov, "ropM", [96, 2, TT], F32, 1)
        ropD_r = Ring(ov, "ropD", [128, 2, TT], F32, 1)
        vaug_r = Ring(ov, "vaug", [128, 8, 4, 128], BF16, 1)
        tok_r = Ring(ov, "tok", [128, 4, 512], BF16, 2)
        tokf_r = Ring(ov, "tokf", [128, 4, 512], F32, 1)
        keep_r = Ring(ov, "keep", [128, 4, TT], F32, 1)
        qdn_r = Ring(ov, "qdn", [128, 3, TT], BF16, 1)
        ckv_r = Ring(ov, "ckv", [128, 2, TT], BF16, 1)
        ckvf_r = Ring(ov, "ckvf", [128, 2, TT], F32, 1)
        krb_r = Ring(ov, "krb", [32, TT], BF16, 1)
        krf_r = Ring(ov, "krf", [32, TT], F32, 1)
        aab_r = Ring(ov, "aab", [32, TT], BF16, 1)
        g4_r = Ring(ov, "g4", [64, 4, TT], BF16, 1)
        r4_r = Ring(ov, "r4", [128, 4, TT], BF16, 1)
        ctok_r = Ring(ov, "ctok", [128, 2, 512], F32, 2)

        def layer_setup(l):
            fw.dma(SP, bmod[:], b_modT[l], writes=[b_bmod])
            fw.dma(SP, gn[:], gnT[l], writes=[b_gn])
            fw.dma(SP, gc[:], gcol[l], writes=[b_gc])
            fw.dma(POOL, wa2[:], w_a2bd[l], writes=[b_wa2])
            fw.dma(POOL, ba[:], b_a[l], writes=[b_ba])
            fw.dma(SP, lamt[:], lam_qk[l], writes=[b_lamt])
            for cb in range(24):
                wm, bwm = wmod_r.get()
                fw.dma(SP, wm[:], w_mod[l, :, cb * 256:(cb + 1) * 256].rearrange("(k p) c -> p k c", p=128), writes=[bwm])
                ps, bps = psA.get()
                for j in range(2):
                    for k in range(8):
                        fw.op(PE, lambda h, j=j, k=k, wm=wm, ps=ps: h.matmul(ps[:, j * 2:j * 2 + 2], lhsT=wm[:, k, j * 128:(j + 1) * 128], rhs=scT[:, k, :],
                                                                          start=(k == 0), stop=(k == 7)),
                              reads=[bwm, b_scT], writes=[bps], inc=(j == 1 and k == 7))
                fw.op(DVE, lambda h, cb=cb, ps=ps: h.tensor_tensor(out=modT[:, cb * 2:(cb + 1) * 2, :], in0=ps[:, 0:4].rearrange("p (j c) -> p j c", c=2),
                                                                 in1=bmod[:, cb * 2:(cb + 1) * 2].unsqueeze(2).broadcast_to([128, 2, 2]), op=ALU.add),
                      reads=[bps, b_bmod], writes=[b_modT_b])
            fw.op(DVE, lambda h: h.scalar_tensor_tensor(out=A1[:], in0=modT[:, 8:16, :], scalar=1.0, in1=gn[:, 0:8].unsqueeze(2).broadcast_to([128, 8, 2]),
                                                        op0=ALU.add, op1=ALU.mult), reads=[b_modT_b, b_gn], writes=[b_A])
            fw.op(DVE, lambda h: h.scalar_tensor_tensor(out=A2[:], in0=modT[:, 32:40, :], scalar=1.0, in1=gn[:, 8:16].unsqueeze(2).broadcast_to([128, 8, 2]),
                                                        op0=ALU.add, op1=ALU.mult), reads=[b_modT_b, b_gn], writes=[b_A])
            lam_init = 0.8 - 0.6 * math.exp(-0.3 * l)
            fw.op(DVE, lambda h: h.tensor_tensor(out=lamt[:, 0:64], in0=lamt[:, 0:64], in1=lamt[:, 64:128], op=ALU.mult), reads=[b_lamt], writes=[b_lamt])
            fw.op(DVE, lambda h: h.tensor_tensor(out=lamt[:, 128:192], in0=lamt[:, 128:192], in1=lamt[:, 192:256], op=ALU.mult), reads=[b_lamt], writes=[b_lamt])
            fw.op(DVE, lambda h: h.reduce_sum(out=lam1[:, 0:1], in_=lamt[:, 0:64], axis=mybir.AxisListType.X), reads=[b_lamt], writes=[b_lam1])
            fw.op(DVE, lambda h: h.reduce_sum(out=lam1[:, 1:2], in_=lamt[:, 128:192], axis=mybir.AxisListType.X), reads=[b_lamt], writes=[b_lam1])
            fw.op(ACT, lambda h: h.activation(out=lam1[:, 2:4], in_=lam1[:, 0:2], func=AF.Exp), reads=[b_lam1], writes=[b_lam1])
            fw.op(DVE, lambda h: h.scalar_tensor_tensor(out=lam1[:, 4:5], in0=lam1[:, 3:4], scalar=-lam_init, in1=lam1[:, 2:3], op0=ALU.add, op1=ALU.subtract),
                  reads=[b_lam1], writes=[b_lam1])
            ps, bps = psA.get()
            fw.op(PE, lambda h: h.matmul(ps[:, 0:1], lhsT=ones_rowf[:, :], rhs=lam1[:, 4:5], start=True, stop=True), reads=[b_onesrowf, b_lam1], writes=[bps])
            fw.op(DVE, lambda h: h.tensor_copy(out=lamc[:, 0:1], in_=ps[:, 0:1]), reads=[bps], writes=[b_lamc])
            return lam_init

        def key_base(ti):
            return 0 if ti == 0 else NP_TOK + PAST + (ti - 1) * TT

        import os as _os
        KP1S = int(_os.environ.get("KP1S", "99"))
        KP1 = int(_os.environ.get("KP1", str(NT)))

        def phase1(l, ti):
            if ti >= KP1:
                return
            lat = ti > 0
            cond = 1 if lat else 0
            t0 = ti * TT
            kb = key_base(ti)
            xt, bxt = load_xT(ti)
            hT, bh = norm_mod(xt, bxt, A1, 0, cond)
            rhs_h = lambda k: hT[:, k, :]
            if lat:
                rM, brM = ropM_r.get()
                fw.dma(SP, rM[:], ropeM[:, :, (ti - 1) * TT:ti * TT].rearrange("c p t -> p c t"), writes=[brM])
                rD, brD = ropD_r.get()
                fw.dma(SP, rD[:], ropeD[:, :, (ti - 1) * TT:ti * TT].rearrange("c p t -> p c t"), writes=[brD])
                tabM = (rM[:, 0, :], rM[:, 1, :], brM)
                tabD = (rD[:, 0, :], rD[:, 1, :], brD)
            else:
                tabM = tabD = None

            if KP1S < 1:
                return
            wt, bw = load_w8(w_in[l, :, O_AQ:O_AQ + 512], 512)
            for which, dst, bdst in ((0, gq_s, b_gq), (1, gk_s, b_gk)):
                g4, bg4 = g4_r.get()
                for hh in range(4):
                    ps, bps = proj_fm(wt, bw, which * 256 + hh * 64, 64, rhs_h, [bh], 8)
                    evac(ps[0:64, :], g4[:, hh, :], bps, bg4, eng=(ACT if hh % 2 else DVE))
                fw.dma(SP, dst[:, :, t0:t0 + TT], g4[:], reads=[bg4], writes=[bdst])
            if KP1S < 2:
                return
            tk, btk = tok_r.get()
            for g in range(4):
                ps, bps = psA.get()
                for k in range(8):
                    fw.op(PE, lambda h, k=k, g=g, ps=ps: h.matmul(ps[:, 0:256], lhsT=hT[:, k, g * 128:(g + 1) * 128], rhs=wt[:, k, 256:512], start=(k == 0), stop=(k == 7)),
                          reads=[bw, bh], writes=[bps], inc=(k == 7))
                evac(ps[:, 0:256], tk[:, g, 0:256], bps, btk, eng=(ACT if g % 2 else DVE))
            fw.dma(SP, gkt_s[t0:t0 + TT, :].rearrange("(g p) c -> p g c", p=128), tk[:, :, 0:256], reads=[btk], writes=[b_gkt])
            wt, bw = load_w8(w_in[l, :, O_AV:O_AV + 512], 512)
            tk, btk = tok_r.get()
            for g in range(4):
                ps, bps = psA.get()
                for k in range(8):
                    fw.op(PE, lambda h, k=k, g=g, ps=ps, wt=wt: h.matmul(ps[:, :], lhsT=hT[:, k, g * 128:(g + 1) * 128], rhs=wt[:, k, 0:512], start=(k == 0), stop=(k == 7)),
                          reads=[bw, bh], writes=[bps], inc=(k == 7))
                evac(ps[:, :], tk[:, g, :], bps, btk, eng=(ACT if g % 2 else DVE))
            fw.dma(SP, gvt_s[t0:t0 + TT, :].rearrange("(g p) c -> p g c", p=128), tk[:], reads=[btk], writes=[b_gvt])
            if KP1S < 3:
                return
            wt, bw = load_w8(w_in[l, :, O_AA:O_AA + 416], 416)
            ps, bps = proj_fm(wt, bw, 0, 32, rhs_h, [bh], 8)
            aab, baab = aab_r.get()
            evac(ps[0:32, :], aab[:, :], bps, baab, eng=DVE)
            gt, bgt = tokf_r.get()
            for g in range(4):
                ps, bps = psA.get()
                fw.op(PE, lambda h, g=g, ps=ps: h.matmul(ps[:, :], lhsT=aab[:, g * 128:(g + 1) * 128], rhs=wa2[:, :], start=True, stop=False),
                      reads=[baab, b_wa2], writes=[bps], inc=False)
                fw.op(PE, lambda h, ps=ps: h.matmul(ps[:, :], lhsT=ones_row[:, :], rhs=ba[:, :], start=False, stop=True), reads=[b_onesrow, b_ba], writes=[bps])
                e1, be1 = f32_r.get()
                fw.op(ACT, lambda h, ps=ps, e1=e1: h.activation(out=e1[:], in_=ps[:], func=AF.Exp, scale=-1.0), reads=[bps], writes=[be1])
                fw.op(ACT, lambda h, g=g, e1=e1: h.activation(out=gt[:, g, :], in_=e1[:], func=AF.Ln, bias=1.0), reads=[be1], writes=[bgt])
            fw.dma(SP, gG_s[t0:t0 + TT, :].rearrange("(g p) c -> p g c", p=128), gt[:], reads=[bgt], writes=[b_gG])

            if KP1S < 4:
                return
            wt_r, bw_r = load_w8(w_in[l, :, O_AR:O_AR + 512], 512)
            rr, brr = r4_r.get()
            for hh in range(4):
                ps_r, bps_r = proj_fm(wt_r, bw_r, hh * 128, 128, rhs_h, [bh], 8)
                fw.op(ACT, lambda h, hh=hh, ps_r=ps_r: h.activation(out=rr[:, hh, :], in_=ps_r[:], func=AF.Silu), reads=[bps_r], writes=[brr])
            fw.dma(SP, rs_s[:, :, t0:t0 + TT], rr[:], reads=[brr], writes=[b_rs])

            if KP1S < 5:
                return
            qdn, bqdn = qdn_r.get()
            qf = []
            for j in range(3):
                ps, bps = proj_fm(wt, bw, 32 + j * 128, 128, rhs_h, [bh], 8)
                xf, bxf = f32_r.get()
                evac(ps[:], xf[:], bps, bxf, eng=DVE)
                sq, bsq = bf_r.get()
                fw.op(ACT, lambda h, sq=sq, xf=xf: h.activation(out=sq[:], in_=xf[:], func=AF.Square), reads=[bxf], writes=[bsq])
                qf.append((xf, bxf, sq, bsq))
            rs, brs = rms_rstd([(q[2][:], q[3]) for q in qf], cmat[:, C_384, :], 128)
            for j in range(3):
                xf, bxf = qf[j][0], qf[j][1]
                fw.op(DVE, lambda h, j=j, xf=xf: h.scalar_tensor_tensor(out=qdn[:, j, :], in0=xf[:], scalar=gc[:, j:j + 1], in1=rs[:], op0=ALU.mult, op1=ALU.mult),
                      reads=[bxf, brs, b_gc], writes=[bqdn])
            for half in range(2):
                wq, bwq = w8.get()
                fw.dma(POOL, wq[:, 0:3, 0:384], w_uq[l, :, half * 4:(half + 1) * 4, :].rearrange("(k p) h c -> p k (h c)", p=128), writes=[bwq])
                for hh in range(4):
                    ps, bps = proj_fm(wq, bwq, hh * 96, 96, lambda k: qdn[:, k, :], [bqdn], 3)
                    head = half * 4 + hh
                    norm_rope_store(ps, bps, 96, cmat[0:96, C_96, 0:96], 5, lat, P_96, tabM, q_s[head, :, t0:t0 + TT], b_q)

            if KP1S < 6:
                return
            wt, bw = load_w8(w_in[l, :, O_KVD:O_KVD + 288], 288)
            ckv, bckv = ckv_r.get()
            ckvf, bckvf = ckvf_r.get()
            kf = []
            for j in range(2):
                ps, bps = proj_fm(wt, bw, j * 128, 128, rhs_h, [bh], 8)
                xf, bxf = f32_r.get()
                evac(ps[:], xf[:], bps, bxf, eng=DVE)
                sq, bsq = bf_r.get()
                fw.op(ACT, lambda h, sq=sq, xf=xf: h.activation(out=sq[:], in_=xf[:], func=AF.Square), reads=[bxf], writes=[bsq])
                kf.append((xf, bxf, sq, bsq))
            rs, brs = rms_rstd([(q[2][:], q[3]) for q in kf], cmat[:, C_256, :], 128)
            for j in range(2):
                xf, bxf = kf[j][0], kf[j][1]
                fw.op(DVE, lambda h, j=j, xf=xf: h.scalar_tensor_tensor(out=ckvf[:, j, :], in0=xf[:], scalar=gc[:, 3 + j:4 + j], in1=rs[:], op0=ALU.mult, op1=ALU.mult),
                      reads=[bxf, brs, b_gc], writes=[bckvf])
            fw.op(ACT, lambda h: h.activation(out=ckv[:], in_=ckvf[:], func=AF.Copy), reads=[bckvf], writes=[bckv])
            ps, bps = proj_fm(wt, bw, 256, 32, rhs_h, [bh], 8)
            krf, bkrf = krf_r.get()
            krb, bkrb = krb_r.get()
            evac(ps[0:32, :], krf[:, :], bps, bkrf, eng=DVE)
            fw.op(ACT, lambda h: h.activation(out=krb[:], in_=krf[:], func=AF.Copy), reads=[bkrf], writes=[bkrb])
            KSUB = _os.environ.get("KSUB", "abcd")
            if not lat:
                if "a" in KSUB:
                    transpose_out(lambda j: ckvf[:, j, :], [bckvf], 128, 256, lambda g: o_ckv[l, g * 128:(g + 1) * 128, :], [(0, 128, 0), (1, 128, 128)])
                if "b" in KSUB:
                    transpose_out(lambda j: krf[:, :], [bkrf], 32, 32, lambda g: o_kr[l, g * 128:(g + 1) * 128, :], [(0, 32, 0)])
            if "c" in KSUB:
                mla_keys(l, lambda k: ckv[:, k, :], [bckv], krb, bkrb, lat, tabM, kb)
            if "d" in KSUB:
                mla_values(l, lambda k, g: ckv[:, k, g * 128:(g + 1) * 128], [bckv], kb)

            if KP1S < 7:
                return
            for which, off, dst, bdst, gcolumn in ((0, O_DQ, dq_s, b_dq, 7), (1, O_DK, dk_s, b_dk, 8)):
                wt, bw = load_w8(w_in[l, :, off:off + 512], 512)
                keep = None
                if which == 1 and not lat:
                    kp, bkp = keep_r.get()
                for hh in range(4):
                    ps, bps = proj_fm(wt, bw, hh * 128, 128, rhs_h, [bh], 8)
                    kf32 = None
                    if which == 1 and not lat:
                        kf32 = lambda xn, bxn, hh=hh: fw.op(ACT, lambda h: h.activation(out=kp[:, hh, :], in_=xn[:], func=AF.Copy), reads=[bxn], writes=[bkp])
                    col0 = t0 if which == 0 else kb
                    norm_rope_store(ps, bps, 128, cmat[:, C_BLK64, :], gcolumn, lat, P_128, tabD, dst[hh, :, col0:col0 + TT], bdst, keep_f32=kf32)
                if which == 1 and not lat:
                    transpose_out(lambda j: kp[:, j, :], [bkp], 128, 512, lambda g: o_dk[l, g * 128:(g + 1) * 128, :], [(j, 128, j * 128) for j in range(4)])
            wt, bw = load_w8(w_in[l, :, O_DV:O_DV + 512], 512)
            tk, btk = tok_r.get()
            if not lat:
                tf, btf = tokf_r.get()
            for g in range(4):
                ps, bps = psA.get()
                for k in range(8):
                    fw.op(PE, lambda h, k=k, g=g, ps=ps, wt=wt: h.matmul(ps[:, :], lhsT=hT[:, k, g * 128:(g + 1) * 128], rhs=wt[:, k, 0:512], start=(k == 0), stop=(k == 7)),
                          reads=[bw, bh], writes=[bps], inc=(k == 7))
                evac(ps[:, :], tk[:, g, :], bps, btk, eng=ACT)
                if not lat:
                    evac(ps[:, :], tf[:, g, :], bps, btf, eng=DVE)
            for hh in range(4):
                fw.dma(SP, dv_s[hh, :, kb // 128:kb // 128 + 4, :], tk[:, :, hh * 128:(hh + 1) * 128], reads=[btk], writes=[b_dv])
            if not lat:
                ob = Buf(); out_bufs.append(ob)
                fw.dma(SP, o_dv[l, :, :].rearrange("(g p) c -> p g c", p=128), tf[:], reads=[btf], writes=[ob])

        def mla_keys(l, ckv_fn, ckv_bufs, krb, bkrb, rope, tabM, kb, ntok=TT):
            for half in range(2):
                wk, bwk = w8.get()
                fw.dma(POOL, wk[:, 0:2, 0:384], w_ukp[l, :, half * 4:(half + 1) * 4, :].rearrange("(k p) h c -> p k (h c)", p=128), writes=[bwk])
                for hh in range(4):
                    ps, bps = psA.get()
                    for k in range(2):
                        fw.op(PE, lambda h, k=k, hh=hh, ps=ps, wk=wk: h.matmul(ps[0:96, 0:ntok], lhsT=wk[:, k, hh * 96:(hh + 1) * 96], rhs=ckv_fn(k), start=(k == 0), stop=False),
                              reads=[bwk] + ckv_bufs, writes=[bps], inc=False)
                    fw.op(PE, lambda h, ps=ps: h.matmul(ps[0:96, 0:ntok], lhsT=cmat[0:32, C_SEL32, 0:96], rhs=krb[:, 0:ntok], start=False, stop=True),
                          reads=[b_cmat, bkrb], writes=[bps])
                    head = half * 4 + hh
                    if ntok == TT:
                        norm_rope_store(ps, bps, 96, cmat[0:96, C_96, 0:96], 6, rope, P_96, tabM, k_s[head, :, kb:kb + TT], b_k)
                    else:
                        norm_store_small(ps, bps, ntok, head, kb)

        def norm_store_small(ps, bps, ntok, head, kb):
            xf, bxf = f32_r.get()
            evac(ps[0:96, 0:ntok], xf[0:96, 0:ntok], bps, bxf, eng=DVE)
            sq, bsq = bf_r.get()
            fw.op(ACT, lambda h: h.activation(out=sq[0:96, 0:ntok], in_=xf[0:96, 0:ntok], func=AF.Square), reads=[bxf], writes=[bsq])
            ps2, bps2 = psA.get()
            fw.op(PE, lambda h: h.matmul(ps2[0:96, 0:ntok], lhsT=cmat[0:96, C_96, 0:96], rhs=sq[0:96, 0:ntok], start=True, stop=True), reads=[bsq, b_cmat], writes=[bps2])
            rs, brs = rs_r.get()
            rsqrt_eps(ps2[0:96, 0:ntok], rs[0:96, 0:ntok], bps2, brs)
            ob, bob = bf_r.get()
            fw.op(DVE, lambda h: h.scalar_tensor_tensor(out=ob[0:96, 0:ntok], in0=xf[0:96, 0:ntok], scalar=gc[0:96, 6:7], in1=rs[0:96, 0:ntok], op0=ALU.mult, op1=ALU.mult),
                  reads=[bxf, brs, b_gc], writes=[bob])
            fw.dma(SP, k_s[head, :, kb:kb + ntok], ob[0:96, 0:ntok], reads=[bob], writes=[b_k])

        vaug_init = [False, False]

        def mla_values(l, ckvT_fn, ckv_bufs, kb, ngrp=4):
            wv, bwv = w8.get()
            fw.dma(POOL, wv[:, 0:2, 0:512], w_uv[l].rearrange("(k p) c -> p k c", p=128), writes=[bwv])
            idx = vaug_r.i
            va, bva = vaug_r.get()
            if not vaug_init[idx]:
                vaug_init[idx] = True
                fw.op(DVE, lambda h: h.memset(va[:], 1.0), writes=[bva])
            for g in range(ngrp):
                ps, bps = psA.get()
                for k in range(2):
                    fw.op(PE, lambda h, k=k, g=g, ps=ps: h.matmul(ps[:, :], lhsT=ckvT_fn(k, g), rhs=wv[:, k, 0:512], start=(k == 0), stop=(k == 1)),
                          reads=[bwv] + ckv_bufs, writes=[bps], inc=(k == 1))
                fw.op(ACT if g % 2 else DVE, (lambda h, g=g, ps=ps: h.activation(out=va[:, :, g, 0:64], in_=ps[:, :].rearrange("p (h c) -> p h c", c=64), func=AF.Copy)) if g % 2
                      else (lambda h, g=g, ps=ps: h.tensor_copy(out=va[:, :, g, 0:64], in_=ps[:, :].rearrange("p (h c) -> p h c", c=64))), reads=[bps], writes=[bva])
            c0 = kb // 128
            fw.dma(SP, v_s[:, :, c0:c0 + ngrp, :].rearrange("h p g e -> p h (g e)"), va[:, :, 0:ngrp, :].rearrange("p h g e -> p h (g e)"), reads=[bva], writes=[b_v])


        def prep_cache(l):
            kb = NP_TOK
            ct, bct = ctok_r.get()
            fw.dma(SP, ct[:, :, 0:256], c_ckv[l].rearrange("(g p) c -> p g c", p=128), writes=[bct])
            ckv, bckv = ckv_r.get()
            for j in range(2):
                ps, bps = psA.get()
                for g in range(2):
                    fw.op(PE, lambda h, j=j, g=g, ps=ps: h.transpose(ps[:, g * 128:(g + 1) * 128], ct[:, g, j * 128:(j + 1) * 128], ident[:]),
                          reads=[bct, b_ident], writes=[bps], inc=(g == 1))
                evac(ps[:, 0:256], ckv[:, j, 0:256], bps, bckv, eng=DVE)
            ct2, bct2 = ctok_r.get()
            fw.dma(SP, ct2[:, :, 0:32], c_kr[l].rearrange("(g p) c -> p g c", p=128), writes=[bct2])
            krb, bkrb = krb_r.get()
            ps, bps = psA.get()
            for g in range(2):
                fw.op(PE, lambda h, g=g, ps=ps: h.transpose(ps[0:32, g * 128:(g + 1) * 128], ct2[:, g, 0:32], ident[:]), reads=[bct2, b_ident], writes=[bps], inc=(g == 1))
            evac(ps[0:32, 0:256], krb[:, 0:256], bps, bkrb, eng=DVE)
            mla_keys(l, lambda k: ckv[:, k, 0:256], [bckv], krb, bkrb, False, None, kb, ntok=256)
            mla_values(l, lambda k, g: ckv[:, k, g * 128:(g + 1) * 128], [bckv], kb, ngrp=2)
            ct, bct = ctok_r.get()
            fw.dma(SP, ct[:], c_dk[l].rearrange("(g p) c -> p g c", p=128), writes=[bct])
            for hh in range(4):
                ps, bps = psA.get()
                for g in range(2):
                    fw.op(PE, lambda h, hh=hh, g=g, ps=ps: h.transpose(ps[:, g * 128:(g + 1) * 128], ct[:, g, hh * 128:(hh + 1) * 128], ident[:]),
                          reads=[bct, b_ident], writes=[bps], inc=(g == 1))
                ob, bob = bf_r.get()
                evac(ps[:, 0:256], ob[:, 0:256], bps, bob, eng=DVE)
                fw.dma(SP, dk_s[hh, :, kb:kb + 256], ob[:, 0:256], reads=[bob], writes=[b_dk])
            ct3, bct3 = ctok_r.get()
            fw.dma(SP, ct3[:], c_dv[l].rearrange("(g p) c -> p g c", p=128), writes=[bct3])
            tkc, btkc = tok_r.get()
            fw.op(DVE, lambda h: h.tensor_copy(out=tkc[:, 0:2, :], in_=ct3[:]), reads=[bct3], writes=[btkc])
            for hh in range(4):
                fw.dma(SP, dv_s[hh, :, kb // 128:kb // 128 + 2, :], tkc[:, 0:2, hh * 128:(hh + 1) * 128], reads=[btkc], writes=[b_dv])

        arena.reset()
        gl_q = Ring(ov, "glq", [64, 4, TT], BF16, 1)
        gl_k = Ring(ov, "glk", [64, 4, TT], BF16, 1)
        gl_kt = Ring(ov, "glkt", [64, 8, 256], BF16, 1)
        gl_vt = Ring(ov, "glvt", [64, 8, 512], BF16, 1)
        gl_G = Ring(ov, "glG", [64, 8, 256], F32, 1)
        gl_o = Ring(ov, "glo", [128, 4, TT], F32, 1)
        gl_of = Ring(ov, "glof", [128, 4, TT], F32, 1)
        gl_r = Ring(ov, "glr", [128, 4, TT], BF16, 2)
        S_f = ov("S_f", [64, 4, 128], F32); S_b = ov("S_b", [64, 4, 128], BF16); b_S = Buf(); b_Sb = Buf()
        sm_r = Ring(ov, "sm", [64, 256], F32, 6)
        smb_r = Ring(ov, "smb", [64, 256], BF16, 8)
        dd_r = Ring(ov, "dd", [64, 4], F32, 3)

        def gla_dir(l, d, seqs):
            incl = tri[:, d, :]
            strict = tri[:, 2 + d, :]
            for (tok0, ntok, seq_idx, lat) in seqs:
                if lat:
                    fw.dma(SP, S_f[:], st_gla[l, d], reads=[b_Sb], writes=[b_S])
                else:
                    fw.op(DVE, lambda h: h.memset(S_f[:], 0.0), reads=[b_Sb], writes=[b_S])
                fw.op(ACT, lambda h: h.activation(out=S_b[:], in_=S_f[:], func=AF.Copy), reads=[b_S], writes=[b_Sb])
                tiles = list(range(tok0, tok0 + ntok, TT)) if ntok >= TT else [tok0]
                if d == 1:
                    tiles = tiles[::-1]
                for tt0 in tiles:
                    base_tile = (tt0 // TT) * TT
                    cq, bcq = gl_q.get(); ck, bck = gl_k.get(); ckt, bckt = gl_kt.get(); cvt, bcvt = gl_vt.get(); cG, bcG = gl_G.get()
                    fw.dma(SP, cq[:], gq_s[:, :, base_tile:base_tile + TT], reads=[b_gq], writes=[bcq])
                    fw.dma(SP, ck[:], gk_s[:, :, base_tile:base_tile + TT], reads=[b_gk], writes=[bck])
                    fw.dma(SP, ckt[:], gkt_s[base_tile:base_tile + TT, :].rearrange("(c p) f -> p c f", p=64), reads=[b_gkt], writes=[bckt])
                    fw.dma(SP, cvt[:], gvt_s[base_tile:base_tile + TT, :].rearrange("(c p) f -> p c f", p=64), reads=[b_gvt], writes=[bcvt])
                    fw.dma(SP, cG[:], gG_s[base_tile:base_tile + TT, d * 256:(d + 1) * 256].rearrange("(c p) f -> p c f", p=64), reads=[b_gG], writes=[bcG])
                    ot, bot = gl_o.get()
                    c_lo = (tt0 - base_tile) // 64
                    nch = min(ntok, TT) // 64
                    chunks = list(range(c_lo, c_lo + nch))
                    if d == 1:
                        chunks = chunks[::-1]
                    for c in chunks:
                        Gc = cG[:, c, :]
                        psb_, bpsb = psA.get()
                        for hh in range(4):
                            fw.op(PE, lambda h, hh=hh, Gc=Gc, p=psb_: h.matmul(p[0:64, hh * 64:(hh + 1) * 64], lhsT=Gc[:, hh * 64:(hh + 1) * 64], rhs=incl, start=True, stop=True),
                                  reads=[bcG, b_tri], writes=[bpsb], inc=(hh == 3))
                        ep, bep = sm_r.get(); en, ben = sm_r.get()
                        fw.op(ACT, lambda h, p=psb_, ep=ep: h.activation(out=ep[:, :], in_=p[0:64, 0:256], func=AF.Exp, scale=-1.0 / 16), reads=[bpsb], writes=[bep])
                        fw.op(ACT, lambda h, p=psb_, en=en: h.activation(out=en[:, :], in_=p[0:64, 0:256], func=AF.Exp, scale=1.0 / 16), reads=[bpsb], writes=[ben])
                        qt, bqt = smb_r.get(); kt_, bkt = smb_r.get()
                        fw.op(DVE, lambda h, c=c, qt=qt, ep=ep: h.scalar_tensor_tensor(out=qt[:, :].rearrange("p (h t) -> p h t", t=64), in0=cq[:, :, c * 64:(c + 1) * 64], scalar=0.125,
                                                                                      in1=ep[:, :].rearrange("p (h t) -> p h t", t=64), op0=ALU.mult, op1=ALU.mult),
                              reads=[bcq, bep], writes=[bqt])
                        fw.op(DVE, lambda h, c=c, kt_=kt_, en=en: h.tensor_tensor(out=kt_[:, :].rearrange("p (h t) -> p h t", t=64), in0=ck[:, :, c * 64:(c + 1) * 64],
                                                                                  in1=en[:, :].rearrange("p (h t) -> p h t", t=64), op=ALU.mult),
                              reads=[bck, ben], writes=[bkt])
                        ps2, bps2 = psA.get()
                        fw.op(PE, lambda h, Gc=Gc, p=ps2: h.matmul(p[0:64, 0:256], lhsT=strict, rhs=Gc, start=True, stop=True), reads=[bcG, b_tri], writes=[bps2])
                        e2, be2 = sm_r.get()
                        fw.op(ACT, lambda h, p=ps2, e2=e2: h.activation(out=e2[:, :], in_=p[0:64, 0:256], func=AF.Exp, scale=-1.0 / 16), reads=[bps2], writes=[be2])
                        kh, bkh = smb_r.get()
                        fw.op(DVE, lambda h, c=c, kh=kh, e2=e2: h.tensor_tensor(out=kh[:, :], in0=ckt[:, c, :], in1=e2[:, :], op=ALU.mult), reads=[bckt, be2], writes=[bkh])
                        ps3, bps3 = psA.get()
                        for hh in range(4):
                            fw.op(PE, lambda h, hh=hh, Gc=Gc, p=ps3: h.matmul(p[0:64, hh:hh + 1], lhsT=Gc[:, hh * 64:(hh + 1) * 64], rhs=ones_col[:, :], start=True, stop=True),
                                  reads=[bcG, b_onescol], writes=[bps3], inc=(hh == 3))
                        dd, bdd = dd_r.get()
                        fw.op(ACT, lambda h, p=ps3, dd=dd: h.activation(out=dd[:, :], in_=p[0:64, 0:4], func=AF.Exp, scale=-1.0 / 16), reads=[bps3], writes=[bdd])
                        ps4, bps4 = psA.get()
                        for hh in range(4):
                            fw.op(PE, lambda h, hh=hh, p=ps4, kt_=kt_, qt=qt: h.matmul(p[0:64, hh * 64:(hh + 1) * 64], lhsT=kt_[:, hh * 64:(hh + 1) * 64], rhs=qt[:, hh * 64:(hh + 1) * 64],
                                                                                     start=True, stop=True), reads=[bkt, bqt], writes=[bps4], inc=(hh == 3))
                        am, bam = smb_r.get()
                        fw.op(DVE, lambda h, p=ps4, am=am: h.tensor_tensor(out=am[:, :].rearrange("p (h t) -> p h t", t=64), in0=p[0:64, 0:256].rearrange("p (h t) -> p h t", t=64),
                                                                         in1=incl.unsqueeze(1).broadcast_to([64, 4, 64]), op=ALU.mult), reads=[bps4, b_tri], writes=[bam])
                        ps5, bps5 = psA.get()
                        for hh in range(4):
                            fw.op(PE, lambda h, hh=hh, c=c, p=ps5, am=am: h.matmul(p[:, hh * 64:(hh + 1) * 64], lhsT=cvt[:, c, hh * 128:(hh + 1) * 128], rhs=am[:, hh * 64:(hh + 1) * 64],
                                                                                 start=True, stop=False), reads=[bcvt, bam], writes=[bps5], inc=False)
                            fw.op(PE, lambda h, hh=hh, p=ps5, qt=qt: h.matmul(p[:, hh * 64:(hh + 1) * 64], lhsT=S_b[:, hh, :], rhs=qt[:, hh * 64:(hh + 1) * 64], start=False, stop=True),
                                  reads=[b_Sb, bqt], writes=[bps5], inc=(hh == 3))
                        fw.op(ACT, lambda h, c=c, p=ps5, ot=ot: h.activation(out=ot[:, :, c * 64:(c + 1) * 64], in_=p[:, 0:256].rearrange("p (h t) -> p h t", t=64), func=AF.Copy),
                              reads=[bps5], writes=[bot])
                        ps6, bps6 = psB.get()
                        for hh in range(4):
                            fw.op(PE, lambda h, hh=hh, c=c, p=ps6, kh=kh: h.matmul(p[0:64, hh * 128:(hh + 1) * 128], lhsT=kh[:, hh * 64:(hh + 1) * 64], rhs=cvt[:, c, hh * 128:(hh + 1) * 128],
                                                                                 start=True, stop=True), reads=[bkh, bcvt], writes=[bps6], inc=(hh == 3))
                        fw.op(DVE, lambda h, dd=dd: h.tensor_tensor(out=S_f[:], in0=S_f[:], in1=dd[:, :].unsqueeze(2).broadcast_to([64, 4, 128]), op=ALU.mult), reads=[bdd], writes=[b_S])
                        fw.op(DVE, lambda h, p=ps6: h.tensor_tensor(out=S_f[:], in0=S_f[:], in1=p[0:64, :].rearrange("p (h v) -> p h v", v=128), op=ALU.add), reads=[bps6], writes=[b_S])
                        fw.op(ACT, lambda h: h.activation(out=S_b[:], in_=S_f[:], func=AF.Copy), reads=[b_S], writes=[b_Sb])
                    cols = slice(tt0 - base_tile, tt0 - base_tile + min(ntok, TT))
                    ncol = min(ntok, TT)
                    if d == 0:
                        fw.dma(SP, go_s[:, :, tt0:tt0 + ncol], ot[:, :, cols], reads=[bot], writes=[b_go])
                    else:
                        of, bof = gl_of.get()
                        fw.dma(SP, of[:, :, 0:ncol], go_s[:, :, tt0:tt0 + ncol], reads=[b_go], writes=[bof])
                        rr, brr = gl_r.get()
                        fw.dma(SP, rr[:, :, 0:ncol], rs_s[:, :, tt0:tt0 + ncol], reads=[b_rs], writes=[brr])
                        fw.op(DVE, lambda h, ot=ot, of=of: h.tensor_tensor(out=of[:, :, 0:ncol], in0=of[:, :, 0:ncol], in1=ot[:, :, cols], op=ALU.add), reads=[bot], writes=[bof])
                        sq, bsq = sq_r.get()
                        fw.op(ACT, lambda h, sq=sq, of=of: h.activation(out=sq[:, 0:4, 0:ncol], in_=of[:, :, 0:ncol], func=AF.Square), reads=[bof], writes=[bsq])
                        ya, bya = gl_r.get()
                        for hh in range(4):
                            ps, bps = psA.get()
                            fw.op(PE, lambda h, hh=hh, ps=ps, sq=sq: h.matmul(ps[:, 0:ncol], lhsT=cmat[:, C_128, :], rhs=sq[:, hh, 0:ncol], start=True, stop=True), reads=[bsq, b_cmat], writes=[bps])
                            rs, brs = rs_r.get()
                            rsqrt_eps(ps[:, 0:ncol], rs[:, 0:ncol], bps, brs)
                            tmp, btmp = f32_r.get()
                            fw.op(DVE, lambda h, hh=hh, tmp=tmp, of=of, rs=rs: h.scalar_tensor_tensor(out=tmp[:, 0:ncol], in0=of[:, hh, 0:ncol], scalar=gc[:, 9:10], in1=rs[:, 0:ncol],
                                                                                                     op0=ALU.mult, op1=ALU.mult), reads=[bof, brs, b_gc], writes=[btmp])
                            fw.op(DVE, lambda h, hh=hh, tmp=tmp, ya=ya, rr=rr: h.tensor_tensor(out=ya[:, hh, 0:ncol], in0=tmp[:, 0:ncol], in1=rr[:, hh, 0:ncol], op=ALU.mult),
                                  reads=[btmp, brr], writes=[bya])
                        fw.dma(SP, ya_s[:, :, tt0:tt0 + ncol].rearrange("h p t -> p h t"), ya[:, :, 0:ncol], reads=[bya], writes=[b_ya])
                if seq_idx is not None:
                    ob = Buf(); out_bufs.append(ob)
                    fw.dma(SP, o_gla[l, seq_idx, d], S_f[:], reads=[b_S], writes=[ob])

        rs_s = dram_tmp("rs_s", [128, 4, T], BF16)
        b_rs = Buf()

        def phase1b(l, ti):
            cond = 1 if ti > 0 else 0
            xt, bxt = load_xT(ti)
            hT, bh = norm_mod(xt, bxt, A1, 0, cond)
            wt, bw = load_w8(w_in[l, :, O_AR:O_AR + 512], 512)
            rr, brr = gl_r.get()
            for hh in range(4):
                ps, bps = proj_fm(wt, bw, hh * 128, 128, lambda k: hT[:, k, :], [bh], 8)
                fw.op(ACT, lambda h, hh=hh, ps=ps: h.activation(out=rr[:, hh, :], in_=ps[:], func=AF.Silu), reads=[bps], writes=[brr])
            fw.dma(SP, rs_s[:, :, ti * TT:(ti + 1) * TT], rr[:], reads=[brr], writes=[b_rs])

        arena.reset()
        at_k = Ring(ov, "atk", [128, TKS], BF16, 2)
        at_v = Ring(ov, "atv", [128, TKS // 128, 128], BF16, 2)
        at_q = Ring(ov, "atq", [128, TT], BF16, 2)
        at_p = Ring(ov, "atp", [128, TT], BF16, 4)
        at_o = Ring(ov, "ato", [128, TT], BF16, 2)

        def mla_attn(l, groups):
            sc = 96 ** -0.5
            for (q0, nq, k0, nk) in groups:
                nkc = nk // 128
                for head in range(8):
                    kt, bkt = at_k.get(); vt, bvt = at_v.get()
                    fw.dma(SP, kt[0:96, 0:nk], k_s[head, :, k0:k0 + nk], reads=[b_k], writes=[bkt])
                    fw.dma(SP, vt[:, 0:nkc, :], v_s[head, :, k0 // 128:k0 // 128 + nkc, :], reads=[b_v], writes=[bvt])
                    for qq in range(q0, q0 + nq, TT):
                        nqt = min(TT, nq)
                        qt, bqt = at_q.get()
                        fw.dma(SP, qt[0:96, 0:nqt], q_s[head, :, qq:qq + nqt], reads=[b_q], writes=[bqt])
                        po, bpo = psB.get()

                        def qk(c, kt=kt, qt=qt, bkt=bkt, bqt=bqt, nqt=nqt):
                            ps, bps = psA.get()
                            fw.op(PE, lambda h: h.matmul(ps[:, 0:nqt], lhsT=kt[0:96, c * 128:(c + 1) * 128], rhs=qt[0:96, 0:nqt], start=True, stop=True),
                                  reads=[bkt, bqt], writes=[bps])
                            return ps, bps

                        nxt = qk(0)
                        for c in range(nkc):
                            ps, bps = nxt
                            if c + 1 < nkc:
                                nxt = qk(c + 1)
                            pt, bpt = at_p.get()
                            fw.op(ACT, lambda h, ps=ps, pt=pt: h.activation(out=pt[:, 0:nqt], in_=ps[:, 0:nqt], func=AF.Exp, scale=sc), reads=[bps], writes=[bpt])
                            fw.op(PE, lambda h, c=c, po=po, vt=vt, pt=pt: h.matmul(po[:, 0:nqt], lhsT=vt[:, c, :], rhs=pt[:, 0:nqt], start=(c == 0), stop=(c == nkc - 1)),
                                  reads=[bvt, bpt], writes=[bpo], inc=(c == nkc - 1))
                        rc, brc = f32_r.get()
                        fw.op(DVE, lambda h, po=po, rc=rc: h.reciprocal(out=rc[64:128, 0:nqt], in_=po[64:128, 0:nqt]), reads=[bpo], writes=[brc])
                        r2, br2 = f32_r.get()
                        fw.op(DVE, lambda h, rc=rc, r2=r2: h.tensor_copy(out=r2[0:64, 0:nqt], in_=rc[64:128, 0:nqt]), reads=[brc], writes=[br2])
                        ob, bob = at_o.get()
                        fw.op(DVE, lambda h, po=po, r2=r2, ob=ob: h.tensor_tensor(out=ob[0:64, 0:nqt], in0=po[0:64, 0:nqt], in1=r2[0:64, 0:nqt], op=ALU.mult), reads=[bpo, br2], writes=[bob])
                        fw.dma(SP, yb_s[head // 2, (head % 2) * 64:(head % 2) * 64 + 64, qq:qq + nqt], ob[0:64, 0:nqt], reads=[bob], writes=[b_yb])

        def diff_attn(l, groups, lam_init):
            sc = 64 ** -0.5
            for (q0, nq, k0, nk) in groups:
                nkc = nk // 128
                for head in range(4):
                    kt, bkt = at_k.get(); vt, bvt = at_v.get()
                    fw.dma(SP, kt[:, 0:nk], dk_s[head, :, k0:k0 + nk], reads=[b_dk], writes=[bkt])
                    fw.dma(SP, vt[:, 0:nkc, :], dv_s[head, :, k0 // 128:k0 // 128 + nkc, :], reads=[b_dv], writes=[bvt])
                    for qq in range(q0, q0 + nq, TT):
                        nqt = min(TT, nq)
                        qt, bqt = at_q.get()
                        fw.dma(SP, qt[:, 0:nqt], dq_s[head, :, qq:qq + nqt], reads=[b_dq], writes=[bqt])
                        acc = [psB.get() for _ in range(4)]

                        def qk(i, kt=kt, qt=qt, bkt=bkt, bqt=bqt, nqt=nqt):
                            c, comp = divmod(i, 2)
                            lo = comp * 64
                            ps, bps = psA.get()
                            fw.op(PE, lambda h: h.matmul(ps[:, 0:nqt], lhsT=kt[lo:lo + 64, c * 128:(c + 1) * 128], rhs=qt[lo:lo + 64, 0:nqt], start=True, stop=True),
                                  reads=[bkt, bqt], writes=[bps])
                            return ps, bps

                        nxt = qk(0)
                        for c in range(nkc):
                            for comp in range(2):
                                ps, bps = nxt
                                if c * 2 + comp + 1 < 2 * nkc:
                                    nxt = qk(c * 2 + comp + 1)
                                pt, bpt = at_p.get()
                                fw.op(ACT, lambda h, ps=ps, pt=pt: h.activation(out=pt[:, 0:nqt], in_=ps[:, 0:nqt], func=AF.Exp, scale=sc), reads=[bps], writes=[bpt])
                                po, bpo = acc[comp]
                                pl, bpl = acc[2 + comp]
                                fw.op(PE, lambda h, c=c, po=po, vt=vt, pt=pt: h.matmul(po[:, 0:nqt], lhsT=vt[:, c, :], rhs=pt[:, 0:nqt], start=(c == 0), stop=(c == nkc - 1)),
                                      reads=[bvt, bpt], writes=[bpo], inc=(c == nkc - 1))
                                fw.op(PE, lambda h, c=c, pl=pl, pt=pt: h.matmul(pl[:, 0:nqt], lhsT=cmat[:, C_ONE, :], rhs=pt[:, 0:nqt], start=(c == 0), stop=(c == nkc - 1)),
                                      reads=[b_cmat, bpt], writes=[bpl], inc=(c == nkc - 1))
                        r0, br0 = f32_r.get(); r1, br1 = f32_r.get()
                        fw.op(DVE, lambda h, r0=r0, p=acc[2][0]: h.reciprocal(out=r0[:, 0:nqt], in_=p[:, 0:nqt]), reads=[acc[2][1]], writes=[br0])
                        fw.op(DVE, lambda h, r1=r1, p=acc[3][0]: h.reciprocal(out=r1[:, 0:nqt], in_=p[:, 0:nqt]), reads=[acc[3][1]], writes=[br1])
                        o0, bo0 = f32_r.get(); o1, bo1 = f32_r.get()
                        fw.op(DVE, lambda h, o0=o0, r0=r0, p=acc[0][0]: h.tensor_tensor(out=o0[:, 0:nqt], in0=p[:, 0:nqt], in1=r0[:, 0:nqt], op=ALU.mult), reads=[acc[0][1], br0], writes=[bo0])
                        fw.op(DVE, lambda h, o1=o1, r1=r1, p=acc[1][0]: h.tensor_tensor(out=o1[:, 0:nqt], in0=p[:, 0:nqt], in1=r1[:, 0:nqt], op=ALU.mult), reads=[acc[1][1], br1], writes=[bo1])
                        fw.op(DVE, lambda h, o0=o0, o1=o1: h.scalar_tensor_tensor(out=o0[:, 0:nqt], in0=o1[:, 0:nqt], scalar=lamc[:, 0:1], in1=o0[:, 0:nqt], op0=ALU.mult, op1=ALU.add),
                              reads=[bo1, b_lamc], writes=[bo0])
                        sq, bsq = bf_r.get()
                        fw.op(ACT, lambda h, sq=sq, o0=o0: h.activation(out=sq[:, 0:nqt], in_=o0[:, 0:nqt], func=AF.Square), reads=[bo0], writes=[bsq])
                        ps, bps = psA.get()
                        fw.op(PE, lambda h, ps=ps, sq=sq: h.matmul(ps[:, 0:nqt], lhsT=cmat[:, C_128, :], rhs=sq[:, 0:nqt], start=True, stop=True), reads=[bsq, b_cmat], writes=[bps])
                        rs, brs = rs_r.get()
                        rsqrt_eps(ps[:, 0:nqt], rs[:, 0:nqt], bps, brs)
                        t1, bt1 = f32_r.get()
                        fw.op(DVE, lambda h, t1=t1, o0=o0, rs=rs: h.scalar_tensor_tensor(out=t1[:, 0:nqt], in0=o0[:, 0:nqt], scalar=gc[:, 10:11], in1=rs[:, 0:nqt], op0=ALU.mult, op1=ALU.mult),
                              reads=[bo0, brs, b_gc], writes=[bt1])
                        ob, bob = at_o.get()
                        fw.op(ACT, lambda h, t1=t1, ob=ob: h.activation(out=ob[:, 0:nqt], in_=t1[:, 0:nqt], func=AF.Copy, scale=(1.0 - lam_init)), reads=[bt1], writes=[bob])
                        fw.dma(SP, yc_s[head, :, qq:qq + nqt], ob[:, 0:nqt], reads=[bob], writes=[b_yc])

        arena.reset()
        y3_r = Ring(ov, "y3", [128, 12, TT], BF16, 1)
        mg_r = Ring(ov, "mg", [128, 8, TT], BF16, 1)
        a_r = Ring(ov, "aT", [128, 32, TT], BF16, 1)
        xo_r = Ring(ov, "xo", [128, 4, D], F32, 1)

        def phase3(l, ti, last):
            cond = 1 if ti > 0 else 0
            t0 = ti * TT
            xt, bxt = load_xT(ti)
            hT, bh = norm_mod(xt, bxt, A1, 0, cond)
            y3, by3 = y3_r.get()
            for bi, (src, bsrc) in enumerate(((ya_s, b_ya), (yb_s, b_yb), (yc_s, b_yc))):
                fw.dma(SP, y3[:, bi * 4:(bi + 1) * 4, :], src[:, :, t0:t0 + TT].rearrange("h p t -> p h t"), reads=[bsrc], writes=[by3])
            mg, bmg = mg_r.get()
            for oc in range(8):
                wg, bwg = w8.get()
                for bi in range(3):
                    c0 = O_G + bi * 1024 + oc * 128
                    fw.dma(POOL, wg[:, :, bi * 128:(bi + 1) * 128], w_in[l, :, c0:c0 + 128].rearrange("(k p) c -> p k c", p=128), writes=[bwg])
                wo, bwo = w4.get()
                for bi in range(3):
                    fw.dma(POOL, wo[:, :, bi * 128:(bi + 1) * 128], w_o3[l, bi, :, oc * 128:(oc + 1) * 128].rearrange("(k p) c -> p k c", p=128), writes=[bwo])
                acc_t, bacc = f32_r.get()
                for bi in range(3):
                    psg, bpsg = proj_fm(wg, bwg, bi * 128, 128, lambda k: hT[:, k, :], [bh], 8)
                    sg, bsg = f32_r.get()
                    fw.op(ACT, lambda h, psg=psg, sg=sg: h.activation(out=sg[:], in_=psg[:], func=AF.Sigmoid), reads=[bpsg], writes=[bsg])
                    pso, bpso = proj_fm(wo, bwo, bi * 128, 128, lambda k, bi=bi: y3[:, bi * 4 + k, :], [by3], 4)
                    if bi == 0:
                        fw.op(DVE, lambda h, sg=sg, pso=pso, a=acc_t: h.tensor_tensor(out=a[:], in0=sg[:], in1=pso[:], op=ALU.mult), reads=[bsg, bpso], writes=[bacc])
                    else:
                        tmp, btmp = f32_r.get()
                        fw.op(DVE, lambda h, sg=sg, pso=pso, tmp=tmp: h.tensor_tensor(out=tmp[:], in0=sg[:], in1=pso[:], op=ALU.mult), reads=[bsg, bpso], writes=[btmp])
                        if bi == 1:
                            fw.op(DVE, lambda h, tmp=tmp, a=acc_t: h.tensor_tensor(out=a[:], in0=a[:], in1=tmp[:], op=ALU.add), reads=[btmp], writes=[bacc])
                        else:
                            fw.op(DVE, lambda h, tmp=tmp, a=acc_t, oc=oc: h.tensor_tensor(out=mg[:, oc, :], in0=a[:], in1=tmp[:], op=ALU.add), reads=[btmp, bacc], writes=[bmg])
            for half in range(2):
                wt, bw = load_w8(w_out[l, :, half * 512:(half + 1) * 512], 512)
                for j in range(4):
                    oc = half * 4 + j
                    ps, bps = proj_fm(wt, bw, j * 128, 128, lambda k: mg[:, k, :], [bmg], 8)
                    fw.op(DVE, lambda h, oc=oc, ps=ps: h.scalar_tensor_tensor(out=xt[:, oc, :], in0=ps[:], scalar=modT[:, 16 + oc, cond:cond + 1], in1=xt[:, oc, :], op0=ALU.mult, op1=ALU.add),
                          reads=[bps, b_modT_b], writes=[bxt])
            h2, bh2 = norm_mod(xt, bxt, A2, 24, cond)
            aT, baT = a_r.get()
            for fb in range(8):
                wt, bw = load_w8(w_m1[l, :, fb * 512:(fb + 1) * 512], 512)
                for j in range(4):
                    ps, bps = proj_fm(wt, bw, j * 128, 128, lambda k: h2[:, k, :], [bh2], 8)
                    rl, brl = f32_r.get()
                    fw.op(ACT, lambda h, ps=ps, rl=rl: h.activation(out=rl[:], in_=ps[:], func=AF.Relu), reads=[bps], writes=[brl])
                    fw.op(DVE, lambda h, rl=rl, fb=fb, j=j: h.tensor_tensor(out=aT[:, fb * 4 + j, :], in0=rl[:], in1=rl[:], op=ALU.mult), reads=[brl], writes=[baT])
            for oc in range(8):
                ps, bps = psB.get()
                for kb4 in range(8):
                    wt, bw = w4.get()
                    fw.dma(POOL, wt[:, :, 0:128], w_m2[l, kb4 * 512:(kb4 + 1) * 512, oc * 128:(oc + 1) * 128].rearrange("(k p) c -> p k c", p=128), writes=[bw])
                    for k in range(4):
                        kk = kb4 * 4 + k
                        fw.op(PE, lambda h, k=k, kk=kk, ps=ps, wt=wt: h.matmul(ps[:], lhsT=wt[:, k, 0:128], rhs=aT[:, kk, :], start=(kk == 0), stop=(kk == 31)),
                              reads=[bw, baT], writes=[bps], inc=(k == 3))
                fw.op(DVE, lambda h, oc=oc, ps=ps: h.scalar_tensor_tensor(out=xt[:, oc, :], in0=ps[:], scalar=modT[:, 40 + oc, cond:cond + 1], in1=xt[:, oc, :], op0=ALU.mult, op1=ALU.add),
                      reads=[bps, b_modT_b], writes=[bxt])
            if not last:
                fw.dma(SP, xT_s[:, :, t0:t0 + TT].rearrange("k p t -> p k t"), xt[:], reads=[bxt], writes=[b_xT[ti]])
            else:
                xo, bxo = xo_r.get()
                for g in range(4):
                    for half in range(2):
                        ps, bps = psA.get()
                        for k in range(4):
                            kk = half * 4 + k
                            fw.op(PE, lambda h, g=g, k=k, kk=kk, ps=ps: h.transpose(ps[:, k * 128:(k + 1) * 128], xt[:, kk, g * 128:(g + 1) * 128], ident[:]),
                                  reads=[bxt, b_ident], writes=[bps], inc=(k == 3))
                        evac(ps[:], xo[:, g, half * 512:(half + 1) * 512], bps, bxo, eng=(ACT if half else DVE))
                ob = Buf(); out_bufs.append(ob)
                fw.dma(SP, y_out[t0:t0 + TT, :].rearrange("(g p) f -> p g f", p=128), xo[:], reads=[bxo], writes=[ob])

        import os
        KSTOP = int(os.environ.get("KSTOP", "99"))
        step = [0]

        def reached():
            step[0] += 1
            return step[0] > KSTOP

        for l in range(L):
            if reached():
                break
            lam_init = layer_setup(l)
            fw.barrier()
            vaug_init[0] = vaug_init[1] = False
            if reached():
                break
            prep_cache(l)
            if reached():
                break
            for ti in range(NT):
                phase1(l, ti)
            fw.barrier()
            if reached():
                break
            seqs = [(0, 256, 0, False), (256, 256, 1, False), (NP_TOK, NS_TOK, None, True)]
            gla_dir(l, 0, seqs)
            fw.barrier()
            gla_dir(l, 1, seqs)
            fw.barrier()
            if reached():
                break
            groups = [(0, 256, 0, 256), (256, 256, 256, 256), (NP_TOK, NS_TOK, NP_TOK, TKS)]
            mla_attn(l, groups)
            if reached():
                break
            diff_attn(l, groups, lam_init)
            if reached():
                break
            fw.barrier()
            for ti in range(NT):
                phase3(l, ti, l == L - 1)
            fw.barrier()

        fw.finish(out_bufs)
    return nc


def _rope_tables():
    t = np.arange(NS_TOK)
    row, col = t // 64, t % 64

    def cs(nf, pos):
        inv = (10000.0 ** (-np.arange(nf, dtype=np.float32) / nf)).astype(np.float32)
        ang = pos.astype(np.float32)[None, :] * inv[:, None]
        c, s = np.cos(ang).astype(np.float32), np.sin(ang).astype(np.float32)
        return np.concatenate([c, c], 0), np.concatenate([s, s], 0)

    cr, sr = cs(8, row); cc, sc_ = cs(8, col)
    cM = np.concatenate([cr, cc, np.ones((64, NS_TOK), np.float32)], 0)
    sM = np.concatenate([sr, sc_, np.zeros((64, NS_TOK), np.float32)], 0)
    cr, sr = cs(16, row); cc, sc_ = cs(16, col)
    c64 = np.concatenate([cr, cc], 0); s64 = np.concatenate([sr, sc_], 0)
    cD = np.concatenate([c64, c64], 0); sD = np.concatenate([s64, s64], 0)
    return np.stack([cM, sM]).astype(np.float32), np.stack([cD, sD]).astype(np.float32)


def _rot_matrix(nf):
    R = np.zeros((2 * nf, 2 * nf), np.float32)
    for i in range(nf):
        R[i, nf + i] = -1.0
        R[nf + i, i] = 1.0
    return R


def _consts():
    bf = ml_dtypes.bfloat16
    cmat = np.zeros((128, 8, 128), np.float32)
    cmat[:, 0, :] = 1.0 / 1024
    cmat[:, 1, :] = 1.0 / 384
    cmat[:, 2, :] = 1.0 / 256
    cmat[:96, 3, :96] = 1.0 / 96
    cmat[:64, 4, :64] = 1.0 / 64
    cmat[64:, 4, 64:] = 1.0 / 64
    cmat[:, 5, :] = 1.0 / 128
    cmat[:, 6, :] = 1.0
    cmat[:32, 7, :32] = np.eye(32)
    pm = np.zeros((128, 3, 128), np.float32)
    R96 = np.zeros((96, 96), np.float32)
    R96[0:16, 0:16] = _rot_matrix(8); R96[16:32, 16:32] = _rot_matrix(8)
    pm[:96, 0, :96] = R96.T
    R64 = np.zeros((64, 64), np.float32)
    R64[0:32, 0:32] = _rot_matrix(16); R64[32:64, 32:64] = _rot_matrix(16)
    R128 = np.zeros((128, 128), np.float32)
    R128[:64, :64] = R64; R128[64:, 64:] = R64
    pm[:, 1, :] = R128.T
    i = np.arange(64)
    tri = np.zeros((64, 4, 64), np.float32)
    tri[:, 0, :] = (i[:, None] <= i[None, :])
    tri[:, 1, :] = (i[:, None] >= i[None, :])
    tri[:, 2, :] = (i[:, None] > i[None, :])
    tri[:, 3, :] = (i[:, None] < i[None, :])
    return cmat.astype(bf), pm.astype(bf), tri


_PROG = {}


def _get_prog():
    if "nc" not in _PROG:
        _PROG["nc"] = build_program()
    return _PROG["nc"]


def make_in_maps(x_prompt, x_sample, state_gla, cache_mla_ckv, cache_mla_krope, cache_diff_k, cache_diff_v, c, c_ctx, w_mod, b_mod, g_norm1, g_norm2, w_in,
                 w_gla_a2, b_gla_a, g_gla_out, g_mla_qa, g_mla_kva, w_mla_uq, w_mla_uk, w_mla_uv, g_mla_q, g_mla_k, g_diff_q, g_diff_k, lam_qk, g_diff_sub,
                 w_o_gla, w_o_mla, w_o_diff, w_out, w_mlp1, w_mlp2):
    f = lambda a: np.ascontiguousarray(np.asarray(a, dtype=np.float32))
    cmat, pm, tri = _consts()
    ropeM, ropeD = _rope_tables()
    perm = np.concatenate([np.arange(64, 96), np.arange(0, 64)])
    w_uq_p = f(w_mla_uq).reshape(L, 384, 8, 96)[:, :, :, perm]
    w_ukp = np.zeros((L, 256, 8, 96), np.float32)
    w_ukp[:, :, :, 32:] = f(w_mla_uk).reshape(L, 256, 8, 64)
    gcol = np.zeros((L, 128, 12), np.float32)
    gcol[:, :, 0:3] = f(g_mla_qa).reshape(L, 3, 128).transpose(0, 2, 1)
    gcol[:, :, 3:5] = f(g_mla_kva).reshape(L, 2, 128).transpose(0, 2, 1)
    gcol[:, :96, 5] = f(g_mla_q)[:, perm]
    gcol[:, :96, 6] = f(g_mla_k)[:, perm]
    gcol[:, :, 7] = np.tile(f(g_diff_q), (1, 2))
    gcol[:, :, 8] = np.tile(f(g_diff_k), (1, 2))
    gcol[:, :, 9] = f(g_gla_out)
    gcol[:, :, 10] = f(g_diff_sub)
    w_a2bd = np.zeros((L, 32, 512), np.float32)
    w_a2bd[:, 0:16, 0:256] = f(w_gla_a2)[:, 0]
    w_a2bd[:, 16:32, 256:512] = f(w_gla_a2)[:, 1]
    b_a = f(b_gla_a).reshape(L, 1, 512)
    gnT = np.concatenate([f(g_norm1).reshape(L, 8, 128).transpose(0, 2, 1), f(g_norm2).reshape(L, 8, 128).transpose(0, 2, 1)], axis=2)
    b_modT = f(b_mod).reshape(L, 48, 128).transpose(0, 2, 1)
    w_o3 = np.stack([f(w_o_gla), f(w_o_mla), f(w_o_diff)], axis=1)
    shared = dict(w_mod=f(w_mod), b_modT=f(b_modT), gnT=f(gnT), w_in=f(w_in), w_a2bd=w_a2bd, b_a=b_a, gcol=gcol, w_uq=f(w_uq_p), w_ukp=w_ukp,
                  w_uv=f(w_mla_uv), lam_qk=f(lam_qk).reshape(L, 1, 256), w_o3=f(w_o3), w_out=f(w_out), w_m1=f(w_mlp1), w_m2=f(w_mlp2),
                  ident=np.eye(128, dtype=np.float32), cmat=cmat, pmat=pm, tri=tri, ropeM=ropeM, ropeD=ropeD)
    xp, xs = f(x_prompt), f(x_sample)
    in_maps = []
    for core in range(8):
        b = core % 4
        m = dict(shared)
        m["xin"] = np.concatenate([xp[2 * core], xp[2 * core + 1], xs[b]], axis=0)
        cond = np.stack([f(c_ctx), f(c)[b]], axis=0)
        m["condT"] = f(cond.reshape(2, 8, 128).transpose(2, 1, 0))
        m["st_gla"] = f(f(state_gla)[b].transpose(0, 1, 3, 2, 4))
        m["c_ckv"] = f(cache_mla_ckv)[b]
        m["c_kr"] = f(cache_mla_krope)[b]
        m["c_dk"] = f(cache_diff_k)[b].reshape(L, PAST, 512)
        m["c_dv"] = f(cache_diff_v)[b].reshape(L, PAST, 512)
        in_maps.append(m)
    return in_maps


def kernel(**inputs):
    nc = _get_prog()
    in_maps = make_in_maps(**inputs)
    res = run_bass_kernel_spmd(nc, in_maps, core_ids=list(range(8)))
    R = res.results
    y_prompt = np.zeros((16, 256, D), np.float32)
    y_sample = np.zeros((4, NS_TOK, D), np.float32)
    n_gla = np.zeros((16, L, 2, 4, 64, 128), np.float32)
    n_ckv = np.zeros((16, L, 256, 256), np.float32)
    n_kr = np.zeros((16, L, 256, 32), np.float32)
    n_dk = np.zeros((16, L, 256, 4, 2, 64), np.float32)
    n_dv = np.zeros((16, L, 256, 4, 128), np.float32)
    for core in range(8):
        r = R[core]
        for s in range(2):
            bi = 2 * core + s
            y_prompt[bi] = r["y_out"][s * 256:(s + 1) * 256]
            n_gla[bi] = r["o_gla"][:, s].transpose(0, 1, 3, 2, 4)
            n_ckv[bi] = r["o_ckv"][:, s * 256:(s + 1) * 256]
            n_kr[bi] = r["o_kr"][:, s * 256:(s + 1) * 256]
            n_dk[bi] = r["o_dk"][:, s * 256:(s + 1) * 256].reshape(L, 256, 4, 2, 64)
            n_dv[bi] = r["o_dv"][:, s * 256:(s + 1) * 256].reshape(L, 256, 4, 128)
        if core < 4:
            y_sample[core] = r["y_out"][NP_TOK:]
    return (y_prompt, y_sample, n_gla, n_ckv, n_kr, n_dk, n_dv)
```

```python
import math
import numpy as np
import ml_dtypes
from contextlib import ExitStack
import concourse.bass as bass
import concourse.mybir as mybir
from concourse.bass_utils import run_bass_kernel_spmd

F32 = mybir.dt.float32
BF16 = mybir.dt.bfloat16
AF = mybir.ActivationFunctionType
ALU = mybir.AluOpType

D = 1024
L = 2
NP_TOK = 512
NS_TOK = 4096
T = NP_TOK + NS_TOK
TT = 512
NT = T // TT
PAST = 256
TKS = PAST + NS_TOK
EPS = 1e-6
O_AQ, O_AK, O_AV, O_AR, O_AA, O_QD, O_KVD, O_KR, O_DQ, O_DK, O_DV, O_G = 0, 256, 512, 1024, 1536, 1568, 1952, 2208, 2240, 2752, 3264, 3776
IN_COLS = 6848


class Buf:
    __slots__ = ("lw", "rd", "psum")

    def __init__(self, psum=False):
        self.lw = None
        self.rd = []
        self.psum = psum


class _Rec:
    def __init__(self):
        self.call = None

    def __getattr__(self, name):
        def f(*a, **k):
            self.call = (name, a, k)
            return self
        return f


class Eng:
    def __init__(self, fw, name, handle):
        self.name = name
        self.h = handle
        self.sem = fw.new_sem("e_" + name)
        self.count = 0
        self.waited = {}
        self.thunks = []
        self.snaps = {}


class FW:
    def __init__(self, nc, ctx, n_dma_sems=48):
        self.nc = nc
        self.ctx = ctx
        self.pe = Eng(self, "pe", nc.tensor)
        self.dve = Eng(self, "dve", nc.vector)
        self.act = Eng(self, "act", nc.scalar)
        self.pool = Eng(self, "pool", nc.gpsimd)
        self.sp = Eng(self, "sp", nc.sync)
        self.dma_sems = [self.new_sem(f"d{i}") for i in range(n_dma_sems)]
        self.dma_cnt = [0] * n_dma_sems
        self.snap_of = {}
        self.dma_rr = 0
        self.ew_rr = 0

    def new_sem(self, name):
        return self.ctx.enter_context(self.nc.semaphore(name))

    def _wait(self, E, ev):
        if ev is None:
            return
        sem, val = ev
        if sem is E.sem and E.name == "pe":
            return
        key = id(sem)
        if E.waited.get(key, 0) >= val:
            return
        E.waited[key] = val
        E.thunks.append(lambda h=E.h, s=sem, v=val: h.wait_ge(s, v))
        snap = self.snap_of.get((key, val))
        if snap:
            w = E.waited
            for k2, v2 in snap.items():
                if w.get(k2, 0) < v2:
                    w[k2] = v2

    def _deps(self, E, reads, writes):
        for b in reads:
            self._wait(E, b.lw)
        for b in writes:
            self._wait(E, b.lw)
            for ev in b.rd:
                self._wait(E, ev)

    def _post(self, ev, reads, writes):
        for b in reads:
            b.rd.append(ev)
            if len(b.rd) > 64:
                b.rd = b.rd[-48:]
        for b in writes:
            b.lw = ev
            b.rd = []

    def op(self, E, fn, reads=(), writes=(), inc=True):
        if any(b.psum for b in reads):
            writes = list(writes) + [b for b in reads if b.psum]
            reads = [b for b in reads if not b.psum]
        self._deps(E, reads, writes)
        rec = _Rec()
        fn(rec)
        mname, margs, mkw = rec.call
        if inc:
            E.count += 1
            ev = (E.sem, E.count)
            self.snap_of[(id(E.sem), E.count)] = dict(E.waited)
            E.thunks.append(lambda h=E.h, n=mname, a=margs, k=mkw, s=E.sem: getattr(h, n)(*a, **k).then_inc(s, 1))
        else:
            ev = (E.sem, E.count + 1)
            E.thunks.append(lambda h=E.h, n=mname, a=margs, k=mkw: getattr(h, n)(*a, **k))
        self._post(ev, reads, writes)
        return ev

    def dma(self, Q, out_ap, in_ap, reads=(), writes=(), **kw):
        self._deps(Q, reads, writes)
        i = self.dma_rr
        self.dma_rr = (self.dma_rr + 1) % len(self.dma_sems)
        self.dma_cnt[i] += 16
        sem = self.dma_sems[i]
        ev = (sem, self.dma_cnt[i])
        self.snap_of[(id(sem), self.dma_cnt[i])] = dict(Q.waited)
        Q.thunks.append(lambda h=Q.h, o=out_ap, a=in_ap, s=sem, k=kw: h.dma_start(out=o, in_=a, **k).then_inc(s, 16))
        self._post(ev, reads, writes)
        return ev

    def barrier(self):
        engs = [self.pe, self.dve, self.act, self.pool, self.sp]
        for E in engs:
            for E2 in engs:
                if E2 is not E and E2.count > 0:
                    self._wait(E, (E2.sem, E2.count))
            for i, sem in enumerate(self.dma_sems):
                if self.dma_cnt[i] > 0:
                    self._wait(E, (sem, self.dma_cnt[i]))

    def finish(self, final_bufs):
        for b in final_bufs:
            self._wait(self.sp, b.lw)
        nc = self.nc
        engs = self
        with nc.Block() as block:
            @block.tensor
            def _(e):
                for t in engs.pe.thunks:
                    t()

            @block.vector
            def _(e):
                for t in engs.dve.thunks:
                    t()

            @block.scalar
            def _(e):
                for t in engs.act.thunks:
                    t()

            @block.gpsimd
            def _(e):
                for t in engs.pool.thunks:
                    t()

            @block.sync
            def _(e):
                for t in engs.sp.thunks:
                    t()


class Arena:
    def __init__(self, tensor, nbytes):
        self.t = tensor
        self.nbytes = nbytes
        self.off = 0
        self.peak = 0

    def reset(self):
        self.off = 0

    def alloc(self, name, shape, dt):
        esz = 2 if dt == BF16 else 4
        n = 1
        for d in shape[1:]:
            n *= d
        nb = (n * esz + 63) // 64 * 64
        assert self.off + nb <= self.nbytes, (name, self.off, nb, self.nbytes)
        ap = self.t[0:shape[0], self.off // 4:(self.off + nb) // 4]
        self.off += nb
        self.peak = max(self.peak, self.off)
        if dt == BF16:
            ap = ap.bitcast(BF16)
        ap = ap[:, 0:n]
        if len(shape) == 3:
            ap = ap.rearrange("p (a b) -> p a b", b=shape[2])
        elif len(shape) == 4:
            ap = ap.rearrange("p (a b c) -> p a b c", b=shape[2], c=shape[3])
        return ap


class Ring:
    def __init__(self, alloc, name, shape, dt, n, psum=False):
        self.t = [alloc(f"{name}{i}", shape, dt) for i in range(n)]
        self.b = [Buf(psum) for _ in range(n)]
        self.i = 0

    def get(self):
        i = self.i
        self.i = (i + 1) % len(self.t)
        return self.t[i], self.b[i]


def build_program(debug=False):
    nc = bass.Bass("TRN2", target_bir_lowering=False)
    dram_in = lambda name, shape, dt=F32: nc.dram_tensor(name, list(shape), dt, kind="ExternalInput").ap()
    dram_out = lambda name, shape, dt=F32: nc.dram_tensor(name, list(shape), dt, kind="ExternalOutput").ap()
    dram_tmp = lambda name, shape, dt=F32: nc.dram_tensor(name, list(shape), dt).ap()

    xin = dram_in("xin", [T, D])
    condT = dram_in("condT", [128, 8, 2])
    st_gla = dram_in("st_gla", [L, 2, 64, 4, 128])
    c_ckv = dram_in("c_ckv", [L, PAST, 256])
    c_kr = dram_in("c_kr", [L, PAST, 32])
    c_dk = dram_in("c_dk", [L, PAST, 512])
    c_dv = dram_in("c_dv", [L, PAST, 512])
    w_mod = dram_in("w_mod", [L, D, 6 * D])
    b_modT = dram_in("b_modT", [L, 128, 48])
    gnT = dram_in("gnT", [L, 128, 16])
    w_in = dram_in("w_in", [L, D, IN_COLS])
    w_a2bd = dram_in("w_a2bd", [L, 32, 512])
    b_a = dram_in("b_a", [L, 1, 512])
    gcol = dram_in("gcol", [L, 128, 12])
    w_uq = dram_in("w_uq", [L, 384, 8, 96])
    w_ukp = dram_in("w_ukp", [L, 256, 8, 96])
    w_uv = dram_in("w_uv", [L, 256, 512])
    lam_qk = dram_in("lam_qk", [L, 1, 256])
    w_o3 = dram_in("w_o3", [L, 3, 512, D])
    w_out = dram_in("w_out", [L, D, D])
    w_m1 = dram_in("w_m1", [L, D, 4 * D])
    w_m2 = dram_in("w_m2", [L, 4 * D, D])
    ident_d = dram_in("ident", [128, 128])
    cmat_d = dram_in("cmat", [128, 8, 128], BF16)
    pmat_d = dram_in("pmat", [128, 3, 128], BF16)
    tri_d = dram_in("tri", [64, 4, 64])
    ropeM = dram_in("ropeM", [2, 96, NS_TOK])
    ropeD = dram_in("ropeD", [2, 128, NS_TOK])

    y_out = dram_out("y_out", [T, D])
    o_gla = dram_out("o_gla", [L, 2, 2, 64, 4, 128])
    o_ckv = dram_out("o_ckv", [L, NP_TOK, 256])
    o_kr = dram_out("o_kr", [L, NP_TOK, 32])
    o_dk = dram_out("o_dk", [L, NP_TOK, 512])
    o_dv = dram_out("o_dv", [L, NP_TOK, 512])

    mk = dram_out if debug else dram_tmp
    xT_s = dram_tmp("xT_s", [8, 128, T])
    gq_s = dram_tmp("gq_s", [64, 4, T], BF16)
    gk_s = dram_tmp("gk_s", [64, 4, T], BF16)
    gkt_s = dram_tmp("gkt_s", [T, 256], BF16)
    gvt_s = dram_tmp("gvt_s", [T, 512], BF16)
    gG_s = dram_tmp("gG_s", [T, 512])
    go_s = dram_tmp("go_s", [128, 4, T])
    q_s = dram_tmp("q_s", [8, 96, T], BF16)
    k_s = dram_tmp("k_s", [8, 96, NP_TOK + TKS], BF16)
    v_s = dram_tmp("v_s", [8, 128, (NP_TOK + TKS) // 128, 128], BF16)
    dq_s = dram_tmp("dq_s", [4, 128, T], BF16)
    dk_s = dram_tmp("dk_s", [4, 128, NP_TOK + TKS], BF16)
    dv_s = dram_tmp("dv_s", [4, 128, (NP_TOK + TKS) // 128, 128], BF16)
    ya_s = mk("ya_s", [4, 128, T], BF16)
    yb_s = mk("yb_s", [4, 128, T], BF16)
    yc_s = mk("yc_s", [4, 128, T], BF16)

    with ExitStack() as ctx:
        fw = FW(nc, ctx)
        PE, DVE, ACT, POOL, SP = fw.pe, fw.dve, fw.act, fw.pool, fw.sp
        sb = lambda name, shape, dt=F32: ctx.enter_context(nc.sbuf_tensor("s_" + name, list(shape), dt))
        psb = lambda name, shape, dt=F32: ctx.enter_context(nc.psum_tensor("p_" + name, list(shape), dt))

        psA = Ring(psb, "psA", [128, 512], F32, 4, psum=True)
        psB = Ring(psb, "psB", [128, 512], F32, 4, psum=True)

        OVB = 72 * 1024
        arena = Arena(sb("arena", [128, OVB // 4], F32), OVB)
        ov = arena.alloc

        def ew():
            fw.ew_rr ^= 1
            return DVE if fw.ew_rr else POOL

        ident = sb("ident", [128, 128]); b_ident = Buf()
        cmat = sb("cmat", [128, 8, 128], BF16); b_cmat = Buf()
        pmat = sb("pmat", [128, 3, 128], BF16); b_pmat = Buf()
        tri = sb("tri", [64, 4, 64]); b_tri = Buf()
        ones_row = sb("ones_row", [1, 128], BF16); b_onesrow = Buf()
        ones_rowf = sb("ones_rowf", [1, 128]); b_onesrowf = Buf()
        ones_col = sb("ones_col", [64, 1]); b_onescol = Buf()
        fw.dma(SP, ident[:], ident_d, writes=[b_ident])
        fw.dma(SP, cmat[:], cmat_d, writes=[b_cmat])
        fw.dma(SP, pmat[:], pmat_d, writes=[b_pmat])
        fw.dma(SP, tri[:], tri_d, writes=[b_tri])
        fw.op(DVE, lambda h: h.memset(ones_row[:], 1.0), writes=[b_onesrow])
        fw.op(DVE, lambda h: h.memset(ones_rowf[:], 1.0), writes=[b_onesrowf])
        fw.op(DVE, lambda h: h.memset(ones_col[:], 1.0), writes=[b_onescol])
        C_1024, C_384, C_256, C_96, C_BLK64, C_128, C_ONE, C_SEL32 = range(8)
        P_96, P_128, _ = range(3)

        cT = sb("cT", [128, 8, 2]); b_cT = Buf()
        scT = sb("scT", [128, 8, 2]); b_scT = Buf()
        fw.dma(SP, cT[:], condT, writes=[b_cT])
        fw.op(ACT, lambda h: h.activation(out=scT[:], in_=cT[:], func=AF.Silu), reads=[b_cT], writes=[b_scT])

        modT = sb("modT", [128, 48, 2]); b_modT_b = Buf()
        bmod = sb("bmod", [128, 48]); b_bmod = Buf()
        gn = sb("gn", [128, 16]); b_gn = Buf()
        A1 = sb("A1", [128, 8, 2]); A2 = sb("A2", [128, 8, 2]); b_A = Buf()
        gc = sb("gc", [128, 12]); b_gc = Buf()
        wa2 = sb("wa2", [32, 512], BF16); b_wa2 = Buf()
        ba = sb("ba", [1, 512], BF16); b_ba = Buf()
        lamt = sb("lamt", [1, 256]); b_lamt = Buf()
        lam1 = sb("lam1", [1, 8]); b_lam1 = Buf()
        lamc = sb("lamc", [128, 2]); b_lamc = Buf()
        arena.reset()
        wmod_r = Ring(ov, "wmod", [128, 8, 256], F32, 2)

        w8 = Ring(sb, "w8", [128, 8, 512], BF16, 3)
        w4 = Ring(sb, "w4", [128, 4, 1024], BF16, 2)

        def load_w8(src_ap_rows_cols, ncols, nk=8):
            t, b = w8.get()
            fw.dma(POOL, t[:, 0:nk, 0:ncols], src_ap_rows_cols.rearrange("(k p) c -> p k c", p=128), writes=[b])
            return t, b

        xt_r = Ring(sb, "xt", [128, 8, TT], F32, 1)
        sq_r = Ring(sb, "sq", [128, 8, TT], BF16, 1)
        hT_r = Ring(sb, "hT", [128, 8, TT], BF16, 2)
        rs_r = Ring(sb, "rs", [128, TT], F32, 3)
        f32_r = Ring(sb, "f32", [128, TT], F32, 6)
        bf_r = Ring(sb, "bf", [128, TT], BF16, 8)

        def evac(ps_ap, out_ap, b_ps, b_out, eng=None, func=None, scale=1.0):
            if func is not None or eng is ACT:
                f = func if func is not None else AF.Copy
                return fw.op(ACT, lambda h: h.activation(out=out_ap, in_=ps_ap, func=f, scale=scale), reads=[b_ps], writes=[b_out])
            return fw.op(DVE, lambda h: h.tensor_copy(out=out_ap, in_=ps_ap), reads=[b_ps], writes=[b_out])

        b_xT = [Buf() for _ in range(NT)]
        arena.reset()
        xtok_r = Ring(ov, "xtok", [128, 4, D], F32, 2)

        def phase0(ti):
            xk, bxk = xtok_r.get()
            fw.dma(SP, xk[:], xin[ti * TT:(ti + 1) * TT, :].rearrange("(g p) f -> p g f", p=128), writes=[bxk])
            xt, bxt = xt_r.get()
            for k in range(8):
                ps, bps = psA.get()
                for g in range(4):
                    fw.op(PE, lambda h, ps=ps, g=g, k=k: h.transpose(ps[:, g * 128:(g + 1) * 128], xk[:, g, k * 128:(k + 1) * 128], ident[:]),
                          reads=[bxk, b_ident], writes=[bps], inc=(g == 3))
                evac(ps[:], xt[:, k, :], bps, bxt, eng=(ACT if k % 2 else DVE))
            fw.dma(SP, xT_s[:, :, ti * TT:(ti + 1) * TT].rearrange("k p t -> p k t"), xt[:], reads=[bxt], writes=[b_xT[ti]])

        import os
        for ti in range(int(os.environ.get("KPH0", str(NT)))):
            phase0(ti)
        fw.barrier()

        def rsqrt_eps(src_ap, dst_ap, bsrc, bdst):
            fw.op(ACT, lambda h: h.activation(out=dst_ap, in_=src_ap, func=AF.Ln, bias=EPS), reads=[bsrc], writes=[bdst])
            fw.op(ACT, lambda h: h.activation(out=dst_ap, in_=dst_ap, func=AF.Exp, scale=-0.5), reads=[bdst], writes=[bdst])

        def rms_rstd(sq_aps, ones_ap, nparts, eps=EPS):
            ps, bps = psA.get()
            n = len(sq_aps)
            for i, (a, b) in enumerate(sq_aps):
                fw.op(PE, lambda h, a=a, i=i: h.matmul(ps[0:nparts, :], lhsT=ones_ap, rhs=a, start=(i == 0), stop=(i == n - 1)),
                      reads=[b, b_cmat], writes=[bps], inc=(i == n - 1))
            rs, brs = rs_r.get()
            rsqrt_eps(ps[0:nparts, :], rs[0:nparts, :], bps, brs)
            return rs, brs

        def load_xT(ti):
            xt, bxt = xt_r.get()
            fw.dma(SP, xt[:], xT_s[:, :, ti * TT:(ti + 1) * TT].rearrange("k p t -> p k t"), reads=[b_xT[ti]], writes=[bxt])
            return xt, bxt

        def norm_mod(xt, bxt, Amod, shift_lo, cond):
            sq, bsq = sq_r.get()
            fw.op(ACT, lambda h: h.activation(out=sq[:], in_=xt[:], func=AF.Square), reads=[bxt], writes=[bsq])
            rs, brs = rms_rstd([(sq[:, k, :], bsq) for k in range(8)], cmat[:, C_1024, :], 128)
            hT, bh = hT_r.get()
            for k in range(8):
                tmp, btmp = f32_r.get()
                e = DVE
                fw.op(e, lambda h, k=k, tmp=tmp: h.tensor_tensor(out=tmp[:], in0=xt[:, k, :], in1=rs[:], op=ALU.mult), reads=[bxt, brs], writes=[btmp])
                fw.op(ACT, lambda h, k=k, tmp=tmp: h.activation(out=hT[:, k, :], in_=tmp[:], func=AF.Identity,
                                                               scale=Amod[:, k, cond:cond + 1], bias=modT[:, shift_lo + k, cond:cond + 1]),
                      reads=[btmp, b_A, b_modT_b], writes=[bh])
            return hT, bh

        def proj_fm(wt, bw, col0, m, rhs_fn, rhs_bufs, nk, out_parts=None):
            ps, bps = psA.get()
            for k in range(nk):
                fw.op(PE, lambda h, k=k: h.matmul(ps[0:m, :], lhsT=wt[:, k, col0:col0 + m], rhs=rhs_fn(k), start=(k == 0), stop=(k == nk - 1)),
                      reads=[bw] + rhs_bufs, writes=[bps], inc=(k == nk - 1))
            return ps, bps

        def norm_rope_store(ps, bps, npart, ones_ap, gcolumn, rope, pm_idx, ropetab, dst_ap, dst_buf, keep_f32=None):
            xf, bxf = f32_r.get()
            evac(ps[0:npart, :], xf[0:npart, :], bps, bxf, eng=DVE)
            sq, bsq = bf_r.get()
            fw.op(ACT, lambda h: h.activation(out=sq[0:npart, :], in_=xf[0:npart, :], func=AF.Square), reads=[bxf], writes=[bsq])
            rs, brs = rms_rstd([(sq[0:npart, :], bsq)], ones_ap, npart)
            xn, bxn = f32_r.get()
            fw.op(DVE, lambda h: h.scalar_tensor_tensor(out=xn[0:npart, :], in0=xf[0:npart, :], scalar=gc[0:npart, gcolumn:gcolumn + 1], in1=rs[0:npart, :],
                                                        op0=ALU.mult, op1=ALU.mult), reads=[bxf, brs, b_gc], writes=[bxn])
            if keep_f32 is not None:
                keep_f32(xn, bxn)
            ob, bob = bf_r.get()
            if not rope:
                fw.op(ACT, lambda h: h.activation(out=ob[0:npart, :], in_=xn[0:npart, :], func=AF.Copy), reads=[bxn], writes=[bob])
            else:
                ctab, stab, btab = ropetab
                xb, bxb = bf_r.get()
                fw.op(ACT, lambda h: h.activation(out=xb[0:npart, :], in_=xn[0:npart, :], func=AF.Copy), reads=[bxn], writes=[bxb])
                ps2, bps2 = psA.get()
                fw.op(PE, lambda h: h.matmul(ps2[0:npart, :], lhsT=pmat[0:npart, pm_idx, 0:npart], rhs=xb[0:npart, :], start=True, stop=True),
                      reads=[bxb, b_pmat], writes=[bps2])
                t1, bt1 = f32_r.get()
                fw.op(DVE, lambda h: h.tensor_tensor(out=t1[0:npart, :], in0=xn[0:npart, :], in1=ctab, op=ALU.mult), reads=[bxn, btab], writes=[bt1])
                t2, bt2 = f32_r.get()
                fw.op(DVE, lambda h: h.tensor_tensor(out=t2[0:npart, :], in0=ps2[0:npart, :], in1=stab, op=ALU.mult), reads=[bps2, btab], writes=[bt2])
                fw.op(DVE, lambda h: h.tensor_tensor(out=ob[0:npart, :], in0=t1[0:npart, :], in1=t2[0:npart, :], op=ALU.add), reads=[bt1, bt2], writes=[bob])
            fw.dma(SP, dst_ap, ob[0:npart, :], reads=[bob], writes=[dst_buf])

        def transpose_out(src_fn, src_bufs, nfeat_parts, ncols_total, dst_rows_fn, colslices):
            for g in range(4):
                ps, bps = psA.get()
                n = len(colslices)
                for i, (j, pj, c0) in enumerate(colslices):
                    fw.op(PE, lambda h, j=j, pj=pj, c0=c0, g=g: h.transpose(ps[:, c0:c0 + pj], src_fn(j)[:, g * 128:(g + 1) * 128], ident[0:pj, 0:pj]),
                          reads=src_bufs + [b_ident], writes=[bps], inc=(i == n - 1))
                o, bo = f32_r.get()
                evac(ps[:, 0:ncols_total], o[:, 0:ncols_total], bps, bo, eng=DVE)
                fw.dma(SP, dst_rows_fn(g), o[:, 0:ncols_total], reads=[bo], writes=[Buf()])

        b_gq = Buf(); b_gk = Buf(); b_gkt = Buf(); b_gvt = Buf(); b_gG = Buf(); b_go = Buf()
        b_q = Buf(); b_k = Buf(); b_v = Buf(); b_dq = Buf(); b_dk = Buf(); b_dv = Buf()
        b_ya = Buf(); b_yb = Buf(); b_yc = Buf()
        out_bufs = []

        arena.reset()
        ropM_r = Ring(ov, "ropM", [96, 2, TT], F32, 1)
        ropD_r = Ring(ov, "ropD", [128, 2, TT], F32, 1)
        vaug_r = Ring(ov, "vaug", [128, 8, 4, 128], BF16, 1)
        tok_r = Ring(ov, "tok", [128, 4, 512], BF16, 2)
        tokf_r = Ring(ov, "tokf", [128, 4, 512], F32, 1)
        keep_r = Ring(ov, "keep", [128, 4, TT], F32, 1)
        qdn_r = Ring(ov, "qdn", [128, 3, TT], BF16, 1)
        ckv_r = Ring(ov, "ckv", [128, 2, TT], BF16, 1)
        ckvf_r = Ring(ov, "ckvf", [128, 2, TT], F32, 1)
        krb_r = Ring(ov, "krb", [32, TT], BF16, 1)
        krf_r = Ring(ov, "krf", [32, TT], F32, 1)
        aab_r = Ring(ov, "aab", [32, TT], BF16, 1)
        g4_r = Ring(ov, "g4", [64, 4, TT], BF16, 1)
        r4_r = Ring(ov, "r4", [128, 4, TT], BF16, 1)
        ctok_r = Ring(ov, "ctok", [128, 2, 512], F32, 2)

        def layer_setup(l):
            fw.dma(SP, bmod[:], b_modT[l], writes=[b_bmod])
            fw.dma(SP, gn[:], gnT[l], writes=[b_gn])
            fw.dma(SP, gc[:], gcol[l], writes=[b_gc])
            fw.dma(POOL, wa2[:], w_a2bd[l], writes=[b_wa2])
            fw.dma(POOL, ba[:], b_a[l], writes=[b_ba])
            fw.dma(SP, lamt[:], lam_qk[l], writes=[b_lamt])
            for cb in range(24):
                wm, bwm = wmod_r.get()
                fw.dma(SP, wm[:], w_mod[l, :, cb * 256:(cb + 1) * 256].rearrange("(k p) c -> p k c", p=128), writes=[bwm])
                ps, bps = psA.get()
                for j in range(2):
                    for k in range(8):
                        fw.op(PE, lambda h, j=j, k=k, wm=wm, ps=ps: h.matmul(ps[:, j * 2:j * 2 + 2], lhsT=wm[:, k, j * 128:(j + 1) * 128], rhs=scT[:, k, :],
                                                                          start=(k == 0), stop=(k == 7)),
                              reads=[bwm, b_scT], writes=[bps], inc=(j == 1 and k == 7))
                fw.op(DVE, lambda h, cb=cb, ps=ps: h.tensor_tensor(out=modT[:, cb * 2:(cb + 1) * 2, :], in0=ps[:, 0:4].rearrange("p (j c) -> p j c", c=2),
                                                                 in1=bmod[:, cb * 2:(cb + 1) * 2].unsqueeze(2).broadcast_to([128, 2, 2]), op=ALU.add),
                      reads=[bps, b_bmod], writes=[b_modT_b])
            fw.op(DVE, lambda h: h.scalar_tensor_tensor(out=A1[:], in0=modT[:, 8:16, :], scalar=1.0, in1=gn[:, 0:8].unsqueeze(2).broadcast_to([128, 8, 2]),
                                                        op0=ALU.add, op1=ALU.mult), reads=[b_modT_b, b_gn], writes=[b_A])
            fw.op(DVE, lambda h: h.scalar_tensor_tensor(out=A2[:], in0=modT[:, 32:40, :], scalar=1.0, in1=gn[:, 8:16].unsqueeze(2).broadcast_to([128, 8, 2]),
                                                        op0=ALU.add, op1=ALU.mult), reads=[b_modT_b, b_gn], writes=[b_A])
            lam_init = 0.8 - 0.6 * math.exp(-0.3 * l)
            fw.op(DVE, lambda h: h.tensor_tensor(out=lamt[:, 0:64], in0=lamt[:, 0:64], in1=lamt[:, 64:128], op=ALU.mult), reads=[b_lamt], writes=[b_lamt])
            fw.op(DVE, lambda h: h.tensor_tensor(out=lamt[:, 128:192], in0=lamt[:, 128:192], in1=lamt[:, 192:256], op=ALU.mult), reads=[b_lamt], writes=[b_lamt])
            fw.op(DVE, lambda h: h.reduce_sum(out=lam1[:, 0:1], in_=lamt[:, 0:64], axis=mybir.AxisListType.X), reads=[b_lamt], writes=[b_lam1])
            fw.op(DVE, lambda h: h.reduce_sum(out=lam1[:, 1:2], in_=lamt[:, 128:192], axis=mybir.AxisListType.X), reads=[b_lamt], writes=[b_lam1])
            fw.op(ACT, lambda h: h.activation(out=lam1[:, 2:4], in_=lam1[:, 0:2], func=AF.Exp), reads=[b_lam1], writes=[b_lam1])
            fw.op(DVE, lambda h: h.scalar_tensor_tensor(out=lam1[:, 4:5], in0=lam1[:, 3:4], scalar=-lam_init, in1=lam1[:, 2:3], op0=ALU.add, op1=ALU.subtract),
                  reads=[b_lam1], writes=[b_lam1])
            ps, bps = psA.get()
            fw.op(PE, lambda h: h.matmul(ps[:, 0:1], lhsT=ones_rowf[:, :], rhs=lam1[:, 4:5], start=True, stop=True), reads=[b_onesrowf, b_lam1], writes=[bps])
            fw.op(DVE, lambda h: h.tensor_copy(out=lamc[:, 0:1], in_=ps[:, 0:1]), reads=[bps], writes=[b_lamc])
            return lam_init

        def key_base(ti):
            return 0 if ti == 0 else NP_TOK + PAST + (ti - 1) * TT

        import os as _os
        KP1S = int(_os.environ.get("KP1S", "99"))
        KP1 = int(_os.environ.get("KP1", str(NT)))

        def phase1(l, ti):
            if ti >= KP1:
                return
            lat = ti > 0
            cond = 1 if lat else 0
            t0 = ti * TT
            kb = key_base(ti)
            xt, bxt = load_xT(ti)
            hT, bh = norm_mod(xt, bxt, A1, 0, cond)
            rhs_h = lambda k: hT[:, k, :]
            if lat:
                rM, brM = ropM_r.get()
                fw.dma(SP, rM[:], ropeM[:, :, (ti - 1) * TT:ti * TT].rearrange("c p t -> p c t"), writes=[brM])
                rD, brD = ropD_r.get()
                fw.dma(SP, rD[:], ropeD[:, :, (ti - 1) * TT:ti * TT].rearrange("c p t -> p c t"), writes=[brD])
                tabM = (rM[:, 0, :], rM[:, 1, :], brM)
                tabD = (rD[:, 0, :], rD[:, 1, :], brD)
            else:
                tabM = tabD = None

            if KP1S < 1:
                return
            wt, bw = load_w8(w_in[l, :, O_AQ:O_AQ + 512], 512)
            for which, dst, bdst in ((0, gq_s, b_gq), (1, gk_s, b_gk)):
                g4, bg4 = g4_r.get()
                for hh in range(4):
                    ps, bps = proj_fm(wt, bw, which * 256 + hh * 64, 64, rhs_h, [bh], 8)
                    evac(ps[0:64, :], g4[:, hh, :], bps, bg4, eng=(ACT if hh % 2 else DVE))
                fw.dma(SP, dst[:, :, t0:t0 + TT], g4[:], reads=[bg4], writes=[bdst])
            if KP1S < 2:
                return
            tk, btk = tok_r.get()
            for g in range(4):
                ps, bps = psA.get()
                for k in range(8):
                    fw.op(PE, lambda h, k=k, g=g, ps=ps: h.matmul(ps[:, 0:256], lhsT=hT[:, k, g * 128:(g + 1) * 128], rhs=wt[:, k, 256:512], start=(k == 0), stop=(k == 7)),
                          reads=[bw, bh], writes=[bps], inc=(k == 7))
                evac(ps[:, 0:256], tk[:, g, 0:256], bps, btk, eng=(ACT if g % 2 else DVE))
            fw.dma(SP, gkt_s[t0:t0 + TT, :].rearrange("(g p) c -> p g c", p=128), tk[:, :, 0:256], reads=[btk], writes=[b_gkt])
            wt, bw = load_w8(w_in[l, :, O_AV:O_AV + 512], 512)
            tk, btk = tok_r.get()
            for g in range(4):
                ps, bps = psA.get()
                for k in range(8):
                    fw.op(PE, lambda h, k=k, g=g, ps=ps, wt=wt: h.matmul(ps[:, :], lhsT=hT[:, k, g * 128:(g + 1) * 128], rhs=wt[:, k, 0:512], start=(k == 0), stop=(k == 7)),
                          reads=[bw, bh], writes=[bps], inc=(k == 7))
                evac(ps[:, :], tk[:, g, :], bps, btk, eng=(ACT if g % 2 else DVE))
            fw.dma(SP, gvt_s[t0:t0 + TT, :].rearrange("(g p) c -> p g c", p=128), tk[:], reads=[btk], writes=[b_gvt])
            if KP1S < 3:
                return
            wt, bw = load_w8(w_in[l, :, O_AA:O_AA + 416], 416)
            ps, bps = proj_fm(wt, bw, 0, 32, rhs_h, [bh], 8)
            aab, baab = aab_r.get()
            evac(ps[0:32, :], aab[:, :], bps, baab, eng=DVE)
            gt, bgt = tokf_r.get()
            for g in range(4):
                ps, bps = psA.get()
                fw.op(PE, lambda h, g=g, ps=ps: h.matmul(ps[:, :], lhsT=aab[:, g * 128:(g + 1) * 128], rhs=wa2[:, :], start=True, stop=False),
                      reads=[baab, b_wa2], writes=[bps], inc=False)
                fw.op(PE, lambda h, ps=ps: h.matmul(ps[:, :], lhsT=ones_row[:, :], rhs=ba[:, :], start=False, stop=True), reads=[b_onesrow, b_ba], writes=[bps])
                e1, be1 = f32_r.get()
                fw.op(ACT, lambda h, ps=ps, e1=e1: h.activation(out=e1[:], in_=ps[:], func=AF.Exp, scale=-1.0), reads=[bps], writes=[be1])
                fw.op(ACT, lambda h, g=g, e1=e1: h.activation(out=gt[:, g, :], in_=e1[:], func=AF.Ln, bias=1.0), reads=[be1], writes=[bgt])
            fw.dma(SP, gG_s[t0:t0 + TT, :].rearrange("(g p) c -> p g c", p=128), gt[:], reads=[bgt], writes=[b_gG])

            if KP1S < 4:
                return
            wt_r, bw_r = load_w8(w_in[l, :, O_AR:O_AR + 512], 512)
            rr, brr = r4_r.get()
            for hh in range(4):
                ps_r, bps_r = proj_fm(wt_r, bw_r, hh * 128, 128, rhs_h, [bh], 8)
                fw.op(ACT, lambda h, hh=hh, ps_r=ps_r: h.activation(out=rr[:, hh, :], in_=ps_r[:], func=AF.Silu), reads=[bps_r], writes=[brr])
            fw.dma(SP, rs_s[:, :, t0:t0 + TT], rr[:], reads=[brr], writes=[b_rs])

            if KP1S < 5:
                return
            qdn, bqdn = qdn_r.get()
            qf = []
            for j in range(3):
                ps, bps = proj_fm(wt, bw, 32 + j * 128, 128, rhs_h, [bh], 8)
                xf, bxf = f32_r.get()
                evac(ps[:], xf[:], bps, bxf, eng=DVE)
                sq, bsq = bf_r.get()
                fw.op(ACT, lambda h, sq=sq, xf=xf: h.activation(out=sq[:], in_=xf[:], func=AF.Square), reads=[bxf], writes=[bsq])
                qf.append((xf, bxf, sq, bsq))
            rs, brs = rms_rstd([(q[2][:], q[3]) for q in qf], cmat[:, C_384, :], 128)
            for j in range(3):
                xf, bxf = qf[j][0], qf[j][1]
                fw.op(DVE, lambda h, j=j, xf=xf: h.scalar_tensor_tensor(out=qdn[:, j, :], in0=xf[:], scalar=gc[:, j:j + 1], in1=rs[:], op0=ALU.mult, op1=ALU.mult),
                      reads=[bxf, brs, b_gc], writes=[bqdn])
            for half in range(2):
                wq, bwq = w8.get()
                fw.dma(POOL, wq[:, 0:3, 0:384], w_uq[l, :, half * 4:(half + 1) * 4, :].rearrange("(k p) h c -> p k (h c)", p=128), writes=[bwq])
                for hh in range(4):
                    ps, bps = proj_fm(wq, bwq, hh * 96, 96, lambda k: qdn[:, k, :], [bqdn], 3)
                    head = half * 4 + hh
                    norm_rope_store(ps, bps, 96, cmat[0:96, C_96, 0:96], 5, lat, P_96, tabM, q_s[head, :, t0:t0 + TT], b_q)

            if KP1S < 6:
                return
            wt, bw = load_w8(w_in[l, :, O_KVD:O_KVD + 288], 288)
            ckv, bckv = ckv_r.get()
            ckvf, bckvf = ckvf_r.get()
            kf = []
            for j in range(2):
                ps, bps = proj_fm(wt, bw, j * 128, 128, rhs_h, [bh], 8)
                xf, bxf = f32_r.get()
                evac(ps[:], xf[:], bps, bxf, eng=DVE)
                sq, bsq = bf_r.get()
                fw.op(ACT, lambda h, sq=sq, xf=xf: h.activation(out=sq[:], in_=xf[:], func=AF.Square), reads=[bxf], writes=[bsq])
                kf.append((xf, bxf, sq, bsq))
            rs, brs = rms_rstd([(q[2][:], q[3]) for q in kf], cmat[:, C_256, :], 128)
            for j in range(2):
                xf, bxf = kf[j][0], kf[j][1]
                fw.op(DVE, lambda h, j=j, xf=xf: h.scalar_tensor_tensor(out=ckvf[:, j, :], in0=xf[:], scalar=gc[:, 3 + j:4 + j], in1=rs[:], op0=ALU.mult, op1=ALU.mult),
                      reads=[bxf, brs, b_gc], writes=[bckvf])
            fw.op(ACT, lambda h: h.activation(out=ckv[:], in_=ckvf[:], func=AF.Copy), reads=[bckvf], writes=[bckv])
            ps, bps = proj_fm(wt, bw, 256, 32, rhs_h, [bh], 8)
            krf, bkrf = krf_r.get()
            krb, bkrb = krb_r.get()
            evac(ps[0:32, :], krf[:, :], bps, bkrf, eng=DVE)
            fw.op(ACT, lambda h: h.activation(out=krb[:], in_=krf[:], func=AF.Copy), reads=[bkrf], writes=[bkrb])
            KSUB = _os.environ.get("KSUB", "abcd")
            if not lat:
                if "a" in KSUB:
                    transpose_out(lambda j: ckvf[:, j, :], [bckvf], 128, 256, lambda g: o_ckv[l, g * 128:(g + 1) * 128, :], [(0, 128, 0), (1, 128, 128)])
                if "b" in KSUB:
                    transpose_out(lambda j: krf[:, :], [bkrf], 32, 32, lambda g: o_kr[l, g * 128:(g + 1) * 128, :], [(0, 32, 0)])
            if "c" in KSUB:
                mla_keys(l, lambda k: ckv[:, k, :], [bckv], krb, bkrb, lat, tabM, kb)
            if "d" in KSUB:
                mla_values(l, lambda k, g: ckv[:, k, g * 128:(g + 1) * 128], [bckv], kb)

            if KP1S < 7:
                return
            for which, off, dst, bdst, gcolumn in ((0, O_DQ, dq_s, b_dq, 7), (1, O_DK, dk_s, b_dk, 8)):
                wt, bw = load_w8(w_in[l, :, off:off + 512], 512)
                keep = None
                if which == 1 and not lat:
                    kp, bkp = keep_r.get()
                for hh in range(4):
                    ps, bps = proj_fm(wt, bw, hh * 128, 128, rhs_h, [bh], 8)
                    kf32 = None
                    if which == 1 and not lat:
                        kf32 = lambda xn, bxn, hh=hh: fw.op(ACT, lambda h: h.activation(out=kp[:, hh, :], in_=xn[:], func=AF.Copy), reads=[bxn], writes=[bkp])
                    col0 = t0 if which == 0 else kb
                    norm_rope_store(ps, bps, 128, cmat[:, C_BLK64, :], gcolumn, lat, P_128, tabD, dst[hh, :, col0:col0 + TT], bdst, keep_f32=kf32)
                if which == 1 and not lat:
                    transpose_out(lambda j: kp[:, j, :], [bkp], 128, 512, lambda g: o_dk[l, g * 128:(g + 1) * 128, :], [(j, 128, j * 128) for j in range(4)])
            wt, bw = load_w8(w_in[l, :, O_DV:O_DV + 512], 512)
            tk, btk = tok_r.get()
            if not lat:
                tf, btf = tokf_r.get()
            for g in range(4):
                ps, bps = psA.get()
                for k in range(8):
                    fw.op(PE, lambda h, k=k, g=g, ps=ps, wt=wt: h.matmul(ps[:, :], lhsT=hT[:, k, g * 128:(g + 1) * 128], rhs=wt[:, k, 0:512], start=(k == 0), stop=(k == 7)),
                          reads=[bw, bh], writes=[bps], inc=(k == 7))
                evac(ps[:, :], tk[:, g, :], bps, btk, eng=ACT)
                if not lat:
                    evac(ps[:, :], tf[:, g, :], bps, btf, eng=DVE)
            for hh in range(4):
                fw.dma(SP, dv_s[hh, :, kb // 128:kb // 128 + 4, :], tk[:, :, hh * 128:(hh + 1) * 128], reads=[btk], writes=[b_dv])
            if not lat:
                ob = Buf(); out_bufs.append(ob)
                fw.dma(SP, o_dv[l, :, :].rearrange("(g p) c -> p g c", p=128), tf[:], reads=[btf], writes=[ob])

        def mla_keys(l, ckv_fn, ckv_bufs, krb, bkrb, rope, tabM, kb, ntok=TT):
            for half in range(2):
                wk, bwk = w8.get()
                fw.dma(POOL, wk[:, 0:2, 0:384], w_ukp[l, :, half * 4:(half + 1) * 4, :].rearrange("(k p) h c -> p k (h c)", p=128), writes=[bwk])
                for hh in range(4):
                    ps, bps = psA.get()
                    for k in range(2):
                        fw.op(PE, lambda h, k=k, hh=hh, ps=ps, wk=wk: h.matmul(ps[0:96, 0:ntok], lhsT=wk[:, k, hh * 96:(hh + 1) * 96], rhs=ckv_fn(k), start=(k == 0), stop=False),
                              reads=[bwk] + ckv_bufs, writes=[bps], inc=False)
                    fw.op(PE, lambda h, ps=ps: h.matmul(ps[0:96, 0:ntok], lhsT=cmat[0:32, C_SEL32, 0:96], rhs=krb[:, 0:ntok], start=False, stop=True),
                          reads=[b_cmat, bkrb], writes=[bps])
                    head = half * 4 + hh
                    if ntok == TT:
                        norm_rope_store(ps, bps, 96, cmat[0:96, C_96, 0:96], 6, rope, P_96, tabM, k_s[head, :, kb:kb + TT], b_k)
                    else:
                        norm_store_small(ps, bps, ntok, head, kb)

        def norm_store_small(ps, bps, ntok, head, kb):
            xf, bxf = f32_r.get()
            evac(ps[0:96, 0:ntok], xf[0:96, 0:ntok], bps, bxf, eng=DVE)
            sq, bsq = bf_r.get()
            fw.op(ACT, lambda h: h.activation(out=sq[0:96, 0:ntok], in_=xf[0:96, 0:ntok], func=AF.Square), reads=[bxf], writes=[bsq])
            ps2, bps2 = psA.get()
            fw.op(PE, lambda h: h.matmul(ps2[0:96, 0:ntok], lhsT=cmat[0:96, C_96, 0:96], rhs=sq[0:96, 0:ntok], start=True, stop=True), reads=[bsq, b_cmat], writes=[bps2])
            rs, brs = rs_r.get()
            rsqrt_eps(ps2[0:96, 0:ntok], rs[0:96, 0:ntok], bps2, brs)
            ob, bob = bf_r.get()
            fw.op(DVE, lambda h: h.scalar_tensor_tensor(out=ob[0:96, 0:ntok], in0=xf[0:96, 0:ntok], scalar=gc[0:96, 6:7], in1=rs[0:96, 0:ntok], op0=ALU.mult, op1=ALU.mult),
                  reads=[bxf, brs, b_gc], writes=[bob])
            fw.dma(SP, k_s[head, :, kb:kb + ntok], ob[0:96, 0:ntok], reads=[bob], writes=[b_k])

        vaug_init = [False, False]

        def mla_values(l, ckvT_fn, ckv_bufs, kb, ngrp=4):
            wv, bwv = w8.get()
            fw.dma(POOL, wv[:, 0:2, 0:512], w_uv[l].rearrange("(k p) c -> p k c", p=128), writes=[bwv])
            idx = vaug_r.i
            va, bva = vaug_r.get()
            if not vaug_init[idx]:
                vaug_init[idx] = True
                fw.op(DVE, lambda h: h.memset(va[:], 1.0), writes=[bva])
            for g in range(ngrp):
                ps, bps = psA.get()
                for k in range(2):
                    fw.op(PE, lambda h, k=k, g=g, ps=ps: h.matmul(ps[:, :], lhsT=ckvT_fn(k, g), rhs=wv[:, k, 0:512], start=(k == 0), stop=(k == 1)),
                          reads=[bwv] + ckv_bufs, writes=[bps], inc=(k == 1))
                fw.op(ACT if g % 2 else DVE, (lambda h, g=g, ps=ps: h.activation(out=va[:, :, g, 0:64], in_=ps[:, :].rearrange("p (h c) -> p h c", c=64), func=AF.Copy)) if g % 2
                      else (lambda h, g=g, ps=ps: h.tensor_copy(out=va[:, :, g, 0:64], in_=ps[:, :].rearrange("p (h c) -> p h c", c=64))), reads=[bps], writes=[bva])
            c0 = kb // 128
            fw.dma(SP, v_s[:, :, c0:c0 + ngrp, :].rearrange("h p g e -> p h (g e)"), va[:, :, 0:ngrp, :].rearrange("p h g e -> p h (g e)"), reads=[bva], writes=[b_v])


        def prep_cache(l):
            kb = NP_TOK
            ct, bct = ctok_r.get()
            fw.dma(SP, ct[:, :, 0:256], c_ckv[l].rearrange("(g p) c -> p g c", p=128), writes=[bct])
            ckv, bckv = ckv_r.get()
            for j in range(2):
                ps, bps = psA.get()
                for g in range(2):
                    fw.op(PE, lambda h, j=j, g=g, ps=ps: h.transpose(ps[:, g * 128:(g + 1) * 128], ct[:, g, j * 128:(j + 1) * 128], ident[:]),
                          reads=[bct, b_ident], writes=[bps], inc=(g == 1))
                evac(ps[:, 0:256], ckv[:, j, 0:256], bps, bckv, eng=DVE)
            ct2, bct2 = ctok_r.get()
            fw.dma(SP, ct2[:, :, 0:32], c_kr[l].rearrange("(g p) c -> p g c", p=128), writes=[bct2])
            krb, bkrb = krb_r.get()
            ps, bps = psA.get()
            for g in range(2):
                fw.op(PE, lambda h, g=g, ps=ps: h.transpose(ps[0:32, g * 128:(g + 1) * 128], ct2[:, g, 0:32], ident[:]), reads=[bct2, b_ident], writes=[bps], inc=(g == 1))
            evac(ps[0:32, 0:256], krb[:, 0:256], bps, bkrb, eng=DVE)
            mla_keys(l, lambda k: ckv[:, k, 0:256], [bckv], krb, bkrb, False, None, kb, ntok=256)
            mla_values(l, lambda k, g: ckv[:, k, g * 128:(g + 1) * 128], [bckv], kb, ngrp=2)
            ct, bct = ctok_r.get()
            fw.dma(SP, ct[:], c_dk[l].rearrange("(g p) c -> p g c", p=128), writes=[bct])
            for hh in range(4):
                ps, bps = psA.get()
                for g in range(2):
                    fw.op(PE, lambda h, hh=hh, g=g, ps=ps: h.transpose(ps[:, g * 128:(g + 1) * 128], ct[:, g, hh * 128:(hh + 1) * 128], ident[:]),
                          reads=[bct, b_ident], writes=[bps], inc=(g == 1))
                ob, bob = bf_r.get()
                evac(ps[:, 0:256], ob[:, 0:256], bps, bob, eng=DVE)
                fw.dma(SP, dk_s[hh, :, kb:kb + 256], ob[:, 0:256], reads=[bob], writes=[b_dk])
            ct3, bct3 = ctok_r.get()
            fw.dma(SP, ct3[:], c_dv[l].rearrange("(g p) c -> p g c", p=128), writes=[bct3])
            tkc, btkc = tok_r.get()
            fw.op(DVE, lambda h: h.tensor_copy(out=tkc[:, 0:2, :], in_=ct3[:]), reads=[bct3], writes=[btkc])
            for hh in range(4):
                fw.dma(SP, dv_s[hh, :, kb // 128:kb // 128 + 2, :], tkc[:, 0:2, hh * 128:(hh + 1) * 128], reads=[btkc], writes=[b_dv])

        arena.reset()
        gl_q = Ring(ov, "glq", [64, 4, TT], BF16, 1)
        gl_k = Ring(ov, "glk", [64, 4, TT], BF16, 1)
        gl_kt = Ring(ov, "glkt", [64, 8, 256], BF16, 1)
        gl_vt = Ring(ov, "glvt", [64, 8, 512], BF16, 1)
        gl_G = Ring(ov, "glG", [64, 8, 256], F32, 1)
        gl_o = Ring(ov, "glo", [128, 4, TT], F32, 1)
        gl_of = Ring(ov, "glof", [128, 4, TT], F32, 1)
        gl_r = Ring(ov, "glr", [128, 4, TT], BF16, 2)
        S_f = ov("S_f", [64, 4, 128], F32); S_b = ov("S_b", [64, 4, 128], BF16); b_S = Buf(); b_Sb = Buf()
        sm_r = Ring(ov, "sm", [64, 256], F32, 6)
        smb_r = Ring(ov, "smb", [64, 256], BF16, 8)
        dd_r = Ring(ov, "dd", [64, 4], F32, 3)

        def gla_dir(l, d, seqs):
            incl = tri[:, d, :]
            strict = tri[:, 2 + d, :]
            for (tok0, ntok, seq_idx, lat) in seqs:
                if lat:
                    fw.dma(SP, S_f[:], st_gla[l, d], reads=[b_Sb], writes=[b_S])
                else:
                    fw.op(DVE, lambda h: h.memset(S_f[:], 0.0), reads=[b_Sb], writes=[b_S])
                fw.op(ACT, lambda h: h.activation(out=S_b[:], in_=S_f[:], func=AF.Copy), reads=[b_S], writes=[b_Sb])
                tiles = list(range(tok0, tok0 + ntok, TT)) if ntok >= TT else [tok0]
                if d == 1:
                    tiles = tiles[::-1]
                for tt0 in tiles:
                    base_tile = (tt0 // TT) * TT
                    cq, bcq = gl_q.get(); ck, bck = gl_k.get(); ckt, bckt = gl_kt.get(); cvt, bcvt = gl_vt.get(); cG, bcG = gl_G.get()
                    fw.dma(SP, cq[:], gq_s[:, :, base_tile:base_tile + TT], reads=[b_gq], writes=[bcq])
                    fw.dma(SP, ck[:], gk_s[:, :, base_tile:base_tile + TT], reads=[b_gk], writes=[bck])
                    fw.dma(SP, ckt[:], gkt_s[base_tile:base_tile + TT, :].rearrange("(c p) f -> p c f", p=64), reads=[b_gkt], writes=[bckt])
                    fw.dma(SP, cvt[:], gvt_s[base_tile:base_tile + TT, :].rearrange("(c p) f -> p c f", p=64), reads=[b_gvt], writes=[bcvt])
                    fw.dma(SP, cG[:], gG_s[base_tile:base_tile + TT, d * 256:(d + 1) * 256].rearrange("(c p) f -> p c f", p=64), reads=[b_gG], writes=[bcG])
                    ot, bot = gl_o.get()
                    c_lo = (tt0 - base_tile) // 64
                    nch = min(ntok, TT) // 64
                    chunks = list(range(c_lo, c_lo + nch))
                    if d == 1:
                        chunks = chunks[::-1]
                    for c in chunks:
                        Gc = cG[:, c, :]
                        psb_, bpsb = psA.get()
                        for hh in range(4):
                            fw.op(PE, lambda h, hh=hh, Gc=Gc, p=psb_: h.matmul(p[0:64, hh * 64:(hh + 1) * 64], lhsT=Gc[:, hh * 64:(hh + 1) * 64], rhs=incl, start=True, stop=True),
                                  reads=[bcG, b_tri], writes=[bpsb], inc=(hh == 3))
                        ep, bep = sm_r.get(); en, ben = sm_r.get()
                        fw.op(ACT, lambda h, p=psb_, ep=ep: h.activation(out=ep[:, :], in_=p[0:64, 0:256], func=AF.Exp, scale=-1.0 / 16), reads=[bpsb], writes=[bep])
                        fw.op(ACT, lambda h, p=psb_, en=en: h.activation(out=en[:, :], in_=p[0:64, 0:256], func=AF.Exp, scale=1.0 / 16), reads=[bpsb], writes=[ben])
                        qt, bqt = smb_r.get(); kt_, bkt = smb_r.get()
                        fw.op(DVE, lambda h, c=c, qt=qt, ep=ep: h.scalar_tensor_tensor(out=qt[:, :].rearrange("p (h t) -> p h t", t=64), in0=cq[:, :, c * 64:(c + 1) * 64], scalar=0.125,
                                                                                      in1=ep[:, :].rearrange("p (h t) -> p h t", t=64), op0=ALU.mult, op1=ALU.mult),
                              reads=[bcq, bep], writes=[bqt])
                        fw.op(DVE, lambda h, c=c, kt_=kt_, en=en: h.tensor_tensor(out=kt_[:, :].rearrange("p (h t) -> p h t", t=64), in0=ck[:, :, c * 64:(c + 1) * 64],
                                                                                  in1=en[:, :].rearrange("p (h t) -> p h t", t=64), op=ALU.mult),
                              reads=[bck, ben], writes=[bkt])
                        ps2, bps2 = psA.get()
                        fw.op(PE, lambda h, Gc=Gc, p=ps2: h.matmul(p[0:64, 0:256], lhsT=strict, rhs=Gc, start=True, stop=True), reads=[bcG, b_tri], writes=[bps2])
                        e2, be2 = sm_r.get()
                        fw.op(ACT, lambda h, p=ps2, e2=e2: h.activation(out=e2[:, :], in_=p[0:64, 0:256], func=AF.Exp, scale=-1.0 / 16), reads=[bps2], writes=[be2])
                        kh, bkh = smb_r.get()
                        fw.op(DVE, lambda h, c=c, kh=kh, e2=e2: h.tensor_tensor(out=kh[:, :], in0=ckt[:, c, :], in1=e2[:, :], op=ALU.mult), reads=[bckt, be2], writes=[bkh])
                        ps3, bps3 = psA.get()
                        for hh in range(4):
                            fw.op(PE, lambda h, hh=hh, Gc=Gc, p=ps3: h.matmul(p[0:64, hh:hh + 1], lhsT=Gc[:, hh * 64:(hh + 1) * 64], rhs=ones_col[:, :], start=True, stop=True),
                                  reads=[bcG, b_onescol], writes=[bps3], inc=(hh == 3))
                        dd, bdd = dd_r.get()
                        fw.op(ACT, lambda h, p=ps3, dd=dd: h.activation(out=dd[:, :], in_=p[0:64, 0:4], func=AF.Exp, scale=-1.0 / 16), reads=[bps3], writes=[bdd])
                        ps4, bps4 = psA.get()
                        for hh in range(4):
                            fw.op(PE, lambda h, hh=hh, p=ps4, kt_=kt_, qt=qt: h.matmul(p[0:64, hh * 64:(hh + 1) * 64], lhsT=kt_[:, hh * 64:(hh + 1) * 64], rhs=qt[:, hh * 64:(hh + 1) * 64],
                                                                                     start=True, stop=True), reads=[bkt, bqt], writes=[bps4], inc=(hh == 3))
                        am, bam = smb_r.get()
                        fw.op(DVE, lambda h, p=ps4, am=am: h.tensor_tensor(out=am[:, :].rearrange("p (h t) -> p h t", t=64), in0=p[0:64, 0:256].rearrange("p (h t) -> p h t", t=64),
                                                                         in1=incl.unsqueeze(1).broadcast_to([64, 4, 64]), op=ALU.mult), reads=[bps4, b_tri], writes=[bam])
                        ps5, bps5 = psA.get()
                        for hh in range(4):
                            fw.op(PE, lambda h, hh=hh, c=c, p=ps5, am=am: h.matmul(p[:, hh * 64:(hh + 1) * 64], lhsT=cvt[:, c, hh * 128:(hh + 1) * 128], rhs=am[:, hh * 64:(hh + 1) * 64],
                                                                                 start=True, stop=False), reads=[bcvt, bam], writes=[bps5], inc=False)
                            fw.op(PE, lambda h, hh=hh, p=ps5, qt=qt: h.matmul(p[:, hh * 64:(hh + 1) * 64], lhsT=S_b[:, hh, :], rhs=qt[:, hh * 64:(hh + 1) * 64], start=False, stop=True),
                                  reads=[b_Sb, bqt], writes=[bps5], inc=(hh == 3))
                        fw.op(ACT, lambda h, c=c, p=ps5, ot=ot: h.activation(out=ot[:, :, c * 64:(c + 1) * 64], in_=p[:, 0:256].rearrange("p (h t) -> p h t", t=64), func=AF.Copy),
                              reads=[bps5], writes=[bot])
                        ps6, bps6 = psB.get()
                        for hh in range(4):
                            fw.op(PE, lambda h, hh=hh, c=c, p=ps6, kh=kh: h.matmul(p[0:64, hh * 128:(hh + 1) * 128], lhsT=kh[:, hh * 64:(hh + 1) * 64], rhs=cvt[:, c, hh * 128:(hh + 1) * 128],
                                                                                 start=True, stop=True), reads=[bkh, bcvt], writes=[bps6], inc=(hh == 3))
                        fw.op(DVE, lambda h, dd=dd: h.tensor_tensor(out=S_f[:], in0=S_f[:], in1=dd[:, :].unsqueeze(2).broadcast_to([64, 4, 128]), op=ALU.mult), reads=[bdd], writes=[b_S])
                        fw.op(DVE, lambda h, p=ps6: h.tensor_tensor(out=S_f[:], in0=S_f[:], in1=p[0:64, :].rearrange("p (h v) -> p h v", v=128), op=ALU.add), reads=[bps6], writes=[b_S])
                        fw.op(ACT, lambda h: h.activation(out=S_b[:], in_=S_f[:], func=AF.Copy), reads=[b_S], writes=[b_Sb])
                    cols = slice(tt0 - base_tile, tt0 - base_tile + min(ntok, TT))
                    ncol = min(ntok, TT)
                    if d == 0:
                        fw.dma(SP, go_s[:, :, tt0:tt0 + ncol], ot[:, :, cols], reads=[bot], writes=[b_go])
                    else:
                        of, bof = gl_of.get()
                        fw.dma(SP, of[:, :, 0:ncol], go_s[:, :, tt0:tt0 + ncol], reads=[b_go], writes=[bof])
                        rr, brr = gl_r.get()
                        fw.dma(SP, rr[:, :, 0:ncol], rs_s[:, :, tt0:tt0 + ncol], reads=[b_rs], writes=[brr])
                        fw.op(DVE, lambda h, ot=ot, of=of: h.tensor_tensor(out=of[:, :, 0:ncol], in0=of[:, :, 0:ncol], in1=ot[:, :, cols], op=ALU.add), reads=[bot], writes=[bof])
                        sq, bsq = sq_r.get()
                        fw.op(ACT, lambda h, sq=sq, of=of: h.activation(out=sq[:, 0:4, 0:ncol], in_=of[:, :, 0:ncol], func=AF.Square), reads=[bof], writes=[bsq])
                        ya, bya = gl_r.get()
                        for hh in range(4):
                            ps, bps = psA.get()
                            fw.op(PE, lambda h, hh=hh, ps=ps, sq=sq: h.matmul(ps[:, 0:ncol], lhsT=cmat[:, C_128, :], rhs=sq[:, hh, 0:ncol], start=True, stop=True), reads=[bsq, b_cmat], writes=[bps])
                            rs, brs = rs_r.get()
                            rsqrt_eps(ps[:, 0:ncol], rs[:, 0:ncol], bps, brs)
                            tmp, btmp = f32_r.get()
                            fw.op(DVE, lambda h, hh=hh, tmp=tmp, of=of, rs=rs: h.scalar_tensor_tensor(out=tmp[:, 0:ncol], in0=of[:, hh, 0:ncol], scalar=gc[:, 9:10], in1=rs[:, 0:ncol],
                                                                                                     op0=ALU.mult, op1=ALU.mult), reads=[bof, brs, b_gc], writes=[btmp])
                            fw.op(DVE, lambda h, hh=hh, tmp=tmp, ya=ya, rr=rr: h.tensor_tensor(out=ya[:, hh, 0:ncol], in0=tmp[:, 0:ncol], in1=rr[:, hh, 0:ncol], op=ALU.mult),
                                  reads=[btmp, brr], writes=[bya])
                        fw.dma(SP, ya_s[:, :, tt0:tt0 + ncol].rearrange("h p t -> p h t"), ya[:, :, 0:ncol], reads=[bya], writes=[b_ya])
                if seq_idx is not None:
                    ob = Buf(); out_bufs.append(ob)
                    fw.dma(SP, o_gla[l, seq_idx, d], S_f[:], reads=[b_S], writes=[ob])

        rs_s = dram_tmp("rs_s", [128, 4, T], BF16)
        b_rs = Buf()

        def phase1b(l, ti):
            cond = 1 if ti > 0 else 0
            xt, bxt = load_xT(ti)
            hT, bh = norm_mod(xt, bxt, A1, 0, cond)
            wt, bw = load_w8(w_in[l, :, O_AR:O_AR + 512], 512)
            rr, brr = gl_r.get()
            for hh in range(4):
                ps, bps = proj_fm(wt, bw, hh * 128, 128, lambda k: hT[:, k, :], [bh], 8)
                fw.op(ACT, lambda h, hh=hh, ps=ps: h.activation(out=rr[:, hh, :], in_=ps[:], func=AF.Silu), reads=[bps], writes=[brr])
            fw.dma(SP, rs_s[:, :, ti * TT:(ti + 1) * TT], rr[:], reads=[brr], writes=[b_rs])

        arena.reset()
        at_k = Ring(ov, "atk", [128, TKS], BF16, 2)
        at_v = Ring(ov, "atv", [128, TKS // 128, 128], BF16, 2)
        at_q = Ring(ov, "atq", [128, TT], BF16, 2)
        at_p = Ring(ov, "atp", [128, TT], BF16, 6)
        at_o = Ring(ov, "ato", [128, TT], BF16, 2)

        def mla_attn(l, groups):
            sc = 96 ** -0.5
            for (q0, nq, k0, nk) in groups:
                nkc = nk // 128
                for head in range(8):
                    kt, bkt = at_k.get(); vt, bvt = at_v.get()
                    fw.dma(SP, kt[0:96, 0:nk], k_s[head, :, k0:k0 + nk], reads=[b_k], writes=[bkt])
                    fw.dma(SP, vt[:, 0:nkc, :], v_s[head, :, k0 // 128:k0 // 128 + nkc, :], reads=[b_v], writes=[bvt])
                    for qq in range(q0, q0 + nq, TT):
                        nqt = min(TT, nq)
                        qt, bqt = at_q.get()
                        fw.dma(SP, qt[0:96, 0:nqt], q_s[head, :, qq:qq + nqt], reads=[b_q], writes=[bqt])
                        po, bpo = psB.get()

                        def qk(c, kt=kt, qt=qt, bkt=bkt, bqt=bqt, nqt=nqt):
                            ps, bps = psA.get()
                            fw.op(PE, lambda h: h.matmul(ps[:, 0:nqt], lhsT=kt[0:96, c * 128:(c + 1) * 128], rhs=qt[0:96, 0:nqt], start=True, stop=True),
                                  reads=[bkt, bqt], writes=[bps])
                            return ps, bps

                        LA = 2
                        pend = [qk(i) for i in range(min(LA, nkc))]
                        for c in range(nkc):
                            ps, bps = pend.pop(0)
                            if c + LA < nkc:
                                pend.append(qk(c + LA))
                            pt, bpt = at_p.get()
                            fw.op(ACT, lambda h, ps=ps, pt=pt: h.activation(out=pt[:, 0:nqt], in_=ps[:, 0:nqt], func=AF.Exp, scale=sc), reads=[bps], writes=[bpt])
                            fw.op(PE, lambda h, c=c, po=po, vt=vt, pt=pt: h.matmul(po[:, 0:nqt], lhsT=vt[:, c, :], rhs=pt[:, 0:nqt], start=(c == 0), stop=(c == nkc - 1)),
                                  reads=[bvt, bpt], writes=[bpo], inc=(c == nkc - 1))
                        rc, brc = f32_r.get()
                        fw.op(DVE, lambda h, po=po, rc=rc: h.reciprocal(out=rc[64:128, 0:nqt], in_=po[64:128, 0:nqt]), reads=[bpo], writes=[brc])
                        r2, br2 = f32_r.get()
                        fw.op(DVE, lambda h, rc=rc, r2=r2: h.tensor_copy(out=r2[0:64, 0:nqt], in_=rc[64:128, 0:nqt]), reads=[brc], writes=[br2])
                        ob, bob = at_o.get()
                        fw.op(DVE, lambda h, po=po, r2=r2, ob=ob: h.tensor_tensor(out=ob[0:64, 0:nqt], in0=po[0:64, 0:nqt], in1=r2[0:64, 0:nqt], op=ALU.mult), reads=[bpo, br2], writes=[bob])
                        fw.dma(SP, yb_s[head // 2, (head % 2) * 64:(head % 2) * 64 + 64, qq:qq + nqt], ob[0:64, 0:nqt], reads=[bob], writes=[b_yb])

        def diff_attn(l, groups, lam_init):
            sc = 64 ** -0.5
            for (q0, nq, k0, nk) in groups:
                nkc = nk // 128
                for head in range(4):
                    kt, bkt = at_k.get(); vt, bvt = at_v.get()
                    fw.dma(SP, kt[:, 0:nk], dk_s[head, :, k0:k0 + nk], reads=[b_dk], writes=[bkt])
                    fw.dma(SP, vt[:, 0:nkc, :], dv_s[head, :, k0 // 128:k0 // 128 + nkc, :], reads=[b_dv], writes=[bvt])
                    for qq in range(q0, q0 + nq, TT):
                        nqt = min(TT, nq)
                        qt, bqt = at_q.get()
                        fw.dma(SP, qt[:, 0:nqt], dq_s[head, :, qq:qq + nqt], reads=[b_dq], writes=[bqt])
                        acc = [psB.get() for _ in range(4)]

                        def qk(i, kt=kt, qt=qt, bkt=bkt, bqt=bqt, nqt=nqt):
                            c, comp = divmod(i, 2)
                            lo = comp * 64
                            ps, bps = psA.get()
                            fw.op(PE, lambda h: h.matmul(ps[:, 0:nqt], lhsT=kt[lo:lo + 64, c * 128:(c + 1) * 128], rhs=qt[lo:lo + 64, 0:nqt], start=True, stop=True),
                                  reads=[bkt, bqt], writes=[bps])
                            return ps, bps

                        LA = 2
                        pend = [qk(i) for i in range(min(LA, 2 * nkc))]
                        for c in range(nkc):
                            for comp in range(2):
                                ps, bps = pend.pop(0)
                                if c * 2 + comp + LA < 2 * nkc:
                                    pend.append(qk(c * 2 + comp + LA))
                                pt, bpt = at_p.get()
                                fw.op(ACT, lambda h, ps=ps, pt=pt: h.activation(out=pt[:, 0:nqt], in_=ps[:, 0:nqt], func=AF.Exp, scale=sc), reads=[bps], writes=[bpt])
                                po, bpo = acc[comp]
                                pl, bpl = acc[2 + comp]
                                fw.op(PE, lambda h, c=c, po=po, vt=vt, pt=pt: h.matmul(po[:, 0:nqt], lhsT=vt[:, c, :], rhs=pt[:, 0:nqt], start=(c == 0), stop=(c == nkc - 1)),
                                      reads=[bvt, bpt], writes=[bpo], inc=(c == nkc - 1))
                                fw.op(PE, lambda h, c=c, pl=pl, pt=pt: h.matmul(pl[:, 0:nqt], lhsT=cmat[:, C_ONE, :], rhs=pt[:, 0:nqt], start=(c == 0), stop=(c == nkc - 1)),
                                      reads=[b_cmat, bpt], writes=[bpl], inc=(c == nkc - 1))
                        r0, br0 = f32_r.get(); r1, br1 = f32_r.get()
                        fw.op(DVE, lambda h, r0=r0, p=acc[2][0]: h.reciprocal(out=r0[:, 0:nqt], in_=p[:, 0:nqt]), reads=[acc[2][1]], writes=[br0])
                        fw.op(DVE, lambda h, r1=r1, p=acc[3][0]: h.reciprocal(out=r1[:, 0:nqt], in_=p[:, 0:nqt]), reads=[acc[3][1]], writes=[br1])
                        o0, bo0 = f32_r.get(); o1, bo1 = f32_r.get()
                        fw.op(DVE, lambda h, o0=o0, r0=r0, p=acc[0][0]: h.tensor_tensor(out=o0[:, 0:nqt], in0=p[:, 0:nqt], in1=r0[:, 0:nqt], op=ALU.mult), reads=[acc[0][1], br0], writes=[bo0])
                        fw.op(DVE, lambda h, o1=o1, r1=r1, p=acc[1][0]: h.tensor_tensor(out=o1[:, 0:nqt], in0=p[:, 0:nqt], in1=r1[:, 0:nqt], op=ALU.mult), reads=[acc[1][1], br1], writes=[bo1])
                        fw.op(DVE, lambda h, o0=o0, o1=o1: h.scalar_tensor_tensor(out=o0[:, 0:nqt], in0=o1[:, 0:nqt], scalar=lamc[:, 0:1], in1=o0[:, 0:nqt], op0=ALU.mult, op1=ALU.add),
                              reads=[bo1, b_lamc], writes=[bo0])
                        sq, bsq = bf_r.get()
                        fw.op(ACT, lambda h, sq=sq, o0=o0: h.activation(out=sq[:, 0:nqt], in_=o0[:, 0:nqt], func=AF.Square), reads=[bo0], writes=[bsq])
                        ps, bps = psA.get()
                        fw.op(PE, lambda h, ps=ps, sq=sq: h.matmul(ps[:, 0:nqt], lhsT=cmat[:, C_128, :], rhs=sq[:, 0:nqt], start=True, stop=True), reads=[bsq, b_cmat], writes=[bps])
                        rs, brs = rs_r.get()
                        rsqrt_eps(ps[:, 0:nqt], rs[:, 0:nqt], bps, brs)
                        t1, bt1 = f32_r.get()
                        fw.op(DVE, lambda h, t1=t1, o0=o0, rs=rs: h.scalar_tensor_tensor(out=t1[:, 0:nqt], in0=o0[:, 0:nqt], scalar=gc[:, 10:11], in1=rs[:, 0:nqt], op0=ALU.mult, op1=ALU.mult),
                              reads=[bo0, brs, b_gc], writes=[bt1])
                        ob, bob = at_o.get()
                        fw.op(ACT, lambda h, t1=t1, ob=ob: h.activation(out=ob[:, 0:nqt], in_=t1[:, 0:nqt], func=AF.Copy, scale=(1.0 - lam_init)), reads=[bt1], writes=[bob])
                        fw.dma(SP, yc_s[head, :, qq:qq + nqt], ob[:, 0:nqt], reads=[bob], writes=[b_yc])

        arena.reset()
        y3_r = Ring(ov, "y3", [128, 12, TT], BF16, 1)
        mg_r = Ring(ov, "mg", [128, 8, TT], BF16, 1)
        macc_r = Ring(ov, "macc", [128, 4, TT], F32, 1)
        aTraw = ov("aTraw", [128, 8192], F32)
        aT_view = aTraw.bitcast(BF16).rearrange("p (a b) -> p a b", b=TT)
        xo_view = aTraw[:, 0:4096].rearrange("p (g f) -> p g f", f=D)
        b_aTraw = Buf()

        def phase3(l, ti, last):
            cond = 1 if ti > 0 else 0
            t0 = ti * TT
            xt, bxt = load_xT(ti)
            hT, bh = norm_mod(xt, bxt, A1, 0, cond)
            y3, by3 = y3_r.get()
            for bi, (src, bsrc) in enumerate(((ya_s, b_ya), (yb_s, b_yb), (yc_s, b_yc))):
                fw.dma(SP, y3[:, bi * 4:(bi + 1) * 4, :], src[:, :, t0:t0 + TT].rearrange("h p t -> p h t"), reads=[bsrc], writes=[by3])
            mg, bmg = mg_r.get()
            macc, bmacc = macc_r.get()
            for half in range(2):
                for bi in range(3):
                    c0 = O_G + bi * 1024 + half * 512
                    wg, bwg = load_w8(w_in[l, :, c0:c0 + 512], 512)
                    wo, bwo = w4.get()
                    fw.dma(POOL, wo[:, :, 0:512], w_o3[l, bi, :, half * 512:(half + 1) * 512].rearrange("(k p) c -> p k c", p=128), writes=[bwo])
                    for j in range(4):
                        oc = half * 4 + j
                        psg, bpsg = proj_fm(wg, bwg, j * 128, 128, lambda k: hT[:, k, :], [bh], 8)
                        sg, bsg = f32_r.get()
                        fw.op(ACT, lambda h: h.activation(out=sg[:], in_=psg[:], func=AF.Sigmoid), reads=[bpsg], writes=[bsg])
                        pso, bpso = proj_fm(wo, bwo, j * 128, 128, lambda k: y3[:, bi * 4 + k, :], [by3], 4)
                        if bi == 0:
                            fw.op(DVE, lambda h: h.tensor_tensor(out=macc[:, j, :], in0=sg[:], in1=pso[:], op=ALU.mult), reads=[bsg, bpso], writes=[bmacc])
                        else:
                            tmp, btmp = f32_r.get()
                            fw.op(DVE, lambda h: h.tensor_tensor(out=tmp[:], in0=sg[:], in1=pso[:], op=ALU.mult), reads=[bsg, bpso], writes=[btmp])
                            if bi == 1:
                                fw.op(DVE, lambda h: h.tensor_tensor(out=macc[:, j, :], in0=macc[:, j, :], in1=tmp[:], op=ALU.add), reads=[btmp], writes=[bmacc])
                            else:
                                fw.op(DVE, lambda h: h.tensor_tensor(out=mg[:, oc, :], in0=macc[:, j, :], in1=tmp[:], op=ALU.add), reads=[btmp, bmacc], writes=[bmg])
            for half in range(2):
                wt, bw = load_w8(w_out[l, :, half * 512:(half + 1) * 512], 512)
                for j in range(4):
                    oc = half * 4 + j
                    ps, bps = proj_fm(wt, bw, j * 128, 128, lambda k: mg[:, k, :], [bmg], 8)
                    fw.op(DVE, lambda h, oc=oc, ps=ps: h.scalar_tensor_tensor(out=xt[:, oc, :], in0=ps[:], scalar=modT[:, 16 + oc, cond:cond + 1], in1=xt[:, oc, :], op0=ALU.mult, op1=ALU.add),
                          reads=[bps, b_modT_b], writes=[bxt])
            h2, bh2 = norm_mod(xt, bxt, A2, 24, cond)
            aT, baT = aT_view, b_aTraw
            for fb in range(8):
                wt, bw = load_w8(w_m1[l, :, fb * 512:(fb + 1) * 512], 512)
                for j in range(4):
                    ps, bps = proj_fm(wt, bw, j * 128, 128, lambda k: h2[:, k, :], [bh2], 8)
                    rl, brl = f32_r.get()
                    fw.op(ACT, lambda h, ps=ps, rl=rl: h.activation(out=rl[:], in_=ps[:], func=AF.Relu), reads=[bps], writes=[brl])
                    fw.op(DVE, lambda h, rl=rl, fb=fb, j=j: h.tensor_tensor(out=aT[:, fb * 4 + j, :], in0=rl[:], in1=rl[:], op=ALU.mult), reads=[brl], writes=[baT])
            for half in range(2):
                accs = [psB.get() for _ in range(4)]
                for kb8 in range(4):
                    wt, bw = load_w8(w_m2[l, kb8 * 1024:(kb8 + 1) * 1024, half * 512:(half + 1) * 512], 512)
                    for j in range(4):
                        ps, bps = accs[j]
                        for k in range(8):
                            kk = kb8 * 8 + k
                            fw.op(PE, lambda h: h.matmul(ps[:], lhsT=wt[:, k, j * 128:(j + 1) * 128], rhs=aT[:, kk, :], start=(kk == 0), stop=(kk == 31)),
                                  reads=[bw, baT], writes=[bps], inc=(k == 7))
                for j in range(4):
                    oc = half * 4 + j
                    ps, bps = accs[j]
                    fw.op(DVE, lambda h: h.scalar_tensor_tensor(out=xt[:, oc, :], in0=ps[:], scalar=modT[:, 40 + oc, cond:cond + 1], in1=xt[:, oc, :], op0=ALU.mult, op1=ALU.add),
                          reads=[bps, b_modT_b], writes=[bxt])
            if not last:
                fw.dma(SP, xT_s[:, :, t0:t0 + TT].rearrange("k p t -> p k t"), xt[:], reads=[bxt], writes=[b_xT[ti]])
            else:
                xo, bxo = xo_view, b_aTraw
                for g in range(4):
                    for half in range(2):
                        ps, bps = psA.get()
                        for k in range(4):
                            kk = half * 4 + k
                            fw.op(PE, lambda h, g=g, k=k, kk=kk, ps=ps: h.transpose(ps[:, k * 128:(k + 1) * 128], xt[:, kk, g * 128:(g + 1) * 128], ident[:]),
                                  reads=[bxt, b_ident], writes=[bps], inc=(k == 3))
                        evac(ps[:], xo[:, g, half * 512:(half + 1) * 512], bps, bxo, eng=(ACT if half else DVE))
                ob = Buf(); out_bufs.append(ob)
                fw.dma(SP, y_out[t0:t0 + TT, :].rearrange("(g p) f -> p g f", p=128), xo[:], reads=[bxo], writes=[ob])

        import os
        KSTOP = int(os.environ.get("KSTOP", "99"))
        step = [0]

        def reached():
            step[0] += 1
            return step[0] > KSTOP

        for l in range(L):
            if reached():
                break
            lam_init = layer_setup(l)
            fw.barrier()
            vaug_init[0] = vaug_init[1] = False
            if reached():
                break
            prep_cache(l)
            if reached():
                break
            for ti in range(NT):
                phase1(l, ti)
            fw.barrier()
            if reached():
                break
            seqs = [(0, 256, 0, False), (256, 256, 1, False), (NP_TOK, NS_TOK, None, True)]
            gla_dir(l, 0, seqs)
            fw.barrier()
            gla_dir(l, 1, seqs)
            fw.barrier()
            if reached():
                break
            groups = [(0, 256, 0, 256), (256, 256, 256, 256), (NP_TOK, NS_TOK, NP_TOK, TKS)]
            mla_attn(l, groups)
            if reached():
                break
            diff_attn(l, groups, lam_init)
            if reached():
                break
            fw.barrier()
            for ti in range(NT):
                phase3(l, ti, l == L - 1)
            fw.barrier()

        fw.finish(out_bufs)
    return nc


def _rope_tables():
    t = np.arange(NS_TOK)
    row, col = t // 64, t % 64

    def cs(nf, pos):
        inv = (10000.0 ** (-np.arange(nf, dtype=np.float32) / nf)).astype(np.float32)
        ang = pos.astype(np.float32)[None, :] * inv[:, None]
        c, s = np.cos(ang).astype(np.float32), np.sin(ang).astype(np.float32)
        return np.concatenate([c, c], 0), np.concatenate([s, s], 0)

    cr, sr = cs(8, row); cc, sc_ = cs(8, col)
    cM = np.concatenate([cr, cc, np.ones((64, NS_TOK), np.float32)], 0)
    sM = np.concatenate([sr, sc_, np.zeros((64, NS_TOK), np.float32)], 0)
    cr, sr = cs(16, row); cc, sc_ = cs(16, col)
    c64 = np.concatenate([cr, cc], 0); s64 = np.concatenate([sr, sc_], 0)
    cD = np.concatenate([c64, c64], 0); sD = np.concatenate([s64, s64], 0)
    return np.stack([cM, sM]).astype(np.float32), np.stack([cD, sD]).astype(np.float32)


def _rot_matrix(nf):
    R = np.zeros((2 * nf, 2 * nf), np.float32)
    for i in range(nf):
        R[i, nf + i] = -1.0
        R[nf + i, i] = 1.0
    return R


def _consts():
    bf = ml_dtypes.bfloat16
    cmat = np.zeros((128, 8, 128), np.float32)
    cmat[:, 0, :] = 1.0 / 1024
    cmat[:, 1, :] = 1.0 / 384
    cmat[:, 2, :] = 1.0 / 256
    cmat[:96, 3, :96] = 1.0 / 96
    cmat[:64, 4, :64] = 1.0 / 64
    cmat[64:, 4, 64:] = 1.0 / 64
    cmat[:, 5, :] = 1.0 / 128
    cmat[:, 6, :] = 1.0
    cmat[:32, 7, :32] = np.eye(32)
    pm = np.zeros((128, 3, 128), np.float32)
    R96 = np.zeros((96, 96), np.float32)
    R96[0:16, 0:16] = _rot_matrix(8); R96[16:32, 16:32] = _rot_matrix(8)
    pm[:96, 0, :96] = R96.T
    R64 = np.zeros((64, 64), np.float32)
    R64[0:32, 0:32] = _rot_matrix(16); R64[32:64, 32:64] = _rot_matrix(16)
    R128 = np.zeros((128, 128), np.float32)
    R128[:64, :64] = R64; R128[64:, 64:] = R64
    pm[:, 1, :] = R128.T
    i = np.arange(64)
    tri = np.zeros((64, 4, 64), np.float32)
    tri[:, 0, :] = (i[:, None] <= i[None, :])
    tri[:, 1, :] = (i[:, None] >= i[None, :])
    tri[:, 2, :] = (i[:, None] > i[None, :])
    tri[:, 3, :] = (i[:, None] < i[None, :])
    return cmat.astype(bf), pm.astype(bf), tri


_PROG = {}


def _get_prog():
    if "nc" not in _PROG:
        _PROG["nc"] = build_program()
    return _PROG["nc"]


def make_in_maps(x_prompt, x_sample, state_gla, cache_mla_ckv, cache_mla_krope, cache_diff_k, cache_diff_v, c, c_ctx, w_mod, b_mod, g_norm1, g_norm2, w_in,
                 w_gla_a2, b_gla_a, g_gla_out, g_mla_qa, g_mla_kva, w_mla_uq, w_mla_uk, w_mla_uv, g_mla_q, g_mla_k, g_diff_q, g_diff_k, lam_qk, g_diff_sub,
                 w_o_gla, w_o_mla, w_o_diff, w_out, w_mlp1, w_mlp2):
    f = lambda a: np.ascontiguousarray(np.asarray(a, dtype=np.float32))
    cmat, pm, tri = _consts()
    ropeM, ropeD = _rope_tables()
    perm = np.concatenate([np.arange(64, 96), np.arange(0, 64)])
    w_uq_p = f(w_mla_uq).reshape(L, 384, 8, 96)[:, :, :, perm]
    w_ukp = np.zeros((L, 256, 8, 96), np.float32)
    w_ukp[:, :, :, 32:] = f(w_mla_uk).reshape(L, 256, 8, 64)
    gcol = np.zeros((L, 128, 12), np.float32)
    gcol[:, :, 0:3] = f(g_mla_qa).reshape(L, 3, 128).transpose(0, 2, 1)
    gcol[:, :, 3:5] = f(g_mla_kva).reshape(L, 2, 128).transpose(0, 2, 1)
    gcol[:, :96, 5] = f(g_mla_q)[:, perm]
    gcol[:, :96, 6] = f(g_mla_k)[:, perm]
    gcol[:, :, 7] = np.tile(f(g_diff_q), (1, 2))
    gcol[:, :, 8] = np.tile(f(g_diff_k), (1, 2))
    gcol[:, :, 9] = f(g_gla_out)
    gcol[:, :, 10] = f(g_diff_sub)
    w_a2bd = np.zeros((L, 32, 512), np.float32)
    w_a2bd[:, 0:16, 0:256] = f(w_gla_a2)[:, 0]
    w_a2bd[:, 16:32, 256:512] = f(w_gla_a2)[:, 1]
    b_a = f(b_gla_a).reshape(L, 1, 512)
    gnT = np.concatenate([f(g_norm1).reshape(L, 8, 128).transpose(0, 2, 1), f(g_norm2).reshape(L, 8, 128).transpose(0, 2, 1)], axis=2)
    b_modT = f(b_mod).reshape(L, 48, 128).transpose(0, 2, 1)
    w_o3 = np.stack([f(w_o_gla), f(w_o_mla), f(w_o_diff)], axis=1)
    shared = dict(w_mod=f(w_mod), b_modT=f(b_modT), gnT=f(gnT), w_in=f(w_in), w_a2bd=w_a2bd, b_a=b_a, gcol=gcol, w_uq=f(w_uq_p), w_ukp=w_ukp,
                  w_uv=f(w_mla_uv), lam_qk=f(lam_qk).reshape(L, 1, 256), w_o3=f(w_o3), w_out=f(w_out), w_m1=f(w_mlp1), w_m2=f(w_mlp2),
                  ident=np.eye(128, dtype=np.float32), cmat=cmat, pmat=pm, tri=tri, ropeM=ropeM, ropeD=ropeD)
    xp, xs = f(x_prompt), f(x_sample)
    in_maps = []
    for core in range(8):
        b = core % 4
        m = dict(shared)
        m["xin"] = np.concatenate([xp[2 * core], xp[2 * core + 1], xs[b]], axis=0)
        cond = np.stack([f(c_ctx), f(c)[b]], axis=0)
        m["condT"] = f(cond.reshape(2, 8, 128).transpose(2, 1, 0))
        m["st_gla"] = f(f(state_gla)[b].transpose(0, 1, 3, 2, 4))
        m["c_ckv"] = f(cache_mla_ckv)[b]
        m["c_kr"] = f(cache_mla_krope)[b]
        m["c_dk"] = f(cache_diff_k)[b].reshape(L, PAST, 512)
        m["c_dv"] = f(cache_diff_v)[b].reshape(L, PAST, 512)
        in_maps.append(m)
    return in_maps


def kernel(**inputs):
    nc = _get_prog()
    in_maps = make_in_maps(**inputs)
    res = run_bass_kernel_spmd(nc, in_maps, core_ids=list(range(8)))
    R = res.results
    y_prompt = np.zeros((16, 256, D), np.float32)
    y_sample = np.zeros((4, NS_TOK, D), np.float32)
    n_gla = np.zeros((16, L, 2, 4, 64, 128), np.float32)
    n_ckv = np.zeros((16, L, 256, 256), np.float32)
    n_kr = np.zeros((16, L, 256, 32), np.float32)
    n_dk = np.zeros((16, L, 256, 4, 2, 64), np.float32)
    n_dv = np.zeros((16, L, 256, 4, 128), np.float32)
    for core in range(8):
        r = R[core]
        for s in range(2):
            bi = 2 * core + s
            y_prompt[bi] = r["y_out"][s * 256:(s + 1) * 256]
            n_gla[bi] = r["o_gla"][:, s].transpose(0, 1, 3, 2, 4)
            n_ckv[bi] = r["o_ckv"][:, s * 256:(s + 1) * 256]
            n_kr[bi] = r["o_kr"][:, s * 256:(s + 1) * 256]
            n_dk[bi] = r["o_dk"][:, s * 256:(s + 1) * 256].reshape(L, 256, 4, 2, 64)
            n_dv[bi] = r["o_dv"][:, s * 256:(s + 1) * 256].reshape(L, 256, 4, 128)
        if core < 4:
            y_sample[core] = r["y_out"][NP_TOK:]
    return (y_prompt, y_sample, n_gla, n_ckv, n_kr, n_dk, n_dv)
```

```python
import math
import numpy as np
import ml_dtypes
from contextlib import ExitStack
import concourse.bass as bass
import concourse.mybir as mybir
from concourse.bass_utils import run_bass_kernel_spmd

F32 = mybir.dt.float32
BF16 = mybir.dt.bfloat16
AF = mybir.ActivationFunctionType
ALU = mybir.AluOpType

D = 1024
L = 2
NP_TOK = 512
NS_TOK = 4096
T = NP_TOK + NS_TOK
TT = 512
NT = T // TT
PAST = 256
TKS = PAST + NS_TOK
EPS = 1e-6
O_AQ, O_AK, O_AV, O_AR, O_AA, O_QD, O_KVD, O_KR, O_DQ, O_DK, O_DV, O_G = 0, 256, 512, 1024, 1536, 1568, 1952, 2208, 2240, 2752, 3264, 3776
IN_COLS = 6848


class Buf:
    __slots__ = ("lw", "rd", "psum")

    def __init__(self, psum=False):
        self.lw = None
        self.rd = []
        self.psum = psum


class _Rec:
    def __init__(self):
        self.call = None

    def __getattr__(self, name):
        def f(*a, **k):
            self.call = (name, a, k)
            return self
        return f


class Eng:
    def __init__(self, fw, name, handle):
        self.name = name
        self.h = handle
        self.sem = fw.new_sem("e_" + name)
        self.count = 0
        self.waited = {}
        self.thunks = []
        self.snaps = {}


class FW:
    def __init__(self, nc, ctx, n_dma_sems=48):
        self.nc = nc
        self.ctx = ctx
        self.pe = Eng(self, "pe", nc.tensor)
        self.dve = Eng(self, "dve", nc.vector)
        self.act = Eng(self, "act", nc.scalar)
        self.pool = Eng(self, "pool", nc.gpsimd)
        self.sp = Eng(self, "sp", nc.sync)
        self.dma_sems = [self.new_sem(f"d{i}") for i in range(n_dma_sems)]
        self.dma_cnt = [0] * n_dma_sems
        self.snap_of = {}
        self.dma_rr = 0
        self.ew_rr = 0

    def new_sem(self, name):
        return self.ctx.enter_context(self.nc.semaphore(name))

    def _wait(self, E, ev):
        if ev is None:
            return
        sem, val = ev
        if sem is E.sem and E.name == "pe":
            return
        key = id(sem)
        if E.waited.get(key, 0) >= val:
            return
        E.waited[key] = val
        E.thunks.append(lambda h=E.h, s=sem, v=val: h.wait_ge(s, v))
        snap = self.snap_of.get((key, val))
        if snap:
            w = E.waited
            for k2, v2 in snap.items():
                if w.get(k2, 0) < v2:
                    w[k2] = v2

    def _deps(self, E, reads, writes):
        for b in reads:
            self._wait(E, b.lw)
        for b in writes:
            self._wait(E, b.lw)
            for ev in b.rd:
                self._wait(E, ev)

    def _post(self, ev, reads, writes):
        for b in reads:
            b.rd.append(ev)
            if len(b.rd) > 64:
                b.rd = b.rd[-48:]
        for b in writes:
            b.lw = ev
            b.rd = []

    def op(self, E, fn, reads=(), writes=(), inc=True):
        if any(b.psum for b in reads):
            writes = list(writes) + [b for b in reads if b.psum]
            reads = [b for b in reads if not b.psum]
        self._deps(E, reads, writes)
        rec = _Rec()
        fn(rec)
        mname, margs, mkw = rec.call
        if inc:
            E.count += 1
            ev = (E.sem, E.count)
            self.snap_of[(id(E.sem), E.count)] = dict(E.waited)
            E.thunks.append(lambda h=E.h, n=mname, a=margs, k=mkw, s=E.sem: getattr(h, n)(*a, **k).then_inc(s, 1))
        else:
            ev = (E.sem, E.count + 1)
            E.thunks.append(lambda h=E.h, n=mname, a=margs, k=mkw: getattr(h, n)(*a, **k))
        self._post(ev, reads, writes)
        return ev

    def dma(self, Q, out_ap, in_ap, reads=(), writes=(), **kw):
        self._deps(Q, reads, writes)
        i = self.dma_rr
        self.dma_rr = (self.dma_rr + 1) % len(self.dma_sems)
        self.dma_cnt[i] += 16
        sem = self.dma_sems[i]
        ev = (sem, self.dma_cnt[i])
        self.snap_of[(id(sem), self.dma_cnt[i])] = dict(Q.waited)
        Q.thunks.append(lambda h=Q.h, o=out_ap, a=in_ap, s=sem, k=kw: h.dma_start(out=o, in_=a, **k).then_inc(s, 16))
        self._post(ev, reads, writes)
        return ev

    def barrier(self):
        engs = [self.pe, self.dve, self.act, self.pool, self.sp]
        for E in engs:
            for E2 in engs:
                if E2 is not E and E2.count > 0:
                    self._wait(E, (E2.sem, E2.count))
            for i, sem in enumerate(self.dma_sems):
                if self.dma_cnt[i] > 0:
                    self._wait(E, (sem, self.dma_cnt[i]))

    def finish(self, final_bufs):
        for b in final_bufs:
            self._wait(self.sp, b.lw)
        nc = self.nc
        engs = self
        with nc.Block() as block:
            @block.tensor
            def _(e):
                for t in engs.pe.thunks:
                    t()

            @block.vector
            def _(e):
                for t in engs.dve.thunks:
                    t()

            @block.scalar
            def _(e):
                for t in engs.act.thunks:
                    t()

            @block.gpsimd
            def _(e):
                for t in engs.pool.thunks:
                    t()

            @block.sync
            def _(e):
                for t in engs.sp.thunks:
                    t()


class Arena:
    def __init__(self, tensor, nbytes):
        self.t = tensor
        self.nbytes = nbytes
        self.off = 0
        self.peak = 0

    def reset(self):
        self.off = 0

    def alloc(self, name, shape, dt):
        esz = 2 if dt == BF16 else 4
        n = 1
        for d in shape[1:]:
            n *= d
        nb = (n * esz + 63) // 64 * 64
        assert self.off + nb <= self.nbytes, (name, self.off, nb, self.nbytes)
        ap = self.t[0:shape[0], self.off // 4:(self.off + nb) // 4]
        self.off += nb
        self.peak = max(self.peak, self.off)
        if dt == BF16:
            ap = ap.bitcast(BF16)
        ap = ap[:, 0:n]
        if len(shape) == 3:
            ap = ap.rearrange("p (a b) -> p a b", b=shape[2])
        elif len(shape) == 4:
            ap = ap.rearrange("p (a b c) -> p a b c", b=shape[2], c=shape[3])
        return ap


class Ring:
    def __init__(self, alloc, name, shape, dt, n, psum=False):
        self.t = [alloc(f"{name}{i}", shape, dt) for i in range(n)]
        self.b = [Buf(psum) for _ in range(n)]
        self.i = 0

    def get(self):
        i = self.i
        self.i = (i + 1) % len(self.t)
        return self.t[i], self.b[i]


def build_program(debug=False):
    nc = bass.Bass("TRN2", target_bir_lowering=False)
    dram_in = lambda name, shape, dt=F32: nc.dram_tensor(name, list(shape), dt, kind="ExternalInput").ap()
    dram_out = lambda name, shape, dt=F32: nc.dram_tensor(name, list(shape), dt, kind="ExternalOutput").ap()
    dram_tmp = lambda name, shape, dt=F32: nc.dram_tensor(name, list(shape), dt).ap()

    xin = dram_in("xin", [T, D])
    condT = dram_in("condT", [128, 8, 2])
    st_gla = dram_in("st_gla", [L, 2, 64, 4, 128])
    c_ckv = dram_in("c_ckv", [L, PAST, 256])
    c_kr = dram_in("c_kr", [L, PAST, 32])
    c_dk = dram_in("c_dk", [L, PAST, 512])
    c_dv = dram_in("c_dv", [L, PAST, 512])
    w_mod = dram_in("w_mod", [L, D, 6 * D])
    b_modT = dram_in("b_modT", [L, 128, 48])
    gnT = dram_in("gnT", [L, 128, 16])
    w_in = dram_in("w_in", [L, D, IN_COLS])
    w_a2bd = dram_in("w_a2bd", [L, 32, 512])
    b_a = dram_in("b_a", [L, 1, 512])
    gcol = dram_in("gcol", [L, 128, 12])
    w_uq = dram_in("w_uq", [L, 384, 8, 96])
    w_ukp = dram_in("w_ukp", [L, 256, 8, 96])
    w_uv = dram_in("w_uv", [L, 256, 512])
    lam_qk = dram_in("lam_qk", [L, 1, 256])
    w_o3 = dram_in("w_o3", [L, 3, 512, D])
    w_out = dram_in("w_out", [L, D, D])
    w_m1 = dram_in("w_m1", [L, D, 4 * D])
    w_m2 = dram_in("w_m2", [L, 4 * D, D])
    ident_d = dram_in("ident", [128, 128])
    cmat_d = dram_in("cmat", [128, 8, 128], BF16)
    pmat_d = dram_in("pmat", [128, 3, 128], BF16)
    tri_d = dram_in("tri", [64, 4, 64])
    ropeM = dram_in("ropeM", [2, 96, NS_TOK])
    ropeD = dram_in("ropeD", [2, 128, NS_TOK])

    y_out = dram_out("y_out", [T, D])
    o_gla = dram_out("o_gla", [L, 2, 2, 64, 4, 128])
    o_ckv = dram_out("o_ckv", [L, NP_TOK, 256])
    o_kr = dram_out("o_kr", [L, NP_TOK, 32])
    o_dk = dram_out("o_dk", [L, NP_TOK, 512])
    o_dv = dram_out("o_dv", [L, NP_TOK, 512])

    mk = dram_out if debug else dram_tmp
    xT_s = dram_tmp("xT_s", [8, 128, T])
    gq_s = dram_tmp("gq_s", [64, 4, T], BF16)
    gk_s = dram_tmp("gk_s", [64, 4, T], BF16)
    gkt_s = dram_tmp("gkt_s", [T, 256], BF16)
    gvt_s = dram_tmp("gvt_s", [T, 512], BF16)
    gG_s = dram_tmp("gG_s", [T, 512])
    go_s = dram_tmp("go_s", [128, 4, T])
    q_s = dram_tmp("q_s", [8, 96, T], BF16)
    k_s = dram_tmp("k_s", [8, 96, NP_TOK + TKS], BF16)
    v_s = dram_tmp("v_s", [8, 128, (NP_TOK + TKS) // 128, 128], BF16)
    dq_s = dram_tmp("dq_s", [4, 128, T], BF16)
    dk_s = dram_tmp("dk_s", [4, 128, NP_TOK + TKS], BF16)
    dv_s = dram_tmp("dv_s", [4, 128, (NP_TOK + TKS) // 128, 128], BF16)
    ya_s = mk("ya_s", [4, 128, T], BF16)
    yb_s = mk("yb_s", [4, 128, T], BF16)
    yc_s = mk("yc_s", [4, 128, T], BF16)

    with ExitStack() as ctx:
        fw = FW(nc, ctx)
        PE, DVE, ACT, POOL, SP = fw.pe, fw.dve, fw.act, fw.pool, fw.sp
        sb = lambda name, shape, dt=F32: ctx.enter_context(nc.sbuf_tensor("s_" + name, list(shape), dt))
        psb = lambda name, shape, dt=F32: ctx.enter_context(nc.psum_tensor("p_" + name, list(shape), dt))

        psA = Ring(psb, "psA", [128, 512], F32, 4, psum=True)
        psB = Ring(psb, "psB", [128, 512], F32, 4, psum=True)

        OVB = 72 * 1024
        arena = Arena(sb("arena", [128, OVB // 4], F32), OVB)
        ov = arena.alloc

        def ew():
            fw.ew_rr ^= 1
            return DVE if fw.ew_rr else POOL

        ident = sb("ident", [128, 128]); b_ident = Buf()
        cmat = sb("cmat", [128, 8, 128], BF16); b_cmat = Buf()
        pmat = sb("pmat", [128, 3, 128], BF16); b_pmat = Buf()
        tri = sb("tri", [64, 4, 64]); b_tri = Buf()
        ones_row = sb("ones_row", [1, 128], BF16); b_onesrow = Buf()
        ones_rowf = sb("ones_rowf", [1, 128]); b_onesrowf = Buf()
        ones_col = sb("ones_col", [64, 1]); b_onescol = Buf()
        onesf = sb("onesf", [128, 128]); b_onesf = Buf()
        fw.dma(SP, ident[:], ident_d, writes=[b_ident])
        fw.dma(SP, cmat[:], cmat_d, writes=[b_cmat])
        fw.dma(SP, pmat[:], pmat_d, writes=[b_pmat])
        fw.dma(SP, tri[:], tri_d, writes=[b_tri])
        fw.op(DVE, lambda h: h.memset(ones_row[:], 1.0), writes=[b_onesrow])
        fw.op(DVE, lambda h: h.memset(ones_rowf[:], 1.0), writes=[b_onesrowf])
        fw.op(DVE, lambda h: h.memset(ones_col[:], 1.0), writes=[b_onescol])
        fw.op(DVE, lambda h: h.memset(onesf[:], 1.0), writes=[b_onesf])
        C_1024, C_384, C_256, C_96, C_BLK64, C_128, C_ONE, C_SEL32 = range(8)
        P_96, P_128, _ = range(3)

        cT = sb("cT", [128, 8, 2]); b_cT = Buf()
        scT = sb("scT", [128, 8, 2]); b_scT = Buf()
        fw.dma(SP, cT[:], condT, writes=[b_cT])
        fw.op(ACT, lambda h: h.activation(out=scT[:], in_=cT[:], func=AF.Silu), reads=[b_cT], writes=[b_scT])

        modT = sb("modT", [128, 48, 2]); b_modT_b = Buf()
        bmod = sb("bmod", [128, 48]); b_bmod = Buf()
        gn = sb("gn", [128, 16]); b_gn = Buf()
        A1 = sb("A1", [128, 8, 2]); A2 = sb("A2", [128, 8, 2]); b_A = Buf()
        gc = sb("gc", [128, 12]); b_gc = Buf()
        wa2 = sb("wa2", [32, 512], BF16); b_wa2 = Buf()
        ba = sb("ba", [1, 512], BF16); b_ba = Buf()
        lamt = sb("lamt", [1, 256]); b_lamt = Buf()
        lam1 = sb("lam1", [1, 8]); b_lam1 = Buf()
        lamc = sb("lamc", [128, 2]); b_lamc = Buf()
        arena.reset()
        wmod_r = Ring(ov, "wmod", [128, 8, 256], F32, 2)

        w8 = Ring(sb, "w8", [128, 8, 512], BF16, 3)
        w4 = Ring(sb, "w4", [128, 4, 1024], BF16, 2)

        def load_w8(src_ap_rows_cols, ncols, nk=8):
            t, b = w8.get()
            fw.dma(POOL, t[:, 0:nk, 0:ncols], src_ap_rows_cols.rearrange("(k p) c -> p k c", p=128), writes=[b])
            return t, b

        xt_r = Ring(sb, "xt", [128, 8, TT], F32, 1)
        sq_r = Ring(sb, "sq", [128, 8, TT], BF16, 1)
        hT_r = Ring(sb, "hT", [128, 8, TT], BF16, 2)
        rs_r = Ring(sb, "rs", [128, TT], F32, 3)
        f32_r = Ring(sb, "f32", [128, TT], F32, 6)
        bf_r = Ring(sb, "bf", [128, TT], BF16, 8)

        def evac(ps_ap, out_ap, b_ps, b_out, eng=None, func=None, scale=1.0):
            if func is not None or eng is ACT:
                f = func if func is not None else AF.Copy
                return fw.op(ACT, lambda h: h.activation(out=out_ap, in_=ps_ap, func=f, scale=scale), reads=[b_ps], writes=[b_out])
            return fw.op(DVE, lambda h: h.tensor_copy(out=out_ap, in_=ps_ap), reads=[b_ps], writes=[b_out])

        b_xT = [Buf() for _ in range(NT)]
        arena.reset()
        xtok_r = Ring(ov, "xtok", [128, 4, D], F32, 2)

        def phase0(ti):
            xk, bxk = xtok_r.get()
            fw.dma(SP, xk[:], xin[ti * TT:(ti + 1) * TT, :].rearrange("(g p) f -> p g f", p=128), writes=[bxk])
            xt, bxt = xt_r.get()
            for k in range(8):
                ps, bps = psA.get()
                for g in range(4):
                    fw.op(PE, lambda h, ps=ps, g=g, k=k: h.transpose(ps[:, g * 128:(g + 1) * 128], xk[:, g, k * 128:(k + 1) * 128], ident[:]),
                          reads=[bxk, b_ident], writes=[bps], inc=(g == 3))
                evac(ps[:], xt[:, k, :], bps, bxt, eng=(ACT if k % 2 else DVE))
            fw.dma(SP, xT_s[:, :, ti * TT:(ti + 1) * TT].rearrange("k p t -> p k t"), xt[:], reads=[bxt], writes=[b_xT[ti]])

        import os
        for ti in range(int(os.environ.get("KPH0", str(NT)))):
            phase0(ti)
        fw.barrier()

        def rsqrt_eps(src_ap, dst_ap, bsrc, bdst):
            fw.op(ACT, lambda h: h.activation(out=dst_ap, in_=src_ap, func=AF.Ln, bias=EPS), reads=[bsrc], writes=[bdst])
            fw.op(ACT, lambda h: h.activation(out=dst_ap, in_=dst_ap, func=AF.Exp, scale=-0.5), reads=[bdst], writes=[bdst])

        def rms_rstd(sq_aps, ones_ap, nparts, eps=EPS):
            ps, bps = psA.get()
            n = len(sq_aps)
            for i, (a, b) in enumerate(sq_aps):
                fw.op(PE, lambda h, a=a, i=i: h.matmul(ps[0:nparts, :], lhsT=ones_ap, rhs=a, start=(i == 0), stop=(i == n - 1)),
                      reads=[b, b_cmat], writes=[bps], inc=(i == n - 1))
            rs, brs = rs_r.get()
            rsqrt_eps(ps[0:nparts, :], rs[0:nparts, :], bps, brs)
            return rs, brs

        def load_xT(ti):
            xt, bxt = xt_r.get()
            fw.dma(SP, xt[:], xT_s[:, :, ti * TT:(ti + 1) * TT].rearrange("k p t -> p k t"), reads=[b_xT[ti]], writes=[bxt])
            return xt, bxt

        def norm_mod(xt, bxt, Amod, shift_lo, cond):
            sq, bsq = sq_r.get()
            fw.op(ACT, lambda h: h.activation(out=sq[:], in_=xt[:], func=AF.Square), reads=[bxt], writes=[bsq])
            rs, brs = rms_rstd([(sq[:, k, :], bsq) for k in range(8)], cmat[:, C_1024, :], 128)
            hT, bh = hT_r.get()
            for k in range(8):
                tmp, btmp = f32_r.get()
                e = DVE
                fw.op(e, lambda h, k=k, tmp=tmp: h.tensor_tensor(out=tmp[:], in0=xt[:, k, :], in1=rs[:], op=ALU.mult), reads=[bxt, brs], writes=[btmp])
                fw.op(ACT, lambda h, k=k, tmp=tmp: h.activation(out=hT[:, k, :], in_=tmp[:], func=AF.Identity,
                                                               scale=Amod[:, k, cond:cond + 1], bias=modT[:, shift_lo + k, cond:cond + 1]),
                      reads=[btmp, b_A, b_modT_b], writes=[bh])
            return hT, bh

        def proj_fm(wt, bw, col0, m, rhs_fn, rhs_bufs, nk, out_parts=None):
            ps, bps = psA.get()
            for k in range(nk):
                fw.op(PE, lambda h, k=k: h.matmul(ps[0:m, :], lhsT=wt[:, k, col0:col0 + m], rhs=rhs_fn(k), start=(k == 0), stop=(k == nk - 1)),
                      reads=[bw] + rhs_bufs, writes=[bps], inc=(k == nk - 1))
            return ps, bps

        def norm_rope_store(ps, bps, npart, ones_ap, gcolumn, rope, pm_idx, ropetab, dst_ap, dst_buf, keep_f32=None):
            xf, bxf = f32_r.get()
            evac(ps[0:npart, :], xf[0:npart, :], bps, bxf, eng=DVE)
            sq, bsq = bf_r.get()
            fw.op(ACT, lambda h: h.activation(out=sq[0:npart, :], in_=xf[0:npart, :], func=AF.Square), reads=[bxf], writes=[bsq])
            rs, brs = rms_rstd([(sq[0:npart, :], bsq)], ones_ap, npart)
            xn, bxn = f32_r.get()
            fw.op(DVE, lambda h: h.scalar_tensor_tensor(out=xn[0:npart, :], in0=xf[0:npart, :], scalar=gc[0:npart, gcolumn:gcolumn + 1], in1=rs[0:npart, :],
                                                        op0=ALU.mult, op1=ALU.mult), reads=[bxf, brs, b_gc], writes=[bxn])
            if keep_f32 is not None:
                keep_f32(xn, bxn)
            ob, bob = bf_r.get()
            if not rope:
                fw.op(ACT, lambda h: h.activation(out=ob[0:npart, :], in_=xn[0:npart, :], func=AF.Copy), reads=[bxn], writes=[bob])
            else:
                ctab, stab, btab = ropetab
                xb, bxb = bf_r.get()
                fw.op(ACT, lambda h: h.activation(out=xb[0:npart, :], in_=xn[0:npart, :], func=AF.Copy), reads=[bxn], writes=[bxb])
                ps2, bps2 = psA.get()
                fw.op(PE, lambda h: h.matmul(ps2[0:npart, :], lhsT=pmat[0:npart, pm_idx, 0:npart], rhs=xb[0:npart, :], start=True, stop=True),
                      reads=[bxb, b_pmat], writes=[bps2])
                t1, bt1 = f32_r.get()
                fw.op(DVE, lambda h: h.tensor_tensor(out=t1[0:npart, :], in0=xn[0:npart, :], in1=ctab, op=ALU.mult), reads=[bxn, btab], writes=[bt1])
                t2, bt2 = f32_r.get()
                fw.op(DVE, lambda h: h.tensor_tensor(out=t2[0:npart, :], in0=ps2[0:npart, :], in1=stab, op=ALU.mult), reads=[bps2, btab], writes=[bt2])
                fw.op(DVE, lambda h: h.tensor_tensor(out=ob[0:npart, :], in0=t1[0:npart, :], in1=t2[0:npart, :], op=ALU.add), reads=[bt1, bt2], writes=[bob])
            fw.dma(SP, dst_ap, ob[0:npart, :], reads=[bob], writes=[dst_buf])

        def transpose_out(src_fn, src_bufs, nfeat_parts, ncols_total, dst_rows_fn, colslices):
            for g in range(4):
                ps, bps = psA.get()
                n = len(colslices)
                for i, (j, pj, c0) in enumerate(colslices):
                    fw.op(PE, lambda h, j=j, pj=pj, c0=c0, g=g: h.transpose(ps[:, c0:c0 + pj], src_fn(j)[:, g * 128:(g + 1) * 128], ident[0:pj, 0:pj]),
                          reads=src_bufs + [b_ident], writes=[bps], inc=(i == n - 1))
                o, bo = f32_r.get()
                evac(ps[:, 0:ncols_total], o[:, 0:ncols_total], bps, bo, eng=DVE)
                fw.dma(SP, dst_rows_fn(g), o[:, 0:ncols_total], reads=[bo], writes=[Buf()])

        b_gq = Buf(); b_gk = Buf(); b_gkt = Buf(); b_gvt = Buf(); b_gG = Buf(); b_go = Buf()
        b_q = Buf(); b_k = Buf(); b_v = Buf(); b_dq = Buf(); b_dk = Buf(); b_dv = Buf()
        b_ya = Buf(); b_yb = Buf(); b_yc = Buf()
        out_bufs = []

        arena.reset()
        ropM_r = Ring(ov, "ropM", [96, 2, TT], F32, 1)
        ropD_r = Ring(ov, "ropD", [128, 2, TT], F32, 1)
        vaug_r = Ring(ov, "vaug", [128, 8, 4, 128], BF16, 1)
        tok_r = Ring(ov, "tok", [128, 4, 512], BF16, 2)
        tokf_r = Ring(ov, "tokf", [128, 4, 512], F32, 1)
        keep_r = Ring(ov, "keep", [128, 4, TT], F32, 1)
        qdn_r = Ring(ov, "qdn", [128, 3, TT], BF16, 1)
        ckv_r = Ring(ov, "ckv", [128, 2, TT], BF16, 1)
        ckvf_r = Ring(ov, "ckvf", [128, 2, TT], F32, 1)
        krb_r = Ring(ov, "krb", [32, TT], BF16, 1)
        krf_r = Ring(ov, "krf", [32, TT], F32, 1)
        aab_r = Ring(ov, "aab", [32, TT], BF16, 1)
        g4_r = Ring(ov, "g4", [64, 4, TT], BF16, 1)
        r4_r = Ring(ov, "r4", [128, 4, TT], BF16, 1)
        ctok_r = Ring(ov, "ctok", [128, 2, 512], F32, 2)

        def layer_setup(l):
            fw.dma(SP, bmod[:], b_modT[l], writes=[b_bmod])
            fw.dma(SP, gn[:], gnT[l], writes=[b_gn])
            fw.dma(SP, gc[:], gcol[l], writes=[b_gc])
            fw.dma(POOL, wa2[:], w_a2bd[l], writes=[b_wa2])
            fw.dma(POOL, ba[:], b_a[l], writes=[b_ba])
            fw.dma(SP, lamt[:], lam_qk[l], writes=[b_lamt])
            for cb in range(24):
                wm, bwm = wmod_r.get()
                fw.dma(SP, wm[:], w_mod[l, :, cb * 256:(cb + 1) * 256].rearrange("(k p) c -> p k c", p=128), writes=[bwm])
                ps, bps = psA.get()
                for j in range(2):
                    for k in range(8):
                        fw.op(PE, lambda h, j=j, k=k, wm=wm, ps=ps: h.matmul(ps[:, j * 2:j * 2 + 2], lhsT=wm[:, k, j * 128:(j + 1) * 128], rhs=scT[:, k, :],
                                                                          start=(k == 0), stop=(k == 7)),
                              reads=[bwm, b_scT], writes=[bps], inc=(j == 1 and k == 7))
                fw.op(DVE, lambda h, cb=cb, ps=ps: h.tensor_tensor(out=modT[:, cb * 2:(cb + 1) * 2, :], in0=ps[:, 0:4].rearrange("p (j c) -> p j c", c=2),
                                                                 in1=bmod[:, cb * 2:(cb + 1) * 2].unsqueeze(2).broadcast_to([128, 2, 2]), op=ALU.add),
                      reads=[bps, b_bmod], writes=[b_modT_b])
            fw.op(DVE, lambda h: h.scalar_tensor_tensor(out=A1[:], in0=modT[:, 8:16, :], scalar=1.0, in1=gn[:, 0:8].unsqueeze(2).broadcast_to([128, 8, 2]),
                                                        op0=ALU.add, op1=ALU.mult), reads=[b_modT_b, b_gn], writes=[b_A])
            fw.op(DVE, lambda h: h.scalar_tensor_tensor(out=A2[:], in0=modT[:, 32:40, :], scalar=1.0, in1=gn[:, 8:16].unsqueeze(2).broadcast_to([128, 8, 2]),
                                                        op0=ALU.add, op1=ALU.mult), reads=[b_modT_b, b_gn], writes=[b_A])
            lam_init = 0.8 - 0.6 * math.exp(-0.3 * l)
            fw.op(DVE, lambda h: h.tensor_tensor(out=lamt[:, 0:64], in0=lamt[:, 0:64], in1=lamt[:, 64:128], op=ALU.mult), reads=[b_lamt], writes=[b_lamt])
            fw.op(DVE, lambda h: h.tensor_tensor(out=lamt[:, 128:192], in0=lamt[:, 128:192], in1=lamt[:, 192:256], op=ALU.mult), reads=[b_lamt], writes=[b_lamt])
            fw.op(DVE, lambda h: h.reduce_sum(out=lam1[:, 0:1], in_=lamt[:, 0:64], axis=mybir.AxisListType.X), reads=[b_lamt], writes=[b_lam1])
            fw.op(DVE, lambda h: h.reduce_sum(out=lam1[:, 1:2], in_=lamt[:, 128:192], axis=mybir.AxisListType.X), reads=[b_lamt], writes=[b_lam1])
            fw.op(ACT, lambda h: h.activation(out=lam1[:, 2:4], in_=lam1[:, 0:2], func=AF.Exp), reads=[b_lam1], writes=[b_lam1])
            fw.op(DVE, lambda h: h.scalar_tensor_tensor(out=lam1[:, 4:5], in0=lam1[:, 3:4], scalar=-lam_init, in1=lam1[:, 2:3], op0=ALU.add, op1=ALU.subtract),
                  reads=[b_lam1], writes=[b_lam1])
            ps, bps = psA.get()
            fw.op(PE, lambda h: h.matmul(ps[:, 0:1], lhsT=ones_rowf[:, :], rhs=lam1[:, 4:5], start=True, stop=True), reads=[b_onesrowf, b_lam1], writes=[bps])
            fw.op(DVE, lambda h: h.tensor_copy(out=lamc[:, 0:1], in_=ps[:, 0:1]), reads=[bps], writes=[b_lamc])
            return lam_init

        def key_base(ti):
            return 0 if ti == 0 else NP_TOK + PAST + (ti - 1) * TT

        import os as _os
        KP1S = int(_os.environ.get("KP1S", "99"))
        KP1 = int(_os.environ.get("KP1", str(NT)))

        def phase1(l, ti):
            if ti >= KP1:
                return
            lat = ti > 0
            cond = 1 if lat else 0
            t0 = ti * TT
            kb = key_base(ti)
            xt, bxt = load_xT(ti)
            hT, bh = norm_mod(xt, bxt, A1, 0, cond)
            rhs_h = lambda k: hT[:, k, :]
            if lat:
                rM, brM = ropM_r.get()
                fw.dma(SP, rM[:], ropeM[:, :, (ti - 1) * TT:ti * TT].rearrange("c p t -> p c t"), writes=[brM])
                rD, brD = ropD_r.get()
                fw.dma(SP, rD[:], ropeD[:, :, (ti - 1) * TT:ti * TT].rearrange("c p t -> p c t"), writes=[brD])
                tabM = (rM[:, 0, :], rM[:, 1, :], brM)
                tabD = (rD[:, 0, :], rD[:, 1, :], brD)
            else:
                tabM = tabD = None

            if KP1S < 1:
                return
            wt, bw = load_w8(w_in[l, :, O_AQ:O_AQ + 512], 512)
            for which, dst, bdst in ((0, gq_s, b_gq), (1, gk_s, b_gk)):
                g4, bg4 = g4_r.get()
                for hh in range(4):
                    ps, bps = proj_fm(wt, bw, which * 256 + hh * 64, 64, rhs_h, [bh], 8)
                    evac(ps[0:64, :], g4[:, hh, :], bps, bg4, eng=(ACT if hh % 2 else DVE))
                fw.dma(SP, dst[:, :, t0:t0 + TT], g4[:], reads=[bg4], writes=[bdst])
            if KP1S < 2:
                return
            tk, btk = tok_r.get()
            for g in range(4):
                ps, bps = psA.get()
                for k in range(8):
                    fw.op(PE, lambda h, k=k, g=g, ps=ps: h.matmul(ps[:, 0:256], lhsT=hT[:, k, g * 128:(g + 1) * 128], rhs=wt[:, k, 256:512], start=(k == 0), stop=(k == 7)),
                          reads=[bw, bh], writes=[bps], inc=(k == 7))
                evac(ps[:, 0:256], tk[:, g, 0:256], bps, btk, eng=(ACT if g % 2 else DVE))
            fw.dma(SP, gkt_s[t0:t0 + TT, :].rearrange("(g p) c -> p g c", p=128), tk[:, :, 0:256], reads=[btk], writes=[b_gkt])
            wt, bw = load_w8(w_in[l, :, O_AV:O_AV + 512], 512)
            tk, btk = tok_r.get()
            for g in range(4):
                ps, bps = psA.get()
                for k in range(8):
                    fw.op(PE, lambda h, k=k, g=g, ps=ps, wt=wt: h.matmul(ps[:, :], lhsT=hT[:, k, g * 128:(g + 1) * 128], rhs=wt[:, k, 0:512], start=(k == 0), stop=(k == 7)),
                          reads=[bw, bh], writes=[bps], inc=(k == 7))
                evac(ps[:, :], tk[:, g, :], bps, btk, eng=(ACT if g % 2 else DVE))
            fw.dma(SP, gvt_s[t0:t0 + TT, :].rearrange("(g p) c -> p g c", p=128), tk[:], reads=[btk], writes=[b_gvt])
            if KP1S < 3:
                return
            wt, bw = load_w8(w_in[l, :, O_AA:O_AA + 416], 416)
            ps, bps = proj_fm(wt, bw, 0, 32, rhs_h, [bh], 8)
            aab, baab = aab_r.get()
            evac(ps[0:32, :], aab[:, :], bps, baab, eng=DVE)
            gt, bgt = tokf_r.get()
            for g in range(4):
                ps, bps = psA.get()
                fw.op(PE, lambda h, g=g, ps=ps: h.matmul(ps[:, :], lhsT=aab[:, g * 128:(g + 1) * 128], rhs=wa2[:, :], start=True, stop=False),
                      reads=[baab, b_wa2], writes=[bps], inc=False)
                fw.op(PE, lambda h, ps=ps: h.matmul(ps[:, :], lhsT=ones_row[:, :], rhs=ba[:, :], start=False, stop=True), reads=[b_onesrow, b_ba], writes=[bps])
                e1, be1 = f32_r.get()
                fw.op(ACT, lambda h, ps=ps, e1=e1: h.activation(out=e1[:], in_=ps[:], func=AF.Exp, scale=-1.0), reads=[bps], writes=[be1])
                fw.op(ACT, lambda h, g=g, e1=e1: h.activation(out=gt[:, g, :], in_=e1[:], func=AF.Ln, bias=1.0), reads=[be1], writes=[bgt])
            fw.dma(SP, gG_s[t0:t0 + TT, :].rearrange("(g p) c -> p g c", p=128), gt[:], reads=[bgt], writes=[b_gG])

            if KP1S < 4:
                return
            wt_r, bw_r = load_w8(w_in[l, :, O_AR:O_AR + 512], 512)
            rr, brr = r4_r.get()
            for hh in range(4):
                ps_r, bps_r = proj_fm(wt_r, bw_r, hh * 128, 128, rhs_h, [bh], 8)
                fw.op(ACT, lambda h, hh=hh, ps_r=ps_r: h.activation(out=rr[:, hh, :], in_=ps_r[:], func=AF.Silu), reads=[bps_r], writes=[brr])
            fw.dma(SP, rs_s[:, :, t0:t0 + TT], rr[:], reads=[brr], writes=[b_rs])

            if KP1S < 5:
                return
            qdn, bqdn = qdn_r.get()
            qf = []
            for j in range(3):
                ps, bps = proj_fm(wt, bw, 32 + j * 128, 128, rhs_h, [bh], 8)
                xf, bxf = f32_r.get()
                evac(ps[:], xf[:], bps, bxf, eng=DVE)
                sq, bsq = bf_r.get()
                fw.op(ACT, lambda h, sq=sq, xf=xf: h.activation(out=sq[:], in_=xf[:], func=AF.Square), reads=[bxf], writes=[bsq])
                qf.append((xf, bxf, sq, bsq))
            rs, brs = rms_rstd([(q[2][:], q[3]) for q in qf], cmat[:, C_384, :], 128)
            for j in range(3):
                xf, bxf = qf[j][0], qf[j][1]
                fw.op(DVE, lambda h, j=j, xf=xf: h.scalar_tensor_tensor(out=qdn[:, j, :], in0=xf[:], scalar=gc[:, j:j + 1], in1=rs[:], op0=ALU.mult, op1=ALU.mult),
                      reads=[bxf, brs, b_gc], writes=[bqdn])
            for half in range(2):
                wq, bwq = w8.get()
                fw.dma(POOL, wq[:, 0:3, 0:384], w_uq[l, :, half * 4:(half + 1) * 4, :].rearrange("(k p) h c -> p k (h c)", p=128), writes=[bwq])
                for hh in range(4):
                    ps, bps = proj_fm(wq, bwq, hh * 96, 96, lambda k: qdn[:, k, :], [bqdn], 3)
                    head = half * 4 + hh
                    norm_rope_store(ps, bps, 96, cmat[0:96, C_96, 0:96], 5, lat, P_96, tabM, q_s[head, :, t0:t0 + TT], b_q)

            if KP1S < 6:
                return
            wt, bw = load_w8(w_in[l, :, O_KVD:O_KVD + 288], 288)
            ckv, bckv = ckv_r.get()
            ckvf, bckvf = ckvf_r.get()
            kf = []
            for j in range(2):
                ps, bps = proj_fm(wt, bw, j * 128, 128, rhs_h, [bh], 8)
                xf, bxf = f32_r.get()
                evac(ps[:], xf[:], bps, bxf, eng=DVE)
                sq, bsq = bf_r.get()
                fw.op(ACT, lambda h, sq=sq, xf=xf: h.activation(out=sq[:], in_=xf[:], func=AF.Square), reads=[bxf], writes=[bsq])
                kf.append((xf, bxf, sq, bsq))
            rs, brs = rms_rstd([(q[2][:], q[3]) for q in kf], cmat[:, C_256, :], 128)
            for j in range(2):
                xf, bxf = kf[j][0], kf[j][1]
                fw.op(DVE, lambda h, j=j, xf=xf: h.scalar_tensor_tensor(out=ckvf[:, j, :], in0=xf[:], scalar=gc[:, 3 + j:4 + j], in1=rs[:], op0=ALU.mult, op1=ALU.mult),
                      reads=[bxf, brs, b_gc], writes=[bckvf])
            fw.op(ACT, lambda h: h.activation(out=ckv[:], in_=ckvf[:], func=AF.Copy), reads=[bckvf], writes=[bckv])
            ps, bps = proj_fm(wt, bw, 256, 32, rhs_h, [bh], 8)
            krf, bkrf = krf_r.get()
            krb, bkrb = krb_r.get()
            evac(ps[0:32, :], krf[:, :], bps, bkrf, eng=DVE)
            fw.op(ACT, lambda h: h.activation(out=krb[:], in_=krf[:], func=AF.Copy), reads=[bkrf], writes=[bkrb])
            KSUB = _os.environ.get("KSUB", "abcd")
            if not lat:
                if "a" in KSUB:
                    transpose_out(lambda j: ckvf[:, j, :], [bckvf], 128, 256, lambda g: o_ckv[l, g * 128:(g + 1) * 128, :], [(0, 128, 0), (1, 128, 128)])
                if "b" in KSUB:
                    transpose_out(lambda j: krf[:, :], [bkrf], 32, 32, lambda g: o_kr[l, g * 128:(g + 1) * 128, :], [(0, 32, 0)])
            if "c" in KSUB:
                mla_keys(l, lambda k: ckv[:, k, :], [bckv], krb, bkrb, lat, tabM, kb)
            if "d" in KSUB:
                mla_values(l, lambda k, g: ckv[:, k, g * 128:(g + 1) * 128], [bckv], kb)

            if KP1S < 7:
                return
            for which, off, dst, bdst, gcolumn in ((0, O_DQ, dq_s, b_dq, 7), (1, O_DK, dk_s, b_dk, 8)):
                wt, bw = load_w8(w_in[l, :, off:off + 512], 512)
                keep = None
                if which == 1 and not lat:
                    kp, bkp = keep_r.get()
                for hh in range(4):
                    ps, bps = proj_fm(wt, bw, hh * 128, 128, rhs_h, [bh], 8)
                    kf32 = None
                    if which == 1 and not lat:
                        kf32 = lambda xn, bxn, hh=hh: fw.op(ACT, lambda h: h.activation(out=kp[:, hh, :], in_=xn[:], func=AF.Copy), reads=[bxn], writes=[bkp])
                    col0 = t0 if which == 0 else kb
                    norm_rope_store(ps, bps, 128, cmat[:, C_BLK64, :], gcolumn, lat, P_128, tabD, dst[hh, :, col0:col0 + TT], bdst, keep_f32=kf32)
                if which == 1 and not lat:
                    transpose_out(lambda j: kp[:, j, :], [bkp], 128, 512, lambda g: o_dk[l, g * 128:(g + 1) * 128, :], [(j, 128, j * 128) for j in range(4)])
            wt, bw = load_w8(w_in[l, :, O_DV:O_DV + 512], 512)
            tk, btk = tok_r.get()
            if not lat:
                tf, btf = tokf_r.get()
            for g in range(4):
                ps, bps = psA.get()
                for k in range(8):
                    fw.op(PE, lambda h, k=k, g=g, ps=ps, wt=wt: h.matmul(ps[:, :], lhsT=hT[:, k, g * 128:(g + 1) * 128], rhs=wt[:, k, 0:512], start=(k == 0), stop=(k == 7)),
                          reads=[bw, bh], writes=[bps], inc=(k == 7))
                evac(ps[:, :], tk[:, g, :], bps, btk, eng=ACT)
                if not lat:
                    evac(ps[:, :], tf[:, g, :], bps, btf, eng=DVE)
            for hh in range(4):
                fw.dma(SP, dv_s[hh, :, kb // 128:kb // 128 + 4, :], tk[:, :, hh * 128:(hh + 1) * 128], reads=[btk], writes=[b_dv])
            if not lat:
                ob = Buf(); out_bufs.append(ob)
                fw.dma(SP, o_dv[l, :, :].rearrange("(g p) c -> p g c", p=128), tf[:], reads=[btf], writes=[ob])

        def mla_keys(l, ckv_fn, ckv_bufs, krb, bkrb, rope, tabM, kb, ntok=TT):
            for half in range(2):
                wk, bwk = w8.get()
                fw.dma(POOL, wk[:, 0:2, 0:384], w_ukp[l, :, half * 4:(half + 1) * 4, :].rearrange("(k p) h c -> p k (h c)", p=128), writes=[bwk])
                for hh in range(4):
                    ps, bps = psA.get()
                    for k in range(2):
                        fw.op(PE, lambda h, k=k, hh=hh, ps=ps, wk=wk: h.matmul(ps[0:96, 0:ntok], lhsT=wk[:, k, hh * 96:(hh + 1) * 96], rhs=ckv_fn(k), start=(k == 0), stop=False),
                              reads=[bwk] + ckv_bufs, writes=[bps], inc=False)
                    fw.op(PE, lambda h, ps=ps: h.matmul(ps[0:96, 0:ntok], lhsT=cmat[0:32, C_SEL32, 0:96], rhs=krb[:, 0:ntok], start=False, stop=True),
                          reads=[b_cmat, bkrb], writes=[bps])
                    head = half * 4 + hh
                    if ntok == TT:
                        norm_rope_store(ps, bps, 96, cmat[0:96, C_96, 0:96], 6, rope, P_96, tabM, k_s[head, :, kb:kb + TT], b_k)
                    else:
                        norm_store_small(ps, bps, ntok, head, kb)

        def norm_store_small(ps, bps, ntok, head, kb):
            xf, bxf = f32_r.get()
            evac(ps[0:96, 0:ntok], xf[0:96, 0:ntok], bps, bxf, eng=DVE)
            sq, bsq = bf_r.get()
            fw.op(ACT, lambda h: h.activation(out=sq[0:96, 0:ntok], in_=xf[0:96, 0:ntok], func=AF.Square), reads=[bxf], writes=[bsq])
            ps2, bps2 = psA.get()
            fw.op(PE, lambda h: h.matmul(ps2[0:96, 0:ntok], lhsT=cmat[0:96, C_96, 0:96], rhs=sq[0:96, 0:ntok], start=True, stop=True), reads=[bsq, b_cmat], writes=[bps2])
            rs, brs = rs_r.get()
            rsqrt_eps(ps2[0:96, 0:ntok], rs[0:96, 0:ntok], bps2, brs)
            ob, bob = bf_r.get()
            fw.op(DVE, lambda h: h.scalar_tensor_tensor(out=ob[0:96, 0:ntok], in0=xf[0:96, 0:ntok], scalar=gc[0:96, 6:7], in1=rs[0:96, 0:ntok], op0=ALU.mult, op1=ALU.mult),
                  reads=[bxf, brs, b_gc], writes=[bob])
            fw.dma(SP, k_s[head, :, kb:kb + ntok], ob[0:96, 0:ntok], reads=[bob], writes=[b_k])

        vaug_init = [False, False]

        def mla_values(l, ckvT_fn, ckv_bufs, kb, ngrp=4):
            wv, bwv = w8.get()
            fw.dma(POOL, wv[:, 0:2, 0:512], w_uv[l].rearrange("(k p) c -> p k c", p=128), writes=[bwv])
            idx = vaug_r.i
            va, bva = vaug_r.get()
            if not vaug_init[idx]:
                vaug_init[idx] = True
                fw.op(DVE, lambda h: h.memset(va[:], 1.0), writes=[bva])
            for g in range(ngrp):
                ps, bps = psA.get()
                for k in range(2):
                    fw.op(PE, lambda h, k=k, g=g, ps=ps: h.matmul(ps[:, :], lhsT=ckvT_fn(k, g), rhs=wv[:, k, 0:512], start=(k == 0), stop=(k == 1)),
                          reads=[bwv] + ckv_bufs, writes=[bps], inc=(k == 1))
                fw.op(ACT if g % 2 else DVE, (lambda h, g=g, ps=ps: h.activation(out=va[:, :, g, 0:64], in_=ps[:, :].rearrange("p (h c) -> p h c", c=64), func=AF.Copy)) if g % 2
                      else (lambda h, g=g, ps=ps: h.tensor_copy(out=va[:, :, g, 0:64], in_=ps[:, :].rearrange("p (h c) -> p h c", c=64))), reads=[bps], writes=[bva])
            c0 = kb // 128
            fw.dma(SP, v_s[:, :, c0:c0 + ngrp, :].rearrange("h p g e -> p h (g e)"), va[:, :, 0:ngrp, :].rearrange("p h g e -> p h (g e)"), reads=[bva], writes=[b_v])


        def prep_cache(l):
            kb = NP_TOK
            ct, bct = ctok_r.get()
            fw.dma(SP, ct[:, :, 0:256], c_ckv[l].rearrange("(g p) c -> p g c", p=128), writes=[bct])
            ckv, bckv = ckv_r.get()
            for j in range(2):
                ps, bps = psA.get()
                for g in range(2):
                    fw.op(PE, lambda h, j=j, g=g, ps=ps: h.transpose(ps[:, g * 128:(g + 1) * 128], ct[:, g, j * 128:(j + 1) * 128], ident[:]),
                          reads=[bct, b_ident], writes=[bps], inc=(g == 1))
                evac(ps[:, 0:256], ckv[:, j, 0:256], bps, bckv, eng=DVE)
            ct2, bct2 = ctok_r.get()
            fw.dma(SP, ct2[:, :, 0:32], c_kr[l].rearrange("(g p) c -> p g c", p=128), writes=[bct2])
            krb, bkrb = krb_r.get()
            ps, bps = psA.get()
            for g in range(2):
                fw.op(PE, lambda h, g=g, ps=ps: h.transpose(ps[0:32, g * 128:(g + 1) * 128], ct2[:, g, 0:32], ident[:]), reads=[bct2, b_ident], writes=[bps], inc=(g == 1))
            evac(ps[0:32, 0:256], krb[:, 0:256], bps, bkrb, eng=DVE)
            mla_keys(l, lambda k: ckv[:, k, 0:256], [bckv], krb, bkrb, False, None, kb, ntok=256)
            mla_values(l, lambda k, g: ckv[:, k, g * 128:(g + 1) * 128], [bckv], kb, ngrp=2)
            ct, bct = ctok_r.get()
            fw.dma(SP, ct[:], c_dk[l].rearrange("(g p) c -> p g c", p=128), writes=[bct])
            for hh in range(4):
                ps, bps = psA.get()
                for g in range(2):
                    fw.op(PE, lambda h, hh=hh, g=g, ps=ps: h.transpose(ps[:, g * 128:(g + 1) * 128], ct[:, g, hh * 128:(hh + 1) * 128], ident[:]),
                          reads=[bct, b_ident], writes=[bps], inc=(g == 1))
                ob, bob = bf_r.get()
                evac(ps[:, 0:256], ob[:, 0:256], bps, bob, eng=DVE)
                fw.dma(SP, dk_s[hh, :, kb:kb + 256], ob[:, 0:256], reads=[bob], writes=[b_dk])
            ct3, bct3 = ctok_r.get()
            fw.dma(SP, ct3[:], c_dv[l].rearrange("(g p) c -> p g c", p=128), writes=[bct3])
            tkc, btkc = tok_r.get()
            fw.op(DVE, lambda h: h.tensor_copy(out=tkc[:, 0:2, :], in_=ct3[:]), reads=[bct3], writes=[btkc])
            for hh in range(4):
                fw.dma(SP, dv_s[hh, :, kb // 128:kb // 128 + 2, :], tkc[:, 0:2, hh * 128:(hh + 1) * 128], reads=[btkc], writes=[b_dv])

        arena.reset()
        gl_q = Ring(ov, "glq", [64, 4, TT], BF16, 1)
        gl_k = Ring(ov, "glk", [64, 4, TT], BF16, 1)
        gl_kt = Ring(ov, "glkt", [64, 8, 256], BF16, 1)
        gl_vt = Ring(ov, "glvt", [64, 8, 512], BF16, 1)
        gl_G = Ring(ov, "glG", [64, 8, 256], F32, 1)
        gl_o = Ring(ov, "glo", [128, 4, TT], F32, 1)
        gl_of = Ring(ov, "glof", [128, 4, TT], F32, 1)
        gl_r = Ring(ov, "glr", [128, 4, TT], BF16, 2)
        S_f = ov("S_f", [64, 4, 128], F32); S_b = ov("S_b", [64, 4, 128], BF16); b_S = Buf(); b_Sb = Buf()
        sm_r = Ring(ov, "sm", [64, 256], F32, 6)
        smb_r = Ring(ov, "smb", [64, 256], BF16, 8)
        dd_r = Ring(ov, "dd", [64, 4], F32, 3)

        def gla_dir(l, d, seqs):
            incl = tri[:, d, :]
            strict = tri[:, 2 + d, :]
            for (tok0, ntok, seq_idx, lat) in seqs:
                if lat:
                    fw.dma(SP, S_f[:], st_gla[l, d], reads=[b_Sb], writes=[b_S])
                else:
                    fw.op(DVE, lambda h: h.memset(S_f[:], 0.0), reads=[b_Sb], writes=[b_S])
                fw.op(ACT, lambda h: h.activation(out=S_b[:], in_=S_f[:], func=AF.Copy), reads=[b_S], writes=[b_Sb])
                tiles = list(range(tok0, tok0 + ntok, TT)) if ntok >= TT else [tok0]
                if d == 1:
                    tiles = tiles[::-1]
                for tt0 in tiles:
                    base_tile = (tt0 // TT) * TT
                    cq, bcq = gl_q.get(); ck, bck = gl_k.get(); ckt, bckt = gl_kt.get(); cvt, bcvt = gl_vt.get(); cG, bcG = gl_G.get()
                    fw.dma(SP, cq[:], gq_s[:, :, base_tile:base_tile + TT], reads=[b_gq], writes=[bcq])
                    fw.dma(SP, ck[:], gk_s[:, :, base_tile:base_tile + TT], reads=[b_gk], writes=[bck])
                    fw.dma(SP, ckt[:], gkt_s[base_tile:base_tile + TT, :].rearrange("(c p) f -> p c f", p=64), reads=[b_gkt], writes=[bckt])
                    fw.dma(SP, cvt[:], gvt_s[base_tile:base_tile + TT, :].rearrange("(c p) f -> p c f", p=64), reads=[b_gvt], writes=[bcvt])
                    fw.dma(SP, cG[:], gG_s[base_tile:base_tile + TT, d * 256:(d + 1) * 256].rearrange("(c p) f -> p c f", p=64), reads=[b_gG], writes=[bcG])
                    ot, bot = gl_o.get()
                    c_lo = (tt0 - base_tile) // 64
                    nch = min(ntok, TT) // 64
                    chunks = list(range(c_lo, c_lo + nch))
                    if d == 1:
                        chunks = chunks[::-1]
                    for c in chunks:
                        Gc = cG[:, c, :]
                        psb_, bpsb = psA.get()
                        for hh in range(4):
                            fw.op(PE, lambda h, hh=hh, Gc=Gc, p=psb_: h.matmul(p[0:64, hh * 64:(hh + 1) * 64], lhsT=Gc[:, hh * 64:(hh + 1) * 64], rhs=incl, start=True, stop=True),
                                  reads=[bcG, b_tri], writes=[bpsb], inc=(hh == 3))
                        ep, bep = sm_r.get(); en, ben = sm_r.get()
                        fw.op(ACT, lambda h, p=psb_, ep=ep: h.activation(out=ep[:, :], in_=p[0:64, 0:256], func=AF.Exp, scale=-1.0 / 16), reads=[bpsb], writes=[bep])
                        fw.op(ACT, lambda h, p=psb_, en=en: h.activation(out=en[:, :], in_=p[0:64, 0:256], func=AF.Exp, scale=1.0 / 16), reads=[bpsb], writes=[ben])
                        qt, bqt = smb_r.get(); kt_, bkt = smb_r.get()
                        fw.op(DVE, lambda h, c=c, qt=qt, ep=ep: h.scalar_tensor_tensor(out=qt[:, :].rearrange("p (h t) -> p h t", t=64), in0=cq[:, :, c * 64:(c + 1) * 64], scalar=0.125,
                                                                                      in1=ep[:, :].rearrange("p (h t) -> p h t", t=64), op0=ALU.mult, op1=ALU.mult),
                              reads=[bcq, bep], writes=[bqt])
                        fw.op(DVE, lambda h, c=c, kt_=kt_, en=en: h.tensor_tensor(out=kt_[:, :].rearrange("p (h t) -> p h t", t=64), in0=ck[:, :, c * 64:(c + 1) * 64],
                                                                                  in1=en[:, :].rearrange("p (h t) -> p h t", t=64), op=ALU.mult),
                              reads=[bck, ben], writes=[bkt])
                        ps2, bps2 = psA.get()
                        fw.op(PE, lambda h, Gc=Gc, p=ps2: h.matmul(p[0:64, 0:256], lhsT=strict, rhs=Gc, start=True, stop=True), reads=[bcG, b_tri], writes=[bps2])
                        e2, be2 = sm_r.get()
                        fw.op(ACT, lambda h, p=ps2, e2=e2: h.activation(out=e2[:, :], in_=p[0:64, 0:256], func=AF.Exp, scale=-1.0 / 16), reads=[bps2], writes=[be2])
                        kh, bkh = smb_r.get()
                        fw.op(DVE, lambda h, c=c, kh=kh, e2=e2: h.tensor_tensor(out=kh[:, :], in0=ckt[:, c, :], in1=e2[:, :], op=ALU.mult), reads=[bckt, be2], writes=[bkh])
                        ps3, bps3 = psA.get()
                        for hh in range(4):
                            fw.op(PE, lambda h, hh=hh, Gc=Gc, p=ps3: h.matmul(p[0:64, hh:hh + 1], lhsT=Gc[:, hh * 64:(hh + 1) * 64], rhs=ones_col[:, :], start=True, stop=True),
                                  reads=[bcG, b_onescol], writes=[bps3], inc=(hh == 3))
                        dd, bdd = dd_r.get()
                        fw.op(ACT, lambda h, p=ps3, dd=dd: h.activation(out=dd[:, :], in_=p[0:64, 0:4], func=AF.Exp, scale=-1.0 / 16), reads=[bps3], writes=[bdd])
                        ps4, bps4 = psA.get()
                        for hh in range(4):
                            fw.op(PE, lambda h, hh=hh, p=ps4, kt_=kt_, qt=qt: h.matmul(p[0:64, hh * 64:(hh + 1) * 64], lhsT=kt_[:, hh * 64:(hh + 1) * 64], rhs=qt[:, hh * 64:(hh + 1) * 64],
                                                                                     start=True, stop=True), reads=[bkt, bqt], writes=[bps4], inc=(hh == 3))
                        am, bam = smb_r.get()
                        fw.op(DVE, lambda h, p=ps4, am=am: h.tensor_tensor(out=am[:, :].rearrange("p (h t) -> p h t", t=64), in0=p[0:64, 0:256].rearrange("p (h t) -> p h t", t=64),
                                                                         in1=incl.unsqueeze(1).broadcast_to([64, 4, 64]), op=ALU.mult), reads=[bps4, b_tri], writes=[bam])
                        ps5, bps5 = psA.get()
                        for hh in range(4):
                            fw.op(PE, lambda h, hh=hh, c=c, p=ps5, am=am: h.matmul(p[:, hh * 64:(hh + 1) * 64], lhsT=cvt[:, c, hh * 128:(hh + 1) * 128], rhs=am[:, hh * 64:(hh + 1) * 64],
                                                                                 start=True, stop=False), reads=[bcvt, bam], writes=[bps5], inc=False)
                            fw.op(PE, lambda h, hh=hh, p=ps5, qt=qt: h.matmul(p[:, hh * 64:(hh + 1) * 64], lhsT=S_b[:, hh, :], rhs=qt[:, hh * 64:(hh + 1) * 64], start=False, stop=True),
                                  reads=[b_Sb, bqt], writes=[bps5], inc=(hh == 3))
                        fw.op(ACT, lambda h, c=c, p=ps5, ot=ot: h.activation(out=ot[:, :, c * 64:(c + 1) * 64], in_=p[:, 0:256].rearrange("p (h t) -> p h t", t=64), func=AF.Copy),
                              reads=[bps5], writes=[bot])
                        ps6, bps6 = psB.get()
                        for hh in range(4):
                            fw.op(PE, lambda h, hh=hh, c=c, p=ps6, kh=kh: h.matmul(p[0:64, hh * 128:(hh + 1) * 128], lhsT=kh[:, hh * 64:(hh + 1) * 64], rhs=cvt[:, c, hh * 128:(hh + 1) * 128],
                                                                                 start=True, stop=True), reads=[bkh, bcvt], writes=[bps6], inc=(hh == 3))
                        fw.op(DVE, lambda h, dd=dd: h.tensor_tensor(out=S_f[:], in0=S_f[:], in1=dd[:, :].unsqueeze(2).broadcast_to([64, 4, 128]), op=ALU.mult), reads=[bdd], writes=[b_S])
                        fw.op(DVE, lambda h, p=ps6: h.tensor_tensor(out=S_f[:], in0=S_f[:], in1=p[0:64, :].rearrange("p (h v) -> p h v", v=128), op=ALU.add), reads=[bps6], writes=[b_S])
                        fw.op(ACT, lambda h: h.activation(out=S_b[:], in_=S_f[:], func=AF.Copy), reads=[b_S], writes=[b_Sb])
                    cols = slice(tt0 - base_tile, tt0 - base_tile + min(ntok, TT))
                    ncol = min(ntok, TT)
                    if d == 0:
                        fw.dma(SP, go_s[:, :, tt0:tt0 + ncol], ot[:, :, cols], reads=[bot], writes=[b_go])
                    else:
                        of, bof = gl_of.get()
                        fw.dma(SP, of[:, :, 0:ncol], go_s[:, :, tt0:tt0 + ncol], reads=[b_go], writes=[bof])
                        rr, brr = gl_r.get()
                        fw.dma(SP, rr[:, :, 0:ncol], rs_s[:, :, tt0:tt0 + ncol], reads=[b_rs], writes=[brr])
                        fw.op(DVE, lambda h, ot=ot, of=of: h.tensor_tensor(out=of[:, :, 0:ncol], in0=of[:, :, 0:ncol], in1=ot[:, :, cols], op=ALU.add), reads=[bot], writes=[bof])
                        sq, bsq = sq_r.get()
                        fw.op(ACT, lambda h, sq=sq, of=of: h.activation(out=sq[:, 0:4, 0:ncol], in_=of[:, :, 0:ncol], func=AF.Square), reads=[bof], writes=[bsq])
                        ya, bya = gl_r.get()
                        for hh in range(4):
                            ps, bps = psA.get()
                            fw.op(PE, lambda h, hh=hh, ps=ps, sq=sq: h.matmul(ps[:, 0:ncol], lhsT=cmat[:, C_128, :], rhs=sq[:, hh, 0:ncol], start=True, stop=True), reads=[bsq, b_cmat], writes=[bps])
                            rs, brs = rs_r.get()
                            rsqrt_eps(ps[:, 0:ncol], rs[:, 0:ncol], bps, brs)
                            tmp, btmp = f32_r.get()
                            fw.op(DVE, lambda h, hh=hh, tmp=tmp, of=of, rs=rs: h.scalar_tensor_tensor(out=tmp[:, 0:ncol], in0=of[:, hh, 0:ncol], scalar=gc[:, 9:10], in1=rs[:, 0:ncol],
                                                                                                     op0=ALU.mult, op1=ALU.mult), reads=[bof, brs, b_gc], writes=[btmp])
                            fw.op(DVE, lambda h, hh=hh, tmp=tmp, ya=ya, rr=rr: h.tensor_tensor(out=ya[:, hh, 0:ncol], in0=tmp[:, 0:ncol], in1=rr[:, hh, 0:ncol], op=ALU.mult),
                                  reads=[btmp, brr], writes=[bya])
                        fw.dma(SP, ya_s[:, :, tt0:tt0 + ncol].rearrange("h p t -> p h t"), ya[:, :, 0:ncol], reads=[bya], writes=[b_ya])
                if seq_idx is not None:
                    ob = Buf(); out_bufs.append(ob)
                    fw.dma(SP, o_gla[l, seq_idx, d], S_f[:], reads=[b_S], writes=[ob])

        rs_s = dram_tmp("rs_s", [128, 4, T], BF16)
        b_rs = Buf()

        def phase1b(l, ti):
            cond = 1 if ti > 0 else 0
            xt, bxt = load_xT(ti)
            hT, bh = norm_mod(xt, bxt, A1, 0, cond)
            wt, bw = load_w8(w_in[l, :, O_AR:O_AR + 512], 512)
            rr, brr = gl_r.get()
            for hh in range(4):
                ps, bps = proj_fm(wt, bw, hh * 128, 128, lambda k: hT[:, k, :], [bh], 8)
                fw.op(ACT, lambda h, hh=hh, ps=ps: h.activation(out=rr[:, hh, :], in_=ps[:], func=AF.Silu), reads=[bps], writes=[brr])
            fw.dma(SP, rs_s[:, :, ti * TT:(ti + 1) * TT], rr[:], reads=[brr], writes=[b_rs])

        arena.reset()
        at_k = Ring(ov, "atk", [128, TKS], BF16, 2)
        at_v = Ring(ov, "atv", [128, TKS // 128, 128], BF16, 2)
        at_q = Ring(ov, "atq", [128, TT], BF16, 2)
        at_p = Ring(ov, "atp", [128, TT], BF16, 6)
        at_o = Ring(ov, "ato", [128, TT], BF16, 2)
        at_qm = [Ring(ov, "atqm0_", [128, TT], BF16, 2), Ring(ov, "atqm1_", [128, TT], BF16, 2)]
        at_l = Ring(ov, "atl", [128, TT], F32, 4)

        def mla_attn(l, groups):
            sc = 96 ** -0.5
            for i in range(2):
                fw.op(DVE, lambda h: h.memset(at_k.t[i][:], 0.0), writes=[at_k.b[i]])
                fw.op(DVE, lambda h: h.memset(at_q.t[i][:], 0.0), writes=[at_q.b[i]])
                for c2 in range(2):
                    fw.op(DVE, lambda h: h.memset(at_qm[c2].t[i][:], 0.0), writes=[at_qm[c2].b[i]])
            for (q0, nq, k0, nk) in groups:
                nkc = nk // 128
                for head in range(8):
                    kt, bkt = at_k.get(); vt, bvt = at_v.get()
                    fw.dma(SP, kt[0:96, 0:nk], k_s[head, :, k0:k0 + nk], reads=[b_k], writes=[bkt])
                    fw.dma(SP, vt[:, 0:nkc, :], v_s[head, :, k0 // 128:k0 // 128 + nkc, :], reads=[b_v], writes=[bvt])
                    for qq in range(q0, q0 + nq, TT):
                        nqt = min(TT, nq)
                        qt, bqt = at_q.get()
                        fw.dma(SP, qt[0:96, 0:nqt], q_s[head, :, qq:qq + nqt], reads=[b_q], writes=[bqt])
                        po, bpo = psB.get()

                        def qk(c, kt=kt, qt=qt, bkt=bkt, bqt=bqt, nqt=nqt):
                            ps, bps = psA.get()
                            fw.op(PE, lambda h: h.matmul(ps[:, 0:nqt], lhsT=kt[:, c * 128:(c + 1) * 128], rhs=qt[:, 0:nqt], start=True, stop=True),
                                  reads=[bkt, bqt], writes=[bps])
                            return ps, bps

                        LA = 2
                        pend = [qk(i) for i in range(min(LA, nkc))]
                        for c in range(nkc):
                            ps, bps = pend.pop(0)
                            if c + LA < nkc:
                                pend.append(qk(c + LA))
                            pt, bpt = at_p.get()
                            fw.op(ACT, lambda h, ps=ps, pt=pt: h.activation(out=pt[:, 0:nqt], in_=ps[:, 0:nqt], func=AF.Exp, scale=sc), reads=[bps], writes=[bpt])
                            fw.op(PE, lambda h, c=c, po=po, vt=vt, pt=pt: h.matmul(po[:, 0:nqt], lhsT=vt[:, c, :], rhs=pt[:, 0:nqt], start=(c == 0), stop=(c == nkc - 1)),
                                  reads=[bvt, bpt], writes=[bpo], inc=(c == nkc - 1))
                        rc, brc = f32_r.get()
                        fw.op(DVE, lambda h, po=po, rc=rc: h.reciprocal(out=rc[64:128, 0:nqt], in_=po[64:128, 0:nqt]), reads=[bpo], writes=[brc])
                        r2, br2 = f32_r.get()
                        fw.op(DVE, lambda h, rc=rc, r2=r2: h.tensor_copy(out=r2[0:64, 0:nqt], in_=rc[64:128, 0:nqt]), reads=[brc], writes=[br2])
                        ob, bob = at_o.get()
                        fw.op(DVE, lambda h, po=po, r2=r2, ob=ob: h.tensor_tensor(out=ob[0:64, 0:nqt], in0=po[0:64, 0:nqt], in1=r2[0:64, 0:nqt], op=ALU.mult), reads=[bpo, br2], writes=[bob])
                        fw.dma(SP, yb_s[head // 2, (head % 2) * 64:(head % 2) * 64 + 64, qq:qq + nqt], ob[0:64, 0:nqt], reads=[bob], writes=[b_yb])

        def diff_attn(l, groups, lam_init):
            sc = 64 ** -0.5
            for (q0, nq, k0, nk) in groups:
                nkc = nk // 128
                for head in range(4):
                    kt, bkt = at_k.get(); vt, bvt = at_v.get()
                    fw.dma(SP, kt[:, 0:nk], dk_s[head, :, k0:k0 + nk], reads=[b_dk], writes=[bkt])
                    fw.dma(SP, vt[:, 0:nkc, :], dv_s[head, :, k0 // 128:k0 // 128 + nkc, :], reads=[b_dv], writes=[bvt])
                    for qq in range(q0, q0 + nq, TT):
                        nqt = min(TT, nq)
                        qm = [at_qm[0].get(), at_qm[1].get()]
                        fw.dma(SP, qm[0][0][0:64, 0:nqt], dq_s[head, 0:64, qq:qq + nqt], reads=[b_dq], writes=[qm[0][1]])
                        fw.dma(SP, qm[1][0][64:128, 0:nqt], dq_s[head, 64:128, qq:qq + nqt], reads=[b_dq], writes=[qm[1][1]])
                        acc = [psB.get() for _ in range(2)]
                        lac = [at_l.get(), at_l.get()]

                        def qk(i, kt=kt, bkt=bkt, nqt=nqt, qm=qm):
                            c, comp = divmod(i, 2)
                            ps, bps = psA.get()
                            fw.op(PE, lambda h: h.matmul(ps[:, 0:nqt], lhsT=kt[:, c * 128:(c + 1) * 128], rhs=qm[comp][0][:, 0:nqt], start=True, stop=True),
                                  reads=[bkt, qm[comp][1]], writes=[bps])
                            return ps, bps

                        LA = 2
                        pend = [qk(i) for i in range(min(LA, 2 * nkc))]
                        for c in range(nkc):
                            for comp in range(2):
                                ps, bps = pend.pop(0)
                                if c * 2 + comp + LA < 2 * nkc:
                                    pend.append(qk(c * 2 + comp + LA))
                                pt, bpt = at_p.get()
                                fw.op(ACT, lambda h: h.activation(out=pt[:, 0:nqt], in_=ps[:, 0:nqt], func=AF.Exp, scale=sc), reads=[bps], writes=[bpt])
                                po, bpo = acc[comp]
                                fw.op(PE, lambda h: h.matmul(po[:, 0:nqt], lhsT=vt[:, c, :], rhs=pt[:, 0:nqt], start=(c == 0), stop=(c == nkc - 1)),
                                      reads=[bvt, bpt], writes=[bpo], inc=(c == nkc - 1))
                                la, bla = lac[comp]
                                EL = DVE if comp == 0 else POOL
                                if c == 0:
                                    fw.op(EL, lambda h: h.tensor_copy(out=la[:, 0:nqt], in_=pt[:, 0:nqt]), reads=[bpt], writes=[bla])
                                else:
                                    fw.op(EL, lambda h: h.tensor_tensor(out=la[:, 0:nqt], in0=la[:, 0:nqt], in1=pt[:, 0:nqt], op=ALU.add), reads=[bpt], writes=[bla])
                        rr_ = []
                        for comp in range(2):
                            pl, bpl = psA.get()
                            fw.op(PE, lambda h: h.matmul(pl[:, 0:nqt], lhsT=onesf[:, :], rhs=lac[comp][0][:, 0:nqt], start=True, stop=True), reads=[b_onesf, lac[comp][1]], writes=[bpl])
                            rc_, brc_ = f32_r.get()
                            fw.op(DVE, lambda h: h.reciprocal(out=rc_[:, 0:nqt], in_=pl[:, 0:nqt]), reads=[bpl], writes=[brc_])
                            rr_.append((rc_, brc_))
                        (r0, br0), (r1, br1) = rr_
                        o0, bo0 = f32_r.get(); o1, bo1 = f32_r.get()
                        fw.op(DVE, lambda h, o0=o0, r0=r0, p=acc[0][0]: h.tensor_tensor(out=o0[:, 0:nqt], in0=p[:, 0:nqt], in1=r0[:, 0:nqt], op=ALU.mult), reads=[acc[0][1], br0], writes=[bo0])
                        fw.op(DVE, lambda h, o1=o1, r1=r1, p=acc[1][0]: h.tensor_tensor(out=o1[:, 0:nqt], in0=p[:, 0:nqt], in1=r1[:, 0:nqt], op=ALU.mult), reads=[acc[1][1], br1], writes=[bo1])
                        fw.op(DVE, lambda h, o0=o0, o1=o1: h.scalar_tensor_tensor(out=o0[:, 0:nqt], in0=o1[:, 0:nqt], scalar=lamc[:, 0:1], in1=o0[:, 0:nqt], op0=ALU.mult, op1=ALU.add),
                              reads=[bo1, b_lamc], writes=[bo0])
                        sq, bsq = bf_r.get()
                        fw.op(ACT, lambda h, sq=sq, o0=o0: h.activation(out=sq[:, 0:nqt], in_=o0[:, 0:nqt], func=AF.Square), reads=[bo0], writes=[bsq])
                        ps, bps = psA.get()
                        fw.op(PE, lambda h, ps=ps, sq=sq: h.matmul(ps[:, 0:nqt], lhsT=cmat[:, C_128, :], rhs=sq[:, 0:nqt], start=True, stop=True), reads=[bsq, b_cmat], writes=[bps])
                        rs, brs = rs_r.get()
                        rsqrt_eps(ps[:, 0:nqt], rs[:, 0:nqt], bps, brs)
                        t1, bt1 = f32_r.get()
                        fw.op(DVE, lambda h, t1=t1, o0=o0, rs=rs: h.scalar_tensor_tensor(out=t1[:, 0:nqt], in0=o0[:, 0:nqt], scalar=gc[:, 10:11], in1=rs[:, 0:nqt], op0=ALU.mult, op1=ALU.mult),
                              reads=[bo0, brs, b_gc], writes=[bt1])
                        ob, bob = at_o.get()
                        fw.op(ACT, lambda h, t1=t1, ob=ob: h.activation(out=ob[:, 0:nqt], in_=t1[:, 0:nqt], func=AF.Copy, scale=(1.0 - lam_init)), reads=[bt1], writes=[bob])
                        fw.dma(SP, yc_s[head, :, qq:qq + nqt], ob[:, 0:nqt], reads=[bob], writes=[b_yc])

        arena.reset()
        y3_r = Ring(ov, "y3", [128, 12, TT], BF16, 1)
        mg_r = Ring(ov, "mg", [128, 8, TT], BF16, 1)
        macc_r = Ring(ov, "macc", [128, 4, TT], F32, 1)
        aTraw = ov("aTraw", [128, 8192], F32)
        aT_view = aTraw.bitcast(BF16).rearrange("p (a b) -> p a b", b=TT)
        xo_view = aTraw[:, 0:4096].rearrange("p (g f) -> p g f", f=D)
        b_aTraw = Buf()

        def phase3(l, ti, last):
            cond = 1 if ti > 0 else 0
            t0 = ti * TT
            xt, bxt = load_xT(ti)
            hT, bh = norm_mod(xt, bxt, A1, 0, cond)
            y3, by3 = y3_r.get()
            for bi, (src, bsrc) in enumerate(((ya_s, b_ya), (yb_s, b_yb), (yc_s, b_yc))):
                fw.dma(SP, y3[:, bi * 4:(bi + 1) * 4, :], src[:, :, t0:t0 + TT].rearrange("h p t -> p h t"), reads=[bsrc], writes=[by3])
            mg, bmg = mg_r.get()
            macc, bmacc = macc_r.get()
            for half in range(2):
                for bi in range(3):
                    c0 = O_G + bi * 1024 + half * 512
                    wg, bwg = load_w8(w_in[l, :, c0:c0 + 512], 512)
                    wo, bwo = w4.get()
                    fw.dma(POOL, wo[:, :, 0:512], w_o3[l, bi, :, half * 512:(half + 1) * 512].rearrange("(k p) c -> p k c", p=128), writes=[bwo])
                    for j in range(4):
                        oc = half * 4 + j
                        psg, bpsg = proj_fm(wg, bwg, j * 128, 128, lambda k: hT[:, k, :], [bh], 8)
                        sg, bsg = f32_r.get()
                        fw.op(ACT, lambda h: h.activation(out=sg[:], in_=psg[:], func=AF.Sigmoid), reads=[bpsg], writes=[bsg])
                        pso, bpso = proj_fm(wo, bwo, j * 128, 128, lambda k: y3[:, bi * 4 + k, :], [by3], 4)
                        if bi == 0:
                            fw.op(DVE, lambda h: h.tensor_tensor(out=macc[:, j, :], in0=sg[:], in1=pso[:], op=ALU.mult), reads=[bsg, bpso], writes=[bmacc])
                        else:
                            tmp, btmp = f32_r.get()
                            fw.op(DVE, lambda h: h.tensor_tensor(out=tmp[:], in0=sg[:], in1=pso[:], op=ALU.mult), reads=[bsg, bpso], writes=[btmp])
                            if bi == 1:
                                fw.op(DVE, lambda h: h.tensor_tensor(out=macc[:, j, :], in0=macc[:, j, :], in1=tmp[:], op=ALU.add), reads=[btmp], writes=[bmacc])
                            else:
                                fw.op(DVE, lambda h: h.tensor_tensor(out=mg[:, oc, :], in0=macc[:, j, :], in1=tmp[:], op=ALU.add), reads=[btmp, bmacc], writes=[bmg])
            for half in range(2):
                wt, bw = load_w8(w_out[l, :, half * 512:(half + 1) * 512], 512)
                for j in range(4):
                    oc = half * 4 + j
                    ps, bps = proj_fm(wt, bw, j * 128, 128, lambda k: mg[:, k, :], [bmg], 8)
                    fw.op(DVE, lambda h, oc=oc, ps=ps: h.scalar_tensor_tensor(out=xt[:, oc, :], in0=ps[:], scalar=modT[:, 16 + oc, cond:cond + 1], in1=xt[:, oc, :], op0=ALU.mult, op1=ALU.add),
                          reads=[bps, b_modT_b], writes=[bxt])
            h2, bh2 = norm_mod(xt, bxt, A2, 24, cond)
            aT, baT = aT_view, b_aTraw
            for fb in range(8):
                wt, bw = load_w8(w_m1[l, :, fb * 512:(fb + 1) * 512], 512)
                for j in range(4):
                    ps, bps = proj_fm(wt, bw, j * 128, 128, lambda k: h2[:, k, :], [bh2], 8)
                    rl, brl = f32_r.get()
                    fw.op(ACT, lambda h, ps=ps, rl=rl: h.activation(out=rl[:], in_=ps[:], func=AF.Relu), reads=[bps], writes=[brl])
                    fw.op(DVE, lambda h, rl=rl, fb=fb, j=j: h.tensor_tensor(out=aT[:, fb * 4 + j, :], in0=rl[:], in1=rl[:], op=ALU.mult), reads=[brl], writes=[baT])
            for half in range(2):
                accs = [psB.get() for _ in range(4)]
                for kb8 in range(4):
                    wt, bw = load_w8(w_m2[l, kb8 * 1024:(kb8 + 1) * 1024, half * 512:(half + 1) * 512], 512)
                    for j in range(4):
                        ps, bps = accs[j]
                        for k in range(8):
                            kk = kb8 * 8 + k
                            fw.op(PE, lambda h: h.matmul(ps[:], lhsT=wt[:, k, j * 128:(j + 1) * 128], rhs=aT[:, kk, :], start=(kk == 0), stop=(kk == 31)),
                                  reads=[bw, baT], writes=[bps], inc=(k == 7))
                for j in range(4):
                    oc = half * 4 + j
                    ps, bps = accs[j]
                    fw.op(DVE, lambda h: h.scalar_tensor_tensor(out=xt[:, oc, :], in0=ps[:], scalar=modT[:, 40 + oc, cond:cond + 1], in1=xt[:, oc, :], op0=ALU.mult, op1=ALU.add),
                          reads=[bps, b_modT_b], writes=[bxt])
            if not last:
                fw.dma(SP, xT_s[:, :, t0:t0 + TT].rearrange("k p t -> p k t"), xt[:], reads=[bxt], writes=[b_xT[ti]])
            else:
                xo, bxo = xo_view, b_aTraw
                for g in range(4):
                    for half in range(2):
                        ps, bps = psA.get()
                        for k in range(4):
                            kk = half * 4 + k
                            fw.op(PE, lambda h, g=g, k=k, kk=kk, ps=ps: h.transpose(ps[:, k * 128:(k + 1) * 128], xt[:, kk, g * 128:(g + 1) * 128], ident[:]),
                                  reads=[bxt, b_ident], writes=[bps], inc=(k == 3))
                        evac(ps[:], xo[:, g, half * 512:(half + 1) * 512], bps, bxo, eng=(ACT if half else DVE))
                ob = Buf(); out_bufs.append(ob)
                fw.dma(SP, y_out[t0:t0 + TT, :].rearrange("(g p) f -> p g f", p=128), xo[:], reads=[bxo], writes=[ob])

        import os
        KSTOP = int(os.environ.get("KSTOP", "99"))
        step = [0]

        def reached():
            step[0] += 1
            return step[0] > KSTOP

        for l in range(L):
            if reached():
                break
            lam_init = layer_setup(l)
            fw.barrier()
            vaug_init[0] = vaug_init[1] = False
            if reached():
                break
            prep_cache(l)
            if reached():
                break
            for ti in range(NT):
                phase1(l, ti)
            fw.barrier()
            if reached():
                break
            seqs = [(0, 256, 0, False), (256, 256, 1, False), (NP_TOK, NS_TOK, None, True)]
            gla_dir(l, 0, seqs)
            fw.barrier()
            gla_dir(l, 1, seqs)
            fw.barrier()
            if reached():
                break
            groups = [(0, 256, 0, 256), (256, 256, 256, 256), (NP_TOK, NS_TOK, NP_TOK, TKS)]
            mla_attn(l, groups)
            if reached():
                break
            diff_attn(l, groups, lam_init)
            if reached():
                break
            fw.barrier()
            for ti in range(NT):
                phase3(l, ti, l == L - 1)
            fw.barrier()

        fw.finish(out_bufs)
    return nc


def _rope_tables():
    t = np.arange(NS_TOK)
    row, col = t // 64, t % 64

    def cs(nf, pos):
        inv = (10000.0 ** (-np.arange(nf, dtype=np.float32) / nf)).astype(np.float32)
        ang = pos.astype(np.float32)[None, :] * inv[:, None]
        c, s = np.cos(ang).astype(np.float32), np.sin(ang).astype(np.float32)
        return np.concatenate([c, c], 0), np.concatenate([s, s], 0)

    cr, sr = cs(8, row); cc, sc_ = cs(8, col)
    cM = np.concatenate([cr, cc, np.ones((64, NS_TOK), np.float32)], 0)
    sM = np.concatenate([sr, sc_, np.zeros((64, NS_TOK), np.float32)], 0)
    cr, sr = cs(16, row); cc, sc_ = cs(16, col)
    c64 = np.concatenate([cr, cc], 0); s64 = np.concatenate([sr, sc_], 0)
    cD = np.concatenate([c64, c64], 0); sD = np.concatenate([s64, s64], 0)
    return np.stack([cM, sM]).astype(np.float32), np.stack([cD, sD]).astype(np.float32)


def _rot_matrix(nf):
    R = np.zeros((2 * nf, 2 * nf), np.float32)
    for i in range(nf):
        R[i, nf + i] = -1.0
        R[nf + i, i] = 1.0
    return R


def _consts():
    bf = ml_dtypes.bfloat16
    cmat = np.zeros((128, 8, 128), np.float32)
    cmat[:, 0, :] = 1.0 / 1024
    cmat[:, 1, :] = 1.0 / 384
    cmat[:, 2, :] = 1.0 / 256
    cmat[:96, 3, :96] = 1.0 / 96
    cmat[:64, 4, :64] = 1.0 / 64
    cmat[64:, 4, 64:] = 1.0 / 64
    cmat[:, 5, :] = 1.0 / 128
    cmat[:, 6, :] = 1.0
    cmat[:32, 7, :32] = np.eye(32)
    pm = np.zeros((128, 3, 128), np.float32)
    R96 = np.zeros((96, 96), np.float32)
    R96[0:16, 0:16] = _rot_matrix(8); R96[16:32, 16:32] = _rot_matrix(8)
    pm[:96, 0, :96] = R96.T
    R64 = np.zeros((64, 64), np.float32)
    R64[0:32, 0:32] = _rot_matrix(16); R64[32:64, 32:64] = _rot_matrix(16)
    R128 = np.zeros((128, 128), np.float32)
    R128[:64, :64] = R64; R128[64:, 64:] = R64
    pm[:, 1, :] = R128.T
    i = np.arange(64)
    tri = np.zeros((64, 4, 64), np.float32)
    tri[:, 0, :] = (i[:, None] <= i[None, :])
    tri[:, 1, :] = (i[:, None] >= i[None, :])
    tri[:, 2, :] = (i[:, None] > i[None, :])
    tri[:, 3, :] = (i[:, None] < i[None, :])
    return cmat.astype(bf), pm.astype(bf), tri


_PROG = {}


def _get_prog():
    if "nc" not in _PROG:
        _PROG["nc"] = build_program()
    return _PROG["nc"]


def make_in_maps(x_prompt, x_sample, state_gla, cache_mla_ckv, cache_mla_krope, cache_diff_k, cache_diff_v, c, c_ctx, w_mod, b_mod, g_norm1, g_norm2, w_in,
                 w_gla_a2, b_gla_a, g_gla_out, g_mla_qa, g_mla_kva, w_mla_uq, w_mla_uk, w_mla_uv, g_mla_q, g_mla_k, g_diff_q, g_diff_k, lam_qk, g_diff_sub,
                 w_o_gla, w_o_mla, w_o_diff, w_out, w_mlp1, w_mlp2):
    f = lambda a: np.ascontiguousarray(np.asarray(a, dtype=np.float32))
    cmat, pm, tri = _consts()
    ropeM, ropeD = _rope_tables()
    perm = np.concatenate([np.arange(64, 96), np.arange(0, 64)])
    w_uq_p = f(w_mla_uq).reshape(L, 384, 8, 96)[:, :, :, perm]
    w_ukp = np.zeros((L, 256, 8, 96), np.float32)
    w_ukp[:, :, :, 32:] = f(w_mla_uk).reshape(L, 256, 8, 64)
    gcol = np.zeros((L, 128, 12), np.float32)
    gcol[:, :, 0:3] = f(g_mla_qa).reshape(L, 3, 128).transpose(0, 2, 1)
    gcol[:, :, 3:5] = f(g_mla_kva).reshape(L, 2, 128).transpose(0, 2, 1)
    gcol[:, :96, 5] = f(g_mla_q)[:, perm]
    gcol[:, :96, 6] = f(g_mla_k)[:, perm]
    gcol[:, :, 7] = np.tile(f(g_diff_q), (1, 2))
    gcol[:, :, 8] = np.tile(f(g_diff_k), (1, 2))
    gcol[:, :, 9] = f(g_gla_out)
    gcol[:, :, 10] = f(g_diff_sub)
    w_a2bd = np.zeros((L, 32, 512), np.float32)
    w_a2bd[:, 0:16, 0:256] = f(w_gla_a2)[:, 0]
    w_a2bd[:, 16:32, 256:512] = f(w_gla_a2)[:, 1]
    b_a = f(b_gla_a).reshape(L, 1, 512)
    gnT = np.concatenate([f(g_norm1).reshape(L, 8, 128).transpose(0, 2, 1), f(g_norm2).reshape(L, 8, 128).transpose(0, 2, 1)], axis=2)
    b_modT = f(b_mod).reshape(L, 48, 128).transpose(0, 2, 1)
    w_o3 = np.stack([f(w_o_gla), f(w_o_mla), f(w_o_diff)], axis=1)
    shared = dict(w_mod=f(w_mod), b_modT=f(b_modT), gnT=f(gnT), w_in=f(w_in), w_a2bd=w_a2bd, b_a=b_a, gcol=gcol, w_uq=f(w_uq_p), w_ukp=w_ukp,
                  w_uv=f(w_mla_uv), lam_qk=f(lam_qk).reshape(L, 1, 256), w_o3=f(w_o3), w_out=f(w_out), w_m1=f(w_mlp1), w_m2=f(w_mlp2),
                  ident=np.eye(128, dtype=np.float32), cmat=cmat, pmat=pm, tri=tri, ropeM=ropeM, ropeD=ropeD)
    xp, xs = f(x_prompt), f(x_sample)
    in_maps = []
    for core in range(8):
        b = core % 4
        m = dict(shared)
        m["xin"] = np.concatenate([xp[2 * core], xp[2 * core + 1], xs[b]], axis=0)
        cond = np.stack([f(c_ctx), f(c)[b]], axis=0)
        m["condT"] = f(cond.reshape(2, 8, 128).transpose(2, 1, 0))
        m["st_gla"] = f(f(state_gla)[b].transpose(0, 1, 3, 2, 4))
        m["c_ckv"] = f(cache_mla_ckv)[b]
        m["c_kr"] = f(cache_mla_krope)[b]
        m["c_dk"] = f(cache_diff_k)[b].reshape(L, PAST, 512)
        m["c_dv"] = f(cache_diff_v)[b].reshape(L, PAST, 512)
        in_maps.append(m)
    return in_maps


def kernel(**inputs):
    nc = _get_prog()
    in_maps = make_in_maps(**inputs)
    res = run_bass_kernel_spmd(nc, in_maps, core_ids=list(range(8)))
    R = res.results
    y_prompt = np.zeros((16, 256, D), np.float32)
    y_sample = np.zeros((4, NS_TOK, D), np.float32)
    n_gla = np.zeros((16, L, 2, 4, 64, 128), np.float32)
    n_ckv = np.zeros((16, L, 256, 256), np.float32)
    n_kr = np.zeros((16, L, 256, 32), np.float32)
    n_dk = np.zeros((16, L, 256, 4, 2, 64), np.float32)
    n_dv = np.zeros((16, L, 256, 4, 128), np.float32)
    for core in range(8):
        r = R[core]
        for s in range(2):
            bi = 2 * core + s
            y_prompt[bi] = r["y_out"][s * 256:(s + 1) * 256]
            n_gla[bi] = r["o_gla"][:, s].transpose(0, 1, 3, 2, 4)
            n_ckv[bi] = r["o_ckv"][:, s * 256:(s + 1) * 256]
            n_kr[bi] = r["o_kr"][:, s * 256:(s + 1) * 256]
            n_dk[bi] = r["o_dk"][:, s * 256:(s + 1) * 256].reshape(L, 256, 4, 2, 64)
            n_dv[bi] = r["o_dv"][:, s * 256:(s + 1) * 256].reshape(L, 256, 4, 128)
        if core < 4:
            y_sample[core] = r["y_out"][NP_TOK:]
    return (y_prompt, y_sample, n_gla, n_ckv, n_kr, n_dk, n_dv)
```

```python
import math
import numpy as np
import ml_dtypes
from contextlib import ExitStack
import concourse.bass as bass
import concourse.mybir as mybir
from concourse.bass_utils import run_bass_kernel_spmd

F32 = mybir.dt.float32
BF16 = mybir.dt.bfloat16
AF = mybir.ActivationFunctionType
ALU = mybir.AluOpType

D = 1024
L = 2
NP_TOK = 512
NS_TOK = 4096
T = NP_TOK + NS_TOK
TT = 512
NT = T // TT
PAST = 256
TKS = PAST + NS_TOK
EPS = 1e-6
O_AQ, O_AK, O_AV, O_AR, O_AA, O_QD, O_KVD, O_KR, O_DQ, O_DK, O_DV, O_G = 0, 256, 512, 1024, 1536, 1568, 1952, 2208, 2240, 2752, 3264, 3776
IN_COLS = 6848


class Buf:
    __slots__ = ("lw", "rd", "psum")

    def __init__(self, psum=False):
        self.lw = None
        self.rd = []
        self.psum = psum


class _Rec:
    def __init__(self):
        self.call = None

    def __getattr__(self, name):
        def f(*a, **k):
            self.call = (name, a, k)
            return self
        return f


class Eng:
    def __init__(self, fw, name, handle):
        self.name = name
        self.h = handle
        self.sem = fw.new_sem("e_" + name)
        self.count = 0
        self.waited = {}
        self.thunks = []
        self.snaps = {}


class FW:
    def __init__(self, nc, ctx, n_dma_sems=48):
        self.nc = nc
        self.ctx = ctx
        self.pe = Eng(self, "pe", nc.tensor)
        self.dve = Eng(self, "dve", nc.vector)
        self.act = Eng(self, "act", nc.scalar)
        self.pool = Eng(self, "pool", nc.gpsimd)
        self.sp = Eng(self, "sp", nc.sync)
        self.dma_sems = [self.new_sem(f"d{i}") for i in range(n_dma_sems)]
        self.dma_cnt = [0] * n_dma_sems
        self.snap_of = {}
        self.dma_rr = 0
        self.ew_rr = 0

    def new_sem(self, name):
        return self.ctx.enter_context(self.nc.semaphore(name))

    def _wait(self, E, ev):
        if ev is None:
            return
        sem, val = ev
        if sem is E.sem and E.name == "pe":
            return
        key = id(sem)
        if E.waited.get(key, 0) >= val:
            return
        E.waited[key] = val
        E.thunks.append(lambda h=E.h, s=sem, v=val: h.wait_ge(s, v))
        snap = self.snap_of.get((key, val))
        if snap:
            w = E.waited
            for k2, v2 in snap.items():
                if w.get(k2, 0) < v2:
                    w[k2] = v2

    def _deps(self, E, reads, writes):
        for b in reads:
            self._wait(E, b.lw)
        for b in writes:
            self._wait(E, b.lw)
            for ev in b.rd:
                self._wait(E, ev)

    def _post(self, ev, reads, writes):
        for b in reads:
            b.rd.append(ev)
            if len(b.rd) > 64:
                b.rd = b.rd[-48:]
        for b in writes:
            b.lw = ev
            b.rd = []

    def op(self, E, fn, reads=(), writes=(), inc=True):
        if any(b.psum for b in reads):
            writes = list(writes) + [b for b in reads if b.psum]
            reads = [b for b in reads if not b.psum]
        self._deps(E, reads, writes)
        rec = _Rec()
        fn(rec)
        mname, margs, mkw = rec.call
        if inc:
            E.count += 1
            ev = (E.sem, E.count)
            self.snap_of[(id(E.sem), E.count)] = dict(E.waited)
            E.thunks.append(lambda h=E.h, n=mname, a=margs, k=mkw, s=E.sem: getattr(h, n)(*a, **k).then_inc(s, 1))
        else:
            ev = (E.sem, E.count + 1)
            E.thunks.append(lambda h=E.h, n=mname, a=margs, k=mkw: getattr(h, n)(*a, **k))
        self._post(ev, reads, writes)
        return ev

    def dma(self, Q, out_ap, in_ap, reads=(), writes=(), **kw):
        self._deps(Q, reads, writes)
        i = self.dma_rr
        self.dma_rr = (self.dma_rr + 1) % len(self.dma_sems)
        self.dma_cnt[i] += 16
        sem = self.dma_sems[i]
        ev = (sem, self.dma_cnt[i])
        self.snap_of[(id(sem), self.dma_cnt[i])] = dict(Q.waited)
        Q.thunks.append(lambda h=Q.h, o=out_ap, a=in_ap, s=sem, k=kw: h.dma_start(out=o, in_=a, **k).then_inc(s, 16))
        self._post(ev, reads, writes)
        return ev

    def barrier(self):
        engs = [self.pe, self.dve, self.act, self.pool, self.sp]
        for E in engs:
            for E2 in engs:
                if E2 is not E and E2.count > 0:
                    self._wait(E, (E2.sem, E2.count))
            for i, sem in enumerate(self.dma_sems):
                if self.dma_cnt[i] > 0:
                    self._wait(E, (sem, self.dma_cnt[i]))

    def finish(self, final_bufs):
        for b in final_bufs:
            self._wait(self.sp, b.lw)
        nc = self.nc
        engs = self
        with nc.Block() as block:
            @block.tensor
            def _(e):
                for t in engs.pe.thunks:
                    t()

            @block.vector
            def _(e):
                for t in engs.dve.thunks:
                    t()

            @block.scalar
            def _(e):
                for t in engs.act.thunks:
                    t()

            @block.gpsimd
            def _(e):
                for t in engs.pool.thunks:
                    t()

            @block.sync
            def _(e):
                for t in engs.sp.thunks:
                    t()


class Arena:
    def __init__(self, tensor, nbytes):
        self.t = tensor
        self.nbytes = nbytes
        self.off = 0
        self.peak = 0

    def reset(self):
        self.off = 0

    def alloc(self, name, shape, dt):
        esz = 2 if dt == BF16 else 4
        n = 1
        for d in shape[1:]:
            n *= d
        nb = (n * esz + 63) // 64 * 64
        assert self.off + nb <= self.nbytes, (name, self.off, nb, self.nbytes)
        ap = self.t[0:shape[0], self.off // 4:(self.off + nb) // 4]
        self.off += nb
        self.peak = max(self.peak, self.off)
        if dt == BF16:
            ap = ap.bitcast(BF16)
        ap = ap[:, 0:n]
        if len(shape) == 3:
            ap = ap.rearrange("p (a b) -> p a b", b=shape[2])
        elif len(shape) == 4:
            ap = ap.rearrange("p (a b c) -> p a b c", b=shape[2], c=shape[3])
        return ap


class Ring:
    def __init__(self, alloc, name, shape, dt, n, psum=False):
        self.t = [alloc(f"{name}{i}", shape, dt) for i in range(n)]
        self.b = [Buf(psum) for _ in range(n)]
        self.i = 0

    def get(self):
        i = self.i
        self.i = (i + 1) % len(self.t)
        return self.t[i], self.b[i]


def build_program(debug=False):
    nc = bass.Bass("TRN2", target_bir_lowering=False)
    dram_in = lambda name, shape, dt=F32: nc.dram_tensor(name, list(shape), dt, kind="ExternalInput").ap()
    dram_out = lambda name, shape, dt=F32: nc.dram_tensor(name, list(shape), dt, kind="ExternalOutput").ap()
    dram_tmp = lambda name, shape, dt=F32: nc.dram_tensor(name, list(shape), dt).ap()

    xin = dram_in("xin", [T, D])
    condT = dram_in("condT", [128, 8, 2])
    st_gla = dram_in("st_gla", [L, 2, 64, 4, 128])
    c_ckv = dram_in("c_ckv", [L, PAST, 256])
    c_kr = dram_in("c_kr", [L, PAST, 32])
    c_dk = dram_in("c_dk", [L, PAST, 512])
    c_dv = dram_in("c_dv", [L, PAST, 512])
    w_mod = dram_in("w_mod", [L, D, 6 * D])
    b_modT = dram_in("b_modT", [L, 128, 48])
    gnT = dram_in("gnT", [L, 128, 16])
    w_in = dram_in("w_in", [L, D, IN_COLS])
    w_a2bd = dram_in("w_a2bd", [L, 32, 512])
    b_a = dram_in("b_a", [L, 1, 512])
    gcol = dram_in("gcol", [L, 128, 12])
    w_uq = dram_in("w_uq", [L, 384, 8, 96])
    w_ukp = dram_in("w_ukp", [L, 256, 8, 96])
    w_uv = dram_in("w_uv", [L, 256, 512])
    lam_qk = dram_in("lam_qk", [L, 1, 256])
    w_o3 = dram_in("w_o3", [L, 3, 512, D])
    w_out = dram_in("w_out", [L, D, D])
    w_m1 = dram_in("w_m1", [L, D, 4 * D])
    w_m2 = dram_in("w_m2", [L, 4 * D, D])
    ident_d = dram_in("ident", [128, 128])
    cmat_d = dram_in("cmat", [128, 8, 128], BF16)
    pmat_d = dram_in("pmat", [128, 3, 128], BF16)
    tri_d = dram_in("tri", [64, 4, 64])
    ropeM = dram_in("ropeM", [2, 96, NS_TOK])
    ropeD = dram_in("ropeD", [2, 128, NS_TOK])

    y_out = dram_out("y_out", [T, D])
    o_gla = dram_out("o_gla", [L, 2, 2, 64, 4, 128])
    o_ckv = dram_out("o_ckv", [L, NP_TOK, 256])
    o_kr = dram_out("o_kr", [L, NP_TOK, 32])
    o_dk = dram_out("o_dk", [L, NP_TOK, 512])
    o_dv = dram_out("o_dv", [L, NP_TOK, 512])

    mk = dram_out if debug else dram_tmp
    xT_s = dram_tmp("xT_s", [8, 128, T])
    gq_s = dram_tmp("gq_s", [64, 4, T], BF16)
    gk_s = dram_tmp("gk_s", [64, 4, T], BF16)
    gkt_s = dram_tmp("gkt_s", [T, 256], BF16)
    gvt_s = dram_tmp("gvt_s", [T, 512], BF16)
    gG_s = dram_tmp("gG_s", [T, 512])
    go_s = dram_tmp("go_s", [128, 4, T])
    q_s = dram_tmp("q_s", [8, 96, T], BF16)
    k_s = dram_tmp("k_s", [8, 96, NP_TOK + TKS], BF16)
    v_s = dram_tmp("v_s", [8, 128, (NP_TOK + TKS) // 128, 128], BF16)
    dq_s = dram_tmp("dq_s", [4, 128, T], BF16)
    dk_s = dram_tmp("dk_s", [4, 128, NP_TOK + TKS], BF16)
    dv_s = dram_tmp("dv_s", [4, 128, (NP_TOK + TKS) // 128, 128], BF16)
    ya_s = mk("ya_s", [4, 128, T], BF16)
    yb_s = mk("yb_s", [4, 128, T], BF16)
    yc_s = mk("yc_s", [4, 128, T], BF16)

    with ExitStack() as ctx:
        fw = FW(nc, ctx)
        PE, DVE, ACT, POOL, SP = fw.pe, fw.dve, fw.act, fw.pool, fw.sp
        sb = lambda name, shape, dt=F32: ctx.enter_context(nc.sbuf_tensor("s_" + name, list(shape), dt))
        psb = lambda name, shape, dt=F32: ctx.enter_context(nc.psum_tensor("p_" + name, list(shape), dt))

        psA = Ring(psb, "psA", [128, 512], F32, 4, psum=True)
        psB = Ring(psb, "psB", [128, 512], F32, 4, psum=True)

        OVB = 72 * 1024
        arena = Arena(sb("arena", [128, OVB // 4], F32), OVB)
        ov = arena.alloc

        def ew():
            fw.ew_rr ^= 1
            return DVE if fw.ew_rr else POOL

        ident = sb("ident", [128, 128]); b_ident = Buf()
        cmat = sb("cmat", [128, 8, 128], BF16); b_cmat = Buf()
        pmat = sb("pmat", [128, 3, 128], BF16); b_pmat = Buf()
        tri = sb("tri", [64, 4, 64]); b_tri = Buf()
        ones_row = sb("ones_row", [1, 128], BF16); b_onesrow = Buf()
        ones_rowf = sb("ones_rowf", [1, 128]); b_onesrowf = Buf()
        ones_col = sb("ones_col", [64, 1]); b_onescol = Buf()
        onesf = sb("onesf", [128, 128]); b_onesf = Buf()
        fw.dma(SP, ident[:], ident_d, writes=[b_ident])
        fw.dma(SP, cmat[:], cmat_d, writes=[b_cmat])
        fw.dma(SP, pmat[:], pmat_d, writes=[b_pmat])
        fw.dma(SP, tri[:], tri_d, writes=[b_tri])
        fw.op(DVE, lambda h: h.memset(ones_row[:], 1.0), writes=[b_onesrow])
        fw.op(DVE, lambda h: h.memset(ones_rowf[:], 1.0), writes=[b_onesrowf])
        fw.op(DVE, lambda h: h.memset(ones_col[:], 1.0), writes=[b_onescol])
        fw.op(DVE, lambda h: h.memset(onesf[:], 1.0), writes=[b_onesf])
        C_1024, C_384, C_256, C_96, C_BLK64, C_128, C_ONE, C_SEL32 = range(8)
        P_96, P_128, _ = range(3)

        cT = sb("cT", [128, 8, 2]); b_cT = Buf()
        scT = sb("scT", [128, 8, 2]); b_scT = Buf()
        fw.dma(SP, cT[:], condT, writes=[b_cT])
        fw.op(ACT, lambda h: h.activation(out=scT[:], in_=cT[:], func=AF.Silu), reads=[b_cT], writes=[b_scT])

        modT = sb("modT", [128, 48, 2]); b_modT_b = Buf()
        bmod = sb("bmod", [128, 48]); b_bmod = Buf()
        gn = sb("gn", [128, 16]); b_gn = Buf()
        A1 = sb("A1", [128, 8, 2]); A2 = sb("A2", [128, 8, 2]); b_A = Buf()
        gc = sb("gc", [128, 12]); b_gc = Buf()
        wa2 = sb("wa2", [32, 512], BF16); b_wa2 = Buf()
        ba = sb("ba", [1, 512], BF16); b_ba = Buf()
        lamt = sb("lamt", [1, 256]); b_lamt = Buf()
        lam1 = sb("lam1", [1, 8]); b_lam1 = Buf()
        lamc = sb("lamc", [128, 2]); b_lamc = Buf()
        arena.reset()
        wmod_r = Ring(ov, "wmod", [128, 8, 256], F32, 2)

        w8 = Ring(sb, "w8", [128, 8, 512], BF16, 3)
        w4 = Ring(sb, "w4", [128, 4, 1024], BF16, 2)

        def load_w8(src_ap_rows_cols, ncols, nk=8):
            t, b = w8.get()
            fw.dma(POOL, t[:, 0:nk, 0:ncols], src_ap_rows_cols.rearrange("(k p) c -> p k c", p=128), writes=[b])
            return t, b

        xt_r = Ring(sb, "xt", [128, 8, TT], F32, 1)
        sq_r = Ring(sb, "sq", [128, 8, TT], BF16, 1)
        hT_r = Ring(sb, "hT", [128, 8, TT], BF16, 2)
        rs_r = Ring(sb, "rs", [128, TT], F32, 4)
        f32_r = Ring(sb, "f32", [128, TT], F32, 12)
        bf_r = Ring(sb, "bf", [128, TT], BF16, 12)

        def evac(ps_ap, out_ap, b_ps, b_out, eng=None, func=None, scale=1.0):
            if func is not None or eng is ACT:
                f = func if func is not None else AF.Copy
                return fw.op(ACT, lambda h: h.activation(out=out_ap, in_=ps_ap, func=f, scale=scale), reads=[b_ps], writes=[b_out])
            return fw.op(DVE, lambda h: h.tensor_copy(out=out_ap, in_=ps_ap), reads=[b_ps], writes=[b_out])

        b_xT = [Buf() for _ in range(NT)]
        arena.reset()
        xtok_r = Ring(ov, "xtok", [128, 4, D], F32, 2)

        def phase0(ti):
            xk, bxk = xtok_r.get()
            fw.dma(SP, xk[:], xin[ti * TT:(ti + 1) * TT, :].rearrange("(g p) f -> p g f", p=128), writes=[bxk])
            xt, bxt = xt_r.get()
            for k in range(8):
                ps, bps = psA.get()
                for g in range(4):
                    fw.op(PE, lambda h, ps=ps, g=g, k=k: h.transpose(ps[:, g * 128:(g + 1) * 128], xk[:, g, k * 128:(k + 1) * 128], ident[:]),
                          reads=[bxk, b_ident], writes=[bps], inc=(g == 3))
                evac(ps[:], xt[:, k, :], bps, bxt, eng=(ACT if k % 2 else DVE))
            fw.dma(SP, xT_s[:, :, ti * TT:(ti + 1) * TT].rearrange("k p t -> p k t"), xt[:], reads=[bxt], writes=[b_xT[ti]])

        import os
        for ti in range(int(os.environ.get("KPH0", str(NT)))):
            phase0(ti)
        fw.barrier()

        def rsqrt_eps(src_ap, dst_ap, bsrc, bdst):
            fw.op(ACT, lambda h: h.activation(out=dst_ap, in_=src_ap, func=AF.Ln, bias=EPS), reads=[bsrc], writes=[bdst])
            fw.op(ACT, lambda h: h.activation(out=dst_ap, in_=dst_ap, func=AF.Exp, scale=-0.5), reads=[bdst], writes=[bdst])

        def rms_rstd(sq_aps, ones_ap, nparts, eps=EPS):
            ps, bps = psA.get()
            n = len(sq_aps)
            for i, (a, b) in enumerate(sq_aps):
                fw.op(PE, lambda h, a=a, i=i: h.matmul(ps[0:nparts, :], lhsT=ones_ap, rhs=a, start=(i == 0), stop=(i == n - 1)),
                      reads=[b, b_cmat], writes=[bps], inc=(i == n - 1))
            rs, brs = rs_r.get()
            rsqrt_eps(ps[0:nparts, :], rs[0:nparts, :], bps, brs)
            return rs, brs

        def load_xT(ti):
            xt, bxt = xt_r.get()
            fw.dma(SP, xt[:], xT_s[:, :, ti * TT:(ti + 1) * TT].rearrange("k p t -> p k t"), reads=[b_xT[ti]], writes=[bxt])
            return xt, bxt

        def norm_mod(xt, bxt, Amod, shift_lo, cond):
            sq, bsq = sq_r.get()
            fw.op(ACT, lambda h: h.activation(out=sq[:], in_=xt[:], func=AF.Square), reads=[bxt], writes=[bsq])
            rs, brs = rms_rstd([(sq[:, k, :], bsq) for k in range(8)], cmat[:, C_1024, :], 128)
            hT, bh = hT_r.get()
            for k in range(8):
                tmp, btmp = f32_r.get()
                e = DVE
                fw.op(e, lambda h, k=k, tmp=tmp: h.tensor_tensor(out=tmp[:], in0=xt[:, k, :], in1=rs[:], op=ALU.mult), reads=[bxt, brs], writes=[btmp])
                fw.op(ACT, lambda h, k=k, tmp=tmp: h.activation(out=hT[:, k, :], in_=tmp[:], func=AF.Identity,
                                                               scale=Amod[:, k, cond:cond + 1], bias=modT[:, shift_lo + k, cond:cond + 1]),
                      reads=[btmp, b_A, b_modT_b], writes=[bh])
            return hT, bh

        def proj_fm(wt, bw, col0, m, rhs_fn, rhs_bufs, nk, out_parts=None):
            ps, bps = psA.get()
            for k in range(nk):
                fw.op(PE, lambda h, k=k: h.matmul(ps[0:m, :], lhsT=wt[:, k, col0:col0 + m], rhs=rhs_fn(k), start=(k == 0), stop=(k == nk - 1)),
                      reads=[bw] + rhs_bufs, writes=[bps], inc=(k == nk - 1))
            return ps, bps

        def norm_rope_store(ps, bps, npart, ones_ap, gcolumn, rope, pm_idx, ropetab, dst_ap, dst_buf, keep_f32=None):
            xf, bxf = f32_r.get()
            evac(ps[0:npart, :], xf[0:npart, :], bps, bxf, eng=DVE)
            sq, bsq = bf_r.get()
            fw.op(ACT, lambda h: h.activation(out=sq[0:npart, :], in_=xf[0:npart, :], func=AF.Square), reads=[bxf], writes=[bsq])
            rs, brs = rms_rstd([(sq[0:npart, :], bsq)], ones_ap, npart)
            xn, bxn = f32_r.get()
            fw.op(DVE, lambda h: h.scalar_tensor_tensor(out=xn[0:npart, :], in0=xf[0:npart, :], scalar=gc[0:npart, gcolumn:gcolumn + 1], in1=rs[0:npart, :],
                                                        op0=ALU.mult, op1=ALU.mult), reads=[bxf, brs, b_gc], writes=[bxn])
            if keep_f32 is not None:
                keep_f32(xn, bxn)
            ob, bob = bf_r.get()
            if not rope:
                fw.op(ACT, lambda h: h.activation(out=ob[0:npart, :], in_=xn[0:npart, :], func=AF.Copy), reads=[bxn], writes=[bob])
            else:
                ctab, stab, btab = ropetab
                xb, bxb = bf_r.get()
                fw.op(ACT, lambda h: h.activation(out=xb[0:npart, :], in_=xn[0:npart, :], func=AF.Copy), reads=[bxn], writes=[bxb])
                ps2, bps2 = psA.get()
                fw.op(PE, lambda h: h.matmul(ps2[0:npart, :], lhsT=pmat[0:npart, pm_idx, 0:npart], rhs=xb[0:npart, :], start=True, stop=True),
                      reads=[bxb, b_pmat], writes=[bps2])
                t1, bt1 = f32_r.get()
                fw.op(DVE, lambda h: h.tensor_tensor(out=t1[0:npart, :], in0=xn[0:npart, :], in1=ctab, op=ALU.mult), reads=[bxn, btab], writes=[bt1])
                t2, bt2 = f32_r.get()
                fw.op(DVE, lambda h: h.tensor_tensor(out=t2[0:npart, :], in0=ps2[0:npart, :], in1=stab, op=ALU.mult), reads=[bps2, btab], writes=[bt2])
                fw.op(DVE, lambda h: h.tensor_tensor(out=ob[0:npart, :], in0=t1[0:npart, :], in1=t2[0:npart, :], op=ALU.add), reads=[bt1, bt2], writes=[bob])
            fw.dma(SP, dst_ap, ob[0:npart, :], reads=[bob], writes=[dst_buf])

        def transpose_out(src_fn, src_bufs, nfeat_parts, ncols_total, dst_rows_fn, colslices):
            for g in range(4):
                ps, bps = psA.get()
                n = len(colslices)
                for i, (j, pj, c0) in enumerate(colslices):
                    fw.op(PE, lambda h, j=j, pj=pj, c0=c0, g=g: h.transpose(ps[:, c0:c0 + pj], src_fn(j)[:, g * 128:(g + 1) * 128], ident[0:pj, 0:pj]),
                          reads=src_bufs + [b_ident], writes=[bps], inc=(i == n - 1))
                o, bo = f32_r.get()
                evac(ps[:, 0:ncols_total], o[:, 0:ncols_total], bps, bo, eng=DVE)
                fw.dma(SP, dst_rows_fn(g), o[:, 0:ncols_total], reads=[bo], writes=[Buf()])

        b_gq = Buf(); b_gk = Buf(); b_gkt = Buf(); b_gvt = Buf(); b_gG = Buf(); b_go = Buf()
        b_q = Buf(); b_k = Buf(); b_v = Buf(); b_dq = Buf(); b_dk = Buf(); b_dv = Buf()
        b_ya = Buf(); b_yb = Buf(); b_yc = Buf()
        out_bufs = []

        arena.reset()
        ropM_r = Ring(ov, "ropM", [96, 2, TT], F32, 1)
        ropD_r = Ring(ov, "ropD", [128, 2, TT], F32, 1)
        vaug_r = Ring(ov, "vaug", [128, 8, 4, 128], BF16, 1)
        tok_r = Ring(ov, "tok", [128, 4, 512], BF16, 2)
        tokf_r = Ring(ov, "tokf", [128, 4, 512], F32, 1)
        keep_r = Ring(ov, "keep", [128, 4, TT], F32, 1)
        qdn_r = Ring(ov, "qdn", [128, 3, TT], BF16, 1)
        ckv_r = Ring(ov, "ckv", [128, 2, TT], BF16, 1)
        ckvf_r = Ring(ov, "ckvf", [128, 2, TT], F32, 1)
        krb_r = Ring(ov, "krb", [32, TT], BF16, 1)
        krf_r = Ring(ov, "krf", [32, TT], F32, 1)
        aab_r = Ring(ov, "aab", [32, TT], BF16, 1)
        g4_r = Ring(ov, "g4", [64, 4, TT], BF16, 1)
        r4_r = Ring(ov, "r4", [128, 4, TT], BF16, 1)
        ctok_r = Ring(ov, "ctok", [128, 2, 512], F32, 2)

        def layer_setup(l):
            fw.dma(SP, bmod[:], b_modT[l], writes=[b_bmod])
            fw.dma(SP, gn[:], gnT[l], writes=[b_gn])
            fw.dma(SP, gc[:], gcol[l], writes=[b_gc])
            fw.dma(POOL, wa2[:], w_a2bd[l], writes=[b_wa2])
            fw.dma(POOL, ba[:], b_a[l], writes=[b_ba])
            fw.dma(SP, lamt[:], lam_qk[l], writes=[b_lamt])
            for cb in range(24):
                wm, bwm = wmod_r.get()
                fw.dma(SP, wm[:], w_mod[l, :, cb * 256:(cb + 1) * 256].rearrange("(k p) c -> p k c", p=128), writes=[bwm])
                ps, bps = psA.get()
                for j in range(2):
                    for k in range(8):
                        fw.op(PE, lambda h, j=j, k=k, wm=wm, ps=ps: h.matmul(ps[:, j * 2:j * 2 + 2], lhsT=wm[:, k, j * 128:(j + 1) * 128], rhs=scT[:, k, :],
                                                                          start=(k == 0), stop=(k == 7)),
                              reads=[bwm, b_scT], writes=[bps], inc=(j == 1 and k == 7))
                fw.op(DVE, lambda h, cb=cb, ps=ps: h.tensor_tensor(out=modT[:, cb * 2:(cb + 1) * 2, :], in0=ps[:, 0:4].rearrange("p (j c) -> p j c", c=2),
                                                                 in1=bmod[:, cb * 2:(cb + 1) * 2].unsqueeze(2).broadcast_to([128, 2, 2]), op=ALU.add),
                      reads=[bps, b_bmod], writes=[b_modT_b])
            fw.op(DVE, lambda h: h.scalar_tensor_tensor(out=A1[:], in0=modT[:, 8:16, :], scalar=1.0, in1=gn[:, 0:8].unsqueeze(2).broadcast_to([128, 8, 2]),
                                                        op0=ALU.add, op1=ALU.mult), reads=[b_modT_b, b_gn], writes=[b_A])
            fw.op(DVE, lambda h: h.scalar_tensor_tensor(out=A2[:], in0=modT[:, 32:40, :], scalar=1.0, in1=gn[:, 8:16].unsqueeze(2).broadcast_to([128, 8, 2]),
                                                        op0=ALU.add, op1=ALU.mult), reads=[b_modT_b, b_gn], writes=[b_A])
            lam_init = 0.8 - 0.6 * math.exp(-0.3 * l)
            fw.op(DVE, lambda h: h.tensor_tensor(out=lamt[:, 0:64], in0=lamt[:, 0:64], in1=lamt[:, 64:128], op=ALU.mult), reads=[b_lamt], writes=[b_lamt])
            fw.op(DVE, lambda h: h.tensor_tensor(out=lamt[:, 128:192], in0=lamt[:, 128:192], in1=lamt[:, 192:256], op=ALU.mult), reads=[b_lamt], writes=[b_lamt])
            fw.op(DVE, lambda h: h.reduce_sum(out=lam1[:, 0:1], in_=lamt[:, 0:64], axis=mybir.AxisListType.X), reads=[b_lamt], writes=[b_lam1])
            fw.op(DVE, lambda h: h.reduce_sum(out=lam1[:, 1:2], in_=lamt[:, 128:192], axis=mybir.AxisListType.X), reads=[b_lamt], writes=[b_lam1])
            fw.op(ACT, lambda h: h.activation(out=lam1[:, 2:4], in_=lam1[:, 0:2], func=AF.Exp), reads=[b_lam1], writes=[b_lam1])
            fw.op(DVE, lambda h: h.scalar_tensor_tensor(out=lam1[:, 4:5], in0=lam1[:, 3:4], scalar=-lam_init, in1=lam1[:, 2:3], op0=ALU.add, op1=ALU.subtract),
                  reads=[b_lam1], writes=[b_lam1])
            ps, bps = psA.get()
            fw.op(PE, lambda h: h.matmul(ps[:, 0:1], lhsT=ones_rowf[:, :], rhs=lam1[:, 4:5], start=True, stop=True), reads=[b_onesrowf, b_lam1], writes=[bps])
            fw.op(DVE, lambda h: h.tensor_copy(out=lamc[:, 0:1], in_=ps[:, 0:1]), reads=[bps], writes=[b_lamc])
            return lam_init

        def key_base(ti):
            return 0 if ti == 0 else NP_TOK + PAST + (ti - 1) * TT

        import os as _os
        KP1S = int(_os.environ.get("KP1S", "99"))
        KP1 = int(_os.environ.get("KP1", str(NT)))

        def phase1(l, ti):
            if ti >= KP1:
                return
            lat = ti > 0
            cond = 1 if lat else 0
            t0 = ti * TT
            kb = key_base(ti)
            xt, bxt = load_xT(ti)
            hT, bh = norm_mod(xt, bxt, A1, 0, cond)
            rhs_h = lambda k: hT[:, k, :]
            if lat:
                rM, brM = ropM_r.get()
                fw.dma(SP, rM[:], ropeM[:, :, (ti - 1) * TT:ti * TT].rearrange("c p t -> p c t"), writes=[brM])
                rD, brD = ropD_r.get()
                fw.dma(SP, rD[:], ropeD[:, :, (ti - 1) * TT:ti * TT].rearrange("c p t -> p c t"), writes=[brD])
                tabM = (rM[:, 0, :], rM[:, 1, :], brM)
                tabD = (rD[:, 0, :], rD[:, 1, :], brD)
            else:
                tabM = tabD = None

            if KP1S < 1:
                return
            wt, bw = load_w8(w_in[l, :, O_AQ:O_AQ + 512], 512)
            for which, dst, bdst in ((0, gq_s, b_gq), (1, gk_s, b_gk)):
                g4, bg4 = g4_r.get()
                for hh in range(4):
                    ps, bps = proj_fm(wt, bw, which * 256 + hh * 64, 64, rhs_h, [bh], 8)
                    evac(ps[0:64, :], g4[:, hh, :], bps, bg4, eng=(ACT if hh % 2 else DVE))
                fw.dma(SP, dst[:, :, t0:t0 + TT], g4[:], reads=[bg4], writes=[bdst])
            if KP1S < 2:
                return
            tk, btk = tok_r.get()
            for g in range(4):
                ps, bps = psA.get()
                for k in range(8):
                    fw.op(PE, lambda h, k=k, g=g, ps=ps: h.matmul(ps[:, 0:256], lhsT=hT[:, k, g * 128:(g + 1) * 128], rhs=wt[:, k, 256:512], start=(k == 0), stop=(k == 7)),
                          reads=[bw, bh], writes=[bps], inc=(k == 7))
                evac(ps[:, 0:256], tk[:, g, 0:256], bps, btk, eng=(ACT if g % 2 else DVE))
            fw.dma(SP, gkt_s[t0:t0 + TT, :].rearrange("(g p) c -> p g c", p=128), tk[:, :, 0:256], reads=[btk], writes=[b_gkt])
            wt, bw = load_w8(w_in[l, :, O_AV:O_AV + 512], 512)
            tk, btk = tok_r.get()
            for g in range(4):
                ps, bps = psA.get()
                for k in range(8):
                    fw.op(PE, lambda h, k=k, g=g, ps=ps, wt=wt: h.matmul(ps[:, :], lhsT=hT[:, k, g * 128:(g + 1) * 128], rhs=wt[:, k, 0:512], start=(k == 0), stop=(k == 7)),
                          reads=[bw, bh], writes=[bps], inc=(k == 7))
                evac(ps[:, :], tk[:, g, :], bps, btk, eng=(ACT if g % 2 else DVE))
            fw.dma(SP, gvt_s[t0:t0 + TT, :].rearrange("(g p) c -> p g c", p=128), tk[:], reads=[btk], writes=[b_gvt])
            if KP1S < 3:
                return
            wt, bw = load_w8(w_in[l, :, O_AA:O_AA + 416], 416)
            ps, bps = proj_fm(wt, bw, 0, 32, rhs_h, [bh], 8)
            aab, baab = aab_r.get()
            evac(ps[0:32, :], aab[:, :], bps, baab, eng=DVE)
            gt, bgt = tokf_r.get()
            for g in range(4):
                ps, bps = psA.get()
                fw.op(PE, lambda h, g=g, ps=ps: h.matmul(ps[:, :], lhsT=aab[:, g * 128:(g + 1) * 128], rhs=wa2[:, :], start=True, stop=False),
                      reads=[baab, b_wa2], writes=[bps], inc=False)
                fw.op(PE, lambda h, ps=ps: h.matmul(ps[:, :], lhsT=ones_row[:, :], rhs=ba[:, :], start=False, stop=True), reads=[b_onesrow, b_ba], writes=[bps])
                e1, be1 = f32_r.get()
                fw.op(ACT, lambda h, ps=ps, e1=e1: h.activation(out=e1[:], in_=ps[:], func=AF.Exp, scale=-1.0), reads=[bps], writes=[be1])
                fw.op(ACT, lambda h, g=g, e1=e1: h.activation(out=gt[:, g, :], in_=e1[:], func=AF.Ln, bias=1.0), reads=[be1], writes=[bgt])
            fw.dma(SP, gG_s[t0:t0 + TT, :].rearrange("(g p) c -> p g c", p=128), gt[:], reads=[bgt], writes=[b_gG])

            if KP1S < 4:
                return
            wt_r, bw_r = load_w8(w_in[l, :, O_AR:O_AR + 512], 512)
            rr, brr = r4_r.get()
            for hh in range(4):
                ps_r, bps_r = proj_fm(wt_r, bw_r, hh * 128, 128, rhs_h, [bh], 8)
                fw.op(ACT, lambda h, hh=hh, ps_r=ps_r: h.activation(out=rr[:, hh, :], in_=ps_r[:], func=AF.Silu), reads=[bps_r], writes=[brr])
            fw.dma(SP, rs_s[:, :, t0:t0 + TT], rr[:], reads=[brr], writes=[b_rs])

            if KP1S < 5:
                return
            qdn, bqdn = qdn_r.get()
            qf = []
            for j in range(3):
                ps, bps = proj_fm(wt, bw, 32 + j * 128, 128, rhs_h, [bh], 8)
                xf, bxf = f32_r.get()
                evac(ps[:], xf[:], bps, bxf, eng=DVE)
                sq, bsq = bf_r.get()
                fw.op(ACT, lambda h, sq=sq, xf=xf: h.activation(out=sq[:], in_=xf[:], func=AF.Square), reads=[bxf], writes=[bsq])
                qf.append((xf, bxf, sq, bsq))
            rs, brs = rms_rstd([(q[2][:], q[3]) for q in qf], cmat[:, C_384, :], 128)
            for j in range(3):
                xf, bxf = qf[j][0], qf[j][1]
                fw.op(DVE, lambda h, j=j, xf=xf: h.scalar_tensor_tensor(out=qdn[:, j, :], in0=xf[:], scalar=gc[:, j:j + 1], in1=rs[:], op0=ALU.mult, op1=ALU.mult),
                      reads=[bxf, brs, b_gc], writes=[bqdn])
            for half in range(2):
                wq, bwq = w8.get()
                fw.dma(POOL, wq[:, 0:3, 0:384], w_uq[l, :, half * 4:(half + 1) * 4, :].rearrange("(k p) h c -> p k (h c)", p=128), writes=[bwq])
                for hh in range(4):
                    ps, bps = proj_fm(wq, bwq, hh * 96, 96, lambda k: qdn[:, k, :], [bqdn], 3)
                    head = half * 4 + hh
                    norm_rope_store(ps, bps, 96, cmat[0:96, C_96, 0:96], 5, lat, P_96, tabM, q_s[head, :, t0:t0 + TT], b_q)

            if KP1S < 6:
                return
            wt, bw = load_w8(w_in[l, :, O_KVD:O_KVD + 288], 288)
            ckv, bckv = ckv_r.get()
            ckvf, bckvf = ckvf_r.get()
            kf = []
            for j in range(2):
                ps, bps = proj_fm(wt, bw, j * 128, 128, rhs_h, [bh], 8)
                xf, bxf = f32_r.get()
                evac(ps[:], xf[:], bps, bxf, eng=DVE)
                sq, bsq = bf_r.get()
                fw.op(ACT, lambda h, sq=sq, xf=xf: h.activation(out=sq[:], in_=xf[:], func=AF.Square), reads=[bxf], writes=[bsq])
                kf.append((xf, bxf, sq, bsq))
            rs, brs = rms_rstd([(q[2][:], q[3]) for q in kf], cmat[:, C_256, :], 128)
            for j in range(2):
                xf, bxf = kf[j][0], kf[j][1]
                fw.op(DVE, lambda h, j=j, xf=xf: h.scalar_tensor_tensor(out=ckvf[:, j, :], in0=xf[:], scalar=gc[:, 3 + j:4 + j], in1=rs[:], op0=ALU.mult, op1=ALU.mult),
                      reads=[bxf, brs, b_gc], writes=[bckvf])
            fw.op(ACT, lambda h: h.activation(out=ckv[:], in_=ckvf[:], func=AF.Copy), reads=[bckvf], writes=[bckv])
            ps, bps = proj_fm(wt, bw, 256, 32, rhs_h, [bh], 8)
            krf, bkrf = krf_r.get()
            krb, bkrb = krb_r.get()
            evac(ps[0:32, :], krf[:, :], bps, bkrf, eng=DVE)
            fw.op(ACT, lambda h: h.activation(out=krb[:], in_=krf[:], func=AF.Copy), reads=[bkrf], writes=[bkrb])
            KSUB = _os.environ.get("KSUB", "abcd")
            if not lat:
                if "a" in KSUB:
                    transpose_out(lambda j: ckvf[:, j, :], [bckvf], 128, 256, lambda g: o_ckv[l, g * 128:(g + 1) * 128, :], [(0, 128, 0), (1, 128, 128)])
                if "b" in KSUB:
                    transpose_out(lambda j: krf[:, :], [bkrf], 32, 32, lambda g: o_kr[l, g * 128:(g + 1) * 128, :], [(0, 32, 0)])
            if "c" in KSUB:
                mla_keys(l, lambda k: ckv[:, k, :], [bckv], krb, bkrb, lat, tabM, kb)
            if "d" in KSUB:
                mla_values(l, lambda k, g: ckv[:, k, g * 128:(g + 1) * 128], [bckv], kb)

            if KP1S < 7:
                return
            for which, off, dst, bdst, gcolumn in ((0, O_DQ, dq_s, b_dq, 7), (1, O_DK, dk_s, b_dk, 8)):
                wt, bw = load_w8(w_in[l, :, off:off + 512], 512)
                keep = None
                if which == 1 and not lat:
                    kp, bkp = keep_r.get()
                for hh in range(4):
                    ps, bps = proj_fm(wt, bw, hh * 128, 128, rhs_h, [bh], 8)
                    kf32 = None
                    if which == 1 and not lat:
                        kf32 = lambda xn, bxn, hh=hh: fw.op(ACT, lambda h: h.activation(out=kp[:, hh, :], in_=xn[:], func=AF.Copy), reads=[bxn], writes=[bkp])
                    col0 = t0 if which == 0 else kb
                    norm_rope_store(ps, bps, 128, cmat[:, C_BLK64, :], gcolumn, lat, P_128, tabD, dst[hh, :, col0:col0 + TT], bdst, keep_f32=kf32)
                if which == 1 and not lat:
                    transpose_out(lambda j: kp[:, j, :], [bkp], 128, 512, lambda g: o_dk[l, g * 128:(g + 1) * 128, :], [(j, 128, j * 128) for j in range(4)])
            wt, bw = load_w8(w_in[l, :, O_DV:O_DV + 512], 512)
            tk, btk = tok_r.get()
            if not lat:
                tf, btf = tokf_r.get()
            for g in range(4):
                ps, bps = psA.get()
                for k in range(8):
                    fw.op(PE, lambda h, k=k, g=g, ps=ps, wt=wt: h.matmul(ps[:, :], lhsT=hT[:, k, g * 128:(g + 1) * 128], rhs=wt[:, k, 0:512], start=(k == 0), stop=(k == 7)),
                          reads=[bw, bh], writes=[bps], inc=(k == 7))
                evac(ps[:, :], tk[:, g, :], bps, btk, eng=ACT)
                if not lat:
                    evac(ps[:, :], tf[:, g, :], bps, btf, eng=DVE)
            for hh in range(4):
                fw.dma(SP, dv_s[hh, :, kb // 128:kb // 128 + 4, :], tk[:, :, hh * 128:(hh + 1) * 128], reads=[btk], writes=[b_dv])
            if not lat:
                ob = Buf(); out_bufs.append(ob)
                fw.dma(SP, o_dv[l, :, :].rearrange("(g p) c -> p g c", p=128), tf[:], reads=[btf], writes=[ob])

        def mla_keys(l, ckv_fn, ckv_bufs, krb, bkrb, rope, tabM, kb, ntok=TT):
            for half in range(2):
                wk, bwk = w8.get()
                fw.dma(POOL, wk[:, 0:2, 0:384], w_ukp[l, :, half * 4:(half + 1) * 4, :].rearrange("(k p) h c -> p k (h c)", p=128), writes=[bwk])
                for hh in range(4):
                    ps, bps = psA.get()
                    for k in range(2):
                        fw.op(PE, lambda h, k=k, hh=hh, ps=ps, wk=wk: h.matmul(ps[0:96, 0:ntok], lhsT=wk[:, k, hh * 96:(hh + 1) * 96], rhs=ckv_fn(k), start=(k == 0), stop=False),
                              reads=[bwk] + ckv_bufs, writes=[bps], inc=False)
                    fw.op(PE, lambda h, ps=ps: h.matmul(ps[0:96, 0:ntok], lhsT=cmat[0:32, C_SEL32, 0:96], rhs=krb[:, 0:ntok], start=False, stop=True),
                          reads=[b_cmat, bkrb], writes=[bps])
                    head = half * 4 + hh
                    if ntok == TT:
                        norm_rope_store(ps, bps, 96, cmat[0:96, C_96, 0:96], 6, rope, P_96, tabM, k_s[head, :, kb:kb + TT], b_k)
                    else:
                        norm_store_small(ps, bps, ntok, head, kb)

        def norm_store_small(ps, bps, ntok, head, kb):
            xf, bxf = f32_r.get()
            evac(ps[0:96, 0:ntok], xf[0:96, 0:ntok], bps, bxf, eng=DVE)
            sq, bsq = bf_r.get()
            fw.op(ACT, lambda h: h.activation(out=sq[0:96, 0:ntok], in_=xf[0:96, 0:ntok], func=AF.Square), reads=[bxf], writes=[bsq])
            ps2, bps2 = psA.get()
            fw.op(PE, lambda h: h.matmul(ps2[0:96, 0:ntok], lhsT=cmat[0:96, C_96, 0:96], rhs=sq[0:96, 0:ntok], start=True, stop=True), reads=[bsq, b_cmat], writes=[bps2])
            rs, brs = rs_r.get()
            rsqrt_eps(ps2[0:96, 0:ntok], rs[0:96, 0:ntok], bps2, brs)
            ob, bob = bf_r.get()
            fw.op(DVE, lambda h: h.scalar_tensor_tensor(out=ob[0:96, 0:ntok], in0=xf[0:96, 0:ntok], scalar=gc[0:96, 6:7], in1=rs[0:96, 0:ntok], op0=ALU.mult, op1=ALU.mult),
                  reads=[bxf, brs, b_gc], writes=[bob])
            fw.dma(SP, k_s[head, :, kb:kb + ntok], ob[0:96, 0:ntok], reads=[bob], writes=[b_k])

        vaug_init = [False, False]

        def mla_values(l, ckvT_fn, ckv_bufs, kb, ngrp=4):
            wv, bwv = w8.get()
            fw.dma(POOL, wv[:, 0:2, 0:512], w_uv[l].rearrange("(k p) c -> p k c", p=128), writes=[bwv])
            idx = vaug_r.i
            va, bva = vaug_r.get()
            if not vaug_init[idx]:
                vaug_init[idx] = True
                fw.op(DVE, lambda h: h.memset(va[:], 1.0), writes=[bva])
            for g in range(ngrp):
                ps, bps = psA.get()
                for k in range(2):
                    fw.op(PE, lambda h, k=k, g=g, ps=ps: h.matmul(ps[:, :], lhsT=ckvT_fn(k, g), rhs=wv[:, k, 0:512], start=(k == 0), stop=(k == 1)),
                          reads=[bwv] + ckv_bufs, writes=[bps], inc=(k == 1))
                fw.op(ACT if g % 2 else DVE, (lambda h, g=g, ps=ps: h.activation(out=va[:, :, g, 0:64], in_=ps[:, :].rearrange("p (h c) -> p h c", c=64), func=AF.Copy)) if g % 2
                      else (lambda h, g=g, ps=ps: h.tensor_copy(out=va[:, :, g, 0:64], in_=ps[:, :].rearrange("p (h c) -> p h c", c=64))), reads=[bps], writes=[bva])
            c0 = kb // 128
            fw.dma(SP, v_s[:, :, c0:c0 + ngrp, :].rearrange("h p g e -> p h (g e)"), va[:, :, 0:ngrp, :].rearrange("p h g e -> p h (g e)"), reads=[bva], writes=[b_v])


        def prep_cache(l):
            kb = NP_TOK
            ct, bct = ctok_r.get()
            fw.dma(SP, ct[:, :, 0:256], c_ckv[l].rearrange("(g p) c -> p g c", p=128), writes=[bct])
            ckv, bckv = ckv_r.get()
            for j in range(2):
                ps, bps = psA.get()
                for g in range(2):
                    fw.op(PE, lambda h, j=j, g=g, ps=ps: h.transpose(ps[:, g * 128:(g + 1) * 128], ct[:, g, j * 128:(j + 1) * 128], ident[:]),
                          reads=[bct, b_ident], writes=[bps], inc=(g == 1))
                evac(ps[:, 0:256], ckv[:, j, 0:256], bps, bckv, eng=DVE)
            ct2, bct2 = ctok_r.get()
            fw.dma(SP, ct2[:, :, 0:32], c_kr[l].rearrange("(g p) c -> p g c", p=128), writes=[bct2])
            krb, bkrb = krb_r.get()
            ps, bps = psA.get()
            for g in range(2):
                fw.op(PE, lambda h, g=g, ps=ps: h.transpose(ps[0:32, g * 128:(g + 1) * 128], ct2[:, g, 0:32], ident[:]), reads=[bct2, b_ident], writes=[bps], inc=(g == 1))
            evac(ps[0:32, 0:256], krb[:, 0:256], bps, bkrb, eng=DVE)
            mla_keys(l, lambda k: ckv[:, k, 0:256], [bckv], krb, bkrb, False, None, kb, ntok=256)
            mla_values(l, lambda k, g: ckv[:, k, g * 128:(g + 1) * 128], [bckv], kb, ngrp=2)
            ct, bct = ctok_r.get()
            fw.dma(SP, ct[:], c_dk[l].rearrange("(g p) c -> p g c", p=128), writes=[bct])
            for hh in range(4):
                ps, bps = psA.get()
                for g in range(2):
                    fw.op(PE, lambda h, hh=hh, g=g, ps=ps: h.transpose(ps[:, g * 128:(g + 1) * 128], ct[:, g, hh * 128:(hh + 1) * 128], ident[:]),
                          reads=[bct, b_ident], writes=[bps], inc=(g == 1))
                ob, bob = bf_r.get()
                evac(ps[:, 0:256], ob[:, 0:256], bps, bob, eng=DVE)
                fw.dma(SP, dk_s[hh, :, kb:kb + 256], ob[:, 0:256], reads=[bob], writes=[b_dk])
            ct3, bct3 = ctok_r.get()
            fw.dma(SP, ct3[:], c_dv[l].rearrange("(g p) c -> p g c", p=128), writes=[bct3])
            tkc, btkc = tok_r.get()
            fw.op(DVE, lambda h: h.tensor_copy(out=tkc[:, 0:2, :], in_=ct3[:]), reads=[bct3], writes=[btkc])
            for hh in range(4):
                fw.dma(SP, dv_s[hh, :, kb // 128:kb // 128 + 2, :], tkc[:, 0:2, hh * 128:(hh + 1) * 128], reads=[btkc], writes=[b_dv])

        arena.reset()
        gl_q = Ring(ov, "glq", [64, 4, TT], BF16, 1)
        gl_k = Ring(ov, "glk", [64, 4, TT], BF16, 1)
        gl_kt = Ring(ov, "glkt", [64, 8, 256], BF16, 1)
        gl_vt = Ring(ov, "glvt", [64, 8, 512], BF16, 1)
        gl_G = Ring(ov, "glG", [64, 8, 256], F32, 1)
        gl_o = Ring(ov, "glo", [128, 4, TT], F32, 1)
        gl_of = Ring(ov, "glof", [128, 4, TT], F32, 1)
        gl_r = Ring(ov, "glr", [128, 4, TT], BF16, 2)
        S_f = ov("S_f", [64, 4, 128], F32); S_b = ov("S_b", [64, 4, 128], BF16); b_S = Buf(); b_Sb = Buf()
        sm_r = Ring(ov, "sm", [64, 256], F32, 6)
        smb_r = Ring(ov, "smb", [64, 256], BF16, 8)
        dd_r = Ring(ov, "dd", [64, 4], F32, 3)

        def gla_dir(l, d, seqs):
            incl = tri[:, d, :]
            strict = tri[:, 2 + d, :]
            for (tok0, ntok, seq_idx, lat) in seqs:
                if lat:
                    fw.dma(SP, S_f[:], st_gla[l, d], reads=[b_Sb], writes=[b_S])
                else:
                    fw.op(DVE, lambda h: h.memset(S_f[:], 0.0), reads=[b_Sb], writes=[b_S])
                fw.op(ACT, lambda h: h.activation(out=S_b[:], in_=S_f[:], func=AF.Copy), reads=[b_S], writes=[b_Sb])
                tiles = list(range(tok0, tok0 + ntok, TT)) if ntok >= TT else [tok0]
                if d == 1:
                    tiles = tiles[::-1]
                for tt0 in tiles:
                    base_tile = (tt0 // TT) * TT
                    cq, bcq = gl_q.get(); ck, bck = gl_k.get(); ckt, bckt = gl_kt.get(); cvt, bcvt = gl_vt.get(); cG, bcG = gl_G.get()
                    fw.dma(SP, cq[:], gq_s[:, :, base_tile:base_tile + TT], reads=[b_gq], writes=[bcq])
                    fw.dma(SP, ck[:], gk_s[:, :, base_tile:base_tile + TT], reads=[b_gk], writes=[bck])
                    fw.dma(SP, ckt[:], gkt_s[base_tile:base_tile + TT, :].rearrange("(c p) f -> p c f", p=64), reads=[b_gkt], writes=[bckt])
                    fw.dma(SP, cvt[:], gvt_s[base_tile:base_tile + TT, :].rearrange("(c p) f -> p c f", p=64), reads=[b_gvt], writes=[bcvt])
                    fw.dma(SP, cG[:], gG_s[base_tile:base_tile + TT, d * 256:(d + 1) * 256].rearrange("(c p) f -> p c f", p=64), reads=[b_gG], writes=[bcG])
                    ot, bot = gl_o.get()
                    c_lo = (tt0 - base_tile) // 64
                    nch = min(ntok, TT) // 64
                    chunks = list(range(c_lo, c_lo + nch))
                    if d == 1:
                        chunks = chunks[::-1]
                    for c in chunks:
                        Gc = cG[:, c, :]
                        psb_, bpsb = psA.get()
                        for hh in range(4):
                            fw.op(PE, lambda h, hh=hh, Gc=Gc, p=psb_: h.matmul(p[0:64, hh * 64:(hh + 1) * 64], lhsT=Gc[:, hh * 64:(hh + 1) * 64], rhs=incl, start=True, stop=True),
                                  reads=[bcG, b_tri], writes=[bpsb], inc=(hh == 3))
                        ep, bep = sm_r.get(); en, ben = sm_r.get()
                        fw.op(ACT, lambda h, p=psb_, ep=ep: h.activation(out=ep[:, :], in_=p[0:64, 0:256], func=AF.Exp, scale=-1.0 / 16), reads=[bpsb], writes=[bep])
                        fw.op(ACT, lambda h, p=psb_, en=en: h.activation(out=en[:, :], in_=p[0:64, 0:256], func=AF.Exp, scale=1.0 / 16), reads=[bpsb], writes=[ben])
                        qt, bqt = smb_r.get(); kt_, bkt = smb_r.get()
                        fw.op(DVE, lambda h, c=c, qt=qt, ep=ep: h.scalar_tensor_tensor(out=qt[:, :].rearrange("p (h t) -> p h t", t=64), in0=cq[:, :, c * 64:(c + 1) * 64], scalar=0.125,
                                                                                      in1=ep[:, :].rearrange("p (h t) -> p h t", t=64), op0=ALU.mult, op1=ALU.mult),
                              reads=[bcq, bep], writes=[bqt])
                        fw.op(DVE, lambda h, c=c, kt_=kt_, en=en: h.tensor_tensor(out=kt_[:, :].rearrange("p (h t) -> p h t", t=64), in0=ck[:, :, c * 64:(c + 1) * 64],
                                                                                  in1=en[:, :].rearrange("p (h t) -> p h t", t=64), op=ALU.mult),
                              reads=[bck, ben], writes=[bkt])
                        ps2, bps2 = psA.get()
                        fw.op(PE, lambda h, Gc=Gc, p=ps2: h.matmul(p[0:64, 0:256], lhsT=strict, rhs=Gc, start=True, stop=True), reads=[bcG, b_tri], writes=[bps2])
                        e2, be2 = sm_r.get()
                        fw.op(ACT, lambda h, p=ps2, e2=e2: h.activation(out=e2[:, :], in_=p[0:64, 0:256], func=AF.Exp, scale=-1.0 / 16), reads=[bps2], writes=[be2])
                        kh, bkh = smb_r.get()
                        fw.op(DVE, lambda h, c=c, kh=kh, e2=e2: h.tensor_tensor(out=kh[:, :], in0=ckt[:, c, :], in1=e2[:, :], op=ALU.mult), reads=[bckt, be2], writes=[bkh])
                        ps3, bps3 = psA.get()
                        for hh in range(4):
                            fw.op(PE, lambda h, hh=hh, Gc=Gc, p=ps3: h.matmul(p[0:64, hh:hh + 1], lhsT=Gc[:, hh * 64:(hh + 1) * 64], rhs=ones_col[:, :], start=True, stop=True),
                                  reads=[bcG, b_onescol], writes=[bps3], inc=(hh == 3))
                        dd, bdd = dd_r.get()
                        fw.op(ACT, lambda h, p=ps3, dd=dd: h.activation(out=dd[:, :], in_=p[0:64, 0:4], func=AF.Exp, scale=-1.0 / 16), reads=[bps3], writes=[bdd])
                        ps4, bps4 = psA.get()
                        for hh in range(4):
                            fw.op(PE, lambda h, hh=hh, p=ps4, kt_=kt_, qt=qt: h.matmul(p[0:64, hh * 64:(hh + 1) * 64], lhsT=kt_[:, hh * 64:(hh + 1) * 64], rhs=qt[:, hh * 64:(hh + 1) * 64],
                                                                                     start=True, stop=True), reads=[bkt, bqt], writes=[bps4], inc=(hh == 3))
                        am, bam = smb_r.get()
                        fw.op(DVE, lambda h, p=ps4, am=am: h.tensor_tensor(out=am[:, :].rearrange("p (h t) -> p h t", t=64), in0=p[0:64, 0:256].rearrange("p (h t) -> p h t", t=64),
                                                                         in1=incl.unsqueeze(1).broadcast_to([64, 4, 64]), op=ALU.mult), reads=[bps4, b_tri], writes=[bam])
                        ps5, bps5 = psA.get()
                        for hh in range(4):
                            fw.op(PE, lambda h, hh=hh, c=c, p=ps5, am=am: h.matmul(p[:, hh * 64:(hh + 1) * 64], lhsT=cvt[:, c, hh * 128:(hh + 1) * 128], rhs=am[:, hh * 64:(hh + 1) * 64],
                                                                                 start=True, stop=False), reads=[bcvt, bam], writes=[bps5], inc=False)
                            fw.op(PE, lambda h, hh=hh, p=ps5, qt=qt: h.matmul(p[:, hh * 64:(hh + 1) * 64], lhsT=S_b[:, hh, :], rhs=qt[:, hh * 64:(hh + 1) * 64], start=False, stop=True),
                                  reads=[b_Sb, bqt], writes=[bps5], inc=(hh == 3))
                        fw.op(ACT, lambda h, c=c, p=ps5, ot=ot: h.activation(out=ot[:, :, c * 64:(c + 1) * 64], in_=p[:, 0:256].rearrange("p (h t) -> p h t", t=64), func=AF.Copy),
                              reads=[bps5], writes=[bot])
                        ps6, bps6 = psB.get()
                        for hh in range(4):
                            fw.op(PE, lambda h, hh=hh, c=c, p=ps6, kh=kh: h.matmul(p[0:64, hh * 128:(hh + 1) * 128], lhsT=kh[:, hh * 64:(hh + 1) * 64], rhs=cvt[:, c, hh * 128:(hh + 1) * 128],
                                                                                 start=True, stop=True), reads=[bkh, bcvt], writes=[bps6], inc=(hh == 3))
                        fw.op(DVE, lambda h, dd=dd: h.tensor_tensor(out=S_f[:], in0=S_f[:], in1=dd[:, :].unsqueeze(2).broadcast_to([64, 4, 128]), op=ALU.mult), reads=[bdd], writes=[b_S])
                        fw.op(DVE, lambda h, p=ps6: h.tensor_tensor(out=S_f[:], in0=S_f[:], in1=p[0:64, :].rearrange("p (h v) -> p h v", v=128), op=ALU.add), reads=[bps6], writes=[b_S])
                        fw.op(ACT, lambda h: h.activation(out=S_b[:], in_=S_f[:], func=AF.Copy), reads=[b_S], writes=[b_Sb])
                    cols = slice(tt0 - base_tile, tt0 - base_tile + min(ntok, TT))
                    ncol = min(ntok, TT)
                    if d == 0:
                        fw.dma(SP, go_s[:, :, tt0:tt0 + ncol], ot[:, :, cols], reads=[bot], writes=[b_go])
                    else:
                        of, bof = gl_of.get()
                        fw.dma(SP, of[:, :, 0:ncol], go_s[:, :, tt0:tt0 + ncol], reads=[b_go], writes=[bof])
                        rr, brr = gl_r.get()
                        fw.dma(SP, rr[:, :, 0:ncol], rs_s[:, :, tt0:tt0 + ncol], reads=[b_rs], writes=[brr])
                        fw.op(DVE, lambda h, ot=ot, of=of: h.tensor_tensor(out=of[:, :, 0:ncol], in0=of[:, :, 0:ncol], in1=ot[:, :, cols], op=ALU.add), reads=[bot], writes=[bof])
                        sq, bsq = sq_r.get()
                        fw.op(ACT, lambda h, sq=sq, of=of: h.activation(out=sq[:, 0:4, 0:ncol], in_=of[:, :, 0:ncol], func=AF.Square), reads=[bof], writes=[bsq])
                        ya, bya = gl_r.get()
                        for hh in range(4):
                            ps, bps = psA.get()
                            fw.op(PE, lambda h, hh=hh, ps=ps, sq=sq: h.matmul(ps[:, 0:ncol], lhsT=cmat[:, C_128, :], rhs=sq[:, hh, 0:ncol], start=True, stop=True), reads=[bsq, b_cmat], writes=[bps])
                            rs, brs = rs_r.get()
                            rsqrt_eps(ps[:, 0:ncol], rs[:, 0:ncol], bps, brs)
                            tmp, btmp = f32_r.get()
                            fw.op(DVE, lambda h, hh=hh, tmp=tmp, of=of, rs=rs: h.scalar_tensor_tensor(out=tmp[:, 0:ncol], in0=of[:, hh, 0:ncol], scalar=gc[:, 9:10], in1=rs[:, 0:ncol],
                                                                                                     op0=ALU.mult, op1=ALU.mult), reads=[bof, brs, b_gc], writes=[btmp])
                            fw.op(DVE, lambda h, hh=hh, tmp=tmp, ya=ya, rr=rr: h.tensor_tensor(out=ya[:, hh, 0:ncol], in0=tmp[:, 0:ncol], in1=rr[:, hh, 0:ncol], op=ALU.mult),
                                  reads=[btmp, brr], writes=[bya])
                        fw.dma(SP, ya_s[:, :, tt0:tt0 + ncol].rearrange("h p t -> p h t"), ya[:, :, 0:ncol], reads=[bya], writes=[b_ya])
                if seq_idx is not None:
                    ob = Buf(); out_bufs.append(ob)
                    fw.dma(SP, o_gla[l, seq_idx, d], S_f[:], reads=[b_S], writes=[ob])

        rs_s = dram_tmp("rs_s", [128, 4, T], BF16)
        b_rs = Buf()

        def phase1b(l, ti):
            cond = 1 if ti > 0 else 0
            xt, bxt = load_xT(ti)
            hT, bh = norm_mod(xt, bxt, A1, 0, cond)
            wt, bw = load_w8(w_in[l, :, O_AR:O_AR + 512], 512)
            rr, brr = gl_r.get()
            for hh in range(4):
                ps, bps = proj_fm(wt, bw, hh * 128, 128, lambda k: hT[:, k, :], [bh], 8)
                fw.op(ACT, lambda h, hh=hh, ps=ps: h.activation(out=rr[:, hh, :], in_=ps[:], func=AF.Silu), reads=[bps], writes=[brr])
            fw.dma(SP, rs_s[:, :, ti * TT:(ti + 1) * TT], rr[:], reads=[brr], writes=[b_rs])

        arena.reset()
        at_k = Ring(ov, "atk", [128, TKS], BF16, 2)
        at_v = Ring(ov, "atv", [128, TKS // 128, 128], BF16, 2)
        at_q = Ring(ov, "atq", [128, TT], BF16, 2)
        at_p = Ring(ov, "atp", [128, TT], BF16, 6)
        at_o = Ring(ov, "ato", [128, TT], BF16, 2)
        at_qm = [Ring(ov, "atqm0_", [128, TT], BF16, 2), Ring(ov, "atqm1_", [128, TT], BF16, 2)]
        at_l = Ring(ov, "atl", [128, TT], F32, 4)

        def mla_attn(l, groups):
            sc = 96 ** -0.5
            for i in range(2):
                fw.op(DVE, lambda h: h.memset(at_k.t[i][:], 0.0), writes=[at_k.b[i]])
                fw.op(DVE, lambda h: h.memset(at_q.t[i][:], 0.0), writes=[at_q.b[i]])
                for c2 in range(2):
                    fw.op(DVE, lambda h: h.memset(at_qm[c2].t[i][:], 0.0), writes=[at_qm[c2].b[i]])
            its = [(gi, head, qq) for gi, (q0, nq, k0, nk) in enumerate(groups) for head in range(8) for qq in range(q0, q0 + nq, TT)]
            kv_cache, q_cache = {}, {}

            def get_kv(gi, head):
                if (gi, head) not in kv_cache:
                    q0, nq, k0, nk = groups[gi]
                    kt, bkt = at_k.get(); vt, bvt = at_v.get()
                    fw.dma(SP, kt[0:96, 0:nk], k_s[head, :, k0:k0 + nk], reads=[b_k], writes=[bkt])
                    fw.dma(SP, vt[:, 0:nk // 128, :], v_s[head, :, k0 // 128:k0 // 128 + nk // 128, :], reads=[b_v], writes=[bvt])
                    kv_cache[(gi, head)] = (kt, bkt, vt, bvt)
                return kv_cache[(gi, head)]

            def get_q(idx):
                if idx not in q_cache:
                    gi, head, qq = its[idx]
                    nqt = min(TT, groups[gi][1])
                    qt, bqt = at_q.get()
                    fw.dma(SP, qt[0:96, 0:nqt], q_s[head, :, qq:qq + nqt], reads=[b_q], writes=[bqt])
                    q_cache[idx] = (qt, bqt)
                return q_cache[idx]

            for idx, (gi, head, qq) in enumerate(its):
                q0, nq, k0, nk = groups[gi]
                nkc = nk // 128
                if True:
                    kt, bkt, vt, bvt = get_kv(gi, head)
                    if True:
                        nqt = min(TT, nq)
                        qt, bqt = get_q(idx)
                        if idx + 1 < len(its):
                            get_kv(its[idx + 1][0], its[idx + 1][1])
                            get_q(idx + 1)
                        po, bpo = psB.get()

                        def qk(c, kt=kt, qt=qt, bkt=bkt, bqt=bqt, nqt=nqt):
                            ps, bps = psA.get()
                            fw.op(PE, lambda h: h.matmul(ps[:, 0:nqt], lhsT=kt[:, c * 128:(c + 1) * 128], rhs=qt[:, 0:nqt], start=True, stop=True),
                                  reads=[bkt, bqt], writes=[bps])
                            return ps, bps

                        LA = 2
                        pend = [qk(i) for i in range(min(LA, nkc))]
                        for c in range(nkc):
                            ps, bps = pend.pop(0)
                            if c + LA < nkc:
                                pend.append(qk(c + LA))
                            pt, bpt = at_p.get()
                            fw.op(ACT, lambda h, ps=ps, pt=pt: h.activation(out=pt[:, 0:nqt], in_=ps[:, 0:nqt], func=AF.Exp, scale=sc), reads=[bps], writes=[bpt])
                            fw.op(PE, lambda h, c=c, po=po, vt=vt, pt=pt: h.matmul(po[:, 0:nqt], lhsT=vt[:, c, :], rhs=pt[:, 0:nqt], start=(c == 0), stop=(c == nkc - 1)),
                                  reads=[bvt, bpt], writes=[bpo], inc=(c == nkc - 1))
                        rc, brc = f32_r.get()
                        fw.op(DVE, lambda h, po=po, rc=rc: h.reciprocal(out=rc[64:128, 0:nqt], in_=po[64:128, 0:nqt]), reads=[bpo], writes=[brc])
                        r2, br2 = f32_r.get()
                        fw.op(DVE, lambda h, rc=rc, r2=r2: h.tensor_copy(out=r2[0:64, 0:nqt], in_=rc[64:128, 0:nqt]), reads=[brc], writes=[br2])
                        ob, bob = at_o.get()
                        fw.op(DVE, lambda h, po=po, r2=r2, ob=ob: h.tensor_tensor(out=ob[0:64, 0:nqt], in0=po[0:64, 0:nqt], in1=r2[0:64, 0:nqt], op=ALU.mult), reads=[bpo, br2], writes=[bob])
                        fw.dma(SP, yb_s[head // 2, (head % 2) * 64:(head % 2) * 64 + 64, qq:qq + nqt], ob[0:64, 0:nqt], reads=[bob], writes=[b_yb])

        def diff_attn(l, groups, lam_init):
            sc = 64 ** -0.5
            its = [(gi, head, qq) for gi, (q0, nq, k0, nk) in enumerate(groups) for head in range(4) for qq in range(q0, q0 + nq, TT)]
            kv_cache, q_cache = {}, {}

            def get_kv(gi, head):
                if (gi, head) not in kv_cache:
                    q0, nq, k0, nk = groups[gi]
                    kt, bkt = at_k.get(); vt, bvt = at_v.get()
                    fw.dma(SP, kt[:, 0:nk], dk_s[head, :, k0:k0 + nk], reads=[b_dk], writes=[bkt])
                    fw.dma(SP, vt[:, 0:nk // 128, :], dv_s[head, :, k0 // 128:k0 // 128 + nk // 128, :], reads=[b_dv], writes=[bvt])
                    kv_cache[(gi, head)] = (kt, bkt, vt, bvt)
                return kv_cache[(gi, head)]

            def get_q(idx):
                if idx not in q_cache:
                    gi, head, qq = its[idx]
                    nqt = min(TT, groups[gi][1])
                    qm = [at_qm[0].get(), at_qm[1].get()]
                    fw.dma(SP, qm[0][0][0:64, 0:nqt], dq_s[head, 0:64, qq:qq + nqt], reads=[b_dq], writes=[qm[0][1]])
                    fw.dma(SP, qm[1][0][64:128, 0:nqt], dq_s[head, 64:128, qq:qq + nqt], reads=[b_dq], writes=[qm[1][1]])
                    q_cache[idx] = qm
                return q_cache[idx]

            for idx, (gi, head, qq) in enumerate(its):
                q0, nq, k0, nk = groups[gi]
                nkc = nk // 128
                if True:
                    kt, bkt, vt, bvt = get_kv(gi, head)
                    if True:
                        nqt = min(TT, nq)
                        qm = get_q(idx)
                        if idx + 1 < len(its):
                            get_kv(its[idx + 1][0], its[idx + 1][1])
                            get_q(idx + 1)
                        acc = [psB.get() for _ in range(3)]
                        lac = [at_l.get()]

                        def qk(i, kt=kt, bkt=bkt, nqt=nqt, qm=qm):
                            c, comp = divmod(i, 2)
                            ps, bps = psA.get()
                            fw.op(PE, lambda h: h.matmul(ps[:, 0:nqt], lhsT=kt[:, c * 128:(c + 1) * 128], rhs=qm[comp][0][:, 0:nqt], start=True, stop=True),
                                  reads=[bkt, qm[comp][1]], writes=[bps])
                            return ps, bps

                        LA = 2
                        pend = [qk(i) for i in range(min(LA, 2 * nkc))]
                        for c in range(nkc):
                            for comp in range(2):
                                ps, bps = pend.pop(0)
                                if c * 2 + comp + LA < 2 * nkc:
                                    pend.append(qk(c * 2 + comp + LA))
                                pt, bpt = at_p.get()
                                fw.op(ACT, lambda h: h.activation(out=pt[:, 0:nqt], in_=ps[:, 0:nqt], func=AF.Exp, scale=sc), reads=[bps], writes=[bpt])
                                po, bpo = acc[comp]
                                fw.op(PE, lambda h: h.matmul(po[:, 0:nqt], lhsT=vt[:, c, :], rhs=pt[:, 0:nqt], start=(c == 0), stop=(c == nkc - 1)),
                                      reads=[bvt, bpt], writes=[bpo], inc=(c == nkc - 1))
                                if comp == 0:
                                    la, bla = lac[0]
                                    if c == 0:
                                        fw.op(DVE, lambda h: h.tensor_copy(out=la[:, 0:nqt], in_=pt[:, 0:nqt]), reads=[bpt], writes=[bla])
                                    else:
                                        fw.op(DVE, lambda h: h.tensor_tensor(out=la[:, 0:nqt], in0=la[:, 0:nqt], in1=pt[:, 0:nqt], op=ALU.add), reads=[bpt], writes=[bla])
                                else:
                                    pl1, bpl1 = acc[2]
                                    fw.op(PE, lambda h: h.matmul(pl1[:, 0:nqt], lhsT=cmat[:, C_ONE, :], rhs=pt[:, 0:nqt], start=(c == 0), stop=(c == nkc - 1)),
                                          reads=[b_cmat, bpt], writes=[bpl1], inc=(c == nkc - 1))
                        rr_ = []
                        pl, bpl = psA.get()
                        fw.op(PE, lambda h: h.matmul(pl[:, 0:nqt], lhsT=onesf[:, :], rhs=lac[0][0][:, 0:nqt], start=True, stop=True), reads=[b_onesf, lac[0][1]], writes=[bpl])
                        for (pl_, bpl_) in ((pl, bpl), acc[2]):
                            rc_, brc_ = f32_r.get()
                            fw.op(DVE, lambda h: h.reciprocal(out=rc_[:, 0:nqt], in_=pl_[:, 0:nqt]), reads=[bpl_], writes=[brc_])
                            rr_.append((rc_, brc_))
                        (r0, br0), (r1, br1) = rr_
                        o0, bo0 = f32_r.get(); o1, bo1 = f32_r.get()
                        fw.op(DVE, lambda h, o0=o0, r0=r0, p=acc[0][0]: h.tensor_tensor(out=o0[:, 0:nqt], in0=p[:, 0:nqt], in1=r0[:, 0:nqt], op=ALU.mult), reads=[acc[0][1], br0], writes=[bo0])
                        fw.op(DVE, lambda h, o1=o1, r1=r1, p=acc[1][0]: h.tensor_tensor(out=o1[:, 0:nqt], in0=p[:, 0:nqt], in1=r1[:, 0:nqt], op=ALU.mult), reads=[acc[1][1], br1], writes=[bo1])
                        fw.op(DVE, lambda h, o0=o0, o1=o1: h.scalar_tensor_tensor(out=o0[:, 0:nqt], in0=o1[:, 0:nqt], scalar=lamc[:, 0:1], in1=o0[:, 0:nqt], op0=ALU.mult, op1=ALU.add),
                              reads=[bo1, b_lamc], writes=[bo0])
                        sq, bsq = bf_r.get()
                        fw.op(ACT, lambda h, sq=sq, o0=o0: h.activation(out=sq[:, 0:nqt], in_=o0[:, 0:nqt], func=AF.Square), reads=[bo0], writes=[bsq])
                        ps, bps = psA.get()
                        fw.op(PE, lambda h, ps=ps, sq=sq: h.matmul(ps[:, 0:nqt], lhsT=cmat[:, C_128, :], rhs=sq[:, 0:nqt], start=True, stop=True), reads=[bsq, b_cmat], writes=[bps])
                        rs, brs = rs_r.get()
                        rsqrt_eps(ps[:, 0:nqt], rs[:, 0:nqt], bps, brs)
                        t1, bt1 = f32_r.get()
                        fw.op(DVE, lambda h, t1=t1, o0=o0, rs=rs: h.scalar_tensor_tensor(out=t1[:, 0:nqt], in0=o0[:, 0:nqt], scalar=gc[:, 10:11], in1=rs[:, 0:nqt], op0=ALU.mult, op1=ALU.mult),
                              reads=[bo0, brs, b_gc], writes=[bt1])
                        ob, bob = at_o.get()
                        fw.op(ACT, lambda h, t1=t1, ob=ob: h.activation(out=ob[:, 0:nqt], in_=t1[:, 0:nqt], func=AF.Copy, scale=(1.0 - lam_init)), reads=[bt1], writes=[bob])
                        fw.dma(SP, yc_s[head, :, qq:qq + nqt], ob[:, 0:nqt], reads=[bob], writes=[b_yc])

        arena.reset()
        y3_r = Ring(ov, "y3", [128, 12, TT], BF16, 1)
        mg_r = Ring(ov, "mg", [128, 8, TT], BF16, 1)
        macc_r = Ring(ov, "macc", [128, 4, TT], F32, 1)
        aTraw = ov("aTraw", [128, 8192], F32)
        aT_view = aTraw.bitcast(BF16).rearrange("p (a b) -> p a b", b=TT)
        xo_view = aTraw[:, 0:4096].rearrange("p (g f) -> p g f", f=D)
        b_aTraw = Buf()

        def phase3(l, ti, last):
            cond = 1 if ti > 0 else 0
            t0 = ti * TT
            xt, bxt = load_xT(ti)
            hT, bh = norm_mod(xt, bxt, A1, 0, cond)
            y3, by3 = y3_r.get()
            for bi, (src, bsrc) in enumerate(((ya_s, b_ya), (yb_s, b_yb), (yc_s, b_yc))):
                fw.dma(SP, y3[:, bi * 4:(bi + 1) * 4, :], src[:, :, t0:t0 + TT].rearrange("h p t -> p h t"), reads=[bsrc], writes=[by3])
            mg, bmg = mg_r.get()
            macc, bmacc = macc_r.get()
            for half in range(2):
                for bi in range(3):
                    c0 = O_G + bi * 1024 + half * 512
                    wg, bwg = load_w8(w_in[l, :, c0:c0 + 512], 512)
                    wo, bwo = w4.get()
                    fw.dma(POOL, wo[:, :, 0:512], w_o3[l, bi, :, half * 512:(half + 1) * 512].rearrange("(k p) c -> p k c", p=128), writes=[bwo])
                    for j in range(4):
                        oc = half * 4 + j
                        psg, bpsg = proj_fm(wg, bwg, j * 128, 128, lambda k: hT[:, k, :], [bh], 8)
                        sg, bsg = f32_r.get()
                        fw.op(ACT, lambda h: h.activation(out=sg[:], in_=psg[:], func=AF.Sigmoid), reads=[bpsg], writes=[bsg])
                        pso, bpso = proj_fm(wo, bwo, j * 128, 128, lambda k: y3[:, bi * 4 + k, :], [by3], 4)
                        if bi == 0:
                            fw.op(DVE, lambda h: h.tensor_tensor(out=macc[:, j, :], in0=sg[:], in1=pso[:], op=ALU.mult), reads=[bsg, bpso], writes=[bmacc])
                        else:
                            tmp, btmp = f32_r.get()
                            fw.op(DVE, lambda h: h.tensor_tensor(out=tmp[:], in0=sg[:], in1=pso[:], op=ALU.mult), reads=[bsg, bpso], writes=[btmp])
                            if bi == 1:
                                fw.op(DVE, lambda h: h.tensor_tensor(out=macc[:, j, :], in0=macc[:, j, :], in1=tmp[:], op=ALU.add), reads=[btmp], writes=[bmacc])
                            else:
                                fw.op(DVE, lambda h: h.tensor_tensor(out=mg[:, oc, :], in0=macc[:, j, :], in1=tmp[:], op=ALU.add), reads=[btmp, bmacc], writes=[bmg])
            for half in range(2):
                wt, bw = load_w8(w_out[l, :, half * 512:(half + 1) * 512], 512)
                for j in range(4):
                    oc = half * 4 + j
                    ps, bps = proj_fm(wt, bw, j * 128, 128, lambda k: mg[:, k, :], [bmg], 8)
                    fw.op(DVE, lambda h, oc=oc, ps=ps: h.scalar_tensor_tensor(out=xt[:, oc, :], in0=ps[:], scalar=modT[:, 16 + oc, cond:cond + 1], in1=xt[:, oc, :], op0=ALU.mult, op1=ALU.add),
                          reads=[bps, b_modT_b], writes=[bxt])
            h2, bh2 = norm_mod(xt, bxt, A2, 24, cond)
            aT, baT = aT_view, b_aTraw
            for fb in range(8):
                wt, bw = load_w8(w_m1[l, :, fb * 512:(fb + 1) * 512], 512)
                for j in range(4):
                    ps, bps = proj_fm(wt, bw, j * 128, 128, lambda k: h2[:, k, :], [bh2], 8)
                    rl, brl = f32_r.get()
                    fw.op(ACT, lambda h, ps=ps, rl=rl: h.activation(out=rl[:], in_=ps[:], func=AF.Relu), reads=[bps], writes=[brl])
                    fw.op(DVE, lambda h, rl=rl, fb=fb, j=j: h.tensor_tensor(out=aT[:, fb * 4 + j, :], in0=rl[:], in1=rl[:], op=ALU.mult), reads=[brl], writes=[baT])
            for half in range(2):
                accs = [psB.get() for _ in range(4)]
                for kb8 in range(4):
                    wt, bw = load_w8(w_m2[l, kb8 * 1024:(kb8 + 1) * 1024, half * 512:(half + 1) * 512], 512)
                    for j in range(4):
                        ps, bps = accs[j]
                        for k in range(8):
                            kk = kb8 * 8 + k
                            fw.op(PE, lambda h: h.matmul(ps[:], lhsT=wt[:, k, j * 128:(j + 1) * 128], rhs=aT[:, kk, :], start=(kk == 0), stop=(kk == 31)),
                                  reads=[bw, baT], writes=[bps], inc=(k == 7))
                for j in range(4):
                    oc = half * 4 + j
                    ps, bps = accs[j]
                    fw.op(DVE, lambda h: h.scalar_tensor_tensor(out=xt[:, oc, :], in0=ps[:], scalar=modT[:, 40 + oc, cond:cond + 1], in1=xt[:, oc, :], op0=ALU.mult, op1=ALU.add),
                          reads=[bps, b_modT_b], writes=[bxt])
            if not last:
                fw.dma(SP, xT_s[:, :, t0:t0 + TT].rearrange("k p t -> p k t"), xt[:], reads=[bxt], writes=[b_xT[ti]])
            else:
                xo, bxo = xo_view, b_aTraw
                for g in range(4):
                    for half in range(2):
                        ps, bps = psA.get()
                        for k in range(4):
                            kk = half * 4 + k
                            fw.op(PE, lambda h, g=g, k=k, kk=kk, ps=ps: h.transpose(ps[:, k * 128:(k + 1) * 128], xt[:, kk, g * 128:(g + 1) * 128], ident[:]),
                                  reads=[bxt, b_ident], writes=[bps], inc=(k == 3))
                        evac(ps[:], xo[:, g, half * 512:(half + 1) * 512], bps, bxo, eng=(ACT if half else DVE))
                ob = Buf(); out_bufs.append(ob)
                fw.dma(SP, y_out[t0:t0 + TT, :].rearrange("(g p) f -> p g f", p=128), xo[:], reads=[bxo], writes=[ob])

        import os
        KSTOP = int(os.environ.get("KSTOP", "99"))
        step = [0]

        def reached():
            step[0] += 1
            return step[0] > KSTOP

        for l in range(L):
            if reached():
                break
            lam_init = layer_setup(l)
            fw.barrier()
            vaug_init[0] = vaug_init[1] = False
            if reached():
                break
            prep_cache(l)
            if reached():
                break
            for ti in range(NT):
                phase1(l, ti)
            fw.barrier()
            if reached():
                break
            seqs = [(0, 256, 0, False), (256, 256, 1, False), (NP_TOK, NS_TOK, None, True)]
            gla_dir(l, 0, seqs)
            fw.barrier()
            gla_dir(l, 1, seqs)
            fw.barrier()
            if reached():
                break
            groups = [(0, 256, 0, 256), (256, 256, 256, 256), (NP_TOK, NS_TOK, NP_TOK, TKS)]
            mla_attn(l, groups)
            if reached():
                break
            diff_attn(l, groups, lam_init)
            if reached():
                break
            fw.barrier()
            for ti in range(NT):
                phase3(l, ti, l == L - 1)
            fw.barrier()

        fw.finish(out_bufs)
    return nc


def _rope_tables():
    t = np.arange(NS_TOK)
    row, col = t // 64, t % 64

    def cs(nf, pos):
        inv = (10000.0 ** (-np.arange(nf, dtype=np.float32) / nf)).astype(np.float32)
        ang = pos.astype(np.float32)[None, :] * inv[:, None]
        c, s = np.cos(ang).astype(np.float32), np.sin(ang).astype(np.float32)
        return np.concatenate([c, c], 0), np.concatenate([s, s], 0)

    cr, sr = cs(8, row); cc, sc_ = cs(8, col)
    cM = np.concatenate([cr, cc, np.ones((64, NS_TOK), np.float32)], 0)
    sM = np.concatenate([sr, sc_, np.zeros((64, NS_TOK), np.float32)], 0)
    cr, sr = cs(16, row); cc, sc_ = cs(16, col)
    c64 = np.concatenate([cr, cc], 0); s64 = np.concatenate([sr, sc_], 0)
    cD = np.concatenate([c64, c64], 0); sD = np.concatenate([s64, s64], 0)
    return np.stack([cM, sM]).astype(np.float32), np.stack([cD, sD]).astype(np.float32)


def _rot_matrix(nf):
    R = np.zeros((2 * nf, 2 * nf), np.float32)
    for i in range(nf):
        R[i, nf + i] = -1.0
        R[nf + i, i] = 1.0
    return R


def _consts():
    bf = ml_dtypes.bfloat16
    cmat = np.zeros((128, 8, 128), np.float32)
    cmat[:, 0, :] = 1.0 / 1024
    cmat[:, 1, :] = 1.0 / 384
    cmat[:, 2, :] = 1.0 / 256
    cmat[:96, 3, :96] = 1.0 / 96
    cmat[:64, 4, :64] = 1.0 / 64
    cmat[64:, 4, 64:] = 1.0 / 64
    cmat[:, 5, :] = 1.0 / 128
    cmat[:, 6, :] = 1.0
    cmat[:32, 7, :32] = np.eye(32)
    pm = np.zeros((128, 3, 128), np.float32)
    R96 = np.zeros((96, 96), np.float32)
    R96[0:16, 0:16] = _rot_matrix(8); R96[16:32, 16:32] = _rot_matrix(8)
    pm[:96, 0, :96] = R96.T
    R64 = np.zeros((64, 64), np.float32)
    R64[0:32, 0:32] = _rot_matrix(16); R64[32:64, 32:64] = _rot_matrix(16)
    R128 = np.zeros((128, 128), np.float32)
    R128[:64, :64] = R64; R128[64:, 64:] = R64
    pm[:, 1, :] = R128.T
    i = np.arange(64)
    tri = np.zeros((64, 4, 64), np.float32)
    tri[:, 0, :] = (i[:, None] <= i[None, :])
    tri[:, 1, :] = (i[:, None] >= i[None, :])
    tri[:, 2, :] = (i[:, None] > i[None, :])
    tri[:, 3, :] = (i[:, None] < i[None, :])
    return cmat.astype(bf), pm.astype(bf), tri


_PROG = {}


def _get_prog():
    if "nc" not in _PROG:
        _PROG["nc"] = build_program()
    return _PROG["nc"]


def make_in_maps(x_prompt, x_sample, state_gla, cache_mla_ckv, cache_mla_krope, cache_diff_k, cache_diff_v, c, c_ctx, w_mod, b_mod, g_norm1, g_norm2, w_in,
                 w_gla_a2, b_gla_a, g_gla_out, g_mla_qa, g_mla_kva, w_mla_uq, w_mla_uk, w_mla_uv, g_mla_q, g_mla_k, g_diff_q, g_diff_k, lam_qk, g_diff_sub,
                 w_o_gla, w_o_mla, w_o_diff, w_out, w_mlp1, w_mlp2):
    f = lambda a: np.ascontiguousarray(np.asarray(a, dtype=np.float32))
    cmat, pm, tri = _consts()
    ropeM, ropeD = _rope_tables()
    perm = np.concatenate([np.arange(64, 96), np.arange(0, 64)])
    w_uq_p = f(w_mla_uq).reshape(L, 384, 8, 96)[:, :, :, perm]
    w_ukp = np.zeros((L, 256, 8, 96), np.float32)
    w_ukp[:, :, :, 32:] = f(w_mla_uk).reshape(L, 256, 8, 64)
    gcol = np.zeros((L, 128, 12), np.float32)
    gcol[:, :, 0:3] = f(g_mla_qa).reshape(L, 3, 128).transpose(0, 2, 1)
    gcol[:, :, 3:5] = f(g_mla_kva).reshape(L, 2, 128).transpose(0, 2, 1)
    gcol[:, :96, 5] = f(g_mla_q)[:, perm]
    gcol[:, :96, 6] = f(g_mla_k)[:, perm]
    gcol[:, :, 7] = np.tile(f(g_diff_q), (1, 2))
    gcol[:, :, 8] = np.tile(f(g_diff_k), (1, 2))
    gcol[:, :, 9] = f(g_gla_out)
    gcol[:, :, 10] = f(g_diff_sub)
    w_a2bd = np.zeros((L, 32, 512), np.float32)
    w_a2bd[:, 0:16, 0:256] = f(w_gla_a2)[:, 0]
    w_a2bd[:, 16:32, 256:512] = f(w_gla_a2)[:, 1]
    b_a = f(b_gla_a).reshape(L, 1, 512)
    gnT = np.concatenate([f(g_norm1).reshape(L, 8, 128).transpose(0, 2, 1), f(g_norm2).reshape(L, 8, 128).transpose(0, 2, 1)], axis=2)
    b_modT = f(b_mod).reshape(L, 48, 128).transpose(0, 2, 1)
    w_o3 = np.stack([f(w_o_gla), f(w_o_mla), f(w_o_diff)], axis=1)
    shared = dict(w_mod=f(w_mod), b_modT=f(b_modT), gnT=f(gnT), w_in=f(w_in), w_a2bd=w_a2bd, b_a=b_a, gcol=gcol, w_uq=f(w_uq_p), w_ukp=w_ukp,
                  w_uv=f(w_mla_uv), lam_qk=f(lam_qk).reshape(L, 1, 256), w_o3=f(w_o3), w_out=f(w_out), w_m1=f(w_mlp1), w_m2=f(w_mlp2),
                  ident=np.eye(128, dtype=np.float32), cmat=cmat, pmat=pm, tri=tri, ropeM=ropeM, ropeD=ropeD)
    xp, xs = f(x_prompt), f(x_sample)
    in_maps = []
    for core in range(8):
        b = core % 4
        m = dict(shared)
        m["xin"] = np.concatenate([xp[2 * core], xp[2 * core + 1], xs[b]], axis=0)
        cond = np.stack([f(c_ctx), f(c)[b]], axis=0)
        m["condT"] = f(cond.reshape(2, 8, 128).transpose(2, 1, 0))
        m["st_gla"] = f(f(state_gla)[b].transpose(0, 1, 3, 2, 4))
        m["c_ckv"] = f(cache_mla_ckv)[b]
        m["c_kr"] = f(cache_mla_krope)[b]
        m["c_dk"] = f(cache_diff_k)[b].reshape(L, PAST, 512)
        m["c_dv"] = f(cache_diff_v)[b].reshape(L, PAST, 512)
        in_maps.append(m)
    return in_maps


def kernel(**inputs):
    nc = _get_prog()
    in_maps = make_in_maps(**inputs)
    res = run_bass_kernel_spmd(nc, in_maps, core_ids=list(range(8)))
    R = res.results
    y_prompt = np.zeros((16, 256, D), np.float32)
    y_sample = np.zeros((4, NS_TOK, D), np.float32)
    n_gla = np.zeros((16, L, 2, 4, 64, 128), np.float32)
    n_ckv = np.zeros((16, L, 256, 256), np.float32)
    n_kr = np.zeros((16, L, 256, 32), np.float32)
    n_dk = np.zeros((16, L, 256, 4, 2, 64), np.float32)
    n_dv = np.zeros((16, L, 256, 4, 128), np.float32)
    for core in range(8):
        r = R[core]
        for s in range(2):
            bi = 2 * core + s
            y_prompt[bi] = r["y_out"][s * 256:(s + 1) * 256]
            n_gla[bi] = r["o_gla"][:, s].transpose(0, 1, 3, 2, 4)
            n_ckv[bi] = r["o_ckv"][:, s * 256:(s + 1) * 256]
            n_kr[bi] = r["o_kr"][:, s * 256:(s + 1) * 256]
            n_dk[bi] = r["o_dk"][:, s * 256:(s + 1) * 256].reshape(L, 256, 4, 2, 64)
            n_dv[bi] = r["o_dv"][:, s * 256:(s + 1) * 256].reshape(L, 256, 4, 128)
        if core < 4:
            y_sample[core] = r["y_out"][NP_TOK:]
    return (y_prompt, y_sample, n_gla, n_ckv, n_kr, n_dk, n_dv)
```

```python
import math
import numpy as np
import ml_dtypes
from contextlib import ExitStack
import concourse.bass as bass
import concourse.mybir as mybir
from concourse.bass_utils import run_bass_kernel_spmd

F32 = mybir.dt.float32
BF16 = mybir.dt.bfloat16
AF = mybir.ActivationFunctionType
ALU = mybir.AluOpType

D = 1024
L = 2
NP_TOK = 512
NS_TOK = 4096
T = NP_TOK + NS_TOK
TT = 512
NT = T // TT
PAST = 256
TKS = PAST + NS_TOK
EPS = 1e-6
O_AQ, O_AK, O_AV, O_AR, O_AA, O_QD, O_KVD, O_KR, O_DQ, O_DK, O_DV, O_G = 0, 256, 512, 1024, 1536, 1568, 1952, 2208, 2240, 2752, 3264, 3776
IN_COLS = 6848


class Buf:
    __slots__ = ("lw", "rd", "psum")

    def __init__(self, psum=False):
        self.lw = None
        self.rd = []
        self.psum = psum


class _Rec:
    def __init__(self):
        self.call = None

    def __getattr__(self, name):
        def f(*a, **k):
            self.call = (name, a, k)
            return self
        return f


class Eng:
    def __init__(self, fw, name, handle):
        self.name = name
        self.h = handle
        self.sem = fw.new_sem("e_" + name)
        self.count = 0
        self.waited = {}
        self.thunks = []
        self.snaps = {}


class FW:
    def __init__(self, nc, ctx, n_dma_sems=48):
        self.nc = nc
        self.ctx = ctx
        self.pe = Eng(self, "pe", nc.tensor)
        self.dve = Eng(self, "dve", nc.vector)
        self.act = Eng(self, "act", nc.scalar)
        self.pool = Eng(self, "pool", nc.gpsimd)
        self.sp = Eng(self, "sp", nc.sync)
        self.dma_sems = [self.new_sem(f"d{i}") for i in range(n_dma_sems)]
        self.dma_cnt = [0] * n_dma_sems
        self.snap_of = {}
        self.dma_rr = 0
        self.ew_rr = 0

    def new_sem(self, name):
        return self.ctx.enter_context(self.nc.semaphore(name))

    def _wait(self, E, ev):
        if ev is None:
            return
        sem, val = ev
        if sem is E.sem and E.name == "pe":
            return
        key = id(sem)
        if E.waited.get(key, 0) >= val:
            return
        E.waited[key] = val
        E.thunks.append(lambda h=E.h, s=sem, v=val: h.wait_ge(s, v))
        snap = self.snap_of.get((key, val))
        if snap:
            w = E.waited
            for k2, v2 in snap.items():
                if w.get(k2, 0) < v2:
                    w[k2] = v2

    def _deps(self, E, reads, writes):
        for b in reads:
            self._wait(E, b.lw)
        for b in writes:
            self._wait(E, b.lw)
            for ev in b.rd:
                self._wait(E, ev)

    def _post(self, ev, reads, writes):
        for b in reads:
            b.rd.append(ev)
            if len(b.rd) > 64:
                b.rd = b.rd[-48:]
        for b in writes:
            b.lw = ev
            b.rd = []

    def op(self, E, fn, reads=(), writes=(), inc=True):
        if any(b.psum for b in reads):
            writes = list(writes) + [b for b in reads if b.psum]
            reads = [b for b in reads if not b.psum]
        self._deps(E, reads, writes)
        rec = _Rec()
        fn(rec)
        mname, margs, mkw = rec.call
        if inc:
            E.count += 1
            ev = (E.sem, E.count)
            self.snap_of[(id(E.sem), E.count)] = dict(E.waited)
            E.thunks.append(lambda h=E.h, n=mname, a=margs, k=mkw, s=E.sem: getattr(h, n)(*a, **k).then_inc(s, 1))
        else:
            ev = (E.sem, E.count + 1)
            E.thunks.append(lambda h=E.h, n=mname, a=margs, k=mkw: getattr(h, n)(*a, **k))
        self._post(ev, reads, writes)
        return ev

    def dma(self, Q, out_ap, in_ap, reads=(), writes=(), **kw):
        self._deps(Q, reads, writes)
        i = self.dma_rr
        self.dma_rr = (self.dma_rr + 1) % len(self.dma_sems)
        self.dma_cnt[i] += 16
        sem = self.dma_sems[i]
        ev = (sem, self.dma_cnt[i])
        self.snap_of[(id(sem), self.dma_cnt[i])] = dict(Q.waited)
        Q.thunks.append(lambda h=Q.h, o=out_ap, a=in_ap, s=sem, k=kw: h.dma_start(out=o, in_=a, **k).then_inc(s, 16))
        self._post(ev, reads, writes)
        return ev

    def barrier(self):
        engs = [self.pe, self.dve, self.act, self.pool, self.sp]
        for E in engs:
            for E2 in engs:
                if E2 is not E and E2.count > 0:
                    self._wait(E, (E2.sem, E2.count))
            for i, sem in enumerate(self.dma_sems):
                if self.dma_cnt[i] > 0:
                    self._wait(E, (sem, self.dma_cnt[i]))

    def finish(self, final_bufs):
        for b in final_bufs:
            self._wait(self.sp, b.lw)
        nc = self.nc
        engs = self
        with nc.Block() as block:
            @block.tensor
            def _(e):
                for t in engs.pe.thunks:
                    t()

            @block.vector
            def _(e):
                for t in engs.dve.thunks:
                    t()

            @block.scalar
            def _(e):
                for t in engs.act.thunks:
                    t()

            @block.gpsimd
            def _(e):
                for t in engs.pool.thunks:
                    t()

            @block.sync
            def _(e):
                for t in engs.sp.thunks:
                    t()


class Arena:
    def __init__(self, tensor, nbytes):
        self.t = tensor
        self.nbytes = nbytes
        self.off = 0
        self.peak = 0

    def reset(self):
        self.off = 0

    def alloc(self, name, shape, dt):
        esz = 2 if dt == BF16 else 4
        n = 1
        for d in shape[1:]:
            n *= d
        nb = (n * esz + 63) // 64 * 64
        assert self.off + nb <= self.nbytes, (name, self.off, nb, self.nbytes)
        ap = self.t[0:shape[0], self.off // 4:(self.off + nb) // 4]
        self.off += nb
        self.peak = max(self.peak, self.off)
        if dt == BF16:
            ap = ap.bitcast(BF16)
        ap = ap[:, 0:n]
        if len(shape) == 3:
            ap = ap.rearrange("p (a b) -> p a b", b=shape[2])
        elif len(shape) == 4:
            ap = ap.rearrange("p (a b c) -> p a b c", b=shape[2], c=shape[3])
        return ap


class Ring:
    def __init__(self, alloc, name, shape, dt, n, psum=False):
        self.t = [alloc(f"{name}{i}", shape, dt) for i in range(n)]
        self.b = [Buf(psum) for _ in range(n)]
        self.i = 0

    def get(self):
        i = self.i
        self.i = (i + 1) % len(self.t)
        return self.t[i], self.b[i]


def build_program(debug=False):
    nc = bass.Bass("TRN2", target_bir_lowering=False)
    dram_in = lambda name, shape, dt=F32: nc.dram_tensor(name, list(shape), dt, kind="ExternalInput").ap()
    dram_out = lambda name, shape, dt=F32: nc.dram_tensor(name, list(shape), dt, kind="ExternalOutput").ap()
    dram_tmp = lambda name, shape, dt=F32: nc.dram_tensor(name, list(shape), dt).ap()

    xin = dram_in("xin", [T, D])
    condT = dram_in("condT", [128, 8, 2])
    st_gla = dram_in("st_gla", [L, 2, 64, 4, 128])
    c_ckv = dram_in("c_ckv", [L, PAST, 256])
    c_kr = dram_in("c_kr", [L, PAST, 32])
    c_dk = dram_in("c_dk", [L, PAST, 512])
    c_dv = dram_in("c_dv", [L, PAST, 512])
    w_mod = dram_in("w_mod", [L, D, 6 * D])
    b_modT = dram_in("b_modT", [L, 128, 48])
    gnT = dram_in("gnT", [L, 128, 16])
    w_in = dram_in("w_in", [L, D, IN_COLS])
    w_a2bd = dram_in("w_a2bd", [L, 32, 512])
    b_a = dram_in("b_a", [L, 1, 512])
    gcol = dram_in("gcol", [L, 128, 12])
    w_uq = dram_in("w_uq", [L, 384, 8, 96])
    w_ukp = dram_in("w_ukp", [L, 256, 8, 96])
    w_uv = dram_in("w_uv", [L, 256, 512])
    lam_qk = dram_in("lam_qk", [L, 1, 256])
    w_o3 = dram_in("w_o3", [L, 3, 512, D])
    w_out = dram_in("w_out", [L, D, D])
    w_m1 = dram_in("w_m1", [L, D, 4 * D])
    w_m2 = dram_in("w_m2", [L, 4 * D, D])
    ident_d = dram_in("ident", [128, 128])
    cmat_d = dram_in("cmat", [128, 8, 128], BF16)
    pmat_d = dram_in("pmat", [128, 3, 128], BF16)
    tri_d = dram_in("tri", [64, 4, 64])
    ropeM = dram_in("ropeM", [2, 96, NS_TOK])
    ropeD = dram_in("ropeD", [2, 128, NS_TOK])

    y_out = dram_out("y_out", [T, D])
    o_gla = dram_out("o_gla", [L, 2, 2, 64, 4, 128])
    o_ckv = dram_out("o_ckv", [L, NP_TOK, 256])
    o_kr = dram_out("o_kr", [L, NP_TOK, 32])
    o_dk = dram_out("o_dk", [L, NP_TOK, 512])
    o_dv = dram_out("o_dv", [L, NP_TOK, 512])

    mk = dram_out if debug else dram_tmp
    xT_s = dram_tmp("xT_s", [8, 128, T])
    gq_s = dram_tmp("gq_s", [64, 4, T], BF16)
    gk_s = dram_tmp("gk_s", [64, 4, T], BF16)
    gkt_s = dram_tmp("gkt_s", [T, 256], BF16)
    gvt_s = dram_tmp("gvt_s", [T, 512], BF16)
    gG_s = dram_tmp("gG_s", [T, 512])
    go_s = dram_tmp("go_s", [128, 4, T])
    q_s = dram_tmp("q_s", [8, 96, T], BF16)
    k_s = dram_tmp("k_s", [8, 96, NP_TOK + TKS], BF16)
    v_s = dram_tmp("v_s", [8, 128, (NP_TOK + TKS) // 128, 128], BF16)
    dq_s = dram_tmp("dq_s", [4, 128, T], BF16)
    dk_s = dram_tmp("dk_s", [4, 128, NP_TOK + TKS], BF16)
    dv_s = dram_tmp("dv_s", [4, 128, (NP_TOK + TKS) // 128, 128], BF16)
    ya_s = mk("ya_s", [4, 128, T], BF16)
    yb_s = mk("yb_s", [4, 128, T], BF16)
    yc_s = mk("yc_s", [4, 128, T], BF16)

    with ExitStack() as ctx:
        fw = FW(nc, ctx)
        PE, DVE, ACT, POOL, SP = fw.pe, fw.dve, fw.act, fw.pool, fw.sp
        sb = lambda name, shape, dt=F32: ctx.enter_context(nc.sbuf_tensor("s_" + name, list(shape), dt))
        psb = lambda name, shape, dt=F32: ctx.enter_context(nc.psum_tensor("p_" + name, list(shape), dt))

        psA = Ring(psb, "psA", [128, 512], F32, 4, psum=True)
        psB = Ring(psb, "psB", [128, 512], F32, 4, psum=True)

        OVB = 72 * 1024
        arena = Arena(sb("arena", [128, OVB // 4], F32), OVB)
        ov = arena.alloc

        def ew():
            fw.ew_rr ^= 1
            return DVE if fw.ew_rr else POOL

        ident = sb("ident", [128, 128]); b_ident = Buf()
        cmat = sb("cmat", [128, 8, 128], BF16); b_cmat = Buf()
        pmat = sb("pmat", [128, 3, 128], BF16); b_pmat = Buf()
        tri = sb("tri", [64, 4, 64]); b_tri = Buf()
        ones_row = sb("ones_row", [1, 128], BF16); b_onesrow = Buf()
        ones_rowf = sb("ones_rowf", [1, 128]); b_onesrowf = Buf()
        ones_col = sb("ones_col", [64, 1]); b_onescol = Buf()
        onesf = sb("onesf", [128, 128]); b_onesf = Buf()
        fw.dma(SP, ident[:], ident_d, writes=[b_ident])
        fw.dma(SP, cmat[:], cmat_d, writes=[b_cmat])
        fw.dma(SP, pmat[:], pmat_d, writes=[b_pmat])
        fw.dma(SP, tri[:], tri_d, writes=[b_tri])
        fw.op(DVE, lambda h: h.memset(ones_row[:], 1.0), writes=[b_onesrow])
        fw.op(DVE, lambda h: h.memset(ones_rowf[:], 1.0), writes=[b_onesrowf])
        fw.op(DVE, lambda h: h.memset(ones_col[:], 1.0), writes=[b_onescol])
        fw.op(DVE, lambda h: h.memset(onesf[:], 1.0), writes=[b_onesf])
        C_1024, C_384, C_256, C_96, C_BLK64, C_128, C_ONE, C_SEL32 = range(8)
        P_96, P_128, _ = range(3)

        cT = sb("cT", [128, 8, 2]); b_cT = Buf()
        scT = sb("scT", [128, 8, 2]); b_scT = Buf()
        fw.dma(SP, cT[:], condT, writes=[b_cT])
        fw.op(ACT, lambda h: h.activation(out=scT[:], in_=cT[:], func=AF.Silu), reads=[b_cT], writes=[b_scT])

        modT = sb("modT", [128, 48, 2]); b_modT_b = Buf()
        bmod = sb("bmod", [128, 48]); b_bmod = Buf()
        gn = sb("gn", [128, 16]); b_gn = Buf()
        A1 = sb("A1", [128, 8, 2]); A2 = sb("A2", [128, 8, 2]); b_A = Buf()
        gc = sb("gc", [128, 12]); b_gc = Buf()
        wa2 = sb("wa2", [32, 512], BF16); b_wa2 = Buf()
        ba = sb("ba", [1, 512], BF16); b_ba = Buf()
        lamt = sb("lamt", [1, 256]); b_lamt = Buf()
        lam1 = sb("lam1", [1, 8]); b_lam1 = Buf()
        lamc = sb("lamc", [128, 2]); b_lamc = Buf()
        arena.reset()
        wmod_r = Ring(ov, "wmod", [128, 8, 256], F32, 2)

        w8 = Ring(sb, "w8", [128, 8, 512], BF16, 3)
        w4 = Ring(sb, "w4", [128, 4, 1024], BF16, 2)

        def load_w8(src_ap_rows_cols, ncols, nk=8):
            t, b = w8.get()
            fw.dma(POOL, t[:, 0:nk, 0:ncols], src_ap_rows_cols.rearrange("(k p) c -> p k c", p=128), writes=[b])
            return t, b

        xt_r = Ring(sb, "xt", [128, 8, TT], F32, 1)
        sq_r = Ring(sb, "sq", [128, 8, TT], BF16, 1)
        hT_r = Ring(sb, "hT", [128, 8, TT], BF16, 2)
        rs_r = Ring(sb, "rs", [128, TT], F32, 4)
        f32_r = Ring(sb, "f32", [128, TT], F32, 12)
        bf_r = Ring(sb, "bf", [128, TT], BF16, 12)

        def evac(ps_ap, out_ap, b_ps, b_out, eng=None, func=None, scale=1.0):
            if func is not None or eng is ACT:
                f = func if func is not None else AF.Copy
                return fw.op(ACT, lambda h: h.activation(out=out_ap, in_=ps_ap, func=f, scale=scale), reads=[b_ps], writes=[b_out])
            return fw.op(DVE, lambda h: h.tensor_copy(out=out_ap, in_=ps_ap), reads=[b_ps], writes=[b_out])

        b_xT = [Buf() for _ in range(NT)]
        arena.reset()
        xtok_r = Ring(ov, "xtok", [128, 4, D], F32, 2)

        def phase0(ti):
            xk, bxk = xtok_r.get()
            fw.dma(SP, xk[:], xin[ti * TT:(ti + 1) * TT, :].rearrange("(g p) f -> p g f", p=128), writes=[bxk])
            xt, bxt = xt_r.get()
            for k in range(8):
                ps, bps = psA.get()
                for g in range(4):
                    fw.op(PE, lambda h, ps=ps, g=g, k=k: h.transpose(ps[:, g * 128:(g + 1) * 128], xk[:, g, k * 128:(k + 1) * 128], ident[:]),
                          reads=[bxk, b_ident], writes=[bps], inc=(g == 3))
                evac(ps[:], xt[:, k, :], bps, bxt, eng=(ACT if k % 2 else DVE))
            fw.dma(SP, xT_s[:, :, ti * TT:(ti + 1) * TT].rearrange("k p t -> p k t"), xt[:], reads=[bxt], writes=[b_xT[ti]])

        import os
        for ti in range(int(os.environ.get("KPH0", str(NT)))):
            phase0(ti)
        fw.barrier()

        def rsqrt_eps(src_ap, dst_ap, bsrc, bdst):
            fw.op(ACT, lambda h: h.activation(out=dst_ap, in_=src_ap, func=AF.Ln, bias=EPS), reads=[bsrc], writes=[bdst])
            fw.op(ACT, lambda h: h.activation(out=dst_ap, in_=dst_ap, func=AF.Exp, scale=-0.5), reads=[bdst], writes=[bdst])

        def rms_rstd(sq_aps, ones_ap, nparts, eps=EPS):
            ps, bps = psA.get()
            n = len(sq_aps)
            for i, (a, b) in enumerate(sq_aps):
                fw.op(PE, lambda h, a=a, i=i: h.matmul(ps[0:nparts, :], lhsT=ones_ap, rhs=a, start=(i == 0), stop=(i == n - 1)),
                      reads=[b, b_cmat], writes=[bps], inc=(i == n - 1))
            rs, brs = rs_r.get()
            rsqrt_eps(ps[0:nparts, :], rs[0:nparts, :], bps, brs)
            return rs, brs

        def load_xT(ti):
            xt, bxt = xt_r.get()
            fw.dma(SP, xt[:], xT_s[:, :, ti * TT:(ti + 1) * TT].rearrange("k p t -> p k t"), reads=[b_xT[ti]], writes=[bxt])
            return xt, bxt

        def norm_mod(xt, bxt, Amod, shift_lo, cond):
            sq, bsq = sq_r.get()
            fw.op(ACT, lambda h: h.activation(out=sq[:], in_=xt[:], func=AF.Square), reads=[bxt], writes=[bsq])
            rs, brs = rms_rstd([(sq[:, k, :], bsq) for k in range(8)], cmat[:, C_1024, :], 128)
            hT, bh = hT_r.get()
            for k in range(8):
                tmp, btmp = f32_r.get()
                e = DVE
                fw.op(e, lambda h, k=k, tmp=tmp: h.tensor_tensor(out=tmp[:], in0=xt[:, k, :], in1=rs[:], op=ALU.mult), reads=[bxt, brs], writes=[btmp])
                fw.op(ACT, lambda h, k=k, tmp=tmp: h.activation(out=hT[:, k, :], in_=tmp[:], func=AF.Identity,
                                                               scale=Amod[:, k, cond:cond + 1], bias=modT[:, shift_lo + k, cond:cond + 1]),
                      reads=[btmp, b_A, b_modT_b], writes=[bh])
            return hT, bh

        def proj_fm(wt, bw, col0, m, rhs_fn, rhs_bufs, nk, out_parts=None):
            ps, bps = psA.get()
            for k in range(nk):
                fw.op(PE, lambda h, k=k: h.matmul(ps[0:m, :], lhsT=wt[:, k, col0:col0 + m], rhs=rhs_fn(k), start=(k == 0), stop=(k == nk - 1)),
                      reads=[bw] + rhs_bufs, writes=[bps], inc=(k == nk - 1))
            return ps, bps

        def norm_rope_gen(ps, bps, npart, ones_ap, gcolumn, rope, pm_idx, ropetab, dst_ap, dst_buf, keep_f32=None):
            xf, bxf = f32_r.get()
            evac(ps[0:npart, :], xf[0:npart, :], bps, bxf, eng=DVE)
            sq, bsq = bf_r.get()
            fw.op(ACT, lambda h: h.activation(out=sq[0:npart, :], in_=xf[0:npart, :], func=AF.Square), reads=[bxf], writes=[bsq])
            yield
            rs, brs = rms_rstd([(sq[0:npart, :], bsq)], ones_ap, npart)
            xn, bxn = f32_r.get()
            fw.op(DVE, lambda h: h.scalar_tensor_tensor(out=xn[0:npart, :], in0=xf[0:npart, :], scalar=gc[0:npart, gcolumn:gcolumn + 1], in1=rs[0:npart, :],
                                                        op0=ALU.mult, op1=ALU.mult), reads=[bxf, brs, b_gc], writes=[bxn])
            if keep_f32 is not None:
                keep_f32(xn, bxn)
            ob, bob = bf_r.get()
            if not rope:
                fw.op(ACT, lambda h: h.activation(out=ob[0:npart, :], in_=xn[0:npart, :], func=AF.Copy), reads=[bxn], writes=[bob])
            else:
                ctab, stab, btab = ropetab
                xb, bxb = bf_r.get()
                fw.op(ACT, lambda h: h.activation(out=xb[0:npart, :], in_=xn[0:npart, :], func=AF.Copy), reads=[bxn], writes=[bxb])
                t1, bt1 = f32_r.get()
                fw.op(DVE, lambda h: h.tensor_tensor(out=t1[0:npart, :], in0=xn[0:npart, :], in1=ctab, op=ALU.mult), reads=[bxn, btab], writes=[bt1])
                yield
                ps2, bps2 = psA.get()
                fw.op(PE, lambda h: h.matmul(ps2[0:npart, :], lhsT=pmat[0:npart, pm_idx, 0:npart], rhs=xb[0:npart, :], start=True, stop=True),
                      reads=[bxb, b_pmat], writes=[bps2])
                t2, bt2 = f32_r.get()
                fw.op(DVE, lambda h: h.tensor_tensor(out=t2[0:npart, :], in0=ps2[0:npart, :], in1=stab, op=ALU.mult), reads=[bps2, btab], writes=[bt2])
                fw.op(DVE, lambda h: h.tensor_tensor(out=ob[0:npart, :], in0=t1[0:npart, :], in1=t2[0:npart, :], op=ALU.add), reads=[bt1, bt2], writes=[bob])
            fw.dma(SP, dst_ap, ob[0:npart, :], reads=[bob], writes=[dst_buf])

        def norm_rope_store(*a, **k):
            for _ in norm_rope_gen(*a, **k):
                pass

        class Pipe:
            def __init__(self):
                self.gens = []

            def push(self, gen):
                next(gen, None)
                for og in list(self.gens):
                    try:
                        next(og)
                    except StopIteration:
                        self.gens.remove(og)
                self.gens.append(gen)

            def drain(self):
                while self.gens:
                    for og in list(self.gens):
                        try:
                            next(og)
                        except StopIteration:
                            self.gens.remove(og)

        def transpose_out(src_fn, src_bufs, nfeat_parts, ncols_total, dst_rows_fn, colslices):
            for g in range(4):
                ps, bps = psA.get()
                n = len(colslices)
                for i, (j, pj, c0) in enumerate(colslices):
                    fw.op(PE, lambda h, j=j, pj=pj, c0=c0, g=g: h.transpose(ps[:, c0:c0 + pj], src_fn(j)[:, g * 128:(g + 1) * 128], ident[0:pj, 0:pj]),
                          reads=src_bufs + [b_ident], writes=[bps], inc=(i == n - 1))
                o, bo = f32_r.get()
                evac(ps[:, 0:ncols_total], o[:, 0:ncols_total], bps, bo, eng=DVE)
                fw.dma(SP, dst_rows_fn(g), o[:, 0:ncols_total], reads=[bo], writes=[Buf()])

        b_gq = Buf(); b_gk = Buf(); b_gkt = Buf(); b_gvt = Buf(); b_gG = Buf(); b_go = Buf()
        b_q = Buf(); b_k = Buf(); b_v = Buf(); b_dq = Buf(); b_dk = Buf(); b_dv = Buf()
        b_ya = Buf(); b_yb = Buf(); b_yc = Buf()
        out_bufs = []

        arena.reset()
        ropM_r = Ring(ov, "ropM", [96, 2, TT], F32, 1)
        ropD_r = Ring(ov, "ropD", [128, 2, TT], F32, 1)
        vaug_r = Ring(ov, "vaug", [128, 8, 4, 128], BF16, 1)
        tok_r = Ring(ov, "tok", [128, 4, 512], BF16, 2)
        tokf_r = Ring(ov, "tokf", [128, 4, 512], F32, 1)
        keep_r = Ring(ov, "keep", [128, 4, TT], F32, 1)
        qdn_r = Ring(ov, "qdn", [128, 3, TT], BF16, 1)
        ckv_r = Ring(ov, "ckv", [128, 2, TT], BF16, 1)
        ckvf_r = Ring(ov, "ckvf", [128, 2, TT], F32, 1)
        krb_r = Ring(ov, "krb", [32, TT], BF16, 1)
        krf_r = Ring(ov, "krf", [32, TT], F32, 1)
        aab_r = Ring(ov, "aab", [32, TT], BF16, 1)
        g4_r = Ring(ov, "g4", [64, 4, TT], BF16, 1)
        r4_r = Ring(ov, "r4", [128, 4, TT], BF16, 1)
        ctok_r = Ring(ov, "ctok", [128, 2, 512], F32, 2)

        def layer_setup(l):
            fw.dma(SP, bmod[:], b_modT[l], writes=[b_bmod])
            fw.dma(SP, gn[:], gnT[l], writes=[b_gn])
            fw.dma(SP, gc[:], gcol[l], writes=[b_gc])
            fw.dma(POOL, wa2[:], w_a2bd[l], writes=[b_wa2])
            fw.dma(POOL, ba[:], b_a[l], writes=[b_ba])
            fw.dma(SP, lamt[:], lam_qk[l], writes=[b_lamt])
            for cb in range(24):
                wm, bwm = wmod_r.get()
                fw.dma(SP, wm[:], w_mod[l, :, cb * 256:(cb + 1) * 256].rearrange("(k p) c -> p k c", p=128), writes=[bwm])
                ps, bps = psA.get()
                for j in range(2):
                    for k in range(8):
                        fw.op(PE, lambda h, j=j, k=k, wm=wm, ps=ps: h.matmul(ps[:, j * 2:j * 2 + 2], lhsT=wm[:, k, j * 128:(j + 1) * 128], rhs=scT[:, k, :],
                                                                          start=(k == 0), stop=(k == 7)),
                              reads=[bwm, b_scT], writes=[bps], inc=(j == 1 and k == 7))
                fw.op(DVE, lambda h, cb=cb, ps=ps: h.tensor_tensor(out=modT[:, cb * 2:(cb + 1) * 2, :], in0=ps[:, 0:4].rearrange("p (j c) -> p j c", c=2),
                                                                 in1=bmod[:, cb * 2:(cb + 1) * 2].unsqueeze(2).broadcast_to([128, 2, 2]), op=ALU.add),
                      reads=[bps, b_bmod], writes=[b_modT_b])
            fw.op(DVE, lambda h: h.scalar_tensor_tensor(out=A1[:], in0=modT[:, 8:16, :], scalar=1.0, in1=gn[:, 0:8].unsqueeze(2).broadcast_to([128, 8, 2]),
                                                        op0=ALU.add, op1=ALU.mult), reads=[b_modT_b, b_gn], writes=[b_A])
            fw.op(DVE, lambda h: h.scalar_tensor_tensor(out=A2[:], in0=modT[:, 32:40, :], scalar=1.0, in1=gn[:, 8:16].unsqueeze(2).broadcast_to([128, 8, 2]),
                                                        op0=ALU.add, op1=ALU.mult), reads=[b_modT_b, b_gn], writes=[b_A])
            lam_init = 0.8 - 0.6 * math.exp(-0.3 * l)
            fw.op(DVE, lambda h: h.tensor_tensor(out=lamt[:, 0:64], in0=lamt[:, 0:64], in1=lamt[:, 64:128], op=ALU.mult), reads=[b_lamt], writes=[b_lamt])
            fw.op(DVE, lambda h: h.tensor_tensor(out=lamt[:, 128:192], in0=lamt[:, 128:192], in1=lamt[:, 192:256], op=ALU.mult), reads=[b_lamt], writes=[b_lamt])
            fw.op(DVE, lambda h: h.reduce_sum(out=lam1[:, 0:1], in_=lamt[:, 0:64], axis=mybir.AxisListType.X), reads=[b_lamt], writes=[b_lam1])
            fw.op(DVE, lambda h: h.reduce_sum(out=lam1[:, 1:2], in_=lamt[:, 128:192], axis=mybir.AxisListType.X), reads=[b_lamt], writes=[b_lam1])
            fw.op(ACT, lambda h: h.activation(out=lam1[:, 2:4], in_=lam1[:, 0:2], func=AF.Exp), reads=[b_lam1], writes=[b_lam1])
            fw.op(DVE, lambda h: h.scalar_tensor_tensor(out=lam1[:, 4:5], in0=lam1[:, 3:4], scalar=-lam_init, in1=lam1[:, 2:3], op0=ALU.add, op1=ALU.subtract),
                  reads=[b_lam1], writes=[b_lam1])
            ps, bps = psA.get()
            fw.op(PE, lambda h: h.matmul(ps[:, 0:1], lhsT=ones_rowf[:, :], rhs=lam1[:, 4:5], start=True, stop=True), reads=[b_onesrowf, b_lam1], writes=[bps])
            fw.op(DVE, lambda h: h.tensor_copy(out=lamc[:, 0:1], in_=ps[:, 0:1]), reads=[bps], writes=[b_lamc])
            return lam_init

        def key_base(ti):
            return 0 if ti == 0 else NP_TOK + PAST + (ti - 1) * TT

        import os as _os
        KP1S = int(_os.environ.get("KP1S", "99"))
        KP1 = int(_os.environ.get("KP1", str(NT)))

        def phase1(l, ti):
            if ti >= KP1:
                return
            lat = ti > 0
            cond = 1 if lat else 0
            t0 = ti * TT
            kb = key_base(ti)
            xt, bxt = load_xT(ti)
            hT, bh = norm_mod(xt, bxt, A1, 0, cond)
            rhs_h = lambda k: hT[:, k, :]
            if lat:
                rM, brM = ropM_r.get()
                fw.dma(SP, rM[:], ropeM[:, :, (ti - 1) * TT:ti * TT].rearrange("c p t -> p c t"), writes=[brM])
                rD, brD = ropD_r.get()
                fw.dma(SP, rD[:], ropeD[:, :, (ti - 1) * TT:ti * TT].rearrange("c p t -> p c t"), writes=[brD])
                tabM = (rM[:, 0, :], rM[:, 1, :], brM)
                tabD = (rD[:, 0, :], rD[:, 1, :], brD)
            else:
                tabM = tabD = None

            if KP1S < 1:
                return
            wt, bw = load_w8(w_in[l, :, O_AQ:O_AQ + 512], 512)
            for which, dst, bdst in ((0, gq_s, b_gq), (1, gk_s, b_gk)):
                g4, bg4 = g4_r.get()
                for hh in range(4):
                    ps, bps = proj_fm(wt, bw, which * 256 + hh * 64, 64, rhs_h, [bh], 8)
                    evac(ps[0:64, :], g4[:, hh, :], bps, bg4, eng=(ACT if hh % 2 else DVE))
                fw.dma(SP, dst[:, :, t0:t0 + TT], g4[:], reads=[bg4], writes=[bdst])
            if KP1S < 2:
                return
            tk, btk = tok_r.get()
            for g in range(4):
                ps, bps = psA.get()
                for k in range(8):
                    fw.op(PE, lambda h, k=k, g=g, ps=ps: h.matmul(ps[:, 0:256], lhsT=hT[:, k, g * 128:(g + 1) * 128], rhs=wt[:, k, 256:512], start=(k == 0), stop=(k == 7)),
                          reads=[bw, bh], writes=[bps], inc=(k == 7))
                evac(ps[:, 0:256], tk[:, g, 0:256], bps, btk, eng=(ACT if g % 2 else DVE))
            fw.dma(SP, gkt_s[t0:t0 + TT, :].rearrange("(g p) c -> p g c", p=128), tk[:, :, 0:256], reads=[btk], writes=[b_gkt])
            wt, bw = load_w8(w_in[l, :, O_AV:O_AV + 512], 512)
            tk, btk = tok_r.get()
            for g in range(4):
                ps, bps = psA.get()
                for k in range(8):
                    fw.op(PE, lambda h, k=k, g=g, ps=ps, wt=wt: h.matmul(ps[:, :], lhsT=hT[:, k, g * 128:(g + 1) * 128], rhs=wt[:, k, 0:512], start=(k == 0), stop=(k == 7)),
                          reads=[bw, bh], writes=[bps], inc=(k == 7))
                evac(ps[:, :], tk[:, g, :], bps, btk, eng=(ACT if g % 2 else DVE))
            fw.dma(SP, gvt_s[t0:t0 + TT, :].rearrange("(g p) c -> p g c", p=128), tk[:], reads=[btk], writes=[b_gvt])
            if KP1S < 3:
                return
            wt, bw = load_w8(w_in[l, :, O_AA:O_AA + 416], 416)
            ps, bps = proj_fm(wt, bw, 0, 32, rhs_h, [bh], 8)
            aab, baab = aab_r.get()
            evac(ps[0:32, :], aab[:, :], bps, baab, eng=DVE)
            gt, bgt = tokf_r.get()
            for g in range(4):
                ps, bps = psA.get()
                fw.op(PE, lambda h, g=g, ps=ps: h.matmul(ps[:, :], lhsT=aab[:, g * 128:(g + 1) * 128], rhs=wa2[:, :], start=True, stop=False),
                      reads=[baab, b_wa2], writes=[bps], inc=False)
                fw.op(PE, lambda h, ps=ps: h.matmul(ps[:, :], lhsT=ones_row[:, :], rhs=ba[:, :], start=False, stop=True), reads=[b_onesrow, b_ba], writes=[bps])
                e1, be1 = f32_r.get()
                fw.op(ACT, lambda h, ps=ps, e1=e1: h.activation(out=e1[:], in_=ps[:], func=AF.Exp, scale=-1.0), reads=[bps], writes=[be1])
                fw.op(ACT, lambda h, g=g, e1=e1: h.activation(out=gt[:, g, :], in_=e1[:], func=AF.Ln, bias=1.0), reads=[be1], writes=[bgt])
            fw.dma(SP, gG_s[t0:t0 + TT, :].rearrange("(g p) c -> p g c", p=128), gt[:], reads=[bgt], writes=[b_gG])

            if KP1S < 4:
                return
            wt_r, bw_r = load_w8(w_in[l, :, O_AR:O_AR + 512], 512)
            rr, brr = r4_r.get()
            for hh in range(4):
                ps_r, bps_r = proj_fm(wt_r, bw_r, hh * 128, 128, rhs_h, [bh], 8)
                fw.op(ACT, lambda h, hh=hh, ps_r=ps_r: h.activation(out=rr[:, hh, :], in_=ps_r[:], func=AF.Silu), reads=[bps_r], writes=[brr])
            fw.dma(SP, rs_s[:, :, t0:t0 + TT], rr[:], reads=[brr], writes=[b_rs])

            if KP1S < 5:
                return
            qdn, bqdn = qdn_r.get()
            qf = []
            for j in range(3):
                ps, bps = proj_fm(wt, bw, 32 + j * 128, 128, rhs_h, [bh], 8)
                xf, bxf = f32_r.get()
                evac(ps[:], xf[:], bps, bxf, eng=DVE)
                sq, bsq = bf_r.get()
                fw.op(ACT, lambda h, sq=sq, xf=xf: h.activation(out=sq[:], in_=xf[:], func=AF.Square), reads=[bxf], writes=[bsq])
                qf.append((xf, bxf, sq, bsq))
            rs, brs = rms_rstd([(q[2][:], q[3]) for q in qf], cmat[:, C_384, :], 128)
            for j in range(3):
                xf, bxf = qf[j][0], qf[j][1]
                fw.op(DVE, lambda h, j=j, xf=xf: h.scalar_tensor_tensor(out=qdn[:, j, :], in0=xf[:], scalar=gc[:, j:j + 1], in1=rs[:], op0=ALU.mult, op1=ALU.mult),
                      reads=[bxf, brs, b_gc], writes=[bqdn])
            pipe = Pipe()
            for half in range(2):
                wq, bwq = w8.get()
                fw.dma(POOL, wq[:, 0:3, 0:384], w_uq[l, :, half * 4:(half + 1) * 4, :].rearrange("(k p) h c -> p k (h c)", p=128), writes=[bwq])
                for hh in range(4):
                    ps, bps = proj_fm(wq, bwq, hh * 96, 96, lambda k: qdn[:, k, :], [bqdn], 3)
                    head = half * 4 + hh
                    pipe.push(norm_rope_gen(ps, bps, 96, cmat[0:96, C_96, 0:96], 5, lat, P_96, tabM, q_s[head, :, t0:t0 + TT], b_q))
            pipe.drain()

            if KP1S < 6:
                return
            wt, bw = load_w8(w_in[l, :, O_KVD:O_KVD + 288], 288)
            ckv, bckv = ckv_r.get()
            ckvf, bckvf = ckvf_r.get()
            kf = []
            for j in range(2):
                ps, bps = proj_fm(wt, bw, j * 128, 128, rhs_h, [bh], 8)
                xf, bxf = f32_r.get()
                evac(ps[:], xf[:], bps, bxf, eng=DVE)
                sq, bsq = bf_r.get()
                fw.op(ACT, lambda h, sq=sq, xf=xf: h.activation(out=sq[:], in_=xf[:], func=AF.Square), reads=[bxf], writes=[bsq])
                kf.append((xf, bxf, sq, bsq))
            rs, brs = rms_rstd([(q[2][:], q[3]) for q in kf], cmat[:, C_256, :], 128)
            for j in range(2):
                xf, bxf = kf[j][0], kf[j][1]
                fw.op(DVE, lambda h, j=j, xf=xf: h.scalar_tensor_tensor(out=ckvf[:, j, :], in0=xf[:], scalar=gc[:, 3 + j:4 + j], in1=rs[:], op0=ALU.mult, op1=ALU.mult),
                      reads=[bxf, brs, b_gc], writes=[bckvf])
            fw.op(ACT, lambda h: h.activation(out=ckv[:], in_=ckvf[:], func=AF.Copy), reads=[bckvf], writes=[bckv])
            ps, bps = proj_fm(wt, bw, 256, 32, rhs_h, [bh], 8)
            krf, bkrf = krf_r.get()
            krb, bkrb = krb_r.get()
            evac(ps[0:32, :], krf[:, :], bps, bkrf, eng=DVE)
            fw.op(ACT, lambda h: h.activation(out=krb[:], in_=krf[:], func=AF.Copy), reads=[bkrf], writes=[bkrb])
            KSUB = _os.environ.get("KSUB", "abcd")
            if not lat:
                if "a" in KSUB:
                    transpose_out(lambda j: ckvf[:, j, :], [bckvf], 128, 256, lambda g: o_ckv[l, g * 128:(g + 1) * 128, :], [(0, 128, 0), (1, 128, 128)])
                if "b" in KSUB:
                    transpose_out(lambda j: krf[:, :], [bkrf], 32, 32, lambda g: o_kr[l, g * 128:(g + 1) * 128, :], [(0, 32, 0)])
            if "c" in KSUB:
                mla_keys(l, lambda k: ckv[:, k, :], [bckv], krb, bkrb, lat, tabM, kb)
            if "d" in KSUB:
                mla_values(l, lambda k, g: ckv[:, k, g * 128:(g + 1) * 128], [bckv], kb)

            if KP1S < 7:
                return
            for which, off, dst, bdst, gcolumn in ((0, O_DQ, dq_s, b_dq, 7), (1, O_DK, dk_s, b_dk, 8)):
                wt, bw = load_w8(w_in[l, :, off:off + 512], 512)
                dpipe = Pipe()
                keep = None
                if which == 1 and not lat:
                    kp, bkp = keep_r.get()
                for hh in range(4):
                    ps, bps = proj_fm(wt, bw, hh * 128, 128, rhs_h, [bh], 8)
                    kf32 = None
                    if which == 1 and not lat:
                        kf32 = lambda xn, bxn, hh=hh: fw.op(ACT, lambda h: h.activation(out=kp[:, hh, :], in_=xn[:], func=AF.Copy), reads=[bxn], writes=[bkp])
                    col0 = t0 if which == 0 else kb
                    dpipe.push(norm_rope_gen(ps, bps, 128, cmat[:, C_BLK64, :], gcolumn, lat, P_128, tabD, dst[hh, :, col0:col0 + TT], bdst, keep_f32=kf32))
                dpipe.drain()
                if which == 1 and not lat:
                    transpose_out(lambda j: kp[:, j, :], [bkp], 128, 512, lambda g: o_dk[l, g * 128:(g + 1) * 128, :], [(j, 128, j * 128) for j in range(4)])
            wt, bw = load_w8(w_in[l, :, O_DV:O_DV + 512], 512)
            tk, btk = tok_r.get()
            if not lat:
                tf, btf = tokf_r.get()
            for g in range(4):
                ps, bps = psA.get()
                for k in range(8):
                    fw.op(PE, lambda h, k=k, g=g, ps=ps, wt=wt: h.matmul(ps[:, :], lhsT=hT[:, k, g * 128:(g + 1) * 128], rhs=wt[:, k, 0:512], start=(k == 0), stop=(k == 7)),
                          reads=[bw, bh], writes=[bps], inc=(k == 7))
                evac(ps[:, :], tk[:, g, :], bps, btk, eng=ACT)
                if not lat:
                    evac(ps[:, :], tf[:, g, :], bps, btf, eng=DVE)
            for hh in range(4):
                fw.dma(SP, dv_s[hh, :, kb // 128:kb // 128 + 4, :], tk[:, :, hh * 128:(hh + 1) * 128], reads=[btk], writes=[b_dv])
            if not lat:
                ob = Buf(); out_bufs.append(ob)
                fw.dma(SP, o_dv[l, :, :].rearrange("(g p) c -> p g c", p=128), tf[:], reads=[btf], writes=[ob])

        def mla_keys(l, ckv_fn, ckv_bufs, krb, bkrb, rope, tabM, kb, ntok=TT):
            pipe = Pipe()
            for half in range(2):
                wk, bwk = w8.get()
                fw.dma(POOL, wk[:, 0:2, 0:384], w_ukp[l, :, half * 4:(half + 1) * 4, :].rearrange("(k p) h c -> p k (h c)", p=128), writes=[bwk])
                for hh in range(4):
                    ps, bps = psA.get()
                    for k in range(2):
                        fw.op(PE, lambda h, k=k, hh=hh, ps=ps, wk=wk: h.matmul(ps[0:96, 0:ntok], lhsT=wk[:, k, hh * 96:(hh + 1) * 96], rhs=ckv_fn(k), start=(k == 0), stop=False),
                              reads=[bwk] + ckv_bufs, writes=[bps], inc=False)
                    fw.op(PE, lambda h, ps=ps: h.matmul(ps[0:96, 0:ntok], lhsT=cmat[0:32, C_SEL32, 0:96], rhs=krb[:, 0:ntok], start=False, stop=True),
                          reads=[b_cmat, bkrb], writes=[bps])
                    head = half * 4 + hh
                    if ntok == TT:
                        pipe.push(norm_rope_gen(ps, bps, 96, cmat[0:96, C_96, 0:96], 6, rope, P_96, tabM, k_s[head, :, kb:kb + TT], b_k))
                    else:
                        norm_store_small(ps, bps, ntok, head, kb)
            pipe.drain()

        def norm_store_small(ps, bps, ntok, head, kb):
            xf, bxf = f32_r.get()
            evac(ps[0:96, 0:ntok], xf[0:96, 0:ntok], bps, bxf, eng=DVE)
            sq, bsq = bf_r.get()
            fw.op(ACT, lambda h: h.activation(out=sq[0:96, 0:ntok], in_=xf[0:96, 0:ntok], func=AF.Square), reads=[bxf], writes=[bsq])
            ps2, bps2 = psA.get()
            fw.op(PE, lambda h: h.matmul(ps2[0:96, 0:ntok], lhsT=cmat[0:96, C_96, 0:96], rhs=sq[0:96, 0:ntok], start=True, stop=True), reads=[bsq, b_cmat], writes=[bps2])
            rs, brs = rs_r.get()
            rsqrt_eps(ps2[0:96, 0:ntok], rs[0:96, 0:ntok], bps2, brs)
            ob, bob = bf_r.get()
            fw.op(DVE, lambda h: h.scalar_tensor_tensor(out=ob[0:96, 0:ntok], in0=xf[0:96, 0:ntok], scalar=gc[0:96, 6:7], in1=rs[0:96, 0:ntok], op0=ALU.mult, op1=ALU.mult),
                  reads=[bxf, brs, b_gc], writes=[bob])
            fw.dma(SP, k_s[head, :, kb:kb + ntok], ob[0:96, 0:ntok], reads=[bob], writes=[b_k])

        vaug_init = [False, False]

        def mla_values(l, ckvT_fn, ckv_bufs, kb, ngrp=4):
            wv, bwv = w8.get()
            fw.dma(POOL, wv[:, 0:2, 0:512], w_uv[l].rearrange("(k p) c -> p k c", p=128), writes=[bwv])
            idx = vaug_r.i
            va, bva = vaug_r.get()
            if not vaug_init[idx]:
                vaug_init[idx] = True
                fw.op(DVE, lambda h: h.memset(va[:], 1.0), writes=[bva])
            for g in range(ngrp):
                ps, bps = psA.get()
                for k in range(2):
                    fw.op(PE, lambda h, k=k, g=g, ps=ps: h.matmul(ps[:, :], lhsT=ckvT_fn(k, g), rhs=wv[:, k, 0:512], start=(k == 0), stop=(k == 1)),
                          reads=[bwv] + ckv_bufs, writes=[bps], inc=(k == 1))
                fw.op(ACT if g % 2 else DVE, (lambda h, g=g, ps=ps: h.activation(out=va[:, :, g, 0:64], in_=ps[:, :].rearrange("p (h c) -> p h c", c=64), func=AF.Copy)) if g % 2
                      else (lambda h, g=g, ps=ps: h.tensor_copy(out=va[:, :, g, 0:64], in_=ps[:, :].rearrange("p (h c) -> p h c", c=64))), reads=[bps], writes=[bva])
            c0 = kb // 128
            fw.dma(SP, v_s[:, :, c0:c0 + ngrp, :].rearrange("h p g e -> p h (g e)"), va[:, :, 0:ngrp, :].rearrange("p h g e -> p h (g e)"), reads=[bva], writes=[b_v])


        def prep_cache(l):
            kb = NP_TOK
            ct, bct = ctok_r.get()
            fw.dma(SP, ct[:, :, 0:256], c_ckv[l].rearrange("(g p) c -> p g c", p=128), writes=[bct])
            ckv, bckv = ckv_r.get()
            for j in range(2):
                ps, bps = psA.get()
                for g in range(2):
                    fw.op(PE, lambda h, j=j, g=g, ps=ps: h.transpose(ps[:, g * 128:(g + 1) * 128], ct[:, g, j * 128:(j + 1) * 128], ident[:]),
                          reads=[bct, b_ident], writes=[bps], inc=(g == 1))
                evac(ps[:, 0:256], ckv[:, j, 0:256], bps, bckv, eng=DVE)
            ct2, bct2 = ctok_r.get()
            fw.dma(SP, ct2[:, :, 0:32], c_kr[l].rearrange("(g p) c -> p g c", p=128), writes=[bct2])
            krb, bkrb = krb_r.get()
            ps, bps = psA.get()
            for g in range(2):
                fw.op(PE, lambda h, g=g, ps=ps: h.transpose(ps[0:32, g * 128:(g + 1) * 128], ct2[:, g, 0:32], ident[:]), reads=[bct2, b_ident], writes=[bps], inc=(g == 1))
            evac(ps[0:32, 0:256], krb[:, 0:256], bps, bkrb, eng=DVE)
            mla_keys(l, lambda k: ckv[:, k, 0:256], [bckv], krb, bkrb, False, None, kb, ntok=256)
            mla_values(l, lambda k, g: ckv[:, k, g * 128:(g + 1) * 128], [bckv], kb, ngrp=2)
            ct, bct = ctok_r.get()
            fw.dma(SP, ct[:], c_dk[l].rearrange("(g p) c -> p g c", p=128), writes=[bct])
            for hh in range(4):
                ps, bps = psA.get()
                for g in range(2):
                    fw.op(PE, lambda h, hh=hh, g=g, ps=ps: h.transpose(ps[:, g * 128:(g + 1) * 128], ct[:, g, hh * 128:(hh + 1) * 128], ident[:]),
                          reads=[bct, b_ident], writes=[bps], inc=(g == 1))
                ob, bob = bf_r.get()
                evac(ps[:, 0:256], ob[:, 0:256], bps, bob, eng=DVE)
                fw.dma(SP, dk_s[hh, :, kb:kb + 256], ob[:, 0:256], reads=[bob], writes=[b_dk])
            ct3, bct3 = ctok_r.get()
            fw.dma(SP, ct3[:], c_dv[l].rearrange("(g p) c -> p g c", p=128), writes=[bct3])
            tkc, btkc = tok_r.get()
            fw.op(DVE, lambda h: h.tensor_copy(out=tkc[:, 0:2, :], in_=ct3[:]), reads=[bct3], writes=[btkc])
            for hh in range(4):
                fw.dma(SP, dv_s[hh, :, kb // 128:kb // 128 + 2, :], tkc[:, 0:2, hh * 128:(hh + 1) * 128], reads=[btkc], writes=[b_dv])

        arena.reset()
        gl_q = Ring(ov, "glq", [64, 4, TT], BF16, 1)
        gl_k = Ring(ov, "glk", [64, 4, TT], BF16, 1)
        gl_kt = Ring(ov, "glkt", [64, 8, 256], BF16, 1)
        gl_vt = Ring(ov, "glvt", [64, 8, 512], BF16, 1)
        gl_G = Ring(ov, "glG", [64, 8, 256], F32, 1)
        gl_o = Ring(ov, "glo", [128, 4, TT], F32, 1)
        gl_of = Ring(ov, "glof", [128, 4, TT], F32, 1)
        gl_r = Ring(ov, "glr", [128, 4, TT], BF16, 2)
        S_f = ov("S_f", [64, 4, 128], F32); S_b = ov("S_b", [64, 4, 128], BF16); b_S = Buf(); b_Sb = Buf()
        sm_r = Ring(ov, "sm", [64, 256], F32, 6)
        smb_r = Ring(ov, "smb", [64, 256], BF16, 8)
        dd_r = Ring(ov, "dd", [64, 4], F32, 3)

        def gla_dir(l, d, seqs):
            incl = tri[:, d, :]
            strict = tri[:, 2 + d, :]
            for (tok0, ntok, seq_idx, lat) in seqs:
                if lat:
                    fw.dma(SP, S_f[:], st_gla[l, d], reads=[b_Sb], writes=[b_S])
                else:
                    fw.op(DVE, lambda h: h.memset(S_f[:], 0.0), reads=[b_Sb], writes=[b_S])
                fw.op(ACT, lambda h: h.activation(out=S_b[:], in_=S_f[:], func=AF.Copy), reads=[b_S], writes=[b_Sb])
                tiles = list(range(tok0, tok0 + ntok, TT)) if ntok >= TT else [tok0]
                if d == 1:
                    tiles = tiles[::-1]
                for tt0 in tiles:
                    base_tile = (tt0 // TT) * TT
                    cq, bcq = gl_q.get(); ck, bck = gl_k.get(); ckt, bckt = gl_kt.get(); cvt, bcvt = gl_vt.get(); cG, bcG = gl_G.get()
                    fw.dma(SP, cq[:], gq_s[:, :, base_tile:base_tile + TT], reads=[b_gq], writes=[bcq])
                    fw.dma(SP, ck[:], gk_s[:, :, base_tile:base_tile + TT], reads=[b_gk], writes=[bck])
                    fw.dma(SP, ckt[:], gkt_s[base_tile:base_tile + TT, :].rearrange("(c p) f -> p c f", p=64), reads=[b_gkt], writes=[bckt])
                    fw.dma(SP, cvt[:], gvt_s[base_tile:base_tile + TT, :].rearrange("(c p) f -> p c f", p=64), reads=[b_gvt], writes=[bcvt])
                    fw.dma(SP, cG[:], gG_s[base_tile:base_tile + TT, d * 256:(d + 1) * 256].rearrange("(c p) f -> p c f", p=64), reads=[b_gG], writes=[bcG])
                    ot, bot = gl_o.get()
                    c_lo = (tt0 - base_tile) // 64
                    nch = min(ntok, TT) // 64
                    chunks = list(range(c_lo, c_lo + nch))
                    if d == 1:
                        chunks = chunks[::-1]
                    for c in chunks:
                        Gc = cG[:, c, :]
                        psb_, bpsb = psA.get()
                        for hh in range(4):
                            fw.op(PE, lambda h, hh=hh, Gc=Gc, p=psb_: h.matmul(p[0:64, hh * 64:(hh + 1) * 64], lhsT=Gc[:, hh * 64:(hh + 1) * 64], rhs=incl, start=True, stop=True),
                                  reads=[bcG, b_tri], writes=[bpsb], inc=(hh == 3))
                        ep, bep = sm_r.get(); en, ben = sm_r.get()
                        fw.op(ACT, lambda h, p=psb_, ep=ep: h.activation(out=ep[:, :], in_=p[0:64, 0:256], func=AF.Exp, scale=-1.0 / 16), reads=[bpsb], writes=[bep])
                        fw.op(ACT, lambda h, p=psb_, en=en: h.activation(out=en[:, :], in_=p[0:64, 0:256], func=AF.Exp, scale=1.0 / 16), reads=[bpsb], writes=[ben])
                        qt, bqt = smb_r.get(); kt_, bkt = smb_r.get()
                        fw.op(DVE, lambda h, c=c, qt=qt, ep=ep: h.scalar_tensor_tensor(out=qt[:, :].rearrange("p (h t) -> p h t", t=64), in0=cq[:, :, c * 64:(c + 1) * 64], scalar=0.125,
                                                                                      in1=ep[:, :].rearrange("p (h t) -> p h t", t=64), op0=ALU.mult, op1=ALU.mult),
                              reads=[bcq, bep], writes=[bqt])
                        fw.op(DVE, lambda h, c=c, kt_=kt_, en=en: h.tensor_tensor(out=kt_[:, :].rearrange("p (h t) -> p h t", t=64), in0=ck[:, :, c * 64:(c + 1) * 64],
                                                                                  in1=en[:, :].rearrange("p (h t) -> p h t", t=64), op=ALU.mult),
                              reads=[bck, ben], writes=[bkt])
                        ps2, bps2 = psA.get()
                        fw.op(PE, lambda h, Gc=Gc, p=ps2: h.matmul(p[0:64, 0:256], lhsT=strict, rhs=Gc, start=True, stop=True), reads=[bcG, b_tri], writes=[bps2])
                        e2, be2 = sm_r.get()
                        fw.op(ACT, lambda h, p=ps2, e2=e2: h.activation(out=e2[:, :], in_=p[0:64, 0:256], func=AF.Exp, scale=-1.0 / 16), reads=[bps2], writes=[be2])
                        kh, bkh = smb_r.get()
                        fw.op(DVE, lambda h, c=c, kh=kh, e2=e2: h.tensor_tensor(out=kh[:, :], in0=ckt[:, c, :], in1=e2[:, :], op=ALU.mult), reads=[bckt, be2], writes=[bkh])
                        ps3, bps3 = psA.get()
                        for hh in range(4):
                            fw.op(PE, lambda h, hh=hh, Gc=Gc, p=ps3: h.matmul(p[0:64, hh:hh + 1], lhsT=Gc[:, hh * 64:(hh + 1) * 64], rhs=ones_col[:, :], start=True, stop=True),
                                  reads=[bcG, b_onescol], writes=[bps3], inc=(hh == 3))
                        dd, bdd = dd_r.get()
                        fw.op(ACT, lambda h, p=ps3, dd=dd: h.activation(out=dd[:, :], in_=p[0:64, 0:4], func=AF.Exp, scale=-1.0 / 16), reads=[bps3], writes=[bdd])
                        ps4, bps4 = psA.get()
                        for hh in range(4):
                            fw.op(PE, lambda h, hh=hh, p=ps4, kt_=kt_, qt=qt: h.matmul(p[0:64, hh * 64:(hh + 1) * 64], lhsT=kt_[:, hh * 64:(hh + 1) * 64], rhs=qt[:, hh * 64:(hh + 1) * 64],
                                                                                     start=True, stop=True), reads=[bkt, bqt], writes=[bps4], inc=(hh == 3))
                        am, bam = smb_r.get()
                        fw.op(DVE, lambda h, p=ps4, am=am: h.tensor_tensor(out=am[:, :].rearrange("p (h t) -> p h t", t=64), in0=p[0:64, 0:256].rearrange("p (h t) -> p h t", t=64),
                                                                         in1=incl.unsqueeze(1).broadcast_to([64, 4, 64]), op=ALU.mult), reads=[bps4, b_tri], writes=[bam])
                        ps5, bps5 = psA.get()
                        for hh in range(4):
                            fw.op(PE, lambda h, hh=hh, c=c, p=ps5, am=am: h.matmul(p[:, hh * 64:(hh + 1) * 64], lhsT=cvt[:, c, hh * 128:(hh + 1) * 128], rhs=am[:, hh * 64:(hh + 1) * 64],
                                                                                 start=True, stop=False), reads=[bcvt, bam], writes=[bps5], inc=False)
                            fw.op(PE, lambda h, hh=hh, p=ps5, qt=qt: h.matmul(p[:, hh * 64:(hh + 1) * 64], lhsT=S_b[:, hh, :], rhs=qt[:, hh * 64:(hh + 1) * 64], start=False, stop=True),
                                  reads=[b_Sb, bqt], writes=[bps5], inc=(hh == 3))
                        fw.op(ACT, lambda h, c=c, p=ps5, ot=ot: h.activation(out=ot[:, :, c * 64:(c + 1) * 64], in_=p[:, 0:256].rearrange("p (h t) -> p h t", t=64), func=AF.Copy),
                              reads=[bps5], writes=[bot])
                        ps6, bps6 = psB.get()
                        for hh in range(4):
                            fw.op(PE, lambda h, hh=hh, c=c, p=ps6, kh=kh: h.matmul(p[0:64, hh * 128:(hh + 1) * 128], lhsT=kh[:, hh * 64:(hh + 1) * 64], rhs=cvt[:, c, hh * 128:(hh + 1) * 128],
                                                                                 start=True, stop=True), reads=[bkh, bcvt], writes=[bps6], inc=(hh == 3))
                        fw.op(DVE, lambda h, dd=dd: h.tensor_tensor(out=S_f[:], in0=S_f[:], in1=dd[:, :].unsqueeze(2).broadcast_to([64, 4, 128]), op=ALU.mult), reads=[bdd], writes=[b_S])
                        fw.op(DVE, lambda h, p=ps6: h.tensor_tensor(out=S_f[:], in0=S_f[:], in1=p[0:64, :].rearrange("p (h v) -> p h v", v=128), op=ALU.add), reads=[bps6], writes=[b_S])
                        fw.op(ACT, lambda h: h.activation(out=S_b[:], in_=S_f[:], func=AF.Copy), reads=[b_S], writes=[b_Sb])
                    cols = slice(tt0 - base_tile, tt0 - base_tile + min(ntok, TT))
                    ncol = min(ntok, TT)
                    if d == 0:
                        fw.dma(SP, go_s[:, :, tt0:tt0 + ncol], ot[:, :, cols], reads=[bot], writes=[b_go])
                    else:
                        of, bof = gl_of.get()
                        fw.dma(SP, of[:, :, 0:ncol], go_s[:, :, tt0:tt0 + ncol], reads=[b_go], writes=[bof])
                        rr, brr = gl_r.get()
                        fw.dma(SP, rr[:, :, 0:ncol], rs_s[:, :, tt0:tt0 + ncol], reads=[b_rs], writes=[brr])
                        fw.op(DVE, lambda h, ot=ot, of=of: h.tensor_tensor(out=of[:, :, 0:ncol], in0=of[:, :, 0:ncol], in1=ot[:, :, cols], op=ALU.add), reads=[bot], writes=[bof])
                        sq, bsq = sq_r.get()
                        fw.op(ACT, lambda h, sq=sq, of=of: h.activation(out=sq[:, 0:4, 0:ncol], in_=of[:, :, 0:ncol], func=AF.Square), reads=[bof], writes=[bsq])
                        ya, bya = gl_r.get()
                        for hh in range(4):
                            ps, bps = psA.get()
                            fw.op(PE, lambda h, hh=hh, ps=ps, sq=sq: h.matmul(ps[:, 0:ncol], lhsT=cmat[:, C_128, :], rhs=sq[:, hh, 0:ncol], start=True, stop=True), reads=[bsq, b_cmat], writes=[bps])
                            rs, brs = rs_r.get()
                            rsqrt_eps(ps[:, 0:ncol], rs[:, 0:ncol], bps, brs)
                            tmp, btmp = f32_r.get()
                            fw.op(DVE, lambda h, hh=hh, tmp=tmp, of=of, rs=rs: h.scalar_tensor_tensor(out=tmp[:, 0:ncol], in0=of[:, hh, 0:ncol], scalar=gc[:, 9:10], in1=rs[:, 0:ncol],
                                                                                                     op0=ALU.mult, op1=ALU.mult), reads=[bof, brs, b_gc], writes=[btmp])
                            fw.op(DVE, lambda h, hh=hh, tmp=tmp, ya=ya, rr=rr: h.tensor_tensor(out=ya[:, hh, 0:ncol], in0=tmp[:, 0:ncol], in1=rr[:, hh, 0:ncol], op=ALU.mult),
                                  reads=[btmp, brr], writes=[bya])
                        fw.dma(SP, ya_s[:, :, tt0:tt0 + ncol].rearrange("h p t -> p h t"), ya[:, :, 0:ncol], reads=[bya], writes=[b_ya])
                if seq_idx is not None:
                    ob = Buf(); out_bufs.append(ob)
                    fw.dma(SP, o_gla[l, seq_idx, d], S_f[:], reads=[b_S], writes=[ob])

        rs_s = dram_tmp("rs_s", [128, 4, T], BF16)
        b_rs = Buf()

        def phase1b(l, ti):
            cond = 1 if ti > 0 else 0
            xt, bxt = load_xT(ti)
            hT, bh = norm_mod(xt, bxt, A1, 0, cond)
            wt, bw = load_w8(w_in[l, :, O_AR:O_AR + 512], 512)
            rr, brr = gl_r.get()
            for hh in range(4):
                ps, bps = proj_fm(wt, bw, hh * 128, 128, lambda k: hT[:, k, :], [bh], 8)
                fw.op(ACT, lambda h, hh=hh, ps=ps: h.activation(out=rr[:, hh, :], in_=ps[:], func=AF.Silu), reads=[bps], writes=[brr])
            fw.dma(SP, rs_s[:, :, ti * TT:(ti + 1) * TT], rr[:], reads=[brr], writes=[b_rs])

        arena.reset()
        at_k = Ring(ov, "atk", [128, TKS], BF16, 2)
        at_v = Ring(ov, "atv", [128, TKS // 128, 128], BF16, 2)
        at_q = Ring(ov, "atq", [128, TT], BF16, 2)
        at_p = Ring(ov, "atp", [128, TT], BF16, 6)
        at_o = Ring(ov, "ato", [128, TT], BF16, 2)
        at_qm = [Ring(ov, "atqm0_", [128, TT], BF16, 2), Ring(ov, "atqm1_", [128, TT], BF16, 2)]
        at_l = Ring(ov, "atl", [128, TT], F32, 4)

        def mla_attn(l, groups):
            sc = 96 ** -0.5
            for i in range(2):
                fw.op(DVE, lambda h: h.memset(at_k.t[i][:], 0.0), writes=[at_k.b[i]])
                fw.op(DVE, lambda h: h.memset(at_q.t[i][:], 0.0), writes=[at_q.b[i]])
                for c2 in range(2):
                    fw.op(DVE, lambda h: h.memset(at_qm[c2].t[i][:], 0.0), writes=[at_qm[c2].b[i]])
            its = [(gi, head, qq) for gi, (q0, nq, k0, nk) in enumerate(groups) for head in range(8) for qq in range(q0, q0 + nq, TT)]
            kv_cache, q_cache = {}, {}

            def get_kv(gi, head):
                if (gi, head) not in kv_cache:
                    q0, nq, k0, nk = groups[gi]
                    kt, bkt = at_k.get(); vt, bvt = at_v.get()
                    fw.dma(SP, kt[0:96, 0:nk], k_s[head, :, k0:k0 + nk], reads=[b_k], writes=[bkt])
                    fw.dma(SP, vt[:, 0:nk // 128, :], v_s[head, :, k0 // 128:k0 // 128 + nk // 128, :], reads=[b_v], writes=[bvt])
                    kv_cache[(gi, head)] = (kt, bkt, vt, bvt)
                return kv_cache[(gi, head)]

            def get_q(idx):
                if idx not in q_cache:
                    gi, head, qq = its[idx]
                    nqt = min(TT, groups[gi][1])
                    qt, bqt = at_q.get()
                    fw.dma(SP, qt[0:96, 0:nqt], q_s[head, :, qq:qq + nqt], reads=[b_q], writes=[bqt])
                    q_cache[idx] = (qt, bqt)
                return q_cache[idx]

            for idx, (gi, head, qq) in enumerate(its):
                q0, nq, k0, nk = groups[gi]
                nkc = nk // 128
                if True:
                    kt, bkt, vt, bvt = get_kv(gi, head)
                    if True:
                        nqt = min(TT, nq)
                        qt, bqt = get_q(idx)
                        if idx + 1 < len(its):
                            get_kv(its[idx + 1][0], its[idx + 1][1])
                            get_q(idx + 1)
                        po, bpo = psB.get()

                        def qk(c, kt=kt, qt=qt, bkt=bkt, bqt=bqt, nqt=nqt):
                            ps, bps = psA.get()
                            fw.op(PE, lambda h: h.matmul(ps[:, 0:nqt], lhsT=kt[:, c * 128:(c + 1) * 128], rhs=qt[:, 0:nqt], start=True, stop=True),
                                  reads=[bkt, bqt], writes=[bps])
                            return ps, bps

                        LA = 2
                        pend = [qk(i) for i in range(min(LA, nkc))]
                        for c in range(nkc):
                            ps, bps = pend.pop(0)
                            if c + LA < nkc:
                                pend.append(qk(c + LA))
                            pt, bpt = at_p.get()
                            fw.op(ACT, lambda h, ps=ps, pt=pt: h.activation(out=pt[:, 0:nqt], in_=ps[:, 0:nqt], func=AF.Exp, scale=sc), reads=[bps], writes=[bpt])
                            fw.op(PE, lambda h, c=c, po=po, vt=vt, pt=pt: h.matmul(po[:, 0:nqt], lhsT=vt[:, c, :], rhs=pt[:, 0:nqt], start=(c == 0), stop=(c == nkc - 1)),
                                  reads=[bvt, bpt], writes=[bpo], inc=(c == nkc - 1))
                        rc, brc = f32_r.get()
                        fw.op(DVE, lambda h, po=po, rc=rc: h.reciprocal(out=rc[64:128, 0:nqt], in_=po[64:128, 0:nqt]), reads=[bpo], writes=[brc])
                        r2, br2 = f32_r.get()
                        fw.op(DVE, lambda h, rc=rc, r2=r2: h.tensor_copy(out=r2[0:64, 0:nqt], in_=rc[64:128, 0:nqt]), reads=[brc], writes=[br2])
                        ob, bob = at_o.get()
                        fw.op(DVE, lambda h, po=po, r2=r2, ob=ob: h.tensor_tensor(out=ob[0:64, 0:nqt], in0=po[0:64, 0:nqt], in1=r2[0:64, 0:nqt], op=ALU.mult), reads=[bpo, br2], writes=[bob])
                        fw.dma(SP, yb_s[head // 2, (head % 2) * 64:(head % 2) * 64 + 64, qq:qq + nqt], ob[0:64, 0:nqt], reads=[bob], writes=[b_yb])

        def diff_attn(l, groups, lam_init):
            sc = 64 ** -0.5
            its = [(gi, head, qq) for gi, (q0, nq, k0, nk) in enumerate(groups) for head in range(4) for qq in range(q0, q0 + nq, TT)]
            kv_cache, q_cache = {}, {}

            def get_kv(gi, head):
                if (gi, head) not in kv_cache:
                    q0, nq, k0, nk = groups[gi]
                    kt, bkt = at_k.get(); vt, bvt = at_v.get()
                    fw.dma(SP, kt[:, 0:nk], dk_s[head, :, k0:k0 + nk], reads=[b_dk], writes=[bkt])
                    fw.dma(SP, vt[:, 0:nk // 128, :], dv_s[head, :, k0 // 128:k0 // 128 + nk // 128, :], reads=[b_dv], writes=[bvt])
                    kv_cache[(gi, head)] = (kt, bkt, vt, bvt)
                return kv_cache[(gi, head)]

            def get_q(idx):
                if idx not in q_cache:
                    gi, head, qq = its[idx]
                    nqt = min(TT, groups[gi][1])
                    qm = [at_qm[0].get(), at_qm[1].get()]
                    fw.dma(SP, qm[0][0][0:64, 0:nqt], dq_s[head, 0:64, qq:qq + nqt], reads=[b_dq], writes=[qm[0][1]])
                    fw.dma(SP, qm[1][0][64:128, 0:nqt], dq_s[head, 64:128, qq:qq + nqt], reads=[b_dq], writes=[qm[1][1]])
                    q_cache[idx] = qm
                return q_cache[idx]

            for idx, (gi, head, qq) in enumerate(its):
                q0, nq, k0, nk = groups[gi]
                nkc = nk // 128
                if True:
                    kt, bkt, vt, bvt = get_kv(gi, head)
                    if True:
                        nqt = min(TT, nq)
                        qm = get_q(idx)
                        if idx + 1 < len(its):
                            get_kv(its[idx + 1][0], its[idx + 1][1])
                            get_q(idx + 1)
                        acc = [psB.get() for _ in range(3)]
                        lac = [at_l.get()]

                        def qk(i, kt=kt, bkt=bkt, nqt=nqt, qm=qm):
                            c, comp = divmod(i, 2)
                            ps, bps = psA.get()
                            fw.op(PE, lambda h: h.matmul(ps[:, 0:nqt], lhsT=kt[:, c * 128:(c + 1) * 128], rhs=qm[comp][0][:, 0:nqt], start=True, stop=True),
                                  reads=[bkt, qm[comp][1]], writes=[bps])
                            return ps, bps

                        LA = 2
                        pend = [qk(i) for i in range(min(LA, 2 * nkc))]
                        for c in range(nkc):
                            for comp in range(2):
                                ps, bps = pend.pop(0)
                                if c * 2 + comp + LA < 2 * nkc:
                                    pend.append(qk(c * 2 + comp + LA))
                                pt, bpt = at_p.get()
                                fw.op(ACT, lambda h: h.activation(out=pt[:, 0:nqt], in_=ps[:, 0:nqt], func=AF.Exp, scale=sc), reads=[bps], writes=[bpt])
                                po, bpo = acc[comp]
                                fw.op(PE, lambda h: h.matmul(po[:, 0:nqt], lhsT=vt[:, c, :], rhs=pt[:, 0:nqt], start=(c == 0), stop=(c == nkc - 1)),
                                      reads=[bvt, bpt], writes=[bpo], inc=(c == nkc - 1))
                                if comp == 0:
                                    la, bla = lac[0]
                                    if c == 0:
                                        fw.op(DVE, lambda h: h.tensor_copy(out=la[:, 0:nqt], in_=pt[:, 0:nqt]), reads=[bpt], writes=[bla])
                                    else:
                                        fw.op(DVE, lambda h: h.tensor_tensor(out=la[:, 0:nqt], in0=la[:, 0:nqt], in1=pt[:, 0:nqt], op=ALU.add), reads=[bpt], writes=[bla])
                                else:
                                    pl1, bpl1 = acc[2]
                                    fw.op(PE, lambda h: h.matmul(pl1[:, 0:nqt], lhsT=cmat[:, C_ONE, :], rhs=pt[:, 0:nqt], start=(c == 0), stop=(c == nkc - 1)),
                                          reads=[b_cmat, bpt], writes=[bpl1], inc=(c == nkc - 1))
                        rr_ = []
                        pl, bpl = psA.get()
                        fw.op(PE, lambda h: h.matmul(pl[:, 0:nqt], lhsT=onesf[:, :], rhs=lac[0][0][:, 0:nqt], start=True, stop=True), reads=[b_onesf, lac[0][1]], writes=[bpl])
                        for (pl_, bpl_) in ((pl, bpl), acc[2]):
                            rc_, brc_ = f32_r.get()
                            fw.op(DVE, lambda h: h.reciprocal(out=rc_[:, 0:nqt], in_=pl_[:, 0:nqt]), reads=[bpl_], writes=[brc_])
                            rr_.append((rc_, brc_))
                        (r0, br0), (r1, br1) = rr_
                        o0, bo0 = f32_r.get(); o1, bo1 = f32_r.get()
                        fw.op(DVE, lambda h, o0=o0, r0=r0, p=acc[0][0]: h.tensor_tensor(out=o0[:, 0:nqt], in0=p[:, 0:nqt], in1=r0[:, 0:nqt], op=ALU.mult), reads=[acc[0][1], br0], writes=[bo0])
                        fw.op(DVE, lambda h, o1=o1, r1=r1, p=acc[1][0]: h.tensor_tensor(out=o1[:, 0:nqt], in0=p[:, 0:nqt], in1=r1[:, 0:nqt], op=ALU.mult), reads=[acc[1][1], br1], writes=[bo1])
                        fw.op(DVE, lambda h, o0=o0, o1=o1: h.scalar_tensor_tensor(out=o0[:, 0:nqt], in0=o1[:, 0:nqt], scalar=lamc[:, 0:1], in1=o0[:, 0:nqt], op0=ALU.mult, op1=ALU.add),
                              reads=[bo1, b_lamc], writes=[bo0])
                        sq, bsq = bf_r.get()
                        fw.op(ACT, lambda h, sq=sq, o0=o0: h.activation(out=sq[:, 0:nqt], in_=o0[:, 0:nqt], func=AF.Square), reads=[bo0], writes=[bsq])
                        ps, bps = psA.get()
                        fw.op(PE, lambda h, ps=ps, sq=sq: h.matmul(ps[:, 0:nqt], lhsT=cmat[:, C_128, :], rhs=sq[:, 0:nqt], start=True, stop=True), reads=[bsq, b_cmat], writes=[bps])
                        rs, brs = rs_r.get()
                        rsqrt_eps(ps[:, 0:nqt], rs[:, 0:nqt], bps, brs)
                        t1, bt1 = f32_r.get()
                        fw.op(DVE, lambda h, t1=t1, o0=o0, rs=rs: h.scalar_tensor_tensor(out=t1[:, 0:nqt], in0=o0[:, 0:nqt], scalar=gc[:, 10:11], in1=rs[:, 0:nqt], op0=ALU.mult, op1=ALU.mult),
                              reads=[bo0, brs, b_gc], writes=[bt1])
                        ob, bob = at_o.get()
                        fw.op(ACT, lambda h, t1=t1, ob=ob: h.activation(out=ob[:, 0:nqt], in_=t1[:, 0:nqt], func=AF.Copy, scale=(1.0 - lam_init)), reads=[bt1], writes=[bob])
                        fw.dma(SP, yc_s[head, :, qq:qq + nqt], ob[:, 0:nqt], reads=[bob], writes=[b_yc])

        arena.reset()
        y3_r = Ring(ov, "y3", [128, 12, TT], BF16, 1)
        mg_r = Ring(ov, "mg", [128, 8, TT], BF16, 1)
        macc_r = Ring(ov, "macc", [128, 4, TT], F32, 1)
        aTraw = ov("aTraw", [128, 8192], F32)
        aT_view = aTraw.bitcast(BF16).rearrange("p (a b) -> p a b", b=TT)
        xo_view = aTraw[:, 0:4096].rearrange("p (g f) -> p g f", f=D)
        b_aTraw = Buf()

        def phase3(l, ti, last):
            cond = 1 if ti > 0 else 0
            t0 = ti * TT
            xt, bxt = load_xT(ti)
            hT, bh = norm_mod(xt, bxt, A1, 0, cond)
            y3, by3 = y3_r.get()
            for bi, (src, bsrc) in enumerate(((ya_s, b_ya), (yb_s, b_yb), (yc_s, b_yc))):
                fw.dma(SP, y3[:, bi * 4:(bi + 1) * 4, :], src[:, :, t0:t0 + TT].rearrange("h p t -> p h t"), reads=[bsrc], writes=[by3])
            mg, bmg = mg_r.get()
            macc, bmacc = macc_r.get()
            for half in range(2):
                for bi in range(3):
                    c0 = O_G + bi * 1024 + half * 512
                    wg, bwg = load_w8(w_in[l, :, c0:c0 + 512], 512)
                    wo, bwo = w4.get()
                    fw.dma(POOL, wo[:, :, 0:512], w_o3[l, bi, :, half * 512:(half + 1) * 512].rearrange("(k p) c -> p k c", p=128), writes=[bwo])
                    for j in range(4):
                        oc = half * 4 + j
                        psg, bpsg = proj_fm(wg, bwg, j * 128, 128, lambda k: hT[:, k, :], [bh], 8)
                        sg, bsg = f32_r.get()
                        fw.op(ACT, lambda h: h.activation(out=sg[:], in_=psg[:], func=AF.Sigmoid), reads=[bpsg], writes=[bsg])
                        pso, bpso = proj_fm(wo, bwo, j * 128, 128, lambda k: y3[:, bi * 4 + k, :], [by3], 4)
                        if bi == 0:
                            fw.op(DVE, lambda h: h.tensor_tensor(out=macc[:, j, :], in0=sg[:], in1=pso[:], op=ALU.mult), reads=[bsg, bpso], writes=[bmacc])
                        else:
                            tmp, btmp = f32_r.get()
                            fw.op(DVE, lambda h: h.tensor_tensor(out=tmp[:], in0=sg[:], in1=pso[:], op=ALU.mult), reads=[bsg, bpso], writes=[btmp])
                            if bi == 1:
                                fw.op(DVE, lambda h: h.tensor_tensor(out=macc[:, j, :], in0=macc[:, j, :], in1=tmp[:], op=ALU.add), reads=[btmp], writes=[bmacc])
                            else:
                                fw.op(DVE, lambda h: h.tensor_tensor(out=mg[:, oc, :], in0=macc[:, j, :], in1=tmp[:], op=ALU.add), reads=[btmp, bmacc], writes=[bmg])
            for half in range(2):
                wt, bw = load_w8(w_out[l, :, half * 512:(half + 1) * 512], 512)
                for j in range(4):
                    oc = half * 4 + j
                    ps, bps = proj_fm(wt, bw, j * 128, 128, lambda k: mg[:, k, :], [bmg], 8)
                    fw.op(DVE, lambda h, oc=oc, ps=ps: h.scalar_tensor_tensor(out=xt[:, oc, :], in0=ps[:], scalar=modT[:, 16 + oc, cond:cond + 1], in1=xt[:, oc, :], op0=ALU.mult, op1=ALU.add),
                          reads=[bps, b_modT_b], writes=[bxt])
            h2, bh2 = norm_mod(xt, bxt, A2, 24, cond)
            aT, baT = aT_view, b_aTraw
            for fb in range(8):
                wt, bw = load_w8(w_m1[l, :, fb * 512:(fb + 1) * 512], 512)
                for j in range(4):
                    ps, bps = proj_fm(wt, bw, j * 128, 128, lambda k: h2[:, k, :], [bh2], 8)
                    rl, brl = f32_r.get()
                    fw.op(ACT, lambda h, ps=ps, rl=rl: h.activation(out=rl[:], in_=ps[:], func=AF.Relu), reads=[bps], writes=[brl])
                    fw.op(DVE, lambda h, rl=rl, fb=fb, j=j: h.tensor_tensor(out=aT[:, fb * 4 + j, :], in0=rl[:], in1=rl[:], op=ALU.mult), reads=[brl], writes=[baT])
            for half in range(2):
                accs = [psB.get() for _ in range(4)]
                for kb8 in range(4):
                    wt, bw = load_w8(w_m2[l, kb8 * 1024:(kb8 + 1) * 1024, half * 512:(half + 1) * 512], 512)
                    for j in range(4):
                        ps, bps = accs[j]
                        for k in range(8):
                            kk = kb8 * 8 + k
                            fw.op(PE, lambda h: h.matmul(ps[:], lhsT=wt[:, k, j * 128:(j + 1) * 128], rhs=aT[:, kk, :], start=(kk == 0), stop=(kk == 31)),
                                  reads=[bw, baT], writes=[bps], inc=(k == 7))
                for j in range(4):
                    oc = half * 4 + j
                    ps, bps = accs[j]
                    fw.op(DVE, lambda h: h.scalar_tensor_tensor(out=xt[:, oc, :], in0=ps[:], scalar=modT[:, 40 + oc, cond:cond + 1], in1=xt[:, oc, :], op0=ALU.mult, op1=ALU.add),
                          reads=[bps, b_modT_b], writes=[bxt])
            if not last:
                fw.dma(SP, xT_s[:, :, t0:t0 + TT].rearrange("k p t -> p k t"), xt[:], reads=[bxt], writes=[b_xT[ti]])
            else:
                xo, bxo = xo_view, b_aTraw
                for g in range(4):
                    for half in range(2):
                        ps, bps = psA.get()
                        for k in range(4):
                            kk = half * 4 + k
                            fw.op(PE, lambda h, g=g, k=k, kk=kk, ps=ps: h.transpose(ps[:, k * 128:(k + 1) * 128], xt[:, kk, g * 128:(g + 1) * 128], ident[:]),
                                  reads=[bxt, b_ident], writes=[bps], inc=(k == 3))
                        evac(ps[:], xo[:, g, half * 512:(half + 1) * 512], bps, bxo, eng=(ACT if half else DVE))
                ob = Buf(); out_bufs.append(ob)
                fw.dma(SP, y_out[t0:t0 + TT, :].rearrange("(g p) f -> p g f", p=128), xo[:], reads=[bxo], writes=[ob])

        import os
        KSTOP = int(os.environ.get("KSTOP", "99"))
        step = [0]

        def reached():
            step[0] += 1
            return step[0] > KSTOP

        for l in range(L):
            if reached():
                break
            lam_init = layer_setup(l)
            fw.barrier()
            vaug_init[0] = vaug_init[1] = False
            if reached():
                break
            prep_cache(l)
            if reached():
                break
            for ti in range(NT):
                phase1(l, ti)
            fw.barrier()
            if reached():
                break
            seqs = [(0, 256, 0, False), (256, 256, 1, False), (NP_TOK, NS_TOK, None, True)]
            gla_dir(l, 0, seqs)
            fw.barrier()
            gla_dir(l, 1, seqs)
            fw.barrier()
            if reached():
                break
            groups = [(0, 256, 0, 256), (256, 256, 256, 256), (NP_TOK, NS_TOK, NP_TOK, TKS)]
            mla_attn(l, groups)
            if reached():
                break
            diff_attn(l, groups, lam_init)
            if reached():
                break
            fw.barrier()
            for ti in range(NT):
                phase3(l, ti, l == L - 1)
            fw.barrier()

        fw.finish(out_bufs)
    return nc


def _rope_tables():
    t = np.arange(NS_TOK)
    row, col = t // 64, t % 64

    def cs(nf, pos):
        inv = (10000.0 ** (-np.arange(nf, dtype=np.float32) / nf)).astype(np.float32)
        ang = pos.astype(np.float32)[None, :] * inv[:, None]
        c, s = np.cos(ang).astype(np.float32), np.sin(ang).astype(np.float32)
        return np.concatenate([c, c], 0), np.concatenate([s, s], 0)

    cr, sr = cs(8, row); cc, sc_ = cs(8, col)
    cM = np.concatenate([cr, cc, np.ones((64, NS_TOK), np.float32)], 0)
    sM = np.concatenate([sr, sc_, np.zeros((64, NS_TOK), np.float32)], 0)
    cr, sr = cs(16, row); cc, sc_ = cs(16, col)
    c64 = np.concatenate([cr, cc], 0); s64 = np.concatenate([sr, sc_], 0)
    cD = np.concatenate([c64, c64], 0); sD = np.concatenate([s64, s64], 0)
    return np.stack([cM, sM]).astype(np.float32), np.stack([cD, sD]).astype(np.float32)


def _rot_matrix(nf):
    R = np.zeros((2 * nf, 2 * nf), np.float32)
    for i in range(nf):
        R[i, nf + i] = -1.0
        R[nf + i, i] = 1.0
    return R


def _consts():
    bf = ml_dtypes.bfloat16
    cmat = np.zeros((128, 8, 128), np.float32)
    cmat[:, 0, :] = 1.0 / 1024
    cmat[:, 1, :] = 1.0 / 384
    cmat[:, 2, :] = 1.0 / 256
    cmat[:96, 3, :96] = 1.0 / 96
    cmat[:64, 4, :64] = 1.0 / 64
    cmat[64:, 4, 64:] = 1.0 / 64
    cmat[:, 5, :] = 1.0 / 128
    cmat[:, 6, :] = 1.0
    cmat[:32, 7, :32] = np.eye(32)
    pm = np.zeros((128, 3, 128), np.float32)
    R96 = np.zeros((96, 96), np.float32)
    R96[0:16, 0:16] = _rot_matrix(8); R96[16:32, 16:32] = _rot_matrix(8)
    pm[:96, 0, :96] = R96.T
    R64 = np.zeros((64, 64), np.float32)
    R64[0:32, 0:32] = _rot_matrix(16); R64[32:64, 32:64] = _rot_matrix(16)
    R128 = np.zeros((128, 128), np.float32)
    R128[:64, :64] = R64; R128[64:, 64:] = R64
    pm[:, 1, :] = R128.T
    i = np.arange(64)
    tri = np.zeros((64, 4, 64), np.float32)
    tri[:, 0, :] = (i[:, None] <= i[None, :])
    tri[:, 1, :] = (i[:, None] >= i[None, :])
    tri[:, 2, :] = (i[:, None] > i[None, :])
    tri[:, 3, :] = (i[:, None] < i[None, :])
    return cmat.astype(bf), pm.astype(bf), tri


_PROG = {}


def _get_prog():
    if "nc" not in _PROG:
        _PROG["nc"] = build_program()
    return _PROG["nc"]


def make_in_maps(x_prompt, x_sample, state_gla, cache_mla_ckv, cache_mla_krope, cache_diff_k, cache_diff_v, c, c_ctx, w_mod, b_mod, g_norm1, g_norm2, w_in,
                 w_gla_a2, b_gla_a, g_gla_out, g_mla_qa, g_mla_kva, w_mla_uq, w_mla_uk, w_mla_uv, g_mla_q, g_mla_k, g_diff_q, g_diff_k, lam_qk, g_diff_sub,
                 w_o_gla, w_o_mla, w_o_diff, w_out, w_mlp1, w_mlp2):
    f = lambda a: np.ascontiguousarray(np.asarray(a, dtype=np.float32))
    cmat, pm, tri = _consts()
    ropeM, ropeD = _rope_tables()
    perm = np.concatenate([np.arange(64, 96), np.arange(0, 64)])
    w_uq_p = f(w_mla_uq).reshape(L, 384, 8, 96)[:, :, :, perm]
    w_ukp = np.zeros((L, 256, 8, 96), np.float32)
    w_ukp[:, :, :, 32:] = f(w_mla_uk).reshape(L, 256, 8, 64)
    gcol = np.zeros((L, 128, 12), np.float32)
    gcol[:, :, 0:3] = f(g_mla_qa).reshape(L, 3, 128).transpose(0, 2, 1)
    gcol[:, :, 3:5] = f(g_mla_kva).reshape(L, 2, 128).transpose(0, 2, 1)
    gcol[:, :96, 5] = f(g_mla_q)[:, perm]
    gcol[:, :96, 6] = f(g_mla_k)[:, perm]
    gcol[:, :, 7] = np.tile(f(g_diff_q), (1, 2))
    gcol[:, :, 8] = np.tile(f(g_diff_k), (1, 2))
    gcol[:, :, 9] = f(g_gla_out)
    gcol[:, :, 10] = f(g_diff_sub)
    w_a2bd = np.zeros((L, 32, 512), np.float32)
    w_a2bd[:, 0:16, 0:256] = f(w_gla_a2)[:, 0]
    w_a2bd[:, 16:32, 256:512] = f(w_gla_a2)[:, 1]
    b_a = f(b_gla_a).reshape(L, 1, 512)
    gnT = np.concatenate([f(g_norm1).reshape(L, 8, 128).transpose(0, 2, 1), f(g_norm2).reshape(L, 8, 128).transpose(0, 2, 1)], axis=2)
    b_modT = f(b_mod).reshape(L, 48, 128).transpose(0, 2, 1)
    w_o3 = np.stack([f(w_o_gla), f(w_o_mla), f(w_o_diff)], axis=1)
    shared = dict(w_mod=f(w_mod), b_modT=f(b_modT), gnT=f(gnT), w_in=f(w_in), w_a2bd=w_a2bd, b_a=b_a, gcol=gcol, w_uq=f(w_uq_p), w_ukp=w_ukp,
                  w_uv=f(w_mla_uv), lam_qk=f(lam_qk).reshape(L, 1, 256), w_o3=f(w_o3), w_out=f(w_out), w_m1=f(w_mlp1), w_m2=f(w_mlp2),
                  ident=np.eye(128, dtype=np.float32), cmat=cmat, pmat=pm, tri=tri, ropeM=ropeM, ropeD=ropeD)
    xp, xs = f(x_prompt), f(x_sample)
    in_maps = []
    for core in range(8):
        b = core % 4
        m = dict(shared)
        m["xin"] = np.concatenate([xp[2 * core], xp[2 * core + 1], xs[b]], axis=0)
        cond = np.stack([f(c_ctx), f(c)[b]], axis=0)
        m["condT"] = f(cond.reshape(2, 8, 128).transpose(2, 1, 0))
        m["st_gla"] = f(f(state_gla)[b].transpose(0, 1, 3, 2, 4))
        m["c_ckv"] = f(cache_mla_ckv)[b]
        m["c_kr"] = f(cache_mla_krope)[b]
        m["c_dk"] = f(cache_diff_k)[b].reshape(L, PAST, 512)
        m["c_dv"] = f(cache_diff_v)[b].reshape(L, PAST, 512)
        in_maps.append(m)
    return in_maps


def kernel(**inputs):
    nc = _get_prog()
    in_maps = make_in_maps(**inputs)
    res = run_bass_kernel_spmd(nc, in_maps, core_ids=list(range(8)))
    R = res.results
    y_prompt = np.zeros((16, 256, D), np.float32)
    y_sample = np.zeros((4, NS_TOK, D), np.float32)
    n_gla = np.zeros((16, L, 2, 4, 64, 128), np.float32)
    n_ckv = np.zeros((16, L, 256, 256), np.float32)
    n_kr = np.zeros((16, L, 256, 32), np.float32)
    n_dk = np.zeros((16, L, 256, 4, 2, 64), np.float32)
    n_dv = np.zeros((16, L, 256, 4, 128), np.float32)
    for core in range(8):
        r = R[core]
        for s in range(2):
            bi = 2 * core + s
            y_prompt[bi] = r["y_out"][s * 256:(s + 1) * 256]
            n_gla[bi] = r["o_gla"][:, s].transpose(0, 1, 3, 2, 4)
            n_ckv[bi] = r["o_ckv"][:, s * 256:(s + 1) * 256]
            n_kr[bi] = r["o_kr"][:, s * 256:(s + 1) * 256]
            n_dk[bi] = r["o_dk"][:, s * 256:(s + 1) * 256].reshape(L, 256, 4, 2, 64)
            n_dv[bi] = r["o_dv"][:, s * 256:(s + 1) * 256].reshape(L, 256, 4, 128)
        if core < 4:
            y_sample[core] = r["y_out"][NP_TOK:]
    return (y_prompt, y_sample, n_gla, n_ckv, n_kr, n_dk, n_dv)
```

```python
import math
import numpy as np
import ml_dtypes
from contextlib import ExitStack
import concourse.bass as bass
import concourse.mybir as mybir
from concourse.bass_utils import run_bass_kernel_spmd

F32 = mybir.dt.float32
BF16 = mybir.dt.bfloat16
AF = mybir.ActivationFunctionType
ALU = mybir.AluOpType

D = 1024
L = 2
NP_TOK = 512
NS_TOK = 4096
T = NP_TOK + NS_TOK
TT = 512
NT = T // TT
PAST = 256
TKS = PAST + NS_TOK
EPS = 1e-6
O_AQ, O_AK, O_AV, O_AR, O_AA, O_QD, O_KVD, O_KR, O_DQ, O_DK, O_DV, O_G = 0, 256, 512, 1024, 1536, 1568, 1952, 2208, 2240, 2752, 3264, 3776
IN_COLS = 6848


class Buf:
    __slots__ = ("lw", "rd", "psum")

    def __init__(self, psum=False):
        self.lw = None
        self.rd = []
        self.psum = psum


class _Rec:
    def __init__(self):
        self.call = None

    def __getattr__(self, name):
        def f(*a, **k):
            self.call = (name, a, k)
            return self
        return f


class Eng:
    def __init__(self, fw, name, handle):
        self.name = name
        self.h = handle
        self.sem = fw.new_sem("e_" + name)
        self.count = 0
        self.waited = {}
        self.thunks = []
        self.snaps = {}


class FW:
    def __init__(self, nc, ctx, n_dma_sems=48):
        self.nc = nc
        self.ctx = ctx
        self.pe = Eng(self, "pe", nc.tensor)
        self.dve = Eng(self, "dve", nc.vector)
        self.act = Eng(self, "act", nc.scalar)
        self.pool = Eng(self, "pool", nc.gpsimd)
        self.sp = Eng(self, "sp", nc.sync)
        self.dma_sems = [self.new_sem(f"d{i}") for i in range(n_dma_sems)]
        self.dma_cnt = [0] * n_dma_sems
        self.snap_of = {}
        self.dma_rr = 0
        self.ew_rr = 0

    def new_sem(self, name):
        return self.ctx.enter_context(self.nc.semaphore(name))

    def _wait(self, E, ev):
        if ev is None:
            return
        sem, val = ev
        if sem is E.sem and E.name == "pe":
            return
        key = id(sem)
        if E.waited.get(key, 0) >= val:
            return
        E.waited[key] = val
        E.thunks.append(lambda h=E.h, s=sem, v=val: h.wait_ge(s, v))
        snap = self.snap_of.get((key, val))
        if snap:
            w = E.waited
            for k2, v2 in snap.items():
                if w.get(k2, 0) < v2:
                    w[k2] = v2

    def _deps(self, E, reads, writes):
        for b in reads:
            self._wait(E, b.lw)
        for b in writes:
            self._wait(E, b.lw)
            for ev in b.rd:
                self._wait(E, ev)

    def _post(self, ev, reads, writes):
        for b in reads:
            b.rd.append(ev)
            if len(b.rd) > 64:
                b.rd = b.rd[-48:]
        for b in writes:
            b.lw = ev
            b.rd = []

    def op(self, E, fn, reads=(), writes=(), inc=True):
        if any(b.psum for b in reads):
            writes = list(writes) + [b for b in reads if b.psum]
            reads = [b for b in reads if not b.psum]
        self._deps(E, reads, writes)
        rec = _Rec()
        fn(rec)
        mname, margs, mkw = rec.call
        if inc:
            E.count += 1
            ev = (E.sem, E.count)
            self.snap_of[(id(E.sem), E.count)] = dict(E.waited)
            E.thunks.append(lambda h=E.h, n=mname, a=margs, k=mkw, s=E.sem: getattr(h, n)(*a, **k).then_inc(s, 1))
        else:
            ev = (E.sem, E.count + 1)
            E.thunks.append(lambda h=E.h, n=mname, a=margs, k=mkw: getattr(h, n)(*a, **k))
        self._post(ev, reads, writes)
        return ev

    def dma(self, Q, out_ap, in_ap, reads=(), writes=(), **kw):
        self._deps(Q, reads, writes)
        i = self.dma_rr
        self.dma_rr = (self.dma_rr + 1) % len(self.dma_sems)
        self.dma_cnt[i] += 16
        sem = self.dma_sems[i]
        ev = (sem, self.dma_cnt[i])
        self.snap_of[(id(sem), self.dma_cnt[i])] = dict(Q.waited)
        Q.thunks.append(lambda h=Q.h, o=out_ap, a=in_ap, s=sem, k=kw: h.dma_start(out=o, in_=a, **k).then_inc(s, 16))
        self._post(ev, reads, writes)
        return ev

    def barrier(self):
        engs = [self.pe, self.dve, self.act, self.pool, self.sp]
        for E in engs:
            for E2 in engs:
                if E2 is not E and E2.count > 0:
                    self._wait(E, (E2.sem, E2.count))
            for i, sem in enumerate(self.dma_sems):
                if self.dma_cnt[i] > 0:
                    self._wait(E, (sem, self.dma_cnt[i]))

    def finish(self, final_bufs):
        for b in final_bufs:
            self._wait(self.sp, b.lw)
        nc = self.nc
        engs = self
        with nc.Block() as block:
            @block.tensor
            def _(e):
                for t in engs.pe.thunks:
                    t()

            @block.vector
            def _(e):
                for t in engs.dve.thunks:
                    t()

            @block.scalar
            def _(e):
                for t in engs.act.thunks:
                    t()

            @block.gpsimd
            def _(e):
                for t in engs.pool.thunks:
                    t()

            @block.sync
            def _(e):
                for t in engs.sp.thunks:
                    t()


class Arena:
    def __init__(self, tensor, nbytes):
        self.t = tensor
        self.nbytes = nbytes
        self.off = 0
        self.peak = 0

    def reset(self):
        self.off = 0

    def alloc(self, name, shape, dt):
        esz = 2 if dt == BF16 else 4
        n = 1
        for d in shape[1:]:
            n *= d
        nb = (n * esz + 63) // 64 * 64
        assert self.off + nb <= self.nbytes, (name, self.off, nb, self.nbytes)
        ap = self.t[0:shape[0], self.off // 4:(self.off + nb) // 4]
        self.off += nb
        self.peak = max(self.peak, self.off)
        if dt == BF16:
            ap = ap.bitcast(BF16)
        ap = ap[:, 0:n]
        if len(shape) == 3:
            ap = ap.rearrange("p (a b) -> p a b", b=shape[2])
        elif len(shape) == 4:
            ap = ap.rearrange("p (a b c) -> p a b c", b=shape[2], c=shape[3])
        return ap


class Ring:
    def __init__(self, alloc, name, shape, dt, n, psum=False):
        self.t = [alloc(f"{name}{i}", shape, dt) for i in range(n)]
        self.b = [Buf(psum) for _ in range(n)]
        self.i = 0

    def get(self):
        i = self.i
        self.i = (i + 1) % len(self.t)
        return self.t[i], self.b[i]


def build_program(debug=False):
    nc = bass.Bass("TRN2", target_bir_lowering=False)
    dram_in = lambda name, shape, dt=F32: nc.dram_tensor(name, list(shape), dt, kind="ExternalInput").ap()
    dram_out = lambda name, shape, dt=F32: nc.dram_tensor(name, list(shape), dt, kind="ExternalOutput").ap()
    dram_tmp = lambda name, shape, dt=F32: nc.dram_tensor(name, list(shape), dt).ap()

    xin = dram_in("xin", [T, D])
    condT = dram_in("condT", [128, 8, 2])
    st_gla = dram_in("st_gla", [L, 2, 64, 4, 128])
    c_ckv = dram_in("c_ckv", [L, PAST, 256])
    c_kr = dram_in("c_kr", [L, PAST, 32])
    c_dk = dram_in("c_dk", [L, PAST, 512])
    c_dv = dram_in("c_dv", [L, PAST, 512])
    w_mod = dram_in("w_mod", [L, D, 6 * D])
    b_modT = dram_in("b_modT", [L, 128, 48])
    gnT = dram_in("gnT", [L, 128, 16])
    w_in = dram_in("w_in", [L, D, IN_COLS])
    w_a2bd = dram_in("w_a2bd", [L, 32, 512])
    b_a = dram_in("b_a", [L, 1, 512])
    gcol = dram_in("gcol", [L, 128, 12])
    w_uq = dram_in("w_uq", [L, 384, 8, 96])
    w_ukp = dram_in("w_ukp", [L, 256, 8, 96])
    w_uv = dram_in("w_uv", [L, 256, 512])
    lam_qk = dram_in("lam_qk", [L, 1, 256])
    w_o3 = dram_in("w_o3", [L, 3, 512, D])
    w_out = dram_in("w_out", [L, D, D])
    w_m1 = dram_in("w_m1", [L, D, 4 * D])
    w_m2 = dram_in("w_m2", [L, 4 * D, D])
    ident_d = dram_in("ident", [128, 128])
    cmat_d = dram_in("cmat", [128, 8, 128], BF16)
    pmat_d = dram_in("pmat", [128, 3, 128], BF16)
    tri_d = dram_in("tri", [64, 4, 64])
    ropeM = dram_in("ropeM", [2, 96, NS_TOK])
    ropeD = dram_in("ropeD", [2, 128, NS_TOK])

    y_out = dram_out("y_out", [T, D])
    o_gla = dram_out("o_gla", [L, 2, 2, 64, 4, 128])
    o_ckv = dram_out("o_ckv", [L, NP_TOK, 256])
    o_kr = dram_out("o_kr", [L, NP_TOK, 32])
    o_dk = dram_out("o_dk", [L, NP_TOK, 512])
    o_dv = dram_out("o_dv", [L, NP_TOK, 512])

    mk = dram_out if debug else dram_tmp
    xT_s = dram_tmp("xT_s", [8, 128, T])
    gq_s = dram_tmp("gq_s", [64, 4, T], BF16)
    gk_s = dram_tmp("gk_s", [64, 4, T], BF16)
    gkt_s = dram_tmp("gkt_s", [T, 256], BF16)
    gvt_s = dram_tmp("gvt_s", [T, 512], BF16)
    gG_s = dram_tmp("gG_s", [T, 512])
    go_s = dram_tmp("go_s", [128, 4, T])
    q_s = dram_tmp("q_s", [8, 96, T], BF16)
    k_s = dram_tmp("k_s", [8, 96, NP_TOK + TKS], BF16)
    v_s = dram_tmp("v_s", [8, 128, (NP_TOK + TKS) // 128, 128], BF16)
    dq_s = dram_tmp("dq_s", [4, 128, T], BF16)
    dk_s = dram_tmp("dk_s", [4, 128, NP_TOK + TKS], BF16)
    dv_s = dram_tmp("dv_s", [4, 128, (NP_TOK + TKS) // 128, 128], BF16)
    ya_s = mk("ya_s", [4, 128, T], BF16)
    yb_s = mk("yb_s", [4, 128, T], BF16)
    yc_s = mk("yc_s", [4, 128, T], BF16)

    with ExitStack() as ctx:
        fw = FW(nc, ctx)
        PE, DVE, ACT, POOL, SP = fw.pe, fw.dve, fw.act, fw.pool, fw.sp
        sb = lambda name, shape, dt=F32: ctx.enter_context(nc.sbuf_tensor("s_" + name, list(shape), dt))
        psb = lambda name, shape, dt=F32: ctx.enter_context(nc.psum_tensor("p_" + name, list(shape), dt))

        psA = Ring(psb, "psA", [128, 512], F32, 4, psum=True)
        psB = Ring(psb, "psB", [128, 512], F32, 4, psum=True)

        OVB = 72 * 1024
        arena = Arena(sb("arena", [128, OVB // 4], F32), OVB)
        ov = arena.alloc

        def ew():
            fw.ew_rr ^= 1
            return DVE if fw.ew_rr else POOL

        ident = sb("ident", [128, 128]); b_ident = Buf()
        cmat = sb("cmat", [128, 8, 128], BF16); b_cmat = Buf()
        pmat = sb("pmat", [128, 3, 128], BF16); b_pmat = Buf()
        tri = sb("tri", [64, 4, 64]); b_tri = Buf()
        ones_row = sb("ones_row", [1, 128], BF16); b_onesrow = Buf()
        ones_rowf = sb("ones_rowf", [1, 128]); b_onesrowf = Buf()
        ones_col = sb("ones_col", [64, 1]); b_onescol = Buf()
        onesf = sb("onesf", [128, 128]); b_onesf = Buf()
        fw.dma(SP, ident[:], ident_d, writes=[b_ident])
        fw.dma(SP, cmat[:], cmat_d, writes=[b_cmat])
        fw.dma(SP, pmat[:], pmat_d, writes=[b_pmat])
        fw.dma(SP, tri[:], tri_d, writes=[b_tri])
        fw.op(DVE, lambda h: h.memset(ones_row[:], 1.0), writes=[b_onesrow])
        fw.op(DVE, lambda h: h.memset(ones_rowf[:], 1.0), writes=[b_onesrowf])
        fw.op(DVE, lambda h: h.memset(ones_col[:], 1.0), writes=[b_onescol])
        fw.op(DVE, lambda h: h.memset(onesf[:], 1.0), writes=[b_onesf])
        C_1024, C_384, C_256, C_96, C_BLK64, C_128, C_ONE, C_SEL32 = range(8)
        P_96, P_128, _ = range(3)

        cT = sb("cT", [128, 8, 2]); b_cT = Buf()
        scT = sb("scT", [128, 8, 2]); b_scT = Buf()
        fw.dma(SP, cT[:], condT, writes=[b_cT])
        fw.op(ACT, lambda h: h.activation(out=scT[:], in_=cT[:], func=AF.Silu), reads=[b_cT], writes=[b_scT])

        modT = sb("modT", [128, 48, 2]); b_modT_b = Buf()
        bmod = sb("bmod", [128, 48]); b_bmod = Buf()
        gn = sb("gn", [128, 16]); b_gn = Buf()
        A1 = sb("A1", [128, 8, 2]); A2 = sb("A2", [128, 8, 2]); b_A = Buf()
        gc = sb("gc", [128, 12]); b_gc = Buf()
        wa2 = sb("wa2", [32, 512], BF16); b_wa2 = Buf()
        ba = sb("ba", [1, 512], BF16); b_ba = Buf()
        lamt = sb("lamt", [1, 256]); b_lamt = Buf()
        lam1 = sb("lam1", [1, 8]); b_lam1 = Buf()
        lamc = sb("lamc", [128, 2]); b_lamc = Buf()
        arena.reset()
        wmod_r = Ring(ov, "wmod", [128, 8, 256], F32, 2)

        w8 = Ring(sb, "w8", [128, 8, 512], BF16, 3)
        w4 = Ring(sb, "w4", [128, 4, 1024], BF16, 2)

        def load_w8(src_ap_rows_cols, ncols, nk=8):
            t, b = w8.get()
            fw.dma(POOL, t[:, 0:nk, 0:ncols], src_ap_rows_cols.rearrange("(k p) c -> p k c", p=128), writes=[b])
            return t, b

        xt_r = Ring(sb, "xt", [128, 8, TT], F32, 1)
        sq_r = Ring(sb, "sq", [128, 8, TT], BF16, 1)
        hT_r = Ring(sb, "hT", [128, 8, TT], BF16, 2)
        rs_r = Ring(sb, "rs", [128, TT], F32, 4)
        f32_r = Ring(sb, "f32", [128, TT], F32, 12)
        bf_r = Ring(sb, "bf", [128, TT], BF16, 12)

        def evac(ps_ap, out_ap, b_ps, b_out, eng=None, func=None, scale=1.0):
            if func is not None or eng is ACT:
                f = func if func is not None else AF.Copy
                return fw.op(ACT, lambda h: h.activation(out=out_ap, in_=ps_ap, func=f, scale=scale), reads=[b_ps], writes=[b_out])
            return fw.op(DVE, lambda h: h.tensor_copy(out=out_ap, in_=ps_ap), reads=[b_ps], writes=[b_out])

        b_xT = [Buf() for _ in range(NT)]
        arena.reset()
        xtok_r = Ring(ov, "xtok", [128, 4, D], F32, 2)

        def phase0(ti):
            xk, bxk = xtok_r.get()
            fw.dma(SP, xk[:], xin[ti * TT:(ti + 1) * TT, :].rearrange("(g p) f -> p g f", p=128), writes=[bxk])
            xt, bxt = xt_r.get()
            for k in range(8):
                ps, bps = psA.get()
                for g in range(4):
                    fw.op(PE, lambda h, ps=ps, g=g, k=k: h.transpose(ps[:, g * 128:(g + 1) * 128], xk[:, g, k * 128:(k + 1) * 128], ident[:]),
                          reads=[bxk, b_ident], writes=[bps], inc=(g == 3))
                evac(ps[:], xt[:, k, :], bps, bxt, eng=(ACT if k % 2 else DVE))
            fw.dma(SP, xT_s[:, :, ti * TT:(ti + 1) * TT].rearrange("k p t -> p k t"), xt[:], reads=[bxt], writes=[b_xT[ti]])

        import os
        for ti in range(int(os.environ.get("KPH0", str(NT)))):
            phase0(ti)
        fw.barrier()

        def rsqrt_eps(src_ap, dst_ap, bsrc, bdst):
            fw.op(ACT, lambda h: h.activation(out=dst_ap, in_=src_ap, func=AF.Ln, bias=EPS), reads=[bsrc], writes=[bdst])
            fw.op(ACT, lambda h: h.activation(out=dst_ap, in_=dst_ap, func=AF.Exp, scale=-0.5), reads=[bdst], writes=[bdst])

        def rms_rstd(sq_aps, ones_ap, nparts, eps=EPS):
            ps, bps = psA.get()
            n = len(sq_aps)
            for i, (a, b) in enumerate(sq_aps):
                fw.op(PE, lambda h, a=a, i=i: h.matmul(ps[0:nparts, :], lhsT=ones_ap, rhs=a, start=(i == 0), stop=(i == n - 1)),
                      reads=[b, b_cmat], writes=[bps], inc=(i == n - 1))
            rs, brs = rs_r.get()
            rsqrt_eps(ps[0:nparts, :], rs[0:nparts, :], bps, brs)
            return rs, brs

        def load_xT(ti):
            xt, bxt = xt_r.get()
            fw.dma(SP, xt[:], xT_s[:, :, ti * TT:(ti + 1) * TT].rearrange("k p t -> p k t"), reads=[b_xT[ti]], writes=[bxt])
            return xt, bxt

        def norm_mod(xt, bxt, Amod, shift_lo, cond):
            sq, bsq = sq_r.get()
            fw.op(ACT, lambda h: h.activation(out=sq[:], in_=xt[:], func=AF.Square), reads=[bxt], writes=[bsq])
            rs, brs = rms_rstd([(sq[:, k, :], bsq) for k in range(8)], cmat[:, C_1024, :], 128)
            hT, bh = hT_r.get()
            for k in range(8):
                tmp, btmp = f32_r.get()
                e = DVE
                fw.op(e, lambda h, k=k, tmp=tmp: h.tensor_tensor(out=tmp[:], in0=xt[:, k, :], in1=rs[:], op=ALU.mult), reads=[bxt, brs], writes=[btmp])
                fw.op(ACT, lambda h, k=k, tmp=tmp: h.activation(out=hT[:, k, :], in_=tmp[:], func=AF.Identity,
                                                               scale=Amod[:, k, cond:cond + 1], bias=modT[:, shift_lo + k, cond:cond + 1]),
                      reads=[btmp, b_A, b_modT_b], writes=[bh])
            return hT, bh

        def proj_fm(wt, bw, col0, m, rhs_fn, rhs_bufs, nk, out_parts=None):
            ps, bps = psA.get()
            for k in range(nk):
                fw.op(PE, lambda h, k=k: h.matmul(ps[0:m, :], lhsT=wt[:, k, col0:col0 + m], rhs=rhs_fn(k), start=(k == 0), stop=(k == nk - 1)),
                      reads=[bw] + rhs_bufs, writes=[bps], inc=(k == nk - 1))
            return ps, bps

        def norm_rope_gen(ps, bps, npart, ones_ap, gcolumn, rope, pm_idx, ropetab, dst_ap, dst_buf, keep_f32=None):
            xf, bxf = f32_r.get()
            evac(ps[0:npart, :], xf[0:npart, :], bps, bxf, eng=DVE)
            sq, bsq = bf_r.get()
            fw.op(ACT, lambda h: h.activation(out=sq[0:npart, :], in_=xf[0:npart, :], func=AF.Square), reads=[bxf], writes=[bsq])
            yield
            rs, brs = rms_rstd([(sq[0:npart, :], bsq)], ones_ap, npart)
            xn, bxn = f32_r.get()
            fw.op(DVE, lambda h: h.scalar_tensor_tensor(out=xn[0:npart, :], in0=xf[0:npart, :], scalar=gc[0:npart, gcolumn:gcolumn + 1], in1=rs[0:npart, :],
                                                        op0=ALU.mult, op1=ALU.mult), reads=[bxf, brs, b_gc], writes=[bxn])
            if keep_f32 is not None:
                keep_f32(xn, bxn)
            ob, bob = bf_r.get()
            if not rope:
                fw.op(ACT, lambda h: h.activation(out=ob[0:npart, :], in_=xn[0:npart, :], func=AF.Copy), reads=[bxn], writes=[bob])
            else:
                ctab, stab, btab = ropetab
                xb, bxb = bf_r.get()
                fw.op(ACT, lambda h: h.activation(out=xb[0:npart, :], in_=xn[0:npart, :], func=AF.Copy), reads=[bxn], writes=[bxb])
                t1, bt1 = f32_r.get()
                fw.op(DVE, lambda h: h.tensor_tensor(out=t1[0:npart, :], in0=xn[0:npart, :], in1=ctab, op=ALU.mult), reads=[bxn, btab], writes=[bt1])
                yield
                ps2, bps2 = psA.get()
                fw.op(PE, lambda h: h.matmul(ps2[0:npart, :], lhsT=pmat[0:npart, pm_idx, 0:npart], rhs=xb[0:npart, :], start=True, stop=True),
                      reads=[bxb, b_pmat], writes=[bps2])
                t2, bt2 = f32_r.get()
                fw.op(DVE, lambda h: h.tensor_tensor(out=t2[0:npart, :], in0=ps2[0:npart, :], in1=stab, op=ALU.mult), reads=[bps2, btab], writes=[bt2])
                fw.op(DVE, lambda h: h.tensor_tensor(out=ob[0:npart, :], in0=t1[0:npart, :], in1=t2[0:npart, :], op=ALU.add), reads=[bt1, bt2], writes=[bob])
            fw.dma(SP, dst_ap, ob[0:npart, :], reads=[bob], writes=[dst_buf])

        def norm_rope_store(*a, **k):
            for _ in norm_rope_gen(*a, **k):
                pass

        class Pipe:
            def __init__(self):
                self.gens = []

            def push(self, gen):
                next(gen, None)
                for og in list(self.gens):
                    try:
                        next(og)
                    except StopIteration:
                        self.gens.remove(og)
                self.gens.append(gen)

            def drain(self):
                while self.gens:
                    for og in list(self.gens):
                        try:
                            next(og)
                        except StopIteration:
                            self.gens.remove(og)

        def transpose_out(src_fn, src_bufs, nfeat_parts, ncols_total, dst_rows_fn, colslices):
            for g in range(4):
                ps, bps = psA.get()
                n = len(colslices)
                for i, (j, pj, c0) in enumerate(colslices):
                    fw.op(PE, lambda h, j=j, pj=pj, c0=c0, g=g: h.transpose(ps[:, c0:c0 + pj], src_fn(j)[:, g * 128:(g + 1) * 128], ident[0:pj, 0:pj]),
                          reads=src_bufs + [b_ident], writes=[bps], inc=(i == n - 1))
                o, bo = f32_r.get()
                evac(ps[:, 0:ncols_total], o[:, 0:ncols_total], bps, bo, eng=DVE)
                fw.dma(SP, dst_rows_fn(g), o[:, 0:ncols_total], reads=[bo], writes=[Buf()])

        b_gq = Buf(); b_gk = Buf(); b_gkt = Buf(); b_gvt = Buf(); b_gG = Buf(); b_go = Buf()
        b_q = Buf(); b_k = Buf(); b_v = Buf(); b_dq = Buf(); b_dk = Buf(); b_dv = Buf()
        b_ya = Buf(); b_yb = Buf(); b_yc = Buf()
        out_bufs = []

        arena.reset()
        ropM_r = Ring(ov, "ropM", [96, 2, TT], F32, 1)
        ropD_r = Ring(ov, "ropD", [128, 2, TT], F32, 1)
        vaug_r = Ring(ov, "vaug", [128, 8, 4, 128], BF16, 1)
        tok_r = Ring(ov, "tok", [128, 4, 512], BF16, 2)
        tokf_r = Ring(ov, "tokf", [128, 4, 512], F32, 1)
        keep_r = Ring(ov, "keep", [128, 4, TT], F32, 1)
        qdn_r = Ring(ov, "qdn", [128, 3, TT], BF16, 1)
        ckv_r = Ring(ov, "ckv", [128, 2, TT], BF16, 1)
        ckvf_r = Ring(ov, "ckvf", [128, 2, TT], F32, 1)
        krb_r = Ring(ov, "krb", [32, TT], BF16, 1)
        krf_r = Ring(ov, "krf", [32, TT], F32, 1)
        aab_r = Ring(ov, "aab", [32, TT], BF16, 1)
        g4_r = Ring(ov, "g4", [64, 4, TT], BF16, 1)
        r4_r = Ring(ov, "r4", [128, 4, TT], BF16, 1)
        ctok_r = Ring(ov, "ctok", [128, 2, 512], F32, 2)

        def layer_setup(l):
            fw.dma(SP, bmod[:], b_modT[l], writes=[b_bmod])
            fw.dma(SP, gn[:], gnT[l], writes=[b_gn])
            fw.dma(SP, gc[:], gcol[l], writes=[b_gc])
            fw.dma(POOL, wa2[:], w_a2bd[l], writes=[b_wa2])
            fw.dma(POOL, ba[:], b_a[l], writes=[b_ba])
            fw.dma(SP, lamt[:], lam_qk[l], writes=[b_lamt])
            for cb in range(24):
                wm, bwm = wmod_r.get()
                fw.dma(SP, wm[:], w_mod[l, :, cb * 256:(cb + 1) * 256].rearrange("(k p) c -> p k c", p=128), writes=[bwm])
                ps, bps = psA.get()
                for j in range(2):
                    for k in range(8):
                        fw.op(PE, lambda h, j=j, k=k, wm=wm, ps=ps: h.matmul(ps[:, j * 2:j * 2 + 2], lhsT=wm[:, k, j * 128:(j + 1) * 128], rhs=scT[:, k, :],
                                                                          start=(k == 0), stop=(k == 7)),
                              reads=[bwm, b_scT], writes=[bps], inc=(j == 1 and k == 7))
                fw.op(DVE, lambda h, cb=cb, ps=ps: h.tensor_tensor(out=modT[:, cb * 2:(cb + 1) * 2, :], in0=ps[:, 0:4].rearrange("p (j c) -> p j c", c=2),
                                                                 in1=bmod[:, cb * 2:(cb + 1) * 2].unsqueeze(2).broadcast_to([128, 2, 2]), op=ALU.add),
                      reads=[bps, b_bmod], writes=[b_modT_b])
            fw.op(DVE, lambda h: h.scalar_tensor_tensor(out=A1[:], in0=modT[:, 8:16, :], scalar=1.0, in1=gn[:, 0:8].unsqueeze(2).broadcast_to([128, 8, 2]),
                                                        op0=ALU.add, op1=ALU.mult), reads=[b_modT_b, b_gn], writes=[b_A])
            fw.op(DVE, lambda h: h.scalar_tensor_tensor(out=A2[:], in0=modT[:, 32:40, :], scalar=1.0, in1=gn[:, 8:16].unsqueeze(2).broadcast_to([128, 8, 2]),
                                                        op0=ALU.add, op1=ALU.mult), reads=[b_modT_b, b_gn], writes=[b_A])
            lam_init = 0.8 - 0.6 * math.exp(-0.3 * l)
            fw.op(DVE, lambda h: h.tensor_tensor(out=lamt[:, 0:64], in0=lamt[:, 0:64], in1=lamt[:, 64:128], op=ALU.mult), reads=[b_lamt], writes=[b_lamt])
            fw.op(DVE, lambda h: h.tensor_tensor(out=lamt[:, 128:192], in0=lamt[:, 128:192], in1=lamt[:, 192:256], op=ALU.mult), reads=[b_lamt], writes=[b_lamt])
            fw.op(DVE, lambda h: h.reduce_sum(out=lam1[:, 0:1], in_=lamt[:, 0:64], axis=mybir.AxisListType.X), reads=[b_lamt], writes=[b_lam1])
            fw.op(DVE, lambda h: h.reduce_sum(out=lam1[:, 1:2], in_=lamt[:, 128:192], axis=mybir.AxisListType.X), reads=[b_lamt], writes=[b_lam1])
            fw.op(ACT, lambda h: h.activation(out=lam1[:, 2:4], in_=lam1[:, 0:2], func=AF.Exp), reads=[b_lam1], writes=[b_lam1])
            fw.op(DVE, lambda h: h.scalar_tensor_tensor(out=lam1[:, 4:5], in0=lam1[:, 3:4], scalar=-lam_init, in1=lam1[:, 2:3], op0=ALU.add, op1=ALU.subtract),
                  reads=[b_lam1], writes=[b_lam1])
            ps, bps = psA.get()
            fw.op(PE, lambda h: h.matmul(ps[:, 0:1], lhsT=ones_rowf[:, :], rhs=lam1[:, 4:5], start=True, stop=True), reads=[b_onesrowf, b_lam1], writes=[bps])
            fw.op(DVE, lambda h: h.tensor_copy(out=lamc[:, 0:1], in_=ps[:, 0:1]), reads=[bps], writes=[b_lamc])
            return lam_init

        def key_base(ti):
            return 0 if ti == 0 else NP_TOK + PAST + (ti - 1) * TT

        import os as _os
        KP1S = int(_os.environ.get("KP1S", "99"))
        KP1 = int(_os.environ.get("KP1", str(NT)))

        def phase1(l, ti):
            if ti >= KP1:
                return
            lat = ti > 0
            cond = 1 if lat else 0
            t0 = ti * TT
            kb = key_base(ti)
            xt, bxt = load_xT(ti)
            hT, bh = norm_mod(xt, bxt, A1, 0, cond)
            rhs_h = lambda k: hT[:, k, :]
            if lat:
                rM, brM = ropM_r.get()
                fw.dma(SP, rM[:], ropeM[:, :, (ti - 1) * TT:ti * TT].rearrange("c p t -> p c t"), writes=[brM])
                rD, brD = ropD_r.get()
                fw.dma(SP, rD[:], ropeD[:, :, (ti - 1) * TT:ti * TT].rearrange("c p t -> p c t"), writes=[brD])
                tabM = (rM[:, 0, :], rM[:, 1, :], brM)
                tabD = (rD[:, 0, :], rD[:, 1, :], brD)
            else:
                tabM = tabD = None

            if KP1S < 1:
                return
            wt, bw = load_w8(w_in[l, :, O_AQ:O_AQ + 512], 512)
            for which, dst, bdst in ((0, gq_s, b_gq), (1, gk_s, b_gk)):
                g4, bg4 = g4_r.get()
                for hh in range(4):
                    ps, bps = proj_fm(wt, bw, which * 256 + hh * 64, 64, rhs_h, [bh], 8)
                    evac(ps[0:64, :], g4[:, hh, :], bps, bg4, eng=(ACT if hh % 2 else DVE))
                fw.dma(SP, dst[:, :, t0:t0 + TT], g4[:], reads=[bg4], writes=[bdst])
            if KP1S < 2:
                return
            tk, btk = tok_r.get()
            for g in range(4):
                ps, bps = psA.get()
                for k in range(8):
                    fw.op(PE, lambda h, k=k, g=g, ps=ps: h.matmul(ps[:, 0:256], lhsT=hT[:, k, g * 128:(g + 1) * 128], rhs=wt[:, k, 256:512], start=(k == 0), stop=(k == 7)),
                          reads=[bw, bh], writes=[bps], inc=(k == 7))
                evac(ps[:, 0:256], tk[:, g, 0:256], bps, btk, eng=(ACT if g % 2 else DVE))
            fw.dma(SP, gkt_s[t0:t0 + TT, :].rearrange("(g p) c -> p g c", p=128), tk[:, :, 0:256], reads=[btk], writes=[b_gkt])
            wt, bw = load_w8(w_in[l, :, O_AV:O_AV + 512], 512)
            tk, btk = tok_r.get()
            for g in range(4):
                ps, bps = psA.get()
                for k in range(8):
                    fw.op(PE, lambda h, k=k, g=g, ps=ps, wt=wt: h.matmul(ps[:, :], lhsT=hT[:, k, g * 128:(g + 1) * 128], rhs=wt[:, k, 0:512], start=(k == 0), stop=(k == 7)),
                          reads=[bw, bh], writes=[bps], inc=(k == 7))
                evac(ps[:, :], tk[:, g, :], bps, btk, eng=(ACT if g % 2 else DVE))
            fw.dma(SP, gvt_s[t0:t0 + TT, :].rearrange("(g p) c -> p g c", p=128), tk[:], reads=[btk], writes=[b_gvt])
            if KP1S < 3:
                return
            wt, bw = load_w8(w_in[l, :, O_AA:O_AA + 416], 416)
            ps, bps = proj_fm(wt, bw, 0, 32, rhs_h, [bh], 8)
            aab, baab = aab_r.get()
            evac(ps[0:32, :], aab[:, :], bps, baab, eng=DVE)
            gt, bgt = tokf_r.get()
            for g in range(4):
                ps, bps = psA.get()
                fw.op(PE, lambda h, g=g, ps=ps: h.matmul(ps[:, :], lhsT=aab[:, g * 128:(g + 1) * 128], rhs=wa2[:, :], start=True, stop=False),
                      reads=[baab, b_wa2], writes=[bps], inc=False)
                fw.op(PE, lambda h, ps=ps: h.matmul(ps[:, :], lhsT=ones_row[:, :], rhs=ba[:, :], start=False, stop=True), reads=[b_onesrow, b_ba], writes=[bps])
                e1, be1 = f32_r.get()
                fw.op(ACT, lambda h, ps=ps, e1=e1: h.activation(out=e1[:], in_=ps[:], func=AF.Exp, scale=-1.0), reads=[bps], writes=[be1])
                fw.op(ACT, lambda h, g=g, e1=e1: h.activation(out=gt[:, g, :], in_=e1[:], func=AF.Ln, bias=1.0), reads=[be1], writes=[bgt])
            fw.dma(SP, gG_s[t0:t0 + TT, :].rearrange("(g p) c -> p g c", p=128), gt[:], reads=[bgt], writes=[b_gG])

            if KP1S < 4:
                return
            wt_r, bw_r = load_w8(w_in[l, :, O_AR:O_AR + 512], 512)
            rr, brr = r4_r.get()
            for hh in range(4):
                ps_r, bps_r = proj_fm(wt_r, bw_r, hh * 128, 128, rhs_h, [bh], 8)
                fw.op(ACT, lambda h, hh=hh, ps_r=ps_r: h.activation(out=rr[:, hh, :], in_=ps_r[:], func=AF.Silu), reads=[bps_r], writes=[brr])
            fw.dma(SP, rs_s[:, :, t0:t0 + TT], rr[:], reads=[brr], writes=[b_rs])

            if KP1S < 5:
                return
            qdn, bqdn = qdn_r.get()
            qf = []
            for j in range(3):
                ps, bps = proj_fm(wt, bw, 32 + j * 128, 128, rhs_h, [bh], 8)
                xf, bxf = f32_r.get()
                evac(ps[:], xf[:], bps, bxf, eng=DVE)
                sq, bsq = bf_r.get()
                fw.op(ACT, lambda h, sq=sq, xf=xf: h.activation(out=sq[:], in_=xf[:], func=AF.Square), reads=[bxf], writes=[bsq])
                qf.append((xf, bxf, sq, bsq))
            rs, brs = rms_rstd([(q[2][:], q[3]) for q in qf], cmat[:, C_384, :], 128)
            for j in range(3):
                xf, bxf = qf[j][0], qf[j][1]
                fw.op(DVE, lambda h, j=j, xf=xf: h.scalar_tensor_tensor(out=qdn[:, j, :], in0=xf[:], scalar=gc[:, j:j + 1], in1=rs[:], op0=ALU.mult, op1=ALU.mult),
                      reads=[bxf, brs, b_gc], writes=[bqdn])
            pipe = Pipe()
            for half in range(2):
                wq, bwq = w8.get()
                fw.dma(POOL, wq[:, 0:3, 0:384], w_uq[l, :, half * 4:(half + 1) * 4, :].rearrange("(k p) h c -> p k (h c)", p=128), writes=[bwq])
                for hh in range(4):
                    ps, bps = proj_fm(wq, bwq, hh * 96, 96, lambda k: qdn[:, k, :], [bqdn], 3)
                    head = half * 4 + hh
                    pipe.push(norm_rope_gen(ps, bps, 96, cmat[0:96, C_96, 0:96], 5, lat, P_96, tabM, q_s[head, :, t0:t0 + TT], b_q))
            pipe.drain()

            if KP1S < 6:
                return
            wt, bw = load_w8(w_in[l, :, O_KVD:O_KVD + 288], 288)
            ckv, bckv = ckv_r.get()
            ckvf, bckvf = ckvf_r.get()
            kf = []
            for j in range(2):
                ps, bps = proj_fm(wt, bw, j * 128, 128, rhs_h, [bh], 8)
                xf, bxf = f32_r.get()
                evac(ps[:], xf[:], bps, bxf, eng=DVE)
                sq, bsq = bf_r.get()
                fw.op(ACT, lambda h, sq=sq, xf=xf: h.activation(out=sq[:], in_=xf[:], func=AF.Square), reads=[bxf], writes=[bsq])
                kf.append((xf, bxf, sq, bsq))
            rs, brs = rms_rstd([(q[2][:], q[3]) for q in kf], cmat[:, C_256, :], 128)
            for j in range(2):
                xf, bxf = kf[j][0], kf[j][1]
                fw.op(DVE, lambda h, j=j, xf=xf: h.scalar_tensor_tensor(out=ckvf[:, j, :], in0=xf[:], scalar=gc[:, 3 + j:4 + j], in1=rs[:], op0=ALU.mult, op1=ALU.mult),
                      reads=[bxf, brs, b_gc], writes=[bckvf])
            fw.op(ACT, lambda h: h.activation(out=ckv[:], in_=ckvf[:], func=AF.Copy), reads=[bckvf], writes=[bckv])
            ps, bps = proj_fm(wt, bw, 256, 32, rhs_h, [bh], 8)
            krf, bkrf = krf_r.get()
            krb, bkrb = krb_r.get()
            evac(ps[0:32, :], krf[:, :], bps, bkrf, eng=DVE)
            fw.op(ACT, lambda h: h.activation(out=krb[:], in_=krf[:], func=AF.Copy), reads=[bkrf], writes=[bkrb])
            KSUB = _os.environ.get("KSUB", "abcd")
            if not lat:
                if "a" in KSUB:
                    transpose_out(lambda j: ckvf[:, j, :], [bckvf], 128, 256, lambda g: o_ckv[l, g * 128:(g + 1) * 128, :], [(0, 128, 0), (1, 128, 128)])
                if "b" in KSUB:
                    transpose_out(lambda j: krf[:, :], [bkrf], 32, 32, lambda g: o_kr[l, g * 128:(g + 1) * 128, :], [(0, 32, 0)])
            if "c" in KSUB:
                mla_keys(l, lambda k: ckv[:, k, :], [bckv], krb, bkrb, lat, tabM, kb)
            if "d" in KSUB:
                mla_values(l, lambda k, g: ckv[:, k, g * 128:(g + 1) * 128], [bckv], kb)

            if KP1S < 7:
                return
            for which, off, dst, bdst, gcolumn in ((0, O_DQ, dq_s, b_dq, 7), (1, O_DK, dk_s, b_dk, 8)):
                wt, bw = load_w8(w_in[l, :, off:off + 512], 512)
                dpipe = Pipe()
                keep = None
                if which == 1 and not lat:
                    kp, bkp = keep_r.get()
                for hh in range(4):
                    ps, bps = proj_fm(wt, bw, hh * 128, 128, rhs_h, [bh], 8)
                    kf32 = None
                    if which == 1 and not lat:
                        kf32 = lambda xn, bxn, hh=hh: fw.op(ACT, lambda h: h.activation(out=kp[:, hh, :], in_=xn[:], func=AF.Copy), reads=[bxn], writes=[bkp])
                    col0 = t0 if which == 0 else kb
                    dpipe.push(norm_rope_gen(ps, bps, 128, cmat[:, C_BLK64, :], gcolumn, lat, P_128, tabD, dst[hh, :, col0:col0 + TT], bdst, keep_f32=kf32))
                dpipe.drain()
                if which == 1 and not lat:
                    transpose_out(lambda j: kp[:, j, :], [bkp], 128, 512, lambda g: o_dk[l, g * 128:(g + 1) * 128, :], [(j, 128, j * 128) for j in range(4)])
            wt, bw = load_w8(w_in[l, :, O_DV:O_DV + 512], 512)
            tk, btk = tok_r.get()
            if not lat:
                tf, btf = tokf_r.get()
            for g in range(4):
                ps, bps = psA.get()
                for k in range(8):
                    fw.op(PE, lambda h, k=k, g=g, ps=ps, wt=wt: h.matmul(ps[:, :], lhsT=hT[:, k, g * 128:(g + 1) * 128], rhs=wt[:, k, 0:512], start=(k == 0), stop=(k == 7)),
                          reads=[bw, bh], writes=[bps], inc=(k == 7))
                evac(ps[:, :], tk[:, g, :], bps, btk, eng=ACT)
                if not lat:
                    evac(ps[:, :], tf[:, g, :], bps, btf, eng=DVE)
            for hh in range(4):
                fw.dma(SP, dv_s[hh, :, kb // 128:kb // 128 + 4, :], tk[:, :, hh * 128:(hh + 1) * 128], reads=[btk], writes=[b_dv])
            if not lat:
                ob = Buf(); out_bufs.append(ob)
                fw.dma(SP, o_dv[l, :, :].rearrange("(g p) c -> p g c", p=128), tf[:], reads=[btf], writes=[ob])

        def mla_keys(l, ckv_fn, ckv_bufs, krb, bkrb, rope, tabM, kb, ntok=TT):
            pipe = Pipe()
            for half in range(2):
                wk, bwk = w8.get()
                fw.dma(POOL, wk[:, 0:2, 0:384], w_ukp[l, :, half * 4:(half + 1) * 4, :].rearrange("(k p) h c -> p k (h c)", p=128), writes=[bwk])
                for hh in range(4):
                    ps, bps = psA.get()
                    for k in range(2):
                        fw.op(PE, lambda h, k=k, hh=hh, ps=ps, wk=wk: h.matmul(ps[0:96, 0:ntok], lhsT=wk[:, k, hh * 96:(hh + 1) * 96], rhs=ckv_fn(k), start=(k == 0), stop=False),
                              reads=[bwk] + ckv_bufs, writes=[bps], inc=False)
                    fw.op(PE, lambda h, ps=ps: h.matmul(ps[0:96, 0:ntok], lhsT=cmat[0:32, C_SEL32, 0:96], rhs=krb[:, 0:ntok], start=False, stop=True),
                          reads=[b_cmat, bkrb], writes=[bps])
                    head = half * 4 + hh
                    if ntok == TT:
                        pipe.push(norm_rope_gen(ps, bps, 96, cmat[0:96, C_96, 0:96], 6, rope, P_96, tabM, k_s[head, :, kb:kb + TT], b_k))
                    else:
                        norm_store_small(ps, bps, ntok, head, kb)
            pipe.drain()

        def norm_store_small(ps, bps, ntok, head, kb):
            xf, bxf = f32_r.get()
            evac(ps[0:96, 0:ntok], xf[0:96, 0:ntok], bps, bxf, eng=DVE)
            sq, bsq = bf_r.get()
            fw.op(ACT, lambda h: h.activation(out=sq[0:96, 0:ntok], in_=xf[0:96, 0:ntok], func=AF.Square), reads=[bxf], writes=[bsq])
            ps2, bps2 = psA.get()
            fw.op(PE, lambda h: h.matmul(ps2[0:96, 0:ntok], lhsT=cmat[0:96, C_96, 0:96], rhs=sq[0:96, 0:ntok], start=True, stop=True), reads=[bsq, b_cmat], writes=[bps2])
            rs, brs = rs_r.get()
            rsqrt_eps(ps2[0:96, 0:ntok], rs[0:96, 0:ntok], bps2, brs)
            ob, bob = bf_r.get()
            fw.op(DVE, lambda h: h.scalar_tensor_tensor(out=ob[0:96, 0:ntok], in0=xf[0:96, 0:ntok], scalar=gc[0:96, 6:7], in1=rs[0:96, 0:ntok], op0=ALU.mult, op1=ALU.mult),
                  reads=[bxf, brs, b_gc], writes=[bob])
            fw.dma(SP, k_s[head, :, kb:kb + ntok], ob[0:96, 0:ntok], reads=[bob], writes=[b_k])

        vaug_init = [False, False]

        def mla_values(l, ckvT_fn, ckv_bufs, kb, ngrp=4):
            wv, bwv = w8.get()
            fw.dma(POOL, wv[:, 0:2, 0:512], w_uv[l].rearrange("(k p) c -> p k c", p=128), writes=[bwv])
            idx = vaug_r.i
            va, bva = vaug_r.get()
            if not vaug_init[idx]:
                vaug_init[idx] = True
                fw.op(DVE, lambda h: h.memset(va[:], 1.0), writes=[bva])
            for g in range(ngrp):
                ps, bps = psA.get()
                for k in range(2):
                    fw.op(PE, lambda h, k=k, g=g, ps=ps: h.matmul(ps[:, :], lhsT=ckvT_fn(k, g), rhs=wv[:, k, 0:512], start=(k == 0), stop=(k == 1)),
                          reads=[bwv] + ckv_bufs, writes=[bps], inc=(k == 1))
                fw.op(ACT if g % 2 else DVE, (lambda h, g=g, ps=ps: h.activation(out=va[:, :, g, 0:64], in_=ps[:, :].rearrange("p (h c) -> p h c", c=64), func=AF.Copy)) if g % 2
                      else (lambda h, g=g, ps=ps: h.tensor_copy(out=va[:, :, g, 0:64], in_=ps[:, :].rearrange("p (h c) -> p h c", c=64))), reads=[bps], writes=[bva])
            c0 = kb // 128
            fw.dma(SP, v_s[:, :, c0:c0 + ngrp, :].rearrange("h p g e -> p h (g e)"), va[:, :, 0:ngrp, :].rearrange("p h g e -> p h (g e)"), reads=[bva], writes=[b_v])


        def prep_cache(l):
            kb = NP_TOK
            ct, bct = ctok_r.get()
            fw.dma(SP, ct[:, :, 0:256], c_ckv[l].rearrange("(g p) c -> p g c", p=128), writes=[bct])
            ckv, bckv = ckv_r.get()
            for j in range(2):
                ps, bps = psA.get()
                for g in range(2):
                    fw.op(PE, lambda h, j=j, g=g, ps=ps: h.transpose(ps[:, g * 128:(g + 1) * 128], ct[:, g, j * 128:(j + 1) * 128], ident[:]),
                          reads=[bct, b_ident], writes=[bps], inc=(g == 1))
                evac(ps[:, 0:256], ckv[:, j, 0:256], bps, bckv, eng=DVE)
            ct2, bct2 = ctok_r.get()
            fw.dma(SP, ct2[:, :, 0:32], c_kr[l].rearrange("(g p) c -> p g c", p=128), writes=[bct2])
            krb, bkrb = krb_r.get()
            ps, bps = psA.get()
            for g in range(2):
                fw.op(PE, lambda h, g=g, ps=ps: h.transpose(ps[0:32, g * 128:(g + 1) * 128], ct2[:, g, 0:32], ident[:]), reads=[bct2, b_ident], writes=[bps], inc=(g == 1))
            evac(ps[0:32, 0:256], krb[:, 0:256], bps, bkrb, eng=DVE)
            mla_keys(l, lambda k: ckv[:, k, 0:256], [bckv], krb, bkrb, False, None, kb, ntok=256)
            mla_values(l, lambda k, g: ckv[:, k, g * 128:(g + 1) * 128], [bckv], kb, ngrp=2)
            ct, bct = ctok_r.get()
            fw.dma(SP, ct[:], c_dk[l].rearrange("(g p) c -> p g c", p=128), writes=[bct])
            for hh in range(4):
                ps, bps = psA.get()
                for g in range(2):
                    fw.op(PE, lambda h, hh=hh, g=g, ps=ps: h.transpose(ps[:, g * 128:(g + 1) * 128], ct[:, g, hh * 128:(hh + 1) * 128], ident[:]),
                          reads=[bct, b_ident], writes=[bps], inc=(g == 1))
                ob, bob = bf_r.get()
                evac(ps[:, 0:256], ob[:, 0:256], bps, bob, eng=DVE)
                fw.dma(SP, dk_s[hh, :, kb:kb + 256], ob[:, 0:256], reads=[bob], writes=[b_dk])
            ct3, bct3 = ctok_r.get()
            fw.dma(SP, ct3[:], c_dv[l].rearrange("(g p) c -> p g c", p=128), writes=[bct3])
            tkc, btkc = tok_r.get()
            fw.op(DVE, lambda h: h.tensor_copy(out=tkc[:, 0:2, :], in_=ct3[:]), reads=[bct3], writes=[btkc])
            for hh in range(4):
                fw.dma(SP, dv_s[hh, :, kb // 128:kb // 128 + 2, :], tkc[:, 0:2, hh * 128:(hh + 1) * 128], reads=[btkc], writes=[b_dv])

        arena.reset()
        gl_q = Ring(ov, "glq", [64, 4, TT], BF16, 1)
        gl_k = Ring(ov, "glk", [64, 4, TT], BF16, 1)
        gl_kt = Ring(ov, "glkt", [64, 8, 256], BF16, 1)
        gl_vt = Ring(ov, "glvt", [64, 8, 512], BF16, 1)
        gl_G = Ring(ov, "glG", [64, 8, 256], F32, 1)
        gl_o = Ring(ov, "glo", [128, 4, TT], F32, 1)
        gl_of = Ring(ov, "glof", [128, 4, TT], F32, 1)
        gl_r = Ring(ov, "glr", [128, 4, TT], BF16, 2)
        S_f = ov("S_f", [64, 4, 128], F32); S_b = ov("S_b", [64, 4, 128], BF16); b_S = Buf(); b_Sb = Buf()
        sm_r = Ring(ov, "sm", [64, 256], F32, 6)
        smb_r = Ring(ov, "smb", [64, 256], BF16, 8)
        dd_r = Ring(ov, "dd", [64, 4], F32, 3)

        def gla_dir(l, d, seqs):
            incl = tri[:, d, :]
            strict = tri[:, 2 + d, :]
            for (tok0, ntok, seq_idx, lat) in seqs:
                if lat:
                    fw.dma(SP, S_f[:], st_gla[l, d], reads=[b_Sb], writes=[b_S])
                else:
                    fw.op(DVE, lambda h: h.memset(S_f[:], 0.0), reads=[b_Sb], writes=[b_S])
                fw.op(ACT, lambda h: h.activation(out=S_b[:], in_=S_f[:], func=AF.Copy), reads=[b_S], writes=[b_Sb])
                tiles = list(range(tok0, tok0 + ntok, TT)) if ntok >= TT else [tok0]
                if d == 1:
                    tiles = tiles[::-1]
                for tt0 in tiles:
                    base_tile = (tt0 // TT) * TT
                    cq, bcq = gl_q.get(); ck, bck = gl_k.get(); ckt, bckt = gl_kt.get(); cvt, bcvt = gl_vt.get(); cG, bcG = gl_G.get()
                    fw.dma(SP, cq[:], gq_s[:, :, base_tile:base_tile + TT], reads=[b_gq], writes=[bcq])
                    fw.dma(SP, ck[:], gk_s[:, :, base_tile:base_tile + TT], reads=[b_gk], writes=[bck])
                    fw.dma(SP, ckt[:], gkt_s[base_tile:base_tile + TT, :].rearrange("(c p) f -> p c f", p=64), reads=[b_gkt], writes=[bckt])
                    fw.dma(SP, cvt[:], gvt_s[base_tile:base_tile + TT, :].rearrange("(c p) f -> p c f", p=64), reads=[b_gvt], writes=[bcvt])
                    fw.dma(SP, cG[:], gG_s[base_tile:base_tile + TT, d * 256:(d + 1) * 256].rearrange("(c p) f -> p c f", p=64), reads=[b_gG], writes=[bcG])
                    ot, bot = gl_o.get()
                    c_lo = (tt0 - base_tile) // 64
                    nch = min(ntok, TT) // 64
                    chunks = list(range(c_lo, c_lo + nch))
                    if d == 1:
                        chunks = chunks[::-1]
                    for c in chunks:
                        Gc = cG[:, c, :]
                        psb_, bpsb = psA.get()
                        for hh in range(4):
                            fw.op(PE, lambda h, hh=hh, Gc=Gc, p=psb_: h.matmul(p[0:64, hh * 64:(hh + 1) * 64], lhsT=Gc[:, hh * 64:(hh + 1) * 64], rhs=incl, start=True, stop=True),
                                  reads=[bcG, b_tri], writes=[bpsb], inc=(hh == 3))
                        ep, bep = sm_r.get(); en, ben = sm_r.get()
                        fw.op(ACT, lambda h, p=psb_, ep=ep: h.activation(out=ep[:, :], in_=p[0:64, 0:256], func=AF.Exp, scale=-1.0 / 16), reads=[bpsb], writes=[bep])
                        fw.op(ACT, lambda h, p=psb_, en=en: h.activation(out=en[:, :], in_=p[0:64, 0:256], func=AF.Exp, scale=1.0 / 16), reads=[bpsb], writes=[ben])
                        qt, bqt = smb_r.get(); kt_, bkt = smb_r.get()
                        fw.op(DVE, lambda h, c=c, qt=qt, ep=ep: h.scalar_tensor_tensor(out=qt[:, :].rearrange("p (h t) -> p h t", t=64), in0=cq[:, :, c * 64:(c + 1) * 64], scalar=0.125,
                                                                                      in1=ep[:, :].rearrange("p (h t) -> p h t", t=64), op0=ALU.mult, op1=ALU.mult),
                              reads=[bcq, bep], writes=[bqt])
                        fw.op(DVE, lambda h, c=c, kt_=kt_, en=en: h.tensor_tensor(out=kt_[:, :].rearrange("p (h t) -> p h t", t=64), in0=ck[:, :, c * 64:(c + 1) * 64],
                                                                                  in1=en[:, :].rearrange("p (h t) -> p h t", t=64), op=ALU.mult),
                              reads=[bck, ben], writes=[bkt])
                        ps2, bps2 = psA.get()
                        fw.op(PE, lambda h, Gc=Gc, p=ps2: h.matmul(p[0:64, 0:256], lhsT=strict, rhs=Gc, start=True, stop=True), reads=[bcG, b_tri], writes=[bps2])
                        e2, be2 = sm_r.get()
                        fw.op(ACT, lambda h, p=ps2, e2=e2: h.activation(out=e2[:, :], in_=p[0:64, 0:256], func=AF.Exp, scale=-1.0 / 16), reads=[bps2], writes=[be2])
                        kh, bkh = smb_r.get()
                        fw.op(DVE, lambda h, c=c, kh=kh, e2=e2: h.tensor_tensor(out=kh[:, :], in0=ckt[:, c, :], in1=e2[:, :], op=ALU.mult), reads=[bckt, be2], writes=[bkh])
                        ps3, bps3 = psA.get()
                        for hh in range(4):
                            fw.op(PE, lambda h, hh=hh, Gc=Gc, p=ps3: h.matmul(p[0:64, hh:hh + 1], lhsT=Gc[:, hh * 64:(hh + 1) * 64], rhs=ones_col[:, :], start=True, stop=True),
                                  reads=[bcG, b_onescol], writes=[bps3], inc=(hh == 3))
                        dd, bdd = dd_r.get()
                        fw.op(ACT, lambda h, p=ps3, dd=dd: h.activation(out=dd[:, :], in_=p[0:64, 0:4], func=AF.Exp, scale=-1.0 / 16), reads=[bps3], writes=[bdd])
                        ps4, bps4 = psA.get()
                        for hh in range(4):
                            fw.op(PE, lambda h, hh=hh, p=ps4, kt_=kt_, qt=qt: h.matmul(p[0:64, hh * 64:(hh + 1) * 64], lhsT=kt_[:, hh * 64:(hh + 1) * 64], rhs=qt[:, hh * 64:(hh + 1) * 64],
                                                                                     start=True, stop=True), reads=[bkt, bqt], writes=[bps4], inc=(hh == 3))
                        am, bam = smb_r.get()
                        fw.op(DVE, lambda h, p=ps4, am=am: h.tensor_tensor(out=am[:, :].rearrange("p (h t) -> p h t", t=64), in0=p[0:64, 0:256].rearrange("p (h t) -> p h t", t=64),
                                                                         in1=incl.unsqueeze(1).broadcast_to([64, 4, 64]), op=ALU.mult), reads=[bps4, b_tri], writes=[bam])
                        ps5, bps5 = psA.get()
                        for hh in range(4):
                            fw.op(PE, lambda h, hh=hh, c=c, p=ps5, am=am: h.matmul(p[:, hh * 64:(hh + 1) * 64], lhsT=cvt[:, c, hh * 128:(hh + 1) * 128], rhs=am[:, hh * 64:(hh + 1) * 64],
                                                                                 start=True, stop=False), reads=[bcvt, bam], writes=[bps5], inc=False)
                            fw.op(PE, lambda h, hh=hh, p=ps5, qt=qt: h.matmul(p[:, hh * 64:(hh + 1) * 64], lhsT=S_b[:, hh, :], rhs=qt[:, hh * 64:(hh + 1) * 64], start=False, stop=True),
                                  reads=[b_Sb, bqt], writes=[bps5], inc=(hh == 3))
                        fw.op(ACT, lambda h, c=c, p=ps5, ot=ot: h.activation(out=ot[:, :, c * 64:(c + 1) * 64], in_=p[:, 0:256].rearrange("p (h t) -> p h t", t=64), func=AF.Copy),
                              reads=[bps5], writes=[bot])
                        ps6, bps6 = psB.get()
                        for hh in range(4):
                            fw.op(PE, lambda h, hh=hh, c=c, p=ps6, kh=kh: h.matmul(p[0:64, hh * 128:(hh + 1) * 128], lhsT=kh[:, hh * 64:(hh + 1) * 64], rhs=cvt[:, c, hh * 128:(hh + 1) * 128],
                                                                                 start=True, stop=True), reads=[bkh, bcvt], writes=[bps6], inc=(hh == 3))
                        fw.op(DVE, lambda h, dd=dd: h.tensor_tensor(out=S_f[:], in0=S_f[:], in1=dd[:, :].unsqueeze(2).broadcast_to([64, 4, 128]), op=ALU.mult), reads=[bdd], writes=[b_S])
                        fw.op(DVE, lambda h, p=ps6: h.tensor_tensor(out=S_f[:], in0=S_f[:], in1=p[0:64, :].rearrange("p (h v) -> p h v", v=128), op=ALU.add), reads=[bps6], writes=[b_S])
                        fw.op(ACT, lambda h: h.activation(out=S_b[:], in_=S_f[:], func=AF.Copy), reads=[b_S], writes=[b_Sb])
                    cols = slice(tt0 - base_tile, tt0 - base_tile + min(ntok, TT))
                    ncol = min(ntok, TT)
                    if d == 0:
                        fw.dma(SP, go_s[:, :, tt0:tt0 + ncol], ot[:, :, cols], reads=[bot], writes=[b_go])
                    else:
                        of, bof = gl_of.get()
                        fw.dma(SP, of[:, :, 0:ncol], go_s[:, :, tt0:tt0 + ncol], reads=[b_go], writes=[bof])
                        rr, brr = gl_r.get()
                        fw.dma(SP, rr[:, :, 0:ncol], rs_s[:, :, tt0:tt0 + ncol], reads=[b_rs], writes=[brr])
                        fw.op(DVE, lambda h, ot=ot, of=of: h.tensor_tensor(out=of[:, :, 0:ncol], in0=of[:, :, 0:ncol], in1=ot[:, :, cols], op=ALU.add), reads=[bot], writes=[bof])
                        sq, bsq = sq_r.get()
                        fw.op(ACT, lambda h, sq=sq, of=of: h.activation(out=sq[:, 0:4, 0:ncol], in_=of[:, :, 0:ncol], func=AF.Square), reads=[bof], writes=[bsq])
                        ya, bya = gl_r.get()
                        for hh in range(4):
                            ps, bps = psA.get()
                            fw.op(PE, lambda h, hh=hh, ps=ps, sq=sq: h.matmul(ps[:, 0:ncol], lhsT=cmat[:, C_128, :], rhs=sq[:, hh, 0:ncol], start=True, stop=True), reads=[bsq, b_cmat], writes=[bps])
                            rs, brs = rs_r.get()
                            rsqrt_eps(ps[:, 0:ncol], rs[:, 0:ncol], bps, brs)
                            tmp, btmp = f32_r.get()
                            fw.op(DVE, lambda h, hh=hh, tmp=tmp, of=of, rs=rs: h.scalar_tensor_tensor(out=tmp[:, 0:ncol], in0=of[:, hh, 0:ncol], scalar=gc[:, 9:10], in1=rs[:, 0:ncol],
                                                                                                     op0=ALU.mult, op1=ALU.mult), reads=[bof, brs, b_gc], writes=[btmp])
                            fw.op(DVE, lambda h, hh=hh, tmp=tmp, ya=ya, rr=rr: h.tensor_tensor(out=ya[:, hh, 0:ncol], in0=tmp[:, 0:ncol], in1=rr[:, hh, 0:ncol], op=ALU.mult),
                                  reads=[btmp, brr], writes=[bya])
                        fw.dma(SP, ya_s[:, :, tt0:tt0 + ncol].rearrange("h p t -> p h t"), ya[:, :, 0:ncol], reads=[bya], writes=[b_ya])
                if seq_idx is not None:
                    ob = Buf(); out_bufs.append(ob)
                    fw.dma(SP, o_gla[l, seq_idx, d], S_f[:], reads=[b_S], writes=[ob])

        rs_s = dram_tmp("rs_s", [128, 4, T], BF16)
        b_rs = Buf()

        def phase1b(l, ti):
            cond = 1 if ti > 0 else 0
            xt, bxt = load_xT(ti)
            hT, bh = norm_mod(xt, bxt, A1, 0, cond)
            wt, bw = load_w8(w_in[l, :, O_AR:O_AR + 512], 512)
            rr, brr = gl_r.get()
            for hh in range(4):
                ps, bps = proj_fm(wt, bw, hh * 128, 128, lambda k: hT[:, k, :], [bh], 8)
                fw.op(ACT, lambda h, hh=hh, ps=ps: h.activation(out=rr[:, hh, :], in_=ps[:], func=AF.Silu), reads=[bps], writes=[brr])
            fw.dma(SP, rs_s[:, :, ti * TT:(ti + 1) * TT], rr[:], reads=[brr], writes=[b_rs])

        arena.reset()
        at_k = Ring(ov, "atk", [128, TKS], BF16, 2)
        at_v = Ring(ov, "atv", [128, TKS // 128, 128], BF16, 2)
        at_q = Ring(ov, "atq", [128, TT], BF16, 2)
        at_p = Ring(ov, "atp", [128, TT], BF16, 6)
        at_o = Ring(ov, "ato", [128, TT], BF16, 2)
        at_qm = [Ring(ov, "atqm0_", [128, TT], BF16, 2), Ring(ov, "atqm1_", [128, TT], BF16, 2)]
        at_l = Ring(ov, "atl", [128, TT], F32, 4)

        def mla_attn(l, groups):
            sc = 96 ** -0.5
            for i in range(2):
                fw.op(DVE, lambda h: h.memset(at_k.t[i][:], 0.0), writes=[at_k.b[i]])
                fw.op(DVE, lambda h: h.memset(at_q.t[i][:], 0.0), writes=[at_q.b[i]])
                for c2 in range(2):
                    fw.op(DVE, lambda h: h.memset(at_qm[c2].t[i][:], 0.0), writes=[at_qm[c2].b[i]])
            its = [(gi, head, qq) for gi, (q0, nq, k0, nk) in enumerate(groups) for head in range(8) for qq in range(q0, q0 + nq, TT)]
            kv_cache, q_cache = {}, {}

            def get_kv(gi, head):
                if (gi, head) not in kv_cache:
                    q0, nq, k0, nk = groups[gi]
                    kt, bkt = at_k.get(); vt, bvt = at_v.get()
                    fw.dma(SP, kt[0:96, 0:nk], k_s[head, :, k0:k0 + nk], reads=[b_k], writes=[bkt])
                    fw.dma(SP, vt[:, 0:nk // 128, :], v_s[head, :, k0 // 128:k0 // 128 + nk // 128, :], reads=[b_v], writes=[bvt])
                    kv_cache[(gi, head)] = (kt, bkt, vt, bvt)
                return kv_cache[(gi, head)]

            def get_q(idx):
                if idx not in q_cache:
                    gi, head, qq = its[idx]
                    nqt = min(TT, groups[gi][1])
                    qt, bqt = at_q.get()
                    fw.dma(SP, qt[0:96, 0:nqt], q_s[head, :, qq:qq + nqt], reads=[b_q], writes=[bqt])
                    q_cache[idx] = (qt, bqt)
                return q_cache[idx]

            for idx, (gi, head, qq) in enumerate(its):
                q0, nq, k0, nk = groups[gi]
                nkc = nk // 128
                if True:
                    kt, bkt, vt, bvt = get_kv(gi, head)
                    if True:
                        nqt = min(TT, nq)
                        qt, bqt = get_q(idx)
                        if idx + 1 < len(its):
                            get_kv(its[idx + 1][0], its[idx + 1][1])
                            get_q(idx + 1)
                        po, bpo = psB.get()

                        def qk(c, kt=kt, qt=qt, bkt=bkt, bqt=bqt, nqt=nqt):
                            ps, bps = psA.get()
                            fw.op(PE, lambda h: h.matmul(ps[:, 0:nqt], lhsT=kt[:, c * 128:(c + 1) * 128], rhs=qt[:, 0:nqt], start=True, stop=True),
                                  reads=[bkt, bqt], writes=[bps])
                            return ps, bps

                        LA = 2
                        pend = [qk(i) for i in range(min(LA, nkc))]
                        for c in range(nkc):
                            ps, bps = pend.pop(0)
                            if c + LA < nkc:
                                pend.append(qk(c + LA))
                            pt, bpt = at_p.get()
                            fw.op(ACT, lambda h, ps=ps, pt=pt: h.activation(out=pt[:, 0:nqt], in_=ps[:, 0:nqt], func=AF.Exp, scale=sc), reads=[bps], writes=[bpt])
                            fw.op(PE, lambda h, c=c, po=po, vt=vt, pt=pt: h.matmul(po[:, 0:nqt], lhsT=vt[:, c, :], rhs=pt[:, 0:nqt], start=(c == 0), stop=(c == nkc - 1)),
                                  reads=[bvt, bpt], writes=[bpo], inc=(c == nkc - 1))
                        rc, brc = f32_r.get()
                        fw.op(DVE, lambda h, po=po, rc=rc: h.reciprocal(out=rc[64:128, 0:nqt], in_=po[64:128, 0:nqt]), reads=[bpo], writes=[brc])
                        r2, br2 = f32_r.get()
                        fw.op(DVE, lambda h, rc=rc, r2=r2: h.tensor_copy(out=r2[0:64, 0:nqt], in_=rc[64:128, 0:nqt]), reads=[brc], writes=[br2])
                        ob, bob = at_o.get()
                        fw.op(DVE, lambda h, po=po, r2=r2, ob=ob: h.tensor_tensor(out=ob[0:64, 0:nqt], in0=po[0:64, 0:nqt], in1=r2[0:64, 0:nqt], op=ALU.mult), reads=[bpo, br2], writes=[bob])
                        fw.dma(SP, yb_s[head // 2, (head % 2) * 64:(head % 2) * 64 + 64, qq:qq + nqt], ob[0:64, 0:nqt], reads=[bob], writes=[b_yb])

        def diff_attn(l, groups, lam_init):
            sc = 64 ** -0.5
            its = [(gi, head, qq) for gi, (q0, nq, k0, nk) in enumerate(groups) for head in range(4) for qq in range(q0, q0 + nq, TT)]
            kv_cache, q_cache = {}, {}

            def get_kv(gi, head):
                if (gi, head) not in kv_cache:
                    q0, nq, k0, nk = groups[gi]
                    kt, bkt = at_k.get(); vt, bvt = at_v.get()
                    fw.dma(SP, kt[:, 0:nk], dk_s[head, :, k0:k0 + nk], reads=[b_dk], writes=[bkt])
                    fw.dma(SP, vt[:, 0:nk // 128, :], dv_s[head, :, k0 // 128:k0 // 128 + nk // 128, :], reads=[b_dv], writes=[bvt])
                    kv_cache[(gi, head)] = (kt, bkt, vt, bvt)
                return kv_cache[(gi, head)]

            def get_q(idx):
                if idx not in q_cache:
                    gi, head, qq = its[idx]
                    nqt = min(TT, groups[gi][1])
                    qm = [at_qm[0].get(), at_qm[1].get()]
                    fw.dma(SP, qm[0][0][0:64, 0:nqt], dq_s[head, 0:64, qq:qq + nqt], reads=[b_dq], writes=[qm[0][1]])
                    fw.dma(SP, qm[1][0][64:128, 0:nqt], dq_s[head, 64:128, qq:qq + nqt], reads=[b_dq], writes=[qm[1][1]])
                    q_cache[idx] = qm
                return q_cache[idx]

            for idx, (gi, head, qq) in enumerate(its):
                q0, nq, k0, nk = groups[gi]
                nkc = nk // 128
                if True:
                    kt, bkt, vt, bvt = get_kv(gi, head)
                    if True:
                        nqt = min(TT, nq)
                        qm = get_q(idx)
                        if idx + 1 < len(its):
                            get_kv(its[idx + 1][0], its[idx + 1][1])
                            get_q(idx + 1)
                        acc = [psB.get() for _ in range(3)]
                        lac = [at_l.get()]

                        def qk(i, kt=kt, bkt=bkt, nqt=nqt, qm=qm):
                            c, comp = divmod(i, 2)
                            ps, bps = psA.get()
                            fw.op(PE, lambda h: h.matmul(ps[:, 0:nqt], lhsT=kt[:, c * 128:(c + 1) * 128], rhs=qm[comp][0][:, 0:nqt], start=True, stop=True),
                                  reads=[bkt, qm[comp][1]], writes=[bps])
                            return ps, bps

                        LA = 2
                        pend = [qk(i) for i in range(min(LA, 2 * nkc))]
                        for c in range(nkc):
                            for comp in range(2):
                                ps, bps = pend.pop(0)
                                if c * 2 + comp + LA < 2 * nkc:
                                    pend.append(qk(c * 2 + comp + LA))
                                pt, bpt = at_p.get()
                                fw.op(ACT, lambda h: h.activation(out=pt[:, 0:nqt], in_=ps[:, 0:nqt], func=AF.Exp, scale=sc), reads=[bps], writes=[bpt])
                                po, bpo = acc[comp]
                                fw.op(PE, lambda h: h.matmul(po[:, 0:nqt], lhsT=vt[:, c, :], rhs=pt[:, 0:nqt], start=(c == 0), stop=(c == nkc - 1)),
                                      reads=[bvt, bpt], writes=[bpo], inc=(c == nkc - 1))
                                if comp == 0:
                                    la, bla = lac[0]
                                    if c == 0:
                                        fw.op(DVE, lambda h: h.tensor_copy(out=la[:, 0:nqt], in_=pt[:, 0:nqt]), reads=[bpt], writes=[bla])
                                    else:
                                        fw.op(DVE, lambda h: h.tensor_tensor(out=la[:, 0:nqt], in0=la[:, 0:nqt], in1=pt[:, 0:nqt], op=ALU.add), reads=[bpt], writes=[bla])
                                else:
                                    pl1, bpl1 = acc[2]
                                    fw.op(PE, lambda h: h.matmul(pl1[:, 0:nqt], lhsT=cmat[:, C_ONE, :], rhs=pt[:, 0:nqt], start=(c == 0), stop=(c == nkc - 1)),
                                          reads=[b_cmat, bpt], writes=[bpl1], inc=(c == nkc - 1))
                        rr_ = []
                        pl, bpl = psA.get()
                        fw.op(PE, lambda h: h.matmul(pl[:, 0:nqt], lhsT=onesf[:, :], rhs=lac[0][0][:, 0:nqt], start=True, stop=True), reads=[b_onesf, lac[0][1]], writes=[bpl])
                        for (pl_, bpl_) in ((pl, bpl), acc[2]):
                            rc_, brc_ = f32_r.get()
                            fw.op(DVE, lambda h: h.reciprocal(out=rc_[:, 0:nqt], in_=pl_[:, 0:nqt]), reads=[bpl_], writes=[brc_])
                            rr_.append((rc_, brc_))
                        (r0, br0), (r1, br1) = rr_
                        o0, bo0 = f32_r.get(); o1, bo1 = f32_r.get()
                        fw.op(DVE, lambda h, o0=o0, r0=r0, p=acc[0][0]: h.tensor_tensor(out=o0[:, 0:nqt], in0=p[:, 0:nqt], in1=r0[:, 0:nqt], op=ALU.mult), reads=[acc[0][1], br0], writes=[bo0])
                        fw.op(DVE, lambda h, o1=o1, r1=r1, p=acc[1][0]: h.tensor_tensor(out=o1[:, 0:nqt], in0=p[:, 0:nqt], in1=r1[:, 0:nqt], op=ALU.mult), reads=[acc[1][1], br1], writes=[bo1])
                        fw.op(DVE, lambda h, o0=o0, o1=o1: h.scalar_tensor_tensor(out=o0[:, 0:nqt], in0=o1[:, 0:nqt], scalar=lamc[:, 0:1], in1=o0[:, 0:nqt], op0=ALU.mult, op1=ALU.add),
                              reads=[bo1, b_lamc], writes=[bo0])
                        sq, bsq = bf_r.get()
                        fw.op(ACT, lambda h, sq=sq, o0=o0: h.activation(out=sq[:, 0:nqt], in_=o0[:, 0:nqt], func=AF.Square), reads=[bo0], writes=[bsq])
                        ps, bps = psA.get()
                        fw.op(PE, lambda h, ps=ps, sq=sq: h.matmul(ps[:, 0:nqt], lhsT=cmat[:, C_128, :], rhs=sq[:, 0:nqt], start=True, stop=True), reads=[bsq, b_cmat], writes=[bps])
                        rs, brs = rs_r.get()
                        rsqrt_eps(ps[:, 0:nqt], rs[:, 0:nqt], bps, brs)
                        t1, bt1 = f32_r.get()
                        fw.op(DVE, lambda h, t1=t1, o0=o0, rs=rs: h.scalar_tensor_tensor(out=t1[:, 0:nqt], in0=o0[:, 0:nqt], scalar=gc[:, 10:11], in1=rs[:, 0:nqt], op0=ALU.mult, op1=ALU.mult),
                              reads=[bo0, brs, b_gc], writes=[bt1])
                        ob, bob = at_o.get()
                        fw.op(ACT, lambda h, t1=t1, ob=ob: h.activation(out=ob[:, 0:nqt], in_=t1[:, 0:nqt], func=AF.Copy, scale=(1.0 - lam_init)), reads=[bt1], writes=[bob])
                        fw.dma(SP, yc_s[head, :, qq:qq + nqt], ob[:, 0:nqt], reads=[bob], writes=[b_yc])

        arena.reset()
        y3_r = Ring(ov, "y3", [128, 12, TT], BF16, 1)
        mg_r = Ring(ov, "mg", [128, 8, TT], BF16, 1)
        macc_r = Ring(ov, "macc", [128, 4, TT], F32, 1)
        aTraw = ov("aTraw", [128, 8192], F32)
        aT_view = aTraw.bitcast(BF16).rearrange("p (a b) -> p a b", b=TT)
        xo_view = aTraw[:, 0:4096].rearrange("p (g f) -> p g f", f=D)
        b_aTraw = Buf()

        def phase3(l, ti, last):
            cond = 1 if ti > 0 else 0
            t0 = ti * TT
            xt, bxt = load_xT(ti)
            hT, bh = norm_mod(xt, bxt, A1, 0, cond)
            y3, by3 = y3_r.get()
            for bi, (src, bsrc) in enumerate(((ya_s, b_ya), (yb_s, b_yb), (yc_s, b_yc))):
                fw.dma(SP, y3[:, bi * 4:(bi + 1) * 4, :], src[:, :, t0:t0 + TT].rearrange("h p t -> p h t"), reads=[bsrc], writes=[by3])
            mg, bmg = mg_r.get()
            macc, bmacc = macc_r.get()
            for half in range(2):
                for bi in range(3):
                    c0 = O_G + bi * 1024 + half * 512
                    wg, bwg = load_w8(w_in[l, :, c0:c0 + 512], 512)
                    wo, bwo = w4.get()
                    fw.dma(POOL, wo[:, :, 0:512], w_o3[l, bi, :, half * 512:(half + 1) * 512].rearrange("(k p) c -> p k c", p=128), writes=[bwo])
                    for j in range(4):
                        oc = half * 4 + j
                        psg, bpsg = proj_fm(wg, bwg, j * 128, 128, lambda k: hT[:, k, :], [bh], 8)
                        sg, bsg = f32_r.get()
                        fw.op(ACT, lambda h: h.activation(out=sg[:], in_=psg[:], func=AF.Sigmoid), reads=[bpsg], writes=[bsg])
                        pso, bpso = proj_fm(wo, bwo, j * 128, 128, lambda k: y3[:, bi * 4 + k, :], [by3], 4)
                        if bi == 0:
                            fw.op(DVE, lambda h: h.tensor_tensor(out=macc[:, j, :], in0=sg[:], in1=pso[:], op=ALU.mult), reads=[bsg, bpso], writes=[bmacc])
                        else:
                            tmp, btmp = f32_r.get()
                            fw.op(DVE, lambda h: h.tensor_tensor(out=tmp[:], in0=sg[:], in1=pso[:], op=ALU.mult), reads=[bsg, bpso], writes=[btmp])
                            if bi == 1:
                                fw.op(DVE, lambda h: h.tensor_tensor(out=macc[:, j, :], in0=macc[:, j, :], in1=tmp[:], op=ALU.add), reads=[btmp], writes=[bmacc])
                            else:
                                fw.op(DVE, lambda h: h.tensor_tensor(out=mg[:, oc, :], in0=macc[:, j, :], in1=tmp[:], op=ALU.add), reads=[btmp, bmacc], writes=[bmg])
            for half in range(2):
                wt, bw = load_w8(w_out[l, :, half * 512:(half + 1) * 512], 512)
                for j in range(4):
                    oc = half * 4 + j
                    ps, bps = proj_fm(wt, bw, j * 128, 128, lambda k: mg[:, k, :], [bmg], 8)
                    fw.op(DVE, lambda h, oc=oc, ps=ps: h.scalar_tensor_tensor(out=xt[:, oc, :], in0=ps[:], scalar=modT[:, 16 + oc, cond:cond + 1], in1=xt[:, oc, :], op0=ALU.mult, op1=ALU.add),
                          reads=[bps, b_modT_b], writes=[bxt])
            h2, bh2 = norm_mod(xt, bxt, A2, 24, cond)
            aT, baT = aT_view, b_aTraw
            for fb in range(8):
                wt, bw = load_w8(w_m1[l, :, fb * 512:(fb + 1) * 512], 512)
                for j in range(4):
                    ps, bps = proj_fm(wt, bw, j * 128, 128, lambda k: h2[:, k, :], [bh2], 8)
                    rl, brl = f32_r.get()
                    fw.op(ACT, lambda h, ps=ps, rl=rl: h.activation(out=rl[:], in_=ps[:], func=AF.Relu), reads=[bps], writes=[brl])
                    fw.op(DVE, lambda h, rl=rl, fb=fb, j=j: h.tensor_tensor(out=aT[:, fb * 4 + j, :], in0=rl[:], in1=rl[:], op=ALU.mult), reads=[brl], writes=[baT])
            for half in range(2):
                accs = [psB.get() for _ in range(4)]
                for kb8 in range(4):
                    wt, bw = load_w8(w_m2[l, kb8 * 1024:(kb8 + 1) * 1024, half * 512:(half + 1) * 512], 512)
                    for j in range(4):
                        ps, bps = accs[j]
                        for k in range(8):
                            kk = kb8 * 8 + k
                            fw.op(PE, lambda h: h.matmul(ps[:], lhsT=wt[:, k, j * 128:(j + 1) * 128], rhs=aT[:, kk, :], start=(kk == 0), stop=(kk == 31)),
                                  reads=[bw, baT], writes=[bps], inc=(k == 7))
                for j in range(4):
                    oc = half * 4 + j
                    ps, bps = accs[j]
                    fw.op(DVE, lambda h: h.scalar_tensor_tensor(out=xt[:, oc, :], in0=ps[:], scalar=modT[:, 40 + oc, cond:cond + 1], in1=xt[:, oc, :], op0=ALU.mult, op1=ALU.add),
                          reads=[bps, b_modT_b], writes=[bxt])
            if not last:
                fw.dma(SP, xT_s[:, :, t0:t0 + TT].rearrange("k p t -> p k t"), xt[:], reads=[bxt], writes=[b_xT[ti]])
            else:
                xo, bxo = xo_view, b_aTraw
                for g in range(4):
                    for half in range(2):
                        ps, bps = psA.get()
                        for k in range(4):
                            kk = half * 4 + k
                            fw.op(PE, lambda h, g=g, k=k, kk=kk, ps=ps: h.transpose(ps[:, k * 128:(k + 1) * 128], xt[:, kk, g * 128:(g + 1) * 128], ident[:]),
                                  reads=[bxt, b_ident], writes=[bps], inc=(k == 3))
                        evac(ps[:], xo[:, g, half * 512:(half + 1) * 512], bps, bxo, eng=(ACT if half else DVE))
                ob = Buf(); out_bufs.append(ob)
                fw.dma(SP, y_out[t0:t0 + TT, :].rearrange("(g p) f -> p g f", p=128), xo[:], reads=[bxo], writes=[ob])

        import os
        KSTOP = int(os.environ.get("KSTOP", "99"))
        step = [0]

        def reached():
            step[0] += 1
            return step[0] > KSTOP

        for l in range(L):
            if reached():
                break
            lam_init = layer_setup(l)
            fw.barrier()
            vaug_init[0] = vaug_init[1] = False
            if reached():
                break
            prep_cache(l)
            if reached():
                break
            for ti in range(NT):
                phase1(l, ti)
            fw.barrier()
            if reached():
                break
            seqs = [(0, 256, 0, False), (256, 256, 1, False), (NP_TOK, NS_TOK, None, True)]
            gla_dir(l, 0, seqs)
            fw.barrier()
            gla_dir(l, 1, seqs)
            fw.barrier()
            if reached():
                break
            groups = [(0, 256, 0, 256), (256, 256, 256, 256), (NP_TOK, NS_TOK, NP_TOK, TKS)]
            mla_attn(l, groups)
            if reached():
                break
            diff_attn(l, groups, lam_init)
            if reached():
                break
            fw.barrier()
            for ti in range(NT):
                phase3(l, ti, l == L - 1)
            fw.barrier()

        fw.finish(out_bufs)
    return nc


def _rope_tables():
    t = np.arange(NS_TOK)
    row, col = t // 64, t % 64

    def cs(nf, pos):
        inv = (10000.0 ** (-np.arange(nf, dtype=np.float32) / nf)).astype(np.float32)
        ang = pos.astype(np.float32)[None, :] * inv[:, None]
        c, s = np.cos(ang).astype(np.float32), np.sin(ang).astype(np.float32)
        return np.concatenate([c, c], 0), np.concatenate([s, s], 0)

    cr, sr = cs(8, row); cc, sc_ = cs(8, col)
    cM = np.concatenate([cr, cc, np.ones((64, NS_TOK), np.float32)], 0)
    sM = np.concatenate([sr, sc_, np.zeros((64, NS_TOK), np.float32)], 0)
    cr, sr = cs(16, row); cc, sc_ = cs(16, col)
    c64 = np.concatenate([cr, cc], 0); s64 = np.concatenate([sr, sc_], 0)
    cD = np.concatenate([c64, c64], 0); sD = np.concatenate([s64, s64], 0)
    return np.stack([cM, sM]).astype(np.float32), np.stack([cD, sD]).astype(np.float32)


def _rot_matrix(nf):
    R = np.zeros((2 * nf, 2 * nf), np.float32)
    for i in range(nf):
        R[i, nf + i] = -1.0
        R[nf + i, i] = 1.0
    return R


def _consts():
    bf = ml_dtypes.bfloat16
    cmat = np.zeros((128, 8, 128), np.float32)
    cmat[:, 0, :] = 1.0 / 1024
    cmat[:, 1, :] = 1.0 / 384
    cmat[:, 2, :] = 1.0 / 256
    cmat[:96, 3, :96] = 1.0 / 96
    cmat[:64, 4, :64] = 1.0 / 64
    cmat[64:, 4, 64:] = 1.0 / 64
    cmat[:, 5, :] = 1.0 / 128
    cmat[:, 6, :] = 1.0
    cmat[:32, 7, :32] = np.eye(32)
    pm = np.zeros((128, 3, 128), np.float32)
    R96 = np.zeros((96, 96), np.float32)
    R96[0:16, 0:16] = _rot_matrix(8); R96[16:32, 16:32] = _rot_matrix(8)
    pm[:96, 0, :96] = R96.T
    R64 = np.zeros((64, 64), np.float32)
    R64[0:32, 0:32] = _rot_matrix(16); R64[32:64, 32:64] = _rot_matrix(16)
    R128 = np.zeros((128, 128), np.float32)
    R128[:64, :64] = R64; R128[64:, 64:] = R64
    pm[:, 1, :] = R128.T
    i = np.arange(64)
    tri = np.zeros((64, 4, 64), np.float32)
    tri[:, 0, :] = (i[:, None] <= i[None, :])
    tri[:, 1, :] = (i[:, None] >= i[None, :])
    tri[:, 2, :] = (i[:, None] > i[None, :])
    tri[:, 3, :] = (i[:, None] < i[None, :])
    return cmat.astype(bf), pm.astype(bf), tri


SAMPLE_CORES = [0, 1, 4, 5]

_PROG = {}


def _get_prog():
    if "nc" not in _PROG:
        _PROG["nc"] = build_program()
    return _PROG["nc"]


def make_in_maps(x_prompt, x_sample, state_gla, cache_mla_ckv, cache_mla_krope, cache_diff_k, cache_diff_v, c, c_ctx, w_mod, b_mod, g_norm1, g_norm2, w_in,
                 w_gla_a2, b_gla_a, g_gla_out, g_mla_qa, g_mla_kva, w_mla_uq, w_mla_uk, w_mla_uv, g_mla_q, g_mla_k, g_diff_q, g_diff_k, lam_qk, g_diff_sub,
                 w_o_gla, w_o_mla, w_o_diff, w_out, w_mlp1, w_mlp2):
    f = lambda a: np.ascontiguousarray(np.asarray(a, dtype=np.float32))
    cmat, pm, tri = _consts()
    ropeM, ropeD = _rope_tables()
    perm = np.concatenate([np.arange(64, 96), np.arange(0, 64)])
    w_uq_p = f(w_mla_uq).reshape(L, 384, 8, 96)[:, :, :, perm]
    w_ukp = np.zeros((L, 256, 8, 96), np.float32)
    w_ukp[:, :, :, 32:] = f(w_mla_uk).reshape(L, 256, 8, 64)
    gcol = np.zeros((L, 128, 12), np.float32)
    gcol[:, :, 0:3] = f(g_mla_qa).reshape(L, 3, 128).transpose(0, 2, 1)
    gcol[:, :, 3:5] = f(g_mla_kva).reshape(L, 2, 128).transpose(0, 2, 1)
    gcol[:, :96, 5] = f(g_mla_q)[:, perm]
    gcol[:, :96, 6] = f(g_mla_k)[:, perm]
    gcol[:, :, 7] = np.tile(f(g_diff_q), (1, 2))
    gcol[:, :, 8] = np.tile(f(g_diff_k), (1, 2))
    gcol[:, :, 9] = f(g_gla_out)
    gcol[:, :, 10] = f(g_diff_sub)
    w_a2bd = np.zeros((L, 32, 512), np.float32)
    w_a2bd[:, 0:16, 0:256] = f(w_gla_a2)[:, 0]
    w_a2bd[:, 16:32, 256:512] = f(w_gla_a2)[:, 1]
    b_a = f(b_gla_a).reshape(L, 1, 512)
    gnT = np.concatenate([f(g_norm1).reshape(L, 8, 128).transpose(0, 2, 1), f(g_norm2).reshape(L, 8, 128).transpose(0, 2, 1)], axis=2)
    b_modT = f(b_mod).reshape(L, 48, 128).transpose(0, 2, 1)
    w_o3 = np.stack([f(w_o_gla), f(w_o_mla), f(w_o_diff)], axis=1)
    shared = dict(w_mod=f(w_mod), b_modT=f(b_modT), gnT=f(gnT), w_in=f(w_in), w_a2bd=w_a2bd, b_a=b_a, gcol=gcol, w_uq=f(w_uq_p), w_ukp=w_ukp,
                  w_uv=f(w_mla_uv), lam_qk=f(lam_qk).reshape(L, 1, 256), w_o3=f(w_o3), w_out=f(w_out), w_m1=f(w_mlp1), w_m2=f(w_mlp2),
                  ident=np.eye(128, dtype=np.float32), cmat=cmat, pmat=pm, tri=tri, ropeM=ropeM, ropeD=ropeD)
    xp, xs = f(x_prompt), f(x_sample)
    in_maps = []
    for core in range(8):
        m = dict(shared)
        if core in SAMPLE_CORES:
            b = SAMPLE_CORES.index(core)
            xs_b, c_b = xs[b], f(c)[b]
            m["st_gla"] = f(f(state_gla)[b].transpose(0, 1, 3, 2, 4))
            m["c_ckv"] = f(cache_mla_ckv)[b]
            m["c_kr"] = f(cache_mla_krope)[b]
            m["c_dk"] = f(cache_diff_k)[b].reshape(L, PAST, 512)
            m["c_dv"] = f(cache_diff_v)[b].reshape(L, PAST, 512)
        else:
            xs_b, c_b = np.zeros((NS_TOK, D), np.float32), np.zeros((D,), np.float32)
            m["st_gla"] = np.zeros((L, 2, 64, 4, 128), np.float32)
            m["c_ckv"] = np.zeros((L, PAST, 256), np.float32)
            m["c_kr"] = np.zeros((L, PAST, 32), np.float32)
            m["c_dk"] = np.zeros((L, PAST, 512), np.float32)
            m["c_dv"] = np.zeros((L, PAST, 512), np.float32)
        m["xin"] = np.concatenate([xp[2 * core], xp[2 * core + 1], xs_b], axis=0)
        cond = np.stack([f(c_ctx), c_b], axis=0)
        m["condT"] = f(cond.reshape(2, 8, 128).transpose(2, 1, 0))
        in_maps.append(m)
    return in_maps


def kernel(**inputs):
    nc = _get_prog()
    in_maps = make_in_maps(**inputs)
    res = run_bass_kernel_spmd(nc, in_maps, core_ids=list(range(8)))
    R = res.results
    y_prompt = np.zeros((16, 256, D), np.float32)
    y_sample = np.zeros((4, NS_TOK, D), np.float32)
    n_gla = np.zeros((16, L, 2, 4, 64, 128), np.float32)
    n_ckv = np.zeros((16, L, 256, 256), np.float32)
    n_kr = np.zeros((16, L, 256, 32), np.float32)
    n_dk = np.zeros((16, L, 256, 4, 2, 64), np.float32)
    n_dv = np.zeros((16, L, 256, 4, 128), np.float32)
    for core in range(8):
        r = R[core]
        for s in range(2):
            bi = 2 * core + s
            y_prompt[bi] = r["y_out"][s * 256:(s + 1) * 256]
            n_gla[bi] = r["o_gla"][:, s].transpose(0, 1, 3, 2, 4)
            n_ckv[bi] = r["o_ckv"][:, s * 256:(s + 1) * 256]
            n_kr[bi] = r["o_kr"][:, s * 256:(s + 1) * 256]
            n_dk[bi] = r["o_dk"][:, s * 256:(s + 1) * 256].reshape(L, 256, 4, 2, 64)
            n_dv[bi] = r["o_dv"][:, s * 256:(s + 1) * 256].reshape(L, 256, 4, 128)
        if core in SAMPLE_CORES:
            y_sample[SAMPLE_CORES.index(core)] = r["y_out"][NP_TOK:]
    return (y_prompt, y_sample, n_gla, n_ckv, n_kr, n_dk, n_dv)
```

```python
import math
import numpy as np
import ml_dtypes
from contextlib import ExitStack
import concourse.bass as bass
import concourse.mybir as mybir
from concourse.bass_utils import run_bass_kernel_spmd

F32 = mybir.dt.float32
BF16 = mybir.dt.bfloat16
AF = mybir.ActivationFunctionType
ALU = mybir.AluOpType

D = 1024
L = 2
NP_TOK = 512
NS_TOK = 4096
T = NP_TOK + NS_TOK
TT = 512
NT = T // TT
PAST = 256
TKS = PAST + NS_TOK
EPS = 1e-6
O_AQ, O_AK, O_AV, O_AR, O_AA, O_QD, O_KVD, O_KR, O_DQ, O_DK, O_DV, O_G = 0, 256, 512, 1024, 1536, 1568, 1952, 2208, 2240, 2752, 3264, 3776
IN_COLS = 6848


class Buf:
    __slots__ = ("lw", "rd", "psum")

    def __init__(self, psum=False):
        self.lw = None
        self.rd = []
        self.psum = psum


class _Rec:
    def __init__(self):
        self.call = None

    def __getattr__(self, name):
        def f(*a, **k):
            self.call = (name, a, k)
            return self
        return f


class Eng:
    def __init__(self, fw, name, handle):
        self.name = name
        self.h = handle
        self.sem = fw.new_sem("e_" + name)
        self.count = 0
        self.waited = {}
        self.thunks = []
        self.snaps = {}


class FW:
    def __init__(self, nc, ctx, n_dma_sems=48):
        self.nc = nc
        self.ctx = ctx
        self.pe = Eng(self, "pe", nc.tensor)
        self.dve = Eng(self, "dve", nc.vector)
        self.act = Eng(self, "act", nc.scalar)
        self.pool = Eng(self, "pool", nc.gpsimd)
        self.sp = Eng(self, "sp", nc.sync)
        self.dma_sems = [self.new_sem(f"d{i}") for i in range(n_dma_sems)]
        self.dma_cnt = [0] * n_dma_sems
        self.snap_of = {}
        self.dma_rr = 0
        self.ew_rr = 0

    def new_sem(self, name):
        return self.ctx.enter_context(self.nc.semaphore(name))

    def _wait(self, E, ev):
        if ev is None:
            return
        sem, val = ev
        if sem is E.sem and E.name == "pe":
            return
        key = id(sem)
        if E.waited.get(key, 0) >= val:
            return
        E.waited[key] = val
        E.thunks.append(lambda h=E.h, s=sem, v=val: h.wait_ge(s, v))
        snap = self.snap_of.get((key, val))
        if snap:
            w = E.waited
            for k2, v2 in snap.items():
                if w.get(k2, 0) < v2:
                    w[k2] = v2

    def _deps(self, E, reads, writes):
        for b in reads:
            self._wait(E, b.lw)
        for b in writes:
            self._wait(E, b.lw)
            for ev in b.rd:
                self._wait(E, ev)

    def _post(self, ev, reads, writes):
        for b in reads:
            b.rd.append(ev)
            if len(b.rd) > 64:
                b.rd = b.rd[-48:]
        for b in writes:
            b.lw = ev
            b.rd = []

    def op(self, E, fn, reads=(), writes=(), inc=True):
        if any(b.psum for b in reads):
            writes = list(writes) + [b for b in reads if b.psum]
            reads = [b for b in reads if not b.psum]
        self._deps(E, reads, writes)
        rec = _Rec()
        fn(rec)
        mname, margs, mkw = rec.call
        if inc:
            E.count += 1
            ev = (E.sem, E.count)
            self.snap_of[(id(E.sem), E.count)] = dict(E.waited)
            E.thunks.append(lambda h=E.h, n=mname, a=margs, k=mkw, s=E.sem: getattr(h, n)(*a, **k).then_inc(s, 1))
        else:
            ev = (E.sem, E.count + 1)
            E.thunks.append(lambda h=E.h, n=mname, a=margs, k=mkw: getattr(h, n)(*a, **k))
        self._post(ev, reads, writes)
        return ev

    def dma(self, Q, out_ap, in_ap, reads=(), writes=(), **kw):
        self._deps(Q, reads, writes)
        i = self.dma_rr
        self.dma_rr = (self.dma_rr + 1) % len(self.dma_sems)
        self.dma_cnt[i] += 16
        sem = self.dma_sems[i]
        ev = (sem, self.dma_cnt[i])
        self.snap_of[(id(sem), self.dma_cnt[i])] = dict(Q.waited)
        Q.thunks.append(lambda h=Q.h, o=out_ap, a=in_ap, s=sem, k=kw: h.dma_start(out=o, in_=a, **k).then_inc(s, 16))
        self._post(ev, reads, writes)
        return ev

    def barrier(self):
        engs = [self.pe, self.dve, self.act, self.pool, self.sp]
        for E in engs:
            for E2 in engs:
                if E2 is not E and E2.count > 0:
                    self._wait(E, (E2.sem, E2.count))
            for i, sem in enumerate(self.dma_sems):
                if self.dma_cnt[i] > 0:
                    self._wait(E, (sem, self.dma_cnt[i]))

    def finish(self, final_bufs):
        for b in final_bufs:
            self._wait(self.sp, b.lw)
        nc = self.nc
        engs = self
        with nc.Block() as block:
            @block.tensor
            def _(e):
                for t in engs.pe.thunks:
                    t()

            @block.vector
            def _(e):
                for t in engs.dve.thunks:
                    t()

            @block.scalar
            def _(e):
                for t in engs.act.thunks:
                    t()

            @block.gpsimd
            def _(e):
                for t in engs.pool.thunks:
                    t()

            @block.sync
            def _(e):
                for t in engs.sp.thunks:
                    t()


class Arena:
    def __init__(self, tensor, nbytes):
        self.t = tensor
        self.nbytes = nbytes
        self.off = 0
        self.peak = 0

    def reset(self):
        self.off = 0

    def alloc(self, name, shape, dt):
        esz = 2 if dt == BF16 else 4
        n = 1
        for d in shape[1:]:
            n *= d
        nb = (n * esz + 63) // 64 * 64
        assert self.off + nb <= self.nbytes, (name, self.off, nb, self.nbytes)
        ap = self.t[0:shape[0], self.off // 4:(self.off + nb) // 4]
        self.off += nb
        self.peak = max(self.peak, self.off)
        if dt == BF16:
            ap = ap.bitcast(BF16)
        ap = ap[:, 0:n]
        if len(shape) == 3:
            ap = ap.rearrange("p (a b) -> p a b", b=shape[2])
        elif len(shape) == 4:
            ap = ap.rearrange("p (a b c) -> p a b c", b=shape[2], c=shape[3])
        return ap


class Ring:
    def __init__(self, alloc, name, shape, dt, n, psum=False):
        self.t = [alloc(f"{name}{i}", shape, dt) for i in range(n)]
        self.b = [Buf(psum) for _ in range(n)]
        self.i = 0

    def get(self):
        i = self.i
        self.i = (i + 1) % len(self.t)
        return self.t[i], self.b[i]


def build_program(debug=False):
    nc = bass.Bass("TRN2", target_bir_lowering=False)
    dram_in = lambda name, shape, dt=F32: nc.dram_tensor(name, list(shape), dt, kind="ExternalInput").ap()
    dram_out = lambda name, shape, dt=F32: nc.dram_tensor(name, list(shape), dt, kind="ExternalOutput").ap()
    dram_tmp = lambda name, shape, dt=F32: nc.dram_tensor(name, list(shape), dt).ap()

    xin = dram_in("xin", [T, D])
    condT = dram_in("condT", [128, 8, 2])
    st_gla = dram_in("st_gla", [L, 2, 64, 4, 128])
    c_ckv = dram_in("c_ckv", [L, PAST, 256])
    c_kr = dram_in("c_kr", [L, PAST, 32])
    c_dk = dram_in("c_dk", [L, PAST, 512])
    c_dv = dram_in("c_dv", [L, PAST, 512])
    w_mod = dram_in("w_mod", [L, D, 6 * D])
    b_modT = dram_in("b_modT", [L, 128, 48])
    gnT = dram_in("gnT", [L, 128, 16])
    w_in = dram_in("w_in", [L, D, IN_COLS])
    w_a2bd = dram_in("w_a2bd", [L, 32, 512])
    b_a = dram_in("b_a", [L, 1, 512])
    gcol = dram_in("gcol", [L, 128, 12])
    w_uq = dram_in("w_uq", [L, 384, 8, 96])
    w_ukp = dram_in("w_ukp", [L, 256, 8, 96])
    w_uv = dram_in("w_uv", [L, 256, 512])
    lam_qk = dram_in("lam_qk", [L, 1, 256])
    w_o3 = dram_in("w_o3", [L, 3, 512, D])
    w_out = dram_in("w_out", [L, D, D])
    w_m1 = dram_in("w_m1", [L, D, 4 * D])
    w_m2 = dram_in("w_m2", [L, 4 * D, D])
    ident_d = dram_in("ident", [128, 128])
    cmat_d = dram_in("cmat", [128, 8, 128], BF16)
    pmat_d = dram_in("pmat", [128, 3, 128], BF16)
    tri_d = dram_in("tri", [64, 4, 64])
    ropeM = dram_in("ropeM", [2, 96, NS_TOK])
    ropeD = dram_in("ropeD", [2, 128, NS_TOK])

    y_out = dram_out("y_out", [T, D])
    o_gla = dram_out("o_gla", [L, 2, 2, 64, 4, 128])
    o_ckv = dram_out("o_ckv", [L, NP_TOK, 256])
    o_kr = dram_out("o_kr", [L, NP_TOK, 32])
    o_dk = dram_out("o_dk", [L, NP_TOK, 512])
    o_dv = dram_out("o_dv", [L, NP_TOK, 512])

    mk = dram_out if debug else dram_tmp
    xT_s = dram_tmp("xT_s", [8, 128, T])
    gq_s = dram_tmp("gq_s", [64, 4, T], BF16)
    gk_s = dram_tmp("gk_s", [64, 4, T], BF16)
    gkt_s = dram_tmp("gkt_s", [T, 256], BF16)
    gvt_s = dram_tmp("gvt_s", [T, 512], BF16)
    gG_s = dram_tmp("gG_s", [T, 512])
    go_s = dram_tmp("go_s", [128, 4, T])
    q_s = dram_tmp("q_s", [8, 96, T], BF16)
    k_s = dram_tmp("k_s", [8, 96, NP_TOK + TKS], BF16)
    v_s = dram_tmp("v_s", [8, 128, (NP_TOK + TKS) // 128, 128], BF16)
    dq_s = dram_tmp("dq_s", [4, 128, T], BF16)
    dk_s = dram_tmp("dk_s", [4, 128, NP_TOK + TKS], BF16)
    dv_s = dram_tmp("dv_s", [4, 128, (NP_TOK + TKS) // 128, 128], BF16)
    hT_s = dram_tmp("hT_s", [128, 8, T], BF16)
    ya_s = mk("ya_s", [4, 128, T], BF16)
    yb_s = mk("yb_s", [4, 128, T], BF16)
    yc_s = mk("yc_s", [4, 128, T], BF16)

    with ExitStack() as ctx:
        fw = FW(nc, ctx)
        PE, DVE, ACT, POOL, SP = fw.pe, fw.dve, fw.act, fw.pool, fw.sp
        sb = lambda name, shape, dt=F32: ctx.enter_context(nc.sbuf_tensor("s_" + name, list(shape), dt))
        psb = lambda name, shape, dt=F32: ctx.enter_context(nc.psum_tensor("p_" + name, list(shape), dt))

        psA = Ring(psb, "psA", [128, 512], F32, 4, psum=True)
        psB = Ring(psb, "psB", [128, 512], F32, 4, psum=True)

        OVB = 72 * 1024
        arena = Arena(sb("arena", [128, OVB // 4], F32), OVB)
        ov = arena.alloc

        def ew():
            fw.ew_rr ^= 1
            return DVE if fw.ew_rr else POOL

        ident = sb("ident", [128, 128]); b_ident = Buf()
        cmat = sb("cmat", [128, 8, 128], BF16); b_cmat = Buf()
        pmat = sb("pmat", [128, 3, 128], BF16); b_pmat = Buf()
        tri = sb("tri", [64, 4, 64]); b_tri = Buf()
        ones_row = sb("ones_row", [1, 128], BF16); b_onesrow = Buf()
        ones_rowf = sb("ones_rowf", [1, 128]); b_onesrowf = Buf()
        ones_col = sb("ones_col", [64, 1]); b_onescol = Buf()
        onesf = sb("onesf", [128, 128]); b_onesf = Buf()
        fw.dma(SP, ident[:], ident_d, writes=[b_ident])
        fw.dma(SP, cmat[:], cmat_d, writes=[b_cmat])
        fw.dma(SP, pmat[:], pmat_d, writes=[b_pmat])
        fw.dma(SP, tri[:], tri_d, writes=[b_tri])
        fw.op(DVE, lambda h: h.memset(ones_row[:], 1.0), writes=[b_onesrow])
        fw.op(DVE, lambda h: h.memset(ones_rowf[:], 1.0), writes=[b_onesrowf])
        fw.op(DVE, lambda h: h.memset(ones_col[:], 1.0), writes=[b_onescol])
        fw.op(DVE, lambda h: h.memset(onesf[:], 1.0), writes=[b_onesf])
        C_1024, C_384, C_256, C_96, C_BLK64, C_128, C_ONE, C_SEL32 = range(8)
        P_96, P_128, _ = range(3)

        cT = sb("cT", [128, 8, 2]); b_cT = Buf()
        scT = sb("scT", [128, 8, 2]); b_scT = Buf()
        fw.dma(SP, cT[:], condT, writes=[b_cT])
        fw.op(ACT, lambda h: h.activation(out=scT[:], in_=cT[:], func=AF.Silu), reads=[b_cT], writes=[b_scT])

        modT = sb("modT", [128, 48, 2]); b_modT_b = Buf()
        bmod = sb("bmod", [128, 48]); b_bmod = Buf()
        gn = sb("gn", [128, 16]); b_gn = Buf()
        A1 = sb("A1", [128, 8, 2]); A2 = sb("A2", [128, 8, 2]); b_A = Buf()
        gc = sb("gc", [128, 12]); b_gc = Buf()
        wa2 = sb("wa2", [32, 512], BF16); b_wa2 = Buf()
        ba = sb("ba", [1, 512], BF16); b_ba = Buf()
        lamt = sb("lamt", [1, 256]); b_lamt = Buf()
        lam1 = sb("lam1", [1, 8]); b_lam1 = Buf()
        lamc = sb("lamc", [128, 2]); b_lamc = Buf()
        arena.reset()
        wmod_r = Ring(ov, "wmod", [128, 8, 256], F32, 2)

        w8 = Ring(sb, "w8", [128, 8, 512], BF16, 3)
        w4 = Ring(sb, "w4", [128, 4, 1024], BF16, 2)

        def load_w8(src_ap_rows_cols, ncols, nk=8):
            t, b = w8.get()
            fw.dma(POOL, t[:, 0:nk, 0:ncols], src_ap_rows_cols.rearrange("(k p) c -> p k c", p=128), writes=[b])
            return t, b

        xt_r = Ring(sb, "xt", [128, 8, TT], F32, 1)
        sq_r = Ring(sb, "sq", [128, 8, TT], BF16, 1)
        hT_r = Ring(sb, "hT", [128, 8, TT], BF16, 2)
        rs_r = Ring(sb, "rs", [128, TT], F32, 4)
        f32_r = Ring(sb, "f32", [128, TT], F32, 12)
        bf_r = Ring(sb, "bf", [128, TT], BF16, 12)

        def evac(ps_ap, out_ap, b_ps, b_out, eng=None, func=None, scale=1.0):
            if func is not None or eng is ACT:
                f = func if func is not None else AF.Copy
                return fw.op(ACT, lambda h: h.activation(out=out_ap, in_=ps_ap, func=f, scale=scale), reads=[b_ps], writes=[b_out])
            return fw.op(DVE, lambda h: h.tensor_copy(out=out_ap, in_=ps_ap), reads=[b_ps], writes=[b_out])

        b_xT = [Buf() for _ in range(NT)]
        arena.reset()
        xtok_r = Ring(ov, "xtok", [128, 4, D], F32, 2)

        def phase0(ti):
            xk, bxk = xtok_r.get()
            fw.dma(SP, xk[:], xin[ti * TT:(ti + 1) * TT, :].rearrange("(g p) f -> p g f", p=128), writes=[bxk])
            xt, bxt = xt_r.get()
            for k in range(8):
                ps, bps = psA.get()
                for g in range(4):
                    fw.op(PE, lambda h, ps=ps, g=g, k=k: h.transpose(ps[:, g * 128:(g + 1) * 128], xk[:, g, k * 128:(k + 1) * 128], ident[:]),
                          reads=[bxk, b_ident], writes=[bps], inc=(g == 3))
                evac(ps[:], xt[:, k, :], bps, bxt, eng=(ACT if k % 2 else DVE))
            fw.dma(SP, xT_s[:, :, ti * TT:(ti + 1) * TT].rearrange("k p t -> p k t"), xt[:], reads=[bxt], writes=[b_xT[ti]])

        import os
        for ti in range(int(os.environ.get("KPH0", str(NT)))):
            phase0(ti)
        fw.barrier()

        def rsqrt_eps(src_ap, dst_ap, bsrc, bdst):
            fw.op(ACT, lambda h: h.activation(out=dst_ap, in_=src_ap, func=AF.Ln, bias=EPS), reads=[bsrc], writes=[bdst])
            fw.op(ACT, lambda h: h.activation(out=dst_ap, in_=dst_ap, func=AF.Exp, scale=-0.5), reads=[bdst], writes=[bdst])

        def rms_rstd(sq_aps, ones_ap, nparts, eps=EPS):
            ps, bps = psA.get()
            n = len(sq_aps)
            for i, (a, b) in enumerate(sq_aps):
                fw.op(PE, lambda h, a=a, i=i: h.matmul(ps[0:nparts, :], lhsT=ones_ap, rhs=a, start=(i == 0), stop=(i == n - 1)),
                      reads=[b, b_cmat], writes=[bps], inc=(i == n - 1))
            rs, brs = rs_r.get()
            rsqrt_eps(ps[0:nparts, :], rs[0:nparts, :], bps, brs)
            return rs, brs

        def load_xT(ti):
            xt, bxt = xt_r.get()
            fw.dma(SP, xt[:], xT_s[:, :, ti * TT:(ti + 1) * TT].rearrange("k p t -> p k t"), reads=[b_xT[ti]], writes=[bxt])
            return xt, bxt

        def norm_mod(xt, bxt, Amod, shift_lo, cond):
            sq, bsq = sq_r.get()
            fw.op(ACT, lambda h: h.activation(out=sq[:], in_=xt[:], func=AF.Square), reads=[bxt], writes=[bsq])
            rs, brs = rms_rstd([(sq[:, k, :], bsq) for k in range(8)], cmat[:, C_1024, :], 128)
            hT, bh = hT_r.get()
            for k in range(8):
                tmp, btmp = f32_r.get()
                e = DVE
                fw.op(e, lambda h, k=k, tmp=tmp: h.tensor_tensor(out=tmp[:], in0=xt[:, k, :], in1=rs[:], op=ALU.mult), reads=[bxt, brs], writes=[btmp])
                fw.op(ACT, lambda h, k=k, tmp=tmp: h.activation(out=hT[:, k, :], in_=tmp[:], func=AF.Identity,
                                                               scale=Amod[:, k, cond:cond + 1], bias=modT[:, shift_lo + k, cond:cond + 1]),
                      reads=[btmp, b_A, b_modT_b], writes=[bh])
            return hT, bh

        def proj_fm(wt, bw, col0, m, rhs_fn, rhs_bufs, nk, out_parts=None):
            ps, bps = psA.get()
            for k in range(nk):
                fw.op(PE, lambda h, k=k: h.matmul(ps[0:m, :], lhsT=wt[:, k, col0:col0 + m], rhs=rhs_fn(k), start=(k == 0), stop=(k == nk - 1)),
                      reads=[bw] + rhs_bufs, writes=[bps], inc=(k == nk - 1))
            return ps, bps

        def norm_rope_gen(ps, bps, npart, ones_ap, gcolumn, rope, pm_idx, ropetab, dst_ap, dst_buf, keep_f32=None):
            xf, bxf = f32_r.get()
            evac(ps[0:npart, :], xf[0:npart, :], bps, bxf, eng=DVE)
            sq, bsq = bf_r.get()
            fw.op(ACT, lambda h: h.activation(out=sq[0:npart, :], in_=xf[0:npart, :], func=AF.Square), reads=[bxf], writes=[bsq])
            yield
            rs, brs = rms_rstd([(sq[0:npart, :], bsq)], ones_ap, npart)
            xn, bxn = f32_r.get()
            fw.op(DVE, lambda h: h.scalar_tensor_tensor(out=xn[0:npart, :], in0=xf[0:npart, :], scalar=gc[0:npart, gcolumn:gcolumn + 1], in1=rs[0:npart, :],
                                                        op0=ALU.mult, op1=ALU.mult), reads=[bxf, brs, b_gc], writes=[bxn])
            if keep_f32 is not None:
                keep_f32(xn, bxn)
            ob, bob = bf_r.get()
            if not rope:
                fw.op(ACT, lambda h: h.activation(out=ob[0:npart, :], in_=xn[0:npart, :], func=AF.Copy), reads=[bxn], writes=[bob])
            else:
                ctab, stab, btab = ropetab
                xb, bxb = bf_r.get()
                fw.op(ACT, lambda h: h.activation(out=xb[0:npart, :], in_=xn[0:npart, :], func=AF.Copy), reads=[bxn], writes=[bxb])
                t1, bt1 = f32_r.get()
                fw.op(DVE, lambda h: h.tensor_tensor(out=t1[0:npart, :], in0=xn[0:npart, :], in1=ctab, op=ALU.mult), reads=[bxn, btab], writes=[bt1])
                yield
                ps2, bps2 = psA.get()
                fw.op(PE, lambda h: h.matmul(ps2[0:npart, :], lhsT=pmat[0:npart, pm_idx, 0:npart], rhs=xb[0:npart, :], start=True, stop=True),
                      reads=[bxb, b_pmat], writes=[bps2])
                t2, bt2 = f32_r.get()
                fw.op(DVE, lambda h: h.tensor_tensor(out=t2[0:npart, :], in0=ps2[0:npart, :], in1=stab, op=ALU.mult), reads=[bps2, btab], writes=[bt2])
                fw.op(DVE, lambda h: h.tensor_tensor(out=ob[0:npart, :], in0=t1[0:npart, :], in1=t2[0:npart, :], op=ALU.add), reads=[bt1, bt2], writes=[bob])
            fw.dma(SP, dst_ap, ob[0:npart, :], reads=[bob], writes=[dst_buf])

        def norm_rope_store(*a, **k):
            for _ in norm_rope_gen(*a, **k):
                pass

        class Pipe:
            def __init__(self):
                self.gens = []

            def push(self, gen):
                next(gen, None)
                for og in list(self.gens):
                    try:
                        next(og)
                    except StopIteration:
                        self.gens.remove(og)
                self.gens.append(gen)

            def drain(self):
                while self.gens:
                    for og in list(self.gens):
                        try:
                            next(og)
                        except StopIteration:
                            self.gens.remove(og)

        def transpose_out(src_fn, src_bufs, nfeat_parts, ncols_total, dst_rows_fn, colslices):
            for g in range(4):
                ps, bps = psA.get()
                n = len(colslices)
                for i, (j, pj, c0) in enumerate(colslices):
                    fw.op(PE, lambda h, j=j, pj=pj, c0=c0, g=g: h.transpose(ps[:, c0:c0 + pj], src_fn(j)[:, g * 128:(g + 1) * 128], ident[0:pj, 0:pj]),
                          reads=src_bufs + [b_ident], writes=[bps], inc=(i == n - 1))
                o, bo = f32_r.get()
                evac(ps[:, 0:ncols_total], o[:, 0:ncols_total], bps, bo, eng=DVE)
                fw.dma(SP, dst_rows_fn(g), o[:, 0:ncols_total], reads=[bo], writes=[Buf()])

        b_hTs = [Buf() for _ in range(NT)]
        p3_pref = {}
        b_gq = Buf(); b_gk = Buf(); b_gkt = Buf(); b_gvt = Buf(); b_gG = Buf(); b_go = Buf()
        b_q = Buf(); b_k = Buf(); b_v = Buf(); b_dq = Buf(); b_dk = Buf(); b_dv = Buf()
        b_ya = Buf(); b_yb = Buf(); b_yc = Buf()
        out_bufs = []

        arena.reset()
        ropM_r = Ring(ov, "ropM", [96, 2, TT], F32, 1)
        ropD_r = Ring(ov, "ropD", [128, 2, TT], F32, 1)
        vaug_r = Ring(ov, "vaug", [128, 8, 4, 128], BF16, 1)
        tok_r = Ring(ov, "tok", [128, 4, 512], BF16, 2)
        tokf_r = Ring(ov, "tokf", [128, 4, 512], F32, 1)
        keep_r = Ring(ov, "keep", [128, 4, TT], F32, 1)
        qdn_r = Ring(ov, "qdn", [128, 3, TT], BF16, 1)
        ckv_r = Ring(ov, "ckv", [128, 2, TT], BF16, 1)
        ckvf_r = Ring(ov, "ckvf", [128, 2, TT], F32, 1)
        krb_r = Ring(ov, "krb", [32, TT], BF16, 1)
        krf_r = Ring(ov, "krf", [32, TT], F32, 1)
        aab_r = Ring(ov, "aab", [32, TT], BF16, 1)
        g4_r = Ring(ov, "g4", [64, 4, TT], BF16, 1)
        r4_r = Ring(ov, "r4", [128, 4, TT], BF16, 1)
        ctok_r = Ring(ov, "ctok", [128, 2, 512], F32, 2)

        def layer_setup(l):
            fw.dma(SP, bmod[:], b_modT[l], writes=[b_bmod])
            fw.dma(SP, gn[:], gnT[l], writes=[b_gn])
            fw.dma(SP, gc[:], gcol[l], writes=[b_gc])
            fw.dma(POOL, wa2[:], w_a2bd[l], writes=[b_wa2])
            fw.dma(POOL, ba[:], b_a[l], writes=[b_ba])
            fw.dma(SP, lamt[:], lam_qk[l], writes=[b_lamt])
            for cb in range(24):
                wm, bwm = wmod_r.get()
                fw.dma(SP, wm[:], w_mod[l, :, cb * 256:(cb + 1) * 256].rearrange("(k p) c -> p k c", p=128), writes=[bwm])
                ps, bps = psA.get()
                for j in range(2):
                    for k in range(8):
                        fw.op(PE, lambda h, j=j, k=k, wm=wm, ps=ps: h.matmul(ps[:, j * 2:j * 2 + 2], lhsT=wm[:, k, j * 128:(j + 1) * 128], rhs=scT[:, k, :],
                                                                          start=(k == 0), stop=(k == 7)),
                              reads=[bwm, b_scT], writes=[bps], inc=(j == 1 and k == 7))
                fw.op(DVE, lambda h, cb=cb, ps=ps: h.tensor_tensor(out=modT[:, cb * 2:(cb + 1) * 2, :], in0=ps[:, 0:4].rearrange("p (j c) -> p j c", c=2),
                                                                 in1=bmod[:, cb * 2:(cb + 1) * 2].unsqueeze(2).broadcast_to([128, 2, 2]), op=ALU.add),
                      reads=[bps, b_bmod], writes=[b_modT_b])
            fw.op(DVE, lambda h: h.scalar_tensor_tensor(out=A1[:], in0=modT[:, 8:16, :], scalar=1.0, in1=gn[:, 0:8].unsqueeze(2).broadcast_to([128, 8, 2]),
                                                        op0=ALU.add, op1=ALU.mult), reads=[b_modT_b, b_gn], writes=[b_A])
            fw.op(DVE, lambda h: h.scalar_tensor_tensor(out=A2[:], in0=modT[:, 32:40, :], scalar=1.0, in1=gn[:, 8:16].unsqueeze(2).broadcast_to([128, 8, 2]),
                                                        op0=ALU.add, op1=ALU.mult), reads=[b_modT_b, b_gn], writes=[b_A])
            lam_init = 0.8 - 0.6 * math.exp(-0.3 * l)
            fw.op(DVE, lambda h: h.tensor_tensor(out=lamt[:, 0:64], in0=lamt[:, 0:64], in1=lamt[:, 64:128], op=ALU.mult), reads=[b_lamt], writes=[b_lamt])
            fw.op(DVE, lambda h: h.tensor_tensor(out=lamt[:, 128:192], in0=lamt[:, 128:192], in1=lamt[:, 192:256], op=ALU.mult), reads=[b_lamt], writes=[b_lamt])
            fw.op(DVE, lambda h: h.reduce_sum(out=lam1[:, 0:1], in_=lamt[:, 0:64], axis=mybir.AxisListType.X), reads=[b_lamt], writes=[b_lam1])
            fw.op(DVE, lambda h: h.reduce_sum(out=lam1[:, 1:2], in_=lamt[:, 128:192], axis=mybir.AxisListType.X), reads=[b_lamt], writes=[b_lam1])
            fw.op(ACT, lambda h: h.activation(out=lam1[:, 2:4], in_=lam1[:, 0:2], func=AF.Exp), reads=[b_lam1], writes=[b_lam1])
            fw.op(DVE, lambda h: h.scalar_tensor_tensor(out=lam1[:, 4:5], in0=lam1[:, 3:4], scalar=-lam_init, in1=lam1[:, 2:3], op0=ALU.add, op1=ALU.subtract),
                  reads=[b_lam1], writes=[b_lam1])
            ps, bps = psA.get()
            fw.op(PE, lambda h: h.matmul(ps[:, 0:1], lhsT=ones_rowf[:, :], rhs=lam1[:, 4:5], start=True, stop=True), reads=[b_onesrowf, b_lam1], writes=[bps])
            fw.op(DVE, lambda h: h.tensor_copy(out=lamc[:, 0:1], in_=ps[:, 0:1]), reads=[bps], writes=[b_lamc])
            return lam_init

        def key_base(ti):
            return 0 if ti == 0 else NP_TOK + PAST + (ti - 1) * TT

        import os as _os
        KP1S = int(_os.environ.get("KP1S", "99"))
        KP1 = int(_os.environ.get("KP1", str(NT)))

        def phase1(l, ti):
            if ti >= KP1:
                return
            lat = ti > 0
            cond = 1 if lat else 0
            t0 = ti * TT
            kb = key_base(ti)
            xt, bxt = load_xT(ti)
            hT, bh = norm_mod(xt, bxt, A1, 0, cond)
            fw.dma(SP, hT_s[:, :, t0:t0 + TT], hT[:], reads=[bh], writes=[b_hTs[ti]])
            rhs_h = lambda k: hT[:, k, :]
            if lat:
                rM, brM = ropM_r.get()
                fw.dma(SP, rM[:], ropeM[:, :, (ti - 1) * TT:ti * TT].rearrange("c p t -> p c t"), writes=[brM])
                rD, brD = ropD_r.get()
                fw.dma(SP, rD[:], ropeD[:, :, (ti - 1) * TT:ti * TT].rearrange("c p t -> p c t"), writes=[brD])
                tabM = (rM[:, 0, :], rM[:, 1, :], brM)
                tabD = (rD[:, 0, :], rD[:, 1, :], brD)
            else:
                tabM = tabD = None

            if KP1S < 1:
                return
            wt, bw = load_w8(w_in[l, :, O_AQ:O_AQ + 512], 512)
            for which, dst, bdst in ((0, gq_s, b_gq), (1, gk_s, b_gk)):
                g4, bg4 = g4_r.get()
                for hh in range(4):
                    ps, bps = proj_fm(wt, bw, which * 256 + hh * 64, 64, rhs_h, [bh], 8)
                    evac(ps[0:64, :], g4[:, hh, :], bps, bg4, eng=(ACT if hh % 2 else DVE))
                fw.dma(SP, dst[:, :, t0:t0 + TT], g4[:], reads=[bg4], writes=[bdst])
            if KP1S < 2:
                return
            tk, btk = tok_r.get()
            for g in range(4):
                ps, bps = psA.get()
                for k in range(8):
                    fw.op(PE, lambda h, k=k, g=g, ps=ps: h.matmul(ps[:, 0:256], lhsT=hT[:, k, g * 128:(g + 1) * 128], rhs=wt[:, k, 256:512], start=(k == 0), stop=(k == 7)),
                          reads=[bw, bh], writes=[bps], inc=(k == 7))
                evac(ps[:, 0:256], tk[:, g, 0:256], bps, btk, eng=(ACT if g % 2 else DVE))
            fw.dma(SP, gkt_s[t0:t0 + TT, :].rearrange("(g p) c -> p g c", p=128), tk[:, :, 0:256], reads=[btk], writes=[b_gkt])
            wt, bw = load_w8(w_in[l, :, O_AV:O_AV + 512], 512)
            tk, btk = tok_r.get()
            for g in range(4):
                ps, bps = psA.get()
                for k in range(8):
                    fw.op(PE, lambda h, k=k, g=g, ps=ps, wt=wt: h.matmul(ps[:, :], lhsT=hT[:, k, g * 128:(g + 1) * 128], rhs=wt[:, k, 0:512], start=(k == 0), stop=(k == 7)),
                          reads=[bw, bh], writes=[bps], inc=(k == 7))
                evac(ps[:, :], tk[:, g, :], bps, btk, eng=(ACT if g % 2 else DVE))
            fw.dma(SP, gvt_s[t0:t0 + TT, :].rearrange("(g p) c -> p g c", p=128), tk[:], reads=[btk], writes=[b_gvt])
            if KP1S < 3:
                return
            wt, bw = load_w8(w_in[l, :, O_AA:O_AA + 416], 416)
            ps, bps = proj_fm(wt, bw, 0, 32, rhs_h, [bh], 8)
            aab, baab = aab_r.get()
            evac(ps[0:32, :], aab[:, :], bps, baab, eng=DVE)
            gt, bgt = tokf_r.get()
            for g in range(4):
                ps, bps = psA.get()
                fw.op(PE, lambda h, g=g, ps=ps: h.matmul(ps[:, :], lhsT=aab[:, g * 128:(g + 1) * 128], rhs=wa2[:, :], start=True, stop=False),
                      reads=[baab, b_wa2], writes=[bps], inc=False)
                fw.op(PE, lambda h, ps=ps: h.matmul(ps[:, :], lhsT=ones_row[:, :], rhs=ba[:, :], start=False, stop=True), reads=[b_onesrow, b_ba], writes=[bps])
                e1, be1 = f32_r.get()
                fw.op(ACT, lambda h, ps=ps, e1=e1: h.activation(out=e1[:], in_=ps[:], func=AF.Exp, scale=-1.0), reads=[bps], writes=[be1])
                fw.op(ACT, lambda h, g=g, e1=e1: h.activation(out=gt[:, g, :], in_=e1[:], func=AF.Ln, bias=1.0), reads=[be1], writes=[bgt])
            fw.dma(SP, gG_s[t0:t0 + TT, :].rearrange("(g p) c -> p g c", p=128), gt[:], reads=[bgt], writes=[b_gG])

            if KP1S < 4:
                return
            wt_r, bw_r = load_w8(w_in[l, :, O_AR:O_AR + 512], 512)
            rr, brr = r4_r.get()
            for hh in range(4):
                ps_r, bps_r = proj_fm(wt_r, bw_r, hh * 128, 128, rhs_h, [bh], 8)
                fw.op(ACT, lambda h, hh=hh, ps_r=ps_r: h.activation(out=rr[:, hh, :], in_=ps_r[:], func=AF.Silu), reads=[bps_r], writes=[brr])
            fw.dma(SP, rs_s[:, :, t0:t0 + TT], rr[:], reads=[brr], writes=[b_rs])

            if KP1S < 5:
                return
            qdn, bqdn = qdn_r.get()
            qf = []
            for j in range(3):
                ps, bps = proj_fm(wt, bw, 32 + j * 128, 128, rhs_h, [bh], 8)
                xf, bxf = f32_r.get()
                evac(ps[:], xf[:], bps, bxf, eng=DVE)
                sq, bsq = bf_r.get()
                fw.op(ACT, lambda h, sq=sq, xf=xf: h.activation(out=sq[:], in_=xf[:], func=AF.Square), reads=[bxf], writes=[bsq])
                qf.append((xf, bxf, sq, bsq))
            rs, brs = rms_rstd([(q[2][:], q[3]) for q in qf], cmat[:, C_384, :], 128)
            for j in range(3):
                xf, bxf = qf[j][0], qf[j][1]
                fw.op(DVE, lambda h, j=j, xf=xf: h.scalar_tensor_tensor(out=qdn[:, j, :], in0=xf[:], scalar=gc[:, j:j + 1], in1=rs[:], op0=ALU.mult, op1=ALU.mult),
                      reads=[bxf, brs, b_gc], writes=[bqdn])
            pipe = Pipe()
            for half in range(2):
                wq, bwq = w8.get()
                fw.dma(POOL, wq[:, 0:3, 0:384], w_uq[l, :, half * 4:(half + 1) * 4, :].rearrange("(k p) h c -> p k (h c)", p=128), writes=[bwq])
                for hh in range(4):
                    ps, bps = proj_fm(wq, bwq, hh * 96, 96, lambda k: qdn[:, k, :], [bqdn], 3)
                    head = half * 4 + hh
                    pipe.push(norm_rope_gen(ps, bps, 96, cmat[0:96, C_96, 0:96], 5, lat, P_96, tabM, q_s[head, :, t0:t0 + TT], b_q))
            pipe.drain()

            if KP1S < 6:
                return
            wt, bw = load_w8(w_in[l, :, O_KVD:O_KVD + 288], 288)
            ckv, bckv = ckv_r.get()
            ckvf, bckvf = ckvf_r.get()
            kf = []
            for j in range(2):
                ps, bps = proj_fm(wt, bw, j * 128, 128, rhs_h, [bh], 8)
                xf, bxf = f32_r.get()
                evac(ps[:], xf[:], bps, bxf, eng=DVE)
                sq, bsq = bf_r.get()
                fw.op(ACT, lambda h, sq=sq, xf=xf: h.activation(out=sq[:], in_=xf[:], func=AF.Square), reads=[bxf], writes=[bsq])
                kf.append((xf, bxf, sq, bsq))
            rs, brs = rms_rstd([(q[2][:], q[3]) for q in kf], cmat[:, C_256, :], 128)
            for j in range(2):
                xf, bxf = kf[j][0], kf[j][1]
                fw.op(DVE, lambda h, j=j, xf=xf: h.scalar_tensor_tensor(out=ckvf[:, j, :], in0=xf[:], scalar=gc[:, 3 + j:4 + j], in1=rs[:], op0=ALU.mult, op1=ALU.mult),
                      reads=[bxf, brs, b_gc], writes=[bckvf])
            fw.op(ACT, lambda h: h.activation(out=ckv[:], in_=ckvf[:], func=AF.Copy), reads=[bckvf], writes=[bckv])
            ps, bps = proj_fm(wt, bw, 256, 32, rhs_h, [bh], 8)
            krf, bkrf = krf_r.get()
            krb, bkrb = krb_r.get()
            evac(ps[0:32, :], krf[:, :], bps, bkrf, eng=DVE)
            fw.op(ACT, lambda h: h.activation(out=krb[:], in_=krf[:], func=AF.Copy), reads=[bkrf], writes=[bkrb])
            KSUB = _os.environ.get("KSUB", "abcd")
            if not lat:
                if "a" in KSUB:
                    transpose_out(lambda j: ckvf[:, j, :], [bckvf], 128, 256, lambda g: o_ckv[l, g * 128:(g + 1) * 128, :], [(0, 128, 0), (1, 128, 128)])
                if "b" in KSUB:
                    transpose_out(lambda j: krf[:, :], [bkrf], 32, 32, lambda g: o_kr[l, g * 128:(g + 1) * 128, :], [(0, 32, 0)])
            if "c" in KSUB:
                mla_keys(l, lambda k: ckv[:, k, :], [bckv], krb, bkrb, lat, tabM, kb)
            if "d" in KSUB:
                mla_values(l, lambda k, g: ckv[:, k, g * 128:(g + 1) * 128], [bckv], kb)

            if KP1S < 7:
                return
            for which, off, dst, bdst, gcolumn in ((0, O_DQ, dq_s, b_dq, 7), (1, O_DK, dk_s, b_dk, 8)):
                wt, bw = load_w8(w_in[l, :, off:off + 512], 512)
                dpipe = Pipe()
                keep = None
                if which == 1 and not lat:
                    kp, bkp = keep_r.get()
                for hh in range(4):
                    ps, bps = proj_fm(wt, bw, hh * 128, 128, rhs_h, [bh], 8)
                    kf32 = None
                    if which == 1 and not lat:
                        kf32 = lambda xn, bxn, hh=hh: fw.op(ACT, lambda h: h.activation(out=kp[:, hh, :], in_=xn[:], func=AF.Copy), reads=[bxn], writes=[bkp])
                    col0 = t0 if which == 0 else kb
                    dpipe.push(norm_rope_gen(ps, bps, 128, cmat[:, C_BLK64, :], gcolumn, lat, P_128, tabD, dst[hh, :, col0:col0 + TT], bdst, keep_f32=kf32))
                dpipe.drain()
                if which == 1 and not lat:
                    transpose_out(lambda j: kp[:, j, :], [bkp], 128, 512, lambda g: o_dk[l, g * 128:(g + 1) * 128, :], [(j, 128, j * 128) for j in range(4)])
            wt, bw = load_w8(w_in[l, :, O_DV:O_DV + 512], 512)
            tk, btk = tok_r.get()
            if not lat:
                tf, btf = tokf_r.get()
            for g in range(4):
                ps, bps = psA.get()
                for k in range(8):
                    fw.op(PE, lambda h, k=k, g=g, ps=ps, wt=wt: h.matmul(ps[:, :], lhsT=hT[:, k, g * 128:(g + 1) * 128], rhs=wt[:, k, 0:512], start=(k == 0), stop=(k == 7)),
                          reads=[bw, bh], writes=[bps], inc=(k == 7))
                evac(ps[:, :], tk[:, g, :], bps, btk, eng=ACT)
                if not lat:
                    evac(ps[:, :], tf[:, g, :], bps, btf, eng=DVE)
            for hh in range(4):
                fw.dma(SP, dv_s[hh, :, kb // 128:kb // 128 + 4, :], tk[:, :, hh * 128:(hh + 1) * 128], reads=[btk], writes=[b_dv])
            if not lat:
                ob = Buf(); out_bufs.append(ob)
                fw.dma(SP, o_dv[l, :, :].rearrange("(g p) c -> p g c", p=128), tf[:], reads=[btf], writes=[ob])

        def mla_keys(l, ckv_fn, ckv_bufs, krb, bkrb, rope, tabM, kb, ntok=TT):
            pipe = Pipe()
            for half in range(2):
                wk, bwk = w8.get()
                fw.dma(POOL, wk[:, 0:2, 0:384], w_ukp[l, :, half * 4:(half + 1) * 4, :].rearrange("(k p) h c -> p k (h c)", p=128), writes=[bwk])
                for hh in range(4):
                    ps, bps = psA.get()
                    for k in range(2):
                        fw.op(PE, lambda h, k=k, hh=hh, ps=ps, wk=wk: h.matmul(ps[0:96, 0:ntok], lhsT=wk[:, k, hh * 96:(hh + 1) * 96], rhs=ckv_fn(k), start=(k == 0), stop=False),
                              reads=[bwk] + ckv_bufs, writes=[bps], inc=False)
                    fw.op(PE, lambda h, ps=ps: h.matmul(ps[0:96, 0:ntok], lhsT=cmat[0:32, C_SEL32, 0:96], rhs=krb[:, 0:ntok], start=False, stop=True),
                          reads=[b_cmat, bkrb], writes=[bps])
                    head = half * 4 + hh
                    if ntok == TT:
                        pipe.push(norm_rope_gen(ps, bps, 96, cmat[0:96, C_96, 0:96], 6, rope, P_96, tabM, k_s[head, :, kb:kb + TT], b_k))
                    else:
                        norm_store_small(ps, bps, ntok, head, kb)
            pipe.drain()

        def norm_store_small(ps, bps, ntok, head, kb):
            xf, bxf = f32_r.get()
            evac(ps[0:96, 0:ntok], xf[0:96, 0:ntok], bps, bxf, eng=DVE)
            sq, bsq = bf_r.get()
            fw.op(ACT, lambda h: h.activation(out=sq[0:96, 0:ntok], in_=xf[0:96, 0:ntok], func=AF.Square), reads=[bxf], writes=[bsq])
            ps2, bps2 = psA.get()
            fw.op(PE, lambda h: h.matmul(ps2[0:96, 0:ntok], lhsT=cmat[0:96, C_96, 0:96], rhs=sq[0:96, 0:ntok], start=True, stop=True), reads=[bsq, b_cmat], writes=[bps2])
            rs, brs = rs_r.get()
            rsqrt_eps(ps2[0:96, 0:ntok], rs[0:96, 0:ntok], bps2, brs)
            ob, bob = bf_r.get()
            fw.op(DVE, lambda h: h.scalar_tensor_tensor(out=ob[0:96, 0:ntok], in0=xf[0:96, 0:ntok], scalar=gc[0:96, 6:7], in1=rs[0:96, 0:ntok], op0=ALU.mult, op1=ALU.mult),
                  reads=[bxf, brs, b_gc], writes=[bob])
            fw.dma(SP, k_s[head, :, kb:kb + ntok], ob[0:96, 0:ntok], reads=[bob], writes=[b_k])

        vaug_init = [False, False]

        def mla_values(l, ckvT_fn, ckv_bufs, kb, ngrp=4):
            wv, bwv = w8.get()
            fw.dma(POOL, wv[:, 0:2, 0:512], w_uv[l].rearrange("(k p) c -> p k c", p=128), writes=[bwv])
            idx = vaug_r.i
            va, bva = vaug_r.get()
            if not vaug_init[idx]:
                vaug_init[idx] = True
                fw.op(DVE, lambda h: h.memset(va[:], 1.0), writes=[bva])
            for g in range(ngrp):
                ps, bps = psA.get()
                for k in range(2):
                    fw.op(PE, lambda h, k=k, g=g, ps=ps: h.matmul(ps[:, :], lhsT=ckvT_fn(k, g), rhs=wv[:, k, 0:512], start=(k == 0), stop=(k == 1)),
                          reads=[bwv] + ckv_bufs, writes=[bps], inc=(k == 1))
                fw.op(ACT if g % 2 else DVE, (lambda h, g=g, ps=ps: h.activation(out=va[:, :, g, 0:64], in_=ps[:, :].rearrange("p (h c) -> p h c", c=64), func=AF.Copy)) if g % 2
                      else (lambda h, g=g, ps=ps: h.tensor_copy(out=va[:, :, g, 0:64], in_=ps[:, :].rearrange("p (h c) -> p h c", c=64))), reads=[bps], writes=[bva])
            c0 = kb // 128
            fw.dma(SP, v_s[:, :, c0:c0 + ngrp, :].rearrange("h p g e -> p h (g e)"), va[:, :, 0:ngrp, :].rearrange("p h g e -> p h (g e)"), reads=[bva], writes=[b_v])


        def prep_cache(l):
            kb = NP_TOK
            ct, bct = ctok_r.get()
            fw.dma(SP, ct[:, :, 0:256], c_ckv[l].rearrange("(g p) c -> p g c", p=128), writes=[bct])
            ckv, bckv = ckv_r.get()
            for j in range(2):
                ps, bps = psA.get()
                for g in range(2):
                    fw.op(PE, lambda h, j=j, g=g, ps=ps: h.transpose(ps[:, g * 128:(g + 1) * 128], ct[:, g, j * 128:(j + 1) * 128], ident[:]),
                          reads=[bct, b_ident], writes=[bps], inc=(g == 1))
                evac(ps[:, 0:256], ckv[:, j, 0:256], bps, bckv, eng=DVE)
            ct2, bct2 = ctok_r.get()
            fw.dma(SP, ct2[:, :, 0:32], c_kr[l].rearrange("(g p) c -> p g c", p=128), writes=[bct2])
            krb, bkrb = krb_r.get()
            ps, bps = psA.get()
            for g in range(2):
                fw.op(PE, lambda h, g=g, ps=ps: h.transpose(ps[0:32, g * 128:(g + 1) * 128], ct2[:, g, 0:32], ident[:]), reads=[bct2, b_ident], writes=[bps], inc=(g == 1))
            evac(ps[0:32, 0:256], krb[:, 0:256], bps, bkrb, eng=DVE)
            mla_keys(l, lambda k: ckv[:, k, 0:256], [bckv], krb, bkrb, False, None, kb, ntok=256)
            mla_values(l, lambda k, g: ckv[:, k, g * 128:(g + 1) * 128], [bckv], kb, ngrp=2)
            ct, bct = ctok_r.get()
            fw.dma(SP, ct[:], c_dk[l].rearrange("(g p) c -> p g c", p=128), writes=[bct])
            for hh in range(4):
                ps, bps = psA.get()
                for g in range(2):
                    fw.op(PE, lambda h, hh=hh, g=g, ps=ps: h.transpose(ps[:, g * 128:(g + 1) * 128], ct[:, g, hh * 128:(hh + 1) * 128], ident[:]),
                          reads=[bct, b_ident], writes=[bps], inc=(g == 1))
                ob, bob = bf_r.get()
                evac(ps[:, 0:256], ob[:, 0:256], bps, bob, eng=DVE)
                fw.dma(SP, dk_s[hh, :, kb:kb + 256], ob[:, 0:256], reads=[bob], writes=[b_dk])
            ct3, bct3 = ctok_r.get()
            fw.dma(SP, ct3[:], c_dv[l].rearrange("(g p) c -> p g c", p=128), writes=[bct3])
            tkc, btkc = tok_r.get()
            fw.op(DVE, lambda h: h.tensor_copy(out=tkc[:, 0:2, :], in_=ct3[:]), reads=[bct3], writes=[btkc])
            for hh in range(4):
                fw.dma(SP, dv_s[hh, :, kb // 128:kb // 128 + 2, :], tkc[:, 0:2, hh * 128:(hh + 1) * 128], reads=[btkc], writes=[b_dv])

        arena.reset()
        gl_q = Ring(ov, "glq", [64, 4, TT], BF16, 1)
        gl_k = Ring(ov, "glk", [64, 4, TT], BF16, 1)
        gl_kt = Ring(ov, "glkt", [64, 8, 256], BF16, 1)
        gl_vt = Ring(ov, "glvt", [64, 8, 512], BF16, 1)
        gl_G = Ring(ov, "glG", [64, 8, 256], F32, 1)
        gl_o = Ring(ov, "glo", [128, 4, TT], F32, 1)
        gl_of = Ring(ov, "glof", [128, 4, TT], F32, 1)
        gl_r = Ring(ov, "glr", [128, 4, TT], BF16, 2)
        S_f = ov("S_f", [64, 4, 128], F32); S_b = ov("S_b", [64, 4, 128], BF16); b_S = Buf(); b_Sb = Buf()
        sm_r = Ring(ov, "sm", [64, 256], F32, 6)
        smb_r = Ring(ov, "smb", [64, 256], BF16, 8)
        dd_r = Ring(ov, "dd", [64, 4], F32, 3)

        def gla_dir(l, d, seqs):
            incl = tri[:, d, :]
            strict = tri[:, 2 + d, :]
            for (tok0, ntok, seq_idx, lat) in seqs:
                if lat:
                    fw.dma(SP, S_f[:], st_gla[l, d], reads=[b_Sb], writes=[b_S])
                else:
                    fw.op(DVE, lambda h: h.memset(S_f[:], 0.0), reads=[b_Sb], writes=[b_S])
                fw.op(ACT, lambda h: h.activation(out=S_b[:], in_=S_f[:], func=AF.Copy), reads=[b_S], writes=[b_Sb])
                tiles = list(range(tok0, tok0 + ntok, TT)) if ntok >= TT else [tok0]
                if d == 1:
                    tiles = tiles[::-1]
                for tt0 in tiles:
                    base_tile = (tt0 // TT) * TT
                    cq, bcq = gl_q.get(); ck, bck = gl_k.get(); ckt, bckt = gl_kt.get(); cvt, bcvt = gl_vt.get(); cG, bcG = gl_G.get()
                    fw.dma(SP, cq[:], gq_s[:, :, base_tile:base_tile + TT], reads=[b_gq], writes=[bcq])
                    fw.dma(SP, ck[:], gk_s[:, :, base_tile:base_tile + TT], reads=[b_gk], writes=[bck])
                    fw.dma(SP, ckt[:], gkt_s[base_tile:base_tile + TT, :].rearrange("(c p) f -> p c f", p=64), reads=[b_gkt], writes=[bckt])
                    fw.dma(SP, cvt[:], gvt_s[base_tile:base_tile + TT, :].rearrange("(c p) f -> p c f", p=64), reads=[b_gvt], writes=[bcvt])
                    fw.dma(SP, cG[:], gG_s[base_tile:base_tile + TT, d * 256:(d + 1) * 256].rearrange("(c p) f -> p c f", p=64), reads=[b_gG], writes=[bcG])
                    ot, bot = gl_o.get()
                    c_lo = (tt0 - base_tile) // 64
                    nch = min(ntok, TT) // 64
                    chunks = list(range(c_lo, c_lo + nch))
                    if d == 1:
                        chunks = chunks[::-1]
                    for c in chunks:
                        Gc = cG[:, c, :]
                        psb_, bpsb = psA.get()
                        for hh in range(4):
                            fw.op(PE, lambda h, hh=hh, Gc=Gc, p=psb_: h.matmul(p[0:64, hh * 64:(hh + 1) * 64], lhsT=Gc[:, hh * 64:(hh + 1) * 64], rhs=incl, start=True, stop=True),
                                  reads=[bcG, b_tri], writes=[bpsb], inc=(hh == 3))
                        ep, bep = sm_r.get(); en, ben = sm_r.get()
                        fw.op(ACT, lambda h, p=psb_, ep=ep: h.activation(out=ep[:, :], in_=p[0:64, 0:256], func=AF.Exp, scale=-1.0 / 16), reads=[bpsb], writes=[bep])
                        fw.op(ACT, lambda h, p=psb_, en=en: h.activation(out=en[:, :], in_=p[0:64, 0:256], func=AF.Exp, scale=1.0 / 16), reads=[bpsb], writes=[ben])
                        qt, bqt = smb_r.get(); kt_, bkt = smb_r.get()
                        fw.op(DVE, lambda h, c=c, qt=qt, ep=ep: h.scalar_tensor_tensor(out=qt[:, :].rearrange("p (h t) -> p h t", t=64), in0=cq[:, :, c * 64:(c + 1) * 64], scalar=0.125,
                                                                                      in1=ep[:, :].rearrange("p (h t) -> p h t", t=64), op0=ALU.mult, op1=ALU.mult),
                              reads=[bcq, bep], writes=[bqt])
                        fw.op(DVE, lambda h, c=c, kt_=kt_, en=en: h.tensor_tensor(out=kt_[:, :].rearrange("p (h t) -> p h t", t=64), in0=ck[:, :, c * 64:(c + 1) * 64],
                                                                                  in1=en[:, :].rearrange("p (h t) -> p h t", t=64), op=ALU.mult),
                              reads=[bck, ben], writes=[bkt])
                        ps2, bps2 = psA.get()
                        fw.op(PE, lambda h, Gc=Gc, p=ps2: h.matmul(p[0:64, 0:256], lhsT=strict, rhs=Gc, start=True, stop=True), reads=[bcG, b_tri], writes=[bps2])
                        e2, be2 = sm_r.get()
                        fw.op(ACT, lambda h, p=ps2, e2=e2: h.activation(out=e2[:, :], in_=p[0:64, 0:256], func=AF.Exp, scale=-1.0 / 16), reads=[bps2], writes=[be2])
                        kh, bkh = smb_r.get()
                        fw.op(DVE, lambda h, c=c, kh=kh, e2=e2: h.tensor_tensor(out=kh[:, :], in0=ckt[:, c, :], in1=e2[:, :], op=ALU.mult), reads=[bckt, be2], writes=[bkh])
                        ps3, bps3 = psA.get()
                        for hh in range(4):
                            fw.op(PE, lambda h, hh=hh, Gc=Gc, p=ps3: h.matmul(p[0:64, hh:hh + 1], lhsT=Gc[:, hh * 64:(hh + 1) * 64], rhs=ones_col[:, :], start=True, stop=True),
                                  reads=[bcG, b_onescol], writes=[bps3], inc=(hh == 3))
                        dd, bdd = dd_r.get()
                        fw.op(ACT, lambda h, p=ps3, dd=dd: h.activation(out=dd[:, :], in_=p[0:64, 0:4], func=AF.Exp, scale=-1.0 / 16), reads=[bps3], writes=[bdd])
                        ps4, bps4 = psA.get()
                        for hh in range(4):
                            fw.op(PE, lambda h, hh=hh, p=ps4, kt_=kt_, qt=qt: h.matmul(p[0:64, hh * 64:(hh + 1) * 64], lhsT=kt_[:, hh * 64:(hh + 1) * 64], rhs=qt[:, hh * 64:(hh + 1) * 64],
                                                                                     start=True, stop=True), reads=[bkt, bqt], writes=[bps4], inc=(hh == 3))
                        am, bam = smb_r.get()
                        fw.op(DVE, lambda h, p=ps4, am=am: h.tensor_tensor(out=am[:, :].rearrange("p (h t) -> p h t", t=64), in0=p[0:64, 0:256].rearrange("p (h t) -> p h t", t=64),
                                                                         in1=incl.unsqueeze(1).broadcast_to([64, 4, 64]), op=ALU.mult), reads=[bps4, b_tri], writes=[bam])
                        ps5, bps5 = psA.get()
                        for hh in range(4):
                            fw.op(PE, lambda h, hh=hh, c=c, p=ps5, am=am: h.matmul(p[:, hh * 64:(hh + 1) * 64], lhsT=cvt[:, c, hh * 128:(hh + 1) * 128], rhs=am[:, hh * 64:(hh + 1) * 64],
                                                                                 start=True, stop=False), reads=[bcvt, bam], writes=[bps5], inc=False)
                            fw.op(PE, lambda h, hh=hh, p=ps5, qt=qt: h.matmul(p[:, hh * 64:(hh + 1) * 64], lhsT=S_b[:, hh, :], rhs=qt[:, hh * 64:(hh + 1) * 64], start=False, stop=True),
                                  reads=[b_Sb, bqt], writes=[bps5], inc=(hh == 3))
                        fw.op(ACT, lambda h, c=c, p=ps5, ot=ot: h.activation(out=ot[:, :, c * 64:(c + 1) * 64], in_=p[:, 0:256].rearrange("p (h t) -> p h t", t=64), func=AF.Copy),
                              reads=[bps5], writes=[bot])
                        ps6, bps6 = psB.get()
                        for hh in range(4):
                            fw.op(PE, lambda h, hh=hh, c=c, p=ps6, kh=kh: h.matmul(p[0:64, hh * 128:(hh + 1) * 128], lhsT=kh[:, hh * 64:(hh + 1) * 64], rhs=cvt[:, c, hh * 128:(hh + 1) * 128],
                                                                                 start=True, stop=True), reads=[bkh, bcvt], writes=[bps6], inc=(hh == 3))
                        fw.op(DVE, lambda h, dd=dd: h.tensor_tensor(out=S_f[:], in0=S_f[:], in1=dd[:, :].unsqueeze(2).broadcast_to([64, 4, 128]), op=ALU.mult), reads=[bdd], writes=[b_S])
                        fw.op(DVE, lambda h, p=ps6: h.tensor_tensor(out=S_f[:], in0=S_f[:], in1=p[0:64, :].rearrange("p (h v) -> p h v", v=128), op=ALU.add), reads=[bps6], writes=[b_S])
                        fw.op(ACT, lambda h: h.activation(out=S_b[:], in_=S_f[:], func=AF.Copy), reads=[b_S], writes=[b_Sb])
                    cols = slice(tt0 - base_tile, tt0 - base_tile + min(ntok, TT))
                    ncol = min(ntok, TT)
                    if d == 0:
                        fw.dma(SP, go_s[:, :, tt0:tt0 + ncol], ot[:, :, cols], reads=[bot], writes=[b_go])
                    else:
                        of, bof = gl_of.get()
                        fw.dma(SP, of[:, :, 0:ncol], go_s[:, :, tt0:tt0 + ncol], reads=[b_go], writes=[bof])
                        rr, brr = gl_r.get()
                        fw.dma(SP, rr[:, :, 0:ncol], rs_s[:, :, tt0:tt0 + ncol], reads=[b_rs], writes=[brr])
                        fw.op(DVE, lambda h, ot=ot, of=of: h.tensor_tensor(out=of[:, :, 0:ncol], in0=of[:, :, 0:ncol], in1=ot[:, :, cols], op=ALU.add), reads=[bot], writes=[bof])
                        sq, bsq = sq_r.get()
                        fw.op(ACT, lambda h, sq=sq, of=of: h.activation(out=sq[:, 0:4, 0:ncol], in_=of[:, :, 0:ncol], func=AF.Square), reads=[bof], writes=[bsq])
                        ya, bya = gl_r.get()
                        for hh in range(4):
                            ps, bps = psA.get()
                            fw.op(PE, lambda h, hh=hh, ps=ps, sq=sq: h.matmul(ps[:, 0:ncol], lhsT=cmat[:, C_128, :], rhs=sq[:, hh, 0:ncol], start=True, stop=True), reads=[bsq, b_cmat], writes=[bps])
                            rs, brs = rs_r.get()
                            rsqrt_eps(ps[:, 0:ncol], rs[:, 0:ncol], bps, brs)
                            tmp, btmp = f32_r.get()
                            fw.op(DVE, lambda h, hh=hh, tmp=tmp, of=of, rs=rs: h.scalar_tensor_tensor(out=tmp[:, 0:ncol], in0=of[:, hh, 0:ncol], scalar=gc[:, 9:10], in1=rs[:, 0:ncol],
                                                                                                     op0=ALU.mult, op1=ALU.mult), reads=[bof, brs, b_gc], writes=[btmp])
                            fw.op(DVE, lambda h, hh=hh, tmp=tmp, ya=ya, rr=rr: h.tensor_tensor(out=ya[:, hh, 0:ncol], in0=tmp[:, 0:ncol], in1=rr[:, hh, 0:ncol], op=ALU.mult),
                                  reads=[btmp, brr], writes=[bya])
                        fw.dma(SP, ya_s[:, :, tt0:tt0 + ncol].rearrange("h p t -> p h t"), ya[:, :, 0:ncol], reads=[bya], writes=[b_ya])
                if seq_idx is not None:
                    ob = Buf(); out_bufs.append(ob)
                    fw.dma(SP, o_gla[l, seq_idx, d], S_f[:], reads=[b_S], writes=[ob])

        rs_s = dram_tmp("rs_s", [128, 4, T], BF16)
        b_rs = Buf()

        def phase1b(l, ti):
            cond = 1 if ti > 0 else 0
            xt, bxt = load_xT(ti)
            hT, bh = norm_mod(xt, bxt, A1, 0, cond)
            wt, bw = load_w8(w_in[l, :, O_AR:O_AR + 512], 512)
            rr, brr = gl_r.get()
            for hh in range(4):
                ps, bps = proj_fm(wt, bw, hh * 128, 128, lambda k: hT[:, k, :], [bh], 8)
                fw.op(ACT, lambda h, hh=hh, ps=ps: h.activation(out=rr[:, hh, :], in_=ps[:], func=AF.Silu), reads=[bps], writes=[brr])
            fw.dma(SP, rs_s[:, :, ti * TT:(ti + 1) * TT], rr[:], reads=[brr], writes=[b_rs])

        arena.reset()
        at_k = Ring(ov, "atk", [128, TKS], BF16, 2)
        at_v = Ring(ov, "atv", [128, TKS // 128, 128], BF16, 2)
        at_q = Ring(ov, "atq", [128, TT], BF16, 2)
        at_p = Ring(ov, "atp", [128, TT], BF16, 6)
        at_o = Ring(ov, "ato", [128, TT], BF16, 2)
        at_qm = [Ring(ov, "atqm0_", [128, TT], BF16, 2), Ring(ov, "atqm1_", [128, TT], BF16, 2)]
        at_l = Ring(ov, "atl", [128, TT], F32, 4)

        def mla_attn(l, groups):
            sc = 96 ** -0.5
            for i in range(2):
                fw.op(DVE, lambda h: h.memset(at_k.t[i][:], 0.0), writes=[at_k.b[i]])
                fw.op(DVE, lambda h: h.memset(at_q.t[i][:], 0.0), writes=[at_q.b[i]])
                for c2 in range(2):
                    fw.op(DVE, lambda h: h.memset(at_qm[c2].t[i][:], 0.0), writes=[at_qm[c2].b[i]])
            its = [(gi, head, qq) for gi, (q0, nq, k0, nk) in enumerate(groups) for head in range(8) for qq in range(q0, q0 + nq, TT)]
            kv_cache, q_cache = {}, {}

            def get_kv(gi, head):
                if (gi, head) not in kv_cache:
                    q0, nq, k0, nk = groups[gi]
                    kt, bkt = at_k.get(); vt, bvt = at_v.get()
                    fw.dma(SP, kt[0:96, 0:nk], k_s[head, :, k0:k0 + nk], reads=[b_k], writes=[bkt])
                    fw.dma(SP, vt[:, 0:nk // 128, :], v_s[head, :, k0 // 128:k0 // 128 + nk // 128, :], reads=[b_v], writes=[bvt])
                    kv_cache[(gi, head)] = (kt, bkt, vt, bvt)
                return kv_cache[(gi, head)]

            def get_q(idx):
                if idx not in q_cache:
                    gi, head, qq = its[idx]
                    nqt = min(TT, groups[gi][1])
                    qt, bqt = at_q.get()
                    fw.dma(SP, qt[0:96, 0:nqt], q_s[head, :, qq:qq + nqt], reads=[b_q], writes=[bqt])
                    q_cache[idx] = (qt, bqt)
                return q_cache[idx]

            for idx, (gi, head, qq) in enumerate(its):
                q0, nq, k0, nk = groups[gi]
                nkc = nk // 128
                if True:
                    kt, bkt, vt, bvt = get_kv(gi, head)
                    if True:
                        nqt = min(TT, nq)
                        qt, bqt = get_q(idx)
                        if idx + 1 < len(its):
                            get_kv(its[idx + 1][0], its[idx + 1][1])
                            get_q(idx + 1)
                        po, bpo = psB.get()

                        def qk(c, kt=kt, qt=qt, bkt=bkt, bqt=bqt, nqt=nqt):
                            ps, bps = psA.get()
                            fw.op(PE, lambda h: h.matmul(ps[:, 0:nqt], lhsT=kt[:, c * 128:(c + 1) * 128], rhs=qt[:, 0:nqt], start=True, stop=True),
                                  reads=[bkt, bqt], writes=[bps])
                            return ps, bps

                        LA = 2
                        pend = [qk(i) for i in range(min(LA, nkc))]
                        for c in range(nkc):
                            ps, bps = pend.pop(0)
                            if c + LA < nkc:
                                pend.append(qk(c + LA))
                            pt, bpt = at_p.get()
                            fw.op(ACT, lambda h, ps=ps, pt=pt: h.activation(out=pt[:, 0:nqt], in_=ps[:, 0:nqt], func=AF.Exp, scale=sc), reads=[bps], writes=[bpt])
                            fw.op(PE, lambda h, c=c, po=po, vt=vt, pt=pt: h.matmul(po[:, 0:nqt], lhsT=vt[:, c, :], rhs=pt[:, 0:nqt], start=(c == 0), stop=(c == nkc - 1)),
                                  reads=[bvt, bpt], writes=[bpo], inc=(c == nkc - 1))
                        rc, brc = f32_r.get()
                        fw.op(DVE, lambda h, po=po, rc=rc: h.reciprocal(out=rc[64:128, 0:nqt], in_=po[64:128, 0:nqt]), reads=[bpo], writes=[brc])
                        r2, br2 = f32_r.get()
                        fw.op(DVE, lambda h, rc=rc, r2=r2: h.tensor_copy(out=r2[0:64, 0:nqt], in_=rc[64:128, 0:nqt]), reads=[brc], writes=[br2])
                        ob, bob = at_o.get()
                        fw.op(DVE, lambda h, po=po, r2=r2, ob=ob: h.tensor_tensor(out=ob[0:64, 0:nqt], in0=po[0:64, 0:nqt], in1=r2[0:64, 0:nqt], op=ALU.mult), reads=[bpo, br2], writes=[bob])
                        fw.dma(SP, yb_s[head // 2, (head % 2) * 64:(head % 2) * 64 + 64, qq:qq + nqt], ob[0:64, 0:nqt], reads=[bob], writes=[b_yb])

        def diff_attn(l, groups, lam_init):
            sc = 64 ** -0.5
            its = [(gi, head, qq) for gi, (q0, nq, k0, nk) in enumerate(groups) for head in range(4) for qq in range(q0, q0 + nq, TT)]
            kv_cache, q_cache = {}, {}

            def get_kv(gi, head):
                if (gi, head) not in kv_cache:
                    q0, nq, k0, nk = groups[gi]
                    kt, bkt = at_k.get(); vt, bvt = at_v.get()
                    fw.dma(SP, kt[:, 0:nk], dk_s[head, :, k0:k0 + nk], reads=[b_dk], writes=[bkt])
                    fw.dma(SP, vt[:, 0:nk // 128, :], dv_s[head, :, k0 // 128:k0 // 128 + nk // 128, :], reads=[b_dv], writes=[bvt])
                    kv_cache[(gi, head)] = (kt, bkt, vt, bvt)
                return kv_cache[(gi, head)]

            def get_q(idx):
                if idx not in q_cache:
                    gi, head, qq = its[idx]
                    nqt = min(TT, groups[gi][1])
                    qm = [at_qm[0].get(), at_qm[1].get()]
                    fw.dma(SP, qm[0][0][0:64, 0:nqt], dq_s[head, 0:64, qq:qq + nqt], reads=[b_dq], writes=[qm[0][1]])
                    fw.dma(SP, qm[1][0][64:128, 0:nqt], dq_s[head, 64:128, qq:qq + nqt], reads=[b_dq], writes=[qm[1][1]])
                    q_cache[idx] = qm
                return q_cache[idx]

            for idx, (gi, head, qq) in enumerate(its):
                q0, nq, k0, nk = groups[gi]
                nkc = nk // 128
                if True:
                    kt, bkt, vt, bvt = get_kv(gi, head)
                    if True:
                        nqt = min(TT, nq)
                        qm = get_q(idx)
                        if idx + 1 < len(its):
                            get_kv(its[idx + 1][0], its[idx + 1][1])
                            get_q(idx + 1)
                        acc = [psB.get() for _ in range(3)]
                        lac = [at_l.get()]

                        def qk(i, kt=kt, bkt=bkt, nqt=nqt, qm=qm):
                            c, comp = divmod(i, 2)
                            ps, bps = psA.get()
                            fw.op(PE, lambda h: h.matmul(ps[:, 0:nqt], lhsT=kt[:, c * 128:(c + 1) * 128], rhs=qm[comp][0][:, 0:nqt], start=True, stop=True),
                                  reads=[bkt, qm[comp][1]], writes=[bps])
                            return ps, bps

                        LA = 2
                        pend = [qk(i) for i in range(min(LA, 2 * nkc))]
                        for c in range(nkc):
                            for comp in range(2):
                                ps, bps = pend.pop(0)
                                if c * 2 + comp + LA < 2 * nkc:
                                    pend.append(qk(c * 2 + comp + LA))
                                pt, bpt = at_p.get()
                                fw.op(ACT, lambda h: h.activation(out=pt[:, 0:nqt], in_=ps[:, 0:nqt], func=AF.Exp, scale=sc), reads=[bps], writes=[bpt])
                                po, bpo = acc[comp]
                                fw.op(PE, lambda h: h.matmul(po[:, 0:nqt], lhsT=vt[:, c, :], rhs=pt[:, 0:nqt], start=(c == 0), stop=(c == nkc - 1)),
                                      reads=[bvt, bpt], writes=[bpo], inc=(c == nkc - 1))
                                if comp == 0:
                                    la, bla = lac[0]
                                    if c == 0:
                                        fw.op(DVE, lambda h: h.tensor_copy(out=la[:, 0:nqt], in_=pt[:, 0:nqt]), reads=[bpt], writes=[bla])
                                    else:
                                        fw.op(DVE, lambda h: h.tensor_tensor(out=la[:, 0:nqt], in0=la[:, 0:nqt], in1=pt[:, 0:nqt], op=ALU.add), reads=[bpt], writes=[bla])
                                else:
                                    pl1, bpl1 = acc[2]
                                    fw.op(PE, lambda h: h.matmul(pl1[:, 0:nqt], lhsT=cmat[:, C_ONE, :], rhs=pt[:, 0:nqt], start=(c == 0), stop=(c == nkc - 1)),
                                          reads=[b_cmat, bpt], writes=[bpl1], inc=(c == nkc - 1))
                        rr_ = []
                        pl, bpl = psA.get()
                        fw.op(PE, lambda h: h.matmul(pl[:, 0:nqt], lhsT=onesf[:, :], rhs=lac[0][0][:, 0:nqt], start=True, stop=True), reads=[b_onesf, lac[0][1]], writes=[bpl])
                        for (pl_, bpl_) in ((pl, bpl), acc[2]):
                            rc_, brc_ = f32_r.get()
                            fw.op(DVE, lambda h: h.reciprocal(out=rc_[:, 0:nqt], in_=pl_[:, 0:nqt]), reads=[bpl_], writes=[brc_])
                            rr_.append((rc_, brc_))
                        (r0, br0), (r1, br1) = rr_
                        o0, bo0 = f32_r.get(); o1, bo1 = f32_r.get()
                        fw.op(DVE, lambda h, o0=o0, r0=r0, p=acc[0][0]: h.tensor_tensor(out=o0[:, 0:nqt], in0=p[:, 0:nqt], in1=r0[:, 0:nqt], op=ALU.mult), reads=[acc[0][1], br0], writes=[bo0])
                        fw.op(DVE, lambda h, o1=o1, r1=r1, p=acc[1][0]: h.tensor_tensor(out=o1[:, 0:nqt], in0=p[:, 0:nqt], in1=r1[:, 0:nqt], op=ALU.mult), reads=[acc[1][1], br1], writes=[bo1])
                        fw.op(DVE, lambda h, o0=o0, o1=o1: h.scalar_tensor_tensor(out=o0[:, 0:nqt], in0=o1[:, 0:nqt], scalar=lamc[:, 0:1], in1=o0[:, 0:nqt], op0=ALU.mult, op1=ALU.add),
                              reads=[bo1, b_lamc], writes=[bo0])
                        sq, bsq = bf_r.get()
                        fw.op(ACT, lambda h, sq=sq, o0=o0: h.activation(out=sq[:, 0:nqt], in_=o0[:, 0:nqt], func=AF.Square), reads=[bo0], writes=[bsq])
                        ps, bps = psA.get()
                        fw.op(PE, lambda h, ps=ps, sq=sq: h.matmul(ps[:, 0:nqt], lhsT=cmat[:, C_128, :], rhs=sq[:, 0:nqt], start=True, stop=True), reads=[bsq, b_cmat], writes=[bps])
                        rs, brs = rs_r.get()
                        rsqrt_eps(ps[:, 0:nqt], rs[:, 0:nqt], bps, brs)
                        t1, bt1 = f32_r.get()
                        fw.op(DVE, lambda h, t1=t1, o0=o0, rs=rs: h.scalar_tensor_tensor(out=t1[:, 0:nqt], in0=o0[:, 0:nqt], scalar=gc[:, 10:11], in1=rs[:, 0:nqt], op0=ALU.mult, op1=ALU.mult),
                              reads=[bo0, brs, b_gc], writes=[bt1])
                        ob, bob = at_o.get()
                        fw.op(ACT, lambda h, t1=t1, ob=ob: h.activation(out=ob[:, 0:nqt], in_=t1[:, 0:nqt], func=AF.Copy, scale=(1.0 - lam_init)), reads=[bt1], writes=[bob])
                        fw.dma(SP, yc_s[head, :, qq:qq + nqt], ob[:, 0:nqt], reads=[bob], writes=[b_yc])

        arena.reset()
        y3_r = Ring(ov, "y3", [128, 12, TT], BF16, 1)
        mg_r = Ring(ov, "mg", [128, 8, TT], BF16, 1)
        macc_r = Ring(ov, "macc", [128, 4, TT], F32, 1)
        aTraw = ov("aTraw", [128, 8192], F32)
        aT_view = aTraw.bitcast(BF16).rearrange("p (a b) -> p a b", b=TT)
        xo_view = aTraw[:, 0:4096].rearrange("p (g f) -> p g f", f=D)
        b_aTraw = Buf()

        def phase3(l, ti, last):
            cond = 1 if ti > 0 else 0
            t0 = ti * TT
            def p3_fetch(tj):
                hT_, bh_ = hT_r.get()
                fw.dma(SP, hT_[:], hT_s[:, :, tj * TT:(tj + 1) * TT], reads=[b_hTs[tj]], writes=[bh_])
                y3_, by3_ = y3_r.get()
                for bi_, (src, bsrc) in enumerate(((ya_s, b_ya), (yb_s, b_yb), (yc_s, b_yc))):
                    fw.dma(SP, y3_[:, bi_ * 4:(bi_ + 1) * 4, :], src[:, :, tj * TT:(tj + 1) * TT].rearrange("h p t -> p h t"), reads=[bsrc], writes=[by3_])
                return hT_, bh_, y3_, by3_

            hT, bh, y3, by3 = p3_pref.pop((l, ti), None) or p3_fetch(ti)
            xt, bxt = load_xT(ti)
            mg, bmg = mg_r.get()
            macc, bmacc = macc_r.get()
            for half in range(2):
                for bi in range(3):
                    c0 = O_G + bi * 1024 + half * 512
                    wg, bwg = load_w8(w_in[l, :, c0:c0 + 512], 512)
                    wo, bwo = w4.get()
                    fw.dma(POOL, wo[:, :, 0:512], w_o3[l, bi, :, half * 512:(half + 1) * 512].rearrange("(k p) c -> p k c", p=128), writes=[bwo])
                    for j in range(4):
                        oc = half * 4 + j
                        psg, bpsg = proj_fm(wg, bwg, j * 128, 128, lambda k: hT[:, k, :], [bh], 8)
                        sg, bsg = f32_r.get()
                        fw.op(ACT, lambda h: h.activation(out=sg[:], in_=psg[:], func=AF.Sigmoid), reads=[bpsg], writes=[bsg])
                        pso, bpso = proj_fm(wo, bwo, j * 128, 128, lambda k: y3[:, bi * 4 + k, :], [by3], 4)
                        if bi == 0:
                            fw.op(DVE, lambda h: h.tensor_tensor(out=macc[:, j, :], in0=sg[:], in1=pso[:], op=ALU.mult), reads=[bsg, bpso], writes=[bmacc])
                        else:
                            tmp, btmp = f32_r.get()
                            fw.op(DVE, lambda h: h.tensor_tensor(out=tmp[:], in0=sg[:], in1=pso[:], op=ALU.mult), reads=[bsg, bpso], writes=[btmp])
                            if bi == 1:
                                fw.op(DVE, lambda h: h.tensor_tensor(out=macc[:, j, :], in0=macc[:, j, :], in1=tmp[:], op=ALU.add), reads=[btmp], writes=[bmacc])
                            else:
                                fw.op(DVE, lambda h: h.tensor_tensor(out=mg[:, oc, :], in0=macc[:, j, :], in1=tmp[:], op=ALU.add), reads=[btmp, bmacc], writes=[bmg])
            if ti + 1 < NT:
                p3_pref[(l, ti + 1)] = p3_fetch(ti + 1)
            for half in range(2):
                wt, bw = load_w8(w_out[l, :, half * 512:(half + 1) * 512], 512)
                for j in range(4):
                    oc = half * 4 + j
                    ps, bps = proj_fm(wt, bw, j * 128, 128, lambda k: mg[:, k, :], [bmg], 8)
                    fw.op(DVE, lambda h, oc=oc, ps=ps: h.scalar_tensor_tensor(out=xt[:, oc, :], in0=ps[:], scalar=modT[:, 16 + oc, cond:cond + 1], in1=xt[:, oc, :], op0=ALU.mult, op1=ALU.add),
                          reads=[bps, b_modT_b], writes=[bxt])
            h2, bh2 = norm_mod(xt, bxt, A2, 24, cond)
            aT, baT = aT_view, b_aTraw
            for fb in range(8):
                wt, bw = load_w8(w_m1[l, :, fb * 512:(fb + 1) * 512], 512)
                for j in range(4):
                    ps, bps = proj_fm(wt, bw, j * 128, 128, lambda k: h2[:, k, :], [bh2], 8)
                    rl, brl = f32_r.get()
                    fw.op(ACT, lambda h, ps=ps, rl=rl: h.activation(out=rl[:], in_=ps[:], func=AF.Relu), reads=[bps], writes=[brl])
                    fw.op(DVE, lambda h, rl=rl, fb=fb, j=j: h.tensor_tensor(out=aT[:, fb * 4 + j, :], in0=rl[:], in1=rl[:], op=ALU.mult), reads=[brl], writes=[baT])
            for half in range(2):
                accs = [psB.get() for _ in range(4)]
                for kb8 in range(4):
                    wt, bw = load_w8(w_m2[l, kb8 * 1024:(kb8 + 1) * 1024, half * 512:(half + 1) * 512], 512)
                    for j in range(4):
                        ps, bps = accs[j]
                        for k in range(8):
                            kk = kb8 * 8 + k
                            fw.op(PE, lambda h: h.matmul(ps[:], lhsT=wt[:, k, j * 128:(j + 1) * 128], rhs=aT[:, kk, :], start=(kk == 0), stop=(kk == 31)),
                                  reads=[bw, baT], writes=[bps], inc=(k == 7))
                for j in range(4):
                    oc = half * 4 + j
                    ps, bps = accs[j]
                    fw.op(DVE, lambda h: h.scalar_tensor_tensor(out=xt[:, oc, :], in0=ps[:], scalar=modT[:, 40 + oc, cond:cond + 1], in1=xt[:, oc, :], op0=ALU.mult, op1=ALU.add),
                          reads=[bps, b_modT_b], writes=[bxt])
            if not last:
                fw.dma(SP, xT_s[:, :, t0:t0 + TT].rearrange("k p t -> p k t"), xt[:], reads=[bxt], writes=[b_xT[ti]])
            else:
                xo, bxo = xo_view, b_aTraw
                for g in range(4):
                    for half in range(2):
                        ps, bps = psA.get()
                        for k in range(4):
                            kk = half * 4 + k
                            fw.op(PE, lambda h, g=g, k=k, kk=kk, ps=ps: h.transpose(ps[:, k * 128:(k + 1) * 128], xt[:, kk, g * 128:(g + 1) * 128], ident[:]),
                                  reads=[bxt, b_ident], writes=[bps], inc=(k == 3))
                        evac(ps[:], xo[:, g, half * 512:(half + 1) * 512], bps, bxo, eng=(ACT if half else DVE))
                ob = Buf(); out_bufs.append(ob)
                fw.dma(SP, y_out[t0:t0 + TT, :].rearrange("(g p) f -> p g f", p=128), xo[:], reads=[bxo], writes=[ob])

        import os
        KSTOP = int(os.environ.get("KSTOP", "99"))
        step = [0]

        def reached():
            step[0] += 1
            return step[0] > KSTOP

        for l in range(L):
            if reached():
                break
            lam_init = layer_setup(l)
            fw.barrier()
            vaug_init[0] = vaug_init[1] = False
            if reached():
                break
            prep_cache(l)
            if reached():
                break
            for ti in range(NT):
                phase1(l, ti)
            fw.barrier()
            if reached():
                break
            seqs = [(0, 256, 0, False), (256, 256, 1, False), (NP_TOK, NS_TOK, None, True)]
            gla_dir(l, 0, seqs)
            fw.barrier()
            gla_dir(l, 1, seqs)
            fw.barrier()
            if reached():
                break
            groups = [(0, 256, 0, 256), (256, 256, 256, 256), (NP_TOK, NS_TOK, NP_TOK, TKS)]
            mla_attn(l, groups)
            if reached():
                break
            diff_attn(l, groups, lam_init)
            if reached():
                break
            fw.barrier()
            for ti in range(NT):
                phase3(l, ti, l == L - 1)
            fw.barrier()

        fw.finish(out_bufs)
    return nc


def _rope_tables():
    t = np.arange(NS_TOK)
    row, col = t // 64, t % 64

    def cs(nf, pos):
        inv = (10000.0 ** (-np.arange(nf, dtype=np.float32) / nf)).astype(np.float32)
        ang = pos.astype(np.float32)[None, :] * inv[:, None]
        c, s = np.cos(ang).astype(np.float32), np.sin(ang).astype(np.float32)
        return np.concatenate([c, c], 0), np.concatenate([s, s], 0)

    cr, sr = cs(8, row); cc, sc_ = cs(8, col)
    cM = np.concatenate([cr, cc, np.ones((64, NS_TOK), np.float32)], 0)
    sM = np.concatenate([sr, sc_, np.zeros((64, NS_TOK), np.float32)], 0)
    cr, sr = cs(16, row); cc, sc_ = cs(16, col)
    c64 = np.concatenate([cr, cc], 0); s64 = np.concatenate([sr, sc_], 0)
    cD = np.concatenate([c64, c64], 0); sD = np.concatenate([s64, s64], 0)
    return np.stack([cM, sM]).astype(np.float32), np.stack([cD, sD]).astype(np.float32)


def _rot_matrix(nf):
    R = np.zeros((2 * nf, 2 * nf), np.float32)
    for i in range(nf):
        R[i, nf + i] = -1.0
        R[nf + i, i] = 1.0
    return R


def _consts():
    bf = ml_dtypes.bfloat16
    cmat = np.zeros((128, 8, 128), np.float32)
    cmat[:, 0, :] = 1.0 / 1024
    cmat[:, 1, :] = 1.0 / 384
    cmat[:, 2, :] = 1.0 / 256
    cmat[:96, 3, :96] = 1.0 / 96
    cmat[:64, 4, :64] = 1.0 / 64
    cmat[64:, 4, 64:] = 1.0 / 64
    cmat[:, 5, :] = 1.0 / 128
    cmat[:, 6, :] = 1.0
    cmat[:32, 7, :32] = np.eye(32)
    pm = np.zeros((128, 3, 128), np.float32)
    R96 = np.zeros((96, 96), np.float32)
    R96[0:16, 0:16] = _rot_matrix(8); R96[16:32, 16:32] = _rot_matrix(8)
    pm[:96, 0, :96] = R96.T
    R64 = np.zeros((64, 64), np.float32)
    R64[0:32, 0:32] = _rot_matrix(16); R64[32:64, 32:64] = _rot_matrix(16)
    R128 = np.zeros((128, 128), np.float32)
    R128[:64, :64] = R64; R128[64:, 64:] = R64
    pm[:, 1, :] = R128.T
    i = np.arange(64)
    tri = np.zeros((64, 4, 64), np.float32)
    tri[:, 0, :] = (i[:, None] <= i[None, :])
    tri[:, 1, :] = (i[:, None] >= i[None, :])
    tri[:, 2, :] = (i[:, None] > i[None, :])
    tri[:, 3, :] = (i[:, None] < i[None, :])
    return cmat.astype(bf), pm.astype(bf), tri


SAMPLE_CORES = [0, 1, 4, 5]

_PROG = {}


def _get_prog():
    if "nc" not in _PROG:
        _PROG["nc"] = build_program()
    return _PROG["nc"]


def make_in_maps(x_prompt, x_sample, state_gla, cache_mla_ckv, cache_mla_krope, cache_diff_k, cache_diff_v, c, c_ctx, w_mod, b_mod, g_norm1, g_norm2, w_in,
                 w_gla_a2, b_gla_a, g_gla_out, g_mla_qa, g_mla_kva, w_mla_uq, w_mla_uk, w_mla_uv, g_mla_q, g_mla_k, g_diff_q, g_diff_k, lam_qk, g_diff_sub,
                 w_o_gla, w_o_mla, w_o_diff, w_out, w_mlp1, w_mlp2):
    f = lambda a: np.ascontiguousarray(np.asarray(a, dtype=np.float32))
    cmat, pm, tri = _consts()
    ropeM, ropeD = _rope_tables()
    perm = np.concatenate([np.arange(64, 96), np.arange(0, 64)])
    w_uq_p = f(w_mla_uq).reshape(L, 384, 8, 96)[:, :, :, perm]
    w_ukp = np.zeros((L, 256, 8, 96), np.float32)
    w_ukp[:, :, :, 32:] = f(w_mla_uk).reshape(L, 256, 8, 64)
    gcol = np.zeros((L, 128, 12), np.float32)
    gcol[:, :, 0:3] = f(g_mla_qa).reshape(L, 3, 128).transpose(0, 2, 1)
    gcol[:, :, 3:5] = f(g_mla_kva).reshape(L, 2, 128).transpose(0, 2, 1)
    gcol[:, :96, 5] = f(g_mla_q)[:, perm]
    gcol[:, :96, 6] = f(g_mla_k)[:, perm]
    gcol[:, :, 7] = np.tile(f(g_diff_q), (1, 2))
    gcol[:, :, 8] = np.tile(f(g_diff_k), (1, 2))
    gcol[:, :, 9] = f(g_gla_out)
    gcol[:, :, 10] = f(g_diff_sub)
    w_a2bd = np.zeros((L, 32, 512), np.float32)
    w_a2bd[:, 0:16, 0:256] = f(w_gla_a2)[:, 0]
    w_a2bd[:, 16:32, 256:512] = f(w_gla_a2)[:, 1]
    b_a = f(b_gla_a).reshape(L, 1, 512)
    gnT = np.concatenate([f(g_norm1).reshape(L, 8, 128).transpose(0, 2, 1), f(g_norm2).reshape(L, 8, 128).transpose(0, 2, 1)], axis=2)
    b_modT = f(b_mod).reshape(L, 48, 128).transpose(0, 2, 1)
    w_o3 = np.stack([f(w_o_gla), f(w_o_mla), f(w_o_diff)], axis=1)
    shared = dict(w_mod=f(w_mod), b_modT=f(b_modT), gnT=f(gnT), w_in=f(w_in), w_a2bd=w_a2bd, b_a=b_a, gcol=gcol, w_uq=f(w_uq_p), w_ukp=w_ukp,
                  w_uv=f(w_mla_uv), lam_qk=f(lam_qk).reshape(L, 1, 256), w_o3=f(w_o3), w_out=f(w_out), w_m1=f(w_mlp1), w_m2=f(w_mlp2),
                  ident=np.eye(128, dtype=np.float32), cmat=cmat, pmat=pm, tri=tri, ropeM=ropeM, ropeD=ropeD)
    xp, xs = f(x_prompt), f(x_sample)
    in_maps = []
    for core in range(8):
        m = dict(shared)
        if core in SAMPLE_CORES:
            b = SAMPLE_CORES.index(core)
            xs_b, c_b = xs[b], f(c)[b]
            m["st_gla"] = f(f(state_gla)[b].transpose(0, 1, 3, 2, 4))
            m["c_ckv"] = f(cache_mla_ckv)[b]
            m["c_kr"] = f(cache_mla_krope)[b]
            m["c_dk"] = f(cache_diff_k)[b].reshape(L, PAST, 512)
            m["c_dv"] = f(cache_diff_v)[b].reshape(L, PAST, 512)
        else:
            xs_b, c_b = np.zeros((NS_TOK, D), np.float32), np.zeros((D,), np.float32)
            m["st_gla"] = np.zeros((L, 2, 64, 4, 128), np.float32)
            m["c_ckv"] = np.zeros((L, PAST, 256), np.float32)
            m["c_kr"] = np.zeros((L, PAST, 32), np.float32)
            m["c_dk"] = np.zeros((L, PAST, 512), np.float32)
            m["c_dv"] = np.zeros((L, PAST, 512), np.float32)
        m["xin"] = np.concatenate([xp[2 * core], xp[2 * core + 1], xs_b], axis=0)
        cond = np.stack([f(c_ctx), c_b], axis=0)
        m["condT"] = f(cond.reshape(2, 8, 128).transpose(2, 1, 0))
        in_maps.append(m)
    return in_maps


def kernel(**inputs):
    nc = _get_prog()
    in_maps = make_in_maps(**inputs)
    res = run_bass_kernel_spmd(nc, in_maps, core_ids=list(range(8)))
    R = res.results
    y_prompt = np.zeros((16, 256, D), np.float32)
    y_sample = np.zeros((4, NS_TOK, D), np.float32)
    n_gla = np.zeros((16, L, 2, 4, 64, 128), np.float32)
    n_ckv = np.zeros((16, L, 256, 256), np.float32)
    n_kr = np.zeros((16, L, 256, 32), np.float32)
    n_dk = np.zeros((16, L, 256, 4, 2, 64), np.float32)
    n_dv = np.zeros((16, L, 256, 4, 128), np.float32)
    for core in range(8):
        r = R[core]
        for s in range(2):
            bi = 2 * core + s
            y_prompt[bi] = r["y_out"][s * 256:(s + 1) * 256]
            n_gla[bi] = r["o_gla"][:, s].transpose(0, 1, 3, 2, 4)
            n_ckv[bi] = r["o_ckv"][:, s * 256:(s + 1) * 256]
            n_kr[bi] = r["o_kr"][:, s * 256:(s + 1) * 256]
            n_dk[bi] = r["o_dk"][:, s * 256:(s + 1) * 256].reshape(L, 256, 4, 2, 64)
            n_dv[bi] = r["o_dv"][:, s * 256:(s + 1) * 256].reshape(L, 256, 4, 128)
        if core in SAMPLE_CORES:
            y_sample[SAMPLE_CORES.index(core)] = r["y_out"][NP_TOK:]
    return (y_prompt, y_sample, n_gla, n_ckv, n_kr, n_dk, n_dv)
```

```python
import math
import numpy as np
import ml_dtypes
from contextlib import ExitStack
import concourse.bass as bass
import concourse.mybir as mybir
from concourse.bass_utils import run_bass_kernel_spmd

F32 = mybir.dt.float32
BF16 = mybir.dt.bfloat16
AF = mybir.ActivationFunctionType
ALU = mybir.AluOpType

D = 1024
L = 2
NP_TOK = 512
NS_TOK = 4096
T = NP_TOK + NS_TOK
TT = 512
NT = T // TT
PAST = 256
TKS = PAST + NS_TOK
EPS = 1e-6
O_AQ, O_AK, O_AV, O_AR, O_AA, O_QD, O_KVD, O_KR, O_DQ, O_DK, O_DV, O_G = 0, 256, 512, 1024, 1536, 1568, 1952, 2208, 2240, 2752, 3264, 3776
IN_COLS = 6848


class Buf:
    __slots__ = ("lw", "rd", "psum")

    def __init__(self, psum=False):
        self.lw = None
        self.rd = []
        self.psum = psum


class _Rec:
    def __init__(self):
        self.call = None

    def __getattr__(self, name):
        def f(*a, **k):
            self.call = (name, a, k)
            return self
        return f


class Eng:
    def __init__(self, fw, name, handle):
        self.name = name
        self.h = handle
        self.sem = fw.new_sem("e_" + name)
        self.count = 0
        self.waited = {}
        self.thunks = []
        self.snaps = {}


class FW:
    def __init__(self, nc, ctx, n_dma_sems=48):
        self.nc = nc
        self.ctx = ctx
        self.pe = Eng(self, "pe", nc.tensor)
        self.dve = Eng(self, "dve", nc.vector)
        self.act = Eng(self, "act", nc.scalar)
        self.pool = Eng(self, "pool", nc.gpsimd)
        self.sp = Eng(self, "sp", nc.sync)
        self.dma_sems = [self.new_sem(f"d{i}") for i in range(n_dma_sems)]
        self.dma_cnt = [0] * n_dma_sems
        self.snap_of = {}
        self.dma_rr = 0
        self.ew_rr = 0

    def new_sem(self, name):
        return self.ctx.enter_context(self.nc.semaphore(name))

    def _wait(self, E, ev):
        if ev is None:
            return
        sem, val = ev
        if sem is E.sem and E.name == "pe":
            return
        key = id(sem)
        if E.waited.get(key, 0) >= val:
            return
        E.waited[key] = val
        E.thunks.append(lambda h=E.h, s=sem, v=val: h.wait_ge(s, v))
        snap = self.snap_of.get((key, val))
        if snap:
            w = E.waited
            for k2, v2 in snap.items():
                if w.get(k2, 0) < v2:
                    w[k2] = v2

    def _deps(self, E, reads, writes):
        for b in reads:
            self._wait(E, b.lw)
        for b in writes:
            self._wait(E, b.lw)
            for ev in b.rd:
                self._wait(E, ev)

    def _post(self, ev, reads, writes):
        for b in reads:
            b.rd.append(ev)
            if len(b.rd) > 64:
                b.rd = b.rd[-48:]
        for b in writes:
            b.lw = ev
            b.rd = []

    def op(self, E, fn, reads=(), writes=(), inc=True):
        if any(b.psum for b in reads):
            writes = list(writes) + [b for b in reads if b.psum]
            reads = [b for b in reads if not b.psum]
        self._deps(E, reads, writes)
        rec = _Rec()
        fn(rec)
        mname, margs, mkw = rec.call
        if inc:
            E.count += 1
            ev = (E.sem, E.count)
            self.snap_of[(id(E.sem), E.count)] = dict(E.waited)
            E.thunks.append(lambda h=E.h, n=mname, a=margs, k=mkw, s=E.sem: getattr(h, n)(*a, **k).then_inc(s, 1))
        else:
            ev = (E.sem, E.count + 1)
            E.thunks.append(lambda h=E.h, n=mname, a=margs, k=mkw: getattr(h, n)(*a, **k))
        self._post(ev, reads, writes)
        return ev

    def dma(self, Q, out_ap, in_ap, reads=(), writes=(), **kw):
        self._deps(Q, reads, writes)
        i = self.dma_rr
        self.dma_rr = (self.dma_rr + 1) % len(self.dma_sems)
        self.dma_cnt[i] += 16
        sem = self.dma_sems[i]
        ev = (sem, self.dma_cnt[i])
        self.snap_of[(id(sem), self.dma_cnt[i])] = dict(Q.waited)
        Q.thunks.append(lambda h=Q.h, o=out_ap, a=in_ap, s=sem, k=kw: h.dma_start(out=o, in_=a, **k).then_inc(s, 16))
        self._post(ev, reads, writes)
        return ev

    def barrier(self):
        engs = [self.pe, self.dve, self.act, self.pool, self.sp]
        for E in engs:
            for E2 in engs:
                if E2 is not E and E2.count > 0:
                    self._wait(E, (E2.sem, E2.count))
            for i, sem in enumerate(self.dma_sems):
                if self.dma_cnt[i] > 0:
                    self._wait(E, (sem, self.dma_cnt[i]))

    def finish(self, final_bufs):
        for b in final_bufs:
            self._wait(self.sp, b.lw)
        nc = self.nc
        engs = self
        with nc.Block() as block:
            @block.tensor
            def _(e):
                for t in engs.pe.thunks:
                    t()

            @block.vector
            def _(e):
                for t in engs.dve.thunks:
                    t()

            @block.scalar
            def _(e):
                for t in engs.act.thunks:
                    t()

            @block.gpsimd
            def _(e):
                for t in engs.pool.thunks:
                    t()

            @block.sync
            def _(e):
                for t in engs.sp.thunks:
                    t()


class Arena:
    def __init__(self, tensor, nbytes):
        self.t = tensor
        self.nbytes = nbytes
        self.off = 0
        self.peak = 0

    def reset(self):
        self.off = 0

    def alloc(self, name, shape, dt):
        esz = 2 if dt == BF16 else 4
        n = 1
        for d in shape[1:]:
            n *= d
        nb = (n * esz + 63) // 64 * 64
        assert self.off + nb <= self.nbytes, (name, self.off, nb, self.nbytes)
        ap = self.t[0:shape[0], self.off // 4:(self.off + nb) // 4]
        self.off += nb
        self.peak = max(self.peak, self.off)
        if dt == BF16:
            ap = ap.bitcast(BF16)
        ap = ap[:, 0:n]
        if len(shape) == 3:
            ap = ap.rearrange("p (a b) -> p a b", b=shape[2])
        elif len(shape) == 4:
            ap = ap.rearrange("p (a b c) -> p a b c", b=shape[2], c=shape[3])
        return ap


class Ring:
    def __init__(self, alloc, name, shape, dt, n, psum=False):
        self.t = [alloc(f"{name}{i}", shape, dt) for i in range(n)]
        self.b = [Buf(psum) for _ in range(n)]
        self.i = 0

    def get(self):
        i = self.i
        self.i = (i + 1) % len(self.t)
        return self.t[i], self.b[i]


def build_program(debug=False):
    nc = bass.Bass("TRN2", target_bir_lowering=False)
    dram_in = lambda name, shape, dt=F32: nc.dram_tensor(name, list(shape), dt, kind="ExternalInput").ap()
    dram_out = lambda name, shape, dt=F32: nc.dram_tensor(name, list(shape), dt, kind="ExternalOutput").ap()
    dram_tmp = lambda name, shape, dt=F32: nc.dram_tensor(name, list(shape), dt).ap()

    xin = dram_in("xin", [T, D])
    condT = dram_in("condT", [128, 8, 2])
    st_gla = dram_in("st_gla", [L, 2, 64, 4, 128])
    c_ckv = dram_in("c_ckv", [L, PAST, 256])
    c_kr = dram_in("c_kr", [L, PAST, 32])
    c_dk = dram_in("c_dk", [L, PAST, 512])
    c_dv = dram_in("c_dv", [L, PAST, 512])
    w_mod = dram_in("w_mod", [L, D, 6 * D])
    b_modT = dram_in("b_modT", [L, 128, 48])
    gnT = dram_in("gnT", [L, 128, 16])
    w_in = dram_in("w_in", [L, D, IN_COLS])
    w_a2bd = dram_in("w_a2bd", [L, 32, 512])
    b_a = dram_in("b_a", [L, 1, 512])
    gcol = dram_in("gcol", [L, 128, 12])
    w_uq = dram_in("w_uq", [L, 384, 8, 96])
    w_ukp = dram_in("w_ukp", [L, 256, 8, 96])
    w_uv = dram_in("w_uv", [L, 256, 512])
    lam_qk = dram_in("lam_qk", [L, 1, 256])
    w_o3 = dram_in("w_o3", [L, 3, 512, D])
    w_out = dram_in("w_out", [L, D, D])
    w_m1 = dram_in("w_m1", [L, D, 4 * D])
    w_m2 = dram_in("w_m2", [L, 4 * D, D])
    ident_d = dram_in("ident", [128, 128])
    cmat_d = dram_in("cmat", [128, 8, 128], BF16)
    pmat_d = dram_in("pmat", [128, 3, 128], BF16)
    tri_d = dram_in("tri", [64, 4, 64])
    ropeM = dram_in("ropeM", [2, 96, NS_TOK])
    ropeD = dram_in("ropeD", [2, 128, NS_TOK])

    y_out = dram_out("y_out", [T, D])
    o_gla = dram_out("o_gla", [L, 2, 2, 64, 4, 128])
    o_ckv = dram_out("o_ckv", [L, NP_TOK, 256])
    o_kr = dram_out("o_kr", [L, NP_TOK, 32])
    o_dk = dram_out("o_dk", [L, NP_TOK, 512])
    o_dv = dram_out("o_dv", [L, NP_TOK, 512])

    mk = dram_out if debug else dram_tmp
    xT_s = dram_tmp("xT_s", [8, 128, T])
    gq_s = dram_tmp("gq_s", [64, 4, T], BF16)
    gk_s = dram_tmp("gk_s", [64, 4, T], BF16)
    gkt_s = dram_tmp("gkt_s", [T, 256], BF16)
    gvt_s = dram_tmp("gvt_s", [T, 512], BF16)
    gG_s = dram_tmp("gG_s", [T, 512])
    go_s = dram_tmp("go_s", [128, 4, T])
    q_s = dram_tmp("q_s", [8, 96, T], BF16)
    k_s = dram_tmp("k_s", [8, 96, NP_TOK + TKS], BF16)
    v_s = dram_tmp("v_s", [8, 128, (NP_TOK + TKS) // 128, 128], BF16)
    dq_s = dram_tmp("dq_s", [4, 128, T], BF16)
    dk_s = dram_tmp("dk_s", [4, 128, NP_TOK + TKS], BF16)
    dv_s = dram_tmp("dv_s", [4, 128, (NP_TOK + TKS) // 128, 128], BF16)
    hT_s = dram_tmp("hT_s", [128, 8, T], BF16)
    ya_s = mk("ya_s", [4, 128, T], BF16)
    yb_s = mk("yb_s", [4, 128, T], BF16)
    yc_s = mk("yc_s", [4, 128, T], BF16)

    with ExitStack() as ctx:
        fw = FW(nc, ctx)
        PE, DVE, ACT, POOL, SP = fw.pe, fw.dve, fw.act, fw.pool, fw.sp
        sb = lambda name, shape, dt=F32: ctx.enter_context(nc.sbuf_tensor("s_" + name, list(shape), dt))
        psb = lambda name, shape, dt=F32: ctx.enter_context(nc.psum_tensor("p_" + name, list(shape), dt))

        psA = Ring(psb, "psA", [128, 512], F32, 4, psum=True)
        psB = Ring(psb, "psB", [128, 512], F32, 4, psum=True)

        OVB = 72 * 1024
        arena = Arena(sb("arena", [128, OVB // 4], F32), OVB)
        ov = arena.alloc

        def ew():
            fw.ew_rr ^= 1
            return DVE if fw.ew_rr else POOL

        ident = sb("ident", [128, 128]); b_ident = Buf()
        cmat = sb("cmat", [128, 8, 128], BF16); b_cmat = Buf()
        pmat = sb("pmat", [128, 3, 128], BF16); b_pmat = Buf()
        tri = sb("tri", [64, 4, 64]); b_tri = Buf()
        ones_row = sb("ones_row", [1, 128], BF16); b_onesrow = Buf()
        ones_rowf = sb("ones_rowf", [1, 128]); b_onesrowf = Buf()
        ones_col = sb("ones_col", [64, 1]); b_onescol = Buf()
        onesf = sb("onesf", [128, 128]); b_onesf = Buf()
        fw.dma(SP, ident[:], ident_d, writes=[b_ident])
        fw.dma(SP, cmat[:], cmat_d, writes=[b_cmat])
        fw.dma(SP, pmat[:], pmat_d, writes=[b_pmat])
        fw.dma(SP, tri[:], tri_d, writes=[b_tri])
        fw.op(DVE, lambda h: h.memset(ones_row[:], 1.0), writes=[b_onesrow])
        fw.op(DVE, lambda h: h.memset(ones_rowf[:], 1.0), writes=[b_onesrowf])
        fw.op(DVE, lambda h: h.memset(ones_col[:], 1.0), writes=[b_onescol])
        fw.op(DVE, lambda h: h.memset(onesf[:], 1.0), writes=[b_onesf])
        C_1024, C_384, C_256, C_96, C_BLK64, C_128, C_ONE, C_SEL32 = range(8)
        P_96, P_128, _ = range(3)

        cT = sb("cT", [128, 8, 2]); b_cT = Buf()
        scT = sb("scT", [128, 8, 2]); b_scT = Buf()
        fw.dma(SP, cT[:], condT, writes=[b_cT])
        fw.op(ACT, lambda h: h.activation(out=scT[:], in_=cT[:], func=AF.Silu), reads=[b_cT], writes=[b_scT])

        modT = sb("modT", [128, 48, 2]); b_modT_b = Buf()
        bmod = sb("bmod", [128, 48]); b_bmod = Buf()
        gn = sb("gn", [128, 16]); b_gn = Buf()
        A1 = sb("A1", [128, 8, 2]); A2 = sb("A2", [128, 8, 2]); b_A = Buf()
        gc = sb("gc", [128, 12]); b_gc = Buf()
        wa2 = sb("wa2", [32, 512], BF16); b_wa2 = Buf()
        ba = sb("ba", [1, 512], BF16); b_ba = Buf()
        lamt = sb("lamt", [1, 256]); b_lamt = Buf()
        lam1 = sb("lam1", [1, 8]); b_lam1 = Buf()
        lamc = sb("lamc", [128, 2]); b_lamc = Buf()
        arena.reset()
        wmod_r = Ring(ov, "wmod", [128, 8, 256], F32, 2)

        w8 = Ring(sb, "w8", [128, 8, 512], BF16, 3)
        w4 = Ring(sb, "w4", [128, 4, 1024], BF16, 2)

        def load_w8(src_ap_rows_cols, ncols, nk=8):
            t, b = w8.get()
            fw.dma(POOL, t[:, 0:nk, 0:ncols], src_ap_rows_cols.rearrange("(k p) c -> p k c", p=128), writes=[b])
            return t, b

        xt_r = Ring(sb, "xt", [128, 8, TT], F32, 1)
        sq_r = Ring(sb, "sq", [128, 8, TT], BF16, 1)
        hT_r = Ring(sb, "hT", [128, 8, TT], BF16, 2)
        rs_r = Ring(sb, "rs", [128, TT], F32, 4)
        f32_r = Ring(sb, "f32", [128, TT], F32, 12)
        bf_r = Ring(sb, "bf", [128, TT], BF16, 12)

        def evac(ps_ap, out_ap, b_ps, b_out, eng=None, func=None, scale=1.0):
            if func is not None or eng is ACT:
                f = func if func is not None else AF.Copy
                return fw.op(ACT, lambda h: h.activation(out=out_ap, in_=ps_ap, func=f, scale=scale), reads=[b_ps], writes=[b_out])
            return fw.op(DVE, lambda h: h.tensor_copy(out=out_ap, in_=ps_ap), reads=[b_ps], writes=[b_out])

        b_xT = [Buf() for _ in range(NT)]
        arena.reset()
        xtok_r = Ring(ov, "xtok", [128, 4, D], F32, 2)

        def phase0(ti):
            xk, bxk = xtok_r.get()
            fw.dma(SP, xk[:], xin[ti * TT:(ti + 1) * TT, :].rearrange("(g p) f -> p g f", p=128), writes=[bxk])
            xt, bxt = xt_r.get()
            for k in range(8):
                ps, bps = psA.get()
                for g in range(4):
                    fw.op(PE, lambda h, ps=ps, g=g, k=k: h.transpose(ps[:, g * 128:(g + 1) * 128], xk[:, g, k * 128:(k + 1) * 128], ident[:]),
                          reads=[bxk, b_ident], writes=[bps], inc=(g == 3))
                evac(ps[:], xt[:, k, :], bps, bxt, eng=(ACT if k % 2 else DVE))
            fw.dma(SP, xT_s[:, :, ti * TT:(ti + 1) * TT].rearrange("k p t -> p k t"), xt[:], reads=[bxt], writes=[b_xT[ti]])

        import os
        for ti in range(int(os.environ.get("KPH0", str(NT)))):
            phase0(ti)
        fw.barrier()

        def rsqrt_eps(src_ap, dst_ap, bsrc, bdst):
            fw.op(ACT, lambda h: h.activation(out=dst_ap, in_=src_ap, func=AF.Ln, bias=EPS), reads=[bsrc], writes=[bdst])
            fw.op(ACT, lambda h: h.activation(out=dst_ap, in_=dst_ap, func=AF.Exp, scale=-0.5), reads=[bdst], writes=[bdst])

        def rms_rstd(sq_aps, ones_ap, nparts, eps=EPS):
            ps, bps = psA.get()
            n = len(sq_aps)
            for i, (a, b) in enumerate(sq_aps):
                fw.op(PE, lambda h, a=a, i=i: h.matmul(ps[0:nparts, :], lhsT=ones_ap, rhs=a, start=(i == 0), stop=(i == n - 1)),
                      reads=[b, b_cmat], writes=[bps], inc=(i == n - 1))
            rs, brs = rs_r.get()
            rsqrt_eps(ps[0:nparts, :], rs[0:nparts, :], bps, brs)
            return rs, brs

        def load_xT(ti):
            xt, bxt = xt_r.get()
            fw.dma(SP, xt[:], xT_s[:, :, ti * TT:(ti + 1) * TT].rearrange("k p t -> p k t"), reads=[b_xT[ti]], writes=[bxt])
            return xt, bxt

        def norm_mod(xt, bxt, Amod, shift_lo, cond):
            sq, bsq = sq_r.get()
            fw.op(ACT, lambda h: h.activation(out=sq[:], in_=xt[:], func=AF.Square), reads=[bxt], writes=[bsq])
            rs, brs = rms_rstd([(sq[:, k, :], bsq) for k in range(8)], cmat[:, C_1024, :], 128)
            hT, bh = hT_r.get()
            for k in range(8):
                tmp, btmp = f32_r.get()
                e = DVE
                fw.op(e, lambda h, k=k, tmp=tmp: h.tensor_tensor(out=tmp[:], in0=xt[:, k, :], in1=rs[:], op=ALU.mult), reads=[bxt, brs], writes=[btmp])
                fw.op(ACT, lambda h, k=k, tmp=tmp: h.activation(out=hT[:, k, :], in_=tmp[:], func=AF.Identity,
                                                               scale=Amod[:, k, cond:cond + 1], bias=modT[:, shift_lo + k, cond:cond + 1]),
                      reads=[btmp, b_A, b_modT_b], writes=[bh])
            return hT, bh

        def proj_fm(wt, bw, col0, m, rhs_fn, rhs_bufs, nk, out_parts=None):
            ps, bps = psA.get()
            for k in range(nk):
                fw.op(PE, lambda h, k=k: h.matmul(ps[0:m, :], lhsT=wt[:, k, col0:col0 + m], rhs=rhs_fn(k), start=(k == 0), stop=(k == nk - 1)),
                      reads=[bw] + rhs_bufs, writes=[bps], inc=(k == nk - 1))
            return ps, bps

        def norm_rope_gen(ps, bps, npart, ones_ap, gcolumn, rope, pm_idx, ropetab, dst_ap, dst_buf, keep_f32=None):
            xf, bxf = f32_r.get()
            evac(ps[0:npart, :], xf[0:npart, :], bps, bxf, eng=DVE)
            sq, bsq = bf_r.get()
            fw.op(ACT, lambda h: h.activation(out=sq[0:npart, :], in_=xf[0:npart, :], func=AF.Square), reads=[bxf], writes=[bsq])
            yield
            rs, brs = rms_rstd([(sq[0:npart, :], bsq)], ones_ap, npart)
            xn, bxn = f32_r.get()
            fw.op(DVE, lambda h: h.scalar_tensor_tensor(out=xn[0:npart, :], in0=xf[0:npart, :], scalar=gc[0:npart, gcolumn:gcolumn + 1], in1=rs[0:npart, :],
                                                        op0=ALU.mult, op1=ALU.mult), reads=[bxf, brs, b_gc], writes=[bxn])
            if keep_f32 is not None:
                keep_f32(xn, bxn)
            ob, bob = bf_r.get()
            if not rope:
                fw.op(ACT, lambda h: h.activation(out=ob[0:npart, :], in_=xn[0:npart, :], func=AF.Copy), reads=[bxn], writes=[bob])
            else:
                ctab, stab, btab = ropetab
                xb, bxb = bf_r.get()
                fw.op(ACT, lambda h: h.activation(out=xb[0:npart, :], in_=xn[0:npart, :], func=AF.Copy), reads=[bxn], writes=[bxb])
                t1, bt1 = f32_r.get()
                fw.op(DVE, lambda h: h.tensor_tensor(out=t1[0:npart, :], in0=xn[0:npart, :], in1=ctab, op=ALU.mult), reads=[bxn, btab], writes=[bt1])
                yield
                ps2, bps2 = psA.get()
                fw.op(PE, lambda h: h.matmul(ps2[0:npart, :], lhsT=pmat[0:npart, pm_idx, 0:npart], rhs=xb[0:npart, :], start=True, stop=True),
                      reads=[bxb, b_pmat], writes=[bps2])
                t2, bt2 = f32_r.get()
                fw.op(DVE, lambda h: h.tensor_tensor(out=t2[0:npart, :], in0=ps2[0:npart, :], in1=stab, op=ALU.mult), reads=[bps2, btab], writes=[bt2])
                fw.op(DVE, lambda h: h.tensor_tensor(out=ob[0:npart, :], in0=t1[0:npart, :], in1=t2[0:npart, :], op=ALU.add), reads=[bt1, bt2], writes=[bob])
            fw.dma(SP, dst_ap, ob[0:npart, :], reads=[bob], writes=[dst_buf])

        def norm_rope_store(*a, **k):
            for _ in norm_rope_gen(*a, **k):
                pass

        class Pipe:
            def __init__(self):
                self.gens = []

            def push(self, gen):
                next(gen, None)
                for og in list(self.gens):
                    try:
                        next(og)
                    except StopIteration:
                        self.gens.remove(og)
                self.gens.append(gen)

            def drain(self):
                while self.gens:
                    for og in list(self.gens):
                        try:
                            next(og)
                        except StopIteration:
                            self.gens.remove(og)

        def transpose_out(src_fn, src_bufs, nfeat_parts, ncols_total, dst_rows_fn, colslices):
            for g in range(4):
                ps, bps = psA.get()
                n = len(colslices)
                for i, (j, pj, c0) in enumerate(colslices):
                    fw.op(PE, lambda h, j=j, pj=pj, c0=c0, g=g: h.transpose(ps[:, c0:c0 + pj], src_fn(j)[:, g * 128:(g + 1) * 128], ident[0:pj, 0:pj]),
                          reads=src_bufs + [b_ident], writes=[bps], inc=(i == n - 1))
                o, bo = f32_r.get()
                evac(ps[:, 0:ncols_total], o[:, 0:ncols_total], bps, bo, eng=DVE)
                fw.dma(SP, dst_rows_fn(g), o[:, 0:ncols_total], reads=[bo], writes=[Buf()])

        b_hTs = [Buf() for _ in range(NT)]
        p3_pref = {}
        p1_pref = {}
        b_gq = Buf(); b_gk = Buf(); b_gkt = Buf(); b_gvt = Buf(); b_gG = Buf(); b_go = Buf()
        b_q = Buf(); b_k = Buf(); b_v = Buf(); b_dq = Buf(); b_dk = Buf(); b_dv = Buf()
        b_ya = Buf(); b_yb = Buf(); b_yc = Buf()
        out_bufs = []

        arena.reset()
        ropM_r = Ring(ov, "ropM", [96, 2, TT], F32, 1)
        ropD_r = Ring(ov, "ropD", [128, 2, TT], F32, 1)
        vaug_r = Ring(ov, "vaug", [128, 8, 4, 128], BF16, 1)
        tok_r = Ring(ov, "tok", [128, 4, 512], BF16, 2)
        tokf_r = Ring(ov, "tokf", [128, 4, 512], F32, 1)
        keep_r = Ring(ov, "keep", [128, 4, TT], F32, 1)
        qdn_r = Ring(ov, "qdn", [128, 3, TT], BF16, 1)
        ckv_r = Ring(ov, "ckv", [128, 2, TT], BF16, 1)
        ckvf_r = Ring(ov, "ckvf", [128, 2, TT], F32, 1)
        krb_r = Ring(ov, "krb", [32, TT], BF16, 1)
        krf_r = Ring(ov, "krf", [32, TT], F32, 1)
        aab_r = Ring(ov, "aab", [32, TT], BF16, 1)
        g4_r = Ring(ov, "g4", [64, 4, TT], BF16, 1)
        r4_r = Ring(ov, "r4", [128, 4, TT], BF16, 1)
        ctok_r = Ring(ov, "ctok", [128, 2, 512], F32, 2)

        def layer_setup(l):
            fw.dma(SP, bmod[:], b_modT[l], writes=[b_bmod])
            fw.dma(SP, gn[:], gnT[l], writes=[b_gn])
            fw.dma(SP, gc[:], gcol[l], writes=[b_gc])
            fw.dma(POOL, wa2[:], w_a2bd[l], writes=[b_wa2])
            fw.dma(POOL, ba[:], b_a[l], writes=[b_ba])
            fw.dma(SP, lamt[:], lam_qk[l], writes=[b_lamt])
            for cb in range(24):
                wm, bwm = wmod_r.get()
                fw.dma(SP, wm[:], w_mod[l, :, cb * 256:(cb + 1) * 256].rearrange("(k p) c -> p k c", p=128), writes=[bwm])
                ps, bps = psA.get()
                for j in range(2):
                    for k in range(8):
                        fw.op(PE, lambda h, j=j, k=k, wm=wm, ps=ps: h.matmul(ps[:, j * 2:j * 2 + 2], lhsT=wm[:, k, j * 128:(j + 1) * 128], rhs=scT[:, k, :],
                                                                          start=(k == 0), stop=(k == 7)),
                              reads=[bwm, b_scT], writes=[bps], inc=(j == 1 and k == 7))
                fw.op(DVE, lambda h, cb=cb, ps=ps: h.tensor_tensor(out=modT[:, cb * 2:(cb + 1) * 2, :], in0=ps[:, 0:4].rearrange("p (j c) -> p j c", c=2),
                                                                 in1=bmod[:, cb * 2:(cb + 1) * 2].unsqueeze(2).broadcast_to([128, 2, 2]), op=ALU.add),
                      reads=[bps, b_bmod], writes=[b_modT_b])
            fw.op(DVE, lambda h: h.scalar_tensor_tensor(out=A1[:], in0=modT[:, 8:16, :], scalar=1.0, in1=gn[:, 0:8].unsqueeze(2).broadcast_to([128, 8, 2]),
                                                        op0=ALU.add, op1=ALU.mult), reads=[b_modT_b, b_gn], writes=[b_A])
            fw.op(DVE, lambda h: h.scalar_tensor_tensor(out=A2[:], in0=modT[:, 32:40, :], scalar=1.0, in1=gn[:, 8:16].unsqueeze(2).broadcast_to([128, 8, 2]),
                                                        op0=ALU.add, op1=ALU.mult), reads=[b_modT_b, b_gn], writes=[b_A])
            lam_init = 0.8 - 0.6 * math.exp(-0.3 * l)
            fw.op(DVE, lambda h: h.tensor_tensor(out=lamt[:, 0:64], in0=lamt[:, 0:64], in1=lamt[:, 64:128], op=ALU.mult), reads=[b_lamt], writes=[b_lamt])
            fw.op(DVE, lambda h: h.tensor_tensor(out=lamt[:, 128:192], in0=lamt[:, 128:192], in1=lamt[:, 192:256], op=ALU.mult), reads=[b_lamt], writes=[b_lamt])
            fw.op(DVE, lambda h: h.reduce_sum(out=lam1[:, 0:1], in_=lamt[:, 0:64], axis=mybir.AxisListType.X), reads=[b_lamt], writes=[b_lam1])
            fw.op(DVE, lambda h: h.reduce_sum(out=lam1[:, 1:2], in_=lamt[:, 128:192], axis=mybir.AxisListType.X), reads=[b_lamt], writes=[b_lam1])
            fw.op(ACT, lambda h: h.activation(out=lam1[:, 2:4], in_=lam1[:, 0:2], func=AF.Exp), reads=[b_lam1], writes=[b_lam1])
            fw.op(DVE, lambda h: h.scalar_tensor_tensor(out=lam1[:, 4:5], in0=lam1[:, 3:4], scalar=-lam_init, in1=lam1[:, 2:3], op0=ALU.add, op1=ALU.subtract),
                  reads=[b_lam1], writes=[b_lam1])
            ps, bps = psA.get()
            fw.op(PE, lambda h: h.matmul(ps[:, 0:1], lhsT=ones_rowf[:, :], rhs=lam1[:, 4:5], start=True, stop=True), reads=[b_onesrowf, b_lam1], writes=[bps])
            fw.op(DVE, lambda h: h.tensor_copy(out=lamc[:, 0:1], in_=ps[:, 0:1]), reads=[bps], writes=[b_lamc])
            return lam_init

        def key_base(ti):
            return 0 if ti == 0 else NP_TOK + PAST + (ti - 1) * TT

        import os as _os
        KP1S = int(_os.environ.get("KP1S", "99"))
        KP1 = int(_os.environ.get("KP1", str(NT)))

        def phase1(l, ti):
            if ti >= KP1:
                return
            lat = ti > 0
            cond = 1 if lat else 0
            t0 = ti * TT
            kb = key_base(ti)
            def p1_prep(tj):
                xt_, bxt_ = load_xT(tj)
                hT_, bh_ = norm_mod(xt_, bxt_, A1, 0, 1 if tj > 0 else 0)
                fw.dma(SP, hT_s[:, :, tj * TT:(tj + 1) * TT], hT_[:], reads=[bh_], writes=[b_hTs[tj]])
                return hT_, bh_

            hT, bh = p1_pref.pop((l, ti), None) or p1_prep(ti)
            rhs_h = lambda k: hT[:, k, :]
            if lat:
                rM, brM = ropM_r.get()
                fw.dma(SP, rM[:], ropeM[:, :, (ti - 1) * TT:ti * TT].rearrange("c p t -> p c t"), writes=[brM])
                rD, brD = ropD_r.get()
                fw.dma(SP, rD[:], ropeD[:, :, (ti - 1) * TT:ti * TT].rearrange("c p t -> p c t"), writes=[brD])
                tabM = (rM[:, 0, :], rM[:, 1, :], brM)
                tabD = (rD[:, 0, :], rD[:, 1, :], brD)
            else:
                tabM = tabD = None

            if KP1S < 1:
                return
            wt, bw = load_w8(w_in[l, :, O_AQ:O_AQ + 512], 512)
            for which, dst, bdst in ((0, gq_s, b_gq), (1, gk_s, b_gk)):
                g4, bg4 = g4_r.get()
                for hh in range(4):
                    ps, bps = proj_fm(wt, bw, which * 256 + hh * 64, 64, rhs_h, [bh], 8)
                    evac(ps[0:64, :], g4[:, hh, :], bps, bg4, eng=(ACT if hh % 2 else DVE))
                fw.dma(SP, dst[:, :, t0:t0 + TT], g4[:], reads=[bg4], writes=[bdst])
            if KP1S < 2:
                return
            tk, btk = tok_r.get()
            for g in range(4):
                ps, bps = psA.get()
                for k in range(8):
                    fw.op(PE, lambda h, k=k, g=g, ps=ps: h.matmul(ps[:, 0:256], lhsT=hT[:, k, g * 128:(g + 1) * 128], rhs=wt[:, k, 256:512], start=(k == 0), stop=(k == 7)),
                          reads=[bw, bh], writes=[bps], inc=(k == 7))
                evac(ps[:, 0:256], tk[:, g, 0:256], bps, btk, eng=(ACT if g % 2 else DVE))
            fw.dma(SP, gkt_s[t0:t0 + TT, :].rearrange("(g p) c -> p g c", p=128), tk[:, :, 0:256], reads=[btk], writes=[b_gkt])
            wt, bw = load_w8(w_in[l, :, O_AV:O_AV + 512], 512)
            tk, btk = tok_r.get()
            for g in range(4):
                ps, bps = psA.get()
                for k in range(8):
                    fw.op(PE, lambda h, k=k, g=g, ps=ps, wt=wt: h.matmul(ps[:, :], lhsT=hT[:, k, g * 128:(g + 1) * 128], rhs=wt[:, k, 0:512], start=(k == 0), stop=(k == 7)),
                          reads=[bw, bh], writes=[bps], inc=(k == 7))
                evac(ps[:, :], tk[:, g, :], bps, btk, eng=(ACT if g % 2 else DVE))
            fw.dma(SP, gvt_s[t0:t0 + TT, :].rearrange("(g p) c -> p g c", p=128), tk[:], reads=[btk], writes=[b_gvt])
            if KP1S < 3:
                return
            if ti + 1 < min(NT, KP1):
                p1_pref[(l, ti + 1)] = p1_prep(ti + 1)
            wt, bw = load_w8(w_in[l, :, O_AA:O_AA + 416], 416)
            ps, bps = proj_fm(wt, bw, 0, 32, rhs_h, [bh], 8)
            aab, baab = aab_r.get()
            evac(ps[0:32, :], aab[:, :], bps, baab, eng=DVE)
            gt, bgt = tokf_r.get()
            for g in range(4):
                ps, bps = psA.get()
                fw.op(PE, lambda h, g=g, ps=ps: h.matmul(ps[:, :], lhsT=aab[:, g * 128:(g + 1) * 128], rhs=wa2[:, :], start=True, stop=False),
                      reads=[baab, b_wa2], writes=[bps], inc=False)
                fw.op(PE, lambda h, ps=ps: h.matmul(ps[:, :], lhsT=ones_row[:, :], rhs=ba[:, :], start=False, stop=True), reads=[b_onesrow, b_ba], writes=[bps])
                e1, be1 = f32_r.get()
                fw.op(ACT, lambda h, ps=ps, e1=e1: h.activation(out=e1[:], in_=ps[:], func=AF.Exp, scale=-1.0), reads=[bps], writes=[be1])
                fw.op(ACT, lambda h, g=g, e1=e1: h.activation(out=gt[:, g, :], in_=e1[:], func=AF.Ln, bias=1.0), reads=[be1], writes=[bgt])
            fw.dma(SP, gG_s[t0:t0 + TT, :].rearrange("(g p) c -> p g c", p=128), gt[:], reads=[bgt], writes=[b_gG])

            if KP1S < 4:
                return
            wt_r, bw_r = load_w8(w_in[l, :, O_AR:O_AR + 512], 512)
            rr, brr = r4_r.get()
            for hh in range(4):
                ps_r, bps_r = proj_fm(wt_r, bw_r, hh * 128, 128, rhs_h, [bh], 8)
                fw.op(ACT, lambda h, hh=hh, ps_r=ps_r: h.activation(out=rr[:, hh, :], in_=ps_r[:], func=AF.Silu), reads=[bps_r], writes=[brr])
            fw.dma(SP, rs_s[:, :, t0:t0 + TT], rr[:], reads=[brr], writes=[b_rs])

            if KP1S < 5:
                return
            qdn, bqdn = qdn_r.get()
            qf = []
            for j in range(3):
                ps, bps = proj_fm(wt, bw, 32 + j * 128, 128, rhs_h, [bh], 8)
                xf, bxf = f32_r.get()
                evac(ps[:], xf[:], bps, bxf, eng=DVE)
                sq, bsq = bf_r.get()
                fw.op(ACT, lambda h, sq=sq, xf=xf: h.activation(out=sq[:], in_=xf[:], func=AF.Square), reads=[bxf], writes=[bsq])
                qf.append((xf, bxf, sq, bsq))
            rs, brs = rms_rstd([(q[2][:], q[3]) for q in qf], cmat[:, C_384, :], 128)
            for j in range(3):
                xf, bxf = qf[j][0], qf[j][1]
                fw.op(DVE, lambda h, j=j, xf=xf: h.scalar_tensor_tensor(out=qdn[:, j, :], in0=xf[:], scalar=gc[:, j:j + 1], in1=rs[:], op0=ALU.mult, op1=ALU.mult),
                      reads=[bxf, brs, b_gc], writes=[bqdn])
            pipe = Pipe()
            for half in range(2):
                wq, bwq = w8.get()
                fw.dma(POOL, wq[:, 0:3, 0:384], w_uq[l, :, half * 4:(half + 1) * 4, :].rearrange("(k p) h c -> p k (h c)", p=128), writes=[bwq])
                for hh in range(4):
                    ps, bps = proj_fm(wq, bwq, hh * 96, 96, lambda k: qdn[:, k, :], [bqdn], 3)
                    head = half * 4 + hh
                    pipe.push(norm_rope_gen(ps, bps, 96, cmat[0:96, C_96, 0:96], 5, lat, P_96, tabM, q_s[head, :, t0:t0 + TT], b_q))
            pipe.drain()

            if KP1S < 6:
                return
            wt, bw = load_w8(w_in[l, :, O_KVD:O_KVD + 288], 288)
            ckv, bckv = ckv_r.get()
            ckvf, bckvf = ckvf_r.get()
            kf = []
            for j in range(2):
                ps, bps = proj_fm(wt, bw, j * 128, 128, rhs_h, [bh], 8)
                xf, bxf = f32_r.get()
                evac(ps[:], xf[:], bps, bxf, eng=DVE)
                sq, bsq = bf_r.get()
                fw.op(ACT, lambda h, sq=sq, xf=xf: h.activation(out=sq[:], in_=xf[:], func=AF.Square), reads=[bxf], writes=[bsq])
                kf.append((xf, bxf, sq, bsq))
            rs, brs = rms_rstd([(q[2][:], q[3]) for q in kf], cmat[:, C_256, :], 128)
            for j in range(2):
                xf, bxf = kf[j][0], kf[j][1]
                fw.op(DVE, lambda h, j=j, xf=xf: h.scalar_tensor_tensor(out=ckvf[:, j, :], in0=xf[:], scalar=gc[:, 3 + j:4 + j], in1=rs[:], op0=ALU.mult, op1=ALU.mult),
                      reads=[bxf, brs, b_gc], writes=[bckvf])
            fw.op(ACT, lambda h: h.activation(out=ckv[:], in_=ckvf[:], func=AF.Copy), reads=[bckvf], writes=[bckv])
            ps, bps = proj_fm(wt, bw, 256, 32, rhs_h, [bh], 8)
            krf, bkrf = krf_r.get()
            krb, bkrb = krb_r.get()
            evac(ps[0:32, :], krf[:, :], bps, bkrf, eng=DVE)
            fw.op(ACT, lambda h: h.activation(out=krb[:], in_=krf[:], func=AF.Copy), reads=[bkrf], writes=[bkrb])
            KSUB = _os.environ.get("KSUB", "abcd")
            if not lat:
                if "a" in KSUB:
                    transpose_out(lambda j: ckvf[:, j, :], [bckvf], 128, 256, lambda g: o_ckv[l, g * 128:(g + 1) * 128, :], [(0, 128, 0), (1, 128, 128)])
                if "b" in KSUB:
                    transpose_out(lambda j: krf[:, :], [bkrf], 32, 32, lambda g: o_kr[l, g * 128:(g + 1) * 128, :], [(0, 32, 0)])
            if "c" in KSUB:
                mla_keys(l, lambda k: ckv[:, k, :], [bckv], krb, bkrb, lat, tabM, kb)
            if "d" in KSUB:
                mla_values(l, lambda k, g: ckv[:, k, g * 128:(g + 1) * 128], [bckv], kb)

            if KP1S < 7:
                return
            for which, off, dst, bdst, gcolumn in ((0, O_DQ, dq_s, b_dq, 7), (1, O_DK, dk_s, b_dk, 8)):
                wt, bw = load_w8(w_in[l, :, off:off + 512], 512)
                dpipe = Pipe()
                keep = None
                if which == 1 and not lat:
                    kp, bkp = keep_r.get()
                for hh in range(4):
                    ps, bps = proj_fm(wt, bw, hh * 128, 128, rhs_h, [bh], 8)
                    kf32 = None
                    if which == 1 and not lat:
                        kf32 = lambda xn, bxn, hh=hh: fw.op(ACT, lambda h: h.activation(out=kp[:, hh, :], in_=xn[:], func=AF.Copy), reads=[bxn], writes=[bkp])
                    col0 = t0 if which == 0 else kb
                    dpipe.push(norm_rope_gen(ps, bps, 128, cmat[:, C_BLK64, :], gcolumn, lat, P_128, tabD, dst[hh, :, col0:col0 + TT], bdst, keep_f32=kf32))
                dpipe.drain()
                if which == 1 and not lat:
                    transpose_out(lambda j: kp[:, j, :], [bkp], 128, 512, lambda g: o_dk[l, g * 128:(g + 1) * 128, :], [(j, 128, j * 128) for j in range(4)])
            wt, bw = load_w8(w_in[l, :, O_DV:O_DV + 512], 512)
            tk, btk = tok_r.get()
            if not lat:
                tf, btf = tokf_r.get()
            for g in range(4):
                ps, bps = psA.get()
                for k in range(8):
                    fw.op(PE, lambda h, k=k, g=g, ps=ps, wt=wt: h.matmul(ps[:, :], lhsT=hT[:, k, g * 128:(g + 1) * 128], rhs=wt[:, k, 0:512], start=(k == 0), stop=(k == 7)),
                          reads=[bw, bh], writes=[bps], inc=(k == 7))
                evac(ps[:, :], tk[:, g, :], bps, btk, eng=ACT)
                if not lat:
                    evac(ps[:, :], tf[:, g, :], bps, btf, eng=DVE)
            for hh in range(4):
                fw.dma(SP, dv_s[hh, :, kb // 128:kb // 128 + 4, :], tk[:, :, hh * 128:(hh + 1) * 128], reads=[btk], writes=[b_dv])
            if not lat:
                ob = Buf(); out_bufs.append(ob)
                fw.dma(SP, o_dv[l, :, :].rearrange("(g p) c -> p g c", p=128), tf[:], reads=[btf], writes=[ob])

        def mla_keys(l, ckv_fn, ckv_bufs, krb, bkrb, rope, tabM, kb, ntok=TT):
            pipe = Pipe()
            for half in range(2):
                wk, bwk = w8.get()
                fw.dma(POOL, wk[:, 0:2, 0:384], w_ukp[l, :, half * 4:(half + 1) * 4, :].rearrange("(k p) h c -> p k (h c)", p=128), writes=[bwk])
                for hh in range(4):
                    ps, bps = psA.get()
                    for k in range(2):
                        fw.op(PE, lambda h, k=k, hh=hh, ps=ps, wk=wk: h.matmul(ps[0:96, 0:ntok], lhsT=wk[:, k, hh * 96:(hh + 1) * 96], rhs=ckv_fn(k), start=(k == 0), stop=False),
                              reads=[bwk] + ckv_bufs, writes=[bps], inc=False)
                    fw.op(PE, lambda h, ps=ps: h.matmul(ps[0:96, 0:ntok], lhsT=cmat[0:32, C_SEL32, 0:96], rhs=krb[:, 0:ntok], start=False, stop=True),
                          reads=[b_cmat, bkrb], writes=[bps])
                    head = half * 4 + hh
                    if ntok == TT:
                        pipe.push(norm_rope_gen(ps, bps, 96, cmat[0:96, C_96, 0:96], 6, rope, P_96, tabM, k_s[head, :, kb:kb + TT], b_k))
                    else:
                        norm_store_small(ps, bps, ntok, head, kb)
            pipe.drain()

        def norm_store_small(ps, bps, ntok, head, kb):
            xf, bxf = f32_r.get()
            evac(ps[0:96, 0:ntok], xf[0:96, 0:ntok], bps, bxf, eng=DVE)
            sq, bsq = bf_r.get()
            fw.op(ACT, lambda h: h.activation(out=sq[0:96, 0:ntok], in_=xf[0:96, 0:ntok], func=AF.Square), reads=[bxf], writes=[bsq])
            ps2, bps2 = psA.get()
            fw.op(PE, lambda h: h.matmul(ps2[0:96, 0:ntok], lhsT=cmat[0:96, C_96, 0:96], rhs=sq[0:96, 0:ntok], start=True, stop=True), reads=[bsq, b_cmat], writes=[bps2])
            rs, brs = rs_r.get()
            rsqrt_eps(ps2[0:96, 0:ntok], rs[0:96, 0:ntok], bps2, brs)
            ob, bob = bf_r.get()
            fw.op(DVE, lambda h: h.scalar_tensor_tensor(out=ob[0:96, 0:ntok], in0=xf[0:96, 0:ntok], scalar=gc[0:96, 6:7], in1=rs[0:96, 0:ntok], op0=ALU.mult, op1=ALU.mult),
                  reads=[bxf, brs, b_gc], writes=[bob])
            fw.dma(SP, k_s[head, :, kb:kb + ntok], ob[0:96, 0:ntok], reads=[bob], writes=[b_k])

        vaug_init = [False, False]

        def mla_values(l, ckvT_fn, ckv_bufs, kb, ngrp=4):
            wv, bwv = w8.get()
            fw.dma(POOL, wv[:, 0:2, 0:512], w_uv[l].rearrange("(k p) c -> p k c", p=128), writes=[bwv])
            idx = vaug_r.i
            va, bva = vaug_r.get()
            if not vaug_init[idx]:
                vaug_init[idx] = True
                fw.op(DVE, lambda h: h.memset(va[:], 1.0), writes=[bva])
            for g in range(ngrp):
                ps, bps = psA.get()
                for k in range(2):
                    fw.op(PE, lambda h, k=k, g=g, ps=ps: h.matmul(ps[:, :], lhsT=ckvT_fn(k, g), rhs=wv[:, k, 0:512], start=(k == 0), stop=(k == 1)),
                          reads=[bwv] + ckv_bufs, writes=[bps], inc=(k == 1))
                fw.op(ACT if g % 2 else DVE, (lambda h, g=g, ps=ps: h.activation(out=va[:, :, g, 0:64], in_=ps[:, :].rearrange("p (h c) -> p h c", c=64), func=AF.Copy)) if g % 2
                      else (lambda h, g=g, ps=ps: h.tensor_copy(out=va[:, :, g, 0:64], in_=ps[:, :].rearrange("p (h c) -> p h c", c=64))), reads=[bps], writes=[bva])
            c0 = kb // 128
            fw.dma(SP, v_s[:, :, c0:c0 + ngrp, :].rearrange("h p g e -> p h (g e)"), va[:, :, 0:ngrp, :].rearrange("p h g e -> p h (g e)"), reads=[bva], writes=[b_v])


        def prep_cache(l):
            kb = NP_TOK
            ct, bct = ctok_r.get()
            fw.dma(SP, ct[:, :, 0:256], c_ckv[l].rearrange("(g p) c -> p g c", p=128), writes=[bct])
            ckv, bckv = ckv_r.get()
            for j in range(2):
                ps, bps = psA.get()
                for g in range(2):
                    fw.op(PE, lambda h, j=j, g=g, ps=ps: h.transpose(ps[:, g * 128:(g + 1) * 128], ct[:, g, j * 128:(j + 1) * 128], ident[:]),
                          reads=[bct, b_ident], writes=[bps], inc=(g == 1))
                evac(ps[:, 0:256], ckv[:, j, 0:256], bps, bckv, eng=DVE)
            ct2, bct2 = ctok_r.get()
            fw.dma(SP, ct2[:, :, 0:32], c_kr[l].rearrange("(g p) c -> p g c", p=128), writes=[bct2])
            krb, bkrb = krb_r.get()
            ps, bps = psA.get()
            for g in range(2):
                fw.op(PE, lambda h, g=g, ps=ps: h.transpose(ps[0:32, g * 128:(g + 1) * 128], ct2[:, g, 0:32], ident[:]), reads=[bct2, b_ident], writes=[bps], inc=(g == 1))
            evac(ps[0:32, 0:256], krb[:, 0:256], bps, bkrb, eng=DVE)
            mla_keys(l, lambda k: ckv[:, k, 0:256], [bckv], krb, bkrb, False, None, kb, ntok=256)
            mla_values(l, lambda k, g: ckv[:, k, g * 128:(g + 1) * 128], [bckv], kb, ngrp=2)
            ct, bct = ctok_r.get()
            fw.dma(SP, ct[:], c_dk[l].rearrange("(g p) c -> p g c", p=128), writes=[bct])
            for hh in range(4):
                ps, bps = psA.get()
                for g in range(2):
                    fw.op(PE, lambda h, hh=hh, g=g, ps=ps: h.transpose(ps[:, g * 128:(g + 1) * 128], ct[:, g, hh * 128:(hh + 1) * 128], ident[:]),
                          reads=[bct, b_ident], writes=[bps], inc=(g == 1))
                ob, bob = bf_r.get()
                evac(ps[:, 0:256], ob[:, 0:256], bps, bob, eng=DVE)
                fw.dma(SP, dk_s[hh, :, kb:kb + 256], ob[:, 0:256], reads=[bob], writes=[b_dk])
            ct3, bct3 = ctok_r.get()
            fw.dma(SP, ct3[:], c_dv[l].rearrange("(g p) c -> p g c", p=128), writes=[bct3])
            tkc, btkc = tok_r.get()
            fw.op(DVE, lambda h: h.tensor_copy(out=tkc[:, 0:2, :], in_=ct3[:]), reads=[bct3], writes=[btkc])
            for hh in range(4):
                fw.dma(SP, dv_s[hh, :, kb // 128:kb // 128 + 2, :], tkc[:, 0:2, hh * 128:(hh + 1) * 128], reads=[btkc], writes=[b_dv])

        arena.reset()
        gl_q = Ring(ov, "glq", [64, 4, TT], BF16, 1)
        gl_k = Ring(ov, "glk", [64, 4, TT], BF16, 1)
        gl_kt = Ring(ov, "glkt", [64, 8, 256], BF16, 1)
        gl_vt = Ring(ov, "glvt", [64, 8, 512], BF16, 1)
        gl_G = Ring(ov, "glG", [64, 8, 256], F32, 1)
        gl_o = Ring(ov, "glo", [128, 4, TT], F32, 1)
        gl_of = Ring(ov, "glof", [128, 4, TT], F32, 1)
        gl_r = Ring(ov, "glr", [128, 4, TT], BF16, 2)
        S_f = ov("S_f", [64, 4, 128], F32); S_b = ov("S_b", [64, 4, 128], BF16); b_S = Buf(); b_Sb = Buf()
        sm_r = Ring(ov, "sm", [64, 256], F32, 6)
        smb_r = Ring(ov, "smb", [64, 256], BF16, 8)
        dd_r = Ring(ov, "dd", [64, 4], F32, 3)

        def gla_dir(l, d, seqs):
            incl = tri[:, d, :]
            strict = tri[:, 2 + d, :]
            for (tok0, ntok, seq_idx, lat) in seqs:
                if lat:
                    fw.dma(SP, S_f[:], st_gla[l, d], reads=[b_Sb], writes=[b_S])
                else:
                    fw.op(DVE, lambda h: h.memset(S_f[:], 0.0), reads=[b_Sb], writes=[b_S])
                fw.op(ACT, lambda h: h.activation(out=S_b[:], in_=S_f[:], func=AF.Copy), reads=[b_S], writes=[b_Sb])
                tiles = list(range(tok0, tok0 + ntok, TT)) if ntok >= TT else [tok0]
                if d == 1:
                    tiles = tiles[::-1]
                for tt0 in tiles:
                    base_tile = (tt0 // TT) * TT
                    cq, bcq = gl_q.get(); ck, bck = gl_k.get(); ckt, bckt = gl_kt.get(); cvt, bcvt = gl_vt.get(); cG, bcG = gl_G.get()
                    fw.dma(SP, cq[:], gq_s[:, :, base_tile:base_tile + TT], reads=[b_gq], writes=[bcq])
                    fw.dma(SP, ck[:], gk_s[:, :, base_tile:base_tile + TT], reads=[b_gk], writes=[bck])
                    fw.dma(SP, ckt[:], gkt_s[base_tile:base_tile + TT, :].rearrange("(c p) f -> p c f", p=64), reads=[b_gkt], writes=[bckt])
                    fw.dma(SP, cvt[:], gvt_s[base_tile:base_tile + TT, :].rearrange("(c p) f -> p c f", p=64), reads=[b_gvt], writes=[bcvt])
                    fw.dma(SP, cG[:], gG_s[base_tile:base_tile + TT, d * 256:(d + 1) * 256].rearrange("(c p) f -> p c f", p=64), reads=[b_gG], writes=[bcG])
                    ot, bot = gl_o.get()
                    c_lo = (tt0 - base_tile) // 64
                    nch = min(ntok, TT) // 64
                    chunks = list(range(c_lo, c_lo + nch))
                    if d == 1:
                        chunks = chunks[::-1]
                    for c in chunks:
                        Gc = cG[:, c, :]
                        psb_, bpsb = psA.get()
                        for hh in range(4):
                            fw.op(PE, lambda h, hh=hh, Gc=Gc, p=psb_: h.matmul(p[0:64, hh * 64:(hh + 1) * 64], lhsT=Gc[:, hh * 64:(hh + 1) * 64], rhs=incl, start=True, stop=True),
                                  reads=[bcG, b_tri], writes=[bpsb], inc=(hh == 3))
                        ep, bep = sm_r.get(); en, ben = sm_r.get()
                        fw.op(ACT, lambda h, p=psb_, ep=ep: h.activation(out=ep[:, :], in_=p[0:64, 0:256], func=AF.Exp, scale=-1.0 / 16), reads=[bpsb], writes=[bep])
                        fw.op(ACT, lambda h, p=psb_, en=en: h.activation(out=en[:, :], in_=p[0:64, 0:256], func=AF.Exp, scale=1.0 / 16), reads=[bpsb], writes=[ben])
                        qt, bqt = smb_r.get(); kt_, bkt = smb_r.get()
                        fw.op(DVE, lambda h, c=c, qt=qt, ep=ep: h.scalar_tensor_tensor(out=qt[:, :].rearrange("p (h t) -> p h t", t=64), in0=cq[:, :, c * 64:(c + 1) * 64], scalar=0.125,
                                                                                      in1=ep[:, :].rearrange("p (h t) -> p h t", t=64), op0=ALU.mult, op1=ALU.mult),
                              reads=[bcq, bep], writes=[bqt])
                        fw.op(DVE, lambda h, c=c, kt_=kt_, en=en: h.tensor_tensor(out=kt_[:, :].rearrange("p (h t) -> p h t", t=64), in0=ck[:, :, c * 64:(c + 1) * 64],
                                                                                  in1=en[:, :].rearrange("p (h t) -> p h t", t=64), op=ALU.mult),
                              reads=[bck, ben], writes=[bkt])
                        ps2, bps2 = psA.get()
                        fw.op(PE, lambda h, Gc=Gc, p=ps2: h.matmul(p[0:64, 0:256], lhsT=strict, rhs=Gc, start=True, stop=True), reads=[bcG, b_tri], writes=[bps2])
                        e2, be2 = sm_r.get()
                        fw.op(ACT, lambda h, p=ps2, e2=e2: h.activation(out=e2[:, :], in_=p[0:64, 0:256], func=AF.Exp, scale=-1.0 / 16), reads=[bps2], writes=[be2])
                        kh, bkh = smb_r.get()
                        fw.op(DVE, lambda h, c=c, kh=kh, e2=e2: h.tensor_tensor(out=kh[:, :], in0=ckt[:, c, :], in1=e2[:, :], op=ALU.mult), reads=[bckt, be2], writes=[bkh])
                        ps3, bps3 = psA.get()
                        for hh in range(4):
                            fw.op(PE, lambda h, hh=hh, Gc=Gc, p=ps3: h.matmul(p[0:64, hh:hh + 1], lhsT=Gc[:, hh * 64:(hh + 1) * 64], rhs=ones_col[:, :], start=True, stop=True),
                                  reads=[bcG, b_onescol], writes=[bps3], inc=(hh == 3))
                        dd, bdd = dd_r.get()
                        fw.op(ACT, lambda h, p=ps3, dd=dd: h.activation(out=dd[:, :], in_=p[0:64, 0:4], func=AF.Exp, scale=-1.0 / 16), reads=[bps3], writes=[bdd])
                        ps4, bps4 = psA.get()
                        for hh in range(4):
                            fw.op(PE, lambda h, hh=hh, p=ps4, kt_=kt_, qt=qt: h.matmul(p[0:64, hh * 64:(hh + 1) * 64], lhsT=kt_[:, hh * 64:(hh + 1) * 64], rhs=qt[:, hh * 64:(hh + 1) * 64],
                                                                                     start=True, stop=True), reads=[bkt, bqt], writes=[bps4], inc=(hh == 3))
                        am, bam = smb_r.get()
                        fw.op(DVE, lambda h, p=ps4, am=am: h.tensor_tensor(out=am[:, :].rearrange("p (h t) -> p h t", t=64), in0=p[0:64, 0:256].rearrange("p (h t) -> p h t", t=64),
                                                                         in1=incl.unsqueeze(1).broadcast_to([64, 4, 64]), op=ALU.mult), reads=[bps4, b_tri], writes=[bam])
                        ps5, bps5 = psA.get()
                        for hh in range(4):
                            fw.op(PE, lambda h, hh=hh, c=c, p=ps5, am=am: h.matmul(p[:, hh * 64:(hh + 1) * 64], lhsT=cvt[:, c, hh * 128:(hh + 1) * 128], rhs=am[:, hh * 64:(hh + 1) * 64],
                                                                                 start=True, stop=False), reads=[bcvt, bam], writes=[bps5], inc=False)
                            fw.op(PE, lambda h, hh=hh, p=ps5, qt=qt: h.matmul(p[:, hh * 64:(hh + 1) * 64], lhsT=S_b[:, hh, :], rhs=qt[:, hh * 64:(hh + 1) * 64], start=False, stop=True),
                                  reads=[b_Sb, bqt], writes=[bps5], inc=(hh == 3))
                        fw.op(ACT, lambda h, c=c, p=ps5, ot=ot: h.activation(out=ot[:, :, c * 64:(c + 1) * 64], in_=p[:, 0:256].rearrange("p (h t) -> p h t", t=64), func=AF.Copy),
                              reads=[bps5], writes=[bot])
                        ps6, bps6 = psB.get()
                        for hh in range(4):
                            fw.op(PE, lambda h, hh=hh, c=c, p=ps6, kh=kh: h.matmul(p[0:64, hh * 128:(hh + 1) * 128], lhsT=kh[:, hh * 64:(hh + 1) * 64], rhs=cvt[:, c, hh * 128:(hh + 1) * 128],
                                                                                 start=True, stop=True), reads=[bkh, bcvt], writes=[bps6], inc=(hh == 3))
                        fw.op(DVE, lambda h, dd=dd: h.tensor_tensor(out=S_f[:], in0=S_f[:], in1=dd[:, :].unsqueeze(2).broadcast_to([64, 4, 128]), op=ALU.mult), reads=[bdd], writes=[b_S])
                        fw.op(DVE, lambda h, p=ps6: h.tensor_tensor(out=S_f[:], in0=S_f[:], in1=p[0:64, :].rearrange("p (h v) -> p h v", v=128), op=ALU.add), reads=[bps6], writes=[b_S])
                        fw.op(ACT, lambda h: h.activation(out=S_b[:], in_=S_f[:], func=AF.Copy), reads=[b_S], writes=[b_Sb])
                    cols = slice(tt0 - base_tile, tt0 - base_tile + min(ntok, TT))
                    ncol = min(ntok, TT)
                    if d == 0:
                        fw.dma(SP, go_s[:, :, tt0:tt0 + ncol], ot[:, :, cols], reads=[bot], writes=[b_go])
                    else:
                        of, bof = gl_of.get()
                        fw.dma(SP, of[:, :, 0:ncol], go_s[:, :, tt0:tt0 + ncol], reads=[b_go], writes=[bof])
                        rr, brr = gl_r.get()
                        fw.dma(SP, rr[:, :, 0:ncol], rs_s[:, :, tt0:tt0 + ncol], reads=[b_rs], writes=[brr])
                        fw.op(DVE, lambda h, ot=ot, of=of: h.tensor_tensor(out=of[:, :, 0:ncol], in0=of[:, :, 0:ncol], in1=ot[:, :, cols], op=ALU.add), reads=[bot], writes=[bof])
                        sq, bsq = sq_r.get()
                        fw.op(ACT, lambda h, sq=sq, of=of: h.activation(out=sq[:, 0:4, 0:ncol], in_=of[:, :, 0:ncol], func=AF.Square), reads=[bof], writes=[bsq])
                        ya, bya = gl_r.get()
                        for hh in range(4):
                            ps, bps = psA.get()
                            fw.op(PE, lambda h, hh=hh, ps=ps, sq=sq: h.matmul(ps[:, 0:ncol], lhsT=cmat[:, C_128, :], rhs=sq[:, hh, 0:ncol], start=True, stop=True), reads=[bsq, b_cmat], writes=[bps])
                            rs, brs = rs_r.get()
                            rsqrt_eps(ps[:, 0:ncol], rs[:, 0:ncol], bps, brs)
                            tmp, btmp = f32_r.get()
                            fw.op(DVE, lambda h, hh=hh, tmp=tmp, of=of, rs=rs: h.scalar_tensor_tensor(out=tmp[:, 0:ncol], in0=of[:, hh, 0:ncol], scalar=gc[:, 9:10], in1=rs[:, 0:ncol],
                                                                                                     op0=ALU.mult, op1=ALU.mult), reads=[bof, brs, b_gc], writes=[btmp])
                            fw.op(DVE, lambda h, hh=hh, tmp=tmp, ya=ya, rr=rr: h.tensor_tensor(out=ya[:, hh, 0:ncol], in0=tmp[:, 0:ncol], in1=rr[:, hh, 0:ncol], op=ALU.mult),
                                  reads=[btmp, brr], writes=[bya])
                        fw.dma(SP, ya_s[:, :, tt0:tt0 + ncol].rearrange("h p t -> p h t"), ya[:, :, 0:ncol], reads=[bya], writes=[b_ya])
                if seq_idx is not None:
                    ob = Buf(); out_bufs.append(ob)
                    fw.dma(SP, o_gla[l, seq_idx, d], S_f[:], reads=[b_S], writes=[ob])

        rs_s = dram_tmp("rs_s", [128, 4, T], BF16)
        b_rs = Buf()

        def phase1b(l, ti):
            cond = 1 if ti > 0 else 0
            xt, bxt = load_xT(ti)
            hT, bh = norm_mod(xt, bxt, A1, 0, cond)
            wt, bw = load_w8(w_in[l, :, O_AR:O_AR + 512], 512)
            rr, brr = gl_r.get()
            for hh in range(4):
                ps, bps = proj_fm(wt, bw, hh * 128, 128, lambda k: hT[:, k, :], [bh], 8)
                fw.op(ACT, lambda h, hh=hh, ps=ps: h.activation(out=rr[:, hh, :], in_=ps[:], func=AF.Silu), reads=[bps], writes=[brr])
            fw.dma(SP, rs_s[:, :, ti * TT:(ti + 1) * TT], rr[:], reads=[brr], writes=[b_rs])

        arena.reset()
        at_k = Ring(ov, "atk", [128, TKS], BF16, 2)
        at_v = Ring(ov, "atv", [128, TKS // 128, 128], BF16, 2)
        at_q = Ring(ov, "atq", [128, TT], BF16, 2)
        at_p = Ring(ov, "atp", [128, TT], BF16, 6)
        at_o = Ring(ov, "ato", [128, TT], BF16, 2)
        at_qm = [Ring(ov, "atqm0_", [128, TT], BF16, 2), Ring(ov, "atqm1_", [128, TT], BF16, 2)]
        at_l = Ring(ov, "atl", [128, TT], F32, 4)

        def mla_attn(l, groups):
            sc = 96 ** -0.5
            for i in range(2):
                fw.op(DVE, lambda h: h.memset(at_k.t[i][:], 0.0), writes=[at_k.b[i]])
                fw.op(DVE, lambda h: h.memset(at_q.t[i][:], 0.0), writes=[at_q.b[i]])
                for c2 in range(2):
                    fw.op(DVE, lambda h: h.memset(at_qm[c2].t[i][:], 0.0), writes=[at_qm[c2].b[i]])
            its = [(gi, head, qq) for gi, (q0, nq, k0, nk) in enumerate(groups) for head in range(8) for qq in range(q0, q0 + nq, TT)]
            kv_cache, q_cache = {}, {}

            def get_kv(gi, head):
                if (gi, head) not in kv_cache:
                    q0, nq, k0, nk = groups[gi]
                    kt, bkt = at_k.get(); vt, bvt = at_v.get()
                    fw.dma(SP, kt[0:96, 0:nk], k_s[head, :, k0:k0 + nk], reads=[b_k], writes=[bkt])
                    fw.dma(SP, vt[:, 0:nk // 128, :], v_s[head, :, k0 // 128:k0 // 128 + nk // 128, :], reads=[b_v], writes=[bvt])
                    kv_cache[(gi, head)] = (kt, bkt, vt, bvt)
                return kv_cache[(gi, head)]

            def get_q(idx):
                if idx not in q_cache:
                    gi, head, qq = its[idx]
                    nqt = min(TT, groups[gi][1])
                    qt, bqt = at_q.get()
                    fw.dma(SP, qt[0:96, 0:nqt], q_s[head, :, qq:qq + nqt], reads=[b_q], writes=[bqt])
                    q_cache[idx] = (qt, bqt)
                return q_cache[idx]

            for idx, (gi, head, qq) in enumerate(its):
                q0, nq, k0, nk = groups[gi]
                nkc = nk // 128
                if True:
                    kt, bkt, vt, bvt = get_kv(gi, head)
                    if True:
                        nqt = min(TT, nq)
                        qt, bqt = get_q(idx)
                        if idx + 1 < len(its):
                            get_kv(its[idx + 1][0], its[idx + 1][1])
                            get_q(idx + 1)
                        po, bpo = psB.get()

                        def qk(c, kt=kt, qt=qt, bkt=bkt, bqt=bqt, nqt=nqt):
                            ps, bps = psA.get()
                            fw.op(PE, lambda h: h.matmul(ps[:, 0:nqt], lhsT=kt[:, c * 128:(c + 1) * 128], rhs=qt[:, 0:nqt], start=True, stop=True),
                                  reads=[bkt, bqt], writes=[bps])
                            return ps, bps

                        LA = 2
                        pend = [qk(i) for i in range(min(LA, nkc))]
                        for c in range(nkc):
                            ps, bps = pend.pop(0)
                            if c + LA < nkc:
                                pend.append(qk(c + LA))
                            pt, bpt = at_p.get()
                            fw.op(ACT, lambda h, ps=ps, pt=pt: h.activation(out=pt[:, 0:nqt], in_=ps[:, 0:nqt], func=AF.Exp, scale=sc), reads=[bps], writes=[bpt])
                            fw.op(PE, lambda h, c=c, po=po, vt=vt, pt=pt: h.matmul(po[:, 0:nqt], lhsT=vt[:, c, :], rhs=pt[:, 0:nqt], start=(c == 0), stop=(c == nkc - 1)),
                                  reads=[bvt, bpt], writes=[bpo], inc=(c == nkc - 1))
                        rc, brc = f32_r.get()
                        fw.op(DVE, lambda h, po=po, rc=rc: h.reciprocal(out=rc[64:128, 0:nqt], in_=po[64:128, 0:nqt]), reads=[bpo], writes=[brc])
                        r2, br2 = f32_r.get()
                        fw.op(DVE, lambda h, rc=rc, r2=r2: h.tensor_copy(out=r2[0:64, 0:nqt], in_=rc[64:128, 0:nqt]), reads=[brc], writes=[br2])
                        ob, bob = at_o.get()
                        fw.op(DVE, lambda h, po=po, r2=r2, ob=ob: h.tensor_tensor(out=ob[0:64, 0:nqt], in0=po[0:64, 0:nqt], in1=r2[0:64, 0:nqt], op=ALU.mult), reads=[bpo, br2], writes=[bob])
                        fw.dma(SP, yb_s[head // 2, (head % 2) * 64:(head % 2) * 64 + 64, qq:qq + nqt], ob[0:64, 0:nqt], reads=[bob], writes=[b_yb])

        def diff_attn(l, groups, lam_init):
            sc = 64 ** -0.5
            its = [(gi, head, qq) for gi, (q0, nq, k0, nk) in enumerate(groups) for head in range(4) for qq in range(q0, q0 + nq, TT)]
            kv_cache, q_cache = {}, {}

            def get_kv(gi, head):
                if (gi, head) not in kv_cache:
                    q0, nq, k0, nk = groups[gi]
                    kt, bkt = at_k.get(); vt, bvt = at_v.get()
                    fw.dma(SP, kt[:, 0:nk], dk_s[head, :, k0:k0 + nk], reads=[b_dk], writes=[bkt])
                    fw.dma(SP, vt[:, 0:nk // 128, :], dv_s[head, :, k0 // 128:k0 // 128 + nk // 128, :], reads=[b_dv], writes=[bvt])
                    kv_cache[(gi, head)] = (kt, bkt, vt, bvt)
                return kv_cache[(gi, head)]

            def get_q(idx):
                if idx not in q_cache:
                    gi, head, qq = its[idx]
                    nqt = min(TT, groups[gi][1])
                    qm = [at_qm[0].get(), at_qm[1].get()]
                    fw.dma(SP, qm[0][0][0:64, 0:nqt], dq_s[head, 0:64, qq:qq + nqt], reads=[b_dq], writes=[qm[0][1]])
                    fw.dma(SP, qm[1][0][64:128, 0:nqt], dq_s[head, 64:128, qq:qq + nqt], reads=[b_dq], writes=[qm[1][1]])
                    q_cache[idx] = qm
                return q_cache[idx]

            for idx, (gi, head, qq) in enumerate(its):
                q0, nq, k0, nk = groups[gi]
                nkc = nk // 128
                if True:
                    kt, bkt, vt, bvt = get_kv(gi, head)
                    if True:
                        nqt = min(TT, nq)
                        qm = get_q(idx)
                        if idx + 1 < len(its):
                            get_kv(its[idx + 1][0], its[idx + 1][1])
                            get_q(idx + 1)
                        acc = [psB.get() for _ in range(3)]
                        lac = [at_l.get()]

                        def qk(i, kt=kt, bkt=bkt, nqt=nqt, qm=qm):
                            c, comp = divmod(i, 2)
                            ps, bps = psA.get()
                            fw.op(PE, lambda h: h.matmul(ps[:, 0:nqt], lhsT=kt[:, c * 128:(c + 1) * 128], rhs=qm[comp][0][:, 0:nqt], start=True, stop=True),
                                  reads=[bkt, qm[comp][1]], writes=[bps])
                            return ps, bps

                        LA = 2
                        pend = [qk(i) for i in range(min(LA, 2 * nkc))]
                        for c in range(nkc):
                            for comp in range(2):
                                ps, bps = pend.pop(0)
                                if c * 2 + comp + LA < 2 * nkc:
                                    pend.append(qk(c * 2 + comp + LA))
                                pt, bpt = at_p.get()
                                fw.op(ACT, lambda h: h.activation(out=pt[:, 0:nqt], in_=ps[:, 0:nqt], func=AF.Exp, scale=sc), reads=[bps], writes=[bpt])
                                po, bpo = acc[comp]
                                fw.op(PE, lambda h: h.matmul(po[:, 0:nqt], lhsT=vt[:, c, :], rhs=pt[:, 0:nqt], start=(c == 0), stop=(c == nkc - 1)),
                                      reads=[bvt, bpt], writes=[bpo], inc=(c == nkc - 1))
                                if comp == 0:
                                    la, bla = lac[0]
                                    if c == 0:
                                        fw.op(DVE, lambda h: h.tensor_copy(out=la[:, 0:nqt], in_=pt[:, 0:nqt]), reads=[bpt], writes=[bla])
                                    else:
                                        fw.op(DVE, lambda h: h.tensor_tensor(out=la[:, 0:nqt], in0=la[:, 0:nqt], in1=pt[:, 0:nqt], op=ALU.add), reads=[bpt], writes=[bla])
                                else:
                                    pl1, bpl1 = acc[2]
                                    fw.op(PE, lambda h: h.matmul(pl1[:, 0:nqt], lhsT=cmat[:, C_ONE, :], rhs=pt[:, 0:nqt], start=(c == 0), stop=(c == nkc - 1)),
                                          reads=[b_cmat, bpt], writes=[bpl1], inc=(c == nkc - 1))
                        rr_ = []
                        pl, bpl = psA.get()
                        fw.op(PE, lambda h: h.matmul(pl[:, 0:nqt], lhsT=onesf[:, :], rhs=lac[0][0][:, 0:nqt], start=True, stop=True), reads=[b_onesf, lac[0][1]], writes=[bpl])
                        for (pl_, bpl_) in ((pl, bpl), acc[2]):
                            rc_, brc_ = f32_r.get()
                            fw.op(DVE, lambda h: h.reciprocal(out=rc_[:, 0:nqt], in_=pl_[:, 0:nqt]), reads=[bpl_], writes=[brc_])
                            rr_.append((rc_, brc_))
                        (r0, br0), (r1, br1) = rr_
                        o0, bo0 = f32_r.get(); o1, bo1 = f32_r.get()
                        fw.op(DVE, lambda h, o0=o0, r0=r0, p=acc[0][0]: h.tensor_tensor(out=o0[:, 0:nqt], in0=p[:, 0:nqt], in1=r0[:, 0:nqt], op=ALU.mult), reads=[acc[0][1], br0], writes=[bo0])
                        fw.op(DVE, lambda h, o1=o1, r1=r1, p=acc[1][0]: h.tensor_tensor(out=o1[:, 0:nqt], in0=p[:, 0:nqt], in1=r1[:, 0:nqt], op=ALU.mult), reads=[acc[1][1], br1], writes=[bo1])
                        fw.op(DVE, lambda h, o0=o0, o1=o1: h.scalar_tensor_tensor(out=o0[:, 0:nqt], in0=o1[:, 0:nqt], scalar=lamc[:, 0:1], in1=o0[:, 0:nqt], op0=ALU.mult, op1=ALU.add),
                              reads=[bo1, b_lamc], writes=[bo0])
                        sq, bsq = bf_r.get()
                        fw.op(ACT, lambda h, sq=sq, o0=o0: h.activation(out=sq[:, 0:nqt], in_=o0[:, 0:nqt], func=AF.Square), reads=[bo0], writes=[bsq])
                        ps, bps = psA.get()
                        fw.op(PE, lambda h, ps=ps, sq=sq: h.matmul(ps[:, 0:nqt], lhsT=cmat[:, C_128, :], rhs=sq[:, 0:nqt], start=True, stop=True), reads=[bsq, b_cmat], writes=[bps])
                        rs, brs = rs_r.get()
                        rsqrt_eps(ps[:, 0:nqt], rs[:, 0:nqt], bps, brs)
                        t1, bt1 = f32_r.get()
                        fw.op(DVE, lambda h, t1=t1, o0=o0, rs=rs: h.scalar_tensor_tensor(out=t1[:, 0:nqt], in0=o0[:, 0:nqt], scalar=gc[:, 10:11], in1=rs[:, 0:nqt], op0=ALU.mult, op1=ALU.mult),
                              reads=[bo0, brs, b_gc], writes=[bt1])
                        ob, bob = at_o.get()
                        fw.op(ACT, lambda h, t1=t1, ob=ob: h.activation(out=ob[:, 0:nqt], in_=t1[:, 0:nqt], func=AF.Copy, scale=(1.0 - lam_init)), reads=[bt1], writes=[bob])
                        fw.dma(SP, yc_s[head, :, qq:qq + nqt], ob[:, 0:nqt], reads=[bob], writes=[b_yc])

        arena.reset()
        y3_r = Ring(ov, "y3", [128, 12, TT], BF16, 1)
        mg_r = Ring(ov, "mg", [128, 8, TT], BF16, 1)
        macc_r = Ring(ov, "macc", [128, 4, TT], F32, 1)
        aTraw = ov("aTraw", [128, 8192], F32)
        aT_view = aTraw.bitcast(BF16).rearrange("p (a b) -> p a b", b=TT)
        xo_view = aTraw[:, 0:4096].rearrange("p (g f) -> p g f", f=D)
        b_aTraw = Buf()

        def phase3(l, ti, last):
            cond = 1 if ti > 0 else 0
            t0 = ti * TT
            def p3_fetch(tj):
                hT_, bh_ = hT_r.get()
                fw.dma(SP, hT_[:], hT_s[:, :, tj * TT:(tj + 1) * TT], reads=[b_hTs[tj]], writes=[bh_])
                y3_, by3_ = y3_r.get()
                for bi_, (src, bsrc) in enumerate(((ya_s, b_ya), (yb_s, b_yb), (yc_s, b_yc))):
                    fw.dma(SP, y3_[:, bi_ * 4:(bi_ + 1) * 4, :], src[:, :, tj * TT:(tj + 1) * TT].rearrange("h p t -> p h t"), reads=[bsrc], writes=[by3_])
                return hT_, bh_, y3_, by3_

            hT, bh, y3, by3 = p3_pref.pop((l, ti), None) or p3_fetch(ti)
            xt, bxt = load_xT(ti)
            mg, bmg = mg_r.get()
            macc, bmacc = macc_r.get()
            for half in range(2):
                for bi in range(3):
                    c0 = O_G + bi * 1024 + half * 512
                    wg, bwg = load_w8(w_in[l, :, c0:c0 + 512], 512)
                    wo, bwo = w4.get()
                    fw.dma(POOL, wo[:, :, 0:512], w_o3[l, bi, :, half * 512:(half + 1) * 512].rearrange("(k p) c -> p k c", p=128), writes=[bwo])
                    for j in range(4):
                        oc = half * 4 + j
                        psg, bpsg = proj_fm(wg, bwg, j * 128, 128, lambda k: hT[:, k, :], [bh], 8)
                        sg, bsg = f32_r.get()
                        fw.op(ACT, lambda h: h.activation(out=sg[:], in_=psg[:], func=AF.Sigmoid), reads=[bpsg], writes=[bsg])
                        pso, bpso = proj_fm(wo, bwo, j * 128, 128, lambda k: y3[:, bi * 4 + k, :], [by3], 4)
                        if bi == 0:
                            fw.op(DVE, lambda h: h.tensor_tensor(out=macc[:, j, :], in0=sg[:], in1=pso[:], op=ALU.mult), reads=[bsg, bpso], writes=[bmacc])
                        else:
                            tmp, btmp = f32_r.get()
                            fw.op(DVE, lambda h: h.tensor_tensor(out=tmp[:], in0=sg[:], in1=pso[:], op=ALU.mult), reads=[bsg, bpso], writes=[btmp])
                            if bi == 1:
                                fw.op(DVE, lambda h: h.tensor_tensor(out=macc[:, j, :], in0=macc[:, j, :], in1=tmp[:], op=ALU.add), reads=[btmp], writes=[bmacc])
                            else:
                                fw.op(DVE, lambda h: h.tensor_tensor(out=mg[:, oc, :], in0=macc[:, j, :], in1=tmp[:], op=ALU.add), reads=[btmp, bmacc], writes=[bmg])
            if ti + 1 < NT:
                p3_pref[(l, ti + 1)] = p3_fetch(ti + 1)
            for half in range(2):
                wt, bw = load_w8(w_out[l, :, half * 512:(half + 1) * 512], 512)
                for j in range(4):
                    oc = half * 4 + j
                    ps, bps = proj_fm(wt, bw, j * 128, 128, lambda k: mg[:, k, :], [bmg], 8)
                    fw.op(DVE, lambda h, oc=oc, ps=ps: h.scalar_tensor_tensor(out=xt[:, oc, :], in0=ps[:], scalar=modT[:, 16 + oc, cond:cond + 1], in1=xt[:, oc, :], op0=ALU.mult, op1=ALU.add),
                          reads=[bps, b_modT_b], writes=[bxt])
            h2, bh2 = norm_mod(xt, bxt, A2, 24, cond)
            aT, baT = aT_view, b_aTraw
            for fb in range(8):
                wt, bw = load_w8(w_m1[l, :, fb * 512:(fb + 1) * 512], 512)
                for j in range(4):
                    ps, bps = proj_fm(wt, bw, j * 128, 128, lambda k: h2[:, k, :], [bh2], 8)
                    rl, brl = f32_r.get()
                    fw.op(ACT, lambda h, ps=ps, rl=rl: h.activation(out=rl[:], in_=ps[:], func=AF.Relu), reads=[bps], writes=[brl])
                    fw.op(DVE, lambda h, rl=rl, fb=fb, j=j: h.tensor_tensor(out=aT[:, fb * 4 + j, :], in0=rl[:], in1=rl[:], op=ALU.mult), reads=[brl], writes=[baT])
            for half in range(2):
                accs = [psB.get() for _ in range(4)]
                for kb8 in range(4):
                    wt, bw = load_w8(w_m2[l, kb8 * 1024:(kb8 + 1) * 1024, half * 512:(half + 1) * 512], 512)
                    for j in range(4):
                        ps, bps = accs[j]
                        for k in range(8):
                            kk = kb8 * 8 + k
                            fw.op(PE, lambda h: h.matmul(ps[:], lhsT=wt[:, k, j * 128:(j + 1) * 128], rhs=aT[:, kk, :], start=(kk == 0), stop=(kk == 31)),
                                  reads=[bw, baT], writes=[bps], inc=(k == 7))
                for j in range(4):
                    oc = half * 4 + j
                    ps, bps = accs[j]
                    fw.op(DVE, lambda h: h.scalar_tensor_tensor(out=xt[:, oc, :], in0=ps[:], scalar=modT[:, 40 + oc, cond:cond + 1], in1=xt[:, oc, :], op0=ALU.mult, op1=ALU.add),
                          reads=[bps, b_modT_b], writes=[bxt])
            if not last:
                fw.dma(SP, xT_s[:, :, t0:t0 + TT].rearrange("k p t -> p k t"), xt[:], reads=[bxt], writes=[b_xT[ti]])
            else:
                xo, bxo = xo_view, b_aTraw
                for g in range(4):
                    for half in range(2):
                        ps, bps = psA.get()
                        for k in range(4):
                            kk = half * 4 + k
                            fw.op(PE, lambda h, g=g, k=k, kk=kk, ps=ps: h.transpose(ps[:, k * 128:(k + 1) * 128], xt[:, kk, g * 128:(g + 1) * 128], ident[:]),
                                  reads=[bxt, b_ident], writes=[bps], inc=(k == 3))
                        evac(ps[:], xo[:, g, half * 512:(half + 1) * 512], bps, bxo, eng=(ACT if half else DVE))
                ob = Buf(); out_bufs.append(ob)
                fw.dma(SP, y_out[t0:t0 + TT, :].rearrange("(g p) f -> p g f", p=128), xo[:], reads=[bxo], writes=[ob])

        import os
        KSTOP = int(os.environ.get("KSTOP", "99"))
        step = [0]

        def reached():
            step[0] += 1
            return step[0] > KSTOP

        for l in range(L):
            if reached():
                break
            lam_init = layer_setup(l)
            fw.barrier()
            vaug_init[0] = vaug_init[1] = False
            if reached():
                break
            prep_cache(l)
            if reached():
                break
            for ti in range(NT):
                phase1(l, ti)
            fw.barrier()
            if reached():
                break
            seqs = [(0, 256, 0, False), (256, 256, 1, False), (NP_TOK, NS_TOK, None, True)]
            gla_dir(l, 0, seqs)
            fw.barrier()
            gla_dir(l, 1, seqs)
            fw.barrier()
            if reached():
                break
            groups = [(0, 256, 0, 256), (256, 256, 256, 256), (NP_TOK, NS_TOK, NP_TOK, TKS)]
            mla_attn(l, groups)
            if reached():
                break
            diff_attn(l, groups, lam_init)
            if reached():
                break
            fw.barrier()
            for ti in range(NT):
                phase3(l, ti, l == L - 1)
            fw.barrier()

        fw.finish(out_bufs)
    return nc


def _rope_tables():
    t = np.arange(NS_TOK)
    row, col = t // 64, t % 64

    def cs(nf, pos):
        inv = (10000.0 ** (-np.arange(nf, dtype=np.float32) / nf)).astype(np.float32)
        ang = pos.astype(np.float32)[None, :] * inv[:, None]
        c, s = np.cos(ang).astype(np.float32), np.sin(ang).astype(np.float32)
        return np.concatenate([c, c], 0), np.concatenate([s, s], 0)

    cr, sr = cs(8, row); cc, sc_ = cs(8, col)
    cM = np.concatenate([cr, cc, np.ones((64, NS_TOK), np.float32)], 0)
    sM = np.concatenate([sr, sc_, np.zeros((64, NS_TOK), np.float32)], 0)
    cr, sr = cs(16, row); cc, sc_ = cs(16, col)
    c64 = np.concatenate([cr, cc], 0); s64 = np.concatenate([sr, sc_], 0)
    cD = np.concatenate([c64, c64], 0); sD = np.concatenate([s64, s64], 0)
    return np.stack([cM, sM]).astype(np.float32), np.stack([cD, sD]).astype(np.float32)


def _rot_matrix(nf):
    R = np.zeros((2 * nf, 2 * nf), np.float32)
    for i in range(nf):
        R[i, nf + i] = -1.0
        R[nf + i, i] = 1.0
    return R


def _consts():
    bf = ml_dtypes.bfloat16
    cmat = np.zeros((128, 8, 128), np.float32)
    cmat[:, 0, :] = 1.0 / 1024
    cmat[:, 1, :] = 1.0 / 384
    cmat[:, 2, :] = 1.0 / 256
    cmat[:96, 3, :96] = 1.0 / 96
    cmat[:64, 4, :64] = 1.0 / 64
    cmat[64:, 4, 64:] = 1.0 / 64
    cmat[:, 5, :] = 1.0 / 128
    cmat[:, 6, :] = 1.0
    cmat[:32, 7, :32] = np.eye(32)
    pm = np.zeros((128, 3, 128), np.float32)
    R96 = np.zeros((96, 96), np.float32)
    R96[0:16, 0:16] = _rot_matrix(8); R96[16:32, 16:32] = _rot_matrix(8)
    pm[:96, 0, :96] = R96.T
    R64 = np.zeros((64, 64), np.float32)
    R64[0:32, 0:32] = _rot_matrix(16); R64[32:64, 32:64] = _rot_matrix(16)
    R128 = np.zeros((128, 128), np.float32)
    R128[:64, :64] = R64; R128[64:, 64:] = R64
    pm[:, 1, :] = R128.T
    i = np.arange(64)
    tri = np.zeros((64, 4, 64), np.float32)
    tri[:, 0, :] = (i[:, None] <= i[None, :])
    tri[:, 1, :] = (i[:, None] >= i[None, :])
    tri[:, 2, :] = (i[:, None] > i[None, :])
    tri[:, 3, :] = (i[:, None] < i[None, :])
    return cmat.astype(bf), pm.astype(bf), tri


SAMPLE_CORES = [0, 1, 4, 5]

_PROG = {}


def _get_prog():
    if "nc" not in _PROG:
        _PROG["nc"] = build_program()
    return _PROG["nc"]


def make_in_maps(x_prompt, x_sample, state_gla, cache_mla_ckv, cache_mla_krope, cache_diff_k, cache_diff_v, c, c_ctx, w_mod, b_mod, g_norm1, g_norm2, w_in,
                 w_gla_a2, b_gla_a, g_gla_out, g_mla_qa, g_mla_kva, w_mla_uq, w_mla_uk, w_mla_uv, g_mla_q, g_mla_k, g_diff_q, g_diff_k, lam_qk, g_diff_sub,
                 w_o_gla, w_o_mla, w_o_diff, w_out, w_mlp1, w_mlp2):
    f = lambda a: np.ascontiguousarray(np.asarray(a, dtype=np.float32))
    cmat, pm, tri = _consts()
    ropeM, ropeD = _rope_tables()
    perm = np.concatenate([np.arange(64, 96), np.arange(0, 64)])
    w_uq_p = f(w_mla_uq).reshape(L, 384, 8, 96)[:, :, :, perm]
    w_ukp = np.zeros((L, 256, 8, 96), np.float32)
    w_ukp[:, :, :, 32:] = f(w_mla_uk).reshape(L, 256, 8, 64)
    gcol = np.zeros((L, 128, 12), np.float32)
    gcol[:, :, 0:3] = f(g_mla_qa).reshape(L, 3, 128).transpose(0, 2, 1)
    gcol[:, :, 3:5] = f(g_mla_kva).reshape(L, 2, 128).transpose(0, 2, 1)
    gcol[:, :96, 5] = f(g_mla_q)[:, perm]
    gcol[:, :96, 6] = f(g_mla_k)[:, perm]
    gcol[:, :, 7] = np.tile(f(g_diff_q), (1, 2))
    gcol[:, :, 8] = np.tile(f(g_diff_k), (1, 2))
    gcol[:, :, 9] = f(g_gla_out)
    gcol[:, :, 10] = f(g_diff_sub)
    w_a2bd = np.zeros((L, 32, 512), np.float32)
    w_a2bd[:, 0:16, 0:256] = f(w_gla_a2)[:, 0]
    w_a2bd[:, 16:32, 256:512] = f(w_gla_a2)[:, 1]
    b_a = f(b_gla_a).reshape(L, 1, 512)
    gnT = np.concatenate([f(g_norm1).reshape(L, 8, 128).transpose(0, 2, 1), f(g_norm2).reshape(L, 8, 128).transpose(0, 2, 1)], axis=2)
    b_modT = f(b_mod).reshape(L, 48, 128).transpose(0, 2, 1)
    w_o3 = np.stack([f(w_o_gla), f(w_o_mla), f(w_o_diff)], axis=1)
    shared = dict(w_mod=f(w_mod), b_modT=f(b_modT), gnT=f(gnT), w_in=f(w_in), w_a2bd=w_a2bd, b_a=b_a, gcol=gcol, w_uq=f(w_uq_p), w_ukp=w_ukp,
                  w_uv=f(w_mla_uv), lam_qk=f(lam_qk).reshape(L, 1, 256), w_o3=f(w_o3), w_out=f(w_out), w_m1=f(w_mlp1), w_m2=f(w_mlp2),
                  ident=np.eye(128, dtype=np.float32), cmat=cmat, pmat=pm, tri=tri, ropeM=ropeM, ropeD=ropeD)
    xp, xs = f(x_prompt), f(x_sample)
    in_maps = []
    for core in range(8):
        m = dict(shared)
        if core in SAMPLE_CORES:
            b = SAMPLE_CORES.index(core)
            xs_b, c_b = xs[b], f(c)[b]
            m["st_gla"] = f(f(state_gla)[b].transpose(0, 1, 3, 2, 4))
            m["c_ckv"] = f(cache_mla_ckv)[b]
            m["c_kr"] = f(cache_mla_krope)[b]
            m["c_dk"] = f(cache_diff_k)[b].reshape(L, PAST, 512)
            m["c_dv"] = f(cache_diff_v)[b].reshape(L, PAST, 512)
        else:
            xs_b, c_b = np.zeros((NS_TOK, D), np.float32), np.zeros((D,), np.float32)
            m["st_gla"] = np.zeros((L, 2, 64, 4, 128), np.float32)
            m["c_ckv"] = np.zeros((L, PAST, 256), np.float32)
            m["c_kr"] = np.zeros((L, PAST, 32), np.float32)
            m["c_dk"] = np.zeros((L, PAST, 512), np.float32)
            m["c_dv"] = np.zeros((L, PAST, 512), np.float32)
        m["xin"] = np.concatenate([xp[2 * core], xp[2 * core + 1], xs_b], axis=0)
        cond = np.stack([f(c_ctx), c_b], axis=0)
        m["condT"] = f(cond.reshape(2, 8, 128).transpose(2, 1, 0))
        in_maps.append(m)
    return in_maps


def kernel(**inputs):
    nc = _get_prog()
    in_maps = make_in_maps(**inputs)
    res = run_bass_kernel_spmd(nc, in_maps, core_ids=list(range(8)))
    R = res.results
    y_prompt = np.zeros((16, 256, D), np.float32)
    y_sample = np.zeros((4, NS_TOK, D), np.float32)
    n_gla = np.zeros((16, L, 2, 4, 64, 128), np.float32)
    n_ckv = np.zeros((16, L, 256, 256), np.float32)
    n_kr = np.zeros((16, L, 256, 32), np.float32)
    n_dk = np.zeros((16, L, 256, 4, 2, 64), np.float32)
    n_dv = np.zeros((16, L, 256, 4, 128), np.float32)
    for core in range(8):
        r = R[core]
        for s in range(2):
            bi = 2 * core + s
            y_prompt[bi] = r["y_out"][s * 256:(s + 1) * 256]
            n_gla[bi] = r["o_gla"][:, s].transpose(0, 1, 3, 2, 4)
            n_ckv[bi] = r["o_ckv"][:, s * 256:(s + 1) * 256]
            n_kr[bi] = r["o_kr"][:, s * 256:(s + 1) * 256]
            n_dk[bi] = r["o_dk"][:, s * 256:(s + 1) * 256].reshape(L, 256, 4, 2, 64)
            n_dv[bi] = r["o_dv"][:, s * 256:(s + 1) * 256].reshape(L, 256, 4, 128)
        if core in SAMPLE_CORES:
            y_sample[SAMPLE_CORES.index(core)] = r["y_out"][NP_TOK:]
    return (y_prompt, y_sample, n_gla, n_ckv, n_kr, n_dk, n_dv)
```
